# Optimizing a Trainium2 kernel written in Bass

```python
import math
import jax
import jax.numpy as jnp
from jax import lax
import numpy as np

D_MODEL = 1024
BATCH = 2
SEQ = 8192
DEPTH = 2

N_EVEN = (DEPTH + 1) // 2
N_ODD = DEPTH // 2
RMS_EPS = 1e-6

GDN_HEADS = 8
GDN_DK = 128
GDN_DV = 128
GDN_CONV = 4
GDN_CHUNK = 64
GDN_QK = GDN_HEADS * GDN_DK
GDN_V = GDN_HEADS * GDN_DV

S5_WIDTH = D_MODEL
S5_GROUP = 16
S5_GROUPS = S5_WIDTH // S5_GROUP
S5_STATE = 64
S5_MIN_NEG = 1e-4

EVEN_SIZES = (2 * GDN_QK + GDN_V, GDN_V, GDN_HEADS, GDN_HEADS, S5_WIDTH, S5_WIDTH)
EVEN_PROJ = sum(EVEN_SIZES)
EVEN_MIX = GDN_V + S5_WIDTH

SC_WIDTH = 2 * D_MODEL
SC_CONV = 3

kernel_name = "hybrid_gdn_s5_shortconv_sandwich"


def rms_norm(x, w):
    xf = x.astype(jnp.float32)
    xf = xf * lax.rsqrt(jnp.mean(xf * xf, axis=-1, keepdims=True) + RMS_EPS)
    return (xf * w.astype(jnp.float32)).astype(x.dtype)


def l2_normalize(x):
    return x * lax.rsqrt(jnp.sum(x * x, axis=-1, keepdims=True) + RMS_EPS)


def causal_dwconv(x, w):
    k, c = w.shape
    return lax.conv_general_dilated(
        x, w[:, None, :], window_strides=(1,), padding=[(k - 1, 0)],
        dimension_numbers=("NWC", "WIO", "NWC"), feature_group_count=c)


def split_cols(t, sizes):
    idx = [int(i) for i in np.cumsum(sizes)[:-1]]
    return jnp.split(t, idx, axis=-1)


def gated_delta_rule_chunked(q, k, v, beta, g):
    bsz, s, h, dk = q.shape
    dv = v.shape[-1]
    c = GDN_CHUNK
    n = s // c

    def to_chunks(t):
        t = t.reshape((bsz, n, c, h) + t.shape[3:])
        return jnp.moveaxis(t, 3, 1)

    q, k, v, beta, g = (to_chunks(t) for t in (q, k, v, beta, g))
    gc = jnp.cumsum(g, axis=-1)
    causal = jnp.tril(jnp.ones((c, c), dtype=bool))
    strict = jnp.tril(jnp.ones((c, c), dtype=bool), -1)
    decay = jnp.exp(jnp.where(causal, gc[..., :, None] - gc[..., None, :], -jnp.inf))
    kb = k * beta[..., None]
    lower = jnp.where(strict, jnp.einsum("bhncd,bhnmd->bhncm", kb, k) * decay, 0.0)
    eye = jnp.eye(c, dtype=jnp.float32)
    rhs = jnp.concatenate([v * beta[..., None], kb * jnp.exp(gc)[..., None]], axis=-1)
    sol = lax.linalg.triangular_solve(lower + eye, rhs, left_side=True, lower=True,
                                      unit_diagonal=True)
    u_c, w_c = sol[..., :dv], sol[..., dv:]
    attn = jnp.einsum("bhncd,bhnmd->bhncm", q, k) * decay
    q_dec = q * jnp.exp(gc)[..., None]
    k_dec = k * jnp.exp(gc[..., -1:] - gc)[..., None]
    g_last = jnp.exp(gc[..., -1])

    def step(state, xs):
        q_n, a_n, u_n, w_n, k_n, gl_n = xs
        v_new = u_n - jnp.einsum("bhcd,bhde->bhce", w_n, state)
        o_n = (jnp.einsum("bhcd,bhde->bhce", q_n, state)
               + jnp.einsum("bhcm,bhme->bhce", a_n, v_new))
        state = state * gl_n[..., None, None] + jnp.einsum("bhcd,bhce->bhde", k_n, v_new)
        return state, o_n

    xs = tuple(jnp.moveaxis(t, 2, 0) for t in (q_dec, attn, u_c, w_c, k_dec, g_last))
    state0 = jnp.zeros((bsz, h, dk, dv), jnp.float32)
    _, o = lax.scan(step, state0, xs)
    return o.transpose(1, 0, 3, 2, 4).reshape(bsz, s, h, dv)


def _complex_affine_combine(e1, e2):
    a1r, a1i, b1r, b1i = e1
    a2r, a2i, b2r, b2i = e2
    return (a2r * a1r - a2i * a1i,
            a2r * a1i + a2i * a1r,
            a2r * b1r - a2i * b1i + b2r,
            a2r * b1i + a2i * b1r + b2i)


def s5_ssm(u, lam_re, lam_im, b_re, b_im, c_re, c_im, log_dt, d):
    bsz, s, _ = u.shape
    f32 = jnp.float32
    uf = u.astype(f32).reshape(bsz, s, S5_GROUPS, S5_GROUP)
    lr = jnp.minimum(lam_re.astype(f32), -S5_MIN_NEG)
    li = lam_im.astype(f32)
    dt = jnp.exp(log_dt.astype(f32))[:, None]
    mag = jnp.exp(lr * dt)
    ab_re = mag * jnp.cos(li * dt)
    ab_im = mag * jnp.sin(li * dt)
    den = lr * lr + li * li
    nr, ni = ab_re - 1.0, ab_im
    f_re = (nr * lr + ni * li) / den
    f_im = (ni * lr - nr * li) / den
    br, bi = b_re.astype(f32), b_im.astype(f32)
    bb_re = f_re[..., None] * br - f_im[..., None] * bi
    bb_im = f_re[..., None] * bi + f_im[..., None] * br
    bu_re = jnp.einsum("bsgh,gph->bsgp", uf, bb_re)
    bu_im = jnp.einsum("bsgh,gph->bsgp", uf, bb_im)
    a_re = jnp.broadcast_to(ab_re, (1, s, S5_GROUPS, S5_STATE))
    a_im = jnp.broadcast_to(ab_im, (1, s, S5_GROUPS, S5_STATE))
    _, _, x_re, x_im = lax.associative_scan(
        _complex_affine_combine, (a_re, a_im, bu_re, bu_im), axis=1)
    y = (jnp.einsum("bsgp,ghp->bsgh", x_re, c_re.astype(f32))
         - jnp.einsum("bsgp,ghp->bsgh", x_im, c_im.astype(f32)))
    y = y.reshape(bsz, s, S5_WIDTH) + d.astype(f32) * u.astype(f32)
    return y.astype(u.dtype)


def even_mixer(h, w_in, conv_w, a_log, dt_bias, gdn_norm_w, lam_re, lam_im,
               b_re, b_im, c_re, c_im, log_dt, s5_d, w_glu, w_out):
    bsz, s, _ = h.shape
    f32 = jnp.float32
    proj = h @ w_in
    qkv, z_a, b_raw, a_raw, u, z_b = split_cols(proj, EVEN_SIZES)
    qkv = jax.nn.silu(causal_dwconv(qkv, conv_w))
    q, k, v = split_cols(qkv, (GDN_QK, GDN_QK, GDN_V))
    q = l2_normalize(q.astype(f32).reshape(bsz, s, GDN_HEADS, GDN_DK)) * (GDN_DK ** -0.5)
    k = l2_normalize(k.astype(f32).reshape(bsz, s, GDN_HEADS, GDN_DK))
    v = v.astype(f32).reshape(bsz, s, GDN_HEADS, GDN_DV)
    beta = jax.nn.sigmoid(b_raw.astype(f32))
    g = -jnp.exp(a_log.astype(f32)) * jax.nn.softplus(a_raw.astype(f32) + dt_bias.astype(f32))
    o = gated_delta_rule_chunked(q, k, v, beta, g)
    o = rms_norm(o, gdn_norm_w).reshape(bsz, s, GDN_V).astype(h.dtype)
    y_a = o * jax.nn.silu(z_a)
    y = jax.nn.gelu(s5_ssm(u, lam_re, lam_im, b_re, b_im, c_re, c_im, log_dt, s5_d))
    y = y * jax.nn.sigmoid(y @ w_glu)
    y_b = y * jax.nn.silu(z_b)
    return jnp.concatenate([y_a, y_b], axis=-1) @ w_out


def odd_mixer(h, w_in, conv_w, w_out):
    gb, gc, hv, z = split_cols(h @ w_in, (SC_WIDTH, SC_WIDTH, SC_WIDTH, SC_WIDTH))
    y = gb * causal_dwconv(gc * hv, conv_w)
    return (y * jax.nn.silu(z)) @ w_out


def setup_inputs(seed: int = 0) -> dict:
    key = jax.random.key(seed)
    ks = jax.random.split(key, 24)
    f32 = jnp.float32
    nrm = lambda k, shp, sc: jax.random.normal(k, shp, f32) * sc
    log_lo, log_hi = math.log(1e-3), math.log(1e-1)
    dt0 = jnp.exp(jax.random.uniform(ks[5], (N_EVEN, GDN_HEADS), f32, log_lo, log_hi))
    lam_im0 = jnp.pi * jnp.arange(S5_STATE, dtype=f32)
    return {
        "x": nrm(ks[0], (BATCH, SEQ, D_MODEL), 1.0),
        "norm_pre": 1.0 + nrm(ks[1], (DEPTH, D_MODEL), 0.02),
        "norm_post": 1.0 + nrm(ks[2], (DEPTH, D_MODEL), 0.02),
        "w_in_even": nrm(ks[3], (N_EVEN, D_MODEL, EVEN_PROJ), D_MODEL ** -0.5),
        "conv_qkv": nrm(ks[4], (N_EVEN, GDN_CONV, 2 * GDN_QK + GDN_V), GDN_CONV ** -0.5),
        "a_log": jnp.log(jax.random.uniform(ks[6], (N_EVEN, GDN_HEADS), f32, 1.0, 16.0)),
        "dt_bias": dt0 + jnp.log(-jnp.expm1(-dt0)),
        "gdn_norm_w": 1.0 + nrm(ks[7], (N_EVEN, GDN_DV), 0.02),
        "s5_lam_re": -0.5 + nrm(ks[8], (N_EVEN, S5_GROUPS, S5_STATE), 1e-3),
        "s5_lam_im": lam_im0 + nrm(ks[9], (N_EVEN, S5_GROUPS, S5_STATE), 1e-3),
        "s5_b_re": nrm(ks[10], (N_EVEN, S5_GROUPS, S5_STATE, S5_GROUP), (2 * S5_GROUP) ** -0.5),
        "s5_b_im": nrm(ks[11], (N_EVEN, S5_GROUPS, S5_STATE, S5_GROUP), (2 * S5_GROUP) ** -0.5),
        "s5_c_re": nrm(ks[12], (N_EVEN, S5_GROUPS, S5_GROUP, S5_STATE), S5_STATE ** -0.5),
        "s5_c_im": nrm(ks[13], (N_EVEN, S5_GROUPS, S5_GROUP, S5_STATE), S5_STATE ** -0.5),
        "s5_log_dt": jax.random.uniform(ks[14], (N_EVEN, S5_GROUPS), f32, log_lo, log_hi),
        "s5_d": nrm(ks[15], (N_EVEN, S5_WIDTH), 1.0),
        "w_glu": nrm(ks[16], (N_EVEN, S5_WIDTH, S5_WIDTH), S5_WIDTH ** -0.5),
        "w_out_even": nrm(ks[17], (N_EVEN, EVEN_MIX, D_MODEL), EVEN_MIX ** -0.5),
        "w_in_odd": nrm(ks[18], (N_ODD, D_MODEL, 4 * SC_WIDTH), D_MODEL ** -0.5),
        "conv_short": nrm(ks[19], (N_ODD, SC_CONV, SC_WIDTH), SC_CONV ** -0.5),
        "w_out_odd": nrm(ks[20], (N_ODD, SC_WIDTH, D_MODEL), SC_WIDTH ** -0.5),
    }


def reference(x, norm_pre, norm_post, w_in_even, conv_qkv, a_log, dt_bias, gdn_norm_w,
              s5_lam_re, s5_lam_im, s5_b_re, s5_b_im, s5_c_re, s5_c_im, s5_log_dt, s5_d,
              w_glu, w_out_even, w_in_odd, conv_short, w_out_odd):
    for layer in range(DEPTH):
        i = layer // 2
        h = rms_norm(x, norm_pre[layer])
        if layer % 2 == 0:
            y = even_mixer(h, w_in_even[i], conv_qkv[i], a_log[i], dt_bias[i], gdn_norm_w[i],
                           s5_lam_re[i], s5_lam_im[i], s5_b_re[i], s5_b_im[i],
                           s5_c_re[i], s5_c_im[i], s5_log_dt[i], s5_d[i],
                           w_glu[i], w_out_even[i])
        else:
            y = odd_mixer(h, w_in_odd[i], conv_short[i], w_out_odd[i])
        x = x + rms_norm(y, norm_post[layer]).astype(x.dtype)
    return x
```

```python
from contextlib import ExitStack
import numpy as np
import concourse.bass as bass
import concourse.mybir as mybir
from concourse.bass_utils import run_bass_kernel_spmd

F32 = mybir.dt.float32
BF16 = mybir.dt.bfloat16
AF = mybir.ActivationFunctionType
ALU = mybir.AluOpType
AX = mybir.AxisListType

NDS = 12


class Buf:
    __slots__ = ("name", "w", "r", "multi")

    def __init__(self, name, multi=False):
        self.name = name
        self.w = [] if multi else None
        self.r = []
        self.multi = multi


class Prog:
    ENG = ("pe", "act", "dve", "pool", "sp")

    def __init__(self, nc, stack):
        self.nc = nc
        self.stack = stack
        self.streams = {e: [] for e in self.ENG}
        self.cnt = {e: 0 for e in self.ENG}
        self.sem = {e: stack.enter_context(nc.semaphore("s_" + e)) for e in self.ENG}
        self.seen = {e: {} for e in self.ENG}
        self.dcnt = {e: 0 for e in self.ENG}
        self.dsem = {}
        for e in ("sp", "pool", "act"):
            self.dsem[e] = [stack.enter_context(nc.semaphore("d_%s%d" % (e, i))) for i in range(NDS)]
        self.same_engine_sync = True
        self.nwaits = 0

    def _wait(self, eng, tok):
        if tok is None:
            return
        kind = tok[0]
        if kind == "c":
            _, e2, n = tok
            if e2 == eng and (eng == "pe" or not self.same_engine_sync):
                return
            key = e2
            if self.seen[eng].get(key, 0) >= n:
                return
            self.seen[eng][key] = n
            sem = self.sem[e2]
            self.streams[eng].append(lambda E, sem=sem, n=n: E.wait_ge(sem, n))
            self.nwaits += 1
        else:
            _, q, slot, val = tok
            key = ("d", q, slot)
            if self.seen[eng].get(key, 0) >= val:
                return
            self.seen[eng][key] = val
            sem = self.dsem[q][slot]
            self.streams[eng].append(lambda E, sem=sem, val=val: E.wait_ge(sem, val))
            self.nwaits += 1

    def _deps(self, eng, reads, writes):
        for b in reads:
            if b.multi:
                for t in b.w:
                    self._wait(eng, t)
            else:
                self._wait(eng, b.w)
        for b in writes:
            if not b.multi:
                self._wait(eng, b.w)
            for t in b.r:
                self._wait(eng, t)

    def _commit(self, tok, reads, writes):
        for b in writes:
            if b.multi:
                b.w.append(tok)
            else:
                b.w = tok
            b.r = []
        for b in reads:
            if b not in writes:
                b.r.append(tok)

    def op(self, eng, fn, reads=(), writes=()):
        reads = list(reads)
        writes = list(writes)
        self._deps(eng, reads, writes)
        self.cnt[eng] += 1
        n = self.cnt[eng]
        sem = self.sem[eng]
        self.streams[eng].append(lambda E, fn=fn, sem=sem: fn(E).then_inc(sem, 1))
        tok = ("c", eng, n)
        self._commit(tok, reads, writes)
        return tok

    def mm_group(self, fns, reads=(), writes=()):
        eng = "pe"
        reads = list(reads)
        writes = list(writes)
        self._deps(eng, reads, writes)
        self.cnt[eng] += 1
        n = self.cnt[eng]
        sem = self.sem[eng]
        for fn in fns[:-1]:
            self.streams[eng].append(lambda E, fn=fn: fn(E))
        last = fns[-1]
        self.streams[eng].append(lambda E, fn=last, sem=sem: fn(E).then_inc(sem, 1))
        tok = ("c", eng, n)
        self._commit(tok, reads, writes)
        return tok

    def dma(self, q, out_ap, in_ap, reads=(), writes=()):
        reads = list(reads)
        writes = list(writes)
        self._deps(q, reads, writes)
        j = self.dcnt[q]
        self.dcnt[q] += 1
        slot = j % NDS
        val = 16 * (j // NDS + 1)
        if j >= NDS:
            self._wait(q, ("d", q, slot, val - 16))
        sem = self.dsem[q][slot]
        self.streams[q].append(
            lambda E, o=out_ap, i=in_ap, sem=sem: E.dma_start(out=o, in_=i).then_inc(sem, 16))
        tok = ("d", q, slot, val)
        self._commit(tok, reads, writes)
        return tok

    def dma_ind(self, q, out_ap, table_ap, idx_ap, reads=(), writes=()):
        reads = list(reads)
        writes = list(writes)
        self._deps(q, reads, writes)
        j = self.dcnt[q]
        self.dcnt[q] += 1
        slot = j % NDS
        val = 16 * (j // NDS + 1)
        if j >= NDS:
            self._wait(q, ("d", q, slot, val - 16))
        sem = self.dsem[q][slot]
        self.streams[q].append(
            lambda E, o=out_ap, t=table_ap, i=idx_ap, sem=sem: E.indirect_dma_start(
                out=o, out_offset=None, in_=t, in_offset=bass.IndirectOffsetOnAxis(ap=i, axis=0)).then_inc(sem, 16))
        tok = ("d", q, slot, val)
        self._commit(tok, reads, writes)
        return tok

    def finish(self, final_bufs):
        for b in final_bufs:
            for t in (b.w if b.multi else [b.w]):
                self._wait("sp", t)
        nc = self.nc
        streams = self.streams
        with nc.Block() as block:
            @block.tensor
            def _(E):
                for f in streams["pe"]:
                    f(E)

            @block.scalar
            def _(E):
                for f in streams["act"]:
                    f(E)

            @block.vector
            def _(E):
                for f in streams["dve"]:
                    f(E)

            @block.gpsimd
            def _(E):
                for f in streams["pool"]:
                    f(E)

            @block.sync
            def _(E):
                for f in streams["sp"]:
                    f(E)


class Ctx:
    def __init__(self, nc, st, P=None, pfx=""):
        self.nc = nc
        self.st = st
        self.pfx = pfx
        if P is None:
            st.enter_context(nc.allow_non_contiguous_dma(reason="small parameter loads / layout transforms"))
            P = Prog(nc, st)
        self.P = P

    def sb(self, name, shape, dt=F32):
        t = self.st.enter_context(self.nc.sbuf_tensor("sb_" + self.pfx + name, shape, dt))
        return t, Buf(name)

    def ps(self, name, shape, dt=F32):
        t = self.st.enter_context(self.nc.psum_tensor("ps_" + self.pfx + name, shape, dt))
        return t, Buf(name)


def bcast_row_load(C, name, dram_vec, n, q="sp"):
    t, b = C.sb(name, [128, n])
    C.P.dma(q, t[:], dram_vec.partition_broadcast(128), writes=[b])
    return t, b


def make_ident(C, dram_ident):
    idf, bidf = C.sb("identf", [128, 128])
    C.P.dma("sp", idf[:], dram_ident, writes=[bidf])
    idb, bidb = C.sb("identb", [128, 128], BF16)
    C.P.op("dve", lambda E: E.tensor_copy(out=idb[:], in_=idf[:]), reads=[bidf], writes=[bidb])
    return idf, bidf, idb, bidb


def rms_rstd(C, src, bsrc, ncols, junk, bjunk, ss, bss, eps=1e-6):
    P = C.P
    P.op("act", lambda E: E.activation(out=junk, in_=src, func=AF.Square, accum_out=ss[:, 0:1]),
         reads=[bsrc], writes=[bjunk, bss])
    P.op("act", lambda E: E.activation(out=ss[:, 0:1], in_=ss[:, 0:1], func=AF.Sqrt, bias=float(eps), scale=float(1.0 / ncols)),
         reads=[bss], writes=[bss])
    P.op("dve", lambda E: E.reciprocal(out=ss[:, 0:1], in_=ss[:, 0:1]), reads=[bss], writes=[bss])


def transpose8(C, src_bf, bsrc, idb, bidb, ptr, bptr, dst3, bdst, eng="act"):
    P = C.P
    fns = [(lambda E, kt=kt: E.transpose(out=ptr[:, kt * 128:(kt + 1) * 128], in_=src_bf[:, kt * 128:(kt + 1) * 128],
                                         identity=idb[:])) for kt in range(8)]
    P.mm_group(fns, reads=[bsrc, bidb], writes=[bptr])
    src3 = ptr[:].rearrange("p (k t) -> p k t", k=8)
    if eng == "act":
        P.op("act", lambda E: E.copy(out=dst3, in_=src3), reads=[bptr], writes=[bdst])
    else:
        P.op("dve", lambda E: E.tensor_copy(out=dst3, in_=src3), reads=[bptr], writes=[bdst])


def outproj_post(C, catT, bcat, nkt, wout, bwout, t, xres, bxres, npw, bnpw, pso, bpso, yo, byo, junk, bjunk, ss, bss,
                 out_dram_rows, bout):
    P = C.P
    for hh in range(2):
        fns = [(lambda E, kt=kt, hh=hh: E.matmul(pso[hh][:], lhsT=catT[:, kt, t * 128:(t + 1) * 128],
                                                 rhs=wout[:, kt, hh * 512:(hh + 1) * 512],
                                                 start=(kt == 0), stop=(kt == nkt - 1))) for kt in range(nkt)]
        P.mm_group(fns, reads=[bcat, bwout], writes=[bpso[hh]])
        P.op("act", lambda E, hh=hh: E.copy(out=yo[:, hh * 512:(hh + 1) * 512], in_=pso[hh][:]),
             reads=[bpso[hh]], writes=[byo])
    rms_rstd(C, yo[:], byo, 1024, junk[:], bjunk, ss, bss)
    P.op("dve", lambda E: E.scalar_tensor_tensor(out=yo[:], in0=yo[:], scalar=ss[:, 0:1], in1=npw[:],
                                                 op0=ALU.mult, op1=ALU.mult), reads=[byo, bss, bnpw], writes=[byo])
    P.op("dve", lambda E: E.tensor_tensor(out=yo[:], in0=yo[:], in1=xres, op=ALU.add), reads=[byo, bxres], writes=[byo])
    P.dma("sp", out_dram_rows, yo[:], reads=[byo], writes=[bout])


def load_w_bf16(C, name, dram_w, kt_n, ncols, chunk=2048):
    w, bw = C.sb(name, [128, kt_n, ncols], BF16)
    src = dram_w.rearrange("(k p) c -> p k c", p=128)
    for kt in range(kt_n):
        for c0 in range(0, ncols, chunk):
            c1 = min(ncols, c0 + chunk)
            C.P.dma("pool", w[:, kt, c0:c1], src[:, kt, c0:c1], writes=[bw])
    return w, bw


def build_L2(ntok=2048, fz=None):
    nc = fz["nc"] if fz else bass.Bass("TRN2", target_bir_lowering=False)
    pfx = fz["pfx"] if fz else ""

    def D(name, shape):
        if fz and name in fz["share"]:
            return fz["share"][name]
        return nc.dram_tensor(pfx + name, shape, F32, kind="ExternalInput").ap()
    x_d = D("x", [ntok, 1024]); o_d = D("o", [ntok, 1024]); ys_d = D("ys", [ntok, 1024])
    wz_d = D("wz", [1024, 2048]); wglu_d = D("wglu", [1024, 1024]); wout_d = D("wout", [2048, 1024])
    npre_d = D("npre", [1024]); npost_d = D("npost", [1024]); gnw_d = D("gnw", [128]); ident_d = D("ident", [128, 128])
    out_d = fz["out"] if fz else nc.dram_tensor("out", [ntok, 1024], F32, kind="ExternalOutput").ap()
    NT = 512
    with ExitStack() as st:
        C = Ctx(nc, st, fz["P"], pfx) if fz else Ctx(nc, st); P = C.P
        idf, bidf, idb, bidb = make_ident(C, ident_d)
        npre, bnpre = bcast_row_load(C, "npre", npre_d, 1024)
        npost, bnpost = bcast_row_load(C, "npost", npost_d, 1024)
        gnw, bgnw = bcast_row_load(C, "gnw", gnw_d, 128)
        wz, bwz = load_w_bf16(C, "wz", wz_d, 8, 2048)
        wglu, bwglu = load_w_bf16(C, "wglu", wglu_d, 8, 1024)
        wout, bwout = load_w_bf16(C, "wout", wout_d, 16, 1024)
        xt4, bxt4 = C.sb("xt4", [128, 4, 1024]); bxt = [Buf("xt%d" % i) for i in range(4)]
        ldo = [C.sb("ldo%d" % i, [128, 1024]) for i in range(2)]
        ldy = [C.sb("ldy%d" % i, [128, 1024]) for i in range(2)]
        sq, bsq = C.sb("sq", [128, 1024])
        hn, bhn = C.sb("hn", [128, 1024], BF16)
        ss, bss = C.sb("ss", [128, 1])
        ss8, bss8 = C.sb("ss8", [128, 8])
        hT, bhT = C.sb("hT", [128, 8, NT], BF16)
        oT, boT = C.sb("oT", [128, 8, NT], BF16)
        yT, byT = C.sb("yT", [128, 8, NT], BF16)
        gz, bgz = C.sb("gz", [128, 8, NT], BF16)
        sg, bsg = C.sb("sg", [128, NT], BF16)
        catT, bcat = C.sb("catT", [128, 16, NT], BF16)
        yo, byo = C.sb("yo", [128, 1024])
        ptr, bptr = C.ps("ptr", [128, 1024], BF16)
        pmm = []; bpmm = []
        for i in range(4):
            t_, b_ = C.ps("pmm%d" % i, [128, 512]); pmm.append(t_); bpmm.append(b_)
        pso = []; bpso = []
        for i in range(2):
            t_, b_ = C.ps("pso%d" % i, [128, 512]); pso.append(t_); bpso.append(b_)
        bout = fz["obuf"] if fz else Buf("out", multi=True)
        if fz:
            sts = [(0, 128)] + [(128 + i * NT, NT) for i in range((ntok - 128) // NT)]
        else:
            sts = [(i * NT, NT) for i in range(ntok // NT)]
        for (t0, n) in sts:
            ntl = n // 128
            for t in range(ntl):
                r0 = t0 + t * 128
                P.dma("sp", xt4[:, t, :], x_d[r0:r0 + 128, :], writes=[bxt[t]])
                rms_rstd(C, xt4[:, t, :], bxt[t], 1024, sq[:], bsq, ss, bss)
                P.op("dve", lambda E, t=t: E.scalar_tensor_tensor(out=hn[:], in0=xt4[:, t, :], scalar=ss[:, 0:1], in1=npre[:],
                                                                  op0=ALU.mult, op1=ALU.mult), reads=[bxt[t], bss, bnpre], writes=[bhn])
                transpose8(C, hn, bhn, idb, bidb, ptr, bptr, hT[:, :, t * 128:(t + 1) * 128], bhT, eng="act")
                ld, bld = ldo[(r0 // 128) % 2]
                if fz:
                    fz["gather"](P, ld, bld, r0 // 128, 0)
                else:
                    P.dma("sp", ld[:], o_d[r0:r0 + 128, :], writes=[bld])
                P.op("act", lambda E, ld=ld: E.activation(out=sq[:], in_=ld[:], func=AF.Square), reads=[bld], writes=[bsq])
                P.op("dve", lambda E: E.tensor_reduce(out=ss8[:], in_=sq[:].rearrange("p (h d) -> p h d", h=8), axis=AX.X, op=ALU.add),
                     reads=[bsq], writes=[bss8])
                P.op("dve", lambda E: E.tensor_scalar(out=ss8[:], in0=ss8[:], scalar1=1.0 / 128, scalar2=1e-6, op0=ALU.mult, op1=ALU.add),
                     reads=[bss8], writes=[bss8])
                P.op("act", lambda E: E.activation(out=ss8[:], in_=ss8[:], func=AF.Sqrt), reads=[bss8], writes=[bss8])
                P.op("dve", lambda E: E.reciprocal(out=ss8[:], in_=ss8[:]), reads=[bss8], writes=[bss8])
                P.op("dve", lambda E, ld=ld: E.tensor_tensor(out=sq[:].rearrange("p (h d) -> p h d", h=8), in0=ld[:].rearrange("p (h d) -> p h d", h=8),
                                                      in1=ss8[:].unsqueeze(2).to_broadcast([128, 8, 128]), op=ALU.mult),
                     reads=[bld, bss8], writes=[bsq])
                P.op("dve", lambda E: E.tensor_tensor(out=hn[:].rearrange("p (h d) -> p h d", h=8), in0=sq[:].rearrange("p (h d) -> p h d", h=8),
                                                      in1=gnw[:].unsqueeze(1).to_broadcast([128, 8, 128]), op=ALU.mult),
                     reads=[bsq, bgnw], writes=[bhn])
                transpose8(C, hn, bhn, idb, bidb, ptr, bptr, oT[:, :, t * 128:(t + 1) * 128], boT, eng="act")
                ld, bld = ldy[(r0 // 128) % 2]
                if fz:
                    fz["gather"](P, ld, bld, r0 // 128, 1)
                else:
                    P.dma("sp", ld[:], ys_d[r0:r0 + 128, :], writes=[bld])
                P.op("act", lambda E, ld=ld: E.activation(out=hn[:], in_=ld[:], func=AF.Gelu_apprx_tanh), reads=[bld], writes=[bhn])
                transpose8(C, hn, bhn, idb, bidb, ptr, bptr, yT[:, :, t * 128:(t + 1) * 128], byT, eng="dve")
            for ct in range(16):
                pb = pmm[ct % 4]; bpb = bpmm[ct % 4]
                fns = [(lambda E, kt=kt, ct=ct, pb=pb, n=n: E.matmul(pb[:, 0:n], lhsT=wz[:, kt, ct * 128:(ct + 1) * 128], rhs=hT[:, kt, 0:n],
                                                                start=(kt == 0), stop=(kt == 7))) for kt in range(8)]
                P.mm_group(fns, reads=[bwz, bhT], writes=[bpb])
                if ct < 8:
                    P.op("act", lambda E, pb=pb, n=n: E.activation(out=sg[:, 0:n], in_=pb[:, 0:n], func=AF.Silu), reads=[bpb], writes=[bsg])
                    P.op("dve", lambda E, ct=ct, n=n: E.tensor_tensor(out=catT[:, ct, 0:n], in0=oT[:, ct, 0:n], in1=sg[:, 0:n], op=ALU.mult),
                         reads=[boT, bsg], writes=[bcat])
                else:
                    P.op("act", lambda E, pb=pb, ct=ct, n=n: E.activation(out=gz[:, ct - 8, 0:n], in_=pb[:, 0:n], func=AF.Silu), reads=[bpb], writes=[bgz])
            for ct in range(8):
                pb = pmm[ct % 4]; bpb = bpmm[ct % 4]
                fns = [(lambda E, kt=kt, ct=ct, pb=pb, n=n: E.matmul(pb[:, 0:n], lhsT=wglu[:, kt, ct * 128:(ct + 1) * 128], rhs=yT[:, kt, 0:n],
                                                                start=(kt == 0), stop=(kt == 7))) for kt in range(8)]
                P.mm_group(fns, reads=[bwglu, byT], writes=[bpb])
                P.op("act", lambda E, pb=pb, n=n: E.activation(out=sg[:, 0:n], in_=pb[:, 0:n], func=AF.Sigmoid), reads=[bpb], writes=[bsg])
                P.op("dve", lambda E, ct=ct, n=n: E.tensor_tensor(out=sg[:, 0:n], in0=sg[:, 0:n], in1=yT[:, ct, 0:n], op=ALU.mult), reads=[bsg, byT], writes=[bsg])
                P.op("dve", lambda E, ct=ct, n=n: E.tensor_tensor(out=catT[:, 8 + ct, 0:n], in0=sg[:, 0:n], in1=gz[:, ct, 0:n], op=ALU.mult),
                     reads=[bsg, bgz], writes=[bcat])
            for t in range(ntl):
                r0 = t0 + t * 128
                outproj_post(C, catT, bcat, 16, wout, bwout, t, xt4[:, t, :], bxt[t], npost, bnpost, pso, bpso, yo, byo, sq, bsq, ss, bss,
                             out_d[r0:r0 + 128, :], bout)
        if fz:
            barrier(P)
        else:
            P.finish([bout])
    return nc


def build_L3(ntok=2048, fz=None):
    nc = fz["nc"] if fz else bass.Bass("TRN2", target_bir_lowering=False)
    pfx = fz["pfx"] if fz else ""

    def D(name, shape):
        if fz and name in fz["share"]:
            return fz["share"][name]
        return nc.dram_tensor(pfx + name, shape, F32, kind="ExternalInput").ap()
    x_d = D("x", [ntok + 128, 1024])
    win_d = D("win", [1024, 8192]); wout_d = D("wout", [2048, 1024]); conv_d = D("conv", [3, 2048])
    npre_d = D("npre", [1024]); npost_d = D("npost", [1024]); ident_d = D("ident", [128, 128])
    out_d = fz["out"] if fz else nc.dram_tensor("out", [ntok, 1024], F32, kind="ExternalOutput").ap()
    NT = 256
    with ExitStack() as st:
        C = Ctx(nc, st, fz["P"], pfx) if fz else Ctx(nc, st); P = C.P
        idf, bidf, idb, bidb = make_ident(C, ident_d)
        npre, bnpre = bcast_row_load(C, "npre", npre_d, 1024)
        npost, bnpost = bcast_row_load(C, "npost", npost_d, 1024)
        cw, bcw = C.sb("cw", [128, 3, 16])
        P.dma("sp", cw[:], conv_d.rearrange("j (c p) -> p j c", p=128), writes=[bcw])
        win, bwin = load_w_bf16(C, "win", win_d, 8, 8192)
        wout, bwout = load_w_bf16(C, "wout", wout_d, 16, 1024)
        xt, bxt = C.sb("xt", [128, 1024])
        sq, bsq = C.sb("sq", [128, 1024])
        hn, bhn = C.sb("hn", [128, 1024], BF16)
        ss, bss = C.sb("ss", [128, 1])
        hT, bhT = C.sb("hT", [128, 8, NT], BF16)
        y1T, by1T = C.sb("y1T", [128, 16, NT], BF16)
        pbuf, bpbuf = C.sb("pbuf", [128, NT + 2])
        phalo, bphalo = C.sb("phalo", [128, 16, 2])
        gcs, bgcs = C.sb("gcs", [128, NT])
        cv, bcv = C.sb("cv", [128, NT])
        sz, bsz = C.sb("sz", [128, NT])
        yo, byo = C.sb("yo", [128, 1024])
        P.op("dve", lambda E: E.memset(phalo[:], 0.0), writes=[bphalo])
        ptr, bptr = C.ps("ptr", [128, 1024], BF16)
        GB = [C.ps("g%d" % i, [128, 512]) for i in range(7)]
        pso = [GB[0][0], GB[1][0]]; bpso = [GB[0][1], GB[1][1]]
        bout = fz["obuf"] if fz else Buf("out", multi=True)
        sts = [(0, 128)] + [(128 + i * NT, NT) for i in range(ntok // NT)]
        for (t0, n) in sts:
            ntl = n // 128
            for t in range(ntl):
                r0 = t0 + t * 128
                P.dma("sp", xt[:], x_d[r0:r0 + 128, :], reads=([fz["xbuf"]] if fz else []), writes=[bxt])
                rms_rstd(C, xt[:], bxt, 1024, sq[:], bsq, ss, bss)
                P.op("dve", lambda E: E.scalar_tensor_tensor(out=hn[:], in0=xt[:], scalar=ss[:, 0:1], in1=npre[:],
                                                             op0=ALU.mult, op1=ALU.mult), reads=[bxt, bss, bnpre], writes=[bhn])
                transpose8(C, hn, bhn, idb, bidb, ptr, bptr, hT[:, :, t * 128:(t + 1) * 128], bhT, eng="act")
            for ct in range(16):
                sel_ = [GB[3 * (ct % 2) + 0], GB[3 * (ct % 2) + 1], GB[3 * (ct % 2) + 2], GB[6]]
                pmm = [x_[0] for x_ in sel_]; bpmm = [x_[1] for x_ in sel_]
                for part in range(4):
                    col0 = (part * 16 + ct) * 128
                    pb = pmm[part]
                    fns = [(lambda E, n=n, kt=kt, col0=col0, pb=pb: E.matmul(pb[:, 0:n], lhsT=win[:, kt, col0:col0 + 128], rhs=hT[:, kt, 0:n],
                                                                        start=(kt == 0), stop=(kt == 7))) for kt in range(8)]
                    P.mm_group(fns, reads=[bwin, bhT], writes=[bpmm[part]])
                P.op("act", lambda E, n=n, pmm=pmm: E.copy(out=gcs[:, 0:n], in_=pmm[1][:, 0:n]), reads=[bpmm[1]], writes=[bgcs])
                P.op("act", lambda E, ct=ct: E.copy(out=pbuf[:, 0:2], in_=phalo[:, ct, :]), reads=[bphalo], writes=[bpbuf])
                P.op("dve", lambda E, n=n, pmm=pmm: E.tensor_tensor(out=pbuf[:, 2:2 + n], in0=gcs[:, 0:n], in1=pmm[2][:, 0:n], op=ALU.mult),
                     reads=[bgcs, bpmm[2]], writes=[bpbuf])
                P.op("act", lambda E, n=n, ct=ct: E.copy(out=phalo[:, ct, :], in_=pbuf[:, n:n + 2]), reads=[bpbuf], writes=[bphalo])
                if t0 == 0:
                    continue
                P.op("dve", lambda E, n=n, ct=ct: E.tensor_scalar(out=cv[:, 0:n], in0=pbuf[:, 0:n], scalar1=cw[:, 0, ct:ct + 1], scalar2=None, op0=ALU.mult),
                     reads=[bpbuf, bcw], writes=[bcv])
                P.op("dve", lambda E, n=n, ct=ct: E.scalar_tensor_tensor(out=cv[:, 0:n], in0=pbuf[:, 1:1 + n], scalar=cw[:, 1, ct:ct + 1], in1=cv[:, 0:n],
                                                                    op0=ALU.mult, op1=ALU.add), reads=[bpbuf, bcw, bcv], writes=[bcv])
                P.op("dve", lambda E, n=n, ct=ct: E.scalar_tensor_tensor(out=cv[:, 0:n], in0=pbuf[:, 2:2 + n], scalar=cw[:, 2, ct:ct + 1], in1=cv[:, 0:n],
                                                                    op0=ALU.mult, op1=ALU.add), reads=[bpbuf, bcw, bcv], writes=[bcv])
                P.op("dve", lambda E, n=n, pmm=pmm: E.tensor_tensor(out=cv[:, 0:n], in0=cv[:, 0:n], in1=pmm[0][:, 0:n], op=ALU.mult), reads=[bcv, bpmm[0]], writes=[bcv])
                P.op("act", lambda E, n=n, pmm=pmm: E.activation(out=sz[:, 0:n], in_=pmm[3][:, 0:n], func=AF.Silu), reads=[bpmm[3]], writes=[bsz])
                P.op("dve", lambda E, n=n, ct=ct: E.tensor_tensor(out=y1T[:, ct, 0:n], in0=cv[:, 0:n], in1=sz[:, 0:n], op=ALU.mult),
                     reads=[bcv, bsz], writes=[by1T])
            if t0 == 0:
                continue
            for t in range(ntl):
                r0 = t0 + t * 128
                P.dma("sp", xt[:], x_d[r0:r0 + 128, :], reads=([fz["xbuf"]] if fz else []), writes=[bxt])
                outproj_post(C, y1T, by1T, 16, wout, bwout, t, xt[:], bxt, npost, bnpost, pso, bpso, yo, byo, sq, bsq, ss, bss,
                             out_d[r0 - 128:r0, :], bout)
        if fz:
            barrier(P)
        else:
            P.finish([bout])
    return nc


_IDENT = np.eye(128, dtype=np.float32)
_CACHE = {}


def _get(name, fn):
    if name not in _CACHE:
        _CACHE[name] = fn()
    return _CACHE[name]


def run_L2(inp, o_full, ys_full):
    nc = _get("L2", build_L2)
    w_in = inp["w_in_even"][0]
    wz = np.ascontiguousarray(np.concatenate([w_in[:, 3072:4096], w_in[:, 5136:6160]], axis=1))
    maps = []
    for c in range(8):
        b, r = divmod(c, 4)
        sl = slice(r * 2048, (r + 1) * 2048)
        maps.append({"x": np.ascontiguousarray(inp["x"][b, sl]), "o": np.ascontiguousarray(o_full[b, sl]),
                     "ys": np.ascontiguousarray(ys_full[b, sl]), "wz": wz, "wglu": np.ascontiguousarray(inp["w_glu"][0]),
                     "wout": np.ascontiguousarray(inp["w_out_even"][0]), "npre": np.ascontiguousarray(inp["norm_pre"][0]),
                     "npost": np.ascontiguousarray(inp["norm_post"][0]), "gnw": np.ascontiguousarray(inp["gdn_norm_w"][0]),
                     "ident": _IDENT})
    res = run_bass_kernel_spmd(nc, maps, core_ids=list(range(8)))
    x1 = np.empty((2, 8192, 1024), np.float32)
    for c in range(8):
        b, r = divmod(c, 4)
        x1[b, r * 2048:(r + 1) * 2048] = res.results[c]["out"]
    return x1


def run_L3(inp, x1):
    nc = _get("L3", build_L3)
    maps = []
    for c in range(8):
        b, r = divmod(c, 4)
        xh = np.zeros((2048 + 128, 1024), np.float32)
        xh[128:] = x1[b, r * 2048:(r + 1) * 2048]
        if r > 0:
            xh[:128] = x1[b, r * 2048 - 128:r * 2048]
        maps.append({"x": xh, "win": np.ascontiguousarray(inp["w_in_odd"][0]), "wout": np.ascontiguousarray(inp["w_out_odd"][0]),
                     "conv": np.ascontiguousarray(inp["conv_short"][0]), "npre": np.ascontiguousarray(inp["norm_pre"][1]),
                     "npost": np.ascontiguousarray(inp["norm_post"][1]), "ident": _IDENT})
    res = run_bass_kernel_spmd(nc, maps, core_ids=list(range(8)))
    out = np.empty((2, 8192, 1024), np.float32)
    for c in range(8):
        b, r = divmod(c, 4)
        out[b, r * 2048:(r + 1) * 2048] = res.results[c]["out"]
    return out


I32 = mybir.dt.int32
TAUS = np.array(list(range(17)) + [32, 64, 128, 256, 512, 1024, 2048, 4096] + list(range(15, -1, -1)), np.float32)
NTAU = len(TAUS)


def _s5_consts():
    mk = np.zeros((128, 2, 16, 16), np.float32)
    idm = np.zeros((128, 2, 16, 16), np.float32)
    for kt2 in range(2):
        for sp in range(8):
            s = kt2 * 8 + sp
            for h in range(16):
                mk[sp * 16 + h, kt2, s:, :] = 1.0
                idm[sp * 16 + h, kt2, s, h] = 1.0
    return mk.reshape(128, 2, 256), idm.reshape(128, 2, 256)


def barrier(P):
    for e in P.ENG:
        for e2 in P.ENG:
            if e2 != e and P.cnt[e2] > 0:
                P._wait(e, ("c", e2, P.cnt[e2]))
        for q in P.dsem:
            j1 = P.dcnt[q]
            for j in range(max(0, j1 - NDS), j1):
                P._wait(e, ("d", q, j % NDS, 16 * (j // NDS + 1)))


def build_L1b(S=8192, fz=None):
    nc = fz["nc"] if fz else bass.Bass("TRN2", target_bir_lowering=False)
    pfx = fz["pfx"] if fz else ""

    def D(name, shape):
        if fz and name in fz["share"]:
            return fz["share"][name]
        return nc.dram_tensor(pfx + name, shape, F32, kind="ExternalInput").ap()
    x_d = D("x", [S, 1024]); npre_d = D("npre", [1024]); wu_d = D("wu", [1024, 256])
    lre_d = D("lre", [16, 64]); lim_d = D("lim", [16, 64]); bre_d = D("bre", [16, 64, 16]); bim_d = D("bim", [16, 64, 16])
    cre_d = D("cre", [16, 16, 64]); cim_d = D("cim", [16, 16, 64]); ldt_d = D("ldt", [16]); dd_d = D("dd", [256])
    taus_d = D("taus", [NTAU]); mk_d = D("mk", [128, 2, 256]); idm_d = D("idm", [128, 2, 256]); ident_d = D("ident", [128, 128])
    ys_d = fz["out"] if fz else nc.dram_tensor("ys", [S, 256], F32, kind="ExternalOutput").ap()
    NCH = S // 16
    NST = S // 512
    with ExitStack() as st:
        C = Ctx(nc, st, fz["P"], pfx) if fz else Ctx(nc, st); P = C.P
        idf, bidf, idb, bidb = make_ident(C, ident_d)
        ptr, bptr = C.ps("ptr", [128, 1024], BF16)
        py, bpy = C.ps("py", [128, 1024])
        G = []; bG = []
        for i in range(4):
            t_, b_ = C.ps("g%d" % i, [128, 512]); G.append(t_); bG.append(b_)
        U, bU = C.sb("U", [128, 2, 16, NCH], BF16)
        with ExitStack() as st2:
            C2 = Ctx(nc, st2, P, C.pfx)
            ext = fz.get("uTp") if fz else None
            if ext:
                uTp, buTp = ext
            else:
                uTp, buTp = C2.sb("uTp", [128, 2, 16, NCH], BF16)
            with ExitStack() as st1:
                C1 = Ctx(nc, st1, P, C.pfx)
                npre, bnpre = bcast_row_load(C1, "npre", npre_d, 1024)
                wu, bwu = load_w_bf16(C1, "wu", wu_d, 8, 256)
                xt, bxt = C1.sb("xt", [128, 1024])
                sq, bsq = C1.sb("sq", [128, 1024])
                hn, bhn = C1.sb("hn", [128, 1024], BF16)
                ss, bss = C1.sb("ss", [128, 1])
                hT, bhT = C1.sb("hT", [128, 8, 512], BF16)
                for s_ in range(0 if ext else NST):
                    for t in range(4):
                        r0 = s_ * 512 + t * 128
                        P.dma("sp", xt[:], x_d[r0:r0 + 128, :], writes=[bxt])
                        rms_rstd(C1, xt[:], bxt, 1024, sq[:], bsq, ss, bss)
                        P.op("dve", lambda E: E.scalar_tensor_tensor(out=hn[:], in0=xt[:], scalar=ss[:, 0:1], in1=npre[:],
                                                                     op0=ALU.mult, op1=ALU.mult), reads=[bxt, bss, bnpre], writes=[bhn])
                        transpose8(C1, hn, bhn, idb, bidb, ptr, bptr, hT[:, :, t * 128:(t + 1) * 128], bhT, eng="act")
                    for blk in range(2):
                        pb = G[blk]
                        fns = [(lambda E, kt=kt, blk=blk, pb=pb: E.matmul(
                            pb[:].rearrange("p (s n) -> p s n", s=16), lhsT=wu[:, kt, blk * 128:(blk + 1) * 128],
                            rhs=hT[:, kt, :].rearrange("p (n s) -> p s n", s=16), start=(kt == 0), stop=(kt == 7))) for kt in range(8)]
                        P.mm_group(fns, reads=[bwu, bhT], writes=[bG[blk]])
                        P.op("act" if blk == 0 else "dve",
                             (lambda E, blk=blk, pb=pb, s_=s_: E.copy(out=uTp[:, blk, :, 32 * s_:32 * s_ + 32], in_=pb[:].rearrange("p (s n) -> p s n", s=16)))
                             if blk == 0 else
                             (lambda E, blk=blk, pb=pb, s_=s_: E.tensor_copy(out=uTp[:, blk, :, 32 * s_:32 * s_ + 32], in_=pb[:].rearrange("p (s n) -> p s n", s=16))),
                             reads=[bG[blk]], writes=[buTp])
                barrier(P)
            for g in range(16):
                for kt2 in range(2):
                    for sp in range(8):
                        P.dma("sp", U[sp * 16:(sp + 1) * 16, kt2, g, :],
                              uTp[(g % 8) * 16:(g % 8 + 1) * 16, g // 8, kt2 * 8 + sp, :], reads=[buTp], writes=[bU])
            barrier(P)
        lre, blre = C.sb("lre", [128, 8]); lim, blim = C.sb("lim", [128, 8]); ldt, bldt = C.sb("ldt", [128, 8])
        TAU, bTAU = bcast_row_load(C, "TAU", taus_d, NTAU)
        Er, bEr = C.sb("Er", [128, 8, NTAU]); Ei, bEi = C.sb("Ei", [128, 8, NTAU]); NEi, bNEi = C.sb("NEi", [128, 8, NTAU])
        Hr, bHr = C.sb("Hr", [128, 8, 17, 16]); nHi, bnHi = C.sb("nHi", [128, 8, 17, 16])
        WbT, bWbT = C.sb("WbT", [128, 2, 8, 2, 128], BF16)
        Toep, bToep = C.sb("Toep", [128, 2, 16, 256], BF16)
        with ExitStack() as st3:
            C3 = Ctx(nc, st3, P, C.pfx)
            Br, bBr = C3.sb("Br", [128, 8, 16]); Bi, bBi = C3.sb("Bi", [128, 8, 16])
            Cr, bCr = C3.sb("Cr", [128, 8, 16]); Ci, bCi = C3.sb("Ci", [128, 8, 16])
            dcol, bdcol = C3.sb("dcol", [128, 16])
            MK, bMK = C3.sb("MK", [128, 2, 256]); IDM, bIDM = C3.sb("IDM", [128, 2, 256])
            P.dma("sp", MK[:], mk_d, writes=[bMK]); P.dma("sp", IDM[:], idm_d, writes=[bIDM])
            for two in range(2):
                hs = slice(64 * two, 64 * two + 64)
                P.dma("sp", lre[hs, :], lre_d.rearrange("(gp two) p -> two p gp", two=2)[two], writes=[blre])
                P.dma("sp", lim[hs, :], lim_d.rearrange("(gp two) p -> two p gp", two=2)[two], writes=[blim])
                P.dma("sp", ldt[hs, :], ldt_d.rearrange("(gp two) -> two gp", two=2)[two].partition_broadcast(64), writes=[bldt])
                P.dma("sp", Br[hs], bre_d.rearrange("(gp two) p h -> two p gp h", two=2)[two], writes=[bBr])
                P.dma("sp", Bi[hs], bim_d.rearrange("(gp two) p h -> two p gp h", two=2)[two], writes=[bBi])
                for gp in range(8):
                    P.dma("sp", Cr[hs, gp, :], cre_d[2 * gp + two].rearrange("h p -> p h"), writes=[bCr])
                    P.dma("sp", Ci[hs, gp, :], cim_d[2 * gp + two].rearrange("h p -> p h"), writes=[bCi])
            for sp in range(8):
                P.dma("sp", dcol[sp * 16:(sp + 1) * 16, :], dd_d.rearrange("(g h) -> h g", h=16), writes=[bdcol])
            sm = {}
            for nm in ("dt", "lr", "lrdt", "th", "den", "nr", "fre", "fim", "t8a", "t8b"):
                sm[nm] = C3.sb("sm_" + nm, [128, 8])
            T41 = {}
            for nm in ("ARG", "MARG", "MAG", "MAGN", "SIN", "COS", "ErN", "EiN", "rt", "rk"):
                T41[nm] = C3.sb("t41_" + nm, [128, 8, NTAU])
            rki, brki = C3.sb("rki", [128, 8, NTAU], I32)

            def tt(eng, out, bo, a, ba, b, bb_, op):
                P.op(eng, lambda E: E.tensor_tensor(out=out, in0=a, in1=b, op=op), reads=[ba, bb_], writes=[bo])

            dt, bdt = sm["dt"]; lr, blr = sm["lr"]; lrdt, blrdt = sm["lrdt"]; th, bth = sm["th"]
            P.op("act", lambda E: E.activation(out=dt[:], in_=ldt[:], func=AF.Exp), reads=[bldt], writes=[bdt])
            P.op("dve", lambda E: E.tensor_scalar(out=lr[:], in0=lre[:], scalar1=-1e-4, scalar2=None, op0=ALU.min), reads=[blre], writes=[blr])
            tt("dve", lrdt[:], blrdt, lr[:], blr, dt[:], bdt, ALU.mult)
            tt("dve", th[:], bth, lim[:], blim, dt[:], bdt, ALU.mult)
            ARG, bARG = T41["ARG"]; MARG, bMARG = T41["MARG"]; MAG, bMAG = T41["MAG"]; MAGN, bMAGN = T41["MAGN"]
            SIN, bSIN = T41["SIN"]; COS, bCOS = T41["COS"]; ErN, bErN = T41["ErN"]; EiN, bEiN = T41["EiN"]
            rt, brt = T41["rt"]; rk, brk = T41["rk"]
            tb = TAU[:].unsqueeze(1).to_broadcast([128, 8, NTAU])
            tt("dve", ARG[:], bARG, th[:].unsqueeze(2).to_broadcast([128, 8, NTAU]), bth, tb, bTAU, ALU.mult)
            tt("dve", MARG[:], bMARG, lrdt[:].unsqueeze(2).to_broadcast([128, 8, NTAU]), blrdt, tb, bTAU, ALU.mult)
            P.op("act", lambda E: E.activation(out=MAG[:], in_=MARG[:], func=AF.Exp), reads=[bMARG], writes=[bMAG])
            P.op("act", lambda E: E.activation(out=MAGN[:], in_=MARG[:], func=AF.Exp, scale=-1.0), reads=[bMARG], writes=[bMAGN])

            def sin_of(dst, bdst, shift):
                P.op("dve", lambda E: E.tensor_scalar(out=rt[:], in0=ARG[:], scalar1=float(shift), scalar2=None, op0=ALU.add), reads=[bARG], writes=[brt])
                P.op("dve", lambda E: E.tensor_scalar(out=rki[:], in0=rt[:], scalar1=float(1.0 / (2 * np.pi)), scalar2=None, op0=ALU.mult), reads=[brt], writes=[brki])
                P.op("dve", lambda E: E.tensor_copy(out=rk[:], in_=rki[:]), reads=[brki], writes=[brk])
                P.op("dve", lambda E: E.scalar_tensor_tensor(out=rt[:], in0=rk[:], scalar=float(-2 * np.pi), in1=rt[:], op0=ALU.mult, op1=ALU.add),
                     reads=[brk, brt], writes=[brt])
                P.op("dve", lambda E: E.tensor_scalar(out=rt[:], in0=rt[:], scalar1=-3.14159, scalar2=3.14159, op0=ALU.max, op1=ALU.min), reads=[brt], writes=[brt])
                P.op("act", lambda E: E.activation(out=dst[:], in_=rt[:], func=AF.Sin), reads=[brt], writes=[bdst])

            sin_of(SIN, bSIN, 0.0)
            sin_of(COS, bCOS, np.pi / 2)
            tt("dve", Er[:], bEr, MAG[:], bMAG, COS[:], bCOS, ALU.mult)
            tt("dve", Ei[:], bEi, MAG[:], bMAG, SIN[:], bSIN, ALU.mult)
            P.op("dve", lambda E: E.tensor_scalar(out=NEi[:], in0=Ei[:], scalar1=-1.0, scalar2=None, op0=ALU.mult), reads=[bEi], writes=[bNEi])
            tt("dve", ErN[:], bErN, MAGN[:], bMAGN, COS[:], bCOS, ALU.mult)
            tt("dve", EiN[:], bEiN, MAGN[:], bMAGN, SIN[:], bSIN, ALU.mult)
            P.op("dve", lambda E: E.tensor_scalar(out=EiN[:], in0=EiN[:], scalar1=-1.0, scalar2=None, op0=ALU.mult), reads=[bEiN], writes=[bEiN])
            den, bden = sm["den"]; nr, bnr = sm["nr"]; fre, bfre = sm["fre"]; fim, bfim = sm["fim"]; t8a, bt8a = sm["t8a"]; t8b, bt8b = sm["t8b"]
            tt("dve", den[:], bden, lr[:], blr, lr[:], blr, ALU.mult)
            tt("dve", t8a[:], bt8a, lim[:], blim, lim[:], blim, ALU.mult)
            tt("dve", den[:], bden, den[:], bden, t8a[:], bt8a, ALU.add)
            P.op("dve", lambda E: E.reciprocal(out=den[:], in_=den[:]), reads=[bden], writes=[bden])
            P.op("dve", lambda E: E.tensor_scalar(out=nr[:], in0=Er[:, :, 1], scalar1=-1.0, scalar2=None, op0=ALU.add), reads=[bEr], writes=[bnr])
            tt("dve", fre[:], bfre, nr[:], bnr, lr[:], blr, ALU.mult)
            tt("dve", t8a[:], bt8a, Ei[:, :, 1], bEi, lim[:], blim, ALU.mult)
            tt("dve", fre[:], bfre, fre[:], bfre, t8a[:], bt8a, ALU.add)
            tt("dve", fre[:], bfre, fre[:], bfre, den[:], bden, ALU.mult)
            tt("dve", fim[:], bfim, Ei[:, :, 1], bEi, lr[:], blr, ALU.mult)
            tt("dve", t8b[:], bt8b, nr[:], bnr, lim[:], blim, ALU.mult)
            tt("dve", fim[:], bfim, fim[:], bfim, t8b[:], bt8b, ALU.subtract)
            tt("dve", fim[:], bfim, fim[:], bfim, den[:], bden, ALU.mult)

            def cmul(outr, boutr, outi, bouti, ar, bar, ai, bai, br_, bbr_, bi_, bbi_, tmp, btmp):
                tt("dve", outr, boutr, ar, bar, br_, bbr_, ALU.mult)
                tt("dve", tmp, btmp, ai, bai, bi_, bbi_, ALU.mult)
                tt("dve", outr, boutr, outr, boutr, tmp, btmp, ALU.subtract)
                tt("dve", outi, bouti, ar, bar, bi_, bbi_, ALU.mult)
                tt("dve", tmp, btmp, ai, bai, br_, bbr_, ALU.mult)
                tt("dve", outi, bouti, outi, bouti, tmp, btmp, ALU.add)

            bbr, bbbr = C3.sb("bbr", [128, 8, 16]); bbi, bbbi = C3.sb("bbi", [128, 8, 16]); tmp16, btmp16 = C3.sb("tmp16", [128, 8, 16])
            fb = lambda t_: t_[:].unsqueeze(2).to_broadcast([128, 8, 16])
            cmul(bbr[:], bbbr, bbi[:], bbbi, fb(fre), bfre, fb(fim), bfim, Br[:], bBr, Bi[:], bBi, tmp16[:], btmp16)
            Gr, bGr = C3.sb("Gr", [128, 8, 16, 16]); Gi, bGi = C3.sb("Gi", [128, 8, 16, 16])
            WPr, bWPr = C3.sb("WPr", [128, 8, 16, 16]); WPi, bWPi = C3.sb("WPi", [128, 8, 16, 16])
            Hi, bHi = C3.sb("Hi", [128, 8, 17, 16]); tmpH, btmpH = C3.sb("tmpH", [128, 8, 17, 16])
            eb = lambda t_, j0, j1: t_[:, :, j0:j1].unsqueeze(3).to_broadcast([128, 8, j1 - j0, 16])
            vb = lambda t_, n_: t_[:].unsqueeze(2).to_broadcast([128, 8, n_, 16])
            cmul(Gr[:], bGr, Gi[:], bGi, eb(ErN, 0, 16), bErN, eb(EiN, 0, 16), bEiN, vb(bbr, 16), bbbr, vb(bbi, 16), bbbi, tmpH[:, :, 0:16, :], btmpH)
            cmul(WPr[:], bWPr, WPi[:], bWPi, eb(Er, 25, 41), bEr, eb(Ei, 25, 41), bEi, vb(bbr, 16), bbbr, vb(bbi, 16), bbbi, tmpH[:, :, 0:16, :], btmpH)
            cmul(Hr[:], bHr, Hi[:], bHi, eb(Er, 0, 17), bEr, eb(Ei, 0, 17), bEi, vb(Cr, 17), bCr, vb(Ci, 17), bCi, tmpH[:], btmpH)
            P.op("dve", lambda E: E.tensor_scalar(out=nHi[:], in0=Hi[:], scalar1=-1.0, scalar2=None, op0=ALU.mult), reads=[bHi], writes=[bnHi])
            for gp in range(8):
                for kt2 in range(2):
                    for c, (WP_, bWP_) in enumerate(((WPr, bWPr), (WPi, bWPi))):
                        P.op("pe", lambda E, gp=gp, kt2=kt2, WP_=WP_: E.transpose(
                            out=G[2][:, 0:128], in_=WP_[:, gp, kt2 * 8:(kt2 + 1) * 8, :].rearrange("p s h -> p (s h)"), identity=idf[:]),
                            reads=[bWP_, bidf], writes=[bG[2]])
                        P.op("act", lambda E, gp=gp, kt2=kt2, c=c: E.copy(out=WbT[:, kt2, gp, c, :], in_=G[2][:, 0:128]), reads=[bG[2]], writes=[bWbT])
            tmpT, btmpT = C3.sb("tmpT", [128, 256])
            for g in range(16):
                gp = g // 2; hs = slice(64 * (g % 2), 64 * (g % 2) + 64)
                for kt2 in range(2):
                    fns = [
                        lambda E, gp=gp, hs=hs, kt2=kt2: E.matmul(G[3][:, 0:256], lhsT=Gr[hs, gp, kt2 * 8:(kt2 + 1) * 8, :].rearrange("p s h -> p (s h)"),
                                                                  rhs=Hr[hs, gp, 0:16, :].rearrange("p t h -> p (t h)"), start=True, stop=False),
                        lambda E, gp=gp, hs=hs, kt2=kt2: E.matmul(G[3][:, 0:256], lhsT=Gi[hs, gp, kt2 * 8:(kt2 + 1) * 8, :].rearrange("p s h -> p (s h)"),
                                                                  rhs=nHi[hs, gp, 0:16, :].rearrange("p t h -> p (t h)"), start=False, stop=True)]
                    P.mm_group(fns, reads=[bGr, bGi, bHr, bnHi], writes=[bG[3]])
                    P.op("dve", lambda E, kt2=kt2: E.tensor_tensor(out=tmpT[:], in0=G[3][:, 0:256], in1=MK[:, kt2, :], op=ALU.mult),
                         reads=[bG[3], bMK], writes=[btmpT])
                    P.op("dve", lambda E, kt2=kt2, g=g: E.scalar_tensor_tensor(out=Toep[:, kt2, g, :], in0=IDM[:, kt2, :], scalar=dcol[:, g:g + 1], in1=tmpT[:],
                                                                               op0=ALU.mult, op1=ALU.add), reads=[bIDM, bdcol, btmpT], writes=[bToep])
            barrier(P)
        X = {}
        for bufn in ("A", "B"):
            for c in ("re", "im"):
                X[(bufn, c)] = (C.sb("X%s%s" % (bufn, c), [128, 8, NCH + 1])[0], [Buf("X%s%s%d" % (bufn, c, gp)) for gp in range(8)])
        Ysb, bYsb = C.sb("Ysb", [128, 16, 256])
        for key in X:
            t_, bl = X[key]
            P.op("dve", lambda E, t_=t_: E.memset(t_[:, :, 0:1], 0.0), writes=bl)
        for gp in range(8):
            for c, cn in enumerate(("re", "im")):
                px = G[c]
                fns = []
                for two in range(2):
                    g = 2 * gp + two
                    for kt2 in range(2):
                        fns.append(lambda E, two=two, g=g, kt2=kt2, gp=gp, c=c, px=px: E.matmul(
                            px[64 * two:64 * two + 64, :], lhsT=WbT[:, kt2, gp, c, 64 * two:64 * two + 64], rhs=U[:, kt2, g, :],
                            start=(kt2 == 0), stop=(kt2 == 1)))
                P.mm_group(fns, reads=[bWbT, bU], writes=[bG[c]])
                xt_, xb_ = X[("A", cn)]
                P.op("act", lambda E, xt_=xt_, gp=gp, px=px: E.copy(out=xt_[:, gp, 1:NCH + 1], in_=px[:]), reads=[bG[c]], writes=[xb_[gp]])
        for k in range(9):
            d = 1 << k
            j = 16 if k == 0 else 16 + k
            src, dst = ("A", "B") if k % 2 == 0 else ("B", "A")
            sre, bsre = X[(src, "re")]; sim, bsim = X[(src, "im")]
            dre, bdre = X[(dst, "re")]; dim_, bdim = X[(dst, "im")]
            P.op("dve", lambda E, dre=dre, sre=sre, d=d: E.tensor_copy(out=dre[:, :, 1:1 + d], in_=sre[:, :, 1:1 + d]), reads=bsre, writes=bdre)
            P.op("pool", lambda E, dim_=dim_, sim=sim, d=d: E.tensor_copy(out=dim_[:, :, 1:1 + d], in_=sim[:, :, 1:1 + d]), reads=bsim, writes=bdim)
            for gp in range(8):
                lo = slice(1, NCH + 1 - d); hi = slice(1 + d, NCH + 1)
                P.op("dve", lambda E, gp=gp, j=j, dre=dre, sre=sre, lo=lo, hi=hi: E.scalar_tensor_tensor(
                    out=dre[:, gp, hi], in0=sre[:, gp, lo], scalar=Er[:, gp, j:j + 1], in1=sre[:, gp, hi], op0=ALU.mult, op1=ALU.add),
                    reads=[bsre[gp], bEr], writes=[bdre[gp]])
                P.op("dve", lambda E, gp=gp, j=j, dre=dre, sim=sim, lo=lo, hi=hi: E.scalar_tensor_tensor(
                    out=dre[:, gp, hi], in0=sim[:, gp, lo], scalar=NEi[:, gp, j:j + 1], in1=dre[:, gp, hi], op0=ALU.mult, op1=ALU.add),
                    reads=[bsim[gp], bNEi, bdre[gp]], writes=[bdre[gp]])
                P.op("dve", lambda E, gp=gp, j=j, dim_=dim_, sim=sim, lo=lo, hi=hi: E.scalar_tensor_tensor(
                    out=dim_[:, gp, hi], in0=sim[:, gp, lo], scalar=Er[:, gp, j:j + 1], in1=sim[:, gp, hi], op0=ALU.mult, op1=ALU.add),
                    reads=[bsim[gp], bEr], writes=[bdim[gp]])
                P.op("dve", lambda E, gp=gp, j=j, dim_=dim_, sre=sre, lo=lo, hi=hi: E.scalar_tensor_tensor(
                    out=dim_[:, gp, hi], in0=sre[:, gp, lo], scalar=Ei[:, gp, j:j + 1], in1=dim_[:, gp, hi], op0=ALU.mult, op1=ALU.add),
                    reads=[bsre[gp], bEi, bdim[gp]], writes=[bdim[gp]])
        fre_, bfre_ = X[("B", "re")]; fim_, bfim_ = X[("B", "im")]
        bys = None if fz else Buf("ys", multi=True)
        ysv = ys_d.rearrange("(n t) c -> n t c", t=16)
        for jt in range(NCH // 128):
            for gq in range(4):
                fns = []
                for gi in range(4):
                    g = 4 * gq + gi; gp = g // 2; hs = slice(64 * (g % 2), 64 * (g % 2) + 64)
                    o_ = (gi * 256, (gi + 1) * 256)
                    for kt2 in range(2):
                        fns.append(lambda E, o_=o_, g=g, kt2=kt2, jt=jt: E.matmul(
                            py[:, o_[0]:o_[1]], lhsT=U[:, kt2, g, jt * 128:(jt + 1) * 128], rhs=Toep[:, kt2, g, :], start=(kt2 == 0), stop=False))
                    fns.append(lambda E, o_=o_, gp=gp, hs=hs, jt=jt: E.matmul(
                        py[:, o_[0]:o_[1]], lhsT=fre_[hs, gp, jt * 128:(jt + 1) * 128], rhs=Hr[hs, gp, 1:17, :].rearrange("p t h -> p (t h)"),
                        start=False, stop=False))
                    fns.append(lambda E, o_=o_, gp=gp, hs=hs, jt=jt: E.matmul(
                        py[:, o_[0]:o_[1]], lhsT=fim_[hs, gp, jt * 128:(jt + 1) * 128], rhs=nHi[hs, gp, 1:17, :].rearrange("p t h -> p (t h)"),
                        start=False, stop=True))
                P.mm_group(fns, reads=[bU, bToep, bHr, bnHi] + bfre_ + bfim_, writes=[bpy])
                P.op("act" if gq % 2 == 0 else "dve",
                     (lambda E, gq=gq: E.copy(out=Ysb[:].rearrange("p t (g h) -> p g t h", h=16)[:, 4 * gq:4 * gq + 4],
                                              in_=py[:].rearrange("p (g t h) -> p g t h", g=4, h=16)))
                     if gq % 2 == 0 else
                     (lambda E, gq=gq: E.tensor_copy(out=Ysb[:].rearrange("p t (g h) -> p g t h", h=16)[:, 4 * gq:4 * gq + 4],
                                                     in_=py[:].rearrange("p (g t h) -> p g t h", g=4, h=16))),
                     reads=[bpy], writes=[bYsb])
            P.dma("sp", ysv[jt * 128:(jt + 1) * 128, :, :], Ysb[:], reads=[bYsb], writes=[fz["obuf_of"](jt) if fz else bys])
            if fz:
                fz["after_chunk"](jt)
        if fz:
            barrier(P)
        else:
            P.finish([bys])
    return nc


def run_L1b(inp):
    nc = _get("L1b", build_L1b)
    mk, idm = _s5_consts()
    w_in = inp["w_in_even"][0]
    maps = []
    for c in range(8):
        b, r = divmod(c, 4)
        gs = slice(16 * r, 16 * r + 16)
        maps.append({"x": np.ascontiguousarray(inp["x"][b]), "npre": np.ascontiguousarray(inp["norm_pre"][0]),
                     "wu": np.ascontiguousarray(w_in[:, 4112 + 256 * r:4112 + 256 * (r + 1)]),
                     "lre": np.ascontiguousarray(inp["s5_lam_re"][0, gs]), "lim": np.ascontiguousarray(inp["s5_lam_im"][0, gs]),
                     "bre": np.ascontiguousarray(inp["s5_b_re"][0, gs]), "bim": np.ascontiguousarray(inp["s5_b_im"][0, gs]),
                     "cre": np.ascontiguousarray(inp["s5_c_re"][0, gs]), "cim": np.ascontiguousarray(inp["s5_c_im"][0, gs]),
                     "ldt": np.ascontiguousarray(inp["s5_log_dt"][0, gs]), "dd": np.ascontiguousarray(inp["s5_d"][0, 256 * r:256 * (r + 1)]),
                     "taus": TAUS, "mk": mk, "idm": idm, "ident": _IDENT})
    res = run_bass_kernel_spmd(nc, maps, core_ids=list(range(8)))
    ys = np.empty((2, 8192, 1024), np.float32)
    for c in range(8):
        b, r = divmod(c, 4)
        ys[b, :, 256 * r:256 * (r + 1)] = res.results[c]["ys"]
    return ys


def _gdn_consts():
    p = np.arange(64)[:, None]; f = np.arange(64)[None, :]
    negu = np.where(f >= p, 0.0, -30000.0)
    negls = np.where(f < p, 0.0, -30000.0)
    nsu = np.where(f > p, -1.0, 0.0)
    i64 = np.eye(64)
    c64 = np.stack([negu, negls, nsu, i64], axis=1).astype(np.float32)
    cmask = np.ones((2, 512), np.float32); cmask[:, 0::64] = 0.0
    sel = np.zeros((2, 2, 128), np.float32); sel[0, 0, :] = 1.0; sel[1, 1, :] = 1.0
    return c64, cmask, sel


def build_L1a(S=8192, fz=None):
    nc = fz["nc"] if fz else bass.Bass("TRN2", target_bir_lowering=False)
    pfx = fz["pfx"] if fz else ""

    def D(name, shape):
        if fz and name in fz["share"]:
            return fz["share"][name]
        return nc.dram_tensor(pfx + name, shape, F32, kind="ExternalInput").ap()
    x_d = D("x", [S, 1024]); npre_d = D("npre", [1024]); w_d = D("w", [1024, 768]); wb_d = D("wb", [1024, 2]); wa_d = D("wa", [1024, 2])
    conv_d = D("conv", [4, 768]); alog_d = D("alog", [2]); dtb_d = D("dtb", [2])
    ident_d = D("ident", [128, 128]); c64_d = D("c64", [64, 4, 64]); cmask_d = D("cmask", [2, 512]); sel_d = D("sel", [2, 2, 128])
    ones_d = D("ones", [128, 128])
    o_d = fz["out"] if fz else nc.dram_tensor("o", [S, 256], F32, kind="ExternalOutput").ap()
    NST = S // 512
    with ExitStack() as st:
        C = Ctx(nc, st, fz["P"], pfx) if fz else Ctx(nc, st); P = C.P
        idf, bidf, idb, bidb = make_ident(C, ident_d)
        npre, bnpre = bcast_row_load(C, "npre", npre_d, 1024)
        w, bw = load_w_bf16(C, "w", w_d, 8, 768)
        wb, bwb = load_w_bf16(C, "wb", wb_d, 8, 2)
        wa, bwa = load_w_bf16(C, "wa", wa_d, 8, 2)
        cw, bcw = C.sb("cw", [128, 4, 6])
        P.dma("sp", cw[:], conv_d.rearrange("j (c p) -> p j c", p=128), writes=[bcw])
        extu = fz.get("uTp") if fz else None
        if extu:
            wu_d = D("wu", [1024, 256])
            wu, bwu = load_w_bf16(C, "wu", wu_d, 8, 256)
            uTp, buTp = extu
        c64, bc64 = C.sb("c64", [64, 4, 64]); P.dma("sp", c64[:], c64_d, writes=[bc64])
        NEGU = c64[:, 0, :]; NEGLS = c64[:, 1, :]; NSU = c64[:, 2, :]; I64 = c64[:, 3, :]
        cmask, bcmask = C.sb("cmask", [2, 512]); P.dma("sp", cmask[:], cmask_d, writes=[bcmask])
        sel, bsel = C.sb("sel", [2, 2, 128]); P.dma("sp", sel[:], sel_d, writes=[bsel])
        ones, bones = C.sb("ones", [128, 128]); P.dma("sp", ones[:], ones_d, writes=[bones])
        alog, balog = C.sb("alog", [2, 1]); P.dma("sp", alog[:], alog_d.rearrange("(a b) -> a b", b=1), writes=[balog])
        dtb, bdtb = C.sb("dtb", [2, 1]); P.dma("sp", dtb[:], dtb_d.rearrange("(a b) -> a b", b=1), writes=[bdtb])
        negA, bnegA = C.sb("negA", [2, 1])
        P.op("act", lambda E: E.activation(out=negA[:], in_=alog[:], func=AF.Exp), reads=[balog], writes=[bnegA])
        P.op("dve", lambda E: E.tensor_scalar(out=negA[:], in0=negA[:], scalar1=-1.0, scalar2=None, op0=ALU.mult), reads=[bnegA], writes=[bnegA])
        xt, bxt = C.sb("xt", [128, 1024]); sq, bsq = C.sb("sq", [128, 1024]); hn, bhn = C.sb("hn", [128, 1024], BF16)
        ss, bss = C.sb("ss", [128, 1]); hT, bhT = C.sb("hT", [128, 8, 512], BF16)
        raw, _ = C.sb("raw", [128, 6, 515]); braw = [Buf("raw%d" % i) for i in range(6)]
        cvq, bcvq = C.sb("cvq", [128, 512])
        act, _ = C.sb("act", [128, 6, 512]); bact = [Buf("act%d" % i) for i in range(6)]
        qk, _ = C.sb("qk", [128, 4, 512]); bqk = [Buf("qk%d" % i) for i in range(4)]
        rn, brn = C.sb("rn", [128, 512])
        brow, bbrow = C.sb("brow", [2, 512]); grow, bgrow = C.sb("grow", [2, 512]); gcrow, bgcrow = C.sb("gcrow", [2, 512])
        GCB = []; BB = []
        for h in range(2):
            GCB.append(C.sb("GCB%d" % h, [128, 512])); BB.append(C.sb("BB%d" % h, [128, 512]))
        m64 = {}
        for nm in ("arg1", "DT", "Ds", "tmp", "tmp2", "BBm"):
            m64[nm] = C.sb("m_" + nm, [64, 512])
        for nm in ("Pa", "Pb", "Qa", "Qb"):
            m64[nm] = C.sb("m_" + nm, [64, 512], BF16)
        heads = []
        for h in range(2):
            H = {}
            H["attnT"] = C.sb("attnT%d" % h, [64, 512], BF16); H["Y"] = C.sb("Y%d" % h, [64, 512]); H["Ybf"] = C.sb("Ybf%d" % h, [64, 512], BF16)
            H["EG"] = C.sb("EG%d" % h, [128, 512]); H["qdec"] = C.sb("qdec%d" % h, [128, 512])
            H["bv"] = C.sb("bv%d" % h, [64, 8, 128]); H["kdec"] = C.sb("kdec%d" % h, [64, 8, 128], BF16)
            H["nbg"] = C.sb("nbg%d" % h, [64, 8]); H["osb"] = C.sb("osb%d" % h, [128, 8, 128])
            H["vnew"] = C.sb("vnew%d" % h, [64, 128], BF16); H["rhs2"] = C.sb("rhs2%d" % h, [64, 128], BF16)
            heads.append(H)
        small = {}
        for nm in ("gccol", "bcol", "nbcol", "elast", "egc"):
            small[nm] = C.sb("s_" + nm, [64, 8])
        Sst = [C.sb("S%d" % h, [128, 128]) for h in range(2)]
        for h in range(2):
            P.op("dve", lambda E, h=h: E.memset(Sst[h][0][:], 0.0), writes=[Sst[h][1]])
        P.op("dve", lambda E: E.memset(raw[:, :, 0:3], 0.0), writes=braw)
        ptr, bptr = C.ps("ptr", [128, 1024], BF16)
        GP = [C.ps("gp%d" % i, [128, 512]) for i in range(5)]
        pT, bpT = C.ps("pT", [128, 1024])
        bo = None if fz else Buf("o", multi=True)

        def tt(out, bo_, a, ba, b, bb_, op, eng="dve"):
            P.op(eng, lambda E: E.tensor_tensor(out=out, in0=a, in1=b, op=op), reads=ba if isinstance(ba, list) else [ba], writes=[bo_])

        for s_ in range(NST):
            for t in range(4):
                r0 = s_ * 512 + t * 128
                P.dma("sp", xt[:], x_d[r0:r0 + 128, :], writes=[bxt])
                rms_rstd(C, xt[:], bxt, 1024, sq[:], bsq, ss, bss)
                P.op("dve", lambda E: E.scalar_tensor_tensor(out=hn[:], in0=xt[:], scalar=ss[:, 0:1], in1=npre[:],
                                                             op0=ALU.mult, op1=ALU.mult), reads=[bxt, bss, bnpre], writes=[bhn])
                transpose8(C, hn, bhn, idb, bidb, ptr, bptr, hT[:, :, t * 128:(t + 1) * 128], bhT, eng="act")
            for ct in range(6):
                pa, bpa = GP[ct % 2]
                fns = [(lambda E, kt=kt, ct=ct, pa=pa: E.matmul(pa[:], lhsT=w[:, kt, ct * 128:(ct + 1) * 128], rhs=hT[:, kt, :],
                                                                start=(kt == 0), stop=(kt == 7))) for kt in range(8)]
                P.mm_group(fns, reads=[bw, bhT], writes=[bpa])
                P.op("act", lambda E, ct=ct, pa=pa: E.copy(out=raw[:, ct, 3:515], in_=pa[:]), reads=[bpa], writes=[braw[ct]])
                P.op("dve", lambda E, ct=ct: E.tensor_scalar(out=cvq[:], in0=raw[:, ct, 0:512], scalar1=cw[:, 0, ct:ct + 1], scalar2=None, op0=ALU.mult),
                     reads=[braw[ct], bcw], writes=[bcvq])
                for j in range(1, 4):
                    P.op("dve", lambda E, ct=ct, j=j: E.scalar_tensor_tensor(out=cvq[:], in0=raw[:, ct, j:j + 512], scalar=cw[:, j, ct:ct + 1], in1=cvq[:],
                                                                             op0=ALU.mult, op1=ALU.add), reads=[braw[ct], bcw, bcvq], writes=[bcvq])
                P.op("act", lambda E, ct=ct: E.copy(out=raw[:, ct, 0:3], in_=raw[:, ct, 512:515]), reads=[braw[ct]], writes=[braw[ct]])
                P.op("act", lambda E, ct=ct: E.activation(out=act[:, ct, :], in_=cvq[:], func=AF.Silu), reads=[bcvq], writes=[bact[ct]])
            if extu:
                for blk in range(2):
                    pa, bpa = GP[2 + blk]
                    fns = [(lambda E, kt=kt, blk=blk, pa=pa: E.matmul(
                        pa[:].rearrange("p (s n) -> p s n", s=16), lhsT=wu[:, kt, blk * 128:(blk + 1) * 128],
                        rhs=hT[:, kt, :].rearrange("p (n s) -> p s n", s=16), start=(kt == 0), stop=(kt == 7))) for kt in range(8)]
                    P.mm_group(fns, reads=[bwu, bhT], writes=[bpa])
                    P.op("act", lambda E, blk=blk, pa=pa, s_=s_: E.copy(out=uTp[:, blk, :, 32 * s_:32 * s_ + 32], in_=pa[:].rearrange("p (s n) -> p s n", s=16)),
                         reads=[bpa], writes=[buTp])
            for ct in range(4):
                pa, bpa = GP[ct % 2]
                P.op("act", lambda E, ct=ct: E.activation(out=cvq[:], in_=act[:, ct, :], func=AF.Square), reads=[bact[ct]], writes=[bcvq])
                P.op("pe", lambda E, pa=pa: E.matmul(pa[:], lhsT=ones[:], rhs=cvq[:], start=True, stop=True), reads=[bones, bcvq], writes=[bpa])
                P.op("dve", lambda E, pa=pa: E.tensor_scalar(out=rn[:], in0=pa[:], scalar1=1e-6, scalar2=None, op0=ALU.add), reads=[bpa], writes=[brn])
                P.op("act", lambda E: E.activation(out=rn[:], in_=rn[:], func=AF.Sqrt), reads=[brn], writes=[brn])
                P.op("dve", lambda E: E.reciprocal(out=rn[:], in_=rn[:]), reads=[brn], writes=[brn])
                if ct < 2:
                    P.op("dve", lambda E, ct=ct: E.scalar_tensor_tensor(out=qk[:, ct, :], in0=act[:, ct, :], scalar=float(128 ** -0.5), in1=rn[:],
                                                                        op0=ALU.mult, op1=ALU.mult), reads=[bact[ct], brn], writes=[bqk[ct]])
                else:
                    P.op("dve", lambda E, ct=ct: E.tensor_tensor(out=qk[:, ct, :], in0=act[:, ct, :], in1=rn[:], op=ALU.mult),
                         reads=[bact[ct], brn], writes=[bqk[ct]])
            pa, bpa = GP[2]
            fns = [(lambda E, kt=kt, pa=pa: E.matmul(pa[0:2, :], lhsT=wb[:, kt, 0:2], rhs=hT[:, kt, :], start=(kt == 0), stop=(kt == 7))) for kt in range(8)]
            P.mm_group(fns, reads=[bwb, bhT], writes=[bpa])
            P.op("act", lambda E, pa=pa: E.activation(out=brow[:], in_=pa[0:2, :], func=AF.Sigmoid), reads=[bpa], writes=[bbrow])
            pa2, bpa2 = GP[3]
            fns = [(lambda E, kt=kt, pa2=pa2: E.matmul(pa2[0:2, :], lhsT=wa[:, kt, 0:2], rhs=hT[:, kt, :], start=(kt == 0), stop=(kt == 7))) for kt in range(8)]
            P.mm_group(fns, reads=[bwa, bhT], writes=[bpa2])
            P.op("act", lambda E, pa2=pa2: E.activation(out=grow[:], in_=pa2[0:2, :], func=AF.Exp, bias=dtb[:, 0:1], scale=1.0), reads=[bpa2, bdtb], writes=[bgrow])
            P.op("act", lambda E: E.activation(out=grow[:], in_=grow[:], func=AF.Ln, bias=1.0, scale=1.0), reads=[bgrow], writes=[bgrow])
            P.op("dve", lambda E: E.tensor_scalar(out=grow[:], in0=grow[:], scalar1=negA[:, 0:1], scalar2=None, op0=ALU.mult), reads=[bgrow, bnegA], writes=[bgrow])
            P.op("dve", lambda E: E.tensor_tensor_scan(out=gcrow[:], data0=cmask[:], data1=grow[:], initial=0.0, op0=ALU.mult, op1=ALU.add),
                 reads=[bcmask, bgrow], writes=[bgcrow])
            for h in range(2):
                pa, bpa = GP[h]
                P.op("pe", lambda E, h=h, pa=pa: E.matmul(pa[:], lhsT=sel[:, h, :], rhs=gcrow[:], start=True, stop=True), reads=[bsel, bgcrow], writes=[bpa])
                P.op("act", lambda E, h=h, pa=pa: E.copy(out=GCB[h][0][:], in_=pa[:]), reads=[bpa], writes=[GCB[h][1]])
                pa, bpa = GP[2 + h]
                P.op("pe", lambda E, h=h, pa=pa: E.matmul(pa[:], lhsT=sel[:, h, :], rhs=brow[:], start=True, stop=True), reads=[bsel, bbrow], writes=[bpa])
                P.op("act", lambda E, h=h, pa=pa: E.copy(out=BB[h][0][:], in_=pa[:]), reads=[bpa], writes=[BB[h][1]])
            for h in range(2):
                qT = qk[:, h, :]; bqT = bqk[h]; kT = qk[:, 2 + h, :]; bkT = bqk[2 + h]; vT = act[:, 4 + h, :]; bvT = bact[4 + h]
                gcb, bgcb = GCB[h]; bb, bbb = BB[h]
                H = heads[h]
                attnT, battnT = H["attnT"]; Y, bY = H["Y"]; EG, bEG = H["EG"]; qdec, bqdec = H["qdec"]
                Ybf, bYbf = H["Ybf"]
                bv, bbv = H["bv"]; kdec, bkdec = H["kdec"]; nbg, bnbg = H["nbg"]
                arg1, barg1 = m64["arg1"]; DT, bDT = m64["DT"]; Ds, bDs = m64["Ds"]
                tmp, btmp = m64["tmp"]; tmp2, btmp2 = m64["tmp2"]; BBm, bBBm = m64["BBm"]
                gccol, bgccol = small["gccol"]; bcol, bbcol = small["bcol"]; nbcol, bnbcol = small["nbcol"]
                elast, belast = small["elast"]; egc, begc = small["egc"]
                v3 = lambda t_: t_[:].rearrange("p (n f) -> p n f", f=64)
                i64b = I64.unsqueeze(1).to_broadcast([64, 8, 64])
                tt(v3(tmp), btmp, gcb[0:64, :].rearrange("p (n f) -> p n f", f=64), [bgcb, bc64], i64b, bc64, ALU.mult)
                P.op("dve", lambda E, tmp=tmp, gccol=gccol: E.tensor_reduce(out=gccol[:], in_=tmp[:].rearrange("p (n f) -> p n f", f=64), axis=AX.X, op=ALU.add), reads=[btmp], writes=[bgccol])
                tt(v3(tmp), btmp, bb[0:64, :].rearrange("p (n f) -> p n f", f=64), [bbb, bc64], i64b, bc64, ALU.mult)
                P.op("dve", lambda E, tmp=tmp, bcol=bcol: E.tensor_reduce(out=bcol[:], in_=tmp[:].rearrange("p (n f) -> p n f", f=64), axis=AX.X, op=ALU.add), reads=[btmp], writes=[bbcol])
                P.op("dve", lambda E: E.tensor_scalar(out=nbcol[:], in0=bcol[:], scalar1=-1.0, scalar2=None, op0=ALU.mult), reads=[bbcol], writes=[bnbcol])
                tt(v3(arg1), barg1, gcb[0:64, :].rearrange("p (n f) -> p n f", f=64), [bgcb, bgccol], gccol[:].unsqueeze(2).to_broadcast([64, 8, 64]), bgccol, ALU.subtract)
                tt(v3(DT), bDT, v3(arg1), [barg1, bc64], NEGU.unsqueeze(1).to_broadcast([64, 8, 64]), bc64, ALU.add)
                P.op("act", lambda E: E.activation(out=DT[:], in_=DT[:], func=AF.Exp), reads=[bDT], writes=[bDT])
                P.op("dve", lambda E: E.scalar_tensor_tensor(out=Ds[:].rearrange("p (n f) -> p n f", f=64), in0=arg1[:].rearrange("p (n f) -> p n f", f=64), scalar=-1.0,
                                                             in1=NEGLS.unsqueeze(1).to_broadcast([64, 8, 64]), op0=ALU.mult, op1=ALU.add), reads=[barg1, bc64], writes=[bDs])
                P.op("act", lambda E: E.activation(out=Ds[:], in_=Ds[:], func=AF.Exp), reads=[bDs], writes=[bDs])
                tt(v3(BBm), bBBm, bb[0:64, :].rearrange("p (n f) -> p n f", f=64), [bbb, bc64], NSU.unsqueeze(1).to_broadcast([64, 8, 64]), bc64, ALU.mult)
                pk, bpk = GP[0]; pq, bpq = GP[1]
                fns = [(lambda E, n=n, pk=pk, kT=kT: E.matmul(pk[0:64, n * 64:(n + 1) * 64], lhsT=kT[:, n * 64:(n + 1) * 64], rhs=kT[:, n * 64:(n + 1) * 64],
                                                              start=True, stop=True)) for n in range(8)]
                P.mm_group(fns, reads=[bkT], writes=[bpk])
                fns = [(lambda E, n=n, pq=pq, kT=kT, qT=qT: E.matmul(pq[0:64, n * 64:(n + 1) * 64], lhsT=kT[:, n * 64:(n + 1) * 64], rhs=qT[:, n * 64:(n + 1) * 64],
                                                                     start=True, stop=True)) for n in range(8)]
                P.mm_group(fns, reads=[bkT, bqT], writes=[bpq])
                tt(attnT[:], battnT, pq[0:64, :], [bpq, bDT], DT[:], bDT, ALU.mult)
                Pc, bPc = m64["Pa"]; Pn, bPn = m64["Pb"]; Qc, bQc = m64["Qa"]; Qn, bQn = m64["Qb"]
                tt(tmp[:], btmp, pk[0:64, :], [bpk, bDT], DT[:], bDT, ALU.mult)
                tt(Qc[:], bQc, tmp[:], [btmp, bBBm], BBm[:], bBBm, ALU.mult)
                tt(tmp2[:], btmp2, pk[0:64, :], [bpk, bDs], Ds[:], bDs, ALU.mult)
                tt(v3(Pc), bPc, v3(tmp2), [btmp2, bnbcol], nbcol[:].unsqueeze(2).to_broadcast([64, 8, 64]), bnbcol, ALU.mult)
                tt(v3(Y), bY, v3(Qc), [bQc, bc64], i64b, bc64, ALU.add)
                P.op("act", lambda E, Ybf=Ybf, Y=Y: E.copy(out=Ybf[:], in_=Y[:]), reads=[bY], writes=[bYbf])
                for j in range(5):
                    pP, bpP = GP[2]; pQ, bpQ = GP[3]
                    fns = [(lambda E, n=n, pP=pP, Qc=Qc, Pc=Pc: E.matmul(pP[0:64, n * 64:(n + 1) * 64], lhsT=Qc[:, n * 64:(n + 1) * 64], rhs=Pc[:, n * 64:(n + 1) * 64],
                                                                         start=True, stop=True)) for n in range(8)]
                    P.mm_group(fns, reads=[bQc, bPc], writes=[bpP])
                    if j < 4:
                        fns = [(lambda E, n=n, pQ=pQ, Qc=Qc, Pc=Pc: E.matmul(pQ[0:64, n * 64:(n + 1) * 64], lhsT=Pc[:, n * 64:(n + 1) * 64], rhs=Qc[:, n * 64:(n + 1) * 64],
                                                                             start=True, stop=True)) for n in range(8)]
                        P.mm_group(fns, reads=[bQc, bPc], writes=[bpQ])
                    P.op("act", lambda E, Pn=Pn, pP=pP: E.copy(out=Pn[:], in_=pP[0:64, :]), reads=[bpP], writes=[bPn])
                    if j < 4:
                        P.op("dve", lambda E, Qn=Qn, pQ=pQ: E.tensor_copy(out=Qn[:], in_=pQ[0:64, :]), reads=[bpQ], writes=[bQn])
                    pY, bpY = GP[0]
                    fns = [(lambda E, n=n, pY=pY, Pn=Pn, Ybf=Ybf: E.matmul(pY[0:64, n * 64:(n + 1) * 64], lhsT=Pn[:, n * 64:(n + 1) * 64], rhs=Ybf[:, n * 64:(n + 1) * 64],
                                                                         start=True, stop=True)) for n in range(8)]
                    P.mm_group(fns, reads=[bPn, bYbf], writes=[bpY])
                    tt(Y[:], bY, Y[:], [bY, bpY], pY[0:64, :], bpY, ALU.add)
                    P.op("act", lambda E, Ybf=Ybf, Y=Y: E.copy(out=Ybf[:], in_=Y[:]), reads=[bY], writes=[bYbf])
                    Pc, bPc, Pn, bPn = Pn, bPn, Pc, bPc
                    Qc, bQc, Qn, bQn = Qn, bQn, Qc, bQc
                fns = [(lambda E, n=n, vT=vT: E.transpose(out=pT[0:64, n * 128:(n + 1) * 128], in_=vT[:, n * 64:(n + 1) * 64], identity=idf[:])) for n in range(8)]
                P.mm_group(fns, reads=[bvT, bidf], writes=[bpT])
                tt(bv[:], bbv, pT[0:64, :].rearrange("p (n d) -> p n d", d=128), [bpT, bbcol], bcol[:].unsqueeze(2).to_broadcast([64, 8, 128]), bbcol, ALU.mult)
                tt(elast[:], belast, gcb[0:64, :].rearrange("p (n f) -> p n f", f=64)[:, :, 63], [bgcb, bgccol], gccol[:], bgccol, ALU.subtract)
                P.op("act", lambda E: E.activation(out=elast[:], in_=elast[:], func=AF.Exp), reads=[belast], writes=[belast])
                fns = [(lambda E, n=n, kT=kT: E.transpose(out=pT[0:64, n * 128:(n + 1) * 128], in_=kT[:, n * 64:(n + 1) * 64], identity=idf[:])) for n in range(8)]
                P.mm_group(fns, reads=[bkT, bidf], writes=[bpT])
                tt(kdec[:], bkdec, pT[0:64, :].rearrange("p (n d) -> p n d", d=128), [bpT, belast], elast[:].unsqueeze(2).to_broadcast([64, 8, 128]), belast, ALU.mult)
                P.op("act", lambda E, gcb=gcb, EG=EG: E.activation(out=EG[:], in_=gcb[:], func=AF.Exp), reads=[bgcb], writes=[bEG])
                tt(qdec[:], bqdec, qT, [bqT, bEG], EG[:], bEG, ALU.mult)
                P.op("act", lambda E: E.activation(out=egc[:], in_=gccol[:], func=AF.Exp), reads=[bgccol], writes=[begc])
                P.op("dve", lambda E, nbg=nbg: E.scalar_tensor_tensor(out=nbg[:], in0=egc[:], scalar=-1.0, in1=bcol[:], op0=ALU.mult, op1=ALU.mult),
                     reads=[begc, bbcol], writes=[bnbg])
            banks = [(GP[0], GP[1], GP[2]), (GP[3], GP[4], (pT, bpT))]
            for n in range(8):
                cs = slice(n * 64, (n + 1) * 64)
                for h in range(2):
                    kT = qk[:, 2 + h, :]; bkT = bqk[2 + h]
                    H = heads[h]; S, bS = Sst[h]
                    attnT, battnT = H["attnT"]; Y, bY = H["Ybf"]; EG, bEG = H["EG"]; qdec, bqdec = H["qdec"]
                    bv, bbv = H["bv"]; kdec, bkdec = H["kdec"]; nbg, bnbg = H["nbg"]
                    vnew, bvnew = H["vnew"]; rhs2, brhs2 = H["rhs2"]; osb, bosb = H["osb"]
                    (KSO, bKSO), (Vb, bVb), (Sb, bSb) = banks[h]
                    P.op("pe", lambda E, cs=cs, kT=kT, S=S, KSO=KSO: E.matmul(KSO[0:64, 0:128], lhsT=kT[:, cs], rhs=S[:], start=True, stop=True),
                         reads=[bkT, bS], writes=[bKSO])
                    P.op("dve", lambda E, n=n, KSO=KSO, rhs2=rhs2, nbg=nbg, bv=bv: E.scalar_tensor_tensor(
                        out=rhs2[:], in0=KSO[0:64, 0:128], scalar=nbg[:, n:n + 1], in1=bv[:, n, :], op0=ALU.mult, op1=ALU.add),
                        reads=[bKSO, bnbg, bbv], writes=[brhs2])
                    P.op("pe", lambda E, cs=cs, Y=Y, Vb=Vb, rhs2=rhs2: E.matmul(Vb[0:64, 0:128], lhsT=Y[:, cs], rhs=rhs2[:], start=True, stop=True),
                         reads=[bY, brhs2], writes=[bVb])
                    P.op("act", lambda E, vnew=vnew, Vb=Vb: E.copy(out=vnew[:], in_=Vb[0:64, 0:128]), reads=[bVb], writes=[bvnew])
                    fns = [lambda E, cs=cs, S=S, KSO=KSO, qdec=qdec: E.matmul(KSO[64:128, 0:128], lhsT=qdec[:, cs], rhs=S[:], start=True, stop=False),
                           lambda E, cs=cs, KSO=KSO, attnT=attnT, vnew=vnew: E.matmul(KSO[64:128, 0:128], lhsT=attnT[:, cs], rhs=vnew[:], start=False, stop=True)]
                    P.mm_group(fns, reads=[bqdec, bS, battnT, bvnew], writes=[bKSO])
                    P.op("pe", lambda E, n=n, Sb=Sb, kdec=kdec, vnew=vnew: E.matmul(Sb[:, 0:128], lhsT=kdec[:, n, :], rhs=vnew[:], start=True, stop=True),
                         reads=[bkdec, bvnew], writes=[bSb])
                    P.op("dve", lambda E, n=n, S=S, EG=EG, Sb=Sb: E.scalar_tensor_tensor(out=S[:], in0=S[:], scalar=EG[:, n * 64 + 63:n * 64 + 64], in1=Sb[:, 0:128],
                                                                                         op0=ALU.mult, op1=ALU.add), reads=[bS, bEG, bSb], writes=[bS])
                    P.op("act", lambda E, n=n, osb=osb, KSO=KSO: E.copy(out=osb[64:128, n, :], in_=KSO[64:128, 0:128]), reads=[bKSO], writes=[bosb])
            for h in range(2):
                osb, bosb = heads[h]["osb"]
                P.dma("sp", o_d[s_ * 512:(s_ + 1) * 512, h * 128:(h + 1) * 128].rearrange("(n c) d -> c n d", c=64), osb[64:128, :, :], reads=[bosb],
                      writes=[fz["obuf_of"](s_) if fz else bo])
            if fz:
                fz["after_chunk"](s_)
        if fz:
            barrier(P)
        else:
            P.finish([bo])
    return nc


def run_L1a(inp):
    nc = _get("L1a", build_L1a)
    c64, cmask, sel = _gdn_consts()
    w_in = inp["w_in_even"][0]
    conv = inp["conv_qkv"][0]
    ones = np.ones((128, 128), np.float32)
    maps = []
    for c in range(8):
        b, r = divmod(c, 4)
        cols = np.concatenate([np.arange(256 * r, 256 * r + 256), 1024 + np.arange(256 * r, 256 * r + 256), 2048 + np.arange(256 * r, 256 * r + 256)])
        maps.append({"x": np.ascontiguousarray(inp["x"][b]), "npre": np.ascontiguousarray(inp["norm_pre"][0]),
                     "w": np.ascontiguousarray(w_in[:, cols]), "wb": np.ascontiguousarray(w_in[:, 4096 + 2 * r:4096 + 2 * r + 2]),
                     "wa": np.ascontiguousarray(w_in[:, 4104 + 2 * r:4104 + 2 * r + 2]), "conv": np.ascontiguousarray(conv[:, cols]),
                     "alog": np.ascontiguousarray(inp["a_log"][0, 2 * r:2 * r + 2]), "dtb": np.ascontiguousarray(inp["dt_bias"][0, 2 * r:2 * r + 2]),
                     "ident": _IDENT, "c64": c64, "cmask": cmask, "sel": sel, "ones": ones})
    res = run_bass_kernel_spmd(nc, maps, core_ids=list(range(8)))
    S_ = inp["x"].shape[1]
    o = np.empty((2, S_, 1024), np.float32)
    for c in range(8):
        b, r = divmod(c, 4)
        o[b, :, 256 * r:256 * (r + 1)] = res.results[c]["o"]
    return o


def kernel_unfused(**inputs):
    inp = {k: np.asarray(v) for k, v in inputs.items()}
    o = run_L1a(inp)
    ys = run_L1b(inp)
    x1 = run_L2(inp, o, ys)
    out = run_L3(inp, x1)
    return out.astype(np.float32)


def build_fused():
    nc = bass.Bass("TRN2", target_bir_lowering=False)
    x_full = nc.dram_tensor("x", [8192, 1024], F32, kind="ExternalInput").ap()
    ident_d = nc.dram_tensor("ident", [128, 128], F32, kind="ExternalInput").ap()
    npre0_d = nc.dram_tensor("npre0", [1024], F32, kind="ExternalInput").ap()
    gidx_d = nc.dram_tensor("gidx", [128, 17, 4], I32, kind="ExternalInput").ap()
    out_d = nc.dram_tensor("out", [2048, 1024], F32, kind="ExternalOutput").ap()
    ag_in = [nc.dram_tensor("ag_in%d" % i, [8192, 256], F32) for i in range(2)]
    ag_out = [nc.dram_tensor("ag_out%d" % i, [4 * 8192, 256], F32) for i in range(2)]
    x1s = nc.dram_tensor("x1s", [2176, 1024], F32)
    GROUPS = [[0, 1, 2, 3], [4, 5, 6, 7]]
    with ExitStack() as st:
        C = Ctx(nc, st); P = C.P
        csem = st.enter_context(nc.semaphore("csem"))
        bag_out = Buf("ag_out"); bx1s = Buf("x1s", multi=True); bout = Buf("out", multi=True)
        bo_ch = [Buf("o_ch%d" % k, multi=True) for k in range(16)]
        by_jt = [Buf("y_jt%d" % k, multi=True) for k in range(4)]
        ncc = [0]

        def emit_cc(which, k, inbuf):
            P._deps("pool", [inbuf], [])
            P.streams["pool"].append(lambda E, which=which, k=k: E.collective_compute(
                "AllGather", ALU.bypass, replica_groups=GROUPS,
                ins=[ag_in[which].ap()[k * 512:(k + 1) * 512, :].opt()], outs=[ag_out[which].ap()[k * 2048:(k + 1) * 2048, :].opt()]).then_inc(csem))
            ncc[0] += 1

        share1 = {"x": x_full, "ident": ident_d, "npre": npre0_d}

        def after_jt(jt):
            for k in range(4 * jt, 4 * jt + 4):
                emit_cc(1, k, by_jt[jt])

        with ExitStack() as stU:
            CU = Ctx(nc, stU, P, "u_")
            uext = CU.sb("uTp", [128, 2, 16, 512], BF16)
            build_L1a(8192, fz={"nc": nc, "P": P, "pfx": "a_", "share": share1, "out": ag_in[0].ap(), "uTp": uext,
                                "obuf_of": lambda s_: bo_ch[s_], "after_chunk": lambda s_: emit_cc(0, s_, bo_ch[s_])})
            build_L1b(8192, fz={"nc": nc, "P": P, "pfx": "b_", "share": share1, "out": ag_in[1].ap(), "uTp": uext,
                                "obuf_of": lambda jt: by_jt[jt], "after_chunk": after_jt})
        P.streams["pool"].append(lambda E: E.wait_ge(csem, ncc[0]))
        gidx, bgidx = C.sb("gidx", [128, 17, 4], I32)
        P.dma("sp", gidx[:], gidx_d, writes=[bgidx])
        P.op("pool", lambda E: E.nop(), reads=[], writes=[bag_out])

        def gather(P_, ld, bld, tile, part):
            for i in range(4):
                P_.dma_ind("pool", ld[:, i * 256:(i + 1) * 256], ag_out[part].ap(), gidx[:, tile, i:i + 1], reads=[bag_out, bgidx], writes=[bld])

        share2 = {"ident": ident_d, "npre": npre0_d, "o": None, "ys": None}
        build_L2(2176, fz={"nc": nc, "P": P, "pfx": "c_", "share": share2, "out": x1s.ap(), "obuf": bx1s, "gather": gather})
        share3 = {"ident": ident_d, "x": x1s.ap()}
        build_L3(2048, fz={"nc": nc, "P": P, "pfx": "d_", "share": share3, "out": out_d, "obuf": bout, "xbuf": bx1s})
        P.finish([bout])
    return nc


def _gidx(r):
    g = np.zeros((128, 17, 4), np.int32)
    p = np.arange(128)[:, None, None]
    tile = np.arange(17)[None, :, None]
    src = np.arange(4)[None, None, :]
    tok = np.clip(2048 * r - 128 + tile * 128 + p, 0, 8191)
    g[:] = ((tok // 512) * 4 + src) * 512 + tok % 512
    return g


def kernel(**inputs):
    inp = {k: np.ascontiguousarray(np.asarray(v)) for k, v in inputs.items()}
    nc = _get("fused", build_fused)
    c64, cmask, sel = _gdn_consts()
    mk, idm = _s5_consts()
    ones = np.ones((128, 128), np.float32)
    w_in = inp["w_in_even"][0]
    conv = inp["conv_qkv"][0]
    wz = np.ascontiguousarray(np.concatenate([w_in[:, 3072:4096], w_in[:, 5136:6160]], axis=1))
    maps = []
    for c in range(8):
        b, r = divmod(c, 4)
        cols = np.concatenate([np.arange(256 * r, 256 * r + 256), 1024 + np.arange(256 * r, 256 * r + 256), 2048 + np.arange(256 * r, 256 * r + 256)])
        gs = slice(16 * r, 16 * r + 16)
        xq = np.zeros((2176, 1024), np.float32)
        xq[128:] = inp["x"][b, 2048 * r:2048 * (r + 1)]
        if r > 0:
            xq[:128] = inp["x"][b, 2048 * r - 128:2048 * r]
        m = {"x": inp["x"][b], "ident": _IDENT, "npre0": inp["norm_pre"][0], "gidx": _gidx(r),
             "a_w": np.ascontiguousarray(w_in[:, cols]), "a_wb": np.ascontiguousarray(w_in[:, 4096 + 2 * r:4096 + 2 * r + 2]),
             "a_wa": np.ascontiguousarray(w_in[:, 4104 + 2 * r:4104 + 2 * r + 2]), "a_conv": np.ascontiguousarray(conv[:, cols]),
             "a_alog": np.ascontiguousarray(inp["a_log"][0, 2 * r:2 * r + 2]), "a_dtb": np.ascontiguousarray(inp["dt_bias"][0, 2 * r:2 * r + 2]),
             "a_c64": c64, "a_cmask": cmask, "a_sel": sel, "a_ones": ones,
             "a_wu": np.ascontiguousarray(w_in[:, 4112 + 256 * r:4112 + 256 * (r + 1)]),
             "b_wu": np.ascontiguousarray(w_in[:, 4112 + 256 * r:4112 + 256 * (r + 1)]),
             "b_lre": np.ascontiguousarray(inp["s5_lam_re"][0, gs]), "b_lim": np.ascontiguousarray(inp["s5_lam_im"][0, gs]),
             "b_bre": np.ascontiguousarray(inp["s5_b_re"][0, gs]), "b_bim": np.ascontiguousarray(inp["s5_b_im"][0, gs]),
             "b_cre": np.ascontiguousarray(inp["s5_c_re"][0, gs]), "b_cim": np.ascontiguousarray(inp["s5_c_im"][0, gs]),
             "b_ldt": np.ascontiguousarray(inp["s5_log_dt"][0, gs]), "b_dd": np.ascontiguousarray(inp["s5_d"][0, 256 * r:256 * (r + 1)]),
             "b_taus": TAUS, "b_mk": mk, "b_idm": idm,
             "c_x": xq, "c_wz": wz, "c_wglu": inp["w_glu"][0], "c_wout": inp["w_out_even"][0], "c_npost": inp["norm_post"][0],
             "c_gnw": inp["gdn_norm_w"][0],
             "d_win": inp["w_in_odd"][0], "d_wout": inp["w_out_odd"][0], "d_conv": inp["conv_short"][0],
             "d_npre": inp["norm_pre"][1], "d_npost": inp["norm_post"][1]}
        maps.append(m)
    res = run_bass_kernel_spmd(nc, maps, core_ids=list(range(8)))
    out = np.empty((2, 8192, 1024), np.float32)
    for c in range(8):
        b, r = divmod(c, 4)
        out[b, r * 2048:(r + 1) * 2048] = res.results[c]["out"]
    return out
```

```python
from contextlib import ExitStack
import numpy as np
import concourse.bass as bass
import concourse.mybir as mybir
from concourse.bass_utils import run_bass_kernel_spmd

F32 = mybir.dt.float32
BF16 = mybir.dt.bfloat16
AF = mybir.ActivationFunctionType
ALU = mybir.AluOpType
AX = mybir.AxisListType

NDS = 12


class Buf:
    __slots__ = ("name", "w", "r", "multi")

    def __init__(self, name, multi=False):
        self.name = name
        self.w = [] if multi else None
        self.r = []
        self.multi = multi


class Prog:
    ENG = ("pe", "act", "dve", "pool", "sp")

    def __init__(self, nc, stack):
        self.nc = nc
        self.stack = stack
        self.streams = {e: [] for e in self.ENG}
        self.cnt = {e: 0 for e in self.ENG}
        self.sem = {e: stack.enter_context(nc.semaphore("s_" + e)) for e in self.ENG}
        self.seen = {e: {} for e in self.ENG}
        self.dcnt = {e: 0 for e in self.ENG}
        self.dsem = {}
        for e in ("sp", "pool", "act"):
            self.dsem[e] = [stack.enter_context(nc.semaphore("d_%s%d" % (e, i))) for i in range(NDS)]
        self.same_engine_sync = True
        self.nwaits = 0

    def _wait(self, eng, tok):
        if tok is None:
            return
        kind = tok[0]
        if kind == "c":
            _, e2, n = tok
            if e2 == eng and (eng == "pe" or not self.same_engine_sync):
                return
            key = e2
            if self.seen[eng].get(key, 0) >= n:
                return
            self.seen[eng][key] = n
            sem = self.sem[e2]
            self.streams[eng].append(lambda E, sem=sem, n=n: E.wait_ge(sem, n))
            self.nwaits += 1
        else:
            _, q, slot, val = tok
            key = ("d", q, slot)
            if self.seen[eng].get(key, 0) >= val:
                return
            self.seen[eng][key] = val
            sem = self.dsem[q][slot]
            self.streams[eng].append(lambda E, sem=sem, val=val: E.wait_ge(sem, val))
            self.nwaits += 1

    def _deps(self, eng, reads, writes):
        for b in reads:
            if b.multi:
                for t in b.w:
                    self._wait(eng, t)
            else:
                self._wait(eng, b.w)
        for b in writes:
            if not b.multi:
                self._wait(eng, b.w)
            for t in b.r:
                self._wait(eng, t)

    def _commit(self, tok, reads, writes):
        for b in writes:
            if b.multi:
                b.w.append(tok)
            else:
                b.w = tok
            b.r = []
        for b in reads:
            if b not in writes:
                b.r.append(tok)

    def op(self, eng, fn, reads=(), writes=()):
        reads = list(reads)
        writes = list(writes)
        self._deps(eng, reads, writes)
        self.cnt[eng] += 1
        n = self.cnt[eng]
        sem = self.sem[eng]
        self.streams[eng].append(lambda E, fn=fn, sem=sem: fn(E).then_inc(sem, 1))
        tok = ("c", eng, n)
        self._commit(tok, reads, writes)
        return tok

    def mm_group(self, fns, reads=(), writes=()):
        eng = "pe"
        reads = list(reads)
        writes = list(writes)
        self._deps(eng, reads, writes)
        self.cnt[eng] += 1
        n = self.cnt[eng]
        sem = self.sem[eng]
        for fn in fns[:-1]:
            self.streams[eng].append(lambda E, fn=fn: fn(E))
        last = fns[-1]
        self.streams[eng].append(lambda E, fn=last, sem=sem: fn(E).then_inc(sem, 1))
        tok = ("c", eng, n)
        self._commit(tok, reads, writes)
        return tok

    def dma(self, q, out_ap, in_ap, reads=(), writes=()):
        reads = list(reads)
        writes = list(writes)
        self._deps(q, reads, writes)
        j = self.dcnt[q]
        self.dcnt[q] += 1
        slot = j % NDS
        val = 16 * (j // NDS + 1)
        if j >= NDS:
            self._wait(q, ("d", q, slot, val - 16))
        sem = self.dsem[q][slot]
        self.streams[q].append(
            lambda E, o=out_ap, i=in_ap, sem=sem: E.dma_start(out=o, in_=i).then_inc(sem, 16))
        tok = ("d", q, slot, val)
        self._commit(tok, reads, writes)
        return tok

    def dma_ind(self, q, out_ap, table_ap, idx_ap, reads=(), writes=()):
        reads = list(reads)
        writes = list(writes)
        self._deps(q, reads, writes)
        j = self.dcnt[q]
        self.dcnt[q] += 1
        slot = j % NDS
        val = 16 * (j // NDS + 1)
        if j >= NDS:
            self._wait(q, ("d", q, slot, val - 16))
        sem = self.dsem[q][slot]
        self.streams[q].append(
            lambda E, o=out_ap, t=table_ap, i=idx_ap, sem=sem: E.indirect_dma_start(
                out=o, out_offset=None, in_=t, in_offset=bass.IndirectOffsetOnAxis(ap=i, axis=0)).then_inc(sem, 16))
        tok = ("d", q, slot, val)
        self._commit(tok, reads, writes)
        return tok

    def finish(self, final_bufs):
        for b in final_bufs:
            for t in (b.w if b.multi else [b.w]):
                self._wait("sp", t)
        nc = self.nc
        streams = self.streams
        with nc.Block() as block:
            @block.tensor
            def _(E):
                for f in streams["pe"]:
                    f(E)

            @block.scalar
            def _(E):
                for f in streams["act"]:
                    f(E)

            @block.vector
            def _(E):
                for f in streams["dve"]:
                    f(E)

            @block.gpsimd
            def _(E):
                for f in streams["pool"]:
                    f(E)

            @block.sync
            def _(E):
                for f in streams["sp"]:
                    f(E)


class Ctx:
    def __init__(self, nc, st, P=None, pfx=""):
        self.nc = nc
        self.st = st
        self.pfx = pfx
        if P is None:
            st.enter_context(nc.allow_non_contiguous_dma(reason="small parameter loads / layout transforms"))
            P = Prog(nc, st)
        self.P = P

    def sb(self, name, shape, dt=F32):
        t = self.st.enter_context(self.nc.sbuf_tensor("sb_" + self.pfx + name, shape, dt))
        return t, Buf(name)

    def ps(self, name, shape, dt=F32):
        t = self.st.enter_context(self.nc.psum_tensor("ps_" + self.pfx + name, shape, dt))
        return t, Buf(name)


def bcast_row_load(C, name, dram_vec, n, q="sp"):
    t, b = C.sb(name, [128, n])
    C.P.dma(q, t[:], dram_vec.partition_broadcast(128), writes=[b])
    return t, b


def make_ident(C, dram_ident):
    idf, bidf = C.sb("identf", [128, 128])
    C.P.dma("sp", idf[:], dram_ident, writes=[bidf])
    idb, bidb = C.sb("identb", [128, 128], BF16)
    C.P.op("dve", lambda E: E.tensor_copy(out=idb[:], in_=idf[:]), reads=[bidf], writes=[bidb])
    return idf, bidf, idb, bidb


def rms_rstd(C, src, bsrc, ncols, junk, bjunk, ss, bss, eps=1e-6):
    P = C.P
    P.op("act", lambda E: E.activation(out=junk, in_=src, func=AF.Square, accum_out=ss[:, 0:1]),
         reads=[bsrc], writes=[bjunk, bss])
    P.op("act", lambda E: E.activation(out=ss[:, 0:1], in_=ss[:, 0:1], func=AF.Sqrt, bias=float(eps), scale=float(1.0 / ncols)),
         reads=[bss], writes=[bss])
    P.op("dve", lambda E: E.reciprocal(out=ss[:, 0:1], in_=ss[:, 0:1]), reads=[bss], writes=[bss])


def transpose8(C, src_bf, bsrc, idb, bidb, ptr, bptr, dst3, bdst, eng="act"):
    P = C.P
    fns = [(lambda E, kt=kt: E.transpose(out=ptr[:, kt * 128:(kt + 1) * 128], in_=src_bf[:, kt * 128:(kt + 1) * 128],
                                         identity=idb[:])) for kt in range(8)]
    P.mm_group(fns, reads=[bsrc, bidb], writes=[bptr])
    src3 = ptr[:].rearrange("p (k t) -> p k t", k=8)
    if eng == "act":
        P.op("act", lambda E: E.copy(out=dst3, in_=src3), reads=[bptr], writes=[bdst])
    else:
        P.op("dve", lambda E: E.tensor_copy(out=dst3, in_=src3), reads=[bptr], writes=[bdst])


def outproj_post(C, catT, bcat, nkt, wout, bwout, t, xres, bxres, npw, bnpw, pso, bpso, yo, byo, junk, bjunk, ss, bss,
                 out_dram_rows, bout):
    P = C.P
    for hh in range(2):
        fns = [(lambda E, kt=kt, hh=hh: E.matmul(pso[hh][:], lhsT=catT[:, kt, t * 128:(t + 1) * 128],
                                                 rhs=wout[:, kt, hh * 512:(hh + 1) * 512],
                                                 start=(kt == 0), stop=(kt == nkt - 1))) for kt in range(nkt)]
        P.mm_group(fns, reads=[bcat, bwout], writes=[bpso[hh]])
        P.op("act", lambda E, hh=hh: E.copy(out=yo[:, hh * 512:(hh + 1) * 512], in_=pso[hh][:]),
             reads=[bpso[hh]], writes=[byo])
    rms_rstd(C, yo[:], byo, 1024, junk[:], bjunk, ss, bss)
    P.op("dve", lambda E: E.scalar_tensor_tensor(out=yo[:], in0=yo[:], scalar=ss[:, 0:1], in1=npw[:],
                                                 op0=ALU.mult, op1=ALU.mult), reads=[byo, bss, bnpw], writes=[byo])
    P.op("dve", lambda E: E.tensor_tensor(out=yo[:], in0=yo[:], in1=xres, op=ALU.add), reads=[byo, bxres], writes=[byo])
    P.dma("sp", out_dram_rows, yo[:], reads=[byo], writes=[bout])


def load_w_bf16(C, name, dram_w, kt_n, ncols, chunk=2048):
    w, bw = C.sb(name, [128, kt_n, ncols], BF16)
    src = dram_w.rearrange("(k p) c -> p k c", p=128)
    for kt in range(kt_n):
        for c0 in range(0, ncols, chunk):
            c1 = min(ncols, c0 + chunk)
            C.P.dma("pool", w[:, kt, c0:c1], src[:, kt, c0:c1], writes=[bw])
    return w, bw


def build_L2(ntok=2048, fz=None):
    nc = fz["nc"] if fz else bass.Bass("TRN2", target_bir_lowering=False)
    pfx = fz["pfx"] if fz else ""

    def D(name, shape):
        if fz and name in fz["share"]:
            return fz["share"][name]
        return nc.dram_tensor(pfx + name, shape, F32, kind="ExternalInput").ap()
    x_d = D("x", [ntok, 1024]); o_d = D("o", [ntok, 1024]); ys_d = D("ys", [ntok, 1024])
    wz_d = D("wz", [1024, 2048]); wglu_d = D("wglu", [1024, 1024]); wout_d = D("wout", [2048, 1024])
    npre_d = D("npre", [1024]); npost_d = D("npost", [1024]); gnw_d = D("gnw", [128]); ident_d = D("ident", [128, 128])
    out_d = fz["out"] if fz else nc.dram_tensor("out", [ntok, 1024], F32, kind="ExternalOutput").ap()
    NT = 512
    with ExitStack() as st:
        C = Ctx(nc, st, fz["P"], pfx) if fz else Ctx(nc, st); P = C.P
        idf, bidf, idb, bidb = make_ident(C, ident_d)
        npre, bnpre = bcast_row_load(C, "npre", npre_d, 1024)
        npost, bnpost = bcast_row_load(C, "npost", npost_d, 1024)
        gnw, bgnw = bcast_row_load(C, "gnw", gnw_d, 128)
        wz, bwz = load_w_bf16(C, "wz", wz_d, 8, 2048)
        wglu, bwglu = load_w_bf16(C, "wglu", wglu_d, 8, 1024)
        wout, bwout = load_w_bf16(C, "wout", wout_d, 16, 1024)
        xt4, bxt4 = C.sb("xt4", [128, 4, 1024]); bxt = [Buf("xt%d" % i) for i in range(4)]
        ldo = [C.sb("ldo%d" % i, [128, 1024]) for i in range(2)]
        ldy = [C.sb("ldy%d" % i, [128, 1024]) for i in range(2)]
        sq, bsq = C.sb("sq", [128, 1024])
        hn, bhn = C.sb("hn", [128, 1024], BF16)
        ss, bss = C.sb("ss", [128, 1])
        ss8, bss8 = C.sb("ss8", [128, 8])
        hT, bhT = C.sb("hT", [128, 8, NT], BF16)
        oT, boT = C.sb("oT", [128, 8, NT], BF16)
        yT, byT = C.sb("yT", [128, 8, NT], BF16)
        gz, bgz = C.sb("gz", [128, 8, NT], BF16)
        sg, bsg = C.sb("sg", [128, NT], BF16)
        catT, bcat = C.sb("catT", [128, 16, NT], BF16)
        yo, byo = C.sb("yo", [128, 1024])
        ptr, bptr = C.ps("ptr", [128, 1024], BF16)
        pmm = []; bpmm = []
        for i in range(4):
            t_, b_ = C.ps("pmm%d" % i, [128, 512]); pmm.append(t_); bpmm.append(b_)
        pso = []; bpso = []
        for i in range(2):
            t_, b_ = C.ps("pso%d" % i, [128, 512]); pso.append(t_); bpso.append(b_)
        bout = fz["obuf"] if fz else Buf("out", multi=True)
        if fz:
            sts = [(0, 128)] + [(128 + i * NT, NT) for i in range((ntok - 128) // NT)]
        else:
            sts = [(i * NT, NT) for i in range(ntok // NT)]
        tile_r0 = [t0_ + t_ * 128 for (t0_, n_) in sts for t_ in range(n_ // 128)]

        def issue_loads(ti):
            r0_ = tile_r0[ti]
            lo, blo = ldo[ti % 2]; ly, bly = ldy[ti % 2]
            if fz:
                fz["gather"](P, lo, blo, r0_ // 128, 0)
                fz["gather"](P, ly, bly, r0_ // 128, 1)
            else:
                P.dma("sp", lo[:], o_d[r0_:r0_ + 128, :], writes=[blo])
                P.dma("sp", ly[:], ys_d[r0_:r0_ + 128, :], writes=[bly])

        issue_loads(0)
        for (t0, n) in sts:
            ntl = n // 128
            for t in range(ntl):
                r0 = t0 + t * 128
                ti = tile_r0.index(r0)
                if ti + 1 < len(tile_r0):
                    issue_loads(ti + 1)
                P.dma("sp", xt4[:, t, :], x_d[r0:r0 + 128, :], writes=[bxt[t]])
                rms_rstd(C, xt4[:, t, :], bxt[t], 1024, sq[:], bsq, ss, bss)
                P.op("dve", lambda E, t=t: E.scalar_tensor_tensor(out=hn[:], in0=xt4[:, t, :], scalar=ss[:, 0:1], in1=npre[:],
                                                                  op0=ALU.mult, op1=ALU.mult), reads=[bxt[t], bss, bnpre], writes=[bhn])
                transpose8(C, hn, bhn, idb, bidb, ptr, bptr, hT[:, :, t * 128:(t + 1) * 128], bhT, eng="act")
                ld, bld = ldo[ti % 2]
                P.op("act", lambda E, ld=ld: E.activation(out=sq[:], in_=ld[:], func=AF.Square), reads=[bld], writes=[bsq])
                P.op("dve", lambda E: E.tensor_reduce(out=ss8[:], in_=sq[:].rearrange("p (h d) -> p h d", h=8), axis=AX.X, op=ALU.add),
                     reads=[bsq], writes=[bss8])
                P.op("dve", lambda E: E.tensor_scalar(out=ss8[:], in0=ss8[:], scalar1=1.0 / 128, scalar2=1e-6, op0=ALU.mult, op1=ALU.add),
                     reads=[bss8], writes=[bss8])
                P.op("act", lambda E: E.activation(out=ss8[:], in_=ss8[:], func=AF.Sqrt), reads=[bss8], writes=[bss8])
                P.op("dve", lambda E: E.reciprocal(out=ss8[:], in_=ss8[:]), reads=[bss8], writes=[bss8])
                P.op("dve", lambda E, ld=ld: E.tensor_tensor(out=sq[:].rearrange("p (h d) -> p h d", h=8), in0=ld[:].rearrange("p (h d) -> p h d", h=8),
                                                      in1=ss8[:].unsqueeze(2).to_broadcast([128, 8, 128]), op=ALU.mult),
                     reads=[bld, bss8], writes=[bsq])
                P.op("dve", lambda E: E.tensor_tensor(out=hn[:].rearrange("p (h d) -> p h d", h=8), in0=sq[:].rearrange("p (h d) -> p h d", h=8),
                                                      in1=gnw[:].unsqueeze(1).to_broadcast([128, 8, 128]), op=ALU.mult),
                     reads=[bsq, bgnw], writes=[bhn])
                transpose8(C, hn, bhn, idb, bidb, ptr, bptr, oT[:, :, t * 128:(t + 1) * 128], boT, eng="act")
                ld, bld = ldy[ti % 2]
                P.op("act", lambda E, ld=ld: E.activation(out=hn[:], in_=ld[:], func=AF.Gelu_apprx_tanh), reads=[bld], writes=[bhn])
                transpose8(C, hn, bhn, idb, bidb, ptr, bptr, yT[:, :, t * 128:(t + 1) * 128], byT, eng="dve")
            for ct in range(16):
                pb = pmm[ct % 4]; bpb = bpmm[ct % 4]
                fns = [(lambda E, kt=kt, ct=ct, pb=pb, n=n: E.matmul(pb[:, 0:n], lhsT=wz[:, kt, ct * 128:(ct + 1) * 128], rhs=hT[:, kt, 0:n],
                                                                start=(kt == 0), stop=(kt == 7))) for kt in range(8)]
                P.mm_group(fns, reads=[bwz, bhT], writes=[bpb])
                if ct < 8:
                    P.op("act", lambda E, pb=pb, n=n: E.activation(out=sg[:, 0:n], in_=pb[:, 0:n], func=AF.Silu), reads=[bpb], writes=[bsg])
                    P.op("dve", lambda E, ct=ct, n=n: E.tensor_tensor(out=catT[:, ct, 0:n], in0=oT[:, ct, 0:n], in1=sg[:, 0:n], op=ALU.mult),
                         reads=[boT, bsg], writes=[bcat])
                else:
                    P.op("act", lambda E, pb=pb, ct=ct, n=n: E.activation(out=gz[:, ct - 8, 0:n], in_=pb[:, 0:n], func=AF.Silu), reads=[bpb], writes=[bgz])
            for ct in range(8):
                pb = pmm[ct % 4]; bpb = bpmm[ct % 4]
                fns = [(lambda E, kt=kt, ct=ct, pb=pb, n=n: E.matmul(pb[:, 0:n], lhsT=wglu[:, kt, ct * 128:(ct + 1) * 128], rhs=yT[:, kt, 0:n],
                                                                start=(kt == 0), stop=(kt == 7))) for kt in range(8)]
                P.mm_group(fns, reads=[bwglu, byT], writes=[bpb])
                P.op("act", lambda E, pb=pb, n=n: E.activation(out=sg[:, 0:n], in_=pb[:, 0:n], func=AF.Sigmoid), reads=[bpb], writes=[bsg])
                P.op("dve", lambda E, ct=ct, n=n: E.tensor_tensor(out=sg[:, 0:n], in0=sg[:, 0:n], in1=yT[:, ct, 0:n], op=ALU.mult), reads=[bsg, byT], writes=[bsg])
                P.op("dve", lambda E, ct=ct, n=n: E.tensor_tensor(out=catT[:, 8 + ct, 0:n], in0=sg[:, 0:n], in1=gz[:, ct, 0:n], op=ALU.mult),
                     reads=[bsg, bgz], writes=[bcat])
            for t in range(ntl):
                r0 = t0 + t * 128
                outproj_post(C, catT, bcat, 16, wout, bwout, t, xt4[:, t, :], bxt[t], npost, bnpost, pso, bpso, yo, byo, sq, bsq, ss, bss,
                             out_d[r0:r0 + 128, :], bout)
        if fz:
            barrier(P)
        else:
            P.finish([bout])
    return nc


def build_L3(ntok=2048, fz=None):
    nc = fz["nc"] if fz else bass.Bass("TRN2", target_bir_lowering=False)
    pfx = fz["pfx"] if fz else ""

    def D(name, shape):
        if fz and name in fz["share"]:
            return fz["share"][name]
        return nc.dram_tensor(pfx + name, shape, F32, kind="ExternalInput").ap()
    x_d = D("x", [ntok + 128, 1024])
    win_d = D("win", [1024, 8192]); wout_d = D("wout", [2048, 1024]); conv_d = D("conv", [3, 2048])
    npre_d = D("npre", [1024]); npost_d = D("npost", [1024]); ident_d = D("ident", [128, 128])
    out_d = fz["out"] if fz else nc.dram_tensor("out", [ntok, 1024], F32, kind="ExternalOutput").ap()
    NT = 256
    with ExitStack() as st:
        C = Ctx(nc, st, fz["P"], pfx) if fz else Ctx(nc, st); P = C.P
        idf, bidf, idb, bidb = make_ident(C, ident_d)
        npre, bnpre = bcast_row_load(C, "npre", npre_d, 1024)
        npost, bnpost = bcast_row_load(C, "npost", npost_d, 1024)
        cw, bcw = C.sb("cw", [128, 3, 16])
        P.dma("sp", cw[:], conv_d.rearrange("j (c p) -> p j c", p=128), writes=[bcw])
        win, bwin = load_w_bf16(C, "win", win_d, 8, 8192)
        wout, bwout = load_w_bf16(C, "wout", wout_d, 16, 1024)
        xt, bxt = C.sb("xt", [128, 1024])
        sq, bsq = C.sb("sq", [128, 1024])
        hn, bhn = C.sb("hn", [128, 1024], BF16)
        ss, bss = C.sb("ss", [128, 1])
        hT, bhT = C.sb("hT", [128, 8, NT], BF16)
        y1T, by1T = C.sb("y1T", [128, 16, NT], BF16)
        pbuf, bpbuf = C.sb("pbuf", [128, NT + 2])
        phalo, bphalo = C.sb("phalo", [128, 16, 2])
        gcs, bgcs = C.sb("gcs", [128, NT])
        cv, bcv = C.sb("cv", [128, NT])
        sz, bsz = C.sb("sz", [128, NT])
        yo, byo = C.sb("yo", [128, 1024])
        P.op("dve", lambda E: E.memset(phalo[:], 0.0), writes=[bphalo])
        ptr, bptr = C.ps("ptr", [128, 1024], BF16)
        GB = [C.ps("g%d" % i, [128, 512]) for i in range(7)]
        pso = [GB[0][0], GB[1][0]]; bpso = [GB[0][1], GB[1][1]]
        bout = fz["obuf"] if fz else Buf("out", multi=True)
        sts = [(0, 128)] + [(128 + i * NT, NT) for i in range(ntok // NT)]
        for (t0, n) in sts:
            ntl = n // 128
            for t in range(ntl):
                r0 = t0 + t * 128
                P.dma("sp", xt[:], x_d[r0:r0 + 128, :], reads=([fz["xbuf"]] if fz else []), writes=[bxt])
                rms_rstd(C, xt[:], bxt, 1024, sq[:], bsq, ss, bss)
                P.op("dve", lambda E: E.scalar_tensor_tensor(out=hn[:], in0=xt[:], scalar=ss[:, 0:1], in1=npre[:],
                                                             op0=ALU.mult, op1=ALU.mult), reads=[bxt, bss, bnpre], writes=[bhn])
                transpose8(C, hn, bhn, idb, bidb, ptr, bptr, hT[:, :, t * 128:(t + 1) * 128], bhT, eng="act")
            for ct in range(16):
                sel_ = [GB[3 * (ct % 2) + 0], GB[3 * (ct % 2) + 1], GB[3 * (ct % 2) + 2], GB[6]]
                pmm = [x_[0] for x_ in sel_]; bpmm = [x_[1] for x_ in sel_]
                for part in range(4):
                    col0 = (part * 16 + ct) * 128
                    pb = pmm[part]
                    fns = [(lambda E, n=n, kt=kt, col0=col0, pb=pb: E.matmul(pb[:, 0:n], lhsT=win[:, kt, col0:col0 + 128], rhs=hT[:, kt, 0:n],
                                                                        start=(kt == 0), stop=(kt == 7))) for kt in range(8)]
                    P.mm_group(fns, reads=[bwin, bhT], writes=[bpmm[part]])
                P.op("act", lambda E, n=n, pmm=pmm: E.copy(out=gcs[:, 0:n], in_=pmm[1][:, 0:n]), reads=[bpmm[1]], writes=[bgcs])
                P.op("act", lambda E, ct=ct: E.copy(out=pbuf[:, 0:2], in_=phalo[:, ct, :]), reads=[bphalo], writes=[bpbuf])
                P.op("dve", lambda E, n=n, pmm=pmm: E.tensor_tensor(out=pbuf[:, 2:2 + n], in0=gcs[:, 0:n], in1=pmm[2][:, 0:n], op=ALU.mult),
                     reads=[bgcs, bpmm[2]], writes=[bpbuf])
                P.op("act", lambda E, n=n, ct=ct: E.copy(out=phalo[:, ct, :], in_=pbuf[:, n:n + 2]), reads=[bpbuf], writes=[bphalo])
                if t0 == 0:
                    continue
                P.op("dve", lambda E, n=n, ct=ct: E.tensor_scalar(out=cv[:, 0:n], in0=pbuf[:, 0:n], scalar1=cw[:, 0, ct:ct + 1], scalar2=None, op0=ALU.mult),
                     reads=[bpbuf, bcw], writes=[bcv])
                P.op("dve", lambda E, n=n, ct=ct: E.scalar_tensor_tensor(out=cv[:, 0:n], in0=pbuf[:, 1:1 + n], scalar=cw[:, 1, ct:ct + 1], in1=cv[:, 0:n],
                                                                    op0=ALU.mult, op1=ALU.add), reads=[bpbuf, bcw, bcv], writes=[bcv])
                P.op("dve", lambda E, n=n, ct=ct: E.scalar_tensor_tensor(out=cv[:, 0:n], in0=pbuf[:, 2:2 + n], scalar=cw[:, 2, ct:ct + 1], in1=cv[:, 0:n],
                                                                    op0=ALU.mult, op1=ALU.add), reads=[bpbuf, bcw, bcv], writes=[bcv])
                P.op("dve", lambda E, n=n, pmm=pmm: E.tensor_tensor(out=cv[:, 0:n], in0=cv[:, 0:n], in1=pmm[0][:, 0:n], op=ALU.mult), reads=[bcv, bpmm[0]], writes=[bcv])
                P.op("act", lambda E, n=n, pmm=pmm: E.activation(out=sz[:, 0:n], in_=pmm[3][:, 0:n], func=AF.Silu), reads=[bpmm[3]], writes=[bsz])
                P.op("dve", lambda E, n=n, ct=ct: E.tensor_tensor(out=y1T[:, ct, 0:n], in0=cv[:, 0:n], in1=sz[:, 0:n], op=ALU.mult),
                     reads=[bcv, bsz], writes=[by1T])
            if t0 == 0:
                continue
            for t in range(ntl):
                r0 = t0 + t * 128
                P.dma("sp", xt[:], x_d[r0:r0 + 128, :], reads=([fz["xbuf"]] if fz else []), writes=[bxt])
                outproj_post(C, y1T, by1T, 16, wout, bwout, t, xt[:], bxt, npost, bnpost, pso, bpso, yo, byo, sq, bsq, ss, bss,
                             out_d[r0 - 128:r0, :], bout)
        if fz:
            barrier(P)
        else:
            P.finish([bout])
    return nc


_IDENT = np.eye(128, dtype=np.float32)
_CACHE = {}


def _get(name, fn):
    if name not in _CACHE:
        _CACHE[name] = fn()
    return _CACHE[name]


def run_L2(inp, o_full, ys_full):
    nc = _get("L2", build_L2)
    w_in = inp["w_in_even"][0]
    wz = np.ascontiguousarray(np.concatenate([w_in[:, 3072:4096], w_in[:, 5136:6160]], axis=1))
    maps = []
    for c in range(8):
        b, r = divmod(c, 4)
        sl = slice(r * 2048, (r + 1) * 2048)
        maps.append({"x": np.ascontiguousarray(inp["x"][b, sl]), "o": np.ascontiguousarray(o_full[b, sl]),
                     "ys": np.ascontiguousarray(ys_full[b, sl]), "wz": wz, "wglu": np.ascontiguousarray(inp["w_glu"][0]),
                     "wout": np.ascontiguousarray(inp["w_out_even"][0]), "npre": np.ascontiguousarray(inp["norm_pre"][0]),
                     "npost": np.ascontiguousarray(inp["norm_post"][0]), "gnw": np.ascontiguousarray(inp["gdn_norm_w"][0]),
                     "ident": _IDENT})
    res = run_bass_kernel_spmd(nc, maps, core_ids=list(range(8)))
    x1 = np.empty((2, 8192, 1024), np.float32)
    for c in range(8):
        b, r = divmod(c, 4)
        x1[b, r * 2048:(r + 1) * 2048] = res.results[c]["out"]
    return x1


def run_L3(inp, x1):
    nc = _get("L3", build_L3)
    maps = []
    for c in range(8):
        b, r = divmod(c, 4)
        xh = np.zeros((2048 + 128, 1024), np.float32)
        xh[128:] = x1[b, r * 2048:(r + 1) * 2048]
        if r > 0:
            xh[:128] = x1[b, r * 2048 - 128:r * 2048]
        maps.append({"x": xh, "win": np.ascontiguousarray(inp["w_in_odd"][0]), "wout": np.ascontiguousarray(inp["w_out_odd"][0]),
                     "conv": np.ascontiguousarray(inp["conv_short"][0]), "npre": np.ascontiguousarray(inp["norm_pre"][1]),
                     "npost": np.ascontiguousarray(inp["norm_post"][1]), "ident": _IDENT})
    res = run_bass_kernel_spmd(nc, maps, core_ids=list(range(8)))
    out = np.empty((2, 8192, 1024), np.float32)
    for c in range(8):
        b, r = divmod(c, 4)
        out[b, r * 2048:(r + 1) * 2048] = res.results[c]["out"]
    return out


I32 = mybir.dt.int32
TAUS = np.array(list(range(17)) + [32, 64, 128, 256, 512, 1024, 2048, 4096] + list(range(15, -1, -1)), np.float32)
NTAU = len(TAUS)


def _s5_consts():
    mk = np.zeros((128, 2, 16, 16), np.float32)
    idm = np.zeros((128, 2, 16, 16), np.float32)
    for kt2 in range(2):
        for sp in range(8):
            s = kt2 * 8 + sp
            for h in range(16):
                mk[sp * 16 + h, kt2, s:, :] = 1.0
                idm[sp * 16 + h, kt2, s, h] = 1.0
    return mk.reshape(128, 2, 256), idm.reshape(128, 2, 256)


def barrier(P):
    for e in P.ENG:
        for e2 in P.ENG:
            if P.cnt[e2] > 0:
                P._wait(e, ("c", e2, P.cnt[e2]))
        for q in P.dsem:
            j1 = P.dcnt[q]
            for j in range(max(0, j1 - NDS), j1):
                P._wait(e, ("d", q, j % NDS, 16 * (j // NDS + 1)))


def build_L1b(S=8192, fz=None):
    nc = fz["nc"] if fz else bass.Bass("TRN2", target_bir_lowering=False)
    pfx = fz["pfx"] if fz else ""

    def D(name, shape):
        if fz and name in fz["share"]:
            return fz["share"][name]
        return nc.dram_tensor(pfx + name, shape, F32, kind="ExternalInput").ap()
    x_d = D("x", [S, 1024]); npre_d = D("npre", [1024]); wu_d = D("wu", [1024, 256])
    lre_d = D("lre", [16, 64]); lim_d = D("lim", [16, 64]); bre_d = D("bre", [16, 64, 16]); bim_d = D("bim", [16, 64, 16])
    cre_d = D("cre", [16, 16, 64]); cim_d = D("cim", [16, 16, 64]); ldt_d = D("ldt", [16]); dd_d = D("dd", [256])
    taus_d = D("taus", [NTAU]); mk_d = D("mk", [128, 2, 256]); idm_d = D("idm", [128, 2, 256]); ident_d = D("ident", [128, 128])
    ys_d = fz["out"] if fz else nc.dram_tensor("ys", [S, 256], F32, kind="ExternalOutput").ap()
    NCH = S // 16
    NST = S // 512
    with ExitStack() as st:
        C = Ctx(nc, st, fz["P"], pfx) if fz else Ctx(nc, st); P = C.P
        idf, bidf, idb, bidb = make_ident(C, ident_d)
        ptr, bptr = C.ps("ptr", [128, 1024], BF16)
        py, bpy = C.ps("py", [128, 1024])
        G = []; bG = []
        for i in range(4):
            t_, b_ = C.ps("g%d" % i, [128, 512]); G.append(t_); bG.append(b_)
        U, bU = C.sb("U", [128, 2, 16, NCH], BF16)
        with ExitStack() as st2:
            C2 = Ctx(nc, st2, P, C.pfx)
            ext = fz.get("uTp") if fz else None
            if ext:
                uTp, buTp = ext
            else:
                uTp, buTp = C2.sb("uTp", [128, 2, 16, NCH], BF16)
            with ExitStack() as st1:
                C1 = Ctx(nc, st1, P, C.pfx)
                npre, bnpre = bcast_row_load(C1, "npre", npre_d, 1024)
                wu, bwu = load_w_bf16(C1, "wu", wu_d, 8, 256)
                xt, bxt = C1.sb("xt", [128, 1024])
                sq, bsq = C1.sb("sq", [128, 1024])
                hn, bhn = C1.sb("hn", [128, 1024], BF16)
                ss, bss = C1.sb("ss", [128, 1])
                hT, bhT = C1.sb("hT", [128, 8, 512], BF16)
                for s_ in range(0 if ext else NST):
                    for t in range(4):
                        r0 = s_ * 512 + t * 128
                        P.dma("sp", xt[:], x_d[r0:r0 + 128, :], writes=[bxt])
                        rms_rstd(C1, xt[:], bxt, 1024, sq[:], bsq, ss, bss)
                        P.op("dve", lambda E: E.scalar_tensor_tensor(out=hn[:], in0=xt[:], scalar=ss[:, 0:1], in1=npre[:],
                                                                     op0=ALU.mult, op1=ALU.mult), reads=[bxt, bss, bnpre], writes=[bhn])
                        transpose8(C1, hn, bhn, idb, bidb, ptr, bptr, hT[:, :, t * 128:(t + 1) * 128], bhT, eng="act")
                    for blk in range(2):
                        pb = G[blk]
                        fns = [(lambda E, kt=kt, blk=blk, pb=pb: E.matmul(
                            pb[:].rearrange("p (s n) -> p s n", s=16), lhsT=wu[:, kt, blk * 128:(blk + 1) * 128],
                            rhs=hT[:, kt, :].rearrange("p (n s) -> p s n", s=16), start=(kt == 0), stop=(kt == 7))) for kt in range(8)]
                        P.mm_group(fns, reads=[bwu, bhT], writes=[bG[blk]])
                        P.op("act" if blk == 0 else "dve",
                             (lambda E, blk=blk, pb=pb, s_=s_: E.copy(out=uTp[:, blk, :, 32 * s_:32 * s_ + 32], in_=pb[:].rearrange("p (s n) -> p s n", s=16)))
                             if blk == 0 else
                             (lambda E, blk=blk, pb=pb, s_=s_: E.tensor_copy(out=uTp[:, blk, :, 32 * s_:32 * s_ + 32], in_=pb[:].rearrange("p (s n) -> p s n", s=16))),
                             reads=[bG[blk]], writes=[buTp])
                barrier(P)
            ud2 = nc.dram_tensor(pfx + "ud2", [16, 2, 8, 16, NCH], BF16)
            bud2 = Buf("ud2", multi=True)
            bU.multi = True; bU.w = []
            for g in range(16):
                P.dma("sp", ud2.ap()[g].rearrange("k sp h n -> h (k sp) n"),
                      uTp[(g % 8) * 16:(g % 8 + 1) * 16, g // 8, :, :], reads=[buTp], writes=[bud2])
            for g in range(16):
                P.dma("sp", U[:, :, g, :], ud2.ap()[g].rearrange("k sp h n -> (sp h) k n"), reads=[bud2], writes=[bU])
            barrier(P)
        lre, blre = C.sb("lre", [128, 8]); lim, blim = C.sb("lim", [128, 8]); ldt, bldt = C.sb("ldt", [128, 8])
        TAU, bTAU = bcast_row_load(C, "TAU", taus_d, NTAU)
        Er, bEr = C.sb("Er", [128, 8, NTAU]); Ei, bEi = C.sb("Ei", [128, 8, NTAU]); NEi, bNEi = C.sb("NEi", [128, 8, NTAU])
        Hr, bHr = C.sb("Hr", [128, 8, 17, 16]); nHi, bnHi = C.sb("nHi", [128, 8, 17, 16])
        WbT, bWbT = C.sb("WbT", [128, 2, 8, 2, 128], BF16)
        Toep, bToep = C.sb("Toep", [128, 2, 16, 256], BF16)
        with ExitStack() as st3:
            C3 = Ctx(nc, st3, P, C.pfx)
            Br, bBr = C3.sb("Br", [128, 8, 16]); Bi, bBi = C3.sb("Bi", [128, 8, 16])
            Cr, bCr = C3.sb("Cr", [128, 8, 16]); Ci, bCi = C3.sb("Ci", [128, 8, 16])
            dcol, bdcol = C3.sb("dcol", [128, 16])
            MK, bMK = C3.sb("MK", [128, 2, 256]); IDM, bIDM = C3.sb("IDM", [128, 2, 256])
            P.dma("sp", MK[:], mk_d, writes=[bMK]); P.dma("sp", IDM[:], idm_d, writes=[bIDM])
            for two in range(2):
                hs = slice(64 * two, 64 * two + 64)
                P.dma("sp", lre[hs, :], lre_d.rearrange("(gp two) p -> two p gp", two=2)[two], writes=[blre])
                P.dma("sp", lim[hs, :], lim_d.rearrange("(gp two) p -> two p gp", two=2)[two], writes=[blim])
                P.dma("sp", ldt[hs, :], ldt_d.rearrange("(gp two) -> two gp", two=2)[two].partition_broadcast(64), writes=[bldt])
                P.dma("sp", Br[hs], bre_d.rearrange("(gp two) p h -> two p gp h", two=2)[two], writes=[bBr])
                P.dma("sp", Bi[hs], bim_d.rearrange("(gp two) p h -> two p gp h", two=2)[two], writes=[bBi])
                for gp in range(8):
                    P.dma("sp", Cr[hs, gp, :], cre_d[2 * gp + two].rearrange("h p -> p h"), writes=[bCr])
                    P.dma("sp", Ci[hs, gp, :], cim_d[2 * gp + two].rearrange("h p -> p h"), writes=[bCi])
            for sp in range(8):
                P.dma("sp", dcol[sp * 16:(sp + 1) * 16, :], dd_d.rearrange("(g h) -> h g", h=16), writes=[bdcol])
            sm = {}
            for nm in ("dt", "lr", "lrdt", "th", "den", "nr", "fre", "fim", "t8a", "t8b"):
                sm[nm] = C3.sb("sm_" + nm, [128, 8])
            T41 = {}
            for nm in ("ARG", "MARG", "MAG", "MAGN", "SIN", "COS", "ErN", "EiN", "rt", "rk"):
                T41[nm] = C3.sb("t41_" + nm, [128, 8, NTAU])
            rki, brki = C3.sb("rki", [128, 8, NTAU], I32)

            def tt(eng, out, bo, a, ba, b, bb_, op):
                P.op(eng, lambda E: E.tensor_tensor(out=out, in0=a, in1=b, op=op), reads=[ba, bb_], writes=[bo])

            dt, bdt = sm["dt"]; lr, blr = sm["lr"]; lrdt, blrdt = sm["lrdt"]; th, bth = sm["th"]
            P.op("act", lambda E: E.activation(out=dt[:], in_=ldt[:], func=AF.Exp), reads=[bldt], writes=[bdt])
            P.op("dve", lambda E: E.tensor_scalar(out=lr[:], in0=lre[:], scalar1=-1e-4, scalar2=None, op0=ALU.min), reads=[blre], writes=[blr])
            tt("dve", lrdt[:], blrdt, lr[:], blr, dt[:], bdt, ALU.mult)
            tt("dve", th[:], bth, lim[:], blim, dt[:], bdt, ALU.mult)
            ARG, bARG = T41["ARG"]; MARG, bMARG = T41["MARG"]; MAG, bMAG = T41["MAG"]; MAGN, bMAGN = T41["MAGN"]
            SIN, bSIN = T41["SIN"]; COS, bCOS = T41["COS"]; ErN, bErN = T41["ErN"]; EiN, bEiN = T41["EiN"]
            rt, brt = T41["rt"]; rk, brk = T41["rk"]
            tb = TAU[:].unsqueeze(1).to_broadcast([128, 8, NTAU])
            tt("dve", ARG[:], bARG, th[:].unsqueeze(2).to_broadcast([128, 8, NTAU]), bth, tb, bTAU, ALU.mult)
            tt("dve", MARG[:], bMARG, lrdt[:].unsqueeze(2).to_broadcast([128, 8, NTAU]), blrdt, tb, bTAU, ALU.mult)
            P.op("act", lambda E: E.activation(out=MAG[:], in_=MARG[:], func=AF.Exp), reads=[bMARG], writes=[bMAG])
            P.op("act", lambda E: E.activation(out=MAGN[:, :, 0:17], in_=MARG[:, :, 0:17], func=AF.Exp, scale=-1.0), reads=[bMARG], writes=[bMAGN])

            def sin_of(dst, bdst, shift):
                P.op("dve", lambda E: E.tensor_scalar(out=rt[:], in0=ARG[:], scalar1=float(shift), scalar2=None, op0=ALU.add), reads=[bARG], writes=[brt])
                P.op("dve", lambda E: E.tensor_scalar(out=rki[:], in0=rt[:], scalar1=float(1.0 / (2 * np.pi)), scalar2=None, op0=ALU.mult), reads=[brt], writes=[brki])
                P.op("dve", lambda E: E.tensor_copy(out=rk[:], in_=rki[:]), reads=[brki], writes=[brk])
                P.op("dve", lambda E: E.scalar_tensor_tensor(out=rt[:], in0=rk[:], scalar=float(-2 * np.pi), in1=rt[:], op0=ALU.mult, op1=ALU.add),
                     reads=[brk, brt], writes=[brt])
                P.op("dve", lambda E: E.tensor_scalar(out=rt[:], in0=rt[:], scalar1=-3.14159, scalar2=3.14159, op0=ALU.max, op1=ALU.min), reads=[brt], writes=[brt])
                P.op("act", lambda E: E.activation(out=dst[:], in_=rt[:], func=AF.Sin), reads=[brt], writes=[bdst])

            sin_of(SIN, bSIN, 0.0)
            sin_of(COS, bCOS, np.pi / 2)
            tt("dve", Er[:], bEr, MAG[:], bMAG, COS[:], bCOS, ALU.mult)
            tt("dve", Ei[:], bEi, MAG[:], bMAG, SIN[:], bSIN, ALU.mult)
            P.op("dve", lambda E: E.tensor_scalar(out=NEi[:], in0=Ei[:], scalar1=-1.0, scalar2=None, op0=ALU.mult), reads=[bEi], writes=[bNEi])
            tt("dve", ErN[:, :, 0:17], bErN, MAGN[:, :, 0:17], bMAGN, COS[:, :, 0:17], bCOS, ALU.mult)
            tt("dve", EiN[:, :, 0:17], bEiN, MAGN[:, :, 0:17], bMAGN, SIN[:, :, 0:17], bSIN, ALU.mult)
            P.op("dve", lambda E: E.tensor_scalar(out=EiN[:, :, 0:17], in0=EiN[:, :, 0:17], scalar1=-1.0, scalar2=None, op0=ALU.mult), reads=[bEiN], writes=[bEiN])
            den, bden = sm["den"]; nr, bnr = sm["nr"]; fre, bfre = sm["fre"]; fim, bfim = sm["fim"]; t8a, bt8a = sm["t8a"]; t8b, bt8b = sm["t8b"]
            tt("dve", den[:], bden, lr[:], blr, lr[:], blr, ALU.mult)
            tt("dve", t8a[:], bt8a, lim[:], blim, lim[:], blim, ALU.mult)
            tt("dve", den[:], bden, den[:], bden, t8a[:], bt8a, ALU.add)
            P.op("dve", lambda E: E.reciprocal(out=den[:], in_=den[:]), reads=[bden], writes=[bden])
            P.op("dve", lambda E: E.tensor_scalar(out=nr[:], in0=Er[:, :, 1], scalar1=-1.0, scalar2=None, op0=ALU.add), reads=[bEr], writes=[bnr])
            tt("dve", fre[:], bfre, nr[:], bnr, lr[:], blr, ALU.mult)
            tt("dve", t8a[:], bt8a, Ei[:, :, 1], bEi, lim[:], blim, ALU.mult)
            tt("dve", fre[:], bfre, fre[:], bfre, t8a[:], bt8a, ALU.add)
            tt("dve", fre[:], bfre, fre[:], bfre, den[:], bden, ALU.mult)
            tt("dve", fim[:], bfim, Ei[:, :, 1], bEi, lr[:], blr, ALU.mult)
            tt("dve", t8b[:], bt8b, nr[:], bnr, lim[:], blim, ALU.mult)
            tt("dve", fim[:], bfim, fim[:], bfim, t8b[:], bt8b, ALU.subtract)
            tt("dve", fim[:], bfim, fim[:], bfim, den[:], bden, ALU.mult)

            def cmul(outr, boutr, outi, bouti, ar, bar, ai, bai, br_, bbr_, bi_, bbi_, tmp, btmp):
                tt("dve", outr, boutr, ar, bar, br_, bbr_, ALU.mult)
                tt("dve", tmp, btmp, ai, bai, bi_, bbi_, ALU.mult)
                tt("dve", outr, boutr, outr, boutr, tmp, btmp, ALU.subtract)
                tt("dve", outi, bouti, ar, bar, bi_, bbi_, ALU.mult)
                tt("dve", tmp, btmp, ai, bai, br_, bbr_, ALU.mult)
                tt("dve", outi, bouti, outi, bouti, tmp, btmp, ALU.add)

            bbr, bbbr = C3.sb("bbr", [128, 8, 16]); bbi, bbbi = C3.sb("bbi", [128, 8, 16]); tmp16, btmp16 = C3.sb("tmp16", [128, 8, 16])
            fb = lambda t_: t_[:].unsqueeze(2).to_broadcast([128, 8, 16])
            cmul(bbr[:], bbbr, bbi[:], bbbi, fb(fre), bfre, fb(fim), bfim, Br[:], bBr, Bi[:], bBi, tmp16[:], btmp16)
            Gr, bGr = C3.sb("Gr", [128, 8, 16, 16]); Gi, bGi = C3.sb("Gi", [128, 8, 16, 16])
            WPr, bWPr = C3.sb("WPr", [128, 8, 16, 16]); WPi, bWPi = C3.sb("WPi", [128, 8, 16, 16])
            Hi, bHi = C3.sb("Hi", [128, 8, 17, 16]); tmpH, btmpH = C3.sb("tmpH", [128, 8, 17, 16])
            eb = lambda t_, j0, j1: t_[:, :, j0:j1].unsqueeze(3).to_broadcast([128, 8, j1 - j0, 16])
            vb = lambda t_, n_: t_[:].unsqueeze(2).to_broadcast([128, 8, n_, 16])
            cmul(Gr[:], bGr, Gi[:], bGi, eb(ErN, 0, 16), bErN, eb(EiN, 0, 16), bEiN, vb(bbr, 16), bbbr, vb(bbi, 16), bbbi, tmpH[:, :, 0:16, :], btmpH)
            cmul(WPr[:], bWPr, WPi[:], bWPi, eb(Er, 25, 41), bEr, eb(Ei, 25, 41), bEi, vb(bbr, 16), bbbr, vb(bbi, 16), bbbi, tmpH[:, :, 0:16, :], btmpH)
            cmul(Hr[:], bHr, Hi[:], bHi, eb(Er, 0, 17), bEr, eb(Ei, 0, 17), bEi, vb(Cr, 17), bCr, vb(Ci, 17), bCi, tmpH[:], btmpH)
            P.op("dve", lambda E: E.tensor_scalar(out=nHi[:], in0=Hi[:], scalar1=-1.0, scalar2=None, op0=ALU.mult), reads=[bHi], writes=[bnHi])
            for gp in range(8):
                for kt2 in range(2):
                    for c, (WP_, bWP_) in enumerate(((WPr, bWPr), (WPi, bWPi))):
                        P.op("pe", lambda E, gp=gp, kt2=kt2, WP_=WP_: E.transpose(
                            out=G[2][:, 0:128], in_=WP_[:, gp, kt2 * 8:(kt2 + 1) * 8, :].rearrange("p s h -> p (s h)"), identity=idf[:]),
                            reads=[bWP_, bidf], writes=[bG[2]])
                        P.op("act", lambda E, gp=gp, kt2=kt2, c=c: E.copy(out=WbT[:, kt2, gp, c, :], in_=G[2][:, 0:128]), reads=[bG[2]], writes=[bWbT])
            tmpT, btmpT = C3.sb("tmpT", [128, 256])
            for g in range(16):
                gp = g // 2; hs = slice(64 * (g % 2), 64 * (g % 2) + 64)
                for kt2 in range(2):
                    fns = [
                        lambda E, gp=gp, hs=hs, kt2=kt2: E.matmul(G[3][:, 0:256], lhsT=Gr[hs, gp, kt2 * 8:(kt2 + 1) * 8, :].rearrange("p s h -> p (s h)"),
                                                                  rhs=Hr[hs, gp, 0:16, :].rearrange("p t h -> p (t h)"), start=True, stop=False),
                        lambda E, gp=gp, hs=hs, kt2=kt2: E.matmul(G[3][:, 0:256], lhsT=Gi[hs, gp, kt2 * 8:(kt2 + 1) * 8, :].rearrange("p s h -> p (s h)"),
                                                                  rhs=nHi[hs, gp, 0:16, :].rearrange("p t h -> p (t h)"), start=False, stop=True)]
                    P.mm_group(fns, reads=[bGr, bGi, bHr, bnHi], writes=[bG[3]])
                    P.op("dve", lambda E, kt2=kt2: E.tensor_tensor(out=tmpT[:], in0=G[3][:, 0:256], in1=MK[:, kt2, :], op=ALU.mult),
                         reads=[bG[3], bMK], writes=[btmpT])
                    P.op("dve", lambda E, kt2=kt2, g=g: E.scalar_tensor_tensor(out=Toep[:, kt2, g, :], in0=IDM[:, kt2, :], scalar=dcol[:, g:g + 1], in1=tmpT[:],
                                                                               op0=ALU.mult, op1=ALU.add), reads=[bIDM, bdcol, btmpT], writes=[bToep])
            barrier(P)
        X = {}
        for bufn in ("A", "B"):
            for c in ("re", "im"):
                X[(bufn, c)] = (C.sb("X%s%s" % (bufn, c), [128, 8, NCH + 1])[0], [Buf("X%s%s%d" % (bufn, c, gp)) for gp in range(8)])
        Ysb, bYsb = C.sb("Ysb", [128, 16, 256])
        for key in X:
            t_, bl = X[key]
            P.op("dve", lambda E, t_=t_: E.memset(t_[:, :, 0:1], 0.0), writes=bl)
        for gp in range(8):
            for c, cn in enumerate(("re", "im")):
                px = G[c]
                fns = []
                for two in range(2):
                    g = 2 * gp + two
                    for kt2 in range(2):
                        fns.append(lambda E, two=two, g=g, kt2=kt2, gp=gp, c=c, px=px: E.matmul(
                            px[64 * two:64 * two + 64, :], lhsT=WbT[:, kt2, gp, c, 64 * two:64 * two + 64], rhs=U[:, kt2, g, :],
                            start=(kt2 == 0), stop=(kt2 == 1)))
                P.mm_group(fns, reads=[bWbT, bU], writes=[bG[c]])
                xt_, xb_ = X[("A", cn)]
                P.op("act", lambda E, xt_=xt_, gp=gp, px=px: E.copy(out=xt_[:, gp, 1:NCH + 1], in_=px[:]), reads=[bG[c]], writes=[xb_[gp]])
        for k in range(9):
            d = 1 << k
            j = 16 if k == 0 else 16 + k
            src, dst = ("A", "B") if k % 2 == 0 else ("B", "A")
            sre, bsre = X[(src, "re")]; sim, bsim = X[(src, "im")]
            dre, bdre = X[(dst, "re")]; dim_, bdim = X[(dst, "im")]
            P.op("dve", lambda E, dre=dre, sre=sre, d=d: E.tensor_copy(out=dre[:, :, 1:1 + d], in_=sre[:, :, 1:1 + d]), reads=bsre, writes=bdre)
            P.op("pool", lambda E, dim_=dim_, sim=sim, d=d: E.tensor_copy(out=dim_[:, :, 1:1 + d], in_=sim[:, :, 1:1 + d]), reads=bsim, writes=bdim)
            for gp in range(8):
                lo = slice(1, NCH + 1 - d); hi = slice(1 + d, NCH + 1)
                P.op("dve", lambda E, gp=gp, j=j, dre=dre, sre=sre, lo=lo, hi=hi: E.scalar_tensor_tensor(
                    out=dre[:, gp, hi], in0=sre[:, gp, lo], scalar=Er[:, gp, j:j + 1], in1=sre[:, gp, hi], op0=ALU.mult, op1=ALU.add),
                    reads=[bsre[gp], bEr], writes=[bdre[gp]])
                P.op("dve", lambda E, gp=gp, j=j, dre=dre, sim=sim, lo=lo, hi=hi: E.scalar_tensor_tensor(
                    out=dre[:, gp, hi], in0=sim[:, gp, lo], scalar=NEi[:, gp, j:j + 1], in1=dre[:, gp, hi], op0=ALU.mult, op1=ALU.add),
                    reads=[bsim[gp], bNEi, bdre[gp]], writes=[bdre[gp]])
                P.op("dve", lambda E, gp=gp, j=j, dim_=dim_, sim=sim, lo=lo, hi=hi: E.scalar_tensor_tensor(
                    out=dim_[:, gp, hi], in0=sim[:, gp, lo], scalar=Er[:, gp, j:j + 1], in1=sim[:, gp, hi], op0=ALU.mult, op1=ALU.add),
                    reads=[bsim[gp], bEr], writes=[bdim[gp]])
                P.op("dve", lambda E, gp=gp, j=j, dim_=dim_, sre=sre, lo=lo, hi=hi: E.scalar_tensor_tensor(
                    out=dim_[:, gp, hi], in0=sre[:, gp, lo], scalar=Ei[:, gp, j:j + 1], in1=dim_[:, gp, hi], op0=ALU.mult, op1=ALU.add),
                    reads=[bsre[gp], bEi, bdim[gp]], writes=[bdim[gp]])
        fre_, bfre_ = X[("B", "re")]; fim_, bfim_ = X[("B", "im")]
        bys = None if fz else Buf("ys", multi=True)
        ysv = ys_d.rearrange("(n t) c -> n t c", t=16)
        for jt in range(NCH // 128):
            for gq in range(4):
                fns = []
                for gi in range(4):
                    g = 4 * gq + gi; gp = g // 2; hs = slice(64 * (g % 2), 64 * (g % 2) + 64)
                    o_ = (gi * 256, (gi + 1) * 256)
                    for kt2 in range(2):
                        fns.append(lambda E, o_=o_, g=g, kt2=kt2, jt=jt: E.matmul(
                            py[:, o_[0]:o_[1]], lhsT=U[:, kt2, g, jt * 128:(jt + 1) * 128], rhs=Toep[:, kt2, g, :], start=(kt2 == 0), stop=False))
                    fns.append(lambda E, o_=o_, gp=gp, hs=hs, jt=jt: E.matmul(
                        py[:, o_[0]:o_[1]], lhsT=fre_[hs, gp, jt * 128:(jt + 1) * 128], rhs=Hr[hs, gp, 1:17, :].rearrange("p t h -> p (t h)"),
                        start=False, stop=False))
                    fns.append(lambda E, o_=o_, gp=gp, hs=hs, jt=jt: E.matmul(
                        py[:, o_[0]:o_[1]], lhsT=fim_[hs, gp, jt * 128:(jt + 1) * 128], rhs=nHi[hs, gp, 1:17, :].rearrange("p t h -> p (t h)"),
                        start=False, stop=True))
                P.mm_group(fns, reads=[bU, bToep, bHr, bnHi] + bfre_ + bfim_, writes=[bpy])
                P.op("act" if gq % 2 == 0 else "dve",
                     (lambda E, gq=gq: E.copy(out=Ysb[:].rearrange("p t (g h) -> p g t h", h=16)[:, 4 * gq:4 * gq + 4],
                                              in_=py[:].rearrange("p (g t h) -> p g t h", g=4, h=16)))
                     if gq % 2 == 0 else
                     (lambda E, gq=gq: E.tensor_copy(out=Ysb[:].rearrange("p t (g h) -> p g t h", h=16)[:, 4 * gq:4 * gq + 4],
                                                     in_=py[:].rearrange("p (g t h) -> p g t h", g=4, h=16))),
                     reads=[bpy], writes=[bYsb])
            P.dma("sp", ysv[jt * 128:(jt + 1) * 128, :, :], Ysb[:], reads=[bYsb], writes=[fz["obuf_of"](jt) if fz else bys])
            if fz:
                fz["after_chunk"](jt)
        if fz:
            barrier(P)
        else:
            P.finish([bys])
    return nc


def run_L1b(inp):
    nc = _get("L1b", build_L1b)
    mk, idm = _s5_consts()
    w_in = inp["w_in_even"][0]
    maps = []
    for c in range(8):
        b, r = divmod(c, 4)
        gs = slice(16 * r, 16 * r + 16)
        maps.append({"x": np.ascontiguousarray(inp["x"][b]), "npre": np.ascontiguousarray(inp["norm_pre"][0]),
                     "wu": np.ascontiguousarray(w_in[:, 4112 + 256 * r:4112 + 256 * (r + 1)]),
                     "lre": np.ascontiguousarray(inp["s5_lam_re"][0, gs]), "lim": np.ascontiguousarray(inp["s5_lam_im"][0, gs]),
                     "bre": np.ascontiguousarray(inp["s5_b_re"][0, gs]), "bim": np.ascontiguousarray(inp["s5_b_im"][0, gs]),
                     "cre": np.ascontiguousarray(inp["s5_c_re"][0, gs]), "cim": np.ascontiguousarray(inp["s5_c_im"][0, gs]),
                     "ldt": np.ascontiguousarray(inp["s5_log_dt"][0, gs]), "dd": np.ascontiguousarray(inp["s5_d"][0, 256 * r:256 * (r + 1)]),
                     "taus": TAUS, "mk": mk, "idm": idm, "ident": _IDENT})
    res = run_bass_kernel_spmd(nc, maps, core_ids=list(range(8)))
    ys = np.empty((2, 8192, 1024), np.float32)
    for c in range(8):
        b, r = divmod(c, 4)
        ys[b, :, 256 * r:256 * (r + 1)] = res.results[c]["ys"]
    return ys


def _gdn_consts():
    p = np.arange(64)[:, None]; f = np.arange(64)[None, :]
    negu = np.where(f >= p, 0.0, -30000.0)
    negls = np.where(f < p, 0.0, -30000.0)
    nsu = np.where(f > p, -1.0, 0.0)
    i64 = np.eye(64)
    c64 = np.stack([negu, negls, nsu, i64], axis=1).astype(np.float32)
    cmask = np.ones((2, 512), np.float32); cmask[:, 0::64] = 0.0
    sel = np.zeros((2, 2, 128), np.float32); sel[0, 0, :] = 1.0; sel[1, 1, :] = 1.0
    return c64, cmask, sel


def build_L1a(S=8192, fz=None):
    nc = fz["nc"] if fz else bass.Bass("TRN2", target_bir_lowering=False)
    pfx = fz["pfx"] if fz else ""

    def D(name, shape):
        if fz and name in fz["share"]:
            return fz["share"][name]
        return nc.dram_tensor(pfx + name, shape, F32, kind="ExternalInput").ap()
    x_d = D("x", [S, 1024]); npre_d = D("npre", [1024]); w_d = D("w", [1024, 768]); wb_d = D("wb", [1024, 2]); wa_d = D("wa", [1024, 2])
    conv_d = D("conv", [4, 768]); alog_d = D("alog", [2]); dtb_d = D("dtb", [2])
    ident_d = D("ident", [128, 128]); c64_d = D("c64", [64, 4, 64]); cmask_d = D("cmask", [2, 512]); sel_d = D("sel", [2, 2, 128])
    ones_d = D("ones", [128, 128])
    o_d = fz["out"] if fz else nc.dram_tensor("o", [S, 256], F32, kind="ExternalOutput").ap()
    NST = S // 512
    with ExitStack() as st:
        C = Ctx(nc, st, fz["P"], pfx) if fz else Ctx(nc, st); P = C.P
        idf, bidf, idb, bidb = make_ident(C, ident_d)
        npre, bnpre = bcast_row_load(C, "npre", npre_d, 1024)
        w, bw = load_w_bf16(C, "w", w_d, 8, 768)
        wb, bwb = load_w_bf16(C, "wb", wb_d, 8, 2)
        wa, bwa = load_w_bf16(C, "wa", wa_d, 8, 2)
        cw, bcw = C.sb("cw", [128, 4, 6])
        P.dma("sp", cw[:], conv_d.rearrange("j (c p) -> p j c", p=128), writes=[bcw])
        extu = fz.get("uTp") if fz else None
        if extu:
            wu_d = D("wu", [1024, 256])
            wu, bwu = load_w_bf16(C, "wu", wu_d, 8, 256)
            uTp, buTp = extu
        c64, bc64 = C.sb("c64", [64, 4, 64]); P.dma("sp", c64[:], c64_d, writes=[bc64])
        NEGU = c64[:, 0, :]; NEGLS = c64[:, 1, :]; NSU = c64[:, 2, :]; I64 = c64[:, 3, :]
        cmask, bcmask = C.sb("cmask", [2, 512]); P.dma("sp", cmask[:], cmask_d, writes=[bcmask])
        sel, bsel = C.sb("sel", [2, 2, 128]); P.dma("sp", sel[:], sel_d, writes=[bsel])
        ones, bones = C.sb("ones", [128, 128]); P.dma("sp", ones[:], ones_d, writes=[bones])
        alog, balog = C.sb("alog", [2, 1]); P.dma("sp", alog[:], alog_d.rearrange("(a b) -> a b", b=1), writes=[balog])
        dtb, bdtb = C.sb("dtb", [2, 1]); P.dma("sp", dtb[:], dtb_d.rearrange("(a b) -> a b", b=1), writes=[bdtb])
        negA, bnegA = C.sb("negA", [2, 1])
        P.op("act", lambda E: E.activation(out=negA[:], in_=alog[:], func=AF.Exp), reads=[balog], writes=[bnegA])
        P.op("dve", lambda E: E.tensor_scalar(out=negA[:], in0=negA[:], scalar1=-1.0, scalar2=None, op0=ALU.mult), reads=[bnegA], writes=[bnegA])
        xt, bxt = C.sb("xt", [128, 1024]); sq, bsq = C.sb("sq", [128, 1024]); hn, bhn = C.sb("hn", [128, 1024], BF16)
        ss, bss = C.sb("ss", [128, 1]); hT, bhT = C.sb("hT", [128, 8, 512], BF16)
        raw, _ = C.sb("raw", [128, 6, 515]); braw = [Buf("raw%d" % i) for i in range(6)]
        cvq, bcvq = C.sb("cvq", [128, 512])
        act, _ = C.sb("act", [128, 6, 512]); bact = [Buf("act%d" % i) for i in range(6)]
        qk, _ = C.sb("qk", [128, 4, 512]); bqk = [Buf("qk%d" % i) for i in range(4)]
        rn, brn = C.sb("rn", [128, 512])
        brow, bbrow = C.sb("brow", [2, 512]); grow, bgrow = C.sb("grow", [2, 512]); gcrow, bgcrow = C.sb("gcrow", [2, 512])
        GCB = []; BB = []
        for h in range(2):
            GCB.append(C.sb("GCB%d" % h, [128, 512])); BB.append(C.sb("BB%d" % h, [128, 512]))
        m64 = {}
        for nm in ("arg1", "DT", "Ds", "tmp", "tmp2", "BBm"):
            m64[nm] = C.sb("m_" + nm, [64, 512])
        for nm in ("Pa", "Pb", "Qa", "Qb"):
            m64[nm] = C.sb("m_" + nm, [64, 512], BF16)
        heads = []
        for h in range(2):
            H = {}
            H["attnT"] = C.sb("attnT%d" % h, [64, 512], BF16); H["Y"] = C.sb("Y%d" % h, [64, 512]); H["Ybf"] = C.sb("Ybf%d" % h, [64, 512], BF16)
            H["EG"] = C.sb("EG%d" % h, [128, 512]); H["qdec"] = C.sb("qdec%d" % h, [128, 512])
            H["bv"] = C.sb("bv%d" % h, [64, 8, 128]); H["kdec"] = C.sb("kdec%d" % h, [64, 8, 128], BF16)
            H["nbg"] = C.sb("nbg%d" % h, [64, 8]); H["osb"] = C.sb("osb%d" % h, [128, 8, 128])
            H["vnew"] = C.sb("vnew%d" % h, [64, 128], BF16); H["rhs2"] = C.sb("rhs2%d" % h, [64, 128], BF16)
            heads.append(H)
        small = {}
        for nm in ("gccol", "bcol", "nbcol", "elast", "egc"):
            small[nm] = C.sb("s_" + nm, [64, 8])
        Sst = [C.sb("S%d" % h, [128, 128]) for h in range(2)]
        for h in range(2):
            P.op("dve", lambda E, h=h: E.memset(Sst[h][0][:], 0.0), writes=[Sst[h][1]])
        P.op("dve", lambda E: E.memset(raw[:, :, 0:3], 0.0), writes=braw)
        ptr, bptr = C.ps("ptr", [128, 1024], BF16)
        GP = [C.ps("gp%d" % i, [128, 512]) for i in range(5)]
        pT, bpT = C.ps("pT", [128, 1024])
        bo = None if fz else Buf("o", multi=True)

        def tt(out, bo_, a, ba, b, bb_, op, eng="dve"):
            P.op(eng, lambda E: E.tensor_tensor(out=out, in0=a, in1=b, op=op), reads=ba if isinstance(ba, list) else [ba], writes=[bo_])

        for s_ in range(NST):
            for t in range(4):
                r0 = s_ * 512 + t * 128
                P.dma("sp", xt[:], x_d[r0:r0 + 128, :], writes=[bxt])
                rms_rstd(C, xt[:], bxt, 1024, sq[:], bsq, ss, bss)
                P.op("dve", lambda E: E.scalar_tensor_tensor(out=hn[:], in0=xt[:], scalar=ss[:, 0:1], in1=npre[:],
                                                             op0=ALU.mult, op1=ALU.mult), reads=[bxt, bss, bnpre], writes=[bhn])
                transpose8(C, hn, bhn, idb, bidb, ptr, bptr, hT[:, :, t * 128:(t + 1) * 128], bhT, eng="act")
            for ct in range(6):
                pa, bpa = GP[ct % 2]
                fns = [(lambda E, kt=kt, ct=ct, pa=pa: E.matmul(pa[:], lhsT=w[:, kt, ct * 128:(ct + 1) * 128], rhs=hT[:, kt, :],
                                                                start=(kt == 0), stop=(kt == 7))) for kt in range(8)]
                P.mm_group(fns, reads=[bw, bhT], writes=[bpa])
                P.op("act", lambda E, ct=ct, pa=pa: E.copy(out=raw[:, ct, 3:515], in_=pa[:]), reads=[bpa], writes=[braw[ct]])
                P.op("dve", lambda E, ct=ct: E.tensor_scalar(out=cvq[:], in0=raw[:, ct, 0:512], scalar1=cw[:, 0, ct:ct + 1], scalar2=None, op0=ALU.mult),
                     reads=[braw[ct], bcw], writes=[bcvq])
                for j in range(1, 4):
                    P.op("dve", lambda E, ct=ct, j=j: E.scalar_tensor_tensor(out=cvq[:], in0=raw[:, ct, j:j + 512], scalar=cw[:, j, ct:ct + 1], in1=cvq[:],
                                                                             op0=ALU.mult, op1=ALU.add), reads=[braw[ct], bcw, bcvq], writes=[bcvq])
                P.op("act", lambda E, ct=ct: E.copy(out=raw[:, ct, 0:3], in_=raw[:, ct, 512:515]), reads=[braw[ct]], writes=[braw[ct]])
                P.op("act", lambda E, ct=ct: E.activation(out=act[:, ct, :], in_=cvq[:], func=AF.Silu), reads=[bcvq], writes=[bact[ct]])
            if extu:
                for blk in range(2):
                    pa, bpa = GP[2 + blk]
                    fns = [(lambda E, kt=kt, blk=blk, pa=pa: E.matmul(
                        pa[:].rearrange("p (s n) -> p s n", s=16), lhsT=wu[:, kt, blk * 128:(blk + 1) * 128],
                        rhs=hT[:, kt, :].rearrange("p (n s) -> p s n", s=16), start=(kt == 0), stop=(kt == 7))) for kt in range(8)]
                    P.mm_group(fns, reads=[bwu, bhT], writes=[bpa])
                    P.op("act", lambda E, blk=blk, pa=pa, s_=s_: E.copy(out=uTp[:, blk, :, 32 * s_:32 * s_ + 32], in_=pa[:].rearrange("p (s n) -> p s n", s=16)),
                         reads=[bpa], writes=[buTp])
            for ct in range(4):
                pa, bpa = GP[ct % 2]
                P.op("act", lambda E, ct=ct: E.activation(out=cvq[:], in_=act[:, ct, :], func=AF.Square), reads=[bact[ct]], writes=[bcvq])
                P.op("pe", lambda E, pa=pa: E.matmul(pa[:], lhsT=ones[:], rhs=cvq[:], start=True, stop=True), reads=[bones, bcvq], writes=[bpa])
                P.op("dve", lambda E, pa=pa: E.tensor_scalar(out=rn[:], in0=pa[:], scalar1=1e-6, scalar2=None, op0=ALU.add), reads=[bpa], writes=[brn])
                P.op("act", lambda E: E.activation(out=rn[:], in_=rn[:], func=AF.Sqrt), reads=[brn], writes=[brn])
                P.op("dve", lambda E: E.reciprocal(out=rn[:], in_=rn[:]), reads=[brn], writes=[brn])
                if ct < 2:
                    P.op("dve", lambda E, ct=ct: E.scalar_tensor_tensor(out=qk[:, ct, :], in0=act[:, ct, :], scalar=float(128 ** -0.5), in1=rn[:],
                                                                        op0=ALU.mult, op1=ALU.mult), reads=[bact[ct], brn], writes=[bqk[ct]])
                else:
                    P.op("dve", lambda E, ct=ct: E.tensor_tensor(out=qk[:, ct, :], in0=act[:, ct, :], in1=rn[:], op=ALU.mult),
                         reads=[bact[ct], brn], writes=[bqk[ct]])
            pa, bpa = GP[2]
            fns = [(lambda E, kt=kt, pa=pa: E.matmul(pa[0:2, :], lhsT=wb[:, kt, 0:2], rhs=hT[:, kt, :], start=(kt == 0), stop=(kt == 7))) for kt in range(8)]
            P.mm_group(fns, reads=[bwb, bhT], writes=[bpa])
            P.op("act", lambda E, pa=pa: E.activation(out=brow[:], in_=pa[0:2, :], func=AF.Sigmoid), reads=[bpa], writes=[bbrow])
            pa2, bpa2 = GP[3]
            fns = [(lambda E, kt=kt, pa2=pa2: E.matmul(pa2[0:2, :], lhsT=wa[:, kt, 0:2], rhs=hT[:, kt, :], start=(kt == 0), stop=(kt == 7))) for kt in range(8)]
            P.mm_group(fns, reads=[bwa, bhT], writes=[bpa2])
            P.op("act", lambda E, pa2=pa2: E.activation(out=grow[:], in_=pa2[0:2, :], func=AF.Exp, bias=dtb[:, 0:1], scale=1.0), reads=[bpa2, bdtb], writes=[bgrow])
            P.op("act", lambda E: E.activation(out=grow[:], in_=grow[:], func=AF.Ln, bias=1.0, scale=1.0), reads=[bgrow], writes=[bgrow])
            P.op("dve", lambda E: E.tensor_scalar(out=grow[:], in0=grow[:], scalar1=negA[:, 0:1], scalar2=None, op0=ALU.mult), reads=[bgrow, bnegA], writes=[bgrow])
            P.op("dve", lambda E: E.tensor_tensor_scan(out=gcrow[:], data0=cmask[:], data1=grow[:], initial=0.0, op0=ALU.mult, op1=ALU.add),
                 reads=[bcmask, bgrow], writes=[bgcrow])
            for h in range(2):
                pa, bpa = GP[h]
                P.op("pe", lambda E, h=h, pa=pa: E.matmul(pa[:], lhsT=sel[:, h, :], rhs=gcrow[:], start=True, stop=True), reads=[bsel, bgcrow], writes=[bpa])
                P.op("act", lambda E, h=h, pa=pa: E.copy(out=GCB[h][0][:], in_=pa[:]), reads=[bpa], writes=[GCB[h][1]])
                pa, bpa = GP[2 + h]
                P.op("pe", lambda E, h=h, pa=pa: E.matmul(pa[:], lhsT=sel[:, h, :], rhs=brow[:], start=True, stop=True), reads=[bsel, bbrow], writes=[bpa])
                P.op("act", lambda E, h=h, pa=pa: E.copy(out=BB[h][0][:], in_=pa[:]), reads=[bpa], writes=[BB[h][1]])
            for h in range(2):
                qT = qk[:, h, :]; bqT = bqk[h]; kT = qk[:, 2 + h, :]; bkT = bqk[2 + h]; vT = act[:, 4 + h, :]; bvT = bact[4 + h]
                gcb, bgcb = GCB[h]; bb, bbb = BB[h]
                H = heads[h]
                attnT, battnT = H["attnT"]; Y, bY = H["Y"]; EG, bEG = H["EG"]; qdec, bqdec = H["qdec"]
                Ybf, bYbf = H["Ybf"]
                bv, bbv = H["bv"]; kdec, bkdec = H["kdec"]; nbg, bnbg = H["nbg"]
                arg1, barg1 = m64["arg1"]; DT, bDT = m64["DT"]; Ds, bDs = m64["Ds"]
                tmp, btmp = m64["tmp"]; tmp2, btmp2 = m64["tmp2"]; BBm, bBBm = m64["BBm"]
                gccol, bgccol = small["gccol"]; bcol, bbcol = small["bcol"]; nbcol, bnbcol = small["nbcol"]
                elast, belast = small["elast"]; egc, begc = small["egc"]
                v3 = lambda t_: t_[:].rearrange("p (n f) -> p n f", f=64)
                i64b = I64.unsqueeze(1).to_broadcast([64, 8, 64])
                tt(v3(tmp), btmp, gcb[0:64, :].rearrange("p (n f) -> p n f", f=64), [bgcb, bc64], i64b, bc64, ALU.mult)
                P.op("dve", lambda E, tmp=tmp, gccol=gccol: E.tensor_reduce(out=gccol[:], in_=tmp[:].rearrange("p (n f) -> p n f", f=64), axis=AX.X, op=ALU.add), reads=[btmp], writes=[bgccol])
                tt(v3(tmp), btmp, bb[0:64, :].rearrange("p (n f) -> p n f", f=64), [bbb, bc64], i64b, bc64, ALU.mult)
                P.op("dve", lambda E, tmp=tmp, bcol=bcol: E.tensor_reduce(out=bcol[:], in_=tmp[:].rearrange("p (n f) -> p n f", f=64), axis=AX.X, op=ALU.add), reads=[btmp], writes=[bbcol])
                P.op("dve", lambda E: E.tensor_scalar(out=nbcol[:], in0=bcol[:], scalar1=-1.0, scalar2=None, op0=ALU.mult), reads=[bbcol], writes=[bnbcol])
                tt(v3(arg1), barg1, gcb[0:64, :].rearrange("p (n f) -> p n f", f=64), [bgcb, bgccol], gccol[:].unsqueeze(2).to_broadcast([64, 8, 64]), bgccol, ALU.subtract)
                tt(v3(DT), bDT, v3(arg1), [barg1, bc64], NEGU.unsqueeze(1).to_broadcast([64, 8, 64]), bc64, ALU.add)
                P.op("act", lambda E: E.activation(out=DT[:], in_=DT[:], func=AF.Exp), reads=[bDT], writes=[bDT])
                P.op("dve", lambda E: E.scalar_tensor_tensor(out=Ds[:].rearrange("p (n f) -> p n f", f=64), in0=arg1[:].rearrange("p (n f) -> p n f", f=64), scalar=-1.0,
                                                             in1=NEGLS.unsqueeze(1).to_broadcast([64, 8, 64]), op0=ALU.mult, op1=ALU.add), reads=[barg1, bc64], writes=[bDs])
                P.op("act", lambda E: E.activation(out=Ds[:], in_=Ds[:], func=AF.Exp), reads=[bDs], writes=[bDs])
                tt(v3(BBm), bBBm, bb[0:64, :].rearrange("p (n f) -> p n f", f=64), [bbb, bc64], NSU.unsqueeze(1).to_broadcast([64, 8, 64]), bc64, ALU.mult)
                pk, bpk = GP[0]; pq, bpq = GP[1]
                fns = [(lambda E, n=n, pk=pk, kT=kT: E.matmul(pk[0:64, n * 64:(n + 1) * 64], lhsT=kT[:, n * 64:(n + 1) * 64], rhs=kT[:, n * 64:(n + 1) * 64],
                                                              start=True, stop=True)) for n in range(8)]
                P.mm_group(fns, reads=[bkT], writes=[bpk])
                fns = [(lambda E, n=n, pq=pq, kT=kT, qT=qT: E.matmul(pq[0:64, n * 64:(n + 1) * 64], lhsT=kT[:, n * 64:(n + 1) * 64], rhs=qT[:, n * 64:(n + 1) * 64],
                                                                     start=True, stop=True)) for n in range(8)]
                P.mm_group(fns, reads=[bkT, bqT], writes=[bpq])
                tt(attnT[:], battnT, pq[0:64, :], [bpq, bDT], DT[:], bDT, ALU.mult)
                Pc, bPc = m64["Pa"]; Pn, bPn = m64["Pb"]; Qc, bQc = m64["Qa"]; Qn, bQn = m64["Qb"]
                tt(tmp[:], btmp, pk[0:64, :], [bpk, bDT], DT[:], bDT, ALU.mult)
                tt(Qc[:], bQc, tmp[:], [btmp, bBBm], BBm[:], bBBm, ALU.mult)
                tt(tmp2[:], btmp2, pk[0:64, :], [bpk, bDs], Ds[:], bDs, ALU.mult)
                tt(v3(Pc), bPc, v3(tmp2), [btmp2, bnbcol], nbcol[:].unsqueeze(2).to_broadcast([64, 8, 64]), bnbcol, ALU.mult)
                tt(v3(Y), bY, v3(Qc), [bQc, bc64], i64b, bc64, ALU.add)
                P.op("act", lambda E, Ybf=Ybf, Y=Y: E.copy(out=Ybf[:], in_=Y[:]), reads=[bY], writes=[bYbf])
                for j in range(5):
                    pP, bpP = GP[2]; pQ, bpQ = GP[3]
                    fns = [(lambda E, n=n, pP=pP, Qc=Qc, Pc=Pc: E.matmul(pP[0:64, n * 64:(n + 1) * 64], lhsT=Qc[:, n * 64:(n + 1) * 64], rhs=Pc[:, n * 64:(n + 1) * 64],
                                                                         start=True, stop=True)) for n in range(8)]
                    P.mm_group(fns, reads=[bQc, bPc], writes=[bpP])
                    if j < 4:
                        fns = [(lambda E, n=n, pQ=pQ, Qc=Qc, Pc=Pc: E.matmul(pQ[0:64, n * 64:(n + 1) * 64], lhsT=Pc[:, n * 64:(n + 1) * 64], rhs=Qc[:, n * 64:(n + 1) * 64],
                                                                             start=True, stop=True)) for n in range(8)]
                        P.mm_group(fns, reads=[bQc, bPc], writes=[bpQ])
                    P.op("act", lambda E, Pn=Pn, pP=pP: E.copy(out=Pn[:], in_=pP[0:64, :]), reads=[bpP], writes=[bPn])
                    if j < 4:
                        P.op("dve", lambda E, Qn=Qn, pQ=pQ: E.tensor_copy(out=Qn[:], in_=pQ[0:64, :]), reads=[bpQ], writes=[bQn])
                    pY, bpY = GP[0]
                    fns = [(lambda E, n=n, pY=pY, Pn=Pn, Ybf=Ybf: E.matmul(pY[0:64, n * 64:(n + 1) * 64], lhsT=Pn[:, n * 64:(n + 1) * 64], rhs=Ybf[:, n * 64:(n + 1) * 64],
                                                                         start=True, stop=True)) for n in range(8)]
                    P.mm_group(fns, reads=[bPn, bYbf], writes=[bpY])
                    tt(Y[:], bY, Y[:], [bY, bpY], pY[0:64, :], bpY, ALU.add)
                    P.op("act", lambda E, Ybf=Ybf, Y=Y: E.copy(out=Ybf[:], in_=Y[:]), reads=[bY], writes=[bYbf])
                    Pc, bPc, Pn, bPn = Pn, bPn, Pc, bPc
                    Qc, bQc, Qn, bQn = Qn, bQn, Qc, bQc
                fns = [(lambda E, n=n, vT=vT: E.transpose(out=pT[0:64, n * 128:(n + 1) * 128], in_=vT[:, n * 64:(n + 1) * 64], identity=idf[:])) for n in range(8)]
                P.mm_group(fns, reads=[bvT, bidf], writes=[bpT])
                tt(bv[:], bbv, pT[0:64, :].rearrange("p (n d) -> p n d", d=128), [bpT, bbcol], bcol[:].unsqueeze(2).to_broadcast([64, 8, 128]), bbcol, ALU.mult)
                tt(elast[:], belast, gcb[0:64, :].rearrange("p (n f) -> p n f", f=64)[:, :, 63], [bgcb, bgccol], gccol[:], bgccol, ALU.subtract)
                P.op("act", lambda E: E.activation(out=elast[:], in_=elast[:], func=AF.Exp), reads=[belast], writes=[belast])
                fns = [(lambda E, n=n, kT=kT: E.transpose(out=pT[0:64, n * 128:(n + 1) * 128], in_=kT[:, n * 64:(n + 1) * 64], identity=idf[:])) for n in range(8)]
                P.mm_group(fns, reads=[bkT, bidf], writes=[bpT])
                tt(kdec[:], bkdec, pT[0:64, :].rearrange("p (n d) -> p n d", d=128), [bpT, belast], elast[:].unsqueeze(2).to_broadcast([64, 8, 128]), belast, ALU.mult)
                P.op("act", lambda E, gcb=gcb, EG=EG: E.activation(out=EG[:], in_=gcb[:], func=AF.Exp), reads=[bgcb], writes=[bEG])
                tt(qdec[:], bqdec, qT, [bqT, bEG], EG[:], bEG, ALU.mult)
                P.op("act", lambda E: E.activation(out=egc[:], in_=gccol[:], func=AF.Exp), reads=[bgccol], writes=[begc])
                P.op("dve", lambda E, nbg=nbg: E.scalar_tensor_tensor(out=nbg[:], in0=egc[:], scalar=-1.0, in1=bcol[:], op0=ALU.mult, op1=ALU.mult),
                     reads=[begc, bbcol], writes=[bnbg])
            banks = [(GP[0], GP[1], GP[2]), (GP[3], GP[4], (pT, bpT))]
            for n in range(8):
                cs = slice(n * 64, (n + 1) * 64)
                for h in range(2):
                    kT = qk[:, 2 + h, :]; bkT = bqk[2 + h]
                    H = heads[h]; S, bS = Sst[h]
                    attnT, battnT = H["attnT"]; Y, bY = H["Ybf"]; EG, bEG = H["EG"]; qdec, bqdec = H["qdec"]
                    bv, bbv = H["bv"]; kdec, bkdec = H["kdec"]; nbg, bnbg = H["nbg"]
                    vnew, bvnew = H["vnew"]; rhs2, brhs2 = H["rhs2"]; osb, bosb = H["osb"]
                    (KSO, bKSO), (Vb, bVb), (Sb, bSb) = banks[h]
                    P.op("pe", lambda E, cs=cs, kT=kT, S=S, KSO=KSO: E.matmul(KSO[0:64, 0:128], lhsT=kT[:, cs], rhs=S[:], start=True, stop=True),
                         reads=[bkT, bS], writes=[bKSO])
                    P.op("dve", lambda E, n=n, KSO=KSO, rhs2=rhs2, nbg=nbg, bv=bv: E.scalar_tensor_tensor(
                        out=rhs2[:], in0=KSO[0:64, 0:128], scalar=nbg[:, n:n + 1], in1=bv[:, n, :], op0=ALU.mult, op1=ALU.add),
                        reads=[bKSO, bnbg, bbv], writes=[brhs2])
                    P.op("pe", lambda E, cs=cs, Y=Y, Vb=Vb, rhs2=rhs2: E.matmul(Vb[0:64, 0:128], lhsT=Y[:, cs], rhs=rhs2[:], start=True, stop=True),
                         reads=[bY, brhs2], writes=[bVb])
                    P.op("act", lambda E, vnew=vnew, Vb=Vb: E.copy(out=vnew[:], in_=Vb[0:64, 0:128]), reads=[bVb], writes=[bvnew])
                    fns = [lambda E, cs=cs, S=S, KSO=KSO, qdec=qdec: E.matmul(KSO[64:128, 0:128], lhsT=qdec[:, cs], rhs=S[:], start=True, stop=False),
                           lambda E, cs=cs, KSO=KSO, attnT=attnT, vnew=vnew: E.matmul(KSO[64:128, 0:128], lhsT=attnT[:, cs], rhs=vnew[:], start=False, stop=True)]
                    P.mm_group(fns, reads=[bqdec, bS, battnT, bvnew], writes=[bKSO])
                    P.op("pe", lambda E, n=n, Sb=Sb, kdec=kdec, vnew=vnew: E.matmul(Sb[:, 0:128], lhsT=kdec[:, n, :], rhs=vnew[:], start=True, stop=True),
                         reads=[bkdec, bvnew], writes=[bSb])
                    P.op("dve", lambda E, n=n, S=S, EG=EG, Sb=Sb: E.scalar_tensor_tensor(out=S[:], in0=S[:], scalar=EG[:, n * 64 + 63:n * 64 + 64], in1=Sb[:, 0:128],
                                                                                         op0=ALU.mult, op1=ALU.add), reads=[bS, bEG, bSb], writes=[bS])
                    P.op("act", lambda E, n=n, osb=osb, KSO=KSO: E.copy(out=osb[64:128, n, :], in_=KSO[64:128, 0:128]), reads=[bKSO], writes=[bosb])
            for h in range(2):
                osb, bosb = heads[h]["osb"]
                P.dma("sp", o_d[s_ * 512:(s_ + 1) * 512, h * 128:(h + 1) * 128].rearrange("(n c) d -> c n d", c=64), osb[64:128, :, :], reads=[bosb],
                      writes=[fz["obuf_of"](s_) if fz else bo])
            if fz:
                fz["after_chunk"](s_)
        if fz:
            barrier(P)
        else:
            P.finish([bo])
    return nc


def run_L1a(inp):
    nc = _get("L1a", build_L1a)
    c64, cmask, sel = _gdn_consts()
    w_in = inp["w_in_even"][0]
    conv = inp["conv_qkv"][0]
    ones = np.ones((128, 128), np.float32)
    maps = []
    for c in range(8):
        b, r = divmod(c, 4)
        cols = np.concatenate([np.arange(256 * r, 256 * r + 256), 1024 + np.arange(256 * r, 256 * r + 256), 2048 + np.arange(256 * r, 256 * r + 256)])
        maps.append({"x": np.ascontiguousarray(inp["x"][b]), "npre": np.ascontiguousarray(inp["norm_pre"][0]),
                     "w": np.ascontiguousarray(w_in[:, cols]), "wb": np.ascontiguousarray(w_in[:, 4096 + 2 * r:4096 + 2 * r + 2]),
                     "wa": np.ascontiguousarray(w_in[:, 4104 + 2 * r:4104 + 2 * r + 2]), "conv": np.ascontiguousarray(conv[:, cols]),
                     "alog": np.ascontiguousarray(inp["a_log"][0, 2 * r:2 * r + 2]), "dtb": np.ascontiguousarray(inp["dt_bias"][0, 2 * r:2 * r + 2]),
                     "ident": _IDENT, "c64": c64, "cmask": cmask, "sel": sel, "ones": ones})
    res = run_bass_kernel_spmd(nc, maps, core_ids=list(range(8)))
    S_ = inp["x"].shape[1]
    o = np.empty((2, S_, 1024), np.float32)
    for c in range(8):
        b, r = divmod(c, 4)
        o[b, :, 256 * r:256 * (r + 1)] = res.results[c]["o"]
    return o


def kernel_unfused(**inputs):
    inp = {k: np.asarray(v) for k, v in inputs.items()}
    o = run_L1a(inp)
    ys = run_L1b(inp)
    x1 = run_L2(inp, o, ys)
    out = run_L3(inp, x1)
    return out.astype(np.float32)


def build_fused():
    nc = bass.Bass("TRN2", target_bir_lowering=False)
    x_full = nc.dram_tensor("x", [8192, 1024], F32, kind="ExternalInput").ap()
    ident_d = nc.dram_tensor("ident", [128, 128], F32, kind="ExternalInput").ap()
    npre0_d = nc.dram_tensor("npre0", [1024], F32, kind="ExternalInput").ap()
    gidx_d = nc.dram_tensor("gidx", [128, 17, 4], I32, kind="ExternalInput").ap()
    out_d = nc.dram_tensor("out", [2048, 1024], F32, kind="ExternalOutput").ap()
    ag_in = [nc.dram_tensor("ag_in%d" % i, [8192, 256], F32) for i in range(2)]
    ag_out = [nc.dram_tensor("ag_out%d" % i, [4 * 8192, 256], F32) for i in range(2)]
    x1s = nc.dram_tensor("x1s", [2176, 1024], F32)
    GROUPS = [[0, 1, 2, 3], [4, 5, 6, 7]]
    with ExitStack() as st:
        C = Ctx(nc, st); P = C.P
        csem = st.enter_context(nc.semaphore("csem"))
        bag_out = Buf("ag_out"); bx1s = Buf("x1s", multi=True); bout = Buf("out", multi=True)
        bo_ch = [Buf("o_ch%d" % k, multi=True) for k in range(16)]
        by_jt = [Buf("y_jt%d" % k, multi=True) for k in range(4)]
        ncc = [0]

        def emit_cc(which, k, inbuf):
            P._deps("pool", [inbuf], [])
            P.streams["pool"].append(lambda E, which=which, k=k: E.collective_compute(
                "AllGather", ALU.bypass, replica_groups=GROUPS,
                ins=[ag_in[which].ap()[k * 512:(k + 1) * 512, :].opt()], outs=[ag_out[which].ap()[k * 2048:(k + 1) * 2048, :].opt()]).then_inc(csem))
            ncc[0] += 1

        share1 = {"x": x_full, "ident": ident_d, "npre": npre0_d}

        def after_jt(jt):
            for k in range(4 * jt, 4 * jt + 4):
                emit_cc(1, k, by_jt[jt])

        with ExitStack() as stU:
            CU = Ctx(nc, stU, P, "u_")
            uext = CU.sb("uTp", [128, 2, 16, 512], BF16)
            build_L1a(8192, fz={"nc": nc, "P": P, "pfx": "a_", "share": share1, "out": ag_in[0].ap(), "uTp": uext,
                                "obuf_of": lambda s_: bo_ch[s_], "after_chunk": lambda s_: emit_cc(0, s_, bo_ch[s_])})
            build_L1b(8192, fz={"nc": nc, "P": P, "pfx": "b_", "share": share1, "out": ag_in[1].ap(), "uTp": uext,
                                "obuf_of": lambda jt: by_jt[jt], "after_chunk": after_jt})
        P.streams["pool"].append(lambda E: E.wait_ge(csem, ncc[0]))
        gidx, bgidx = C.sb("gidx", [128, 17, 4], I32)
        P.dma("sp", gidx[:], gidx_d, writes=[bgidx])
        P.op("pool", lambda E: E.nop(), reads=[], writes=[bag_out])

        def gather(P_, ld, bld, tile, part):
            for i in range(4):
                P_.dma_ind("pool", ld[:, i * 256:(i + 1) * 256], ag_out[part].ap(), gidx[:, tile, i:i + 1], reads=[bag_out, bgidx], writes=[bld])

        share2 = {"ident": ident_d, "npre": npre0_d, "o": None, "ys": None}
        build_L2(2176, fz={"nc": nc, "P": P, "pfx": "c_", "share": share2, "out": x1s.ap(), "obuf": bx1s, "gather": gather})
        share3 = {"ident": ident_d, "x": x1s.ap()}
        build_L3(2048, fz={"nc": nc, "P": P, "pfx": "d_", "share": share3, "out": out_d, "obuf": bout, "xbuf": bx1s})
        P.finish([bout])
    return nc


def _gidx(r):
    g = np.zeros((128, 17, 4), np.int32)
    p = np.arange(128)[:, None, None]
    tile = np.arange(17)[None, :, None]
    src = np.arange(4)[None, None, :]
    tok = np.clip(2048 * r - 128 + tile * 128 + p, 0, 8191)
    g[:] = ((tok // 512) * 4 + src) * 512 + tok % 512
    return g


def kernel(**inputs):
    inp = {k: np.ascontiguousarray(np.asarray(v)) for k, v in inputs.items()}
    nc = _get("fused", build_fused)
    c64, cmask, sel = _gdn_consts()
    mk, idm = _s5_consts()
    ones = np.ones((128, 128), np.float32)
    w_in = inp["w_in_even"][0]
    conv = inp["conv_qkv"][0]
    wz = np.ascontiguousarray(np.concatenate([w_in[:, 3072:4096], w_in[:, 5136:6160]], axis=1))
    maps = []
    for c in range(8):
        b, r = divmod(c, 4)
        cols = np.concatenate([np.arange(256 * r, 256 * r + 256), 1024 + np.arange(256 * r, 256 * r + 256), 2048 + np.arange(256 * r, 256 * r + 256)])
        gs = slice(16 * r, 16 * r + 16)
        xq = np.zeros((2176, 1024), np.float32)
        xq[128:] = inp["x"][b, 2048 * r:2048 * (r + 1)]
        if r > 0:
            xq[:128] = inp["x"][b, 2048 * r - 128:2048 * r]
        m = {"x": inp["x"][b], "ident": _IDENT, "npre0": inp["norm_pre"][0], "gidx": _gidx(r),
             "a_w": np.ascontiguousarray(w_in[:, cols]), "a_wb": np.ascontiguousarray(w_in[:, 4096 + 2 * r:4096 + 2 * r + 2]),
             "a_wa": np.ascontiguousarray(w_in[:, 4104 + 2 * r:4104 + 2 * r + 2]), "a_conv": np.ascontiguousarray(conv[:, cols]),
             "a_alog": np.ascontiguousarray(inp["a_log"][0, 2 * r:2 * r + 2]), "a_dtb": np.ascontiguousarray(inp["dt_bias"][0, 2 * r:2 * r + 2]),
             "a_c64": c64, "a_cmask": cmask, "a_sel": sel, "a_ones": ones,
             "a_wu": np.ascontiguousarray(w_in[:, 4112 + 256 * r:4112 + 256 * (r + 1)]),
             "b_wu": np.ascontiguousarray(w_in[:, 4112 + 256 * r:4112 + 256 * (r + 1)]),
             "b_lre": np.ascontiguousarray(inp["s5_lam_re"][0, gs]), "b_lim": np.ascontiguousarray(inp["s5_lam_im"][0, gs]),
             "b_bre": np.ascontiguousarray(inp["s5_b_re"][0, gs]), "b_bim": np.ascontiguousarray(inp["s5_b_im"][0, gs]),
             "b_cre": np.ascontiguousarray(inp["s5_c_re"][0, gs]), "b_cim": np.ascontiguousarray(inp["s5_c_im"][0, gs]),
             "b_ldt": np.ascontiguousarray(inp["s5_log_dt"][0, gs]), "b_dd": np.ascontiguousarray(inp["s5_d"][0, 256 * r:256 * (r + 1)]),
             "b_taus": TAUS, "b_mk": mk, "b_idm": idm,
             "c_x": xq, "c_wz": wz, "c_wglu": inp["w_glu"][0], "c_wout": inp["w_out_even"][0], "c_npost": inp["norm_post"][0],
             "c_gnw": inp["gdn_norm_w"][0],
             "d_win": inp["w_in_odd"][0], "d_wout": inp["w_out_odd"][0], "d_conv": inp["conv_short"][0],
             "d_npre": inp["norm_pre"][1], "d_npost": inp["norm_post"][1]}
        maps.append(m)
    res = run_bass_kernel_spmd(nc, maps, core_ids=list(range(8)))
    out = np.empty((2, 8192, 1024), np.float32)
    for c in range(8):
        b, r = divmod(c, 4)
        out[b, r * 2048:(r + 1) * 2048] = res.results[c]["out"]
    return out
```

```python
from contextlib import ExitStack
import numpy as np
import concourse.bass as bass
import concourse.mybir as mybir
from concourse.bass_utils import run_bass_kernel_spmd

F32 = mybir.dt.float32
BF16 = mybir.dt.bfloat16
AF = mybir.ActivationFunctionType
ALU = mybir.AluOpType
AX = mybir.AxisListType

NDS = 12


class Buf:
    __slots__ = ("name", "w", "r", "multi")

    def __init__(self, name, multi=False):
        self.name = name
        self.w = [] if multi else None
        self.r = []
        self.multi = multi


class Prog:
    ENG = ("pe", "act", "dve", "pool", "sp")

    def __init__(self, nc, stack):
        self.nc = nc
        self.stack = stack
        self.streams = {e: [] for e in self.ENG}
        self.cnt = {e: 0 for e in self.ENG}
        self.sem = {e: stack.enter_context(nc.semaphore("s_" + e)) for e in self.ENG}
        self.seen = {e: {} for e in self.ENG}
        self.dcnt = {e: 0 for e in self.ENG}
        self.dsem = {}
        for e in ("sp", "pool", "act"):
            self.dsem[e] = [stack.enter_context(nc.semaphore("d_%s%d" % (e, i))) for i in range(NDS)]
        self.same_engine_sync = True
        self.nwaits = 0

    def _wait(self, eng, tok):
        if tok is None:
            return
        kind = tok[0]
        if kind == "c":
            _, e2, n = tok
            if e2 == eng and (eng == "pe" or not self.same_engine_sync):
                return
            key = e2
            if self.seen[eng].get(key, 0) >= n:
                return
            self.seen[eng][key] = n
            sem = self.sem[e2]
            self.streams[eng].append(lambda E, sem=sem, n=n: E.wait_ge(sem, n))
            self.nwaits += 1
        else:
            _, q, slot, val = tok
            key = ("d", q, slot)
            if self.seen[eng].get(key, 0) >= val:
                return
            self.seen[eng][key] = val
            sem = self.dsem[q][slot]
            self.streams[eng].append(lambda E, sem=sem, val=val: E.wait_ge(sem, val))
            self.nwaits += 1

    def _deps(self, eng, reads, writes):
        for b in reads:
            if b.multi:
                for t in b.w:
                    self._wait(eng, t)
            else:
                self._wait(eng, b.w)
        for b in writes:
            if not b.multi:
                self._wait(eng, b.w)
            for t in b.r:
                self._wait(eng, t)

    def _commit(self, tok, reads, writes):
        for b in writes:
            if b.multi:
                b.w.append(tok)
            else:
                b.w = tok
            b.r = []
        for b in reads:
            if b not in writes:
                b.r.append(tok)

    def op(self, eng, fn, reads=(), writes=()):
        reads = list(reads)
        writes = list(writes)
        self._deps(eng, reads, writes)
        self.cnt[eng] += 1
        n = self.cnt[eng]
        sem = self.sem[eng]
        self.streams[eng].append(lambda E, fn=fn, sem=sem: fn(E).then_inc(sem, 1))
        tok = ("c", eng, n)
        self._commit(tok, reads, writes)
        return tok

    def mm_group(self, fns, reads=(), writes=()):
        eng = "pe"
        reads = list(reads)
        writes = list(writes)
        self._deps(eng, reads, writes)
        self.cnt[eng] += 1
        n = self.cnt[eng]
        sem = self.sem[eng]
        for fn in fns[:-1]:
            self.streams[eng].append(lambda E, fn=fn: fn(E))
        last = fns[-1]
        self.streams[eng].append(lambda E, fn=last, sem=sem: fn(E).then_inc(sem, 1))
        tok = ("c", eng, n)
        self._commit(tok, reads, writes)
        return tok

    def dma(self, q, out_ap, in_ap, reads=(), writes=()):
        reads = list(reads)
        writes = list(writes)
        self._deps(q, reads, writes)
        j = self.dcnt[q]
        self.dcnt[q] += 1
        slot = j % NDS
        val = 16 * (j // NDS + 1)
        if j >= NDS:
            self._wait(q, ("d", q, slot, val - 16))
        sem = self.dsem[q][slot]
        self.streams[q].append(
            lambda E, o=out_ap, i=in_ap, sem=sem: E.dma_start(out=o, in_=i).then_inc(sem, 16))
        tok = ("d", q, slot, val)
        self._commit(tok, reads, writes)
        return tok

    def dma_ind(self, q, out_ap, table_ap, idx_ap, reads=(), writes=()):
        reads = list(reads)
        writes = list(writes)
        self._deps(q, reads, writes)
        j = self.dcnt[q]
        self.dcnt[q] += 1
        slot = j % NDS
        val = 16 * (j // NDS + 1)
        if j >= NDS:
            self._wait(q, ("d", q, slot, val - 16))
        sem = self.dsem[q][slot]
        self.streams[q].append(
            lambda E, o=out_ap, t=table_ap, i=idx_ap, sem=sem: E.indirect_dma_start(
                out=o, out_offset=None, in_=t, in_offset=bass.IndirectOffsetOnAxis(ap=i, axis=0)).then_inc(sem, 16))
        tok = ("d", q, slot, val)
        self._commit(tok, reads, writes)
        return tok

    def finish(self, final_bufs):
        for b in final_bufs:
            for t in (b.w if b.multi else [b.w]):
                self._wait("sp", t)
        nc = self.nc
        streams = self.streams
        with nc.Block() as block:
            @block.tensor
            def _(E):
                for f in streams["pe"]:
                    f(E)

            @block.scalar
            def _(E):
                for f in streams["act"]:
                    f(E)

            @block.vector
            def _(E):
                for f in streams["dve"]:
                    f(E)

            @block.gpsimd
            def _(E):
                for f in streams["pool"]:
                    f(E)

            @block.sync
            def _(E):
                for f in streams["sp"]:
                    f(E)


class Ctx:
    def __init__(self, nc, st, P=None, pfx=""):
        self.nc = nc
        self.st = st
        self.pfx = pfx
        if P is None:
            st.enter_context(nc.allow_non_contiguous_dma(reason="small parameter loads / layout transforms"))
            P = Prog(nc, st)
        self.P = P

    def sb(self, name, shape, dt=F32):
        t = self.st.enter_context(self.nc.sbuf_tensor("sb_" + self.pfx + name, shape, dt))
        return t, Buf(name)

    def ps(self, name, shape, dt=F32):
        t = self.st.enter_context(self.nc.psum_tensor("ps_" + self.pfx + name, shape, dt))
        return t, Buf(name)


def bcast_row_load(C, name, dram_vec, n, q="sp"):
    t, b = C.sb(name, [128, n])
    C.P.dma(q, t[:], dram_vec.partition_broadcast(128), writes=[b])
    return t, b


def make_ident(C, dram_ident):
    idf, bidf = C.sb("identf", [128, 128])
    C.P.dma("sp", idf[:], dram_ident, writes=[bidf])
    idb, bidb = C.sb("identb", [128, 128], BF16)
    C.P.op("dve", lambda E: E.tensor_copy(out=idb[:], in_=idf[:]), reads=[bidf], writes=[bidb])
    return idf, bidf, idb, bidb


def rms_rstd(C, src, bsrc, ncols, junk, bjunk, ss, bss, eps=1e-6):
    P = C.P
    P.op("act", lambda E: E.activation(out=junk, in_=src, func=AF.Square, accum_out=ss[:, 0:1]),
         reads=[bsrc], writes=[bjunk, bss])
    P.op("act", lambda E: E.activation(out=ss[:, 0:1], in_=ss[:, 0:1], func=AF.Sqrt, bias=float(eps), scale=float(1.0 / ncols)),
         reads=[bss], writes=[bss])
    P.op("dve", lambda E: E.reciprocal(out=ss[:, 0:1], in_=ss[:, 0:1]), reads=[bss], writes=[bss])


def transpose8(C, src_bf, bsrc, idb, bidb, ptr, bptr, dst3, bdst, eng="act"):
    P = C.P
    fns = [(lambda E, kt=kt: E.transpose(out=ptr[:, kt * 128:(kt + 1) * 128], in_=src_bf[:, kt * 128:(kt + 1) * 128],
                                         identity=idb[:])) for kt in range(8)]
    P.mm_group(fns, reads=[bsrc, bidb], writes=[bptr])
    src3 = ptr[:].rearrange("p (k t) -> p k t", k=8)
    if eng == "act":
        P.op("act", lambda E: E.copy(out=dst3, in_=src3), reads=[bptr], writes=[bdst])
    else:
        P.op("dve", lambda E: E.tensor_copy(out=dst3, in_=src3), reads=[bptr], writes=[bdst])


def outproj_post(C, catT, bcat, nkt, wout, bwout, t, xres, bxres, npw, bnpw, pso, bpso, yo, byo, junk, bjunk, ss, bss,
                 out_dram_rows, bout):
    P = C.P
    for hh in range(2):
        fns = [(lambda E, kt=kt, hh=hh: E.matmul(pso[hh][:], lhsT=catT[:, kt, t * 128:(t + 1) * 128],
                                                 rhs=wout[:, kt, hh * 512:(hh + 1) * 512],
                                                 start=(kt == 0), stop=(kt == nkt - 1))) for kt in range(nkt)]
        P.mm_group(fns, reads=[bcat, bwout], writes=[bpso[hh]])
        P.op("act", lambda E, hh=hh: E.copy(out=yo[:, hh * 512:(hh + 1) * 512], in_=pso[hh][:]),
             reads=[bpso[hh]], writes=[byo])
    rms_rstd(C, yo[:], byo, 1024, junk[:], bjunk, ss, bss)
    P.op("dve", lambda E: E.scalar_tensor_tensor(out=yo[:], in0=yo[:], scalar=ss[:, 0:1], in1=npw[:],
                                                 op0=ALU.mult, op1=ALU.mult), reads=[byo, bss, bnpw], writes=[byo])
    P.op("dve", lambda E: E.tensor_tensor(out=yo[:], in0=yo[:], in1=xres, op=ALU.add), reads=[byo, bxres], writes=[byo])
    P.dma("sp", out_dram_rows, yo[:], reads=[byo], writes=[bout])


def load_w_bf16(C, name, dram_w, kt_n, ncols, chunk=2048):
    w, bw = C.sb(name, [128, kt_n, ncols], BF16)
    src = dram_w.rearrange("(k p) c -> p k c", p=128)
    for kt in range(kt_n):
        for c0 in range(0, ncols, chunk):
            c1 = min(ncols, c0 + chunk)
            C.P.dma("pool", w[:, kt, c0:c1], src[:, kt, c0:c1], writes=[bw])
    return w, bw


def build_L2(ntok=2048, fz=None):
    nc = fz["nc"] if fz else bass.Bass("TRN2", target_bir_lowering=False)
    pfx = fz["pfx"] if fz else ""

    def D(name, shape):
        if fz and name in fz["share"]:
            return fz["share"][name]
        return nc.dram_tensor(pfx + name, shape, F32, kind="ExternalInput").ap()
    x_d = D("x", [ntok, 1024]); o_d = D("o", [ntok, 1024]); ys_d = D("ys", [ntok, 1024])
    wz_d = D("wz", [1024, 2048]); wglu_d = D("wglu", [1024, 1024]); wout_d = D("wout", [2048, 1024])
    npre_d = D("npre", [1024]); npost_d = D("npost", [1024]); gnw_d = D("gnw", [128]); ident_d = D("ident", [128, 128])
    out_d = fz["out"] if fz else nc.dram_tensor("out", [ntok, 1024], F32, kind="ExternalOutput").ap()
    NT = 512
    with ExitStack() as st:
        C = Ctx(nc, st, fz["P"], pfx) if fz else Ctx(nc, st); P = C.P
        idf, bidf, idb, bidb = make_ident(C, ident_d)
        npre, bnpre = bcast_row_load(C, "npre", npre_d, 1024)
        npost, bnpost = bcast_row_load(C, "npost", npost_d, 1024)
        gnw, bgnw = bcast_row_load(C, "gnw", gnw_d, 128)
        wz, bwz = load_w_bf16(C, "wz", wz_d, 8, 2048)
        wglu, bwglu = load_w_bf16(C, "wglu", wglu_d, 8, 1024)
        wout, bwout = load_w_bf16(C, "wout", wout_d, 16, 1024)
        xt4, bxt4 = C.sb("xt4", [128, 4, 1024]); bxt = [Buf("xt%d" % i) for i in range(4)]
        ldo = [C.sb("ldo%d" % i, [128, 1024]) for i in range(2)]
        ldy = [C.sb("ldy%d" % i, [128, 1024]) for i in range(2)]
        sq, bsq = C.sb("sq", [128, 1024])
        hn, bhn = C.sb("hn", [128, 1024], BF16)
        ss, bss = C.sb("ss", [128, 1])
        ss8, bss8 = C.sb("ss8", [128, 8])
        hT, bhT = C.sb("hT", [128, 8, NT], BF16)
        oT, boT = C.sb("oT", [128, 8, NT], BF16)
        yT, byT = C.sb("yT", [128, 8, NT], BF16)
        gz, bgz = C.sb("gz", [128, 8, NT], BF16)
        sg, bsg = C.sb("sg", [128, NT], BF16)
        catT, bcat = C.sb("catT", [128, 16, NT], BF16)
        yo, byo = C.sb("yo", [128, 1024])
        ptr, bptr = C.ps("ptr", [128, 1024], BF16)
        pmm = []; bpmm = []
        for i in range(4):
            t_, b_ = C.ps("pmm%d" % i, [128, 512]); pmm.append(t_); bpmm.append(b_)
        pso = []; bpso = []
        for i in range(2):
            t_, b_ = C.ps("pso%d" % i, [128, 512]); pso.append(t_); bpso.append(b_)
        bout = fz["obuf"] if fz else Buf("out", multi=True)
        if fz:
            sts = [(0, 128)] + [(128 + i * NT, NT) for i in range((ntok - 128) // NT)]
        else:
            sts = [(i * NT, NT) for i in range(ntok // NT)]
        tile_r0 = [t0_ + t_ * 128 for (t0_, n_) in sts for t_ in range(n_ // 128)]

        def issue_loads(ti):
            r0_ = tile_r0[ti]
            lo, blo = ldo[ti % 2]; ly, bly = ldy[ti % 2]
            if fz:
                fz["gather"](P, lo, blo, r0_ // 128, 0)
                fz["gather"](P, ly, bly, r0_ // 128, 1)
            else:
                P.dma("sp", lo[:], o_d[r0_:r0_ + 128, :], writes=[blo])
                P.dma("sp", ly[:], ys_d[r0_:r0_ + 128, :], writes=[bly])

        issue_loads(0)
        for (t0, n) in sts:
            ntl = n // 128
            for t in range(ntl):
                r0 = t0 + t * 128
                ti = tile_r0.index(r0)
                if ti + 1 < len(tile_r0):
                    issue_loads(ti + 1)
                P.dma("sp", xt4[:, t, :], x_d[r0:r0 + 128, :], writes=[bxt[t]])
                rms_rstd(C, xt4[:, t, :], bxt[t], 1024, sq[:], bsq, ss, bss)
                P.op("dve", lambda E, t=t: E.scalar_tensor_tensor(out=hn[:], in0=xt4[:, t, :], scalar=ss[:, 0:1], in1=npre[:],
                                                                  op0=ALU.mult, op1=ALU.mult), reads=[bxt[t], bss, bnpre], writes=[bhn])
                transpose8(C, hn, bhn, idb, bidb, ptr, bptr, hT[:, :, t * 128:(t + 1) * 128], bhT, eng="act")
                ld, bld = ldo[ti % 2]
                P.op("act", lambda E, ld=ld: E.activation(out=sq[:], in_=ld[:], func=AF.Square), reads=[bld], writes=[bsq])
                P.op("dve", lambda E: E.tensor_reduce(out=ss8[:], in_=sq[:].rearrange("p (h d) -> p h d", h=8), axis=AX.X, op=ALU.add),
                     reads=[bsq], writes=[bss8])
                P.op("dve", lambda E: E.tensor_scalar(out=ss8[:], in0=ss8[:], scalar1=1.0 / 128, scalar2=1e-6, op0=ALU.mult, op1=ALU.add),
                     reads=[bss8], writes=[bss8])
                P.op("act", lambda E: E.activation(out=ss8[:], in_=ss8[:], func=AF.Sqrt), reads=[bss8], writes=[bss8])
                P.op("dve", lambda E: E.reciprocal(out=ss8[:], in_=ss8[:]), reads=[bss8], writes=[bss8])
                P.op("dve", lambda E, ld=ld: E.tensor_tensor(out=sq[:].rearrange("p (h d) -> p h d", h=8), in0=ld[:].rearrange("p (h d) -> p h d", h=8),
                                                      in1=ss8[:].unsqueeze(2).to_broadcast([128, 8, 128]), op=ALU.mult),
                     reads=[bld, bss8], writes=[bsq])
                P.op("dve", lambda E: E.tensor_tensor(out=hn[:].rearrange("p (h d) -> p h d", h=8), in0=sq[:].rearrange("p (h d) -> p h d", h=8),
                                                      in1=gnw[:].unsqueeze(1).to_broadcast([128, 8, 128]), op=ALU.mult),
                     reads=[bsq, bgnw], writes=[bhn])
                transpose8(C, hn, bhn, idb, bidb, ptr, bptr, oT[:, :, t * 128:(t + 1) * 128], boT, eng="act")
                ld, bld = ldy[ti % 2]
                P.op("act", lambda E, ld=ld: E.activation(out=hn[:], in_=ld[:], func=AF.Gelu_apprx_tanh), reads=[bld], writes=[bhn])
                transpose8(C, hn, bhn, idb, bidb, ptr, bptr, yT[:, :, t * 128:(t + 1) * 128], byT, eng="dve")
            for ct in range(16):
                pb = pmm[ct % 4]; bpb = bpmm[ct % 4]
                fns = [(lambda E, kt=kt, ct=ct, pb=pb, n=n: E.matmul(pb[:, 0:n], lhsT=wz[:, kt, ct * 128:(ct + 1) * 128], rhs=hT[:, kt, 0:n],
                                                                start=(kt == 0), stop=(kt == 7))) for kt in range(8)]
                P.mm_group(fns, reads=[bwz, bhT], writes=[bpb])
                if ct < 8:
                    P.op("act", lambda E, pb=pb, n=n: E.activation(out=sg[:, 0:n], in_=pb[:, 0:n], func=AF.Silu), reads=[bpb], writes=[bsg])
                    P.op("dve", lambda E, ct=ct, n=n: E.tensor_tensor(out=catT[:, ct, 0:n], in0=oT[:, ct, 0:n], in1=sg[:, 0:n], op=ALU.mult),
                         reads=[boT, bsg], writes=[bcat])
                else:
                    P.op("act", lambda E, pb=pb, ct=ct, n=n: E.activation(out=gz[:, ct - 8, 0:n], in_=pb[:, 0:n], func=AF.Silu), reads=[bpb], writes=[bgz])
            for ct in range(8):
                pb = pmm[ct % 4]; bpb = bpmm[ct % 4]
                fns = [(lambda E, kt=kt, ct=ct, pb=pb, n=n: E.matmul(pb[:, 0:n], lhsT=wglu[:, kt, ct * 128:(ct + 1) * 128], rhs=yT[:, kt, 0:n],
                                                                start=(kt == 0), stop=(kt == 7))) for kt in range(8)]
                P.mm_group(fns, reads=[bwglu, byT], writes=[bpb])
                P.op("act", lambda E, pb=pb, n=n: E.activation(out=sg[:, 0:n], in_=pb[:, 0:n], func=AF.Sigmoid), reads=[bpb], writes=[bsg])
                P.op("dve", lambda E, ct=ct, n=n: E.tensor_tensor(out=sg[:, 0:n], in0=sg[:, 0:n], in1=yT[:, ct, 0:n], op=ALU.mult), reads=[bsg, byT], writes=[bsg])
                P.op("dve", lambda E, ct=ct, n=n: E.tensor_tensor(out=catT[:, 8 + ct, 0:n], in0=sg[:, 0:n], in1=gz[:, ct, 0:n], op=ALU.mult),
                     reads=[bsg, bgz], writes=[bcat])
            for t in range(ntl):
                r0 = t0 + t * 128
                outproj_post(C, catT, bcat, 16, wout, bwout, t, xt4[:, t, :], bxt[t], npost, bnpost, pso, bpso, yo, byo, sq, bsq, ss, bss,
                             out_d[r0:r0 + 128, :], bout)
        if fz:
            barrier(P)
        else:
            P.finish([bout])
    return nc


def build_L3(ntok=2048, fz=None):
    nc = fz["nc"] if fz else bass.Bass("TRN2", target_bir_lowering=False)
    pfx = fz["pfx"] if fz else ""

    def D(name, shape):
        if fz and name in fz["share"]:
            return fz["share"][name]
        return nc.dram_tensor(pfx + name, shape, F32, kind="ExternalInput").ap()
    x_d = D("x", [ntok + 128, 1024])
    win_d = D("win", [1024, 8192]); wout_d = D("wout", [2048, 1024]); conv_d = D("conv", [3, 2048])
    npre_d = D("npre", [1024]); npost_d = D("npost", [1024]); ident_d = D("ident", [128, 128])
    out_d = fz["out"] if fz else nc.dram_tensor("out", [ntok, 1024], F32, kind="ExternalOutput").ap()
    NT = 256
    with ExitStack() as st:
        C = Ctx(nc, st, fz["P"], pfx) if fz else Ctx(nc, st); P = C.P
        idf, bidf, idb, bidb = make_ident(C, ident_d)
        npre, bnpre = bcast_row_load(C, "npre", npre_d, 1024)
        npost, bnpost = bcast_row_load(C, "npost", npost_d, 1024)
        cw, bcw = C.sb("cw", [128, 3, 16])
        P.dma("sp", cw[:], conv_d.rearrange("j (c p) -> p j c", p=128), writes=[bcw])
        win, bwin = load_w_bf16(C, "win", win_d, 8, 8192)
        wout, bwout = load_w_bf16(C, "wout", wout_d, 16, 1024)
        xt, bxt = C.sb("xt", [128, 1024])
        sq, bsq = C.sb("sq", [128, 1024])
        hn, bhn = C.sb("hn", [128, 1024], BF16)
        ss, bss = C.sb("ss", [128, 1])
        hT, bhT = C.sb("hT", [128, 8, NT], BF16)
        y1T, by1T = C.sb("y1T", [128, 16, NT], BF16)
        pbuf, bpbuf = C.sb("pbuf", [128, NT + 2])
        phalo, bphalo = C.sb("phalo", [128, 16, 2])
        gcs, bgcs = C.sb("gcs", [128, NT])
        cv, bcv = C.sb("cv", [128, NT])
        sz, bsz = C.sb("sz", [128, NT])
        yo, byo = C.sb("yo", [128, 1024])
        P.op("dve", lambda E: E.memset(phalo[:], 0.0), writes=[bphalo])
        ptr, bptr = C.ps("ptr", [128, 1024], BF16)
        GB = [C.ps("g%d" % i, [128, 512]) for i in range(7)]
        pso = [GB[0][0], GB[1][0]]; bpso = [GB[0][1], GB[1][1]]
        bout = fz["obuf"] if fz else Buf("out", multi=True)
        sts = [(0, 128)] + [(128 + i * NT, NT) for i in range(ntok // NT)]
        for (t0, n) in sts:
            ntl = n // 128
            for t in range(ntl):
                r0 = t0 + t * 128
                P.dma("sp", xt[:], x_d[r0:r0 + 128, :], reads=([fz["xbuf"]] if fz else []), writes=[bxt])
                rms_rstd(C, xt[:], bxt, 1024, sq[:], bsq, ss, bss)
                P.op("dve", lambda E: E.scalar_tensor_tensor(out=hn[:], in0=xt[:], scalar=ss[:, 0:1], in1=npre[:],
                                                             op0=ALU.mult, op1=ALU.mult), reads=[bxt, bss, bnpre], writes=[bhn])
                transpose8(C, hn, bhn, idb, bidb, ptr, bptr, hT[:, :, t * 128:(t + 1) * 128], bhT, eng="act")
            for ct in range(16):
                sel_ = [GB[3 * (ct % 2) + 0], GB[3 * (ct % 2) + 1], GB[3 * (ct % 2) + 2], GB[6]]
                pmm = [x_[0] for x_ in sel_]; bpmm = [x_[1] for x_ in sel_]
                for part in range(4):
                    col0 = (part * 16 + ct) * 128
                    pb = pmm[part]
                    fns = [(lambda E, n=n, kt=kt, col0=col0, pb=pb: E.matmul(pb[:, 0:n], lhsT=win[:, kt, col0:col0 + 128], rhs=hT[:, kt, 0:n],
                                                                        start=(kt == 0), stop=(kt == 7))) for kt in range(8)]
                    P.mm_group(fns, reads=[bwin, bhT], writes=[bpmm[part]])
                P.op("act", lambda E, n=n, pmm=pmm: E.copy(out=gcs[:, 0:n], in_=pmm[1][:, 0:n]), reads=[bpmm[1]], writes=[bgcs])
                P.op("act", lambda E, ct=ct: E.copy(out=pbuf[:, 0:2], in_=phalo[:, ct, :]), reads=[bphalo], writes=[bpbuf])
                P.op("dve", lambda E, n=n, pmm=pmm: E.tensor_tensor(out=pbuf[:, 2:2 + n], in0=gcs[:, 0:n], in1=pmm[2][:, 0:n], op=ALU.mult),
                     reads=[bgcs, bpmm[2]], writes=[bpbuf])
                P.op("act", lambda E, n=n, ct=ct: E.copy(out=phalo[:, ct, :], in_=pbuf[:, n:n + 2]), reads=[bpbuf], writes=[bphalo])
                if t0 == 0:
                    continue
                P.op("dve", lambda E, n=n, ct=ct: E.tensor_scalar(out=cv[:, 0:n], in0=pbuf[:, 0:n], scalar1=cw[:, 0, ct:ct + 1], scalar2=None, op0=ALU.mult),
                     reads=[bpbuf, bcw], writes=[bcv])
                P.op("dve", lambda E, n=n, ct=ct: E.scalar_tensor_tensor(out=cv[:, 0:n], in0=pbuf[:, 1:1 + n], scalar=cw[:, 1, ct:ct + 1], in1=cv[:, 0:n],
                                                                    op0=ALU.mult, op1=ALU.add), reads=[bpbuf, bcw, bcv], writes=[bcv])
                P.op("dve", lambda E, n=n, ct=ct: E.scalar_tensor_tensor(out=cv[:, 0:n], in0=pbuf[:, 2:2 + n], scalar=cw[:, 2, ct:ct + 1], in1=cv[:, 0:n],
                                                                    op0=ALU.mult, op1=ALU.add), reads=[bpbuf, bcw, bcv], writes=[bcv])
                P.op("dve", lambda E, n=n, pmm=pmm: E.tensor_tensor(out=cv[:, 0:n], in0=cv[:, 0:n], in1=pmm[0][:, 0:n], op=ALU.mult), reads=[bcv, bpmm[0]], writes=[bcv])
                P.op("act", lambda E, n=n, pmm=pmm: E.activation(out=sz[:, 0:n], in_=pmm[3][:, 0:n], func=AF.Silu), reads=[bpmm[3]], writes=[bsz])
                P.op("dve", lambda E, n=n, ct=ct: E.tensor_tensor(out=y1T[:, ct, 0:n], in0=cv[:, 0:n], in1=sz[:, 0:n], op=ALU.mult),
                     reads=[bcv, bsz], writes=[by1T])
            if t0 == 0:
                continue
            for t in range(ntl):
                r0 = t0 + t * 128
                P.dma("sp", xt[:], x_d[r0:r0 + 128, :], reads=([fz["xbuf"]] if fz else []), writes=[bxt])
                outproj_post(C, y1T, by1T, 16, wout, bwout, t, xt[:], bxt, npost, bnpost, pso, bpso, yo, byo, sq, bsq, ss, bss,
                             out_d[r0 - 128:r0, :], bout)
        if fz:
            barrier(P)
        else:
            P.finish([bout])
    return nc


_IDENT = np.eye(128, dtype=np.float32)
_CACHE = {}


def _get(name, fn):
    if name not in _CACHE:
        _CACHE[name] = fn()
    return _CACHE[name]


def run_L2(inp, o_full, ys_full):
    nc = _get("L2", build_L2)
    w_in = inp["w_in_even"][0]
    wz = np.ascontiguousarray(np.concatenate([w_in[:, 3072:4096], w_in[:, 5136:6160]], axis=1))
    maps = []
    for c in range(8):
        b, r = divmod(c, 4)
        sl = slice(r * 2048, (r + 1) * 2048)
        maps.append({"x": np.ascontiguousarray(inp["x"][b, sl]), "o": np.ascontiguousarray(o_full[b, sl]),
                     "ys": np.ascontiguousarray(ys_full[b, sl]), "wz": wz, "wglu": np.ascontiguousarray(inp["w_glu"][0]),
                     "wout": np.ascontiguousarray(inp["w_out_even"][0]), "npre": np.ascontiguousarray(inp["norm_pre"][0]),
                     "npost": np.ascontiguousarray(inp["norm_post"][0]), "gnw": np.ascontiguousarray(inp["gdn_norm_w"][0]),
                     "ident": _IDENT})
    res = run_bass_kernel_spmd(nc, maps, core_ids=list(range(8)))
    x1 = np.empty((2, 8192, 1024), np.float32)
    for c in range(8):
        b, r = divmod(c, 4)
        x1[b, r * 2048:(r + 1) * 2048] = res.results[c]["out"]
    return x1


def run_L3(inp, x1):
    nc = _get("L3", build_L3)
    maps = []
    for c in range(8):
        b, r = divmod(c, 4)
        xh = np.zeros((2048 + 128, 1024), np.float32)
        xh[128:] = x1[b, r * 2048:(r + 1) * 2048]
        if r > 0:
            xh[:128] = x1[b, r * 2048 - 128:r * 2048]
        maps.append({"x": xh, "win": np.ascontiguousarray(inp["w_in_odd"][0]), "wout": np.ascontiguousarray(inp["w_out_odd"][0]),
                     "conv": np.ascontiguousarray(inp["conv_short"][0]), "npre": np.ascontiguousarray(inp["norm_pre"][1]),
                     "npost": np.ascontiguousarray(inp["norm_post"][1]), "ident": _IDENT})
    res = run_bass_kernel_spmd(nc, maps, core_ids=list(range(8)))
    out = np.empty((2, 8192, 1024), np.float32)
    for c in range(8):
        b, r = divmod(c, 4)
        out[b, r * 2048:(r + 1) * 2048] = res.results[c]["out"]
    return out


I32 = mybir.dt.int32
TAUS = np.array(list(range(17)) + [32, 64, 128, 256, 512, 1024, 2048, 4096] + list(range(15, -1, -1)), np.float32)
NTAU = len(TAUS)


def _s5_consts():
    mk = np.zeros((128, 2, 16, 16), np.float32)
    idm = np.zeros((128, 2, 16, 16), np.float32)
    for kt2 in range(2):
        for sp in range(8):
            s = kt2 * 8 + sp
            for h in range(16):
                mk[sp * 16 + h, kt2, s:, :] = 1.0
                idm[sp * 16 + h, kt2, s, h] = 1.0
    return mk.reshape(128, 2, 256), idm.reshape(128, 2, 256)


def barrier(P):
    for e in P.ENG:
        for e2 in P.ENG:
            if P.cnt[e2] > 0:
                P._wait(e, ("c", e2, P.cnt[e2]))
        for q in P.dsem:
            j1 = P.dcnt[q]
            for j in range(max(0, j1 - NDS), j1):
                P._wait(e, ("d", q, j % NDS, 16 * (j // NDS + 1)))


def build_L1b(S=8192, fz=None):
    nc = fz["nc"] if fz else bass.Bass("TRN2", target_bir_lowering=False)
    pfx = fz["pfx"] if fz else ""

    def D(name, shape):
        if fz and name in fz["share"]:
            return fz["share"][name]
        return nc.dram_tensor(pfx + name, shape, F32, kind="ExternalInput").ap()
    x_d = D("x", [S, 1024]); npre_d = D("npre", [1024]); wu_d = D("wu", [1024, 256])
    lre_d = D("lre", [16, 64]); lim_d = D("lim", [16, 64]); bre_d = D("bre", [16, 64, 16]); bim_d = D("bim", [16, 64, 16])
    cre_d = D("cre", [16, 16, 64]); cim_d = D("cim", [16, 16, 64]); ldt_d = D("ldt", [16]); dd_d = D("dd", [256])
    taus_d = D("taus", [NTAU]); mk_d = D("mk", [128, 2, 256]); idm_d = D("idm", [128, 2, 256]); ident_d = D("ident", [128, 128])
    ys_d = fz["out"] if fz else nc.dram_tensor("ys", [S, 256], F32, kind="ExternalOutput").ap()
    NCH = S // 16
    NST = S // 512
    with ExitStack() as st:
        C = Ctx(nc, st, fz["P"], pfx) if fz else Ctx(nc, st); P = C.P
        idf, bidf, idb, bidb = make_ident(C, ident_d)
        ptr, bptr = C.ps("ptr", [128, 1024], BF16)
        py, bpy = C.ps("py", [128, 1024])
        G = []; bG = []
        for i in range(4):
            t_, b_ = C.ps("g%d" % i, [128, 512]); G.append(t_); bG.append(b_)
        U, bU = C.sb("U", [128, 2, 16, NCH], BF16)
        with ExitStack() as st2:
            C2 = Ctx(nc, st2, P, C.pfx)
            ext = fz.get("uTp") if fz else None
            if ext:
                uTp, buTp = ext
            else:
                uTp, buTp = C2.sb("uTp", [128, 2, 16, NCH], BF16)
            with ExitStack() as st1:
                C1 = Ctx(nc, st1, P, C.pfx)
                npre, bnpre = bcast_row_load(C1, "npre", npre_d, 1024)
                wu, bwu = load_w_bf16(C1, "wu", wu_d, 8, 256)
                xt, bxt = C1.sb("xt", [128, 1024])
                sq, bsq = C1.sb("sq", [128, 1024])
                hn, bhn = C1.sb("hn", [128, 1024], BF16)
                ss, bss = C1.sb("ss", [128, 1])
                hT, bhT = C1.sb("hT", [128, 8, 512], BF16)
                for s_ in range(0 if ext else NST):
                    for t in range(4):
                        r0 = s_ * 512 + t * 128
                        P.dma("sp", xt[:], x_d[r0:r0 + 128, :], writes=[bxt])
                        rms_rstd(C1, xt[:], bxt, 1024, sq[:], bsq, ss, bss)
                        P.op("dve", lambda E: E.scalar_tensor_tensor(out=hn[:], in0=xt[:], scalar=ss[:, 0:1], in1=npre[:],
                                                                     op0=ALU.mult, op1=ALU.mult), reads=[bxt, bss, bnpre], writes=[bhn])
                        transpose8(C1, hn, bhn, idb, bidb, ptr, bptr, hT[:, :, t * 128:(t + 1) * 128], bhT, eng="act")
                    for blk in range(2):
                        pb = G[blk]
                        fns = [(lambda E, kt=kt, blk=blk, pb=pb: E.matmul(
                            pb[:].rearrange("p (s n) -> p s n", s=16), lhsT=wu[:, kt, blk * 128:(blk + 1) * 128],
                            rhs=hT[:, kt, :].rearrange("p (n s) -> p s n", s=16), start=(kt == 0), stop=(kt == 7))) for kt in range(8)]
                        P.mm_group(fns, reads=[bwu, bhT], writes=[bG[blk]])
                        P.op("act" if blk == 0 else "dve",
                             (lambda E, blk=blk, pb=pb, s_=s_: E.copy(out=uTp[:, blk, :, 32 * s_:32 * s_ + 32], in_=pb[:].rearrange("p (s n) -> p s n", s=16)))
                             if blk == 0 else
                             (lambda E, blk=blk, pb=pb, s_=s_: E.tensor_copy(out=uTp[:, blk, :, 32 * s_:32 * s_ + 32], in_=pb[:].rearrange("p (s n) -> p s n", s=16))),
                             reads=[bG[blk]], writes=[buTp])
                barrier(P)
            ud2 = nc.dram_tensor(pfx + "ud2", [16, 2, 8, 16, NCH], BF16)
            bud2 = Buf("ud2", multi=True)
            bU.multi = True; bU.w = []
            for g in range(16):
                P.dma("sp", ud2.ap()[g].rearrange("k sp h n -> h (k sp) n"),
                      uTp[(g % 8) * 16:(g % 8 + 1) * 16, g // 8, :, :], reads=[buTp], writes=[bud2])
            for g in range(16):
                P.dma("sp", U[:, :, g, :], ud2.ap()[g].rearrange("k sp h n -> (sp h) k n"), reads=[bud2], writes=[bU])
            barrier(P)
        lre, blre = C.sb("lre", [128, 8]); lim, blim = C.sb("lim", [128, 8]); ldt, bldt = C.sb("ldt", [128, 8])
        TAU, bTAU = bcast_row_load(C, "TAU", taus_d, NTAU)
        Er, bEr = C.sb("Er", [128, 8, NTAU]); Ei, bEi = C.sb("Ei", [128, 8, NTAU]); NEi, bNEi = C.sb("NEi", [128, 8, NTAU])
        Hr, bHr = C.sb("Hr", [128, 8, 17, 16]); nHi, bnHi = C.sb("nHi", [128, 8, 17, 16])
        WbT, bWbT = C.sb("WbT", [128, 2, 8, 2, 128], BF16)
        Toep, bToep = C.sb("Toep", [128, 2, 16, 256], BF16)
        with ExitStack() as st3:
            C3 = Ctx(nc, st3, P, C.pfx)
            Br, bBr = C3.sb("Br", [128, 8, 16]); Bi, bBi = C3.sb("Bi", [128, 8, 16])
            Cr, bCr = C3.sb("Cr", [128, 8, 16]); Ci, bCi = C3.sb("Ci", [128, 8, 16])
            dcol, bdcol = C3.sb("dcol", [128, 16])
            MK, bMK = C3.sb("MK", [128, 2, 256]); IDM, bIDM = C3.sb("IDM", [128, 2, 256])
            P.dma("sp", MK[:], mk_d, writes=[bMK]); P.dma("sp", IDM[:], idm_d, writes=[bIDM])
            for two in range(2):
                hs = slice(64 * two, 64 * two + 64)
                P.dma("sp", lre[hs, :], lre_d.rearrange("(gp two) p -> two p gp", two=2)[two], writes=[blre])
                P.dma("sp", lim[hs, :], lim_d.rearrange("(gp two) p -> two p gp", two=2)[two], writes=[blim])
                P.dma("sp", ldt[hs, :], ldt_d.rearrange("(gp two) -> two gp", two=2)[two].partition_broadcast(64), writes=[bldt])
                P.dma("sp", Br[hs], bre_d.rearrange("(gp two) p h -> two p gp h", two=2)[two], writes=[bBr])
                P.dma("sp", Bi[hs], bim_d.rearrange("(gp two) p h -> two p gp h", two=2)[two], writes=[bBi])
                for gp in range(8):
                    P.dma("sp", Cr[hs, gp, :], cre_d[2 * gp + two].rearrange("h p -> p h"), writes=[bCr])
                    P.dma("sp", Ci[hs, gp, :], cim_d[2 * gp + two].rearrange("h p -> p h"), writes=[bCi])
            for sp in range(8):
                P.dma("sp", dcol[sp * 16:(sp + 1) * 16, :], dd_d.rearrange("(g h) -> h g", h=16), writes=[bdcol])
            sm = {}
            for nm in ("dt", "lr", "lrdt", "th", "den", "nr", "fre", "fim", "t8a", "t8b"):
                sm[nm] = C3.sb("sm_" + nm, [128, 8])
            T41 = {}
            for nm in ("ARG", "MARG", "MAG", "MAGN", "SIN", "COS", "ErN", "EiN", "rt", "rk"):
                T41[nm] = C3.sb("t41_" + nm, [128, 8, NTAU])
            rki, brki = C3.sb("rki", [128, 8, NTAU], I32)

            def tt(eng, out, bo, a, ba, b, bb_, op):
                P.op(eng, lambda E: E.tensor_tensor(out=out, in0=a, in1=b, op=op), reads=[ba, bb_], writes=[bo])

            dt, bdt = sm["dt"]; lr, blr = sm["lr"]; lrdt, blrdt = sm["lrdt"]; th, bth = sm["th"]
            P.op("act", lambda E: E.activation(out=dt[:], in_=ldt[:], func=AF.Exp), reads=[bldt], writes=[bdt])
            P.op("dve", lambda E: E.tensor_scalar(out=lr[:], in0=lre[:], scalar1=-1e-4, scalar2=None, op0=ALU.min), reads=[blre], writes=[blr])
            tt("dve", lrdt[:], blrdt, lr[:], blr, dt[:], bdt, ALU.mult)
            tt("dve", th[:], bth, lim[:], blim, dt[:], bdt, ALU.mult)
            ARG, bARG = T41["ARG"]; MARG, bMARG = T41["MARG"]; MAG, bMAG = T41["MAG"]; MAGN, bMAGN = T41["MAGN"]
            SIN, bSIN = T41["SIN"]; COS, bCOS = T41["COS"]; ErN, bErN = T41["ErN"]; EiN, bEiN = T41["EiN"]
            rt, brt = T41["rt"]; rk, brk = T41["rk"]
            tb = TAU[:].unsqueeze(1).to_broadcast([128, 8, NTAU])
            tt("dve", ARG[:], bARG, th[:].unsqueeze(2).to_broadcast([128, 8, NTAU]), bth, tb, bTAU, ALU.mult)
            tt("dve", MARG[:], bMARG, lrdt[:].unsqueeze(2).to_broadcast([128, 8, NTAU]), blrdt, tb, bTAU, ALU.mult)
            P.op("act", lambda E: E.activation(out=MAG[:], in_=MARG[:], func=AF.Exp), reads=[bMARG], writes=[bMAG])
            P.op("act", lambda E: E.activation(out=MAGN[:, :, 0:17], in_=MARG[:, :, 0:17], func=AF.Exp, scale=-1.0), reads=[bMARG], writes=[bMAGN])

            def sin_of(dst, bdst, shift):
                P.op("dve", lambda E: E.tensor_scalar(out=rt[:], in0=ARG[:], scalar1=float(shift), scalar2=None, op0=ALU.add), reads=[bARG], writes=[brt])
                P.op("dve", lambda E: E.tensor_scalar(out=rki[:], in0=rt[:], scalar1=float(1.0 / (2 * np.pi)), scalar2=None, op0=ALU.mult), reads=[brt], writes=[brki])
                P.op("dve", lambda E: E.tensor_copy(out=rk[:], in_=rki[:]), reads=[brki], writes=[brk])
                P.op("dve", lambda E: E.scalar_tensor_tensor(out=rt[:], in0=rk[:], scalar=float(-2 * np.pi), in1=rt[:], op0=ALU.mult, op1=ALU.add),
                     reads=[brk, brt], writes=[brt])
                P.op("dve", lambda E: E.tensor_scalar(out=rt[:], in0=rt[:], scalar1=-3.14159, scalar2=3.14159, op0=ALU.max, op1=ALU.min), reads=[brt], writes=[brt])
                P.op("act", lambda E: E.activation(out=dst[:], in_=rt[:], func=AF.Sin), reads=[brt], writes=[bdst])

            sin_of(SIN, bSIN, 0.0)
            sin_of(COS, bCOS, np.pi / 2)
            tt("dve", Er[:], bEr, MAG[:], bMAG, COS[:], bCOS, ALU.mult)
            tt("dve", Ei[:], bEi, MAG[:], bMAG, SIN[:], bSIN, ALU.mult)
            P.op("dve", lambda E: E.tensor_scalar(out=NEi[:], in0=Ei[:], scalar1=-1.0, scalar2=None, op0=ALU.mult), reads=[bEi], writes=[bNEi])
            tt("dve", ErN[:, :, 0:17], bErN, MAGN[:, :, 0:17], bMAGN, COS[:, :, 0:17], bCOS, ALU.mult)
            tt("dve", EiN[:, :, 0:17], bEiN, MAGN[:, :, 0:17], bMAGN, SIN[:, :, 0:17], bSIN, ALU.mult)
            P.op("dve", lambda E: E.tensor_scalar(out=EiN[:, :, 0:17], in0=EiN[:, :, 0:17], scalar1=-1.0, scalar2=None, op0=ALU.mult), reads=[bEiN], writes=[bEiN])
            den, bden = sm["den"]; nr, bnr = sm["nr"]; fre, bfre = sm["fre"]; fim, bfim = sm["fim"]; t8a, bt8a = sm["t8a"]; t8b, bt8b = sm["t8b"]
            tt("dve", den[:], bden, lr[:], blr, lr[:], blr, ALU.mult)
            tt("dve", t8a[:], bt8a, lim[:], blim, lim[:], blim, ALU.mult)
            tt("dve", den[:], bden, den[:], bden, t8a[:], bt8a, ALU.add)
            P.op("dve", lambda E: E.reciprocal(out=den[:], in_=den[:]), reads=[bden], writes=[bden])
            P.op("dve", lambda E: E.tensor_scalar(out=nr[:], in0=Er[:, :, 1], scalar1=-1.0, scalar2=None, op0=ALU.add), reads=[bEr], writes=[bnr])
            tt("dve", fre[:], bfre, nr[:], bnr, lr[:], blr, ALU.mult)
            tt("dve", t8a[:], bt8a, Ei[:, :, 1], bEi, lim[:], blim, ALU.mult)
            tt("dve", fre[:], bfre, fre[:], bfre, t8a[:], bt8a, ALU.add)
            tt("dve", fre[:], bfre, fre[:], bfre, den[:], bden, ALU.mult)
            tt("dve", fim[:], bfim, Ei[:, :, 1], bEi, lr[:], blr, ALU.mult)
            tt("dve", t8b[:], bt8b, nr[:], bnr, lim[:], blim, ALU.mult)
            tt("dve", fim[:], bfim, fim[:], bfim, t8b[:], bt8b, ALU.subtract)
            tt("dve", fim[:], bfim, fim[:], bfim, den[:], bden, ALU.mult)

            def cmul(outr, boutr, outi, bouti, ar, bar, ai, bai, br_, bbr_, bi_, bbi_, tmp, btmp):
                tt("dve", outr, boutr, ar, bar, br_, bbr_, ALU.mult)
                tt("dve", tmp, btmp, ai, bai, bi_, bbi_, ALU.mult)
                tt("dve", outr, boutr, outr, boutr, tmp, btmp, ALU.subtract)
                tt("dve", outi, bouti, ar, bar, bi_, bbi_, ALU.mult)
                tt("dve", tmp, btmp, ai, bai, br_, bbr_, ALU.mult)
                tt("dve", outi, bouti, outi, bouti, tmp, btmp, ALU.add)

            bbr, bbbr = C3.sb("bbr", [128, 8, 16]); bbi, bbbi = C3.sb("bbi", [128, 8, 16]); tmp16, btmp16 = C3.sb("tmp16", [128, 8, 16])
            fb = lambda t_: t_[:].unsqueeze(2).to_broadcast([128, 8, 16])
            cmul(bbr[:], bbbr, bbi[:], bbbi, fb(fre), bfre, fb(fim), bfim, Br[:], bBr, Bi[:], bBi, tmp16[:], btmp16)
            Gr, bGr = C3.sb("Gr", [128, 8, 16, 16]); Gi, bGi = C3.sb("Gi", [128, 8, 16, 16])
            WPr, bWPr = C3.sb("WPr", [128, 8, 16, 16]); WPi, bWPi = C3.sb("WPi", [128, 8, 16, 16])
            Hi, bHi = C3.sb("Hi", [128, 8, 17, 16]); tmpH, btmpH = C3.sb("tmpH", [128, 8, 17, 16])
            eb = lambda t_, j0, j1: t_[:, :, j0:j1].unsqueeze(3).to_broadcast([128, 8, j1 - j0, 16])
            vb = lambda t_, n_: t_[:].unsqueeze(2).to_broadcast([128, 8, n_, 16])
            cmul(Gr[:], bGr, Gi[:], bGi, eb(ErN, 0, 16), bErN, eb(EiN, 0, 16), bEiN, vb(bbr, 16), bbbr, vb(bbi, 16), bbbi, tmpH[:, :, 0:16, :], btmpH)
            cmul(WPr[:], bWPr, WPi[:], bWPi, eb(Er, 25, 41), bEr, eb(Ei, 25, 41), bEi, vb(bbr, 16), bbbr, vb(bbi, 16), bbbi, tmpH[:, :, 0:16, :], btmpH)
            cmul(Hr[:], bHr, Hi[:], bHi, eb(Er, 0, 17), bEr, eb(Ei, 0, 17), bEi, vb(Cr, 17), bCr, vb(Ci, 17), bCi, tmpH[:], btmpH)
            P.op("dve", lambda E: E.tensor_scalar(out=nHi[:], in0=Hi[:], scalar1=-1.0, scalar2=None, op0=ALU.mult), reads=[bHi], writes=[bnHi])
            for gp in range(8):
                for kt2 in range(2):
                    for c, (WP_, bWP_) in enumerate(((WPr, bWPr), (WPi, bWPi))):
                        P.op("pe", lambda E, gp=gp, kt2=kt2, WP_=WP_: E.transpose(
                            out=G[2][:, 0:128], in_=WP_[:, gp, kt2 * 8:(kt2 + 1) * 8, :].rearrange("p s h -> p (s h)"), identity=idf[:]),
                            reads=[bWP_, bidf], writes=[bG[2]])
                        P.op("act", lambda E, gp=gp, kt2=kt2, c=c: E.copy(out=WbT[:, kt2, gp, c, :], in_=G[2][:, 0:128]), reads=[bG[2]], writes=[bWbT])
            tmpT, btmpT = C3.sb("tmpT", [128, 256])
            for g in range(16):
                gp = g // 2; hs = slice(64 * (g % 2), 64 * (g % 2) + 64)
                for kt2 in range(2):
                    fns = [
                        lambda E, gp=gp, hs=hs, kt2=kt2: E.matmul(G[3][:, 0:256], lhsT=Gr[hs, gp, kt2 * 8:(kt2 + 1) * 8, :].rearrange("p s h -> p (s h)"),
                                                                  rhs=Hr[hs, gp, 0:16, :].rearrange("p t h -> p (t h)"), start=True, stop=False),
                        lambda E, gp=gp, hs=hs, kt2=kt2: E.matmul(G[3][:, 0:256], lhsT=Gi[hs, gp, kt2 * 8:(kt2 + 1) * 8, :].rearrange("p s h -> p (s h)"),
                                                                  rhs=nHi[hs, gp, 0:16, :].rearrange("p t h -> p (t h)"), start=False, stop=True)]
                    P.mm_group(fns, reads=[bGr, bGi, bHr, bnHi], writes=[bG[3]])
                    P.op("dve", lambda E, kt2=kt2: E.tensor_tensor(out=tmpT[:], in0=G[3][:, 0:256], in1=MK[:, kt2, :], op=ALU.mult),
                         reads=[bG[3], bMK], writes=[btmpT])
                    P.op("dve", lambda E, kt2=kt2, g=g: E.scalar_tensor_tensor(out=Toep[:, kt2, g, :], in0=IDM[:, kt2, :], scalar=dcol[:, g:g + 1], in1=tmpT[:],
                                                                               op0=ALU.mult, op1=ALU.add), reads=[bIDM, bdcol, btmpT], writes=[bToep])
            barrier(P)
        X = {}
        for bufn in ("A", "B"):
            for c in ("re", "im"):
                X[(bufn, c)] = (C.sb("X%s%s" % (bufn, c), [128, 8, NCH + 1])[0], [Buf("X%s%s%d" % (bufn, c, gp)) for gp in range(8)])
        Ysb, bYsb = C.sb("Ysb", [128, 16, 256])
        for key in X:
            t_, bl = X[key]
            P.op("dve", lambda E, t_=t_: E.memset(t_[:, :, 0:1], 0.0), writes=bl)
        for gp in range(8):
            for c, cn in enumerate(("re", "im")):
                px = G[c]
                fns = []
                for two in range(2):
                    g = 2 * gp + two
                    for kt2 in range(2):
                        fns.append(lambda E, two=two, g=g, kt2=kt2, gp=gp, c=c, px=px: E.matmul(
                            px[64 * two:64 * two + 64, :], lhsT=WbT[:, kt2, gp, c, 64 * two:64 * two + 64], rhs=U[:, kt2, g, :],
                            start=(kt2 == 0), stop=(kt2 == 1)))
                P.mm_group(fns, reads=[bWbT, bU], writes=[bG[c]])
                xt_, xb_ = X[("A", cn)]
                P.op("act", lambda E, xt_=xt_, gp=gp, px=px: E.copy(out=xt_[:, gp, 1:NCH + 1], in_=px[:]), reads=[bG[c]], writes=[xb_[gp]])
        for k in range(9):
            d = 1 << k
            j = 16 if k == 0 else 16 + k
            src, dst = ("A", "B") if k % 2 == 0 else ("B", "A")
            sre, bsre = X[(src, "re")]; sim, bsim = X[(src, "im")]
            dre, bdre = X[(dst, "re")]; dim_, bdim = X[(dst, "im")]
            P.op("dve", lambda E, dre=dre, sre=sre, d=d: E.tensor_copy(out=dre[:, :, 1:1 + d], in_=sre[:, :, 1:1 + d]), reads=bsre, writes=bdre)
            P.op("pool", lambda E, dim_=dim_, sim=sim, d=d: E.tensor_copy(out=dim_[:, :, 1:1 + d], in_=sim[:, :, 1:1 + d]), reads=bsim, writes=bdim)
            for gp in range(8):
                lo = slice(1, NCH + 1 - d); hi = slice(1 + d, NCH + 1)
                P.op("dve", lambda E, gp=gp, j=j, dre=dre, sre=sre, lo=lo, hi=hi: E.scalar_tensor_tensor(
                    out=dre[:, gp, hi], in0=sre[:, gp, lo], scalar=Er[:, gp, j:j + 1], in1=sre[:, gp, hi], op0=ALU.mult, op1=ALU.add),
                    reads=[bsre[gp], bEr], writes=[bdre[gp]])
                P.op("dve", lambda E, gp=gp, j=j, dre=dre, sim=sim, lo=lo, hi=hi: E.scalar_tensor_tensor(
                    out=dre[:, gp, hi], in0=sim[:, gp, lo], scalar=NEi[:, gp, j:j + 1], in1=dre[:, gp, hi], op0=ALU.mult, op1=ALU.add),
                    reads=[bsim[gp], bNEi, bdre[gp]], writes=[bdre[gp]])
                P.op("dve", lambda E, gp=gp, j=j, dim_=dim_, sim=sim, lo=lo, hi=hi: E.scalar_tensor_tensor(
                    out=dim_[:, gp, hi], in0=sim[:, gp, lo], scalar=Er[:, gp, j:j + 1], in1=sim[:, gp, hi], op0=ALU.mult, op1=ALU.add),
                    reads=[bsim[gp], bEr], writes=[bdim[gp]])
                P.op("dve", lambda E, gp=gp, j=j, dim_=dim_, sre=sre, lo=lo, hi=hi: E.scalar_tensor_tensor(
                    out=dim_[:, gp, hi], in0=sre[:, gp, lo], scalar=Ei[:, gp, j:j + 1], in1=dim_[:, gp, hi], op0=ALU.mult, op1=ALU.add),
                    reads=[bsre[gp], bEi, bdim[gp]], writes=[bdim[gp]])
        fre_, bfre_ = X[("B", "re")]; fim_, bfim_ = X[("B", "im")]
        bys = None if fz else Buf("ys", multi=True)
        ysv = ys_d.rearrange("(n t) c -> n t c", t=16)
        for jt in range(NCH // 128):
            for gq in range(4):
                fns = []
                for gi in range(4):
                    g = 4 * gq + gi; gp = g // 2; hs = slice(64 * (g % 2), 64 * (g % 2) + 64)
                    o_ = (gi * 256, (gi + 1) * 256)
                    for kt2 in range(2):
                        fns.append(lambda E, o_=o_, g=g, kt2=kt2, jt=jt: E.matmul(
                            py[:, o_[0]:o_[1]], lhsT=U[:, kt2, g, jt * 128:(jt + 1) * 128], rhs=Toep[:, kt2, g, :], start=(kt2 == 0), stop=False))
                    fns.append(lambda E, o_=o_, gp=gp, hs=hs, jt=jt: E.matmul(
                        py[:, o_[0]:o_[1]], lhsT=fre_[hs, gp, jt * 128:(jt + 1) * 128], rhs=Hr[hs, gp, 1:17, :].rearrange("p t h -> p (t h)"),
                        start=False, stop=False))
                    fns.append(lambda E, o_=o_, gp=gp, hs=hs, jt=jt: E.matmul(
                        py[:, o_[0]:o_[1]], lhsT=fim_[hs, gp, jt * 128:(jt + 1) * 128], rhs=nHi[hs, gp, 1:17, :].rearrange("p t h -> p (t h)"),
                        start=False, stop=True))
                P.mm_group(fns, reads=[bU, bToep, bHr, bnHi] + bfre_ + bfim_, writes=[bpy])
                P.op("act" if gq % 2 == 0 else "dve",
                     (lambda E, gq=gq: E.copy(out=Ysb[:].rearrange("p t (g h) -> p g t h", h=16)[:, 4 * gq:4 * gq + 4],
                                              in_=py[:].rearrange("p (g t h) -> p g t h", g=4, h=16)))
                     if gq % 2 == 0 else
                     (lambda E, gq=gq: E.tensor_copy(out=Ysb[:].rearrange("p t (g h) -> p g t h", h=16)[:, 4 * gq:4 * gq + 4],
                                                     in_=py[:].rearrange("p (g t h) -> p g t h", g=4, h=16))),
                     reads=[bpy], writes=[bYsb])
            P.dma("sp", ysv[jt * 128:(jt + 1) * 128, :, :], Ysb[:], reads=[bYsb], writes=[fz["obuf_of"](jt) if fz else bys])
            if fz:
                fz["after_chunk"](jt)
        if fz:
            barrier(P)
        else:
            P.finish([bys])
    return nc


def run_L1b(inp):
    nc = _get("L1b", build_L1b)
    mk, idm = _s5_consts()
    w_in = inp["w_in_even"][0]
    maps = []
    for c in range(8):
        b, r = divmod(c, 4)
        gs = slice(16 * r, 16 * r + 16)
        maps.append({"x": np.ascontiguousarray(inp["x"][b]), "npre": np.ascontiguousarray(inp["norm_pre"][0]),
                     "wu": np.ascontiguousarray(w_in[:, 4112 + 256 * r:4112 + 256 * (r + 1)]),
                     "lre": np.ascontiguousarray(inp["s5_lam_re"][0, gs]), "lim": np.ascontiguousarray(inp["s5_lam_im"][0, gs]),
                     "bre": np.ascontiguousarray(inp["s5_b_re"][0, gs]), "bim": np.ascontiguousarray(inp["s5_b_im"][0, gs]),
                     "cre": np.ascontiguousarray(inp["s5_c_re"][0, gs]), "cim": np.ascontiguousarray(inp["s5_c_im"][0, gs]),
                     "ldt": np.ascontiguousarray(inp["s5_log_dt"][0, gs]), "dd": np.ascontiguousarray(inp["s5_d"][0, 256 * r:256 * (r + 1)]),
                     "taus": TAUS, "mk": mk, "idm": idm, "ident": _IDENT})
    res = run_bass_kernel_spmd(nc, maps, core_ids=list(range(8)))
    ys = np.empty((2, 8192, 1024), np.float32)
    for c in range(8):
        b, r = divmod(c, 4)
        ys[b, :, 256 * r:256 * (r + 1)] = res.results[c]["ys"]
    return ys


def _gdn_consts():
    p = np.arange(64)[:, None]; f = np.arange(64)[None, :]
    negu = np.where(f >= p, 0.0, -30000.0)
    negls = np.where(f < p, 0.0, -30000.0)
    nsu = np.where(f > p, -1.0, 0.0)
    i64 = np.eye(64)
    c64 = np.stack([negu, negls, nsu, i64], axis=1).astype(np.float32)
    cmask = np.ones((2, 512), np.float32); cmask[:, 0::64] = 0.0
    sel = np.zeros((2, 2, 128), np.float32); sel[0, 0, :] = 1.0; sel[1, 1, :] = 1.0
    return c64, cmask, sel


def build_L1a(S=8192, fz=None):
    nc = fz["nc"] if fz else bass.Bass("TRN2", target_bir_lowering=False)
    pfx = fz["pfx"] if fz else ""

    def D(name, shape):
        if fz and name in fz["share"]:
            return fz["share"][name]
        return nc.dram_tensor(pfx + name, shape, F32, kind="ExternalInput").ap()
    x_d = D("x", [S, 1024]); npre_d = D("npre", [1024]); w_d = D("w", [1024, 768]); wb_d = D("wb", [1024, 2]); wa_d = D("wa", [1024, 2])
    conv_d = D("conv", [4, 768]); alog_d = D("alog", [2]); dtb_d = D("dtb", [2])
    ident_d = D("ident", [128, 128]); c64_d = D("c64", [64, 4, 64]); cmask_d = D("cmask", [2, 512]); sel_d = D("sel", [2, 2, 128])
    ones_d = D("ones", [128, 128])
    o_d = fz["out"] if fz else nc.dram_tensor("o", [S, 256], F32, kind="ExternalOutput").ap()
    NST = S // 512
    with ExitStack() as st:
        C = Ctx(nc, st, fz["P"], pfx) if fz else Ctx(nc, st); P = C.P
        idf, bidf, idb, bidb = make_ident(C, ident_d)
        npre, bnpre = bcast_row_load(C, "npre", npre_d, 1024)
        w, bw = load_w_bf16(C, "w", w_d, 8, 768)
        wb, bwb = load_w_bf16(C, "wb", wb_d, 8, 2)
        wa, bwa = load_w_bf16(C, "wa", wa_d, 8, 2)
        cw, bcw = C.sb("cw", [128, 4, 6])
        P.dma("sp", cw[:], conv_d.rearrange("j (c p) -> p j c", p=128), writes=[bcw])
        extu = fz.get("uTp") if fz else None
        if extu:
            wu_d = D("wu", [1024, 256])
            wu, bwu = load_w_bf16(C, "wu", wu_d, 8, 256)
            uTp, buTp = extu
        c64, bc64 = C.sb("c64", [64, 4, 64]); P.dma("sp", c64[:], c64_d, writes=[bc64])
        NEGU = c64[:, 0, :]; NEGLS = c64[:, 1, :]; NSU = c64[:, 2, :]; I64 = c64[:, 3, :]
        cmask, bcmask = C.sb("cmask", [2, 512]); P.dma("sp", cmask[:], cmask_d, writes=[bcmask])
        sel, bsel = C.sb("sel", [2, 2, 128]); P.dma("sp", sel[:], sel_d, writes=[bsel])
        ones, bones = C.sb("ones", [128, 128]); P.dma("sp", ones[:], ones_d, writes=[bones])
        alog, balog = C.sb("alog", [2, 1]); P.dma("sp", alog[:], alog_d.rearrange("(a b) -> a b", b=1), writes=[balog])
        dtb, bdtb = C.sb("dtb", [2, 1]); P.dma("sp", dtb[:], dtb_d.rearrange("(a b) -> a b", b=1), writes=[bdtb])
        negA, bnegA = C.sb("negA", [2, 1])
        P.op("act", lambda E: E.activation(out=negA[:], in_=alog[:], func=AF.Exp), reads=[balog], writes=[bnegA])
        P.op("dve", lambda E: E.tensor_scalar(out=negA[:], in0=negA[:], scalar1=-1.0, scalar2=None, op0=ALU.mult), reads=[bnegA], writes=[bnegA])
        xt, bxt = C.sb("xt", [128, 1024]); sq, bsq = C.sb("sq", [128, 1024]); hn, bhn = C.sb("hn", [128, 1024], BF16)
        ss, bss = C.sb("ss", [128, 1]); hT, bhT = C.sb("hT", [128, 8, 512], BF16)
        raw, _ = C.sb("raw", [128, 6, 515]); braw = [Buf("raw%d" % i) for i in range(6)]
        cvq, bcvq = C.sb("cvq", [128, 512])
        act, _ = C.sb("act", [128, 6, 512]); bact = [Buf("act%d" % i) for i in range(6)]
        qk, _ = C.sb("qk", [128, 4, 512]); bqk = [Buf("qk%d" % i) for i in range(4)]
        rn, brn = C.sb("rn", [128, 512])
        brow, bbrow = C.sb("brow", [2, 512]); grow, bgrow = C.sb("grow", [2, 512]); gcrow, bgcrow = C.sb("gcrow", [2, 512])
        GCB = []; BB = []
        for h in range(2):
            GCB.append(C.sb("GCB%d" % h, [128, 512])); BB.append(C.sb("BB%d" % h, [128, 512]))
        m64 = {}
        for nm in ("arg1", "DT", "Ds", "tmp", "tmp2", "BBm"):
            m64[nm] = C.sb("m_" + nm, [64, 512])
        for nm in ("Pa", "Pb", "Qa", "Qb"):
            m64[nm] = C.sb("m_" + nm, [64, 512], BF16)
        heads = []
        for h in range(2):
            H = {}
            H["attnT"] = C.sb("attnT%d" % h, [64, 512], BF16); H["Y"] = C.sb("Y%d" % h, [64, 512]); H["Ybf"] = C.sb("Ybf%d" % h, [64, 512], BF16)
            H["EG"] = C.sb("EG%d" % h, [128, 512]); H["qdec"] = C.sb("qdec%d" % h, [128, 512])
            H["bv"] = C.sb("bv%d" % h, [64, 8, 128]); H["kdec"] = C.sb("kdec%d" % h, [64, 8, 128], BF16)
            H["nbg"] = C.sb("nbg%d" % h, [64, 8]); H["osb"] = C.sb("osb%d" % h, [128, 8, 128])
            H["vnew"] = C.sb("vnew%d" % h, [64, 128], BF16); H["rhs2"] = C.sb("rhs2%d" % h, [64, 128], BF16)
            heads.append(H)
        small = {}
        for nm in ("gccol", "bcol", "nbcol", "elast", "egc"):
            small[nm] = C.sb("s_" + nm, [64, 8])
        Sst = [C.sb("S%d" % h, [128, 128]) for h in range(2)]
        for h in range(2):
            P.op("dve", lambda E, h=h: E.memset(Sst[h][0][:], 0.0), writes=[Sst[h][1]])
        P.op("dve", lambda E: E.memset(raw[:, :, 0:3], 0.0), writes=braw)
        ptr, bptr = C.ps("ptr", [128, 1024], BF16)
        GP = [C.ps("gp%d" % i, [128, 512]) for i in range(5)]
        pT, bpT = C.ps("pT", [128, 1024])
        bo = None if fz else Buf("o", multi=True)
        if fz is not None and fz.get("debug"):
            print("L1a sbuf remaining", nc.sbuf_bytes_remaining)

        def tt(out, bo_, a, ba, b, bb_, op, eng="dve"):
            P.op(eng, lambda E: E.tensor_tensor(out=out, in0=a, in1=b, op=op), reads=ba if isinstance(ba, list) else [ba], writes=[bo_])

        for s_ in range(NST):
            for t in range(4):
                r0 = s_ * 512 + t * 128
                P.dma("sp", xt[:], x_d[r0:r0 + 128, :], writes=[bxt])
                rms_rstd(C, xt[:], bxt, 1024, sq[:], bsq, ss, bss)
                P.op("dve", lambda E: E.scalar_tensor_tensor(out=hn[:], in0=xt[:], scalar=ss[:, 0:1], in1=npre[:],
                                                             op0=ALU.mult, op1=ALU.mult), reads=[bxt, bss, bnpre], writes=[bhn])
                transpose8(C, hn, bhn, idb, bidb, ptr, bptr, hT[:, :, t * 128:(t + 1) * 128], bhT, eng="act")
            for ct in range(6):
                pa, bpa = GP[ct % 2]
                fns = [(lambda E, kt=kt, ct=ct, pa=pa: E.matmul(pa[:], lhsT=w[:, kt, ct * 128:(ct + 1) * 128], rhs=hT[:, kt, :],
                                                                start=(kt == 0), stop=(kt == 7))) for kt in range(8)]
                P.mm_group(fns, reads=[bw, bhT], writes=[bpa])
                P.op("act", lambda E, ct=ct, pa=pa: E.copy(out=raw[:, ct, 3:515], in_=pa[:]), reads=[bpa], writes=[braw[ct]])
                P.op("dve", lambda E, ct=ct: E.tensor_scalar(out=cvq[:], in0=raw[:, ct, 0:512], scalar1=cw[:, 0, ct:ct + 1], scalar2=None, op0=ALU.mult),
                     reads=[braw[ct], bcw], writes=[bcvq])
                for j in range(1, 4):
                    P.op("dve", lambda E, ct=ct, j=j: E.scalar_tensor_tensor(out=cvq[:], in0=raw[:, ct, j:j + 512], scalar=cw[:, j, ct:ct + 1], in1=cvq[:],
                                                                             op0=ALU.mult, op1=ALU.add), reads=[braw[ct], bcw, bcvq], writes=[bcvq])
                P.op("act", lambda E, ct=ct: E.copy(out=raw[:, ct, 0:3], in_=raw[:, ct, 512:515]), reads=[braw[ct]], writes=[braw[ct]])
                P.op("act", lambda E, ct=ct: E.activation(out=act[:, ct, :], in_=cvq[:], func=AF.Silu), reads=[bcvq], writes=[bact[ct]])
            if extu:
                for blk in range(2):
                    pa, bpa = GP[2 + blk]
                    fns = [(lambda E, kt=kt, blk=blk, pa=pa: E.matmul(
                        pa[:].rearrange("p (s n) -> p s n", s=16), lhsT=wu[:, kt, blk * 128:(blk + 1) * 128],
                        rhs=hT[:, kt, :].rearrange("p (n s) -> p s n", s=16), start=(kt == 0), stop=(kt == 7))) for kt in range(8)]
                    P.mm_group(fns, reads=[bwu, bhT], writes=[bpa])
                    P.op("act", lambda E, blk=blk, pa=pa, s_=s_: E.copy(out=uTp[:, blk, :, 32 * s_:32 * s_ + 32], in_=pa[:].rearrange("p (s n) -> p s n", s=16)),
                         reads=[bpa], writes=[buTp])
            for ct in range(4):
                pa, bpa = GP[ct % 2]
                P.op("act", lambda E, ct=ct: E.activation(out=cvq[:], in_=act[:, ct, :], func=AF.Square), reads=[bact[ct]], writes=[bcvq])
                P.op("pe", lambda E, pa=pa: E.matmul(pa[:], lhsT=ones[:], rhs=cvq[:], start=True, stop=True), reads=[bones, bcvq], writes=[bpa])
                P.op("act", lambda E, pa=pa: E.activation(out=rn[:], in_=pa[:], func=AF.Ln, bias=1e-6, scale=1.0), reads=[bpa], writes=[brn])
                P.op("act", lambda E: E.activation(out=rn[:], in_=rn[:], func=AF.Exp, scale=-0.5), reads=[brn], writes=[brn])
                if ct < 2:
                    P.op("dve", lambda E, ct=ct: E.scalar_tensor_tensor(out=qk[:, ct, :], in0=act[:, ct, :], scalar=float(128 ** -0.5), in1=rn[:],
                                                                        op0=ALU.mult, op1=ALU.mult), reads=[bact[ct], brn], writes=[bqk[ct]])
                else:
                    P.op("dve", lambda E, ct=ct: E.tensor_tensor(out=qk[:, ct, :], in0=act[:, ct, :], in1=rn[:], op=ALU.mult),
                         reads=[bact[ct], brn], writes=[bqk[ct]])
            pa, bpa = GP[2]
            fns = [(lambda E, kt=kt, pa=pa: E.matmul(pa[0:2, :], lhsT=wb[:, kt, 0:2], rhs=hT[:, kt, :], start=(kt == 0), stop=(kt == 7))) for kt in range(8)]
            P.mm_group(fns, reads=[bwb, bhT], writes=[bpa])
            P.op("act", lambda E, pa=pa: E.activation(out=brow[:], in_=pa[0:2, :], func=AF.Sigmoid), reads=[bpa], writes=[bbrow])
            pa2, bpa2 = GP[3]
            fns = [(lambda E, kt=kt, pa2=pa2: E.matmul(pa2[0:2, :], lhsT=wa[:, kt, 0:2], rhs=hT[:, kt, :], start=(kt == 0), stop=(kt == 7))) for kt in range(8)]
            P.mm_group(fns, reads=[bwa, bhT], writes=[bpa2])
            P.op("act", lambda E, pa2=pa2: E.activation(out=grow[:], in_=pa2[0:2, :], func=AF.Exp, bias=dtb[:, 0:1], scale=1.0), reads=[bpa2, bdtb], writes=[bgrow])
            P.op("act", lambda E: E.activation(out=grow[:], in_=grow[:], func=AF.Ln, bias=1.0, scale=1.0), reads=[bgrow], writes=[bgrow])
            P.op("dve", lambda E: E.tensor_scalar(out=grow[:], in0=grow[:], scalar1=negA[:, 0:1], scalar2=None, op0=ALU.mult), reads=[bgrow, bnegA], writes=[bgrow])
            P.op("dve", lambda E: E.tensor_tensor_scan(out=gcrow[:], data0=cmask[:], data1=grow[:], initial=0.0, op0=ALU.mult, op1=ALU.add),
                 reads=[bcmask, bgrow], writes=[bgcrow])
            for h in range(2):
                pa, bpa = GP[h]
                P.op("pe", lambda E, h=h, pa=pa: E.matmul(pa[:], lhsT=sel[:, h, :], rhs=gcrow[:], start=True, stop=True), reads=[bsel, bgcrow], writes=[bpa])
                P.op("act", lambda E, h=h, pa=pa: E.copy(out=GCB[h][0][:], in_=pa[:]), reads=[bpa], writes=[GCB[h][1]])
                pa, bpa = GP[2 + h]
                P.op("pe", lambda E, h=h, pa=pa: E.matmul(pa[:], lhsT=sel[:, h, :], rhs=brow[:], start=True, stop=True), reads=[bsel, bbrow], writes=[bpa])
                P.op("act", lambda E, h=h, pa=pa: E.copy(out=BB[h][0][:], in_=pa[:]), reads=[bpa], writes=[BB[h][1]])
            for h in range(2):
                qT = qk[:, h, :]; bqT = bqk[h]; kT = qk[:, 2 + h, :]; bkT = bqk[2 + h]; vT = act[:, 4 + h, :]; bvT = bact[4 + h]
                gcb, bgcb = GCB[h]; bb, bbb = BB[h]
                H = heads[h]
                attnT, battnT = H["attnT"]; Y, bY = H["Y"]; EG, bEG = H["EG"]; qdec, bqdec = H["qdec"]
                Ybf, bYbf = H["Ybf"]
                bv, bbv = H["bv"]; kdec, bkdec = H["kdec"]; nbg, bnbg = H["nbg"]
                arg1, barg1 = m64["arg1"]; DT, bDT = m64["DT"]; Ds, bDs = m64["Ds"]
                tmp, btmp = m64["tmp"]; tmp2, btmp2 = m64["tmp2"]; BBm, bBBm = m64["BBm"]
                gccol, bgccol = small["gccol"]; bcol, bbcol = small["bcol"]; nbcol, bnbcol = small["nbcol"]
                elast, belast = small["elast"]; egc, begc = small["egc"]
                v3 = lambda t_: t_[:].rearrange("p (n f) -> p n f", f=64)
                i64b = I64.unsqueeze(1).to_broadcast([64, 8, 64])
                tt(v3(tmp), btmp, gcb[0:64, :].rearrange("p (n f) -> p n f", f=64), [bgcb, bc64], i64b, bc64, ALU.mult)
                P.op("dve", lambda E, tmp=tmp, gccol=gccol: E.tensor_reduce(out=gccol[:], in_=tmp[:].rearrange("p (n f) -> p n f", f=64), axis=AX.X, op=ALU.add), reads=[btmp], writes=[bgccol])
                tt(v3(tmp), btmp, bb[0:64, :].rearrange("p (n f) -> p n f", f=64), [bbb, bc64], i64b, bc64, ALU.mult)
                P.op("dve", lambda E, tmp=tmp, bcol=bcol: E.tensor_reduce(out=bcol[:], in_=tmp[:].rearrange("p (n f) -> p n f", f=64), axis=AX.X, op=ALU.add), reads=[btmp], writes=[bbcol])
                P.op("dve", lambda E: E.tensor_scalar(out=nbcol[:], in0=bcol[:], scalar1=-1.0, scalar2=None, op0=ALU.mult), reads=[bbcol], writes=[bnbcol])
                tt(v3(arg1), barg1, gcb[0:64, :].rearrange("p (n f) -> p n f", f=64), [bgcb, bgccol], gccol[:].unsqueeze(2).to_broadcast([64, 8, 64]), bgccol, ALU.subtract)
                tt(v3(DT), bDT, v3(arg1), [barg1, bc64], NEGU.unsqueeze(1).to_broadcast([64, 8, 64]), bc64, ALU.add)
                P.op("act", lambda E: E.activation(out=DT[:], in_=DT[:], func=AF.Exp), reads=[bDT], writes=[bDT])
                P.op("dve", lambda E: E.scalar_tensor_tensor(out=Ds[:].rearrange("p (n f) -> p n f", f=64), in0=arg1[:].rearrange("p (n f) -> p n f", f=64), scalar=-1.0,
                                                             in1=NEGLS.unsqueeze(1).to_broadcast([64, 8, 64]), op0=ALU.mult, op1=ALU.add), reads=[barg1, bc64], writes=[bDs])
                P.op("act", lambda E: E.activation(out=Ds[:], in_=Ds[:], func=AF.Exp), reads=[bDs], writes=[bDs])
                tt(v3(BBm), bBBm, bb[0:64, :].rearrange("p (n f) -> p n f", f=64), [bbb, bc64], NSU.unsqueeze(1).to_broadcast([64, 8, 64]), bc64, ALU.mult)
                pk, bpk = GP[0]; pq, bpq = GP[1]
                fns = [(lambda E, n=n, pk=pk, kT=kT: E.matmul(pk[0:64, n * 64:(n + 1) * 64], lhsT=kT[:, n * 64:(n + 1) * 64], rhs=kT[:, n * 64:(n + 1) * 64],
                                                              start=True, stop=True)) for n in range(8)]
                P.mm_group(fns, reads=[bkT], writes=[bpk])
                fns = [(lambda E, n=n, pq=pq, kT=kT, qT=qT: E.matmul(pq[0:64, n * 64:(n + 1) * 64], lhsT=kT[:, n * 64:(n + 1) * 64], rhs=qT[:, n * 64:(n + 1) * 64],
                                                                     start=True, stop=True)) for n in range(8)]
                P.mm_group(fns, reads=[bkT, bqT], writes=[bpq])
                tt(attnT[:], battnT, pq[0:64, :], [bpq, bDT], DT[:], bDT, ALU.mult)
                Pc, bPc = m64["Pa"]; Pn, bPn = m64["Pb"]; Qc, bQc = m64["Qa"]; Qn, bQn = m64["Qb"]
                tt(tmp[:], btmp, pk[0:64, :], [bpk, bDT], DT[:], bDT, ALU.mult)
                tt(Qc[:], bQc, tmp[:], [btmp, bBBm], BBm[:], bBBm, ALU.mult)
                tt(tmp2[:], btmp2, pk[0:64, :], [bpk, bDs], Ds[:], bDs, ALU.mult)
                tt(v3(Pc), bPc, v3(tmp2), [btmp2, bnbcol], nbcol[:].unsqueeze(2).to_broadcast([64, 8, 64]), bnbcol, ALU.mult)
                tt(v3(Y), bY, v3(Qc), [bQc, bc64], i64b, bc64, ALU.add)
                P.op("act", lambda E, Ybf=Ybf, Y=Y: E.copy(out=Ybf[:], in_=Y[:]), reads=[bY], writes=[bYbf])
                for j in range(5):
                    pP, bpP = GP[2]; pQ, bpQ = GP[3]
                    fns = [(lambda E, n=n, pP=pP, Qc=Qc, Pc=Pc: E.matmul(pP[0:64, n * 64:(n + 1) * 64], lhsT=Qc[:, n * 64:(n + 1) * 64], rhs=Pc[:, n * 64:(n + 1) * 64],
                                                                         start=True, stop=True)) for n in range(8)]
                    P.mm_group(fns, reads=[bQc, bPc], writes=[bpP])
                    if j < 4:
                        fns = [(lambda E, n=n, pQ=pQ, Qc=Qc, Pc=Pc: E.matmul(pQ[0:64, n * 64:(n + 1) * 64], lhsT=Pc[:, n * 64:(n + 1) * 64], rhs=Qc[:, n * 64:(n + 1) * 64],
                                                                             start=True, stop=True)) for n in range(8)]
                        P.mm_group(fns, reads=[bQc, bPc], writes=[bpQ])
                    P.op("act", lambda E, Pn=Pn, pP=pP: E.copy(out=Pn[:], in_=pP[0:64, :]), reads=[bpP], writes=[bPn])
                    if j < 4:
                        P.op("dve", lambda E, Qn=Qn, pQ=pQ: E.tensor_copy(out=Qn[:], in_=pQ[0:64, :]), reads=[bpQ], writes=[bQn])
                    pY, bpY = GP[0]
                    fns = [(lambda E, n=n, pY=pY, Pn=Pn, Ybf=Ybf: E.matmul(pY[0:64, n * 64:(n + 1) * 64], lhsT=Pn[:, n * 64:(n + 1) * 64], rhs=Ybf[:, n * 64:(n + 1) * 64],
                                                                         start=True, stop=True)) for n in range(8)]
                    P.mm_group(fns, reads=[bPn, bYbf], writes=[bpY])
                    tt(Y[:], bY, Y[:], [bY, bpY], pY[0:64, :], bpY, ALU.add)
                    P.op("act", lambda E, Ybf=Ybf, Y=Y: E.copy(out=Ybf[:], in_=Y[:]), reads=[bY], writes=[bYbf])
                    Pc, bPc, Pn, bPn = Pn, bPn, Pc, bPc
                    Qc, bQc, Qn, bQn = Qn, bQn, Qc, bQc
                fns = [(lambda E, n=n, vT=vT: E.transpose(out=pT[0:64, n * 128:(n + 1) * 128], in_=vT[:, n * 64:(n + 1) * 64], identity=idf[:])) for n in range(8)]
                P.mm_group(fns, reads=[bvT, bidf], writes=[bpT])
                tt(bv[:], bbv, pT[0:64, :].rearrange("p (n d) -> p n d", d=128), [bpT, bbcol], bcol[:].unsqueeze(2).to_broadcast([64, 8, 128]), bbcol, ALU.mult)
                tt(elast[:], belast, gcb[0:64, :].rearrange("p (n f) -> p n f", f=64)[:, :, 63], [bgcb, bgccol], gccol[:], bgccol, ALU.subtract)
                P.op("act", lambda E: E.activation(out=elast[:], in_=elast[:], func=AF.Exp), reads=[belast], writes=[belast])
                fns = [(lambda E, n=n, kT=kT: E.transpose(out=pT[0:64, n * 128:(n + 1) * 128], in_=kT[:, n * 64:(n + 1) * 64], identity=idf[:])) for n in range(8)]
                P.mm_group(fns, reads=[bkT, bidf], writes=[bpT])
                tt(kdec[:], bkdec, pT[0:64, :].rearrange("p (n d) -> p n d", d=128), [bpT, belast], elast[:].unsqueeze(2).to_broadcast([64, 8, 128]), belast, ALU.mult)
                P.op("act", lambda E, gcb=gcb, EG=EG: E.activation(out=EG[:], in_=gcb[:], func=AF.Exp), reads=[bgcb], writes=[bEG])
                tt(qdec[:], bqdec, qT, [bqT, bEG], EG[:], bEG, ALU.mult)
                P.op("act", lambda E: E.activation(out=egc[:], in_=gccol[:], func=AF.Exp), reads=[bgccol], writes=[begc])
                P.op("dve", lambda E, nbg=nbg: E.scalar_tensor_tensor(out=nbg[:], in0=egc[:], scalar=-1.0, in1=bcol[:], op0=ALU.mult, op1=ALU.mult),
                     reads=[begc, bbcol], writes=[bnbg])
            banks = [(GP[0], GP[1], GP[2]), (GP[3], GP[4], (pT, bpT))]
            for n in range(8):
                cs = slice(n * 64, (n + 1) * 64)
                for h in range(2):
                    kT = qk[:, 2 + h, :]; bkT = bqk[2 + h]
                    H = heads[h]; S, bS = Sst[h]
                    attnT, battnT = H["attnT"]; Y, bY = H["Ybf"]; EG, bEG = H["EG"]; qdec, bqdec = H["qdec"]
                    bv, bbv = H["bv"]; kdec, bkdec = H["kdec"]; nbg, bnbg = H["nbg"]
                    vnew, bvnew = H["vnew"]; rhs2, brhs2 = H["rhs2"]; osb, bosb = H["osb"]
                    (KSO, bKSO), (Vb, bVb), (Sb, bSb) = banks[h]
                    P.op("pe", lambda E, cs=cs, kT=kT, S=S, KSO=KSO: E.matmul(KSO[0:64, 0:128], lhsT=kT[:, cs], rhs=S[:], start=True, stop=True),
                         reads=[bkT, bS], writes=[bKSO])
                    P.op("dve", lambda E, n=n, KSO=KSO, rhs2=rhs2, nbg=nbg, bv=bv: E.scalar_tensor_tensor(
                        out=rhs2[:], in0=KSO[0:64, 0:128], scalar=nbg[:, n:n + 1], in1=bv[:, n, :], op0=ALU.mult, op1=ALU.add),
                        reads=[bKSO, bnbg, bbv], writes=[brhs2])
                    P.op("pe", lambda E, cs=cs, Y=Y, Vb=Vb, rhs2=rhs2: E.matmul(Vb[0:64, 0:128], lhsT=Y[:, cs], rhs=rhs2[:], start=True, stop=True),
                         reads=[bY, brhs2], writes=[bVb])
                    P.op("act", lambda E, vnew=vnew, Vb=Vb: E.copy(out=vnew[:], in_=Vb[0:64, 0:128]), reads=[bVb], writes=[bvnew])
                    fns = [lambda E, cs=cs, S=S, KSO=KSO, qdec=qdec: E.matmul(KSO[64:128, 0:128], lhsT=qdec[:, cs], rhs=S[:], start=True, stop=False),
                           lambda E, cs=cs, KSO=KSO, attnT=attnT, vnew=vnew: E.matmul(KSO[64:128, 0:128], lhsT=attnT[:, cs], rhs=vnew[:], start=False, stop=True)]
                    P.mm_group(fns, reads=[bqdec, bS, battnT, bvnew], writes=[bKSO])
                    P.op("pe", lambda E, n=n, Sb=Sb, kdec=kdec, vnew=vnew: E.matmul(Sb[:, 0:128], lhsT=kdec[:, n, :], rhs=vnew[:], start=True, stop=True),
                         reads=[bkdec, bvnew], writes=[bSb])
                    P.op("dve", lambda E, n=n, S=S, EG=EG, Sb=Sb: E.scalar_tensor_tensor(out=S[:], in0=S[:], scalar=EG[:, n * 64 + 63:n * 64 + 64], in1=Sb[:, 0:128],
                                                                                         op0=ALU.mult, op1=ALU.add), reads=[bS, bEG, bSb], writes=[bS])
                    P.op("act", lambda E, n=n, osb=osb, KSO=KSO: E.copy(out=osb[64:128, n, :], in_=KSO[64:128, 0:128]), reads=[bKSO], writes=[bosb])
            for h in range(2):
                osb, bosb = heads[h]["osb"]
                P.dma("sp", o_d[s_ * 512:(s_ + 1) * 512, h * 128:(h + 1) * 128].rearrange("(n c) d -> c n d", c=64), osb[64:128, :, :], reads=[bosb],
                      writes=[fz["obuf_of"](s_) if fz else bo])
            if fz:
                fz["after_chunk"](s_)
        if fz:
            barrier(P)
        else:
            P.finish([bo])
    return nc


def run_L1a(inp):
    nc = _get("L1a", build_L1a)
    c64, cmask, sel = _gdn_consts()
    w_in = inp["w_in_even"][0]
    conv = inp["conv_qkv"][0]
    ones = np.ones((128, 128), np.float32)
    maps = []
    for c in range(8):
        b, r = divmod(c, 4)
        cols = np.concatenate([np.arange(256 * r, 256 * r + 256), 1024 + np.arange(256 * r, 256 * r + 256), 2048 + np.arange(256 * r, 256 * r + 256)])
        maps.append({"x": np.ascontiguousarray(inp["x"][b]), "npre": np.ascontiguousarray(inp["norm_pre"][0]),
                     "w": np.ascontiguousarray(w_in[:, cols]), "wb": np.ascontiguousarray(w_in[:, 4096 + 2 * r:4096 + 2 * r + 2]),
                     "wa": np.ascontiguousarray(w_in[:, 4104 + 2 * r:4104 + 2 * r + 2]), "conv": np.ascontiguousarray(conv[:, cols]),
                     "alog": np.ascontiguousarray(inp["a_log"][0, 2 * r:2 * r + 2]), "dtb": np.ascontiguousarray(inp["dt_bias"][0, 2 * r:2 * r + 2]),
                     "ident": _IDENT, "c64": c64, "cmask": cmask, "sel": sel, "ones": ones})
    res = run_bass_kernel_spmd(nc, maps, core_ids=list(range(8)))
    S_ = inp["x"].shape[1]
    o = np.empty((2, S_, 1024), np.float32)
    for c in range(8):
        b, r = divmod(c, 4)
        o[b, :, 256 * r:256 * (r + 1)] = res.results[c]["o"]
    return o


def kernel_unfused(**inputs):
    inp = {k: np.asarray(v) for k, v in inputs.items()}
    o = run_L1a(inp)
    ys = run_L1b(inp)
    x1 = run_L2(inp, o, ys)
    out = run_L3(inp, x1)
    return out.astype(np.float32)


def build_fused():
    nc = bass.Bass("TRN2", target_bir_lowering=False)
    x_full = nc.dram_tensor("x", [8192, 1024], F32, kind="ExternalInput").ap()
    ident_d = nc.dram_tensor("ident", [128, 128], F32, kind="ExternalInput").ap()
    npre0_d = nc.dram_tensor("npre0", [1024], F32, kind="ExternalInput").ap()
    gidx_d = nc.dram_tensor("gidx", [128, 17, 4], I32, kind="ExternalInput").ap()
    out_d = nc.dram_tensor("out", [2048, 1024], F32, kind="ExternalOutput").ap()
    ag_in = [nc.dram_tensor("ag_in%d" % i, [8192, 256], F32) for i in range(2)]
    ag_out = [nc.dram_tensor("ag_out%d" % i, [4 * 8192, 256], F32) for i in range(2)]
    x1s = nc.dram_tensor("x1s", [2176, 1024], F32)
    GROUPS = [[0, 1, 2, 3], [4, 5, 6, 7]]
    with ExitStack() as st:
        C = Ctx(nc, st); P = C.P
        csem = st.enter_context(nc.semaphore("csem"))
        bag_out = Buf("ag_out"); bx1s = Buf("x1s", multi=True); bout = Buf("out", multi=True)
        bo_ch = [Buf("o_ch%d" % k, multi=True) for k in range(16)]
        by_jt = [Buf("y_jt%d" % k, multi=True) for k in range(4)]
        ncc = [0]

        def emit_cc(which, k, inbuf):
            P._deps("pool", [inbuf], [])
            P.streams["pool"].append(lambda E, which=which, k=k: E.collective_compute(
                "AllGather", ALU.bypass, replica_groups=GROUPS,
                ins=[ag_in[which].ap()[k * 512:(k + 1) * 512, :].opt()], outs=[ag_out[which].ap()[k * 2048:(k + 1) * 2048, :].opt()]).then_inc(csem))
            ncc[0] += 1

        share1 = {"x": x_full, "ident": ident_d, "npre": npre0_d}

        def after_jt(jt):
            for k in range(4 * jt, 4 * jt + 4):
                emit_cc(1, k, by_jt[jt])

        with ExitStack() as stU:
            CU = Ctx(nc, stU, P, "u_")
            uext = CU.sb("uTp", [128, 2, 16, 512], BF16)
            build_L1a(8192, fz={"nc": nc, "P": P, "pfx": "a_", "share": share1, "out": ag_in[0].ap(), "uTp": uext,
                                "obuf_of": lambda s_: bo_ch[s_], "after_chunk": lambda s_: emit_cc(0, s_, bo_ch[s_])})
            build_L1b(8192, fz={"nc": nc, "P": P, "pfx": "b_", "share": share1, "out": ag_in[1].ap(), "uTp": uext,
                                "obuf_of": lambda jt: by_jt[jt], "after_chunk": after_jt})
        P.streams["pool"].append(lambda E: E.wait_ge(csem, ncc[0]))
        gidx, bgidx = C.sb("gidx", [128, 17, 4], I32)
        P.dma("sp", gidx[:], gidx_d, writes=[bgidx])
        P.op("pool", lambda E: E.nop(), reads=[], writes=[bag_out])

        def gather(P_, ld, bld, tile, part):
            for i in range(4):
                P_.dma_ind("pool", ld[:, i * 256:(i + 1) * 256], ag_out[part].ap(), gidx[:, tile, i:i + 1], reads=[bag_out, bgidx], writes=[bld])

        share2 = {"ident": ident_d, "npre": npre0_d, "o": None, "ys": None}
        build_L2(2176, fz={"nc": nc, "P": P, "pfx": "c_", "share": share2, "out": x1s.ap(), "obuf": bx1s, "gather": gather})
        share3 = {"ident": ident_d, "x": x1s.ap()}
        build_L3(2048, fz={"nc": nc, "P": P, "pfx": "d_", "share": share3, "out": out_d, "obuf": bout, "xbuf": bx1s})
        P.finish([bout])
    return nc


def _gidx(r):
    g = np.zeros((128, 17, 4), np.int32)
    p = np.arange(128)[:, None, None]
    tile = np.arange(17)[None, :, None]
    src = np.arange(4)[None, None, :]
    tok = np.clip(2048 * r - 128 + tile * 128 + p, 0, 8191)
    g[:] = ((tok // 512) * 4 + src) * 512 + tok % 512
    return g


def kernel(**inputs):
    inp = {k: np.ascontiguousarray(np.asarray(v)) for k, v in inputs.items()}
    nc = _get("fused", build_fused)
    c64, cmask, sel = _gdn_consts()
    mk, idm = _s5_consts()
    ones = np.ones((128, 128), np.float32)
    w_in = inp["w_in_even"][0]
    conv = inp["conv_qkv"][0]
    wz = np.ascontiguousarray(np.concatenate([w_in[:, 3072:4096], w_in[:, 5136:6160]], axis=1))
    maps = []
    for c in range(8):
        b, r = divmod(c, 4)
        cols = np.concatenate([np.arange(256 * r, 256 * r + 256), 1024 + np.arange(256 * r, 256 * r + 256), 2048 + np.arange(256 * r, 256 * r + 256)])
        gs = slice(16 * r, 16 * r + 16)
        xq = np.zeros((2176, 1024), np.float32)
        xq[128:] = inp["x"][b, 2048 * r:2048 * (r + 1)]
        if r > 0:
            xq[:128] = inp["x"][b, 2048 * r - 128:2048 * r]
        m = {"x": inp["x"][b], "ident": _IDENT, "npre0": inp["norm_pre"][0], "gidx": _gidx(r),
             "a_w": np.ascontiguousarray(w_in[:, cols]), "a_wb": np.ascontiguousarray(w_in[:, 4096 + 2 * r:4096 + 2 * r + 2]),
             "a_wa": np.ascontiguousarray(w_in[:, 4104 + 2 * r:4104 + 2 * r + 2]), "a_conv": np.ascontiguousarray(conv[:, cols]),
             "a_alog": np.ascontiguousarray(inp["a_log"][0, 2 * r:2 * r + 2]), "a_dtb": np.ascontiguousarray(inp["dt_bias"][0, 2 * r:2 * r + 2]),
             "a_c64": c64, "a_cmask": cmask, "a_sel": sel, "a_ones": ones,
             "a_wu": np.ascontiguousarray(w_in[:, 4112 + 256 * r:4112 + 256 * (r + 1)]),
             "b_wu": np.ascontiguousarray(w_in[:, 4112 + 256 * r:4112 + 256 * (r + 1)]),
             "b_lre": np.ascontiguousarray(inp["s5_lam_re"][0, gs]), "b_lim": np.ascontiguousarray(inp["s5_lam_im"][0, gs]),
             "b_bre": np.ascontiguousarray(inp["s5_b_re"][0, gs]), "b_bim": np.ascontiguousarray(inp["s5_b_im"][0, gs]),
             "b_cre": np.ascontiguousarray(inp["s5_c_re"][0, gs]), "b_cim": np.ascontiguousarray(inp["s5_c_im"][0, gs]),
             "b_ldt": np.ascontiguousarray(inp["s5_log_dt"][0, gs]), "b_dd": np.ascontiguousarray(inp["s5_d"][0, 256 * r:256 * (r + 1)]),
             "b_taus": TAUS, "b_mk": mk, "b_idm": idm,
             "c_x": xq, "c_wz": wz, "c_wglu": inp["w_glu"][0], "c_wout": inp["w_out_even"][0], "c_npost": inp["norm_post"][0],
             "c_gnw": inp["gdn_norm_w"][0],
             "d_win": inp["w_in_odd"][0], "d_wout": inp["w_out_odd"][0], "d_conv": inp["conv_short"][0],
             "d_npre": inp["norm_pre"][1], "d_npost": inp["norm_post"][1]}
        maps.append(m)
    res = run_bass_kernel_spmd(nc, maps, core_ids=list(range(8)))
    out = np.empty((2, 8192, 1024), np.float32)
    for c in range(8):
        b, r = divmod(c, 4)
        out[b, r * 2048:(r + 1) * 2048] = res.results[c]["out"]
    return out
```

```python
from contextlib import ExitStack
import numpy as np
import concourse.bass as bass
import concourse.mybir as mybir
from concourse.bass_utils import run_bass_kernel_spmd

F32 = mybir.dt.float32
BF16 = mybir.dt.bfloat16
AF = mybir.ActivationFunctionType
ALU = mybir.AluOpType
AX = mybir.AxisListType

NDS = 12


class Buf:
    __slots__ = ("name", "w", "r", "multi")

    def __init__(self, name, multi=False):
        self.name = name
        self.w = [] if multi else None
        self.r = []
        self.multi = multi


class Prog:
    ENG = ("pe", "act", "dve", "pool", "sp")

    def __init__(self, nc, stack):
        self.nc = nc
        self.stack = stack
        self.streams = {e: [] for e in self.ENG}
        self.cnt = {e: 0 for e in self.ENG}
        self.sem = {e: stack.enter_context(nc.semaphore("s_" + e)) for e in self.ENG}
        self.seen = {e: {} for e in self.ENG}
        self.dcnt = {e: 0 for e in self.ENG}
        self.dsem = {}
        for e in ("sp", "pool", "act"):
            self.dsem[e] = [stack.enter_context(nc.semaphore("d_%s%d" % (e, i))) for i in range(NDS)]
        self.same_engine_sync = True
        self.nwaits = 0

    def _wait(self, eng, tok):
        if tok is None:
            return
        kind = tok[0]
        if kind == "c":
            _, e2, n = tok
            if e2 == eng and (eng == "pe" or not self.same_engine_sync):
                return
            key = e2
            if self.seen[eng].get(key, 0) >= n:
                return
            self.seen[eng][key] = n
            sem = self.sem[e2]
            self.streams[eng].append(lambda E, sem=sem, n=n: E.wait_ge(sem, n))
            self.nwaits += 1
        else:
            _, q, slot, val = tok
            key = ("d", q, slot)
            if self.seen[eng].get(key, 0) >= val:
                return
            self.seen[eng][key] = val
            sem = self.dsem[q][slot]
            self.streams[eng].append(lambda E, sem=sem, val=val: E.wait_ge(sem, val))
            self.nwaits += 1

    def _deps(self, eng, reads, writes):
        for b in reads:
            if b.multi:
                for t in b.w:
                    self._wait(eng, t)
            else:
                self._wait(eng, b.w)
        for b in writes:
            if not b.multi:
                self._wait(eng, b.w)
            for t in b.r:
                self._wait(eng, t)

    def _commit(self, tok, reads, writes):
        for b in writes:
            if b.multi:
                b.w.append(tok)
            else:
                b.w = tok
            b.r = []
        for b in reads:
            if b not in writes:
                b.r.append(tok)

    def op(self, eng, fn, reads=(), writes=()):
        reads = list(reads)
        writes = list(writes)
        self._deps(eng, reads, writes)
        self.cnt[eng] += 1
        n = self.cnt[eng]
        sem = self.sem[eng]
        self.streams[eng].append(lambda E, fn=fn, sem=sem: fn(E).then_inc(sem, 1))
        tok = ("c", eng, n)
        self._commit(tok, reads, writes)
        return tok

    def mm_group(self, fns, reads=(), writes=()):
        eng = "pe"
        reads = list(reads)
        writes = list(writes)
        self._deps(eng, reads, writes)
        self.cnt[eng] += 1
        n = self.cnt[eng]
        sem = self.sem[eng]
        for fn in fns[:-1]:
            self.streams[eng].append(lambda E, fn=fn: fn(E))
        last = fns[-1]
        self.streams[eng].append(lambda E, fn=last, sem=sem: fn(E).then_inc(sem, 1))
        tok = ("c", eng, n)
        self._commit(tok, reads, writes)
        return tok

    def dma(self, q, out_ap, in_ap, reads=(), writes=()):
        reads = list(reads)
        writes = list(writes)
        self._deps(q, reads, writes)
        j = self.dcnt[q]
        self.dcnt[q] += 1
        slot = j % NDS
        val = 16 * (j // NDS + 1)
        if j >= NDS:
            self._wait(q, ("d", q, slot, val - 16))
        sem = self.dsem[q][slot]
        self.streams[q].append(
            lambda E, o=out_ap, i=in_ap, sem=sem: E.dma_start(out=o, in_=i).then_inc(sem, 16))
        tok = ("d", q, slot, val)
        self._commit(tok, reads, writes)
        return tok

    def dma_ind(self, q, out_ap, table_ap, idx_ap, reads=(), writes=()):
        reads = list(reads)
        writes = list(writes)
        self._deps(q, reads, writes)
        j = self.dcnt[q]
        self.dcnt[q] += 1
        slot = j % NDS
        val = 16 * (j // NDS + 1)
        if j >= NDS:
            self._wait(q, ("d", q, slot, val - 16))
        sem = self.dsem[q][slot]
        self.streams[q].append(
            lambda E, o=out_ap, t=table_ap, i=idx_ap, sem=sem: E.indirect_dma_start(
                out=o, out_offset=None, in_=t, in_offset=bass.IndirectOffsetOnAxis(ap=i, axis=0)).then_inc(sem, 16))
        tok = ("d", q, slot, val)
        self._commit(tok, reads, writes)
        return tok

    def finish(self, final_bufs):
        for b in final_bufs:
            for t in (b.w if b.multi else [b.w]):
                self._wait("sp", t)
        nc = self.nc
        streams = self.streams
        with nc.Block() as block:
            @block.tensor
            def _(E):
                for f in streams["pe"]:
                    f(E)

            @block.scalar
            def _(E):
                for f in streams["act"]:
                    f(E)

            @block.vector
            def _(E):
                for f in streams["dve"]:
                    f(E)

            @block.gpsimd
            def _(E):
                for f in streams["pool"]:
                    f(E)

            @block.sync
            def _(E):
                for f in streams["sp"]:
                    f(E)


class Ctx:
    def __init__(self, nc, st, P=None, pfx=""):
        self.nc = nc
        self.st = st
        self.pfx = pfx
        if P is None:
            st.enter_context(nc.allow_non_contiguous_dma(reason="small parameter loads / layout transforms"))
            P = Prog(nc, st)
        self.P = P

    def sb(self, name, shape, dt=F32):
        t = self.st.enter_context(self.nc.sbuf_tensor("sb_" + self.pfx + name, shape, dt))
        return t, Buf(name)

    def ps(self, name, shape, dt=F32):
        t = self.st.enter_context(self.nc.psum_tensor("ps_" + self.pfx + name, shape, dt))
        return t, Buf(name)


def bcast_row_load(C, name, dram_vec, n, q="sp"):
    t, b = C.sb(name, [128, n])
    C.P.dma(q, t[:], dram_vec.partition_broadcast(128), writes=[b])
    return t, b


def make_ident(C, dram_ident):
    idf, bidf = C.sb("identf", [128, 128])
    C.P.dma("sp", idf[:], dram_ident, writes=[bidf])
    idb, bidb = C.sb("identb", [128, 128], BF16)
    C.P.op("dve", lambda E: E.tensor_copy(out=idb[:], in_=idf[:]), reads=[bidf], writes=[bidb])
    return idf, bidf, idb, bidb


def rms_rstd(C, src, bsrc, ncols, junk, bjunk, ss, bss, eps=1e-6):
    P = C.P
    P.op("act", lambda E: E.activation(out=junk, in_=src, func=AF.Square, accum_out=ss[:, 0:1]),
         reads=[bsrc], writes=[bjunk, bss])
    P.op("act", lambda E: E.activation(out=ss[:, 0:1], in_=ss[:, 0:1], func=AF.Sqrt, bias=float(eps), scale=float(1.0 / ncols)),
         reads=[bss], writes=[bss])
    P.op("dve", lambda E: E.reciprocal(out=ss[:, 0:1], in_=ss[:, 0:1]), reads=[bss], writes=[bss])


def transpose8(C, src_bf, bsrc, idb, bidb, ptr, bptr, dst3, bdst, eng="act"):
    P = C.P
    fns = [(lambda E, kt=kt: E.transpose(out=ptr[:, kt * 128:(kt + 1) * 128], in_=src_bf[:, kt * 128:(kt + 1) * 128],
                                         identity=idb[:])) for kt in range(8)]
    P.mm_group(fns, reads=[bsrc, bidb], writes=[bptr])
    src3 = ptr[:].rearrange("p (k t) -> p k t", k=8)
    if eng == "act":
        P.op("act", lambda E: E.copy(out=dst3, in_=src3), reads=[bptr], writes=[bdst])
    else:
        P.op("dve", lambda E: E.tensor_copy(out=dst3, in_=src3), reads=[bptr], writes=[bdst])


def outproj_post(C, catT, bcat, nkt, wout, bwout, t, xres, bxres, npw, bnpw, pso, bpso, yo, byo, junk, bjunk, ss, bss,
                 out_dram_rows, bout):
    P = C.P
    for hh in range(2):
        fns = [(lambda E, kt=kt, hh=hh: E.matmul(pso[hh][:], lhsT=catT[:, kt, t * 128:(t + 1) * 128],
                                                 rhs=wout[:, kt, hh * 512:(hh + 1) * 512],
                                                 start=(kt == 0), stop=(kt == nkt - 1))) for kt in range(nkt)]
        P.mm_group(fns, reads=[bcat, bwout], writes=[bpso[hh]])
        P.op("act", lambda E, hh=hh: E.copy(out=yo[:, hh * 512:(hh + 1) * 512], in_=pso[hh][:]),
             reads=[bpso[hh]], writes=[byo])
    rms_rstd(C, yo[:], byo, 1024, junk[:], bjunk, ss, bss)
    P.op("dve", lambda E: E.scalar_tensor_tensor(out=yo[:], in0=yo[:], scalar=ss[:, 0:1], in1=npw[:],
                                                 op0=ALU.mult, op1=ALU.mult), reads=[byo, bss, bnpw], writes=[byo])
    P.op("dve", lambda E: E.tensor_tensor(out=yo[:], in0=yo[:], in1=xres, op=ALU.add), reads=[byo, bxres], writes=[byo])
    P.dma("sp", out_dram_rows, yo[:], reads=[byo], writes=[bout])


def load_w_bf16(C, name, dram_w, kt_n, ncols, chunk=2048):
    w, bw = C.sb(name, [128, kt_n, ncols], BF16)
    src = dram_w.rearrange("(k p) c -> p k c", p=128)
    for kt in range(kt_n):
        for c0 in range(0, ncols, chunk):
            c1 = min(ncols, c0 + chunk)
            C.P.dma("pool", w[:, kt, c0:c1], src[:, kt, c0:c1], writes=[bw])
    return w, bw


def build_L2(ntok=2048, fz=None):
    nc = fz["nc"] if fz else bass.Bass("TRN2", target_bir_lowering=False)
    pfx = fz["pfx"] if fz else ""

    def D(name, shape):
        if fz and name in fz["share"]:
            return fz["share"][name]
        return nc.dram_tensor(pfx + name, shape, F32, kind="ExternalInput").ap()
    x_d = D("x", [ntok, 1024]); o_d = D("o", [ntok, 1024]); ys_d = D("ys", [ntok, 1024])
    wz_d = D("wz", [1024, 2048]); wglu_d = D("wglu", [1024, 1024]); wout_d = D("wout", [2048, 1024])
    npre_d = D("npre", [1024]); npost_d = D("npost", [1024]); gnw_d = D("gnw", [128]); ident_d = D("ident", [128, 128])
    out_d = fz["out"] if fz else nc.dram_tensor("out", [ntok, 1024], F32, kind="ExternalOutput").ap()
    NT = 512
    with ExitStack() as st:
        C = Ctx(nc, st, fz["P"], pfx) if fz else Ctx(nc, st); P = C.P
        idf, bidf, idb, bidb = make_ident(C, ident_d)
        npre, bnpre = bcast_row_load(C, "npre", npre_d, 1024)
        npost, bnpost = bcast_row_load(C, "npost", npost_d, 1024)
        gnw, bgnw = bcast_row_load(C, "gnw", gnw_d, 128)
        wz, bwz = load_w_bf16(C, "wz", wz_d, 8, 2048)
        wglu, bwglu = load_w_bf16(C, "wglu", wglu_d, 8, 1024)
        wout, bwout = load_w_bf16(C, "wout", wout_d, 16, 1024)
        xt4, bxt4 = C.sb("xt4", [128, 4, 1024]); bxt = [Buf("xt%d" % i) for i in range(4)]
        ldo = [C.sb("ldo%d" % i, [128, 1024]) for i in range(2)]
        ldy = [C.sb("ldy%d" % i, [128, 1024]) for i in range(2)]
        sq, bsq = C.sb("sq", [128, 1024])
        hn, bhn = C.sb("hn", [128, 1024], BF16)
        ss, bss = C.sb("ss", [128, 1])
        ss8, bss8 = C.sb("ss8", [128, 8])
        hT, bhT = C.sb("hT", [128, 8, NT], BF16)
        oT, boT = C.sb("oT", [128, 8, NT], BF16)
        yT, byT = C.sb("yT", [128, 8, NT], BF16)
        gz, bgz = C.sb("gz", [128, 8, NT], BF16)
        sg, bsg = C.sb("sg", [128, NT], BF16)
        catT, bcat = C.sb("catT", [128, 16, NT], BF16)
        yo, byo = C.sb("yo", [128, 1024])
        ptr, bptr = C.ps("ptr", [128, 1024], BF16)
        pmm = []; bpmm = []
        for i in range(4):
            t_, b_ = C.ps("pmm%d" % i, [128, 512]); pmm.append(t_); bpmm.append(b_)
        pso = []; bpso = []
        for i in range(2):
            t_, b_ = C.ps("pso%d" % i, [128, 512]); pso.append(t_); bpso.append(b_)
        bout = fz["obuf"] if fz else Buf("out", multi=True)
        if fz:
            sts = [(0, 128)] + [(128 + i * NT, NT) for i in range((ntok - 128) // NT)]
        else:
            sts = [(i * NT, NT) for i in range(ntok // NT)]
        tile_r0 = [t0_ + t_ * 128 for (t0_, n_) in sts for t_ in range(n_ // 128)]

        def issue_loads(ti):
            r0_ = tile_r0[ti]
            lo, blo = ldo[ti % 2]; ly, bly = ldy[ti % 2]
            if fz:
                fz["gather"](P, lo, blo, r0_ // 128, 0)
                fz["gather"](P, ly, bly, r0_ // 128, 1)
            else:
                P.dma("sp", lo[:], o_d[r0_:r0_ + 128, :], writes=[blo])
                P.dma("sp", ly[:], ys_d[r0_:r0_ + 128, :], writes=[bly])

        issue_loads(0)
        for (t0, n) in sts:
            ntl = n // 128
            for t in range(ntl):
                r0 = t0 + t * 128
                ti = tile_r0.index(r0)
                if ti + 1 < len(tile_r0):
                    issue_loads(ti + 1)
                P.dma("sp", xt4[:, t, :], x_d[r0:r0 + 128, :], writes=[bxt[t]])
                rms_rstd(C, xt4[:, t, :], bxt[t], 1024, sq[:], bsq, ss, bss)
                P.op("dve", lambda E, t=t: E.scalar_tensor_tensor(out=hn[:], in0=xt4[:, t, :], scalar=ss[:, 0:1], in1=npre[:],
                                                                  op0=ALU.mult, op1=ALU.mult), reads=[bxt[t], bss, bnpre], writes=[bhn])
                transpose8(C, hn, bhn, idb, bidb, ptr, bptr, hT[:, :, t * 128:(t + 1) * 128], bhT, eng="act")
                ld, bld = ldo[ti % 2]
                P.op("act", lambda E, ld=ld: E.activation(out=sq[:], in_=ld[:], func=AF.Square), reads=[bld], writes=[bsq])
                P.op("dve", lambda E: E.tensor_reduce(out=ss8[:], in_=sq[:].rearrange("p (h d) -> p h d", h=8), axis=AX.X, op=ALU.add),
                     reads=[bsq], writes=[bss8])
                P.op("dve", lambda E: E.tensor_scalar(out=ss8[:], in0=ss8[:], scalar1=1.0 / 128, scalar2=1e-6, op0=ALU.mult, op1=ALU.add),
                     reads=[bss8], writes=[bss8])
                P.op("act", lambda E: E.activation(out=ss8[:], in_=ss8[:], func=AF.Sqrt), reads=[bss8], writes=[bss8])
                P.op("dve", lambda E: E.reciprocal(out=ss8[:], in_=ss8[:]), reads=[bss8], writes=[bss8])
                P.op("dve", lambda E, ld=ld: E.tensor_tensor(out=sq[:].rearrange("p (h d) -> p h d", h=8), in0=ld[:].rearrange("p (h d) -> p h d", h=8),
                                                      in1=ss8[:].unsqueeze(2).to_broadcast([128, 8, 128]), op=ALU.mult),
                     reads=[bld, bss8], writes=[bsq])
                P.op("dve", lambda E: E.tensor_tensor(out=hn[:].rearrange("p (h d) -> p h d", h=8), in0=sq[:].rearrange("p (h d) -> p h d", h=8),
                                                      in1=gnw[:].unsqueeze(1).to_broadcast([128, 8, 128]), op=ALU.mult),
                     reads=[bsq, bgnw], writes=[bhn])
                transpose8(C, hn, bhn, idb, bidb, ptr, bptr, oT[:, :, t * 128:(t + 1) * 128], boT, eng="act")
                ld, bld = ldy[ti % 2]
                P.op("act", lambda E, ld=ld: E.activation(out=hn[:], in_=ld[:], func=AF.Gelu_apprx_tanh), reads=[bld], writes=[bhn])
                transpose8(C, hn, bhn, idb, bidb, ptr, bptr, yT[:, :, t * 128:(t + 1) * 128], byT, eng="dve")
            for ct in range(16):
                pb = pmm[ct % 4]; bpb = bpmm[ct % 4]
                fns = [(lambda E, kt=kt, ct=ct, pb=pb, n=n: E.matmul(pb[:, 0:n], lhsT=wz[:, kt, ct * 128:(ct + 1) * 128], rhs=hT[:, kt, 0:n],
                                                                start=(kt == 0), stop=(kt == 7))) for kt in range(8)]
                P.mm_group(fns, reads=[bwz, bhT], writes=[bpb])
                if ct < 8:
                    P.op("act", lambda E, pb=pb, n=n: E.activation(out=sg[:, 0:n], in_=pb[:, 0:n], func=AF.Silu), reads=[bpb], writes=[bsg])
                    P.op("dve", lambda E, ct=ct, n=n: E.tensor_tensor(out=catT[:, ct, 0:n], in0=oT[:, ct, 0:n], in1=sg[:, 0:n], op=ALU.mult),
                         reads=[boT, bsg], writes=[bcat])
                else:
                    P.op("act", lambda E, pb=pb, ct=ct, n=n: E.activation(out=gz[:, ct - 8, 0:n], in_=pb[:, 0:n], func=AF.Silu), reads=[bpb], writes=[bgz])
            for ct in range(8):
                pb = pmm[ct % 4]; bpb = bpmm[ct % 4]
                fns = [(lambda E, kt=kt, ct=ct, pb=pb, n=n: E.matmul(pb[:, 0:n], lhsT=wglu[:, kt, ct * 128:(ct + 1) * 128], rhs=yT[:, kt, 0:n],
                                                                start=(kt == 0), stop=(kt == 7))) for kt in range(8)]
                P.mm_group(fns, reads=[bwglu, byT], writes=[bpb])
                P.op("act", lambda E, pb=pb, n=n: E.activation(out=sg[:, 0:n], in_=pb[:, 0:n], func=AF.Sigmoid), reads=[bpb], writes=[bsg])
                P.op("dve", lambda E, ct=ct, n=n: E.tensor_tensor(out=sg[:, 0:n], in0=sg[:, 0:n], in1=yT[:, ct, 0:n], op=ALU.mult), reads=[bsg, byT], writes=[bsg])
                P.op("dve", lambda E, ct=ct, n=n: E.tensor_tensor(out=catT[:, 8 + ct, 0:n], in0=sg[:, 0:n], in1=gz[:, ct, 0:n], op=ALU.mult),
                     reads=[bsg, bgz], writes=[bcat])
            for t in range(ntl):
                r0 = t0 + t * 128
                outproj_post(C, catT, bcat, 16, wout, bwout, t, xt4[:, t, :], bxt[t], npost, bnpost, pso, bpso, yo, byo, sq, bsq, ss, bss,
                             out_d[r0:r0 + 128, :], bout)
        if fz:
            barrier(P)
        else:
            P.finish([bout])
    return nc


def build_L3(ntok=2048, fz=None):
    nc = fz["nc"] if fz else bass.Bass("TRN2", target_bir_lowering=False)
    pfx = fz["pfx"] if fz else ""

    def D(name, shape):
        if fz and name in fz["share"]:
            return fz["share"][name]
        return nc.dram_tensor(pfx + name, shape, F32, kind="ExternalInput").ap()
    x_d = D("x", [ntok + 128, 1024])
    win_d = D("win", [1024, 8192]); wout_d = D("wout", [2048, 1024]); conv_d = D("conv", [3, 2048])
    npre_d = D("npre", [1024]); npost_d = D("npost", [1024]); ident_d = D("ident", [128, 128])
    out_d = fz["out"] if fz else nc.dram_tensor("out", [ntok, 1024], F32, kind="ExternalOutput").ap()
    NT = 256
    with ExitStack() as st:
        C = Ctx(nc, st, fz["P"], pfx) if fz else Ctx(nc, st); P = C.P
        idf, bidf, idb, bidb = make_ident(C, ident_d)
        npre, bnpre = bcast_row_load(C, "npre", npre_d, 1024)
        npost, bnpost = bcast_row_load(C, "npost", npost_d, 1024)
        cw, bcw = C.sb("cw", [128, 3, 16])
        P.dma("sp", cw[:], conv_d.rearrange("j (c p) -> p j c", p=128), writes=[bcw])
        win, bwin = load_w_bf16(C, "win", win_d, 8, 8192)
        wout, bwout = load_w_bf16(C, "wout", wout_d, 16, 1024)
        xt, bxt = C.sb("xt", [128, 1024])
        sq, bsq = C.sb("sq", [128, 1024])
        hn, bhn = C.sb("hn", [128, 1024], BF16)
        ss, bss = C.sb("ss", [128, 1])
        hT, bhT = C.sb("hT", [128, 8, NT], BF16)
        y1T, by1T = C.sb("y1T", [128, 16, NT], BF16)
        pbuf, bpbuf = C.sb("pbuf", [128, NT + 2])
        phalo, bphalo = C.sb("phalo", [128, 16, 2])
        gcs, bgcs = C.sb("gcs", [128, NT])
        cv, bcv = C.sb("cv", [128, NT])
        sz, bsz = C.sb("sz", [128, NT])
        yo, byo = C.sb("yo", [128, 1024])
        P.op("dve", lambda E: E.memset(phalo[:], 0.0), writes=[bphalo])
        ptr, bptr = C.ps("ptr", [128, 1024], BF16)
        GB = [C.ps("g%d" % i, [128, 512]) for i in range(7)]
        pso = [GB[0][0], GB[1][0]]; bpso = [GB[0][1], GB[1][1]]
        bout = fz["obuf"] if fz else Buf("out", multi=True)
        sts = [(0, 128)] + [(128 + i * NT, NT) for i in range(ntok // NT)]
        for (t0, n) in sts:
            ntl = n // 128
            for t in range(ntl):
                r0 = t0 + t * 128
                P.dma("sp", xt[:], x_d[r0:r0 + 128, :], reads=([fz["xbuf"]] if fz else []), writes=[bxt])
                rms_rstd(C, xt[:], bxt, 1024, sq[:], bsq, ss, bss)
                P.op("dve", lambda E: E.scalar_tensor_tensor(out=hn[:], in0=xt[:], scalar=ss[:, 0:1], in1=npre[:],
                                                             op0=ALU.mult, op1=ALU.mult), reads=[bxt, bss, bnpre], writes=[bhn])
                transpose8(C, hn, bhn, idb, bidb, ptr, bptr, hT[:, :, t * 128:(t + 1) * 128], bhT, eng="act")
            for ct in range(16):
                sel_ = [GB[3 * (ct % 2) + 0], GB[3 * (ct % 2) + 1], GB[3 * (ct % 2) + 2], GB[6]]
                pmm = [x_[0] for x_ in sel_]; bpmm = [x_[1] for x_ in sel_]
                for part in range(4):
                    col0 = (part * 16 + ct) * 128
                    pb = pmm[part]
                    fns = [(lambda E, n=n, kt=kt, col0=col0, pb=pb: E.matmul(pb[:, 0:n], lhsT=win[:, kt, col0:col0 + 128], rhs=hT[:, kt, 0:n],
                                                                        start=(kt == 0), stop=(kt == 7))) for kt in range(8)]
                    P.mm_group(fns, reads=[bwin, bhT], writes=[bpmm[part]])
                P.op("act", lambda E, n=n, pmm=pmm: E.copy(out=gcs[:, 0:n], in_=pmm[1][:, 0:n]), reads=[bpmm[1]], writes=[bgcs])
                P.op("act", lambda E, ct=ct: E.copy(out=pbuf[:, 0:2], in_=phalo[:, ct, :]), reads=[bphalo], writes=[bpbuf])
                P.op("dve", lambda E, n=n, pmm=pmm: E.tensor_tensor(out=pbuf[:, 2:2 + n], in0=gcs[:, 0:n], in1=pmm[2][:, 0:n], op=ALU.mult),
                     reads=[bgcs, bpmm[2]], writes=[bpbuf])
                P.op("act", lambda E, n=n, ct=ct: E.copy(out=phalo[:, ct, :], in_=pbuf[:, n:n + 2]), reads=[bpbuf], writes=[bphalo])
                if t0 == 0:
                    continue
                P.op("dve", lambda E, n=n, ct=ct: E.tensor_scalar(out=cv[:, 0:n], in0=pbuf[:, 0:n], scalar1=cw[:, 0, ct:ct + 1], scalar2=None, op0=ALU.mult),
                     reads=[bpbuf, bcw], writes=[bcv])
                P.op("dve", lambda E, n=n, ct=ct: E.scalar_tensor_tensor(out=cv[:, 0:n], in0=pbuf[:, 1:1 + n], scalar=cw[:, 1, ct:ct + 1], in1=cv[:, 0:n],
                                                                    op0=ALU.mult, op1=ALU.add), reads=[bpbuf, bcw, bcv], writes=[bcv])
                P.op("dve", lambda E, n=n, ct=ct: E.scalar_tensor_tensor(out=cv[:, 0:n], in0=pbuf[:, 2:2 + n], scalar=cw[:, 2, ct:ct + 1], in1=cv[:, 0:n],
                                                                    op0=ALU.mult, op1=ALU.add), reads=[bpbuf, bcw, bcv], writes=[bcv])
                P.op("dve", lambda E, n=n, pmm=pmm: E.tensor_tensor(out=cv[:, 0:n], in0=cv[:, 0:n], in1=pmm[0][:, 0:n], op=ALU.mult), reads=[bcv, bpmm[0]], writes=[bcv])
                P.op("act", lambda E, n=n, pmm=pmm: E.activation(out=sz[:, 0:n], in_=pmm[3][:, 0:n], func=AF.Silu), reads=[bpmm[3]], writes=[bsz])
                P.op("dve", lambda E, n=n, ct=ct: E.tensor_tensor(out=y1T[:, ct, 0:n], in0=cv[:, 0:n], in1=sz[:, 0:n], op=ALU.mult),
                     reads=[bcv, bsz], writes=[by1T])
            if t0 == 0:
                continue
            for t in range(ntl):
                r0 = t0 + t * 128
                P.dma("sp", xt[:], x_d[r0:r0 + 128, :], reads=([fz["xbuf"]] if fz else []), writes=[bxt])
                outproj_post(C, y1T, by1T, 16, wout, bwout, t, xt[:], bxt, npost, bnpost, pso, bpso, yo, byo, sq, bsq, ss, bss,
                             out_d[r0 - 128:r0, :], bout)
        if fz:
            barrier(P)
        else:
            P.finish([bout])
    return nc


_IDENT = np.eye(128, dtype=np.float32)
_CACHE = {}


def _get(name, fn):
    if name not in _CACHE:
        _CACHE[name] = fn()
    return _CACHE[name]


def run_L2(inp, o_full, ys_full):
    nc = _get("L2", build_L2)
    w_in = inp["w_in_even"][0]
    wz = np.ascontiguousarray(np.concatenate([w_in[:, 3072:4096], w_in[:, 5136:6160]], axis=1))
    maps = []
    for c in range(8):
        b, r = divmod(c, 4)
        sl = slice(r * 2048, (r + 1) * 2048)
        maps.append({"x": np.ascontiguousarray(inp["x"][b, sl]), "o": np.ascontiguousarray(o_full[b, sl]),
                     "ys": np.ascontiguousarray(ys_full[b, sl]), "wz": wz, "wglu": np.ascontiguousarray(inp["w_glu"][0]),
                     "wout": np.ascontiguousarray(inp["w_out_even"][0]), "npre": np.ascontiguousarray(inp["norm_pre"][0]),
                     "npost": np.ascontiguousarray(inp["norm_post"][0]), "gnw": np.ascontiguousarray(inp["gdn_norm_w"][0]),
                     "ident": _IDENT})
    res = run_bass_kernel_spmd(nc, maps, core_ids=list(range(8)))
    x1 = np.empty((2, 8192, 1024), np.float32)
    for c in range(8):
        b, r = divmod(c, 4)
        x1[b, r * 2048:(r + 1) * 2048] = res.results[c]["out"]
    return x1


def run_L3(inp, x1):
    nc = _get("L3", build_L3)
    maps = []
    for c in range(8):
        b, r = divmod(c, 4)
        xh = np.zeros((2048 + 128, 1024), np.float32)
        xh[128:] = x1[b, r * 2048:(r + 1) * 2048]
        if r > 0:
            xh[:128] = x1[b, r * 2048 - 128:r * 2048]
        maps.append({"x": xh, "win": np.ascontiguousarray(inp["w_in_odd"][0]), "wout": np.ascontiguousarray(inp["w_out_odd"][0]),
                     "conv": np.ascontiguousarray(inp["conv_short"][0]), "npre": np.ascontiguousarray(inp["norm_pre"][1]),
                     "npost": np.ascontiguousarray(inp["norm_post"][1]), "ident": _IDENT})
    res = run_bass_kernel_spmd(nc, maps, core_ids=list(range(8)))
    out = np.empty((2, 8192, 1024), np.float32)
    for c in range(8):
        b, r = divmod(c, 4)
        out[b, r * 2048:(r + 1) * 2048] = res.results[c]["out"]
    return out


I32 = mybir.dt.int32
TAUS = np.array(list(range(17)) + [32, 64, 128, 256, 512, 1024, 2048, 4096] + list(range(15, -1, -1)), np.float32)
NTAU = len(TAUS)


def _s5_consts():
    mk = np.zeros((128, 2, 16, 16), np.float32)
    idm = np.zeros((128, 2, 16, 16), np.float32)
    for kt2 in range(2):
        for sp in range(8):
            s = kt2 * 8 + sp
            for h in range(16):
                mk[sp * 16 + h, kt2, s:, :] = 1.0
                idm[sp * 16 + h, kt2, s, h] = 1.0
    return mk.reshape(128, 2, 256), idm.reshape(128, 2, 256)


def barrier(P):
    for e in P.ENG:
        for e2 in P.ENG:
            if P.cnt[e2] > 0:
                P._wait(e, ("c", e2, P.cnt[e2]))
        for q in P.dsem:
            j1 = P.dcnt[q]
            for j in range(max(0, j1 - NDS), j1):
                P._wait(e, ("d", q, j % NDS, 16 * (j // NDS + 1)))


def build_L1b(S=8192, fz=None):
    nc = fz["nc"] if fz else bass.Bass("TRN2", target_bir_lowering=False)
    pfx = fz["pfx"] if fz else ""

    def D(name, shape):
        if fz and name in fz["share"]:
            return fz["share"][name]
        return nc.dram_tensor(pfx + name, shape, F32, kind="ExternalInput").ap()
    x_d = D("x", [S, 1024]); npre_d = D("npre", [1024]); wu_d = D("wu", [1024, 256])
    lre_d = D("lre", [16, 64]); lim_d = D("lim", [16, 64]); bre_d = D("bre", [16, 64, 16]); bim_d = D("bim", [16, 64, 16])
    cre_d = D("cre", [16, 16, 64]); cim_d = D("cim", [16, 16, 64]); ldt_d = D("ldt", [16]); dd_d = D("dd", [256])
    taus_d = D("taus", [NTAU]); mk_d = D("mk", [128, 2, 256]); idm_d = D("idm", [128, 2, 256]); ident_d = D("ident", [128, 128])
    ys_d = fz["out"] if fz else nc.dram_tensor("ys", [S, 256], F32, kind="ExternalOutput").ap()
    NCH = S // 16
    NST = S // 512
    with ExitStack() as st:
        C = Ctx(nc, st, fz["P"], pfx) if fz else Ctx(nc, st); P = C.P
        idf, bidf, idb, bidb = make_ident(C, ident_d)
        ptr, bptr = C.ps("ptr", [128, 1024], BF16)
        py, bpy = C.ps("py", [128, 1024])
        G = []; bG = []
        for i in range(4):
            t_, b_ = C.ps("g%d" % i, [128, 512]); G.append(t_); bG.append(b_)
        U, bU = C.sb("U", [128, 2, 16, NCH], BF16)
        with ExitStack() as st2:
            C2 = Ctx(nc, st2, P, C.pfx)
            ext = fz.get("uTp") if fz else None
            if ext:
                uTp, buTp = ext
            else:
                uTp, buTp = C2.sb("uTp", [128, 2, 16, NCH], BF16)
            with ExitStack() as st1:
                C1 = Ctx(nc, st1, P, C.pfx)
                npre, bnpre = bcast_row_load(C1, "npre", npre_d, 1024)
                wu, bwu = load_w_bf16(C1, "wu", wu_d, 8, 256)
                xt, bxt = C1.sb("xt", [128, 1024])
                sq, bsq = C1.sb("sq", [128, 1024])
                hn, bhn = C1.sb("hn", [128, 1024], BF16)
                ss, bss = C1.sb("ss", [128, 1])
                hT, bhT = C1.sb("hT", [128, 8, 512], BF16)
                for s_ in range(0 if ext else NST):
                    for t in range(4):
                        r0 = s_ * 512 + t * 128
                        P.dma("sp", xt[:], x_d[r0:r0 + 128, :], writes=[bxt])
                        rms_rstd(C1, xt[:], bxt, 1024, sq[:], bsq, ss, bss)
                        P.op("dve", lambda E: E.scalar_tensor_tensor(out=hn[:], in0=xt[:], scalar=ss[:, 0:1], in1=npre[:],
                                                                     op0=ALU.mult, op1=ALU.mult), reads=[bxt, bss, bnpre], writes=[bhn])
                        transpose8(C1, hn, bhn, idb, bidb, ptr, bptr, hT[:, :, t * 128:(t + 1) * 128], bhT, eng="act")
                    for blk in range(2):
                        pb = G[blk]
                        fns = [(lambda E, kt=kt, blk=blk, pb=pb: E.matmul(
                            pb[:].rearrange("p (s n) -> p s n", s=16), lhsT=wu[:, kt, blk * 128:(blk + 1) * 128],
                            rhs=hT[:, kt, :].rearrange("p (n s) -> p s n", s=16), start=(kt == 0), stop=(kt == 7))) for kt in range(8)]
                        P.mm_group(fns, reads=[bwu, bhT], writes=[bG[blk]])
                        P.op("act" if blk == 0 else "dve",
                             (lambda E, blk=blk, pb=pb, s_=s_: E.copy(out=uTp[:, blk, :, 32 * s_:32 * s_ + 32], in_=pb[:].rearrange("p (s n) -> p s n", s=16)))
                             if blk == 0 else
                             (lambda E, blk=blk, pb=pb, s_=s_: E.tensor_copy(out=uTp[:, blk, :, 32 * s_:32 * s_ + 32], in_=pb[:].rearrange("p (s n) -> p s n", s=16))),
                             reads=[bG[blk]], writes=[buTp])
                barrier(P)
            ud2 = nc.dram_tensor(pfx + "ud2", [16, 2, 8, 16, NCH], BF16)
            bud2 = Buf("ud2", multi=True)
            bU.multi = True; bU.w = []
            for g in range(16):
                P.dma("sp", ud2.ap()[g].rearrange("k sp h n -> h (k sp) n"),
                      uTp[(g % 8) * 16:(g % 8 + 1) * 16, g // 8, :, :], reads=[buTp], writes=[bud2])
            for g in range(16):
                P.dma("sp", U[:, :, g, :], ud2.ap()[g].rearrange("k sp h n -> (sp h) k n"), reads=[bud2], writes=[bU])
            barrier(P)
        lre, blre = C.sb("lre", [128, 8]); lim, blim = C.sb("lim", [128, 8]); ldt, bldt = C.sb("ldt", [128, 8])
        TAU, bTAU = bcast_row_load(C, "TAU", taus_d, NTAU)
        Er, bEr = C.sb("Er", [128, 8, NTAU]); Ei, bEi = C.sb("Ei", [128, 8, NTAU]); NEi, bNEi = C.sb("NEi", [128, 8, NTAU])
        Hr, bHr = C.sb("Hr", [128, 8, 17, 16]); nHi, bnHi = C.sb("nHi", [128, 8, 17, 16])
        WbT, bWbT = C.sb("WbT", [128, 2, 8, 2, 128], BF16)
        Toep, bToep = C.sb("Toep", [128, 2, 16, 256], BF16)
        with ExitStack() as st3:
            C3 = Ctx(nc, st3, P, C.pfx)
            Br, bBr = C3.sb("Br", [128, 8, 16]); Bi, bBi = C3.sb("Bi", [128, 8, 16])
            Cr, bCr = C3.sb("Cr", [128, 8, 16]); Ci, bCi = C3.sb("Ci", [128, 8, 16])
            dcol, bdcol = C3.sb("dcol", [128, 16])
            MK, bMK = C3.sb("MK", [128, 2, 256]); IDM, bIDM = C3.sb("IDM", [128, 2, 256])
            P.dma("sp", MK[:], mk_d, writes=[bMK]); P.dma("sp", IDM[:], idm_d, writes=[bIDM])
            for two in range(2):
                hs = slice(64 * two, 64 * two + 64)
                P.dma("sp", lre[hs, :], lre_d.rearrange("(gp two) p -> two p gp", two=2)[two], writes=[blre])
                P.dma("sp", lim[hs, :], lim_d.rearrange("(gp two) p -> two p gp", two=2)[two], writes=[blim])
                P.dma("sp", ldt[hs, :], ldt_d.rearrange("(gp two) -> two gp", two=2)[two].partition_broadcast(64), writes=[bldt])
                P.dma("sp", Br[hs], bre_d.rearrange("(gp two) p h -> two p gp h", two=2)[two], writes=[bBr])
                P.dma("sp", Bi[hs], bim_d.rearrange("(gp two) p h -> two p gp h", two=2)[two], writes=[bBi])
                for gp in range(8):
                    P.dma("sp", Cr[hs, gp, :], cre_d[2 * gp + two].rearrange("h p -> p h"), writes=[bCr])
                    P.dma("sp", Ci[hs, gp, :], cim_d[2 * gp + two].rearrange("h p -> p h"), writes=[bCi])
            for sp in range(8):
                P.dma("sp", dcol[sp * 16:(sp + 1) * 16, :], dd_d.rearrange("(g h) -> h g", h=16), writes=[bdcol])
            sm = {}
            for nm in ("dt", "lr", "lrdt", "th", "den", "nr", "fre", "fim", "t8a", "t8b"):
                sm[nm] = C3.sb("sm_" + nm, [128, 8])
            T41 = {}
            for nm in ("ARG", "MARG", "MAG", "MAGN", "SIN", "COS", "ErN", "EiN", "rt", "rk"):
                T41[nm] = C3.sb("t41_" + nm, [128, 8, NTAU])
            rki, brki = C3.sb("rki", [128, 8, NTAU], I32)

            def tt(eng, out, bo, a, ba, b, bb_, op):
                P.op(eng, lambda E: E.tensor_tensor(out=out, in0=a, in1=b, op=op), reads=[ba, bb_], writes=[bo])

            dt, bdt = sm["dt"]; lr, blr = sm["lr"]; lrdt, blrdt = sm["lrdt"]; th, bth = sm["th"]
            P.op("act", lambda E: E.activation(out=dt[:], in_=ldt[:], func=AF.Exp), reads=[bldt], writes=[bdt])
            P.op("dve", lambda E: E.tensor_scalar(out=lr[:], in0=lre[:], scalar1=-1e-4, scalar2=None, op0=ALU.min), reads=[blre], writes=[blr])
            tt("dve", lrdt[:], blrdt, lr[:], blr, dt[:], bdt, ALU.mult)
            tt("dve", th[:], bth, lim[:], blim, dt[:], bdt, ALU.mult)
            ARG, bARG = T41["ARG"]; MARG, bMARG = T41["MARG"]; MAG, bMAG = T41["MAG"]; MAGN, bMAGN = T41["MAGN"]
            SIN, bSIN = T41["SIN"]; COS, bCOS = T41["COS"]; ErN, bErN = T41["ErN"]; EiN, bEiN = T41["EiN"]
            rt, brt = T41["rt"]; rk, brk = T41["rk"]
            tb = TAU[:].unsqueeze(1).to_broadcast([128, 8, NTAU])
            tt("dve", ARG[:], bARG, th[:].unsqueeze(2).to_broadcast([128, 8, NTAU]), bth, tb, bTAU, ALU.mult)
            tt("dve", MARG[:], bMARG, lrdt[:].unsqueeze(2).to_broadcast([128, 8, NTAU]), blrdt, tb, bTAU, ALU.mult)
            P.op("act", lambda E: E.activation(out=MAG[:], in_=MARG[:], func=AF.Exp), reads=[bMARG], writes=[bMAG])
            P.op("act", lambda E: E.activation(out=MAGN[:, :, 0:17], in_=MARG[:, :, 0:17], func=AF.Exp, scale=-1.0), reads=[bMARG], writes=[bMAGN])

            def sin_of(dst, bdst, shift):
                P.op("dve", lambda E: E.tensor_scalar(out=rt[:], in0=ARG[:], scalar1=float(shift), scalar2=None, op0=ALU.add), reads=[bARG], writes=[brt])
                P.op("dve", lambda E: E.tensor_scalar(out=rki[:], in0=rt[:], scalar1=float(1.0 / (2 * np.pi)), scalar2=None, op0=ALU.mult), reads=[brt], writes=[brki])
                P.op("dve", lambda E: E.tensor_copy(out=rk[:], in_=rki[:]), reads=[brki], writes=[brk])
                P.op("dve", lambda E: E.scalar_tensor_tensor(out=rt[:], in0=rk[:], scalar=float(-2 * np.pi), in1=rt[:], op0=ALU.mult, op1=ALU.add),
                     reads=[brk, brt], writes=[brt])
                P.op("dve", lambda E: E.tensor_scalar(out=rt[:], in0=rt[:], scalar1=-3.14159, scalar2=3.14159, op0=ALU.max, op1=ALU.min), reads=[brt], writes=[brt])
                P.op("act", lambda E: E.activation(out=dst[:], in_=rt[:], func=AF.Sin), reads=[brt], writes=[bdst])

            sin_of(SIN, bSIN, 0.0)
            sin_of(COS, bCOS, np.pi / 2)
            tt("dve", Er[:], bEr, MAG[:], bMAG, COS[:], bCOS, ALU.mult)
            tt("dve", Ei[:], bEi, MAG[:], bMAG, SIN[:], bSIN, ALU.mult)
            P.op("dve", lambda E: E.tensor_scalar(out=NEi[:], in0=Ei[:], scalar1=-1.0, scalar2=None, op0=ALU.mult), reads=[bEi], writes=[bNEi])
            tt("dve", ErN[:, :, 0:17], bErN, MAGN[:, :, 0:17], bMAGN, COS[:, :, 0:17], bCOS, ALU.mult)
            tt("dve", EiN[:, :, 0:17], bEiN, MAGN[:, :, 0:17], bMAGN, SIN[:, :, 0:17], bSIN, ALU.mult)
            P.op("dve", lambda E: E.tensor_scalar(out=EiN[:, :, 0:17], in0=EiN[:, :, 0:17], scalar1=-1.0, scalar2=None, op0=ALU.mult), reads=[bEiN], writes=[bEiN])
            den, bden = sm["den"]; nr, bnr = sm["nr"]; fre, bfre = sm["fre"]; fim, bfim = sm["fim"]; t8a, bt8a = sm["t8a"]; t8b, bt8b = sm["t8b"]
            tt("dve", den[:], bden, lr[:], blr, lr[:], blr, ALU.mult)
            tt("dve", t8a[:], bt8a, lim[:], blim, lim[:], blim, ALU.mult)
            tt("dve", den[:], bden, den[:], bden, t8a[:], bt8a, ALU.add)
            P.op("dve", lambda E: E.reciprocal(out=den[:], in_=den[:]), reads=[bden], writes=[bden])
            P.op("dve", lambda E: E.tensor_scalar(out=nr[:], in0=Er[:, :, 1], scalar1=-1.0, scalar2=None, op0=ALU.add), reads=[bEr], writes=[bnr])
            tt("dve", fre[:], bfre, nr[:], bnr, lr[:], blr, ALU.mult)
            tt("dve", t8a[:], bt8a, Ei[:, :, 1], bEi, lim[:], blim, ALU.mult)
            tt("dve", fre[:], bfre, fre[:], bfre, t8a[:], bt8a, ALU.add)
            tt("dve", fre[:], bfre, fre[:], bfre, den[:], bden, ALU.mult)
            tt("dve", fim[:], bfim, Ei[:, :, 1], bEi, lr[:], blr, ALU.mult)
            tt("dve", t8b[:], bt8b, nr[:], bnr, lim[:], blim, ALU.mult)
            tt("dve", fim[:], bfim, fim[:], bfim, t8b[:], bt8b, ALU.subtract)
            tt("dve", fim[:], bfim, fim[:], bfim, den[:], bden, ALU.mult)

            def cmul(outr, boutr, outi, bouti, ar, bar, ai, bai, br_, bbr_, bi_, bbi_, tmp, btmp):
                tt("dve", outr, boutr, ar, bar, br_, bbr_, ALU.mult)
                tt("dve", tmp, btmp, ai, bai, bi_, bbi_, ALU.mult)
                tt("dve", outr, boutr, outr, boutr, tmp, btmp, ALU.subtract)
                tt("dve", outi, bouti, ar, bar, bi_, bbi_, ALU.mult)
                tt("dve", tmp, btmp, ai, bai, br_, bbr_, ALU.mult)
                tt("dve", outi, bouti, outi, bouti, tmp, btmp, ALU.add)

            bbr, bbbr = C3.sb("bbr", [128, 8, 16]); bbi, bbbi = C3.sb("bbi", [128, 8, 16]); tmp16, btmp16 = C3.sb("tmp16", [128, 8, 16])
            fb = lambda t_: t_[:].unsqueeze(2).to_broadcast([128, 8, 16])
            cmul(bbr[:], bbbr, bbi[:], bbbi, fb(fre), bfre, fb(fim), bfim, Br[:], bBr, Bi[:], bBi, tmp16[:], btmp16)
            Gr, bGr = C3.sb("Gr", [128, 8, 16, 16]); Gi, bGi = C3.sb("Gi", [128, 8, 16, 16])
            WPr, bWPr = C3.sb("WPr", [128, 8, 16, 16]); WPi, bWPi = C3.sb("WPi", [128, 8, 16, 16])
            Hi, bHi = C3.sb("Hi", [128, 8, 17, 16]); tmpH, btmpH = C3.sb("tmpH", [128, 8, 17, 16])
            eb = lambda t_, j0, j1: t_[:, :, j0:j1].unsqueeze(3).to_broadcast([128, 8, j1 - j0, 16])
            vb = lambda t_, n_: t_[:].unsqueeze(2).to_broadcast([128, 8, n_, 16])
            cmul(Gr[:], bGr, Gi[:], bGi, eb(ErN, 0, 16), bErN, eb(EiN, 0, 16), bEiN, vb(bbr, 16), bbbr, vb(bbi, 16), bbbi, tmpH[:, :, 0:16, :], btmpH)
            cmul(WPr[:], bWPr, WPi[:], bWPi, eb(Er, 25, 41), bEr, eb(Ei, 25, 41), bEi, vb(bbr, 16), bbbr, vb(bbi, 16), bbbi, tmpH[:, :, 0:16, :], btmpH)
            cmul(Hr[:], bHr, Hi[:], bHi, eb(Er, 0, 17), bEr, eb(Ei, 0, 17), bEi, vb(Cr, 17), bCr, vb(Ci, 17), bCi, tmpH[:], btmpH)
            P.op("dve", lambda E: E.tensor_scalar(out=nHi[:], in0=Hi[:], scalar1=-1.0, scalar2=None, op0=ALU.mult), reads=[bHi], writes=[bnHi])
            for gp in range(8):
                for kt2 in range(2):
                    for c, (WP_, bWP_) in enumerate(((WPr, bWPr), (WPi, bWPi))):
                        P.op("pe", lambda E, gp=gp, kt2=kt2, WP_=WP_: E.transpose(
                            out=G[2][:, 0:128], in_=WP_[:, gp, kt2 * 8:(kt2 + 1) * 8, :].rearrange("p s h -> p (s h)"), identity=idf[:]),
                            reads=[bWP_, bidf], writes=[bG[2]])
                        P.op("act", lambda E, gp=gp, kt2=kt2, c=c: E.copy(out=WbT[:, kt2, gp, c, :], in_=G[2][:, 0:128]), reads=[bG[2]], writes=[bWbT])
            tmpT, btmpT = C3.sb("tmpT", [128, 256])
            for g in range(16):
                gp = g // 2; hs = slice(64 * (g % 2), 64 * (g % 2) + 64)
                for kt2 in range(2):
                    fns = [
                        lambda E, gp=gp, hs=hs, kt2=kt2: E.matmul(G[3][:, 0:256], lhsT=Gr[hs, gp, kt2 * 8:(kt2 + 1) * 8, :].rearrange("p s h -> p (s h)"),
                                                                  rhs=Hr[hs, gp, 0:16, :].rearrange("p t h -> p (t h)"), start=True, stop=False),
                        lambda E, gp=gp, hs=hs, kt2=kt2: E.matmul(G[3][:, 0:256], lhsT=Gi[hs, gp, kt2 * 8:(kt2 + 1) * 8, :].rearrange("p s h -> p (s h)"),
                                                                  rhs=nHi[hs, gp, 0:16, :].rearrange("p t h -> p (t h)"), start=False, stop=True)]
                    P.mm_group(fns, reads=[bGr, bGi, bHr, bnHi], writes=[bG[3]])
                    P.op("dve", lambda E, kt2=kt2: E.tensor_tensor(out=tmpT[:], in0=G[3][:, 0:256], in1=MK[:, kt2, :], op=ALU.mult),
                         reads=[bG[3], bMK], writes=[btmpT])
                    P.op("dve", lambda E, kt2=kt2, g=g: E.scalar_tensor_tensor(out=Toep[:, kt2, g, :], in0=IDM[:, kt2, :], scalar=dcol[:, g:g + 1], in1=tmpT[:],
                                                                               op0=ALU.mult, op1=ALU.add), reads=[bIDM, bdcol, btmpT], writes=[bToep])
            barrier(P)
        X = {}
        for bufn in ("A", "B"):
            for c in ("re", "im"):
                X[(bufn, c)] = (C.sb("X%s%s" % (bufn, c), [128, 8, NCH + 1])[0], [Buf("X%s%s%d" % (bufn, c, gp)) for gp in range(8)])
        Ysb, bYsb = C.sb("Ysb", [128, 16, 256])
        for key in X:
            t_, bl = X[key]
            P.op("dve", lambda E, t_=t_: E.memset(t_[:, :, 0:1], 0.0), writes=bl)
        for gp in range(8):
            for c, cn in enumerate(("re", "im")):
                px = G[c]
                fns = []
                for two in range(2):
                    g = 2 * gp + two
                    for kt2 in range(2):
                        fns.append(lambda E, two=two, g=g, kt2=kt2, gp=gp, c=c, px=px: E.matmul(
                            px[64 * two:64 * two + 64, :], lhsT=WbT[:, kt2, gp, c, 64 * two:64 * two + 64], rhs=U[:, kt2, g, :],
                            start=(kt2 == 0), stop=(kt2 == 1)))
                P.mm_group(fns, reads=[bWbT, bU], writes=[bG[c]])
                xt_, xb_ = X[("A", cn)]
                P.op("act", lambda E, xt_=xt_, gp=gp, px=px: E.copy(out=xt_[:, gp, 1:NCH + 1], in_=px[:]), reads=[bG[c]], writes=[xb_[gp]])
        for k in range(9):
            d = 1 << k
            j = 16 if k == 0 else 16 + k
            src, dst = ("A", "B") if k % 2 == 0 else ("B", "A")
            sre, bsre = X[(src, "re")]; sim, bsim = X[(src, "im")]
            dre, bdre = X[(dst, "re")]; dim_, bdim = X[(dst, "im")]
            P.op("dve", lambda E, dre=dre, sre=sre, d=d: E.tensor_copy(out=dre[:, :, 1:1 + d], in_=sre[:, :, 1:1 + d]), reads=bsre, writes=bdre)
            P.op("pool", lambda E, dim_=dim_, sim=sim, d=d: E.tensor_copy(out=dim_[:, :, 1:1 + d], in_=sim[:, :, 1:1 + d]), reads=bsim, writes=bdim)
            for gp in range(8):
                lo = slice(1, NCH + 1 - d); hi = slice(1 + d, NCH + 1)
                P.op("dve", lambda E, gp=gp, j=j, dre=dre, sre=sre, lo=lo, hi=hi: E.scalar_tensor_tensor(
                    out=dre[:, gp, hi], in0=sre[:, gp, lo], scalar=Er[:, gp, j:j + 1], in1=sre[:, gp, hi], op0=ALU.mult, op1=ALU.add),
                    reads=[bsre[gp], bEr], writes=[bdre[gp]])
                P.op("dve", lambda E, gp=gp, j=j, dre=dre, sim=sim, lo=lo, hi=hi: E.scalar_tensor_tensor(
                    out=dre[:, gp, hi], in0=sim[:, gp, lo], scalar=NEi[:, gp, j:j + 1], in1=dre[:, gp, hi], op0=ALU.mult, op1=ALU.add),
                    reads=[bsim[gp], bNEi, bdre[gp]], writes=[bdre[gp]])
                P.op("dve", lambda E, gp=gp, j=j, dim_=dim_, sim=sim, lo=lo, hi=hi: E.scalar_tensor_tensor(
                    out=dim_[:, gp, hi], in0=sim[:, gp, lo], scalar=Er[:, gp, j:j + 1], in1=sim[:, gp, hi], op0=ALU.mult, op1=ALU.add),
                    reads=[bsim[gp], bEr], writes=[bdim[gp]])
                P.op("dve", lambda E, gp=gp, j=j, dim_=dim_, sre=sre, lo=lo, hi=hi: E.scalar_tensor_tensor(
                    out=dim_[:, gp, hi], in0=sre[:, gp, lo], scalar=Ei[:, gp, j:j + 1], in1=dim_[:, gp, hi], op0=ALU.mult, op1=ALU.add),
                    reads=[bsre[gp], bEi, bdim[gp]], writes=[bdim[gp]])
        fre_, bfre_ = X[("B", "re")]; fim_, bfim_ = X[("B", "im")]
        bys = None if fz else Buf("ys", multi=True)
        ysv = ys_d.rearrange("(n t) c -> n t c", t=16)
        for jt in range(NCH // 128):
            for gq in range(4):
                fns = []
                for gi in range(4):
                    g = 4 * gq + gi; gp = g // 2; hs = slice(64 * (g % 2), 64 * (g % 2) + 64)
                    o_ = (gi * 256, (gi + 1) * 256)
                    for kt2 in range(2):
                        fns.append(lambda E, o_=o_, g=g, kt2=kt2, jt=jt: E.matmul(
                            py[:, o_[0]:o_[1]], lhsT=U[:, kt2, g, jt * 128:(jt + 1) * 128], rhs=Toep[:, kt2, g, :], start=(kt2 == 0), stop=False))
                    fns.append(lambda E, o_=o_, gp=gp, hs=hs, jt=jt: E.matmul(
                        py[:, o_[0]:o_[1]], lhsT=fre_[hs, gp, jt * 128:(jt + 1) * 128], rhs=Hr[hs, gp, 1:17, :].rearrange("p t h -> p (t h)"),
                        start=False, stop=False))
                    fns.append(lambda E, o_=o_, gp=gp, hs=hs, jt=jt: E.matmul(
                        py[:, o_[0]:o_[1]], lhsT=fim_[hs, gp, jt * 128:(jt + 1) * 128], rhs=nHi[hs, gp, 1:17, :].rearrange("p t h -> p (t h)"),
                        start=False, stop=True))
                P.mm_group(fns, reads=[bU, bToep, bHr, bnHi] + bfre_ + bfim_, writes=[bpy])
                P.op("act" if gq % 2 == 0 else "dve",
                     (lambda E, gq=gq: E.copy(out=Ysb[:].rearrange("p t (g h) -> p g t h", h=16)[:, 4 * gq:4 * gq + 4],
                                              in_=py[:].rearrange("p (g t h) -> p g t h", g=4, h=16)))
                     if gq % 2 == 0 else
                     (lambda E, gq=gq: E.tensor_copy(out=Ysb[:].rearrange("p t (g h) -> p g t h", h=16)[:, 4 * gq:4 * gq + 4],
                                                     in_=py[:].rearrange("p (g t h) -> p g t h", g=4, h=16))),
                     reads=[bpy], writes=[bYsb])
            P.dma("sp", ysv[jt * 128:(jt + 1) * 128, :, :], Ysb[:], reads=[bYsb], writes=[fz["obuf_of"](jt) if fz else bys])
            if fz:
                fz["after_chunk"](jt)
        if fz:
            barrier(P)
        else:
            P.finish([bys])
    return nc


def run_L1b(inp):
    nc = _get("L1b", build_L1b)
    mk, idm = _s5_consts()
    w_in = inp["w_in_even"][0]
    maps = []
    for c in range(8):
        b, r = divmod(c, 4)
        gs = slice(16 * r, 16 * r + 16)
        maps.append({"x": np.ascontiguousarray(inp["x"][b]), "npre": np.ascontiguousarray(inp["norm_pre"][0]),
                     "wu": np.ascontiguousarray(w_in[:, 4112 + 256 * r:4112 + 256 * (r + 1)]),
                     "lre": np.ascontiguousarray(inp["s5_lam_re"][0, gs]), "lim": np.ascontiguousarray(inp["s5_lam_im"][0, gs]),
                     "bre": np.ascontiguousarray(inp["s5_b_re"][0, gs]), "bim": np.ascontiguousarray(inp["s5_b_im"][0, gs]),
                     "cre": np.ascontiguousarray(inp["s5_c_re"][0, gs]), "cim": np.ascontiguousarray(inp["s5_c_im"][0, gs]),
                     "ldt": np.ascontiguousarray(inp["s5_log_dt"][0, gs]), "dd": np.ascontiguousarray(inp["s5_d"][0, 256 * r:256 * (r + 1)]),
                     "taus": TAUS, "mk": mk, "idm": idm, "ident": _IDENT})
    res = run_bass_kernel_spmd(nc, maps, core_ids=list(range(8)))
    ys = np.empty((2, 8192, 1024), np.float32)
    for c in range(8):
        b, r = divmod(c, 4)
        ys[b, :, 256 * r:256 * (r + 1)] = res.results[c]["ys"]
    return ys


def _gdn_consts():
    p = np.arange(64)[:, None]; f = np.arange(64)[None, :]
    negu = np.where(f >= p, 0.0, -30000.0)
    negls = np.where(f < p, 0.0, -30000.0)
    nsu = np.where(f > p, -1.0, 0.0)
    i64 = np.eye(64)
    c64 = np.stack([negu, negls, nsu, i64], axis=1).astype(np.float32)
    cmask = np.ones((2, 512), np.float32); cmask[:, 0::64] = 0.0
    sel = np.zeros((2, 2, 128), np.float32); sel[0, 0, :] = 1.0; sel[1, 1, :] = 1.0
    return c64, cmask, sel


def build_L1a(S=8192, fz=None):
    nc = fz["nc"] if fz else bass.Bass("TRN2", target_bir_lowering=False)
    pfx = fz["pfx"] if fz else ""

    def D(name, shape):
        if fz and name in fz["share"]:
            return fz["share"][name]
        return nc.dram_tensor(pfx + name, shape, F32, kind="ExternalInput").ap()
    x_d = D("x", [S, 1024]); npre_d = D("npre", [1024]); w_d = D("w", [1024, 768]); wb_d = D("wb", [1024, 2]); wa_d = D("wa", [1024, 2])
    conv_d = D("conv", [4, 768]); alog_d = D("alog", [2]); dtb_d = D("dtb", [2])
    ident_d = D("ident", [128, 128]); c64_d = D("c64", [64, 4, 64]); cmask_d = D("cmask", [2, 512]); sel_d = D("sel", [2, 2, 128])
    ones_d = D("ones", [128, 128])
    o_d = fz["out"] if fz else nc.dram_tensor("o", [S, 256], F32, kind="ExternalOutput").ap()
    NST = S // 512
    with ExitStack() as st:
        C = Ctx(nc, st, fz["P"], pfx) if fz else Ctx(nc, st); P = C.P
        idf, bidf, idb, bidb = make_ident(C, ident_d)
        npre, bnpre = bcast_row_load(C, "npre", npre_d, 1024)
        w, bw = load_w_bf16(C, "w", w_d, 8, 768)
        wb, bwb = load_w_bf16(C, "wb", wb_d, 8, 2)
        wa, bwa = load_w_bf16(C, "wa", wa_d, 8, 2)
        cw, bcw = C.sb("cw", [128, 4, 6])
        P.dma("sp", cw[:], conv_d.rearrange("j (c p) -> p j c", p=128), writes=[bcw])
        extu = fz.get("uTp") if fz else None
        if extu:
            wu_d = D("wu", [1024, 256])
            wu, bwu = load_w_bf16(C, "wu", wu_d, 8, 256)
            uTp, buTp = extu
        c64, bc64 = C.sb("c64", [64, 4, 64]); P.dma("sp", c64[:], c64_d, writes=[bc64])
        NEGU = c64[:, 0, :]; NEGLS = c64[:, 1, :]; NSU = c64[:, 2, :]; I64 = c64[:, 3, :]
        cmask, bcmask = C.sb("cmask", [2, 512]); P.dma("sp", cmask[:], cmask_d, writes=[bcmask])
        sel, bsel = C.sb("sel", [2, 2, 128]); P.dma("sp", sel[:], sel_d, writes=[bsel])
        ones, bones = C.sb("ones", [128, 128]); P.dma("sp", ones[:], ones_d, writes=[bones])
        alog, balog = C.sb("alog", [2, 1]); P.dma("sp", alog[:], alog_d.rearrange("(a b) -> a b", b=1), writes=[balog])
        dtb, bdtb = C.sb("dtb", [2, 1]); P.dma("sp", dtb[:], dtb_d.rearrange("(a b) -> a b", b=1), writes=[bdtb])
        negA, bnegA = C.sb("negA", [2, 1])
        P.op("act", lambda E: E.activation(out=negA[:], in_=alog[:], func=AF.Exp), reads=[balog], writes=[bnegA])
        P.op("dve", lambda E: E.tensor_scalar(out=negA[:], in0=negA[:], scalar1=-1.0, scalar2=None, op0=ALU.mult), reads=[bnegA], writes=[bnegA])
        xt, bxt = C.sb("xt", [128, 1024]); sq, bsq = C.sb("sq", [128, 1024]); hn, bhn = C.sb("hn", [128, 1024], BF16)
        ss, bss = C.sb("ss", [128, 1]); hT, bhT = C.sb("hT", [128, 8, 512], BF16)
        raw, _ = C.sb("raw", [128, 6, 515]); braw = [Buf("raw%d" % i) for i in range(6)]
        cvq, bcvq = C.sb("cvq", [128, 512])
        act, _ = C.sb("act", [128, 4, 512]); bact = [Buf("act%d" % i) for i in range(4)]
        vbuf2 = []; qk2 = []; bqk2 = []
        for par_ in range(2):
            vt_, _ = C.sb("vbuf%d" % par_, [128, 2, 512]); vbuf2.append((vt_, [Buf("vb%d_%d" % (par_, i)) for i in range(2)]))
            qt_, _ = C.sb("qk%d" % par_, [128, 4, 512]); qk2.append(qt_); bqk2.append([Buf("qk%d_%d" % (par_, i)) for i in range(4)])
        rn, brn = C.sb("rn", [128, 512])
        brow, bbrow = C.sb("brow", [2, 512]); grow, bgrow = C.sb("grow", [2, 512]); gcrow, bgcrow = C.sb("gcrow", [2, 512])
        GCB2 = []; BB2 = []
        for par_ in range(2):
            GCB2.append([C.sb("GCB%d_%d" % (par_, h), [128, 512]) for h in range(2)])
            BB2.append([C.sb("BB%d_%d" % (par_, h), [128, 512]) for h in range(2)])
        m64 = {}
        for nm in ("arg1", "DT", "Ds", "tmp", "tmp2", "BBm"):
            m64[nm] = C.sb("m_" + nm, [64, 512])
        for nm in ("Pa", "Pb", "Qa", "Qb"):
            m64[nm] = C.sb("m_" + nm, [64, 512], BF16)
        heads = []
        for h in range(2):
            H = {}
            H["attnT"] = C.sb("attnT%d" % h, [64, 512], BF16); H["Y"] = C.sb("Y%d" % h, [64, 512]); H["Ybf"] = C.sb("Ybf%d" % h, [64, 512], BF16)
            H["EG"] = C.sb("EG%d" % h, [128, 512]); H["qdec"] = C.sb("qdec%d" % h, [128, 512])
            H["bv"] = C.sb("bv%d" % h, [64, 8, 128]); H["kdec"] = C.sb("kdec%d" % h, [64, 8, 128], BF16)
            H["nbg"] = C.sb("nbg%d" % h, [64, 8]); H["osb"] = C.sb("osb%d" % h, [128, 8, 128])
            H["vnew"] = C.sb("vnew%d" % h, [64, 128], BF16); H["rhs2"] = C.sb("rhs2%d" % h, [64, 128], BF16)
            heads.append(H)
        small = {}
        for nm in ("gccol", "bcol", "nbcol", "elast", "egc"):
            small[nm] = C.sb("s_" + nm, [64, 8])
        Sst = [C.sb("S%d" % h, [128, 128]) for h in range(2)]
        for h in range(2):
            P.op("dve", lambda E, h=h: E.memset(Sst[h][0][:], 0.0), writes=[Sst[h][1]])
        P.op("dve", lambda E: E.memset(raw[:, :, 0:3], 0.0), writes=braw)
        ptr, bptr = C.ps("ptr", [128, 1024], BF16)
        G = [C.ps("gp%d" % i, [128, 512]) for i in range(7)]
        GP = G[0:4]
        GA = G[4:7]
        ga_ctr = [0]

        def next_ga():
            ga_ctr[0] += 1
            return GA[ga_ctr[0] % 3]
        bo = None if fz else Buf("o", multi=True)
        if fz is not None and fz.get("debug"):
            print("L1a sbuf remaining", nc.sbuf_bytes_remaining)

        def tt(out, bo_, a, ba, b, bb_, op, eng="dve"):
            P.op(eng, lambda E: E.tensor_tensor(out=out, in0=a, in1=b, op=op), reads=ba if isinstance(ba, list) else [ba], writes=[bo_])

        def stageA(s_):
            par = s_ % 2
            qk = qk2[par]; bqk = bqk2[par]; GCB = GCB2[par]; BB = BB2[par]; vb, bvb = vbuf2[par]
            for t in range(4):
                r0 = s_ * 512 + t * 128
                P.dma("sp", xt[:], x_d[r0:r0 + 128, :], writes=[bxt])
                rms_rstd(C, xt[:], bxt, 1024, sq[:], bsq, ss, bss)
                P.op("dve", lambda E: E.scalar_tensor_tensor(out=hn[:], in0=xt[:], scalar=ss[:, 0:1], in1=npre[:],
                                                             op0=ALU.mult, op1=ALU.mult), reads=[bxt, bss, bnpre], writes=[bhn])
                transpose8(C, hn, bhn, idb, bidb, ptr, bptr, hT[:, :, t * 128:(t + 1) * 128], bhT, eng="act")
                yield
            for ct in range(6):
                pa, bpa = next_ga()
                fns = [(lambda E, kt=kt, ct=ct, pa=pa: E.matmul(pa[:], lhsT=w[:, kt, ct * 128:(ct + 1) * 128], rhs=hT[:, kt, :],
                                                                start=(kt == 0), stop=(kt == 7))) for kt in range(8)]
                P.mm_group(fns, reads=[bw, bhT], writes=[bpa])
                P.op("act", lambda E, ct=ct, pa=pa: E.copy(out=raw[:, ct, 3:515], in_=pa[:]), reads=[bpa], writes=[braw[ct]])
                P.op("dve", lambda E, ct=ct: E.tensor_scalar(out=cvq[:], in0=raw[:, ct, 0:512], scalar1=cw[:, 0, ct:ct + 1], scalar2=None, op0=ALU.mult),
                     reads=[braw[ct], bcw], writes=[bcvq])
                for j in range(1, 4):
                    P.op("dve", lambda E, ct=ct, j=j: E.scalar_tensor_tensor(out=cvq[:], in0=raw[:, ct, j:j + 512], scalar=cw[:, j, ct:ct + 1], in1=cvq[:],
                                                                             op0=ALU.mult, op1=ALU.add), reads=[braw[ct], bcw, bcvq], writes=[bcvq])
                P.op("act", lambda E, ct=ct: E.copy(out=raw[:, ct, 0:3], in_=raw[:, ct, 512:515]), reads=[braw[ct]], writes=[braw[ct]])
                if ct < 4:
                    P.op("act", lambda E, ct=ct: E.activation(out=act[:, ct, :], in_=cvq[:], func=AF.Silu), reads=[bcvq], writes=[bact[ct]])
                else:
                    P.op("act", lambda E, ct=ct, vb=vb: E.activation(out=vb[:, ct - 4, :], in_=cvq[:], func=AF.Silu), reads=[bcvq], writes=[bvb[ct - 4]])
                yield
            if extu:
                for blk in range(2):
                    pa, bpa = next_ga()
                    fns = [(lambda E, kt=kt, blk=blk, pa=pa: E.matmul(
                        pa[:].rearrange("p (s n) -> p s n", s=16), lhsT=wu[:, kt, blk * 128:(blk + 1) * 128],
                        rhs=hT[:, kt, :].rearrange("p (n s) -> p s n", s=16), start=(kt == 0), stop=(kt == 7))) for kt in range(8)]
                    P.mm_group(fns, reads=[bwu, bhT], writes=[bpa])
                    P.op("act", lambda E, blk=blk, pa=pa, s_=s_: E.copy(out=uTp[:, blk, :, 32 * s_:32 * s_ + 32], in_=pa[:].rearrange("p (s n) -> p s n", s=16)),
                         reads=[bpa], writes=[buTp])
                    yield
            for ct in range(4):
                pa, bpa = next_ga()
                P.op("act", lambda E, ct=ct: E.activation(out=cvq[:], in_=act[:, ct, :], func=AF.Square), reads=[bact[ct]], writes=[bcvq])
                P.op("pe", lambda E, pa=pa: E.matmul(pa[:], lhsT=ones[:], rhs=cvq[:], start=True, stop=True), reads=[bones, bcvq], writes=[bpa])
                P.op("act", lambda E, pa=pa: E.activation(out=rn[:], in_=pa[:], func=AF.Ln, bias=1e-6, scale=1.0), reads=[bpa], writes=[brn])
                P.op("act", lambda E: E.activation(out=rn[:], in_=rn[:], func=AF.Exp, scale=-0.5), reads=[brn], writes=[brn])
                if ct < 2:
                    P.op("dve", lambda E, ct=ct, qk=qk: E.scalar_tensor_tensor(out=qk[:, ct, :], in0=act[:, ct, :], scalar=float(128 ** -0.5), in1=rn[:],
                                                                               op0=ALU.mult, op1=ALU.mult), reads=[bact[ct], brn], writes=[bqk[ct]])
                else:
                    P.op("dve", lambda E, ct=ct, qk=qk: E.tensor_tensor(out=qk[:, ct, :], in0=act[:, ct, :], in1=rn[:], op=ALU.mult),
                         reads=[bact[ct], brn], writes=[bqk[ct]])
                yield
            pa, bpa = next_ga()
            fns = [(lambda E, kt=kt, pa=pa: E.matmul(pa[0:2, :], lhsT=wb[:, kt, 0:2], rhs=hT[:, kt, :], start=(kt == 0), stop=(kt == 7))) for kt in range(8)]
            P.mm_group(fns, reads=[bwb, bhT], writes=[bpa])
            P.op("act", lambda E, pa=pa: E.activation(out=brow[:], in_=pa[0:2, :], func=AF.Sigmoid), reads=[bpa], writes=[bbrow])
            pa2, bpa2 = next_ga()
            fns = [(lambda E, kt=kt, pa2=pa2: E.matmul(pa2[0:2, :], lhsT=wa[:, kt, 0:2], rhs=hT[:, kt, :], start=(kt == 0), stop=(kt == 7))) for kt in range(8)]
            P.mm_group(fns, reads=[bwa, bhT], writes=[bpa2])
            P.op("act", lambda E, pa2=pa2: E.activation(out=grow[:], in_=pa2[0:2, :], func=AF.Exp, bias=dtb[:, 0:1], scale=1.0), reads=[bpa2, bdtb], writes=[bgrow])
            P.op("act", lambda E: E.activation(out=grow[:], in_=grow[:], func=AF.Ln, bias=1.0, scale=1.0), reads=[bgrow], writes=[bgrow])
            P.op("dve", lambda E: E.tensor_scalar(out=grow[:], in0=grow[:], scalar1=negA[:, 0:1], scalar2=None, op0=ALU.mult), reads=[bgrow, bnegA], writes=[bgrow])
            P.op("dve", lambda E: E.tensor_tensor_scan(out=gcrow[:], data0=cmask[:], data1=grow[:], initial=0.0, op0=ALU.mult, op1=ALU.add),
                 reads=[bcmask, bgrow], writes=[bgcrow])
            yield
            for h in range(2):
                pa, bpa = next_ga()
                P.op("pe", lambda E, h=h, pa=pa: E.matmul(pa[:], lhsT=sel[:, h, :], rhs=gcrow[:], start=True, stop=True), reads=[bsel, bgcrow], writes=[bpa])
                P.op("act", lambda E, h=h, pa=pa, GCB=GCB: E.copy(out=GCB[h][0][:], in_=pa[:]), reads=[bpa], writes=[GCB[h][1]])
                pa, bpa = next_ga()
                P.op("pe", lambda E, h=h, pa=pa: E.matmul(pa[:], lhsT=sel[:, h, :], rhs=brow[:], start=True, stop=True), reads=[bsel, bbrow], writes=[bpa])
                P.op("act", lambda E, h=h, pa=pa, BB=BB: E.copy(out=BB[h][0][:], in_=pa[:]), reads=[bpa], writes=[BB[h][1]])
                yield

        for _ in stageA(0):
            pass
        for s_ in range(NST):
            par = s_ % 2
            qk = qk2[par]; bqk = bqk2[par]; GCB = GCB2[par]; BB = BB2[par]; vb, bvb = vbuf2[par]
            nxt = stageA(s_ + 1) if s_ + 1 < NST else None

            def advance(k):
                if nxt is not None:
                    for _ in range(k):
                        next(nxt, None)
            for h in range(2):
                qT = qk[:, h, :]; bqT = bqk[h]; kT = qk[:, 2 + h, :]; bkT = bqk[2 + h]; vT = vb[:, h, :]; bvT = bvb[h]
                gcb, bgcb = GCB[h]; bb, bbb = BB[h]
                H = heads[h]
                attnT, battnT = H["attnT"]; Y, bY = H["Y"]; EG, bEG = H["EG"]; qdec, bqdec = H["qdec"]
                Ybf, bYbf = H["Ybf"]
                bv, bbv = H["bv"]; kdec, bkdec = H["kdec"]; nbg, bnbg = H["nbg"]
                arg1, barg1 = m64["arg1"]; DT, bDT = m64["DT"]; Ds, bDs = m64["Ds"]
                tmp, btmp = m64["tmp"]; tmp2, btmp2 = m64["tmp2"]; BBm, bBBm = m64["BBm"]
                gccol, bgccol = small["gccol"]; bcol, bbcol = small["bcol"]; nbcol, bnbcol = small["nbcol"]
                elast, belast = small["elast"]; egc, begc = small["egc"]
                v3 = lambda t_: t_[:].rearrange("p (n f) -> p n f", f=64)
                i64b = I64.unsqueeze(1).to_broadcast([64, 8, 64])
                tt(v3(tmp), btmp, gcb[0:64, :].rearrange("p (n f) -> p n f", f=64), [bgcb, bc64], i64b, bc64, ALU.mult)
                P.op("dve", lambda E, tmp=tmp, gccol=gccol: E.tensor_reduce(out=gccol[:], in_=tmp[:].rearrange("p (n f) -> p n f", f=64), axis=AX.X, op=ALU.add), reads=[btmp], writes=[bgccol])
                tt(v3(tmp), btmp, bb[0:64, :].rearrange("p (n f) -> p n f", f=64), [bbb, bc64], i64b, bc64, ALU.mult)
                P.op("dve", lambda E, tmp=tmp, bcol=bcol: E.tensor_reduce(out=bcol[:], in_=tmp[:].rearrange("p (n f) -> p n f", f=64), axis=AX.X, op=ALU.add), reads=[btmp], writes=[bbcol])
                P.op("dve", lambda E: E.tensor_scalar(out=nbcol[:], in0=bcol[:], scalar1=-1.0, scalar2=None, op0=ALU.mult), reads=[bbcol], writes=[bnbcol])
                tt(v3(arg1), barg1, gcb[0:64, :].rearrange("p (n f) -> p n f", f=64), [bgcb, bgccol], gccol[:].unsqueeze(2).to_broadcast([64, 8, 64]), bgccol, ALU.subtract)
                tt(v3(DT), bDT, v3(arg1), [barg1, bc64], NEGU.unsqueeze(1).to_broadcast([64, 8, 64]), bc64, ALU.add)
                P.op("act", lambda E: E.activation(out=DT[:], in_=DT[:], func=AF.Exp), reads=[bDT], writes=[bDT])
                P.op("dve", lambda E: E.scalar_tensor_tensor(out=Ds[:].rearrange("p (n f) -> p n f", f=64), in0=arg1[:].rearrange("p (n f) -> p n f", f=64), scalar=-1.0,
                                                             in1=NEGLS.unsqueeze(1).to_broadcast([64, 8, 64]), op0=ALU.mult, op1=ALU.add), reads=[barg1, bc64], writes=[bDs])
                P.op("act", lambda E: E.activation(out=Ds[:], in_=Ds[:], func=AF.Exp), reads=[bDs], writes=[bDs])
                tt(v3(BBm), bBBm, bb[0:64, :].rearrange("p (n f) -> p n f", f=64), [bbb, bc64], NSU.unsqueeze(1).to_broadcast([64, 8, 64]), bc64, ALU.mult)
                pk, bpk = GP[0]; pq, bpq = GP[1]
                fns = [(lambda E, n=n, pk=pk, kT=kT: E.matmul(pk[0:64, n * 64:(n + 1) * 64], lhsT=kT[:, n * 64:(n + 1) * 64], rhs=kT[:, n * 64:(n + 1) * 64],
                                                              start=True, stop=True)) for n in range(8)]
                P.mm_group(fns, reads=[bkT], writes=[bpk])
                fns = [(lambda E, n=n, pq=pq, kT=kT, qT=qT: E.matmul(pq[0:64, n * 64:(n + 1) * 64], lhsT=kT[:, n * 64:(n + 1) * 64], rhs=qT[:, n * 64:(n + 1) * 64],
                                                                     start=True, stop=True)) for n in range(8)]
                P.mm_group(fns, reads=[bkT, bqT], writes=[bpq])
                tt(attnT[:], battnT, pq[0:64, :], [bpq, bDT], DT[:], bDT, ALU.mult)
                Pc, bPc = m64["Pa"]; Pn, bPn = m64["Pb"]; Qc, bQc = m64["Qa"]; Qn, bQn = m64["Qb"]
                tt(tmp[:], btmp, pk[0:64, :], [bpk, bDT], DT[:], bDT, ALU.mult)
                tt(Qc[:], bQc, tmp[:], [btmp, bBBm], BBm[:], bBBm, ALU.mult)
                tt(tmp2[:], btmp2, pk[0:64, :], [bpk, bDs], Ds[:], bDs, ALU.mult)
                tt(v3(Pc), bPc, v3(tmp2), [btmp2, bnbcol], nbcol[:].unsqueeze(2).to_broadcast([64, 8, 64]), bnbcol, ALU.mult)
                tt(v3(Y), bY, v3(Qc), [bQc, bc64], i64b, bc64, ALU.add)
                P.op("act", lambda E, Ybf=Ybf, Y=Y: E.copy(out=Ybf[:], in_=Y[:]), reads=[bY], writes=[bYbf])
                for j in range(5):
                    pP, bpP = GP[2]; pQ, bpQ = GP[3]
                    fns = [(lambda E, n=n, pP=pP, Qc=Qc, Pc=Pc: E.matmul(pP[0:64, n * 64:(n + 1) * 64], lhsT=Qc[:, n * 64:(n + 1) * 64], rhs=Pc[:, n * 64:(n + 1) * 64],
                                                                         start=True, stop=True)) for n in range(8)]
                    P.mm_group(fns, reads=[bQc, bPc], writes=[bpP])
                    if j < 4:
                        fns = [(lambda E, n=n, pQ=pQ, Qc=Qc, Pc=Pc: E.matmul(pQ[0:64, n * 64:(n + 1) * 64], lhsT=Pc[:, n * 64:(n + 1) * 64], rhs=Qc[:, n * 64:(n + 1) * 64],
                                                                             start=True, stop=True)) for n in range(8)]
                        P.mm_group(fns, reads=[bQc, bPc], writes=[bpQ])
                    P.op("act", lambda E, Pn=Pn, pP=pP: E.copy(out=Pn[:], in_=pP[0:64, :]), reads=[bpP], writes=[bPn])
                    if j < 4:
                        P.op("dve", lambda E, Qn=Qn, pQ=pQ: E.tensor_copy(out=Qn[:], in_=pQ[0:64, :]), reads=[bpQ], writes=[bQn])
                    pY, bpY = GP[0]
                    fns = [(lambda E, n=n, pY=pY, Pn=Pn, Ybf=Ybf: E.matmul(pY[0:64, n * 64:(n + 1) * 64], lhsT=Pn[:, n * 64:(n + 1) * 64], rhs=Ybf[:, n * 64:(n + 1) * 64],
                                                                         start=True, stop=True)) for n in range(8)]
                    P.mm_group(fns, reads=[bPn, bYbf], writes=[bpY])
                    tt(Y[:], bY, Y[:], [bY, bpY], pY[0:64, :], bpY, ALU.add)
                    P.op("act", lambda E, Ybf=Ybf, Y=Y: E.copy(out=Ybf[:], in_=Y[:]), reads=[bY], writes=[bYbf])
                    Pc, bPc, Pn, bPn = Pn, bPn, Pc, bPc
                    Qc, bQc, Qn, bQn = Qn, bQn, Qc, bQc
                for hf in range(2):
                    pth, bpth = GA[hf]
                    fns = [(lambda E, n=n, vT=vT, pth=pth, hf=hf: E.transpose(out=pth[0:64, n * 128:(n + 1) * 128], in_=vT[:, (4 * hf + n) * 64:(4 * hf + n + 1) * 64],
                                                                              identity=idf[:])) for n in range(4)]
                    P.mm_group(fns, reads=[bvT, bidf], writes=[bpth])
                    tt(bv[:, 4 * hf:4 * hf + 4, :], bbv, pth[0:64, :].rearrange("p (n d) -> p n d", d=128), [bpth, bbcol],
                       bcol[:, 4 * hf:4 * hf + 4].unsqueeze(2).to_broadcast([64, 4, 128]), bbcol, ALU.mult)
                tt(elast[:], belast, gcb[0:64, :].rearrange("p (n f) -> p n f", f=64)[:, :, 63], [bgcb, bgccol], gccol[:], bgccol, ALU.subtract)
                P.op("act", lambda E: E.activation(out=elast[:], in_=elast[:], func=AF.Exp), reads=[belast], writes=[belast])
                for hf in range(2):
                    pth, bpth = GA[hf]
                    fns = [(lambda E, n=n, kT=kT, pth=pth, hf=hf: E.transpose(out=pth[0:64, n * 128:(n + 1) * 128], in_=kT[:, (4 * hf + n) * 64:(4 * hf + n + 1) * 64],
                                                                              identity=idf[:])) for n in range(4)]
                    P.mm_group(fns, reads=[bkT, bidf], writes=[bpth])
                    tt(kdec[:, 4 * hf:4 * hf + 4, :], bkdec, pth[0:64, :].rearrange("p (n d) -> p n d", d=128), [bpth, belast],
                       elast[:, 4 * hf:4 * hf + 4].unsqueeze(2).to_broadcast([64, 4, 128]), belast, ALU.mult)
                P.op("act", lambda E, gcb=gcb, EG=EG: E.activation(out=EG[:], in_=gcb[:], func=AF.Exp), reads=[bgcb], writes=[bEG])
                tt(qdec[:], bqdec, qT, [bqT, bEG], EG[:], bEG, ALU.mult)
                P.op("act", lambda E: E.activation(out=egc[:], in_=gccol[:], func=AF.Exp), reads=[bgccol], writes=[begc])
                P.op("dve", lambda E, nbg=nbg: E.scalar_tensor_tensor(out=nbg[:], in0=egc[:], scalar=-1.0, in1=bcol[:], op0=ALU.mult, op1=ALU.mult),
                     reads=[begc, bbcol], writes=[bnbg])
            banks = [(GP[0], GP[1]), (GP[2], GP[3])]
            for n in range(8):
                cs = slice(n * 64, (n + 1) * 64)
                for h in range(2):
                    kT = qk[:, 2 + h, :]; bkT = bqk[2 + h]
                    H = heads[h]; S, bS = Sst[h]
                    attnT, battnT = H["attnT"]; Y, bY = H["Ybf"]; EG, bEG = H["EG"]; qdec, bqdec = H["qdec"]
                    bv, bbv = H["bv"]; kdec, bkdec = H["kdec"]; nbg, bnbg = H["nbg"]
                    vnew, bvnew = H["vnew"]; rhs2, brhs2 = H["rhs2"]; osb, bosb = H["osb"]
                    (KSO, bKSO), (Sb, bSb) = banks[h]
                    Vb, bVb = KSO, bKSO
                    P.op("pe", lambda E, cs=cs, kT=kT, S=S, KSO=KSO: E.matmul(KSO[0:64, 0:128], lhsT=kT[:, cs], rhs=S[:], start=True, stop=True),
                         reads=[bkT, bS], writes=[bKSO])
                    P.op("dve", lambda E, n=n, KSO=KSO, rhs2=rhs2, nbg=nbg, bv=bv: E.scalar_tensor_tensor(
                        out=rhs2[:], in0=KSO[0:64, 0:128], scalar=nbg[:, n:n + 1], in1=bv[:, n, :], op0=ALU.mult, op1=ALU.add),
                        reads=[bKSO, bnbg, bbv], writes=[brhs2])
                    P.op("pe", lambda E, cs=cs, Y=Y, Vb=Vb, rhs2=rhs2: E.matmul(Vb[0:64, 128:256], lhsT=Y[:, cs], rhs=rhs2[:], start=True, stop=True),
                         reads=[bY, brhs2], writes=[bVb])
                    P.op("act", lambda E, vnew=vnew, Vb=Vb: E.copy(out=vnew[:], in_=Vb[0:64, 128:256]), reads=[bVb], writes=[bvnew])
                    fns = [lambda E, cs=cs, S=S, KSO=KSO, qdec=qdec: E.matmul(KSO[64:128, 0:128], lhsT=qdec[:, cs], rhs=S[:], start=True, stop=False),
                           lambda E, cs=cs, KSO=KSO, attnT=attnT, vnew=vnew: E.matmul(KSO[64:128, 0:128], lhsT=attnT[:, cs], rhs=vnew[:], start=False, stop=True)]
                    P.mm_group(fns, reads=[bqdec, bS, battnT, bvnew], writes=[bKSO])
                    P.op("pe", lambda E, n=n, Sb=Sb, kdec=kdec, vnew=vnew: E.matmul(Sb[:, 0:128], lhsT=kdec[:, n, :], rhs=vnew[:], start=True, stop=True),
                         reads=[bkdec, bvnew], writes=[bSb])
                    P.op("dve", lambda E, n=n, S=S, EG=EG, Sb=Sb: E.scalar_tensor_tensor(out=S[:], in0=S[:], scalar=EG[:, n * 64 + 63:n * 64 + 64], in1=Sb[:, 0:128],
                                                                                         op0=ALU.mult, op1=ALU.add), reads=[bS, bEG, bSb], writes=[bS])
                    P.op("act", lambda E, n=n, osb=osb, KSO=KSO: E.copy(out=osb[64:128, n, :], in_=KSO[64:128, 0:128]), reads=[bKSO], writes=[bosb])
                    advance(1)
                advance(1)
            advance(100)
            for h in range(2):
                osb, bosb = heads[h]["osb"]
                P.dma("sp", o_d[s_ * 512:(s_ + 1) * 512, h * 128:(h + 1) * 128].rearrange("(n c) d -> c n d", c=64), osb[64:128, :, :], reads=[bosb],
                      writes=[fz["obuf_of"](s_) if fz else bo])
            if fz:
                fz["after_chunk"](s_)
        if fz:
            barrier(P)
        else:
            P.finish([bo])
    return nc


def run_L1a(inp):
    nc = _get("L1a", build_L1a)
    c64, cmask, sel = _gdn_consts()
    w_in = inp["w_in_even"][0]
    conv = inp["conv_qkv"][0]
    ones = np.ones((128, 128), np.float32)
    maps = []
    for c in range(8):
        b, r = divmod(c, 4)
        cols = np.concatenate([np.arange(256 * r, 256 * r + 256), 1024 + np.arange(256 * r, 256 * r + 256), 2048 + np.arange(256 * r, 256 * r + 256)])
        maps.append({"x": np.ascontiguousarray(inp["x"][b]), "npre": np.ascontiguousarray(inp["norm_pre"][0]),
                     "w": np.ascontiguousarray(w_in[:, cols]), "wb": np.ascontiguousarray(w_in[:, 4096 + 2 * r:4096 + 2 * r + 2]),
                     "wa": np.ascontiguousarray(w_in[:, 4104 + 2 * r:4104 + 2 * r + 2]), "conv": np.ascontiguousarray(conv[:, cols]),
                     "alog": np.ascontiguousarray(inp["a_log"][0, 2 * r:2 * r + 2]), "dtb": np.ascontiguousarray(inp["dt_bias"][0, 2 * r:2 * r + 2]),
                     "ident": _IDENT, "c64": c64, "cmask": cmask, "sel": sel, "ones": ones})
    res = run_bass_kernel_spmd(nc, maps, core_ids=list(range(8)))
    S_ = inp["x"].shape[1]
    o = np.empty((2, S_, 1024), np.float32)
    for c in range(8):
        b, r = divmod(c, 4)
        o[b, :, 256 * r:256 * (r + 1)] = res.results[c]["o"]
    return o


def kernel_unfused(**inputs):
    inp = {k: np.asarray(v) for k, v in inputs.items()}
    o = run_L1a(inp)
    ys = run_L1b(inp)
    x1 = run_L2(inp, o, ys)
    out = run_L3(inp, x1)
    return out.astype(np.float32)


def build_fused():
    nc = bass.Bass("TRN2", target_bir_lowering=False)
    x_full = nc.dram_tensor("x", [8192, 1024], F32, kind="ExternalInput").ap()
    ident_d = nc.dram_tensor("ident", [128, 128], F32, kind="ExternalInput").ap()
    npre0_d = nc.dram_tensor("npre0", [1024], F32, kind="ExternalInput").ap()
    gidx_d = nc.dram_tensor("gidx", [128, 17, 4], I32, kind="ExternalInput").ap()
    out_d = nc.dram_tensor("out", [2048, 1024], F32, kind="ExternalOutput").ap()
    ag_in = [nc.dram_tensor("ag_in%d" % i, [8192, 256], F32) for i in range(2)]
    ag_out = [nc.dram_tensor("ag_out%d" % i, [4 * 8192, 256], F32) for i in range(2)]
    x1s = nc.dram_tensor("x1s", [2176, 1024], F32)
    GROUPS = [[0, 1, 2, 3], [4, 5, 6, 7]]
    with ExitStack() as st:
        C = Ctx(nc, st); P = C.P
        csem = st.enter_context(nc.semaphore("csem"))
        bag_out = Buf("ag_out"); bx1s = Buf("x1s", multi=True); bout = Buf("out", multi=True)
        bo_ch = [Buf("o_ch%d" % k, multi=True) for k in range(16)]
        by_jt = [Buf("y_jt%d" % k, multi=True) for k in range(4)]
        ncc = [0]

        def emit_cc(which, k, inbuf):
            P._deps("pool", [inbuf], [])
            P.streams["pool"].append(lambda E, which=which, k=k: E.collective_compute(
                "AllGather", ALU.bypass, replica_groups=GROUPS,
                ins=[ag_in[which].ap()[k * 512:(k + 1) * 512, :].opt()], outs=[ag_out[which].ap()[k * 2048:(k + 1) * 2048, :].opt()]).then_inc(csem))
            ncc[0] += 1

        share1 = {"x": x_full, "ident": ident_d, "npre": npre0_d}

        def after_jt(jt):
            for k in range(4 * jt, 4 * jt + 4):
                emit_cc(1, k, by_jt[jt])

        with ExitStack() as stU:
            CU = Ctx(nc, stU, P, "u_")
            uext = CU.sb("uTp", [128, 2, 16, 512], BF16)
            build_L1a(8192, fz={"nc": nc, "P": P, "pfx": "a_", "share": share1, "out": ag_in[0].ap(), "uTp": uext,
                                "obuf_of": lambda s_: bo_ch[s_], "after_chunk": lambda s_: emit_cc(0, s_, bo_ch[s_])})
            build_L1b(8192, fz={"nc": nc, "P": P, "pfx": "b_", "share": share1, "out": ag_in[1].ap(), "uTp": uext,
                                "obuf_of": lambda jt: by_jt[jt], "after_chunk": after_jt})
        P.streams["pool"].append(lambda E: E.wait_ge(csem, ncc[0]))
        gidx, bgidx = C.sb("gidx", [128, 17, 4], I32)
        P.dma("sp", gidx[:], gidx_d, writes=[bgidx])
        P.op("pool", lambda E: E.nop(), reads=[], writes=[bag_out])

        def gather(P_, ld, bld, tile, part):
            for i in range(4):
                P_.dma_ind("pool", ld[:, i * 256:(i + 1) * 256], ag_out[part].ap(), gidx[:, tile, i:i + 1], reads=[bag_out, bgidx], writes=[bld])

        share2 = {"ident": ident_d, "npre": npre0_d, "o": None, "ys": None}
        build_L2(2176, fz={"nc": nc, "P": P, "pfx": "c_", "share": share2, "out": x1s.ap(), "obuf": bx1s, "gather": gather})
        share3 = {"ident": ident_d, "x": x1s.ap()}
        build_L3(2048, fz={"nc": nc, "P": P, "pfx": "d_", "share": share3, "out": out_d, "obuf": bout, "xbuf": bx1s})
        P.finish([bout])
    return nc


def _gidx(r):
    g = np.zeros((128, 17, 4), np.int32)
    p = np.arange(128)[:, None, None]
    tile = np.arange(17)[None, :, None]
    src = np.arange(4)[None, None, :]
    tok = np.clip(2048 * r - 128 + tile * 128 + p, 0, 8191)
    g[:] = ((tok // 512) * 4 + src) * 512 + tok % 512
    return g


def kernel(**inputs):
    inp = {k: np.ascontiguousarray(np.asarray(v)) for k, v in inputs.items()}
    nc = _get("fused", build_fused)
    c64, cmask, sel = _gdn_consts()
    mk, idm = _s5_consts()
    ones = np.ones((128, 128), np.float32)
    w_in = inp["w_in_even"][0]
    conv = inp["conv_qkv"][0]
    wz = np.ascontiguousarray(np.concatenate([w_in[:, 3072:4096], w_in[:, 5136:6160]], axis=1))
    maps = []
    for c in range(8):
        b, r = divmod(c, 4)
        cols = np.concatenate([np.arange(256 * r, 256 * r + 256), 1024 + np.arange(256 * r, 256 * r + 256), 2048 + np.arange(256 * r, 256 * r + 256)])
        gs = slice(16 * r, 16 * r + 16)
        xq = np.zeros((2176, 1024), np.float32)
        xq[128:] = inp["x"][b, 2048 * r:2048 * (r + 1)]
        if r > 0:
            xq[:128] = inp["x"][b, 2048 * r - 128:2048 * r]
        m = {"x": inp["x"][b], "ident": _IDENT, "npre0": inp["norm_pre"][0], "gidx": _gidx(r),
             "a_w": np.ascontiguousarray(w_in[:, cols]), "a_wb": np.ascontiguousarray(w_in[:, 4096 + 2 * r:4096 + 2 * r + 2]),
             "a_wa": np.ascontiguousarray(w_in[:, 4104 + 2 * r:4104 + 2 * r + 2]), "a_conv": np.ascontiguousarray(conv[:, cols]),
             "a_alog": np.ascontiguousarray(inp["a_log"][0, 2 * r:2 * r + 2]), "a_dtb": np.ascontiguousarray(inp["dt_bias"][0, 2 * r:2 * r + 2]),
             "a_c64": c64, "a_cmask": cmask, "a_sel": sel, "a_ones": ones,
             "a_wu": np.ascontiguousarray(w_in[:, 4112 + 256 * r:4112 + 256 * (r + 1)]),
             "b_wu": np.ascontiguousarray(w_in[:, 4112 + 256 * r:4112 + 256 * (r + 1)]),
             "b_lre": np.ascontiguousarray(inp["s5_lam_re"][0, gs]), "b_lim": np.ascontiguousarray(inp["s5_lam_im"][0, gs]),
             "b_bre": np.ascontiguousarray(inp["s5_b_re"][0, gs]), "b_bim": np.ascontiguousarray(inp["s5_b_im"][0, gs]),
             "b_cre": np.ascontiguousarray(inp["s5_c_re"][0, gs]), "b_cim": np.ascontiguousarray(inp["s5_c_im"][0, gs]),
             "b_ldt": np.ascontiguousarray(inp["s5_log_dt"][0, gs]), "b_dd": np.ascontiguousarray(inp["s5_d"][0, 256 * r:256 * (r + 1)]),
             "b_taus": TAUS, "b_mk": mk, "b_idm": idm,
             "c_x": xq, "c_wz": wz, "c_wglu": inp["w_glu"][0], "c_wout": inp["w_out_even"][0], "c_npost": inp["norm_post"][0],
             "c_gnw": inp["gdn_norm_w"][0],
             "d_win": inp["w_in_odd"][0], "d_wout": inp["w_out_odd"][0], "d_conv": inp["conv_short"][0],
             "d_npre": inp["norm_pre"][1], "d_npost": inp["norm_post"][1]}
        maps.append(m)
    res = run_bass_kernel_spmd(nc, maps, core_ids=list(range(8)))
    out = np.empty((2, 8192, 1024), np.float32)
    for c in range(8):
        b, r = divmod(c, 4)
        out[b, r * 2048:(r + 1) * 2048] = res.results[c]["out"]
    return out
```

```python
from contextlib import ExitStack
import numpy as np
import concourse.bass as bass
import concourse.mybir as mybir
from concourse.bass_utils import run_bass_kernel_spmd

F32 = mybir.dt.float32
BF16 = mybir.dt.bfloat16
AF = mybir.ActivationFunctionType
ALU = mybir.AluOpType
AX = mybir.AxisListType

NDS = 12


class Buf:
    __slots__ = ("name", "w", "r", "multi")

    def __init__(self, name, multi=False):
        self.name = name
        self.w = [] if multi else None
        self.r = []
        self.multi = multi


class Prog:
    ENG = ("pe", "act", "dve", "pool", "sp")

    def __init__(self, nc, stack):
        self.nc = nc
        self.stack = stack
        self.streams = {e: [] for e in self.ENG}
        self.cnt = {e: 0 for e in self.ENG}
        self.sem = {e: stack.enter_context(nc.semaphore("s_" + e)) for e in self.ENG}
        self.seen = {e: {} for e in self.ENG}
        self.dcnt = {e: 0 for e in self.ENG}
        self.dsem = {}
        for e in ("sp", "pool", "act"):
            self.dsem[e] = [stack.enter_context(nc.semaphore("d_%s%d" % (e, i))) for i in range(NDS)]
        self.same_engine_sync = True
        self.nwaits = 0

    def _wait(self, eng, tok):
        if tok is None:
            return
        kind = tok[0]
        if kind == "c":
            _, e2, n = tok
            if e2 == eng and (eng == "pe" or not self.same_engine_sync):
                return
            key = e2
            if self.seen[eng].get(key, 0) >= n:
                return
            self.seen[eng][key] = n
            sem = self.sem[e2]
            self.streams[eng].append(lambda E, sem=sem, n=n: E.wait_ge(sem, n))
            self.nwaits += 1
        else:
            _, q, slot, val = tok
            key = ("d", q, slot)
            if self.seen[eng].get(key, 0) >= val:
                return
            self.seen[eng][key] = val
            sem = self.dsem[q][slot]
            self.streams[eng].append(lambda E, sem=sem, val=val: E.wait_ge(sem, val))
            self.nwaits += 1

    def _deps(self, eng, reads, writes):
        for b in reads:
            if b.multi:
                for t in b.w:
                    self._wait(eng, t)
            else:
                self._wait(eng, b.w)
        for b in writes:
            if not b.multi:
                self._wait(eng, b.w)
            for t in b.r:
                self._wait(eng, t)

    def _commit(self, tok, reads, writes):
        for b in writes:
            if b.multi:
                b.w.append(tok)
            else:
                b.w = tok
            b.r = []
        for b in reads:
            if b not in writes:
                b.r.append(tok)

    def op(self, eng, fn, reads=(), writes=()):
        reads = list(reads)
        writes = list(writes)
        self._deps(eng, reads, writes)
        self.cnt[eng] += 1
        n = self.cnt[eng]
        sem = self.sem[eng]
        self.streams[eng].append(lambda E, fn=fn, sem=sem: fn(E).then_inc(sem, 1))
        tok = ("c", eng, n)
        self._commit(tok, reads, writes)
        return tok

    def mm_group(self, fns, reads=(), writes=()):
        eng = "pe"
        reads = list(reads)
        writes = list(writes)
        self._deps(eng, reads, writes)
        self.cnt[eng] += 1
        n = self.cnt[eng]
        sem = self.sem[eng]
        for fn in fns[:-1]:
            self.streams[eng].append(lambda E, fn=fn: fn(E))
        last = fns[-1]
        self.streams[eng].append(lambda E, fn=last, sem=sem: fn(E).then_inc(sem, 1))
        tok = ("c", eng, n)
        self._commit(tok, reads, writes)
        return tok

    def dma(self, q, out_ap, in_ap, reads=(), writes=()):
        reads = list(reads)
        writes = list(writes)
        self._deps(q, reads, writes)
        j = self.dcnt[q]
        self.dcnt[q] += 1
        slot = j % NDS
        val = 16 * (j // NDS + 1)
        if j >= NDS:
            self._wait(q, ("d", q, slot, val - 16))
        sem = self.dsem[q][slot]
        self.streams[q].append(
            lambda E, o=out_ap, i=in_ap, sem=sem: E.dma_start(out=o, in_=i).then_inc(sem, 16))
        tok = ("d", q, slot, val)
        self._commit(tok, reads, writes)
        return tok

    def dma_ind(self, q, out_ap, table_ap, idx_ap, reads=(), writes=()):
        reads = list(reads)
        writes = list(writes)
        self._deps(q, reads, writes)
        j = self.dcnt[q]
        self.dcnt[q] += 1
        slot = j % NDS
        val = 16 * (j // NDS + 1)
        if j >= NDS:
            self._wait(q, ("d", q, slot, val - 16))
        sem = self.dsem[q][slot]
        self.streams[q].append(
            lambda E, o=out_ap, t=table_ap, i=idx_ap, sem=sem: E.indirect_dma_start(
                out=o, out_offset=None, in_=t, in_offset=bass.IndirectOffsetOnAxis(ap=i, axis=0)).then_inc(sem, 16))
        tok = ("d", q, slot, val)
        self._commit(tok, reads, writes)
        return tok

    def finish(self, final_bufs):
        for b in final_bufs:
            for t in (b.w if b.multi else [b.w]):
                self._wait("sp", t)
        nc = self.nc
        streams = self.streams
        with nc.Block() as block:
            @block.tensor
            def _(E):
                for f in streams["pe"]:
                    f(E)

            @block.scalar
            def _(E):
                for f in streams["act"]:
                    f(E)

            @block.vector
            def _(E):
                for f in streams["dve"]:
                    f(E)

            @block.gpsimd
            def _(E):
                for f in streams["pool"]:
                    f(E)

            @block.sync
            def _(E):
                for f in streams["sp"]:
                    f(E)


class Ctx:
    def __init__(self, nc, st, P=None, pfx=""):
        self.nc = nc
        self.st = st
        self.pfx = pfx
        if P is None:
            st.enter_context(nc.allow_non_contiguous_dma(reason="small parameter loads / layout transforms"))
            P = Prog(nc, st)
        self.P = P

    def sb(self, name, shape, dt=F32):
        t = self.st.enter_context(self.nc.sbuf_tensor("sb_" + self.pfx + name, shape, dt))
        return t, Buf(name)

    def ps(self, name, shape, dt=F32):
        t = self.st.enter_context(self.nc.psum_tensor("ps_" + self.pfx + name, shape, dt))
        return t, Buf(name)


def bcast_row_load(C, name, dram_vec, n, q="sp"):
    t, b = C.sb(name, [128, n])
    C.P.dma(q, t[:], dram_vec.partition_broadcast(128), writes=[b])
    return t, b


def make_ident(C, dram_ident):
    idf, bidf = C.sb("identf", [128, 128])
    C.P.dma("sp", idf[:], dram_ident, writes=[bidf])
    idb, bidb = C.sb("identb", [128, 128], BF16)
    C.P.op("dve", lambda E: E.tensor_copy(out=idb[:], in_=idf[:]), reads=[bidf], writes=[bidb])
    return idf, bidf, idb, bidb


def rms_rstd(C, src, bsrc, ncols, junk, bjunk, ss, bss, eps=1e-6):
    P = C.P
    P.op("act", lambda E: E.activation(out=junk, in_=src, func=AF.Square, accum_out=ss[:, 0:1]),
         reads=[bsrc], writes=[bjunk, bss])
    P.op("act", lambda E: E.activation(out=ss[:, 0:1], in_=ss[:, 0:1], func=AF.Sqrt, bias=float(eps), scale=float(1.0 / ncols)),
         reads=[bss], writes=[bss])
    P.op("dve", lambda E: E.reciprocal(out=ss[:, 0:1], in_=ss[:, 0:1]), reads=[bss], writes=[bss])


def transpose8(C, src_bf, bsrc, idb, bidb, ptr, bptr, dst3, bdst, eng="act"):
    P = C.P
    fns = [(lambda E, kt=kt: E.transpose(out=ptr[:, kt * 128:(kt + 1) * 128], in_=src_bf[:, kt * 128:(kt + 1) * 128],
                                         identity=idb[:])) for kt in range(8)]
    P.mm_group(fns, reads=[bsrc, bidb], writes=[bptr])
    src3 = ptr[:].rearrange("p (k t) -> p k t", k=8)
    if eng == "act":
        P.op("act", lambda E: E.copy(out=dst3, in_=src3), reads=[bptr], writes=[bdst])
    else:
        P.op("dve", lambda E: E.tensor_copy(out=dst3, in_=src3), reads=[bptr], writes=[bdst])


def outproj_post(C, catT, bcat, nkt, wout, bwout, t, xres, bxres, npw, bnpw, pso, bpso, yo, byo, junk, bjunk, ss, bss,
                 out_dram_rows, bout):
    P = C.P
    for hh in range(2):
        fns = [(lambda E, kt=kt, hh=hh: E.matmul(pso[hh][:], lhsT=catT[:, kt, t * 128:(t + 1) * 128],
                                                 rhs=wout[:, kt, hh * 512:(hh + 1) * 512],
                                                 start=(kt == 0), stop=(kt == nkt - 1))) for kt in range(nkt)]
        P.mm_group(fns, reads=[bcat, bwout], writes=[bpso[hh]])
        P.op("act", lambda E, hh=hh: E.copy(out=yo[:, hh * 512:(hh + 1) * 512], in_=pso[hh][:]),
             reads=[bpso[hh]], writes=[byo])
    rms_rstd(C, yo[:], byo, 1024, junk[:], bjunk, ss, bss)
    P.op("dve", lambda E: E.scalar_tensor_tensor(out=yo[:], in0=yo[:], scalar=ss[:, 0:1], in1=npw[:],
                                                 op0=ALU.mult, op1=ALU.mult), reads=[byo, bss, bnpw], writes=[byo])
    P.op("dve", lambda E: E.tensor_tensor(out=yo[:], in0=yo[:], in1=xres, op=ALU.add), reads=[byo, bxres], writes=[byo])
    P.dma("sp", out_dram_rows, yo[:], reads=[byo], writes=[bout])


def load_w_bf16(C, name, dram_w, kt_n, ncols, chunk=2048):
    w, bw = C.sb(name, [128, kt_n, ncols], BF16)
    src = dram_w.rearrange("(k p) c -> p k c", p=128)
    for kt in range(kt_n):
        for c0 in range(0, ncols, chunk):
            c1 = min(ncols, c0 + chunk)
            C.P.dma("pool", w[:, kt, c0:c1], src[:, kt, c0:c1], writes=[bw])
    return w, bw


def build_L2(ntok=2048, fz=None):
    nc = fz["nc"] if fz else bass.Bass("TRN2", target_bir_lowering=False)
    pfx = fz["pfx"] if fz else ""

    def D(name, shape):
        if fz and name in fz["share"]:
            return fz["share"][name]
        return nc.dram_tensor(pfx + name, shape, F32, kind="ExternalInput").ap()
    x_d = D("x", [ntok, 1024]); o_d = D("o", [ntok, 1024]); ys_d = D("ys", [ntok, 1024])
    wz_d = D("wz", [1024, 2048]); wglu_d = D("wglu", [1024, 1024]); wout_d = D("wout", [2048, 1024])
    npre_d = D("npre", [1024]); npost_d = D("npost", [1024]); gnw_d = D("gnw", [128]); ident_d = D("ident", [128, 128])
    out_d = fz["out"] if fz else nc.dram_tensor("out", [ntok, 1024], F32, kind="ExternalOutput").ap()
    NT = 512
    with ExitStack() as st:
        C = Ctx(nc, st, fz["P"], pfx) if fz else Ctx(nc, st); P = C.P
        idf, bidf, idb, bidb = make_ident(C, ident_d)
        npre, bnpre = bcast_row_load(C, "npre", npre_d, 1024)
        npost, bnpost = bcast_row_load(C, "npost", npost_d, 1024)
        gnw, bgnw = bcast_row_load(C, "gnw", gnw_d, 128)
        wz, bwz = load_w_bf16(C, "wz", wz_d, 8, 2048)
        wglu, bwglu = load_w_bf16(C, "wglu", wglu_d, 8, 1024)
        wout, bwout = load_w_bf16(C, "wout", wout_d, 16, 1024)
        xt4, bxt4 = C.sb("xt4", [128, 4, 1024]); bxt = [Buf("xt%d" % i) for i in range(4)]
        ldo = [C.sb("ldo%d" % i, [128, 1024]) for i in range(2)]
        ldy = [C.sb("ldy%d" % i, [128, 1024]) for i in range(2)]
        sq, bsq = C.sb("sq", [128, 1024])
        hn, bhn = C.sb("hn", [128, 1024], BF16)
        ss, bss = C.sb("ss", [128, 1])
        ss8, bss8 = C.sb("ss8", [128, 8])
        hT, bhT = C.sb("hT", [128, 8, NT], BF16)
        oT, boT = C.sb("oT", [128, 8, NT], BF16)
        yT, byT = C.sb("yT", [128, 8, NT], BF16)
        gz, bgz = C.sb("gz", [128, 8, NT], BF16)
        sg, bsg = C.sb("sg", [128, NT], BF16)
        catT, bcat = C.sb("catT", [128, 16, NT], BF16)
        yo, byo = C.sb("yo", [128, 1024])
        ptr, bptr = C.ps("ptr", [128, 1024], BF16)
        pmm = []; bpmm = []
        for i in range(4):
            t_, b_ = C.ps("pmm%d" % i, [128, 512]); pmm.append(t_); bpmm.append(b_)
        pso = []; bpso = []
        for i in range(2):
            t_, b_ = C.ps("pso%d" % i, [128, 512]); pso.append(t_); bpso.append(b_)
        bout = fz["obuf"] if fz else Buf("out", multi=True)
        if fz:
            sts = [(0, 128)] + [(128 + i * NT, NT) for i in range((ntok - 128) // NT)]
        else:
            sts = [(i * NT, NT) for i in range(ntok // NT)]
        tile_r0 = [t0_ + t_ * 128 for (t0_, n_) in sts for t_ in range(n_ // 128)]

        def issue_loads(ti):
            r0_ = tile_r0[ti]
            lo, blo = ldo[ti % 2]; ly, bly = ldy[ti % 2]
            if fz:
                fz["gather"](P, lo, blo, r0_ // 128, 0)
                fz["gather"](P, ly, bly, r0_ // 128, 1)
            else:
                P.dma("sp", lo[:], o_d[r0_:r0_ + 128, :], writes=[blo])
                P.dma("sp", ly[:], ys_d[r0_:r0_ + 128, :], writes=[bly])

        issue_loads(0)
        for (t0, n) in sts:
            ntl = n // 128
            for t in range(ntl):
                r0 = t0 + t * 128
                ti = tile_r0.index(r0)
                if ti + 1 < len(tile_r0):
                    issue_loads(ti + 1)
                P.dma("sp", xt4[:, t, :], x_d[r0:r0 + 128, :], writes=[bxt[t]])
                rms_rstd(C, xt4[:, t, :], bxt[t], 1024, sq[:], bsq, ss, bss)
                P.op("dve", lambda E, t=t: E.scalar_tensor_tensor(out=hn[:], in0=xt4[:, t, :], scalar=ss[:, 0:1], in1=npre[:],
                                                                  op0=ALU.mult, op1=ALU.mult), reads=[bxt[t], bss, bnpre], writes=[bhn])
                transpose8(C, hn, bhn, idb, bidb, ptr, bptr, hT[:, :, t * 128:(t + 1) * 128], bhT, eng="act")
                ld, bld = ldo[ti % 2]
                P.op("act", lambda E, ld=ld: E.activation(out=sq[:], in_=ld[:], func=AF.Square), reads=[bld], writes=[bsq])
                P.op("dve", lambda E: E.tensor_reduce(out=ss8[:], in_=sq[:].rearrange("p (h d) -> p h d", h=8), axis=AX.X, op=ALU.add),
                     reads=[bsq], writes=[bss8])
                P.op("dve", lambda E: E.tensor_scalar(out=ss8[:], in0=ss8[:], scalar1=1.0 / 128, scalar2=1e-6, op0=ALU.mult, op1=ALU.add),
                     reads=[bss8], writes=[bss8])
                P.op("act", lambda E: E.activation(out=ss8[:], in_=ss8[:], func=AF.Sqrt), reads=[bss8], writes=[bss8])
                P.op("dve", lambda E: E.reciprocal(out=ss8[:], in_=ss8[:]), reads=[bss8], writes=[bss8])
                P.op("dve", lambda E, ld=ld: E.tensor_tensor(out=sq[:].rearrange("p (h d) -> p h d", h=8), in0=ld[:].rearrange("p (h d) -> p h d", h=8),
                                                      in1=ss8[:].unsqueeze(2).to_broadcast([128, 8, 128]), op=ALU.mult),
                     reads=[bld, bss8], writes=[bsq])
                P.op("dve", lambda E: E.tensor_tensor(out=hn[:].rearrange("p (h d) -> p h d", h=8), in0=sq[:].rearrange("p (h d) -> p h d", h=8),
                                                      in1=gnw[:].unsqueeze(1).to_broadcast([128, 8, 128]), op=ALU.mult),
                     reads=[bsq, bgnw], writes=[bhn])
                transpose8(C, hn, bhn, idb, bidb, ptr, bptr, oT[:, :, t * 128:(t + 1) * 128], boT, eng="act")
                ld, bld = ldy[ti % 2]
                P.op("act", lambda E, ld=ld: E.activation(out=hn[:], in_=ld[:], func=AF.Gelu_apprx_tanh), reads=[bld], writes=[bhn])
                transpose8(C, hn, bhn, idb, bidb, ptr, bptr, yT[:, :, t * 128:(t + 1) * 128], byT, eng="dve")
            for ct in range(16):
                pb = pmm[ct % 4]; bpb = bpmm[ct % 4]
                fns = [(lambda E, kt=kt, ct=ct, pb=pb, n=n: E.matmul(pb[:, 0:n], lhsT=wz[:, kt, ct * 128:(ct + 1) * 128], rhs=hT[:, kt, 0:n],
                                                                start=(kt == 0), stop=(kt == 7))) for kt in range(8)]
                P.mm_group(fns, reads=[bwz, bhT], writes=[bpb])
                if ct < 8:
                    P.op("act", lambda E, pb=pb, n=n: E.activation(out=sg[:, 0:n], in_=pb[:, 0:n], func=AF.Silu), reads=[bpb], writes=[bsg])
                    P.op("dve", lambda E, ct=ct, n=n: E.tensor_tensor(out=catT[:, ct, 0:n], in0=oT[:, ct, 0:n], in1=sg[:, 0:n], op=ALU.mult),
                         reads=[boT, bsg], writes=[bcat])
                else:
                    P.op("act", lambda E, pb=pb, ct=ct, n=n: E.activation(out=gz[:, ct - 8, 0:n], in_=pb[:, 0:n], func=AF.Silu), reads=[bpb], writes=[bgz])
            for ct in range(8):
                pb = pmm[ct % 4]; bpb = bpmm[ct % 4]
                fns = [(lambda E, kt=kt, ct=ct, pb=pb, n=n: E.matmul(pb[:, 0:n], lhsT=wglu[:, kt, ct * 128:(ct + 1) * 128], rhs=yT[:, kt, 0:n],
                                                                start=(kt == 0), stop=(kt == 7))) for kt in range(8)]
                P.mm_group(fns, reads=[bwglu, byT], writes=[bpb])
                P.op("act", lambda E, pb=pb, n=n: E.activation(out=sg[:, 0:n], in_=pb[:, 0:n], func=AF.Sigmoid), reads=[bpb], writes=[bsg])
                P.op("dve", lambda E, ct=ct, n=n: E.tensor_tensor(out=sg[:, 0:n], in0=sg[:, 0:n], in1=yT[:, ct, 0:n], op=ALU.mult), reads=[bsg, byT], writes=[bsg])
                P.op("dve", lambda E, ct=ct, n=n: E.tensor_tensor(out=catT[:, 8 + ct, 0:n], in0=sg[:, 0:n], in1=gz[:, ct, 0:n], op=ALU.mult),
                     reads=[bsg, bgz], writes=[bcat])
            for t in range(ntl):
                r0 = t0 + t * 128
                outproj_post(C, catT, bcat, 16, wout, bwout, t, xt4[:, t, :], bxt[t], npost, bnpost, pso, bpso, yo, byo, sq, bsq, ss, bss,
                             out_d[r0:r0 + 128, :], bout)
        if fz:
            barrier(P)
        else:
            P.finish([bout])
    return nc


def build_L3(ntok=2048, fz=None):
    nc = fz["nc"] if fz else bass.Bass("TRN2", target_bir_lowering=False)
    pfx = fz["pfx"] if fz else ""

    def D(name, shape):
        if fz and name in fz["share"]:
            return fz["share"][name]
        return nc.dram_tensor(pfx + name, shape, F32, kind="ExternalInput").ap()
    x_d = D("x", [ntok + 128, 1024])
    win_d = D("win", [1024, 8192]); wout_d = D("wout", [2048, 1024]); conv_d = D("conv", [3, 2048])
    npre_d = D("npre", [1024]); npost_d = D("npost", [1024]); ident_d = D("ident", [128, 128])
    out_d = fz["out"] if fz else nc.dram_tensor("out", [ntok, 1024], F32, kind="ExternalOutput").ap()
    NT = 256
    with ExitStack() as st:
        C = Ctx(nc, st, fz["P"], pfx) if fz else Ctx(nc, st); P = C.P
        idf, bidf, idb, bidb = make_ident(C, ident_d)
        npre, bnpre = bcast_row_load(C, "npre", npre_d, 1024)
        npost, bnpost = bcast_row_load(C, "npost", npost_d, 1024)
        cw, bcw = C.sb("cw", [128, 3, 16])
        P.dma("sp", cw[:], conv_d.rearrange("j (c p) -> p j c", p=128), writes=[bcw])
        win, bwin = load_w_bf16(C, "win", win_d, 8, 8192)
        wout, bwout = load_w_bf16(C, "wout", wout_d, 16, 1024)
        xt, bxt = C.sb("xt", [128, 1024])
        sq, bsq = C.sb("sq", [128, 1024])
        hn, bhn = C.sb("hn", [128, 1024], BF16)
        ss, bss = C.sb("ss", [128, 1])
        hT, bhT = C.sb("hT", [128, 8, NT], BF16)
        y1T, by1T = C.sb("y1T", [128, 16, NT], BF16)
        pbuf, bpbuf = C.sb("pbuf", [128, NT + 2])
        phalo, bphalo = C.sb("phalo", [128, 16, 2])
        gcs, bgcs = C.sb("gcs", [128, NT])
        cv, bcv = C.sb("cv", [128, NT])
        sz, bsz = C.sb("sz", [128, NT])
        yo, byo = C.sb("yo", [128, 1024])
        P.op("dve", lambda E: E.memset(phalo[:], 0.0), writes=[bphalo])
        ptr, bptr = C.ps("ptr", [128, 1024], BF16)
        GB = [C.ps("g%d" % i, [128, 512]) for i in range(7)]
        pso = [GB[0][0], GB[1][0]]; bpso = [GB[0][1], GB[1][1]]
        bout = fz["obuf"] if fz else Buf("out", multi=True)
        sts = [(0, 128)] + [(128 + i * NT, NT) for i in range(ntok // NT)]
        for (t0, n) in sts:
            ntl = n // 128
            for t in range(ntl):
                r0 = t0 + t * 128
                P.dma("sp", xt[:], x_d[r0:r0 + 128, :], reads=([fz["xbuf"]] if fz else []), writes=[bxt])
                rms_rstd(C, xt[:], bxt, 1024, sq[:], bsq, ss, bss)
                P.op("dve", lambda E: E.scalar_tensor_tensor(out=hn[:], in0=xt[:], scalar=ss[:, 0:1], in1=npre[:],
                                                             op0=ALU.mult, op1=ALU.mult), reads=[bxt, bss, bnpre], writes=[bhn])
                transpose8(C, hn, bhn, idb, bidb, ptr, bptr, hT[:, :, t * 128:(t + 1) * 128], bhT, eng="act")
            for ct in range(16):
                sel_ = [GB[3 * (ct % 2) + 0], GB[3 * (ct % 2) + 1], GB[3 * (ct % 2) + 2], GB[6]]
                pmm = [x_[0] for x_ in sel_]; bpmm = [x_[1] for x_ in sel_]
                for part in range(4):
                    col0 = (part * 16 + ct) * 128
                    pb = pmm[part]
                    fns = [(lambda E, n=n, kt=kt, col0=col0, pb=pb: E.matmul(pb[:, 0:n], lhsT=win[:, kt, col0:col0 + 128], rhs=hT[:, kt, 0:n],
                                                                        start=(kt == 0), stop=(kt == 7))) for kt in range(8)]
                    P.mm_group(fns, reads=[bwin, bhT], writes=[bpmm[part]])
                P.op("act", lambda E, n=n, pmm=pmm: E.copy(out=gcs[:, 0:n], in_=pmm[1][:, 0:n]), reads=[bpmm[1]], writes=[bgcs])
                P.op("act", lambda E, ct=ct: E.copy(out=pbuf[:, 0:2], in_=phalo[:, ct, :]), reads=[bphalo], writes=[bpbuf])
                P.op("dve", lambda E, n=n, pmm=pmm: E.tensor_tensor(out=pbuf[:, 2:2 + n], in0=gcs[:, 0:n], in1=pmm[2][:, 0:n], op=ALU.mult),
                     reads=[bgcs, bpmm[2]], writes=[bpbuf])
                P.op("act", lambda E, n=n, ct=ct: E.copy(out=phalo[:, ct, :], in_=pbuf[:, n:n + 2]), reads=[bpbuf], writes=[bphalo])
                if t0 == 0:
                    continue
                P.op("dve", lambda E, n=n, ct=ct: E.tensor_scalar(out=cv[:, 0:n], in0=pbuf[:, 0:n], scalar1=cw[:, 0, ct:ct + 1], scalar2=None, op0=ALU.mult),
                     reads=[bpbuf, bcw], writes=[bcv])
                P.op("dve", lambda E, n=n, ct=ct: E.scalar_tensor_tensor(out=cv[:, 0:n], in0=pbuf[:, 1:1 + n], scalar=cw[:, 1, ct:ct + 1], in1=cv[:, 0:n],
                                                                    op0=ALU.mult, op1=ALU.add), reads=[bpbuf, bcw, bcv], writes=[bcv])
                P.op("dve", lambda E, n=n, ct=ct: E.scalar_tensor_tensor(out=cv[:, 0:n], in0=pbuf[:, 2:2 + n], scalar=cw[:, 2, ct:ct + 1], in1=cv[:, 0:n],
                                                                    op0=ALU.mult, op1=ALU.add), reads=[bpbuf, bcw, bcv], writes=[bcv])
                P.op("dve", lambda E, n=n, pmm=pmm: E.tensor_tensor(out=cv[:, 0:n], in0=cv[:, 0:n], in1=pmm[0][:, 0:n], op=ALU.mult), reads=[bcv, bpmm[0]], writes=[bcv])
                P.op("act", lambda E, n=n, pmm=pmm: E.activation(out=sz[:, 0:n], in_=pmm[3][:, 0:n], func=AF.Silu), reads=[bpmm[3]], writes=[bsz])
                P.op("dve", lambda E, n=n, ct=ct: E.tensor_tensor(out=y1T[:, ct, 0:n], in0=cv[:, 0:n], in1=sz[:, 0:n], op=ALU.mult),
                     reads=[bcv, bsz], writes=[by1T])
            if t0 == 0:
                continue
            for t in range(ntl):
                r0 = t0 + t * 128
                P.dma("sp", xt[:], x_d[r0:r0 + 128, :], reads=([fz["xbuf"]] if fz else []), writes=[bxt])
                outproj_post(C, y1T, by1T, 16, wout, bwout, t, xt[:], bxt, npost, bnpost, pso, bpso, yo, byo, sq, bsq, ss, bss,
                             out_d[r0 - 128:r0, :], bout)
        if fz:
            barrier(P)
        else:
            P.finish([bout])
    return nc


_IDENT = np.eye(128, dtype=np.float32)
_CACHE = {}


def _get(name, fn):
    if name not in _CACHE:
        _CACHE[name] = fn()
    return _CACHE[name]


def run_L2(inp, o_full, ys_full):
    nc = _get("L2", build_L2)
    w_in = inp["w_in_even"][0]
    wz = np.ascontiguousarray(np.concatenate([w_in[:, 3072:4096], w_in[:, 5136:6160]], axis=1))
    maps = []
    for c in range(8):
        b, r = divmod(c, 4)
        sl = slice(r * 2048, (r + 1) * 2048)
        maps.append({"x": np.ascontiguousarray(inp["x"][b, sl]), "o": np.ascontiguousarray(o_full[b, sl]),
                     "ys": np.ascontiguousarray(ys_full[b, sl]), "wz": wz, "wglu": np.ascontiguousarray(inp["w_glu"][0]),
                     "wout": np.ascontiguousarray(inp["w_out_even"][0]), "npre": np.ascontiguousarray(inp["norm_pre"][0]),
                     "npost": np.ascontiguousarray(inp["norm_post"][0]), "gnw": np.ascontiguousarray(inp["gdn_norm_w"][0]),
                     "ident": _IDENT})
    res = run_bass_kernel_spmd(nc, maps, core_ids=list(range(8)))
    x1 = np.empty((2, 8192, 1024), np.float32)
    for c in range(8):
        b, r = divmod(c, 4)
        x1[b, r * 2048:(r + 1) * 2048] = res.results[c]["out"]
    return x1


def run_L3(inp, x1):
    nc = _get("L3", build_L3)
    maps = []
    for c in range(8):
        b, r = divmod(c, 4)
        xh = np.zeros((2048 + 128, 1024), np.float32)
        xh[128:] = x1[b, r * 2048:(r + 1) * 2048]
        if r > 0:
            xh[:128] = x1[b, r * 2048 - 128:r * 2048]
        maps.append({"x": xh, "win": np.ascontiguousarray(inp["w_in_odd"][0]), "wout": np.ascontiguousarray(inp["w_out_odd"][0]),
                     "conv": np.ascontiguousarray(inp["conv_short"][0]), "npre": np.ascontiguousarray(inp["norm_pre"][1]),
                     "npost": np.ascontiguousarray(inp["norm_post"][1]), "ident": _IDENT})
    res = run_bass_kernel_spmd(nc, maps, core_ids=list(range(8)))
    out = np.empty((2, 8192, 1024), np.float32)
    for c in range(8):
        b, r = divmod(c, 4)
        out[b, r * 2048:(r + 1) * 2048] = res.results[c]["out"]
    return out


I32 = mybir.dt.int32
TAUS = np.array(list(range(17)) + [32, 64, 128, 256, 512, 1024, 2048, 4096] + list(range(15, -1, -1)), np.float32)
NTAU = len(TAUS)


def _s5_consts():
    mk = np.zeros((128, 2, 16, 16), np.float32)
    idm = np.zeros((128, 2, 16, 16), np.float32)
    for kt2 in range(2):
        for sp in range(8):
            s = kt2 * 8 + sp
            for h in range(16):
                mk[sp * 16 + h, kt2, s:, :] = 1.0
                idm[sp * 16 + h, kt2, s, h] = 1.0
    return mk.reshape(128, 2, 256), idm.reshape(128, 2, 256)


def barrier(P):
    for e in P.ENG:
        for e2 in P.ENG:
            if P.cnt[e2] > 0:
                P._wait(e, ("c", e2, P.cnt[e2]))
        for q in P.dsem:
            j1 = P.dcnt[q]
            for j in range(max(0, j1 - NDS), j1):
                P._wait(e, ("d", q, j % NDS, 16 * (j // NDS + 1)))


def build_L1b(S=8192, fz=None):
    nc = fz["nc"] if fz else bass.Bass("TRN2", target_bir_lowering=False)
    pfx = fz["pfx"] if fz else ""

    def D(name, shape):
        if fz and name in fz["share"]:
            return fz["share"][name]
        return nc.dram_tensor(pfx + name, shape, F32, kind="ExternalInput").ap()
    x_d = D("x", [S, 1024]); npre_d = D("npre", [1024]); wu_d = D("wu", [1024, 256])
    lre_d = D("lre", [16, 64]); lim_d = D("lim", [16, 64]); bre_d = D("bre", [16, 64, 16]); bim_d = D("bim", [16, 64, 16])
    cre_d = D("cre", [16, 16, 64]); cim_d = D("cim", [16, 16, 64]); ldt_d = D("ldt", [16]); dd_d = D("dd", [256])
    taus_d = D("taus", [NTAU]); mk_d = D("mk", [128, 2, 256]); idm_d = D("idm", [128, 2, 256]); ident_d = D("ident", [128, 128])
    ys_d = fz["out"] if fz else nc.dram_tensor("ys", [S, 256], F32, kind="ExternalOutput").ap()
    NCH = S // 16
    NST = S // 512
    with ExitStack() as st:
        C = Ctx(nc, st, fz["P"], pfx) if fz else Ctx(nc, st); P = C.P
        idf, bidf, idb, bidb = make_ident(C, ident_d)
        ptr, bptr = C.ps("ptr", [128, 1024], BF16)
        py, bpy = C.ps("py", [128, 1024])
        G = []; bG = []
        for i in range(4):
            t_, b_ = C.ps("g%d" % i, [128, 512]); G.append(t_); bG.append(b_)
        U, bU = C.sb("U", [128, 2, 16, NCH], BF16)
        with ExitStack() as st2:
            C2 = Ctx(nc, st2, P, C.pfx)
            ext = fz.get("uTp") if fz else None
            if ext:
                uTp, buTp = ext
            else:
                uTp, buTp = C2.sb("uTp", [128, 2, 16, NCH], BF16)
            with ExitStack() as st1:
                C1 = Ctx(nc, st1, P, C.pfx)
                npre, bnpre = bcast_row_load(C1, "npre", npre_d, 1024)
                wu, bwu = load_w_bf16(C1, "wu", wu_d, 8, 256)
                xt, bxt = C1.sb("xt", [128, 1024])
                sq, bsq = C1.sb("sq", [128, 1024])
                hn, bhn = C1.sb("hn", [128, 1024], BF16)
                ss, bss = C1.sb("ss", [128, 1])
                hT, bhT = C1.sb("hT", [128, 8, 512], BF16)
                for s_ in range(0 if ext else NST):
                    for t in range(4):
                        r0 = s_ * 512 + t * 128
                        P.dma("sp", xt[:], x_d[r0:r0 + 128, :], writes=[bxt])
                        rms_rstd(C1, xt[:], bxt, 1024, sq[:], bsq, ss, bss)
                        P.op("dve", lambda E: E.scalar_tensor_tensor(out=hn[:], in0=xt[:], scalar=ss[:, 0:1], in1=npre[:],
                                                                     op0=ALU.mult, op1=ALU.mult), reads=[bxt, bss, bnpre], writes=[bhn])
                        transpose8(C1, hn, bhn, idb, bidb, ptr, bptr, hT[:, :, t * 128:(t + 1) * 128], bhT, eng="act")
                    for blk in range(2):
                        pb = G[blk]
                        fns = [(lambda E, kt=kt, blk=blk, pb=pb: E.matmul(
                            pb[:].rearrange("p (s n) -> p s n", s=16), lhsT=wu[:, kt, blk * 128:(blk + 1) * 128],
                            rhs=hT[:, kt, :].rearrange("p (n s) -> p s n", s=16), start=(kt == 0), stop=(kt == 7))) for kt in range(8)]
                        P.mm_group(fns, reads=[bwu, bhT], writes=[bG[blk]])
                        P.op("act" if blk == 0 else "dve",
                             (lambda E, blk=blk, pb=pb, s_=s_: E.copy(out=uTp[:, blk, :, 32 * s_:32 * s_ + 32], in_=pb[:].rearrange("p (s n) -> p s n", s=16)))
                             if blk == 0 else
                             (lambda E, blk=blk, pb=pb, s_=s_: E.tensor_copy(out=uTp[:, blk, :, 32 * s_:32 * s_ + 32], in_=pb[:].rearrange("p (s n) -> p s n", s=16))),
                             reads=[bG[blk]], writes=[buTp])
                barrier(P)
            ud2 = nc.dram_tensor(pfx + "ud2", [16, 2, 8, 16, NCH], BF16)
            bud2 = Buf("ud2", multi=True)
            bU.multi = True; bU.w = []
            for g in range(16):
                P.dma("sp", ud2.ap()[g].rearrange("k sp h n -> h (k sp) n"),
                      uTp[(g % 8) * 16:(g % 8 + 1) * 16, g // 8, :, :], reads=[buTp], writes=[bud2])
            for g in range(16):
                P.dma("sp", U[:, :, g, :], ud2.ap()[g].rearrange("k sp h n -> (sp h) k n"), reads=[bud2], writes=[bU])
            barrier(P)
        lre, blre = C.sb("lre", [128, 8]); lim, blim = C.sb("lim", [128, 8]); ldt, bldt = C.sb("ldt", [128, 8])
        TAU, bTAU = bcast_row_load(C, "TAU", taus_d, NTAU)
        Er, bEr = C.sb("Er", [128, 8, NTAU]); Ei, bEi = C.sb("Ei", [128, 8, NTAU]); NEi, bNEi = C.sb("NEi", [128, 8, NTAU])
        Hr, bHr = C.sb("Hr", [128, 8, 17, 16]); nHi, bnHi = C.sb("nHi", [128, 8, 17, 16])
        WbT, bWbT = C.sb("WbT", [128, 2, 8, 2, 128], BF16)
        Toep, bToep = C.sb("Toep", [128, 2, 16, 256], BF16)
        with ExitStack() as st3:
            C3 = Ctx(nc, st3, P, C.pfx)
            Br, bBr = C3.sb("Br", [128, 8, 16]); Bi, bBi = C3.sb("Bi", [128, 8, 16])
            Cr, bCr = C3.sb("Cr", [128, 8, 16]); Ci, bCi = C3.sb("Ci", [128, 8, 16])
            dcol, bdcol = C3.sb("dcol", [128, 16])
            MK, bMK = C3.sb("MK", [128, 2, 256]); IDM, bIDM = C3.sb("IDM", [128, 2, 256])
            P.dma("sp", MK[:], mk_d, writes=[bMK]); P.dma("sp", IDM[:], idm_d, writes=[bIDM])
            for two in range(2):
                hs = slice(64 * two, 64 * two + 64)
                P.dma("sp", lre[hs, :], lre_d.rearrange("(gp two) p -> two p gp", two=2)[two], writes=[blre])
                P.dma("sp", lim[hs, :], lim_d.rearrange("(gp two) p -> two p gp", two=2)[two], writes=[blim])
                P.dma("sp", ldt[hs, :], ldt_d.rearrange("(gp two) -> two gp", two=2)[two].partition_broadcast(64), writes=[bldt])
                P.dma("sp", Br[hs], bre_d.rearrange("(gp two) p h -> two p gp h", two=2)[two], writes=[bBr])
                P.dma("sp", Bi[hs], bim_d.rearrange("(gp two) p h -> two p gp h", two=2)[two], writes=[bBi])
                for gp in range(8):
                    P.dma("sp", Cr[hs, gp, :], cre_d[2 * gp + two].rearrange("h p -> p h"), writes=[bCr])
                    P.dma("sp", Ci[hs, gp, :], cim_d[2 * gp + two].rearrange("h p -> p h"), writes=[bCi])
            for sp in range(8):
                P.dma("sp", dcol[sp * 16:(sp + 1) * 16, :], dd_d.rearrange("(g h) -> h g", h=16), writes=[bdcol])
            sm = {}
            for nm in ("dt", "lr", "lrdt", "th", "den", "nr", "fre", "fim", "t8a", "t8b"):
                sm[nm] = C3.sb("sm_" + nm, [128, 8])
            T41 = {}
            for nm in ("ARG", "MARG", "MAG", "MAGN", "SIN", "COS", "ErN", "EiN", "rt", "rk"):
                T41[nm] = C3.sb("t41_" + nm, [128, 8, NTAU])
            rki, brki = C3.sb("rki", [128, 8, NTAU], I32)

            def tt(eng, out, bo, a, ba, b, bb_, op):
                P.op(eng, lambda E: E.tensor_tensor(out=out, in0=a, in1=b, op=op), reads=[ba, bb_], writes=[bo])

            dt, bdt = sm["dt"]; lr, blr = sm["lr"]; lrdt, blrdt = sm["lrdt"]; th, bth = sm["th"]
            P.op("act", lambda E: E.activation(out=dt[:], in_=ldt[:], func=AF.Exp), reads=[bldt], writes=[bdt])
            P.op("dve", lambda E: E.tensor_scalar(out=lr[:], in0=lre[:], scalar1=-1e-4, scalar2=None, op0=ALU.min), reads=[blre], writes=[blr])
            tt("dve", lrdt[:], blrdt, lr[:], blr, dt[:], bdt, ALU.mult)
            tt("dve", th[:], bth, lim[:], blim, dt[:], bdt, ALU.mult)
            ARG, bARG = T41["ARG"]; MARG, bMARG = T41["MARG"]; MAG, bMAG = T41["MAG"]; MAGN, bMAGN = T41["MAGN"]
            SIN, bSIN = T41["SIN"]; COS, bCOS = T41["COS"]; ErN, bErN = T41["ErN"]; EiN, bEiN = T41["EiN"]
            rt, brt = T41["rt"]; rk, brk = T41["rk"]
            tb = TAU[:].unsqueeze(1).to_broadcast([128, 8, NTAU])
            tt("dve", ARG[:], bARG, th[:].unsqueeze(2).to_broadcast([128, 8, NTAU]), bth, tb, bTAU, ALU.mult)
            tt("dve", MARG[:], bMARG, lrdt[:].unsqueeze(2).to_broadcast([128, 8, NTAU]), blrdt, tb, bTAU, ALU.mult)
            P.op("act", lambda E: E.activation(out=MAG[:], in_=MARG[:], func=AF.Exp), reads=[bMARG], writes=[bMAG])
            P.op("act", lambda E: E.activation(out=MAGN[:, :, 0:17], in_=MARG[:, :, 0:17], func=AF.Exp, scale=-1.0), reads=[bMARG], writes=[bMAGN])

            def sin_of(dst, bdst, shift):
                P.op("dve", lambda E: E.tensor_scalar(out=rt[:], in0=ARG[:], scalar1=float(shift), scalar2=None, op0=ALU.add), reads=[bARG], writes=[brt])
                P.op("dve", lambda E: E.tensor_scalar(out=rki[:], in0=rt[:], scalar1=float(1.0 / (2 * np.pi)), scalar2=None, op0=ALU.mult), reads=[brt], writes=[brki])
                P.op("dve", lambda E: E.tensor_copy(out=rk[:], in_=rki[:]), reads=[brki], writes=[brk])
                P.op("dve", lambda E: E.scalar_tensor_tensor(out=rt[:], in0=rk[:], scalar=float(-2 * np.pi), in1=rt[:], op0=ALU.mult, op1=ALU.add),
                     reads=[brk, brt], writes=[brt])
                P.op("dve", lambda E: E.tensor_scalar(out=rt[:], in0=rt[:], scalar1=-3.14159, scalar2=3.14159, op0=ALU.max, op1=ALU.min), reads=[brt], writes=[brt])
                P.op("act", lambda E: E.activation(out=dst[:], in_=rt[:], func=AF.Sin), reads=[brt], writes=[bdst])

            sin_of(SIN, bSIN, 0.0)
            sin_of(COS, bCOS, np.pi / 2)
            tt("dve", Er[:], bEr, MAG[:], bMAG, COS[:], bCOS, ALU.mult)
            tt("dve", Ei[:], bEi, MAG[:], bMAG, SIN[:], bSIN, ALU.mult)
            P.op("dve", lambda E: E.tensor_scalar(out=NEi[:], in0=Ei[:], scalar1=-1.0, scalar2=None, op0=ALU.mult), reads=[bEi], writes=[bNEi])
            tt("dve", ErN[:, :, 0:17], bErN, MAGN[:, :, 0:17], bMAGN, COS[:, :, 0:17], bCOS, ALU.mult)
            tt("dve", EiN[:, :, 0:17], bEiN, MAGN[:, :, 0:17], bMAGN, SIN[:, :, 0:17], bSIN, ALU.mult)
            P.op("dve", lambda E: E.tensor_scalar(out=EiN[:, :, 0:17], in0=EiN[:, :, 0:17], scalar1=-1.0, scalar2=None, op0=ALU.mult), reads=[bEiN], writes=[bEiN])
            den, bden = sm["den"]; nr, bnr = sm["nr"]; fre, bfre = sm["fre"]; fim, bfim = sm["fim"]; t8a, bt8a = sm["t8a"]; t8b, bt8b = sm["t8b"]
            tt("dve", den[:], bden, lr[:], blr, lr[:], blr, ALU.mult)
            tt("dve", t8a[:], bt8a, lim[:], blim, lim[:], blim, ALU.mult)
            tt("dve", den[:], bden, den[:], bden, t8a[:], bt8a, ALU.add)
            P.op("dve", lambda E: E.reciprocal(out=den[:], in_=den[:]), reads=[bden], writes=[bden])
            P.op("dve", lambda E: E.tensor_scalar(out=nr[:], in0=Er[:, :, 1], scalar1=-1.0, scalar2=None, op0=ALU.add), reads=[bEr], writes=[bnr])
            tt("dve", fre[:], bfre, nr[:], bnr, lr[:], blr, ALU.mult)
            tt("dve", t8a[:], bt8a, Ei[:, :, 1], bEi, lim[:], blim, ALU.mult)
            tt("dve", fre[:], bfre, fre[:], bfre, t8a[:], bt8a, ALU.add)
            tt("dve", fre[:], bfre, fre[:], bfre, den[:], bden, ALU.mult)
            tt("dve", fim[:], bfim, Ei[:, :, 1], bEi, lr[:], blr, ALU.mult)
            tt("dve", t8b[:], bt8b, nr[:], bnr, lim[:], blim, ALU.mult)
            tt("dve", fim[:], bfim, fim[:], bfim, t8b[:], bt8b, ALU.subtract)
            tt("dve", fim[:], bfim, fim[:], bfim, den[:], bden, ALU.mult)

            def cmul(outr, boutr, outi, bouti, ar, bar, ai, bai, br_, bbr_, bi_, bbi_, tmp, btmp):
                tt("dve", outr, boutr, ar, bar, br_, bbr_, ALU.mult)
                tt("dve", tmp, btmp, ai, bai, bi_, bbi_, ALU.mult)
                tt("dve", outr, boutr, outr, boutr, tmp, btmp, ALU.subtract)
                tt("dve", outi, bouti, ar, bar, bi_, bbi_, ALU.mult)
                tt("dve", tmp, btmp, ai, bai, br_, bbr_, ALU.mult)
                tt("dve", outi, bouti, outi, bouti, tmp, btmp, ALU.add)

            bbr, bbbr = C3.sb("bbr", [128, 8, 16]); bbi, bbbi = C3.sb("bbi", [128, 8, 16]); tmp16, btmp16 = C3.sb("tmp16", [128, 8, 16])
            fb = lambda t_: t_[:].unsqueeze(2).to_broadcast([128, 8, 16])
            cmul(bbr[:], bbbr, bbi[:], bbbi, fb(fre), bfre, fb(fim), bfim, Br[:], bBr, Bi[:], bBi, tmp16[:], btmp16)
            Gr, bGr = C3.sb("Gr", [128, 8, 16, 16]); Gi, bGi = C3.sb("Gi", [128, 8, 16, 16])
            WPr, bWPr = C3.sb("WPr", [128, 8, 16, 16]); WPi, bWPi = C3.sb("WPi", [128, 8, 16, 16])
            Hi, bHi = C3.sb("Hi", [128, 8, 17, 16]); tmpH, btmpH = C3.sb("tmpH", [128, 8, 17, 16])
            eb = lambda t_, j0, j1: t_[:, :, j0:j1].unsqueeze(3).to_broadcast([128, 8, j1 - j0, 16])
            vb = lambda t_, n_: t_[:].unsqueeze(2).to_broadcast([128, 8, n_, 16])
            cmul(Gr[:], bGr, Gi[:], bGi, eb(ErN, 0, 16), bErN, eb(EiN, 0, 16), bEiN, vb(bbr, 16), bbbr, vb(bbi, 16), bbbi, tmpH[:, :, 0:16, :], btmpH)
            cmul(WPr[:], bWPr, WPi[:], bWPi, eb(Er, 25, 41), bEr, eb(Ei, 25, 41), bEi, vb(bbr, 16), bbbr, vb(bbi, 16), bbbi, tmpH[:, :, 0:16, :], btmpH)
            cmul(Hr[:], bHr, Hi[:], bHi, eb(Er, 0, 17), bEr, eb(Ei, 0, 17), bEi, vb(Cr, 17), bCr, vb(Ci, 17), bCi, tmpH[:], btmpH)
            P.op("dve", lambda E: E.tensor_scalar(out=nHi[:], in0=Hi[:], scalar1=-1.0, scalar2=None, op0=ALU.mult), reads=[bHi], writes=[bnHi])
            for gp in range(8):
                for kt2 in range(2):
                    for c, (WP_, bWP_) in enumerate(((WPr, bWPr), (WPi, bWPi))):
                        P.op("pe", lambda E, gp=gp, kt2=kt2, WP_=WP_: E.transpose(
                            out=G[2][:, 0:128], in_=WP_[:, gp, kt2 * 8:(kt2 + 1) * 8, :].rearrange("p s h -> p (s h)"), identity=idf[:]),
                            reads=[bWP_, bidf], writes=[bG[2]])
                        P.op("act", lambda E, gp=gp, kt2=kt2, c=c: E.copy(out=WbT[:, kt2, gp, c, :], in_=G[2][:, 0:128]), reads=[bG[2]], writes=[bWbT])
            tmpT, btmpT = C3.sb("tmpT", [128, 256])
            for g in range(16):
                gp = g // 2; hs = slice(64 * (g % 2), 64 * (g % 2) + 64)
                for kt2 in range(2):
                    fns = [
                        lambda E, gp=gp, hs=hs, kt2=kt2: E.matmul(G[3][:, 0:256], lhsT=Gr[hs, gp, kt2 * 8:(kt2 + 1) * 8, :].rearrange("p s h -> p (s h)"),
                                                                  rhs=Hr[hs, gp, 0:16, :].rearrange("p t h -> p (t h)"), start=True, stop=False),
                        lambda E, gp=gp, hs=hs, kt2=kt2: E.matmul(G[3][:, 0:256], lhsT=Gi[hs, gp, kt2 * 8:(kt2 + 1) * 8, :].rearrange("p s h -> p (s h)"),
                                                                  rhs=nHi[hs, gp, 0:16, :].rearrange("p t h -> p (t h)"), start=False, stop=True)]
                    P.mm_group(fns, reads=[bGr, bGi, bHr, bnHi], writes=[bG[3]])
                    P.op("dve", lambda E, kt2=kt2: E.tensor_tensor(out=tmpT[:], in0=G[3][:, 0:256], in1=MK[:, kt2, :], op=ALU.mult),
                         reads=[bG[3], bMK], writes=[btmpT])
                    P.op("dve", lambda E, kt2=kt2, g=g: E.scalar_tensor_tensor(out=Toep[:, kt2, g, :], in0=IDM[:, kt2, :], scalar=dcol[:, g:g + 1], in1=tmpT[:],
                                                                               op0=ALU.mult, op1=ALU.add), reads=[bIDM, bdcol, btmpT], writes=[bToep])
            barrier(P)
        X = {}
        for bufn in ("A", "B"):
            for c in ("re", "im"):
                X[(bufn, c)] = (C.sb("X%s%s" % (bufn, c), [128, 8, NCH + 1])[0], [Buf("X%s%s%d" % (bufn, c, gp)) for gp in range(8)])
        Ysb, bYsb = C.sb("Ysb", [128, 16, 256])
        for key in X:
            t_, bl = X[key]
            P.op("dve", lambda E, t_=t_: E.memset(t_[:, :, 0:1], 0.0), writes=bl)
        for gp in range(8):
            for c, cn in enumerate(("re", "im")):
                px = G[c]
                fns = []
                for two in range(2):
                    g = 2 * gp + two
                    for kt2 in range(2):
                        fns.append(lambda E, two=two, g=g, kt2=kt2, gp=gp, c=c, px=px: E.matmul(
                            px[64 * two:64 * two + 64, :], lhsT=WbT[:, kt2, gp, c, 64 * two:64 * two + 64], rhs=U[:, kt2, g, :],
                            start=(kt2 == 0), stop=(kt2 == 1)))
                P.mm_group(fns, reads=[bWbT, bU], writes=[bG[c]])
                xt_, xb_ = X[("A", cn)]
                P.op("act", lambda E, xt_=xt_, gp=gp, px=px: E.copy(out=xt_[:, gp, 1:NCH + 1], in_=px[:]), reads=[bG[c]], writes=[xb_[gp]])
        for k in range(9):
            d = 1 << k
            j = 16 if k == 0 else 16 + k
            src, dst = ("A", "B") if k % 2 == 0 else ("B", "A")
            sre, bsre = X[(src, "re")]; sim, bsim = X[(src, "im")]
            dre, bdre = X[(dst, "re")]; dim_, bdim = X[(dst, "im")]
            P.op("dve", lambda E, dre=dre, sre=sre, d=d: E.tensor_copy(out=dre[:, :, 1:1 + d], in_=sre[:, :, 1:1 + d]), reads=bsre, writes=bdre)
            P.op("pool", lambda E, dim_=dim_, sim=sim, d=d: E.tensor_copy(out=dim_[:, :, 1:1 + d], in_=sim[:, :, 1:1 + d]), reads=bsim, writes=bdim)
            for gp in range(8):
                lo = slice(1, NCH + 1 - d); hi = slice(1 + d, NCH + 1)
                P.op("dve", lambda E, gp=gp, j=j, dre=dre, sre=sre, lo=lo, hi=hi: E.scalar_tensor_tensor(
                    out=dre[:, gp, hi], in0=sre[:, gp, lo], scalar=Er[:, gp, j:j + 1], in1=sre[:, gp, hi], op0=ALU.mult, op1=ALU.add),
                    reads=[bsre[gp], bEr], writes=[bdre[gp]])
                P.op("dve", lambda E, gp=gp, j=j, dre=dre, sim=sim, lo=lo, hi=hi: E.scalar_tensor_tensor(
                    out=dre[:, gp, hi], in0=sim[:, gp, lo], scalar=NEi[:, gp, j:j + 1], in1=dre[:, gp, hi], op0=ALU.mult, op1=ALU.add),
                    reads=[bsim[gp], bNEi, bdre[gp]], writes=[bdre[gp]])
                P.op("dve", lambda E, gp=gp, j=j, dim_=dim_, sim=sim, lo=lo, hi=hi: E.scalar_tensor_tensor(
                    out=dim_[:, gp, hi], in0=sim[:, gp, lo], scalar=Er[:, gp, j:j + 1], in1=sim[:, gp, hi], op0=ALU.mult, op1=ALU.add),
                    reads=[bsim[gp], bEr], writes=[bdim[gp]])
                P.op("dve", lambda E, gp=gp, j=j, dim_=dim_, sre=sre, lo=lo, hi=hi: E.scalar_tensor_tensor(
                    out=dim_[:, gp, hi], in0=sre[:, gp, lo], scalar=Ei[:, gp, j:j + 1], in1=dim_[:, gp, hi], op0=ALU.mult, op1=ALU.add),
                    reads=[bsre[gp], bEi, bdim[gp]], writes=[bdim[gp]])
        fre_, bfre_ = X[("B", "re")]; fim_, bfim_ = X[("B", "im")]
        bys = None if fz else Buf("ys", multi=True)
        ysv = ys_d.rearrange("(n t) c -> n t c", t=16)
        for jt in range(NCH // 128):
            for gq in range(4):
                fns = []
                for gi in range(4):
                    g = 4 * gq + gi; gp = g // 2; hs = slice(64 * (g % 2), 64 * (g % 2) + 64)
                    o_ = (gi * 256, (gi + 1) * 256)
                    for kt2 in range(2):
                        fns.append(lambda E, o_=o_, g=g, kt2=kt2, jt=jt: E.matmul(
                            py[:, o_[0]:o_[1]], lhsT=U[:, kt2, g, jt * 128:(jt + 1) * 128], rhs=Toep[:, kt2, g, :], start=(kt2 == 0), stop=False))
                    fns.append(lambda E, o_=o_, gp=gp, hs=hs, jt=jt: E.matmul(
                        py[:, o_[0]:o_[1]], lhsT=fre_[hs, gp, jt * 128:(jt + 1) * 128], rhs=Hr[hs, gp, 1:17, :].rearrange("p t h -> p (t h)"),
                        start=False, stop=False))
                    fns.append(lambda E, o_=o_, gp=gp, hs=hs, jt=jt: E.matmul(
                        py[:, o_[0]:o_[1]], lhsT=fim_[hs, gp, jt * 128:(jt + 1) * 128], rhs=nHi[hs, gp, 1:17, :].rearrange("p t h -> p (t h)"),
                        start=False, stop=True))
                P.mm_group(fns, reads=[bU, bToep, bHr, bnHi] + bfre_ + bfim_, writes=[bpy])
                P.op("act" if gq % 2 == 0 else "dve",
                     (lambda E, gq=gq: E.copy(out=Ysb[:].rearrange("p t (g h) -> p g t h", h=16)[:, 4 * gq:4 * gq + 4],
                                              in_=py[:].rearrange("p (g t h) -> p g t h", g=4, h=16)))
                     if gq % 2 == 0 else
                     (lambda E, gq=gq: E.tensor_copy(out=Ysb[:].rearrange("p t (g h) -> p g t h", h=16)[:, 4 * gq:4 * gq + 4],
                                                     in_=py[:].rearrange("p (g t h) -> p g t h", g=4, h=16))),
                     reads=[bpy], writes=[bYsb])
            P.dma("sp", ysv[jt * 128:(jt + 1) * 128, :, :], Ysb[:], reads=[bYsb], writes=[fz["obuf_of"](jt) if fz else bys])
            if fz:
                fz["after_chunk"](jt)
        if fz:
            barrier(P)
        else:
            P.finish([bys])
    return nc


def run_L1b(inp):
    nc = _get("L1b", build_L1b)
    mk, idm = _s5_consts()
    w_in = inp["w_in_even"][0]
    maps = []
    for c in range(8):
        b, r = divmod(c, 4)
        gs = slice(16 * r, 16 * r + 16)
        maps.append({"x": np.ascontiguousarray(inp["x"][b]), "npre": np.ascontiguousarray(inp["norm_pre"][0]),
                     "wu": np.ascontiguousarray(w_in[:, 4112 + 256 * r:4112 + 256 * (r + 1)]),
                     "lre": np.ascontiguousarray(inp["s5_lam_re"][0, gs]), "lim": np.ascontiguousarray(inp["s5_lam_im"][0, gs]),
                     "bre": np.ascontiguousarray(inp["s5_b_re"][0, gs]), "bim": np.ascontiguousarray(inp["s5_b_im"][0, gs]),
                     "cre": np.ascontiguousarray(inp["s5_c_re"][0, gs]), "cim": np.ascontiguousarray(inp["s5_c_im"][0, gs]),
                     "ldt": np.ascontiguousarray(inp["s5_log_dt"][0, gs]), "dd": np.ascontiguousarray(inp["s5_d"][0, 256 * r:256 * (r + 1)]),
                     "taus": TAUS, "mk": mk, "idm": idm, "ident": _IDENT})
    res = run_bass_kernel_spmd(nc, maps, core_ids=list(range(8)))
    ys = np.empty((2, 8192, 1024), np.float32)
    for c in range(8):
        b, r = divmod(c, 4)
        ys[b, :, 256 * r:256 * (r + 1)] = res.results[c]["ys"]
    return ys


def _gdn_consts():
    p = np.arange(64)[:, None]; f = np.arange(64)[None, :]
    negu = np.where(f >= p, 0.0, -30000.0)
    negls = np.where(f < p, 0.0, -30000.0)
    nsu = np.where(f > p, -1.0, 0.0)
    i64 = np.eye(64)
    c64 = np.stack([negu, negls, nsu, i64], axis=1).astype(np.float32)
    cmask = np.ones((2, 512), np.float32); cmask[:, 0::64] = 0.0
    sel = np.zeros((2, 2, 128), np.float32); sel[0, 0, :] = 1.0; sel[1, 1, :] = 1.0
    return c64, cmask, sel


def build_L1a(S=8192, fz=None):
    nc = fz["nc"] if fz else bass.Bass("TRN2", target_bir_lowering=False)
    pfx = fz["pfx"] if fz else ""

    def D(name, shape):
        if fz and name in fz["share"]:
            return fz["share"][name]
        return nc.dram_tensor(pfx + name, shape, F32, kind="ExternalInput").ap()
    x_d = D("x", [S, 1024]); npre_d = D("npre", [1024]); w_d = D("w", [1024, 768]); wb_d = D("wb", [1024, 2]); wa_d = D("wa", [1024, 2])
    conv_d = D("conv", [4, 768]); alog_d = D("alog", [2]); dtb_d = D("dtb", [2])
    ident_d = D("ident", [128, 128]); c64_d = D("c64", [64, 4, 64]); cmask_d = D("cmask", [2, 512]); sel_d = D("sel", [2, 2, 128])
    ones_d = D("ones", [128, 128])
    o_d = fz["out"] if fz else nc.dram_tensor("o", [S, 256], F32, kind="ExternalOutput").ap()
    NST = S // 512
    with ExitStack() as st:
        C = Ctx(nc, st, fz["P"], pfx) if fz else Ctx(nc, st); P = C.P
        idf, bidf, idb, bidb = make_ident(C, ident_d)
        npre, bnpre = bcast_row_load(C, "npre", npre_d, 1024)
        w, bw = load_w_bf16(C, "w", w_d, 8, 768)
        wb, bwb = load_w_bf16(C, "wb", wb_d, 8, 2)
        wa, bwa = load_w_bf16(C, "wa", wa_d, 8, 2)
        cw, bcw = C.sb("cw", [128, 4, 6])
        P.dma("sp", cw[:], conv_d.rearrange("j (c p) -> p j c", p=128), writes=[bcw])
        extu = fz.get("uTp") if fz else None
        if extu:
            wu_d = D("wu", [1024, 256])
            wu, bwu = load_w_bf16(C, "wu", wu_d, 8, 256)
            uTp, buTp = extu
        c64, bc64 = C.sb("c64", [64, 4, 64]); P.dma("sp", c64[:], c64_d, writes=[bc64])
        NEGU = c64[:, 0, :]; NEGLS = c64[:, 1, :]; NSU = c64[:, 2, :]; I64 = c64[:, 3, :]
        cmask, bcmask = C.sb("cmask", [2, 512]); P.dma("sp", cmask[:], cmask_d, writes=[bcmask])
        sel, bsel = C.sb("sel", [2, 2, 128]); P.dma("sp", sel[:], sel_d, writes=[bsel])
        ones, bones = C.sb("ones", [128, 128]); P.dma("sp", ones[:], ones_d, writes=[bones])
        onesb, bonesb = C.sb("onesb", [128, 128], BF16)
        P.op("dve", lambda E: E.tensor_copy(out=onesb[:], in_=ones[:]), reads=[bones], writes=[bonesb])
        sqb, bsqb = C.sb("sqb", [128, 512], BF16)
        alog, balog = C.sb("alog", [2, 1]); P.dma("sp", alog[:], alog_d.rearrange("(a b) -> a b", b=1), writes=[balog])
        dtb, bdtb = C.sb("dtb", [2, 1]); P.dma("sp", dtb[:], dtb_d.rearrange("(a b) -> a b", b=1), writes=[bdtb])
        negA, bnegA = C.sb("negA", [2, 1])
        P.op("act", lambda E: E.activation(out=negA[:], in_=alog[:], func=AF.Exp), reads=[balog], writes=[bnegA])
        P.op("dve", lambda E: E.tensor_scalar(out=negA[:], in0=negA[:], scalar1=-1.0, scalar2=None, op0=ALU.mult), reads=[bnegA], writes=[bnegA])
        xt, bxt = C.sb("xt", [128, 1024]); sq, bsq = C.sb("sq", [128, 1024]); hn, bhn = C.sb("hn", [128, 1024], BF16)
        ss, bss = C.sb("ss", [128, 1]); hT, bhT = C.sb("hT", [128, 8, 512], BF16)
        raw, _ = C.sb("raw", [128, 6, 515]); braw = [Buf("raw%d" % i) for i in range(6)]
        cvq, bcvq = C.sb("cvq", [128, 512])
        act, _ = C.sb("act", [128, 4, 512]); bact = [Buf("act%d" % i) for i in range(4)]
        vbuf2 = []; qk2 = []; bqk2 = []
        for par_ in range(2):
            vt_, _ = C.sb("vbuf%d" % par_, [128, 2, 512]); vbuf2.append((vt_, [Buf("vb%d_%d" % (par_, i)) for i in range(2)]))
            qt_, _ = C.sb("qk%d" % par_, [128, 4, 512]); qk2.append(qt_); bqk2.append([Buf("qk%d_%d" % (par_, i)) for i in range(4)])
        rn, brn = C.sb("rn", [128, 512])
        brow, bbrow = C.sb("brow", [2, 512]); grow, bgrow = C.sb("grow", [2, 512]); gcrow, bgcrow = C.sb("gcrow", [2, 512])
        GCB2 = []; BB2 = []
        for par_ in range(2):
            GCB2.append([C.sb("GCB%d_%d" % (par_, h), [128, 512]) for h in range(2)])
            BB2.append([C.sb("BB%d_%d" % (par_, h), [128, 512]) for h in range(2)])
        m64 = {}
        for nm in ("arg1", "DT", "Ds", "tmp", "tmp2", "BBm"):
            m64[nm] = C.sb("m_" + nm, [64, 512])
        for nm in ("Pa", "Pb", "Qa", "Qb"):
            m64[nm] = C.sb("m_" + nm, [64, 512], BF16)
        heads = []
        for h in range(2):
            H = {}
            H["attnT"] = C.sb("attnT%d" % h, [64, 512], BF16); H["Y"] = C.sb("Y%d" % h, [64, 512]); H["Ybf"] = C.sb("Ybf%d" % h, [64, 512], BF16)
            H["EG"] = C.sb("EG%d" % h, [128, 512]); H["qdec"] = C.sb("qdec%d" % h, [128, 512], BF16)
            H["kTb"] = C.sb("kTb%d" % h, [128, 512], BF16); H["Sbf"] = C.sb("Sbf%d" % h, [128, 128], BF16)
            H["bv"] = C.sb("bv%d" % h, [64, 8, 128]); H["kdec"] = C.sb("kdec%d" % h, [64, 8, 128], BF16)
            H["nbg"] = C.sb("nbg%d" % h, [64, 8]); H["osb"] = C.sb("osb%d" % h, [128, 8, 128])
            H["vnew"] = C.sb("vnew%d" % h, [64, 128], BF16); H["rhs2"] = C.sb("rhs2%d" % h, [64, 128], BF16)
            heads.append(H)
        small = {}
        for nm in ("gccol", "bcol", "nbcol", "elast", "egc"):
            small[nm] = C.sb("s_" + nm, [64, 8])
        Sst = [C.sb("S%d" % h, [128, 128]) for h in range(2)]
        for h in range(2):
            P.op("dve", lambda E, h=h: E.memset(Sst[h][0][:], 0.0), writes=[Sst[h][1]])
            P.op("dve", lambda E, h=h: E.memset(heads[h]["Sbf"][0][:], 0.0), writes=[heads[h]["Sbf"][1]])
        P.op("dve", lambda E: E.memset(raw[:, :, 0:3], 0.0), writes=braw)
        ptr, bptr = C.ps("ptr", [128, 1024], BF16)
        G = [C.ps("gp%d" % i, [128, 512]) for i in range(7)]
        GP = G[0:4]
        GA = G[4:7]
        ga_ctr = [0]

        def next_ga():
            ga_ctr[0] += 1
            return GA[ga_ctr[0] % 3]
        bo = None if fz else Buf("o", multi=True)
        if fz is not None and fz.get("debug"):
            print("L1a sbuf remaining", nc.sbuf_bytes_remaining)

        def tt(out, bo_, a, ba, b, bb_, op, eng="dve"):
            P.op(eng, lambda E: E.tensor_tensor(out=out, in0=a, in1=b, op=op), reads=ba if isinstance(ba, list) else [ba], writes=[bo_])

        def stageA(s_):
            par = s_ % 2
            qk = qk2[par]; bqk = bqk2[par]; GCB = GCB2[par]; BB = BB2[par]; vb, bvb = vbuf2[par]
            for t in range(4):
                r0 = s_ * 512 + t * 128
                P.dma("sp", xt[:], x_d[r0:r0 + 128, :], writes=[bxt])
                rms_rstd(C, xt[:], bxt, 1024, sq[:], bsq, ss, bss)
                P.op("dve", lambda E: E.scalar_tensor_tensor(out=hn[:], in0=xt[:], scalar=ss[:, 0:1], in1=npre[:],
                                                             op0=ALU.mult, op1=ALU.mult), reads=[bxt, bss, bnpre], writes=[bhn])
                transpose8(C, hn, bhn, idb, bidb, ptr, bptr, hT[:, :, t * 128:(t + 1) * 128], bhT, eng="act")
                yield
            for ct in range(6):
                pa, bpa = next_ga()
                fns = [(lambda E, kt=kt, ct=ct, pa=pa: E.matmul(pa[:], lhsT=w[:, kt, ct * 128:(ct + 1) * 128], rhs=hT[:, kt, :],
                                                                start=(kt == 0), stop=(kt == 7))) for kt in range(8)]
                P.mm_group(fns, reads=[bw, bhT], writes=[bpa])
                P.op("act", lambda E, ct=ct, pa=pa: E.copy(out=raw[:, ct, 3:515], in_=pa[:]), reads=[bpa], writes=[braw[ct]])
                P.op("dve", lambda E, ct=ct: E.tensor_scalar(out=cvq[:], in0=raw[:, ct, 0:512], scalar1=cw[:, 0, ct:ct + 1], scalar2=None, op0=ALU.mult),
                     reads=[braw[ct], bcw], writes=[bcvq])
                for j in range(1, 4):
                    P.op("dve", lambda E, ct=ct, j=j: E.scalar_tensor_tensor(out=cvq[:], in0=raw[:, ct, j:j + 512], scalar=cw[:, j, ct:ct + 1], in1=cvq[:],
                                                                             op0=ALU.mult, op1=ALU.add), reads=[braw[ct], bcw, bcvq], writes=[bcvq])
                P.op("act", lambda E, ct=ct: E.copy(out=raw[:, ct, 0:3], in_=raw[:, ct, 512:515]), reads=[braw[ct]], writes=[braw[ct]])
                if ct < 4:
                    P.op("act", lambda E, ct=ct: E.activation(out=act[:, ct, :], in_=cvq[:], func=AF.Silu), reads=[bcvq], writes=[bact[ct]])
                else:
                    P.op("act", lambda E, ct=ct, vb=vb: E.activation(out=vb[:, ct - 4, :], in_=cvq[:], func=AF.Silu), reads=[bcvq], writes=[bvb[ct - 4]])
                yield
            if extu:
                for blk in range(2):
                    pa, bpa = next_ga()
                    fns = [(lambda E, kt=kt, blk=blk, pa=pa: E.matmul(
                        pa[:].rearrange("p (s n) -> p s n", s=16), lhsT=wu[:, kt, blk * 128:(blk + 1) * 128],
                        rhs=hT[:, kt, :].rearrange("p (n s) -> p s n", s=16), start=(kt == 0), stop=(kt == 7))) for kt in range(8)]
                    P.mm_group(fns, reads=[bwu, bhT], writes=[bpa])
                    P.op("act", lambda E, blk=blk, pa=pa, s_=s_: E.copy(out=uTp[:, blk, :, 32 * s_:32 * s_ + 32], in_=pa[:].rearrange("p (s n) -> p s n", s=16)),
                         reads=[bpa], writes=[buTp])
                    yield
            for ct in range(4):
                pa, bpa = next_ga()
                P.op("act", lambda E, ct=ct: E.activation(out=sqb[:], in_=act[:, ct, :], func=AF.Square), reads=[bact[ct]], writes=[bsqb])
                P.op("pe", lambda E, pa=pa: E.matmul(pa[:], lhsT=onesb[:], rhs=sqb[:], start=True, stop=True), reads=[bonesb, bsqb], writes=[bpa])
                P.op("act", lambda E, pa=pa: E.activation(out=rn[:], in_=pa[:], func=AF.Ln, bias=1e-6, scale=1.0), reads=[bpa], writes=[brn])
                P.op("act", lambda E: E.activation(out=rn[:], in_=rn[:], func=AF.Exp, scale=-0.5), reads=[brn], writes=[brn])
                if ct < 2:
                    P.op("dve", lambda E, ct=ct, qk=qk: E.scalar_tensor_tensor(out=qk[:, ct, :], in0=act[:, ct, :], scalar=float(128 ** -0.5), in1=rn[:],
                                                                               op0=ALU.mult, op1=ALU.mult), reads=[bact[ct], brn], writes=[bqk[ct]])
                else:
                    P.op("dve", lambda E, ct=ct, qk=qk: E.tensor_tensor(out=qk[:, ct, :], in0=act[:, ct, :], in1=rn[:], op=ALU.mult),
                         reads=[bact[ct], brn], writes=[bqk[ct]])
                yield
            pa, bpa = next_ga()
            fns = [(lambda E, kt=kt, pa=pa: E.matmul(pa[0:2, :], lhsT=wb[:, kt, 0:2], rhs=hT[:, kt, :], start=(kt == 0), stop=(kt == 7))) for kt in range(8)]
            P.mm_group(fns, reads=[bwb, bhT], writes=[bpa])
            P.op("act", lambda E, pa=pa: E.activation(out=brow[:], in_=pa[0:2, :], func=AF.Sigmoid), reads=[bpa], writes=[bbrow])
            pa2, bpa2 = next_ga()
            fns = [(lambda E, kt=kt, pa2=pa2: E.matmul(pa2[0:2, :], lhsT=wa[:, kt, 0:2], rhs=hT[:, kt, :], start=(kt == 0), stop=(kt == 7))) for kt in range(8)]
            P.mm_group(fns, reads=[bwa, bhT], writes=[bpa2])
            P.op("act", lambda E, pa2=pa2: E.activation(out=grow[:], in_=pa2[0:2, :], func=AF.Exp, bias=dtb[:, 0:1], scale=1.0), reads=[bpa2, bdtb], writes=[bgrow])
            P.op("act", lambda E: E.activation(out=grow[:], in_=grow[:], func=AF.Ln, bias=1.0, scale=1.0), reads=[bgrow], writes=[bgrow])
            P.op("dve", lambda E: E.tensor_scalar(out=grow[:], in0=grow[:], scalar1=negA[:, 0:1], scalar2=None, op0=ALU.mult), reads=[bgrow, bnegA], writes=[bgrow])
            P.op("dve", lambda E: E.tensor_tensor_scan(out=gcrow[:], data0=cmask[:], data1=grow[:], initial=0.0, op0=ALU.mult, op1=ALU.add),
                 reads=[bcmask, bgrow], writes=[bgcrow])
            yield
            for h in range(2):
                pa, bpa = next_ga()
                P.op("pe", lambda E, h=h, pa=pa: E.matmul(pa[:], lhsT=sel[:, h, :], rhs=gcrow[:], start=True, stop=True), reads=[bsel, bgcrow], writes=[bpa])
                P.op("act", lambda E, h=h, pa=pa, GCB=GCB: E.copy(out=GCB[h][0][:], in_=pa[:]), reads=[bpa], writes=[GCB[h][1]])
                pa, bpa = next_ga()
                P.op("pe", lambda E, h=h, pa=pa: E.matmul(pa[:], lhsT=sel[:, h, :], rhs=brow[:], start=True, stop=True), reads=[bsel, bbrow], writes=[bpa])
                P.op("act", lambda E, h=h, pa=pa, BB=BB: E.copy(out=BB[h][0][:], in_=pa[:]), reads=[bpa], writes=[BB[h][1]])
                yield

        for _ in stageA(0):
            pass
        for s_ in range(NST):
            par = s_ % 2
            qk = qk2[par]; bqk = bqk2[par]; GCB = GCB2[par]; BB = BB2[par]; vb, bvb = vbuf2[par]
            nxt = stageA(s_ + 1) if s_ + 1 < NST else None

            def advance(k):
                if nxt is not None:
                    for _ in range(k):
                        next(nxt, None)
            for h in range(2):
                qT = qk[:, h, :]; bqT = bqk[h]; kT = qk[:, 2 + h, :]; bkT = bqk[2 + h]; vT = vb[:, h, :]; bvT = bvb[h]
                gcb, bgcb = GCB[h]; bb, bbb = BB[h]
                H = heads[h]
                attnT, battnT = H["attnT"]; Y, bY = H["Y"]; EG, bEG = H["EG"]; qdec, bqdec = H["qdec"]
                Ybf, bYbf = H["Ybf"]
                bv, bbv = H["bv"]; kdec, bkdec = H["kdec"]; nbg, bnbg = H["nbg"]
                arg1, barg1 = m64["arg1"]; DT, bDT = m64["DT"]; Ds, bDs = m64["Ds"]
                tmp, btmp = m64["tmp"]; tmp2, btmp2 = m64["tmp2"]; BBm, bBBm = m64["BBm"]
                gccol, bgccol = small["gccol"]; bcol, bbcol = small["bcol"]; nbcol, bnbcol = small["nbcol"]
                elast, belast = small["elast"]; egc, begc = small["egc"]
                v3 = lambda t_: t_[:].rearrange("p (n f) -> p n f", f=64)
                i64b = I64.unsqueeze(1).to_broadcast([64, 8, 64])
                tt(v3(tmp), btmp, gcb[0:64, :].rearrange("p (n f) -> p n f", f=64), [bgcb, bc64], i64b, bc64, ALU.mult)
                P.op("dve", lambda E, tmp=tmp, gccol=gccol: E.tensor_reduce(out=gccol[:], in_=tmp[:].rearrange("p (n f) -> p n f", f=64), axis=AX.X, op=ALU.add), reads=[btmp], writes=[bgccol])
                tt(v3(tmp), btmp, bb[0:64, :].rearrange("p (n f) -> p n f", f=64), [bbb, bc64], i64b, bc64, ALU.mult)
                P.op("dve", lambda E, tmp=tmp, bcol=bcol: E.tensor_reduce(out=bcol[:], in_=tmp[:].rearrange("p (n f) -> p n f", f=64), axis=AX.X, op=ALU.add), reads=[btmp], writes=[bbcol])
                P.op("dve", lambda E: E.tensor_scalar(out=nbcol[:], in0=bcol[:], scalar1=-1.0, scalar2=None, op0=ALU.mult), reads=[bbcol], writes=[bnbcol])
                tt(v3(arg1), barg1, gcb[0:64, :].rearrange("p (n f) -> p n f", f=64), [bgcb, bgccol], gccol[:].unsqueeze(2).to_broadcast([64, 8, 64]), bgccol, ALU.subtract)
                tt(v3(DT), bDT, v3(arg1), [barg1, bc64], NEGU.unsqueeze(1).to_broadcast([64, 8, 64]), bc64, ALU.add)
                P.op("act", lambda E: E.activation(out=DT[:], in_=DT[:], func=AF.Exp), reads=[bDT], writes=[bDT])
                P.op("dve", lambda E: E.scalar_tensor_tensor(out=Ds[:].rearrange("p (n f) -> p n f", f=64), in0=arg1[:].rearrange("p (n f) -> p n f", f=64), scalar=-1.0,
                                                             in1=NEGLS.unsqueeze(1).to_broadcast([64, 8, 64]), op0=ALU.mult, op1=ALU.add), reads=[barg1, bc64], writes=[bDs])
                P.op("act", lambda E: E.activation(out=Ds[:], in_=Ds[:], func=AF.Exp), reads=[bDs], writes=[bDs])
                tt(v3(BBm), bBBm, bb[0:64, :].rearrange("p (n f) -> p n f", f=64), [bbb, bc64], NSU.unsqueeze(1).to_broadcast([64, 8, 64]), bc64, ALU.mult)
                pk, bpk = GP[0]; pq, bpq = GP[1]
                fns = [(lambda E, n=n, pk=pk, kT=kT: E.matmul(pk[0:64, n * 64:(n + 1) * 64], lhsT=kT[:, n * 64:(n + 1) * 64], rhs=kT[:, n * 64:(n + 1) * 64],
                                                              start=True, stop=True)) for n in range(8)]
                P.mm_group(fns, reads=[bkT], writes=[bpk])
                fns = [(lambda E, n=n, pq=pq, kT=kT, qT=qT: E.matmul(pq[0:64, n * 64:(n + 1) * 64], lhsT=kT[:, n * 64:(n + 1) * 64], rhs=qT[:, n * 64:(n + 1) * 64],
                                                                     start=True, stop=True)) for n in range(8)]
                P.mm_group(fns, reads=[bkT, bqT], writes=[bpq])
                tt(attnT[:], battnT, pq[0:64, :], [bpq, bDT], DT[:], bDT, ALU.mult)
                Pc, bPc = m64["Pa"]; Pn, bPn = m64["Pb"]; Qc, bQc = m64["Qa"]; Qn, bQn = m64["Qb"]
                tt(tmp[:], btmp, pk[0:64, :], [bpk, bDT], DT[:], bDT, ALU.mult)
                tt(Qc[:], bQc, tmp[:], [btmp, bBBm], BBm[:], bBBm, ALU.mult)
                tt(tmp2[:], btmp2, pk[0:64, :], [bpk, bDs], Ds[:], bDs, ALU.mult)
                tt(v3(Pc), bPc, v3(tmp2), [btmp2, bnbcol], nbcol[:].unsqueeze(2).to_broadcast([64, 8, 64]), bnbcol, ALU.mult)
                tt(v3(Y), bY, v3(Qc), [bQc, bc64], i64b, bc64, ALU.add)
                P.op("act", lambda E, Ybf=Ybf, Y=Y: E.copy(out=Ybf[:], in_=Y[:]), reads=[bY], writes=[bYbf])
                for j in range(5):
                    pP, bpP = GP[2]; pQ, bpQ = GP[3]
                    fns = [(lambda E, n=n, pP=pP, Qc=Qc, Pc=Pc: E.matmul(pP[0:64, n * 64:(n + 1) * 64], lhsT=Qc[:, n * 64:(n + 1) * 64], rhs=Pc[:, n * 64:(n + 1) * 64],
                                                                         start=True, stop=True)) for n in range(8)]
                    P.mm_group(fns, reads=[bQc, bPc], writes=[bpP])
                    if j < 4:
                        fns = [(lambda E, n=n, pQ=pQ, Qc=Qc, Pc=Pc: E.matmul(pQ[0:64, n * 64:(n + 1) * 64], lhsT=Pc[:, n * 64:(n + 1) * 64], rhs=Qc[:, n * 64:(n + 1) * 64],
                                                                             start=True, stop=True)) for n in range(8)]
                        P.mm_group(fns, reads=[bQc, bPc], writes=[bpQ])
                    P.op("act", lambda E, Pn=Pn, pP=pP: E.copy(out=Pn[:], in_=pP[0:64, :]), reads=[bpP], writes=[bPn])
                    if j < 4:
                        P.op("dve", lambda E, Qn=Qn, pQ=pQ: E.tensor_copy(out=Qn[:], in_=pQ[0:64, :]), reads=[bpQ], writes=[bQn])
                    pY, bpY = GP[0]
                    fns = [(lambda E, n=n, pY=pY, Pn=Pn, Ybf=Ybf: E.matmul(pY[0:64, n * 64:(n + 1) * 64], lhsT=Pn[:, n * 64:(n + 1) * 64], rhs=Ybf[:, n * 64:(n + 1) * 64],
                                                                         start=True, stop=True)) for n in range(8)]
                    P.mm_group(fns, reads=[bPn, bYbf], writes=[bpY])
                    tt(Y[:], bY, Y[:], [bY, bpY], pY[0:64, :], bpY, ALU.add)
                    P.op("act", lambda E, Ybf=Ybf, Y=Y: E.copy(out=Ybf[:], in_=Y[:]), reads=[bY], writes=[bYbf])
                    Pc, bPc, Pn, bPn = Pn, bPn, Pc, bPc
                    Qc, bQc, Qn, bQn = Qn, bQn, Qc, bQc
                for hf in range(2):
                    pth, bpth = GA[hf]
                    fns = [(lambda E, n=n, vT=vT, pth=pth, hf=hf: E.transpose(out=pth[0:64, n * 128:(n + 1) * 128], in_=vT[:, (4 * hf + n) * 64:(4 * hf + n + 1) * 64],
                                                                              identity=idf[:])) for n in range(4)]
                    P.mm_group(fns, reads=[bvT, bidf], writes=[bpth])
                    tt(bv[:, 4 * hf:4 * hf + 4, :], bbv, pth[0:64, :].rearrange("p (n d) -> p n d", d=128), [bpth, bbcol],
                       bcol[:, 4 * hf:4 * hf + 4].unsqueeze(2).to_broadcast([64, 4, 128]), bbcol, ALU.mult)
                tt(elast[:], belast, gcb[0:64, :].rearrange("p (n f) -> p n f", f=64)[:, :, 63], [bgcb, bgccol], gccol[:], bgccol, ALU.subtract)
                P.op("act", lambda E: E.activation(out=elast[:], in_=elast[:], func=AF.Exp), reads=[belast], writes=[belast])
                for hf in range(2):
                    pth, bpth = GA[hf]
                    fns = [(lambda E, n=n, kT=kT, pth=pth, hf=hf: E.transpose(out=pth[0:64, n * 128:(n + 1) * 128], in_=kT[:, (4 * hf + n) * 64:(4 * hf + n + 1) * 64],
                                                                              identity=idf[:])) for n in range(4)]
                    P.mm_group(fns, reads=[bkT, bidf], writes=[bpth])
                    tt(kdec[:, 4 * hf:4 * hf + 4, :], bkdec, pth[0:64, :].rearrange("p (n d) -> p n d", d=128), [bpth, belast],
                       elast[:, 4 * hf:4 * hf + 4].unsqueeze(2).to_broadcast([64, 4, 128]), belast, ALU.mult)
                P.op("act", lambda E, gcb=gcb, EG=EG: E.activation(out=EG[:], in_=gcb[:], func=AF.Exp), reads=[bgcb], writes=[bEG])
                tt(qdec[:], bqdec, qT, [bqT, bEG], EG[:], bEG, ALU.mult)
                kTb, bkTb = H["kTb"]
                P.op("act", lambda E, kTb=kTb, kT=kT: E.copy(out=kTb[:], in_=kT), reads=[bkT], writes=[bkTb])
                P.op("act", lambda E: E.activation(out=egc[:], in_=gccol[:], func=AF.Exp), reads=[bgccol], writes=[begc])
                P.op("dve", lambda E, nbg=nbg: E.scalar_tensor_tensor(out=nbg[:], in0=egc[:], scalar=-1.0, in1=bcol[:], op0=ALU.mult, op1=ALU.mult),
                     reads=[begc, bbcol], writes=[bnbg])
            banks = [(GP[0], GP[1]), (GP[2], GP[3])]
            for n in range(8):
                cs = slice(n * 64, (n + 1) * 64)
                for h in range(2):
                    H = heads[h]; S, bS = Sst[h]
                    kT, bkT = H["kTb"]; Sbf, bSbf = H["Sbf"]
                    attnT, battnT = H["attnT"]; Y, bY = H["Ybf"]; EG, bEG = H["EG"]; qdec, bqdec = H["qdec"]
                    bv, bbv = H["bv"]; kdec, bkdec = H["kdec"]; nbg, bnbg = H["nbg"]
                    vnew, bvnew = H["vnew"]; rhs2, brhs2 = H["rhs2"]; osb, bosb = H["osb"]
                    (KSO, bKSO), (Sb, bSb) = banks[h]
                    Vb, bVb = KSO, bKSO
                    P.op("pe", lambda E, cs=cs, kT=kT, Sbf=Sbf, KSO=KSO: E.matmul(KSO[0:64, 0:128], lhsT=kT[:, cs], rhs=Sbf[:], start=True, stop=True),
                         reads=[bkT, bSbf], writes=[bKSO])
                    P.op("dve", lambda E, n=n, KSO=KSO, rhs2=rhs2, nbg=nbg, bv=bv: E.scalar_tensor_tensor(
                        out=rhs2[:], in0=KSO[0:64, 0:128], scalar=nbg[:, n:n + 1], in1=bv[:, n, :], op0=ALU.mult, op1=ALU.add),
                        reads=[bKSO, bnbg, bbv], writes=[brhs2])
                    P.op("pe", lambda E, cs=cs, Y=Y, Vb=Vb, rhs2=rhs2: E.matmul(Vb[0:64, 128:256], lhsT=Y[:, cs], rhs=rhs2[:], start=True, stop=True),
                         reads=[bY, brhs2], writes=[bVb])
                    P.op("act", lambda E, vnew=vnew, Vb=Vb: E.copy(out=vnew[:], in_=Vb[0:64, 128:256]), reads=[bVb], writes=[bvnew])
                    fns = [lambda E, cs=cs, Sbf=Sbf, KSO=KSO, qdec=qdec: E.matmul(KSO[64:128, 0:128], lhsT=qdec[:, cs], rhs=Sbf[:], start=True, stop=False),
                           lambda E, cs=cs, KSO=KSO, attnT=attnT, vnew=vnew: E.matmul(KSO[64:128, 0:128], lhsT=attnT[:, cs], rhs=vnew[:], start=False, stop=True)]
                    P.mm_group(fns, reads=[bqdec, bSbf, battnT, bvnew], writes=[bKSO])
                    P.op("pe", lambda E, n=n, Sb=Sb, kdec=kdec, vnew=vnew: E.matmul(Sb[:, 0:128], lhsT=kdec[:, n, :], rhs=vnew[:], start=True, stop=True),
                         reads=[bkdec, bvnew], writes=[bSb])
                    P.op("dve", lambda E, n=n, S=S, EG=EG, Sb=Sb: E.scalar_tensor_tensor(out=S[:], in0=S[:], scalar=EG[:, n * 64 + 63:n * 64 + 64], in1=Sb[:, 0:128],
                                                                                         op0=ALU.mult, op1=ALU.add), reads=[bS, bEG, bSb], writes=[bS])
                    P.op("act", lambda E, S=S, Sbf=Sbf: E.copy(out=Sbf[:], in_=S[:]), reads=[bS], writes=[bSbf])
                    P.op("act", lambda E, n=n, osb=osb, KSO=KSO: E.copy(out=osb[64:128, n, :], in_=KSO[64:128, 0:128]), reads=[bKSO], writes=[bosb])
                    advance(1)
                advance(1)
            advance(100)
            for h in range(2):
                osb, bosb = heads[h]["osb"]
                P.dma("sp", o_d[s_ * 512:(s_ + 1) * 512, h * 128:(h + 1) * 128].rearrange("(n c) d -> c n d", c=64), osb[64:128, :, :], reads=[bosb],
                      writes=[fz["obuf_of"](s_) if fz else bo])
            if fz:
                fz["after_chunk"](s_)
        if fz:
            barrier(P)
        else:
            P.finish([bo])
    return nc


def run_L1a(inp):
    nc = _get("L1a", build_L1a)
    c64, cmask, sel = _gdn_consts()
    w_in = inp["w_in_even"][0]
    conv = inp["conv_qkv"][0]
    ones = np.ones((128, 128), np.float32)
    maps = []
    for c in range(8):
        b, r = divmod(c, 4)
        cols = np.concatenate([np.arange(256 * r, 256 * r + 256), 1024 + np.arange(256 * r, 256 * r + 256), 2048 + np.arange(256 * r, 256 * r + 256)])
        maps.append({"x": np.ascontiguousarray(inp["x"][b]), "npre": np.ascontiguousarray(inp["norm_pre"][0]),
                     "w": np.ascontiguousarray(w_in[:, cols]), "wb": np.ascontiguousarray(w_in[:, 4096 + 2 * r:4096 + 2 * r + 2]),
                     "wa": np.ascontiguousarray(w_in[:, 4104 + 2 * r:4104 + 2 * r + 2]), "conv": np.ascontiguousarray(conv[:, cols]),
                     "alog": np.ascontiguousarray(inp["a_log"][0, 2 * r:2 * r + 2]), "dtb": np.ascontiguousarray(inp["dt_bias"][0, 2 * r:2 * r + 2]),
                     "ident": _IDENT, "c64": c64, "cmask": cmask, "sel": sel, "ones": ones})
    res = run_bass_kernel_spmd(nc, maps, core_ids=list(range(8)))
    S_ = inp["x"].shape[1]
    o = np.empty((2, S_, 1024), np.float32)
    for c in range(8):
        b, r = divmod(c, 4)
        o[b, :, 256 * r:256 * (r + 1)] = res.results[c]["o"]
    return o


def kernel_unfused(**inputs):
    inp = {k: np.asarray(v) for k, v in inputs.items()}
    o = run_L1a(inp)
    ys = run_L1b(inp)
    x1 = run_L2(inp, o, ys)
    out = run_L3(inp, x1)
    return out.astype(np.float32)


def build_fused():
    nc = bass.Bass("TRN2", target_bir_lowering=False)
    x_full = nc.dram_tensor("x", [8192, 1024], F32, kind="ExternalInput").ap()
    ident_d = nc.dram_tensor("ident", [128, 128], F32, kind="ExternalInput").ap()
    npre0_d = nc.dram_tensor("npre0", [1024], F32, kind="ExternalInput").ap()
    gidx_d = nc.dram_tensor("gidx", [128, 2, 17, 4], I32, kind="ExternalInput").ap()
    out_d = nc.dram_tensor("out", [2048, 1024], F32, kind="ExternalOutput").ap()
    ag_in = [nc.dram_tensor("ag_in%d" % i, [8192, 256], F32) for i in range(2)]
    ag_out = [nc.dram_tensor("ag_out%d" % i, [4 * 8192, 256], F32) for i in range(2)]
    x1s = nc.dram_tensor("x1s", [2176, 1024], F32)
    GROUPS = [[0, 1, 2, 3], [4, 5, 6, 7]]
    with ExitStack() as st:
        C = Ctx(nc, st); P = C.P
        csem = st.enter_context(nc.semaphore("csem"))
        bag_out = Buf("ag_out"); bx1s = Buf("x1s", multi=True); bout = Buf("out", multi=True)
        bo_ch = [Buf("o_ch%d" % k, multi=True) for k in range(16)]
        by_jt = [Buf("y_jt%d" % k, multi=True) for k in range(4)]
        ncc = [0]

        def emit_cc(which, k, inbuf, rows=512):
            P._deps("pool", [inbuf], [])
            P.streams["pool"].append(lambda E, which=which, k=k, rows=rows: E.collective_compute(
                "AllGather", ALU.bypass, replica_groups=GROUPS,
                ins=[ag_in[which].ap()[k * rows:(k + 1) * rows, :].opt()],
                outs=[ag_out[which].ap()[k * 4 * rows:(k + 1) * 4 * rows, :].opt()]).then_inc(csem))
            ncc[0] += 1

        share1 = {"x": x_full, "ident": ident_d, "npre": npre0_d}

        def after_jt(jt):
            for k in range(2 * jt, 2 * jt + 2):
                emit_cc(1, k, by_jt[jt], rows=1024)

        with ExitStack() as stU:
            CU = Ctx(nc, stU, P, "u_")
            uext = CU.sb("uTp", [128, 2, 16, 512], BF16)
            build_L1a(8192, fz={"nc": nc, "P": P, "pfx": "a_", "share": share1, "out": ag_in[0].ap(), "uTp": uext,
                                "obuf_of": lambda s_: bo_ch[s_], "after_chunk": lambda s_: emit_cc(0, s_, bo_ch[s_])})
            build_L1b(8192, fz={"nc": nc, "P": P, "pfx": "b_", "share": share1, "out": ag_in[1].ap(), "uTp": uext,
                                "obuf_of": lambda jt: by_jt[jt], "after_chunk": after_jt})
        P.streams["pool"].append(lambda E: E.wait_ge(csem, ncc[0]))
        gidx, bgidx = C.sb("gidx", [128, 2, 17, 4], I32)
        P.dma("sp", gidx[:], gidx_d, writes=[bgidx])
        P.op("pool", lambda E: E.nop(), reads=[], writes=[bag_out])

        def gather(P_, ld, bld, tile, part):
            for i in range(4):
                P_.dma_ind("pool", ld[:, i * 256:(i + 1) * 256], ag_out[part].ap(), gidx[:, part, tile, i:i + 1], reads=[bag_out, bgidx], writes=[bld])

        share2 = {"ident": ident_d, "npre": npre0_d, "o": None, "ys": None}
        build_L2(2176, fz={"nc": nc, "P": P, "pfx": "c_", "share": share2, "out": x1s.ap(), "obuf": bx1s, "gather": gather})
        share3 = {"ident": ident_d, "x": x1s.ap()}
        build_L3(2048, fz={"nc": nc, "P": P, "pfx": "d_", "share": share3, "out": out_d, "obuf": bout, "xbuf": bx1s})
        P.finish([bout])
    return nc


def _gidx(r):
    g = np.zeros((128, 2, 17, 4), np.int32)
    p = np.arange(128)[:, None, None]
    tile = np.arange(17)[None, :, None]
    src = np.arange(4)[None, None, :]
    tok = np.clip(2048 * r - 128 + tile * 128 + p, 0, 8191)
    for part, R in ((0, 512), (1, 1024)):
        g[:, part] = ((tok // R) * 4 + src) * R + tok % R
    return g


def kernel(**inputs):
    inp = {k: np.ascontiguousarray(np.asarray(v)) for k, v in inputs.items()}
    nc = _get("fused", build_fused)
    c64, cmask, sel = _gdn_consts()
    mk, idm = _s5_consts()
    ones = np.ones((128, 128), np.float32)
    w_in = inp["w_in_even"][0]
    conv = inp["conv_qkv"][0]
    wz = np.ascontiguousarray(np.concatenate([w_in[:, 3072:4096], w_in[:, 5136:6160]], axis=1))
    maps = []
    for c in range(8):
        b, r = divmod(c, 4)
        cols = np.concatenate([np.arange(256 * r, 256 * r + 256), 1024 + np.arange(256 * r, 256 * r + 256), 2048 + np.arange(256 * r, 256 * r + 256)])
        gs = slice(16 * r, 16 * r + 16)
        xq = np.zeros((2176, 1024), np.float32)
        xq[128:] = inp["x"][b, 2048 * r:2048 * (r + 1)]
        if r > 0:
            xq[:128] = inp["x"][b, 2048 * r - 128:2048 * r]
        m = {"x": inp["x"][b], "ident": _IDENT, "npre0": inp["norm_pre"][0], "gidx": _gidx(r),
             "a_w": np.ascontiguousarray(w_in[:, cols]), "a_wb": np.ascontiguousarray(w_in[:, 4096 + 2 * r:4096 + 2 * r + 2]),
             "a_wa": np.ascontiguousarray(w_in[:, 4104 + 2 * r:4104 + 2 * r + 2]), "a_conv": np.ascontiguousarray(conv[:, cols]),
             "a_alog": np.ascontiguousarray(inp["a_log"][0, 2 * r:2 * r + 2]), "a_dtb": np.ascontiguousarray(inp["dt_bias"][0, 2 * r:2 * r + 2]),
             "a_c64": c64, "a_cmask": cmask, "a_sel": sel, "a_ones": ones,
             "a_wu": np.ascontiguousarray(w_in[:, 4112 + 256 * r:4112 + 256 * (r + 1)]),
             "b_wu": np.ascontiguousarray(w_in[:, 4112 + 256 * r:4112 + 256 * (r + 1)]),
             "b_lre": np.ascontiguousarray(inp["s5_lam_re"][0, gs]), "b_lim": np.ascontiguousarray(inp["s5_lam_im"][0, gs]),
             "b_bre": np.ascontiguousarray(inp["s5_b_re"][0, gs]), "b_bim": np.ascontiguousarray(inp["s5_b_im"][0, gs]),
             "b_cre": np.ascontiguousarray(inp["s5_c_re"][0, gs]), "b_cim": np.ascontiguousarray(inp["s5_c_im"][0, gs]),
             "b_ldt": np.ascontiguousarray(inp["s5_log_dt"][0, gs]), "b_dd": np.ascontiguousarray(inp["s5_d"][0, 256 * r:256 * (r + 1)]),
             "b_taus": TAUS, "b_mk": mk, "b_idm": idm,
             "c_x": xq, "c_wz": wz, "c_wglu": inp["w_glu"][0], "c_wout": inp["w_out_even"][0], "c_npost": inp["norm_post"][0],
             "c_gnw": inp["gdn_norm_w"][0],
             "d_win": inp["w_in_odd"][0], "d_wout": inp["w_out_odd"][0], "d_conv": inp["conv_short"][0],
             "d_npre": inp["norm_pre"][1], "d_npost": inp["norm_post"][1]}
        maps.append(m)
    res = run_bass_kernel_spmd(nc, maps, core_ids=list(range(8)))
    out = np.empty((2, 8192, 1024), np.float32)
    for c in range(8):
        b, r = divmod(c, 4)
        out[b, r * 2048:(r + 1) * 2048] = res.results[c]["out"]
    return out
```

```python
from contextlib import ExitStack
import numpy as np
import concourse.bass as bass
import concourse.mybir as mybir
from concourse.bass_utils import run_bass_kernel_spmd

F32 = mybir.dt.float32
BF16 = mybir.dt.bfloat16
AF = mybir.ActivationFunctionType
ALU = mybir.AluOpType
AX = mybir.AxisListType

NDS = 12


class Buf:
    __slots__ = ("name", "w", "r", "multi")

    def __init__(self, name, multi=False):
        self.name = name
        self.w = [] if multi else None
        self.r = []
        self.multi = multi


class Prog:
    ENG = ("pe", "act", "dve", "pool", "sp")

    def __init__(self, nc, stack):
        self.nc = nc
        self.stack = stack
        self.streams = {e: [] for e in self.ENG}
        self.cnt = {e: 0 for e in self.ENG}
        self.sem = {e: stack.enter_context(nc.semaphore("s_" + e)) for e in self.ENG}
        self.seen = {e: {} for e in self.ENG}
        self.dcnt = {e: 0 for e in self.ENG}
        self.dsem = {}
        for e in ("sp", "pool", "act"):
            self.dsem[e] = [stack.enter_context(nc.semaphore("d_%s%d" % (e, i))) for i in range(NDS)]
        self.same_engine_sync = True
        self.nwaits = 0

    def _wait(self, eng, tok):
        if tok is None:
            return
        kind = tok[0]
        if kind == "c":
            _, e2, n = tok
            if e2 == eng and (eng == "pe" or not self.same_engine_sync):
                return
            key = e2
            if self.seen[eng].get(key, 0) >= n:
                return
            self.seen[eng][key] = n
            sem = self.sem[e2]
            self.streams[eng].append(lambda E, sem=sem, n=n: E.wait_ge(sem, n))
            self.nwaits += 1
        else:
            _, q, slot, val = tok
            key = ("d", q, slot)
            if self.seen[eng].get(key, 0) >= val:
                return
            self.seen[eng][key] = val
            sem = self.dsem[q][slot]
            self.streams[eng].append(lambda E, sem=sem, val=val: E.wait_ge(sem, val))
            self.nwaits += 1

    def _deps(self, eng, reads, writes):
        for b in reads:
            if b.multi:
                for t in b.w:
                    self._wait(eng, t)
            else:
                self._wait(eng, b.w)
        for b in writes:
            if not b.multi:
                self._wait(eng, b.w)
            for t in b.r:
                self._wait(eng, t)

    def _commit(self, tok, reads, writes):
        for b in writes:
            if b.multi:
                b.w.append(tok)
            else:
                b.w = tok
            b.r = []
        for b in reads:
            if b not in writes:
                b.r.append(tok)

    def op(self, eng, fn, reads=(), writes=()):
        reads = list(reads)
        writes = list(writes)
        self._deps(eng, reads, writes)
        self.cnt[eng] += 1
        n = self.cnt[eng]
        sem = self.sem[eng]
        self.streams[eng].append(lambda E, fn=fn, sem=sem: fn(E).then_inc(sem, 1))
        tok = ("c", eng, n)
        self._commit(tok, reads, writes)
        return tok

    def mm_group(self, fns, reads=(), writes=()):
        eng = "pe"
        reads = list(reads)
        writes = list(writes)
        self._deps(eng, reads, writes)
        self.cnt[eng] += 1
        n = self.cnt[eng]
        sem = self.sem[eng]
        for fn in fns[:-1]:
            self.streams[eng].append(lambda E, fn=fn: fn(E))
        last = fns[-1]
        self.streams[eng].append(lambda E, fn=last, sem=sem: fn(E).then_inc(sem, 1))
        tok = ("c", eng, n)
        self._commit(tok, reads, writes)
        return tok

    def dma(self, q, out_ap, in_ap, reads=(), writes=()):
        reads = list(reads)
        writes = list(writes)
        self._deps(q, reads, writes)
        j = self.dcnt[q]
        self.dcnt[q] += 1
        slot = j % NDS
        val = 16 * (j // NDS + 1)
        if j >= NDS:
            self._wait(q, ("d", q, slot, val - 16))
        sem = self.dsem[q][slot]
        self.streams[q].append(
            lambda E, o=out_ap, i=in_ap, sem=sem: E.dma_start(out=o, in_=i).then_inc(sem, 16))
        tok = ("d", q, slot, val)
        self._commit(tok, reads, writes)
        return tok

    def dma_ind(self, q, out_ap, table_ap, idx_ap, reads=(), writes=()):
        reads = list(reads)
        writes = list(writes)
        self._deps(q, reads, writes)
        j = self.dcnt[q]
        self.dcnt[q] += 1
        slot = j % NDS
        val = 16 * (j // NDS + 1)
        if j >= NDS:
            self._wait(q, ("d", q, slot, val - 16))
        sem = self.dsem[q][slot]
        self.streams[q].append(
            lambda E, o=out_ap, t=table_ap, i=idx_ap, sem=sem: E.indirect_dma_start(
                out=o, out_offset=None, in_=t, in_offset=bass.IndirectOffsetOnAxis(ap=i, axis=0)).then_inc(sem, 16))
        tok = ("d", q, slot, val)
        self._commit(tok, reads, writes)
        return tok

    def finish(self, final_bufs):
        for b in final_bufs:
            for t in (b.w if b.multi else [b.w]):
                self._wait("sp", t)
        nc = self.nc
        streams = self.streams
        with nc.Block() as block:
            @block.tensor
            def _(E):
                for f in streams["pe"]:
                    f(E)

            @block.scalar
            def _(E):
                for f in streams["act"]:
                    f(E)

            @block.vector
            def _(E):
                for f in streams["dve"]:
                    f(E)

            @block.gpsimd
            def _(E):
                for f in streams["pool"]:
                    f(E)

            @block.sync
            def _(E):
                for f in streams["sp"]:
                    f(E)


class Ctx:
    def __init__(self, nc, st, P=None, pfx=""):
        self.nc = nc
        self.st = st
        self.pfx = pfx
        if P is None:
            st.enter_context(nc.allow_non_contiguous_dma(reason="small parameter loads / layout transforms"))
            P = Prog(nc, st)
        self.P = P

    def sb(self, name, shape, dt=F32):
        t = self.st.enter_context(self.nc.sbuf_tensor("sb_" + self.pfx + name, shape, dt))
        return t, Buf(name)

    def ps(self, name, shape, dt=F32):
        t = self.st.enter_context(self.nc.psum_tensor("ps_" + self.pfx + name, shape, dt))
        return t, Buf(name)


def bcast_row_load(C, name, dram_vec, n, q="sp"):
    t, b = C.sb(name, [128, n])
    C.P.dma(q, t[:], dram_vec.partition_broadcast(128), writes=[b])
    return t, b


def make_ident(C, dram_ident):
    idf, bidf = C.sb("identf", [128, 128])
    C.P.dma("sp", idf[:], dram_ident, writes=[bidf])
    idb, bidb = C.sb("identb", [128, 128], BF16)
    C.P.op("dve", lambda E: E.tensor_copy(out=idb[:], in_=idf[:]), reads=[bidf], writes=[bidb])
    return idf, bidf, idb, bidb


def rms_rstd(C, src, bsrc, ncols, junk, bjunk, ss, bss, eps=1e-6):
    P = C.P
    P.op("act", lambda E: E.activation(out=junk, in_=src, func=AF.Square, accum_out=ss[:, 0:1]),
         reads=[bsrc], writes=[bjunk, bss])
    P.op("act", lambda E: E.activation(out=ss[:, 0:1], in_=ss[:, 0:1], func=AF.Sqrt, bias=float(eps), scale=float(1.0 / ncols)),
         reads=[bss], writes=[bss])
    P.op("dve", lambda E: E.reciprocal(out=ss[:, 0:1], in_=ss[:, 0:1]), reads=[bss], writes=[bss])


def transpose8(C, src_bf, bsrc, idb, bidb, ptr, bptr, dst3, bdst, eng="act"):
    P = C.P
    fns = [(lambda E, kt=kt: E.transpose(out=ptr[:, kt * 128:(kt + 1) * 128], in_=src_bf[:, kt * 128:(kt + 1) * 128],
                                         identity=idb[:])) for kt in range(8)]
    P.mm_group(fns, reads=[bsrc, bidb], writes=[bptr])
    src3 = ptr[:].rearrange("p (k t) -> p k t", k=8)
    if eng == "act":
        P.op("act", lambda E: E.copy(out=dst3, in_=src3), reads=[bptr], writes=[bdst])
    else:
        P.op("dve", lambda E: E.tensor_copy(out=dst3, in_=src3), reads=[bptr], writes=[bdst])


def outproj_post(C, catT, bcat, nkt, wout, bwout, t, xres, bxres, npw, bnpw, pso, bpso, yo, byo, junk, bjunk, ss, bss,
                 out_dram_rows, bout):
    P = C.P
    for hh in range(2):
        fns = [(lambda E, kt=kt, hh=hh: E.matmul(pso[hh][:], lhsT=catT[:, kt, t * 128:(t + 1) * 128],
                                                 rhs=wout[:, kt, hh * 512:(hh + 1) * 512],
                                                 start=(kt == 0), stop=(kt == nkt - 1))) for kt in range(nkt)]
        P.mm_group(fns, reads=[bcat, bwout], writes=[bpso[hh]])
        P.op("act", lambda E, hh=hh: E.copy(out=yo[:, hh * 512:(hh + 1) * 512], in_=pso[hh][:]),
             reads=[bpso[hh]], writes=[byo])
    rms_rstd(C, yo[:], byo, 1024, junk[:], bjunk, ss, bss)
    P.op("dve", lambda E: E.scalar_tensor_tensor(out=yo[:], in0=yo[:], scalar=ss[:, 0:1], in1=npw[:],
                                                 op0=ALU.mult, op1=ALU.mult), reads=[byo, bss, bnpw], writes=[byo])
    P.op("dve", lambda E: E.tensor_tensor(out=yo[:], in0=yo[:], in1=xres, op=ALU.add), reads=[byo, bxres], writes=[byo])
    P.dma("sp", out_dram_rows, yo[:], reads=[byo], writes=[bout])


def load_w_bf16(C, name, dram_w, kt_n, ncols, chunk=2048):
    w, bw = C.sb(name, [128, kt_n, ncols], BF16)
    src = dram_w.rearrange("(k p) c -> p k c", p=128)
    for kt in range(kt_n):
        for c0 in range(0, ncols, chunk):
            c1 = min(ncols, c0 + chunk)
            C.P.dma("pool", w[:, kt, c0:c1], src[:, kt, c0:c1], writes=[bw])
    return w, bw


def build_L2(ntok=2048, fz=None):
    nc = fz["nc"] if fz else bass.Bass("TRN2", target_bir_lowering=False)
    pfx = fz["pfx"] if fz else ""

    def D(name, shape):
        if fz and name in fz["share"]:
            return fz["share"][name]
        return nc.dram_tensor(pfx + name, shape, F32, kind="ExternalInput").ap()
    x_d = D("x", [ntok, 1024]); o_d = D("o", [ntok, 1024]); ys_d = D("ys", [ntok, 1024])
    wz_d = D("wz", [1024, 2048]); wglu_d = D("wglu", [1024, 1024]); wout_d = D("wout", [2048, 1024])
    npre_d = D("npre", [1024]); npost_d = D("npost", [1024]); gnw_d = D("gnw", [128]); ident_d = D("ident", [128, 128])
    out_d = fz["out"] if fz else nc.dram_tensor("out", [ntok, 1024], F32, kind="ExternalOutput").ap()
    NT = 512
    with ExitStack() as st:
        C = Ctx(nc, st, fz["P"], pfx) if fz else Ctx(nc, st); P = C.P
        idf, bidf, idb, bidb = make_ident(C, ident_d)
        npre, bnpre = bcast_row_load(C, "npre", npre_d, 1024)
        npost, bnpost = bcast_row_load(C, "npost", npost_d, 1024)
        gnw, bgnw = bcast_row_load(C, "gnw", gnw_d, 128)
        wz, bwz = load_w_bf16(C, "wz", wz_d, 8, 2048)
        wglu, bwglu = load_w_bf16(C, "wglu", wglu_d, 8, 1024)
        wout, bwout = load_w_bf16(C, "wout", wout_d, 16, 1024)
        xt4, bxt4 = C.sb("xt4", [128, 4, 1024]); bxt = [Buf("xt%d" % i) for i in range(4)]
        ldo = [C.sb("ldo%d" % i, [128, 1024]) for i in range(2)]
        ldy = [C.sb("ldy%d" % i, [128, 1024]) for i in range(2)]
        sq, bsq = C.sb("sq", [128, 1024])
        hn, bhn = C.sb("hn", [128, 1024], BF16)
        ss, bss = C.sb("ss", [128, 1])
        ss8, bss8 = C.sb("ss8", [128, 8])
        hT, bhT = C.sb("hT", [128, 8, NT], BF16)
        oT, boT = C.sb("oT", [128, 8, NT], BF16)
        yT, byT = C.sb("yT", [128, 8, NT], BF16)
        gz, bgz = C.sb("gz", [128, 8, NT], BF16)
        sg, bsg = C.sb("sg", [128, NT], BF16)
        catT, bcat = C.sb("catT", [128, 16, NT], BF16)
        yo, byo = C.sb("yo", [128, 1024])
        ptr, bptr = C.ps("ptr", [128, 1024], BF16)
        pmm = []; bpmm = []
        for i in range(4):
            t_, b_ = C.ps("pmm%d" % i, [128, 512]); pmm.append(t_); bpmm.append(b_)
        pso = []; bpso = []
        for i in range(2):
            t_, b_ = C.ps("pso%d" % i, [128, 512]); pso.append(t_); bpso.append(b_)
        bout = fz["obuf"] if fz else Buf("out", multi=True)
        if fz:
            sts = [(0, 128)] + [(128 + i * NT, NT) for i in range((ntok - 128) // NT)]
        else:
            sts = [(i * NT, NT) for i in range(ntok // NT)]
        tile_r0 = [t0_ + t_ * 128 for (t0_, n_) in sts for t_ in range(n_ // 128)]

        def issue_loads(ti):
            r0_ = tile_r0[ti]
            lo, blo = ldo[ti % 2]; ly, bly = ldy[ti % 2]
            if fz:
                fz["gather"](P, lo, blo, r0_ // 128, 0)
                fz["gather"](P, ly, bly, r0_ // 128, 1)
            else:
                P.dma("sp", lo[:], o_d[r0_:r0_ + 128, :], writes=[blo])
                P.dma("sp", ly[:], ys_d[r0_:r0_ + 128, :], writes=[bly])

        issue_loads(0)
        for (t0, n) in sts:
            ntl = n // 128
            for t in range(ntl):
                r0 = t0 + t * 128
                ti = tile_r0.index(r0)
                if ti + 1 < len(tile_r0):
                    issue_loads(ti + 1)
                P.dma("sp", xt4[:, t, :], x_d[r0:r0 + 128, :], writes=[bxt[t]])
                rms_rstd(C, xt4[:, t, :], bxt[t], 1024, sq[:], bsq, ss, bss)
                P.op("dve", lambda E, t=t: E.scalar_tensor_tensor(out=hn[:], in0=xt4[:, t, :], scalar=ss[:, 0:1], in1=npre[:],
                                                                  op0=ALU.mult, op1=ALU.mult), reads=[bxt[t], bss, bnpre], writes=[bhn])
                transpose8(C, hn, bhn, idb, bidb, ptr, bptr, hT[:, :, t * 128:(t + 1) * 128], bhT, eng="act")
                ld, bld = ldo[ti % 2]
                P.op("act", lambda E, ld=ld: E.activation(out=sq[:], in_=ld[:], func=AF.Square), reads=[bld], writes=[bsq])
                P.op("dve", lambda E: E.tensor_reduce(out=ss8[:], in_=sq[:].rearrange("p (h d) -> p h d", h=8), axis=AX.X, op=ALU.add),
                     reads=[bsq], writes=[bss8])
                P.op("dve", lambda E: E.tensor_scalar(out=ss8[:], in0=ss8[:], scalar1=1.0 / 128, scalar2=1e-6, op0=ALU.mult, op1=ALU.add),
                     reads=[bss8], writes=[bss8])
                P.op("act", lambda E: E.activation(out=ss8[:], in_=ss8[:], func=AF.Sqrt), reads=[bss8], writes=[bss8])
                P.op("dve", lambda E: E.reciprocal(out=ss8[:], in_=ss8[:]), reads=[bss8], writes=[bss8])
                P.op("dve", lambda E, ld=ld: E.tensor_tensor(out=sq[:].rearrange("p (h d) -> p h d", h=8), in0=ld[:].rearrange("p (h d) -> p h d", h=8),
                                                      in1=ss8[:].unsqueeze(2).to_broadcast([128, 8, 128]), op=ALU.mult),
                     reads=[bld, bss8], writes=[bsq])
                P.op("dve", lambda E: E.tensor_tensor(out=hn[:].rearrange("p (h d) -> p h d", h=8), in0=sq[:].rearrange("p (h d) -> p h d", h=8),
                                                      in1=gnw[:].unsqueeze(1).to_broadcast([128, 8, 128]), op=ALU.mult),
                     reads=[bsq, bgnw], writes=[bhn])
                transpose8(C, hn, bhn, idb, bidb, ptr, bptr, oT[:, :, t * 128:(t + 1) * 128], boT, eng="act")
                ld, bld = ldy[ti % 2]
                P.op("act", lambda E, ld=ld: E.activation(out=hn[:], in_=ld[:], func=AF.Gelu_apprx_tanh), reads=[bld], writes=[bhn])
                transpose8(C, hn, bhn, idb, bidb, ptr, bptr, yT[:, :, t * 128:(t + 1) * 128], byT, eng="dve")
            for ct in range(16):
                pb = pmm[ct % 4]; bpb = bpmm[ct % 4]
                fns = [(lambda E, kt=kt, ct=ct, pb=pb, n=n: E.matmul(pb[:, 0:n], lhsT=wz[:, kt, ct * 128:(ct + 1) * 128], rhs=hT[:, kt, 0:n],
                                                                start=(kt == 0), stop=(kt == 7))) for kt in range(8)]
                P.mm_group(fns, reads=[bwz, bhT], writes=[bpb])
                if ct < 8:
                    P.op("act", lambda E, pb=pb, n=n: E.activation(out=sg[:, 0:n], in_=pb[:, 0:n], func=AF.Silu), reads=[bpb], writes=[bsg])
                    P.op("dve", lambda E, ct=ct, n=n: E.tensor_tensor(out=catT[:, ct, 0:n], in0=oT[:, ct, 0:n], in1=sg[:, 0:n], op=ALU.mult),
                         reads=[boT, bsg], writes=[bcat])
                else:
                    P.op("act", lambda E, pb=pb, ct=ct, n=n: E.activation(out=gz[:, ct - 8, 0:n], in_=pb[:, 0:n], func=AF.Silu), reads=[bpb], writes=[bgz])
            for ct in range(8):
                pb = pmm[ct % 4]; bpb = bpmm[ct % 4]
                fns = [(lambda E, kt=kt, ct=ct, pb=pb, n=n: E.matmul(pb[:, 0:n], lhsT=wglu[:, kt, ct * 128:(ct + 1) * 128], rhs=yT[:, kt, 0:n],
                                                                start=(kt == 0), stop=(kt == 7))) for kt in range(8)]
                P.mm_group(fns, reads=[bwglu, byT], writes=[bpb])
                P.op("act", lambda E, pb=pb, n=n: E.activation(out=sg[:, 0:n], in_=pb[:, 0:n], func=AF.Sigmoid), reads=[bpb], writes=[bsg])
                P.op("dve", lambda E, ct=ct, n=n: E.tensor_tensor(out=sg[:, 0:n], in0=sg[:, 0:n], in1=yT[:, ct, 0:n], op=ALU.mult), reads=[bsg, byT], writes=[bsg])
                P.op("dve", lambda E, ct=ct, n=n: E.tensor_tensor(out=catT[:, 8 + ct, 0:n], in0=sg[:, 0:n], in1=gz[:, ct, 0:n], op=ALU.mult),
                     reads=[bsg, bgz], writes=[bcat])
            for t in range(ntl):
                r0 = t0 + t * 128
                outproj_post(C, catT, bcat, 16, wout, bwout, t, xt4[:, t, :], bxt[t], npost, bnpost, pso, bpso, yo, byo, sq, bsq, ss, bss,
                             out_d[r0:r0 + 128, :], bout)
        if fz:
            barrier(P)
        else:
            P.finish([bout])
    return nc


def build_L3(ntok=2048, fz=None):
    nc = fz["nc"] if fz else bass.Bass("TRN2", target_bir_lowering=False)
    pfx = fz["pfx"] if fz else ""

    def D(name, shape):
        if fz and name in fz["share"]:
            return fz["share"][name]
        return nc.dram_tensor(pfx + name, shape, F32, kind="ExternalInput").ap()
    x_d = D("x", [ntok + 128, 1024])
    win_d = D("win", [1024, 8192]); wout_d = D("wout", [2048, 1024]); conv_d = D("conv", [3, 2048])
    npre_d = D("npre", [1024]); npost_d = D("npost", [1024]); ident_d = D("ident", [128, 128])
    out_d = fz["out"] if fz else nc.dram_tensor("out", [ntok, 1024], F32, kind="ExternalOutput").ap()
    NT = 256
    with ExitStack() as st:
        C = Ctx(nc, st, fz["P"], pfx) if fz else Ctx(nc, st); P = C.P
        idf, bidf, idb, bidb = make_ident(C, ident_d)
        npre, bnpre = bcast_row_load(C, "npre", npre_d, 1024)
        npost, bnpost = bcast_row_load(C, "npost", npost_d, 1024)
        cw, bcw = C.sb("cw", [128, 3, 16])
        P.dma("sp", cw[:], conv_d.rearrange("j (c p) -> p j c", p=128), writes=[bcw])
        win, bwin = load_w_bf16(C, "win", win_d, 8, 8192)
        wout, bwout = load_w_bf16(C, "wout", wout_d, 16, 1024)
        xt, bxt = C.sb("xt", [128, 1024])
        sq, bsq = C.sb("sq", [128, 1024])
        hn, bhn = C.sb("hn", [128, 1024], BF16)
        ss, bss = C.sb("ss", [128, 1])
        hT, bhT = C.sb("hT", [128, 8, NT], BF16)
        y1T, by1T = C.sb("y1T", [128, 16, NT], BF16)
        pbuf, bpbuf = C.sb("pbuf", [128, NT + 2])
        phalo, bphalo = C.sb("phalo", [128, 16, 2])
        gcs, bgcs = C.sb("gcs", [128, NT])
        cv, bcv = C.sb("cv", [128, NT])
        sz, bsz = C.sb("sz", [128, NT])
        yo, byo = C.sb("yo", [128, 1024])
        P.op("dve", lambda E: E.memset(phalo[:], 0.0), writes=[bphalo])
        ptr, bptr = C.ps("ptr", [128, 1024], BF16)
        GB = [C.ps("g%d" % i, [128, 512]) for i in range(7)]
        pso = [GB[0][0], GB[1][0]]; bpso = [GB[0][1], GB[1][1]]
        bout = fz["obuf"] if fz else Buf("out", multi=True)
        sts = [(0, 128)] + [(128 + i * NT, NT) for i in range(ntok // NT)]
        for (t0, n) in sts:
            ntl = n // 128
            for t in range(ntl):
                r0 = t0 + t * 128
                P.dma("sp", xt[:], x_d[r0:r0 + 128, :], reads=([fz["xbuf"]] if fz else []), writes=[bxt])
                rms_rstd(C, xt[:], bxt, 1024, sq[:], bsq, ss, bss)
                P.op("dve", lambda E: E.scalar_tensor_tensor(out=hn[:], in0=xt[:], scalar=ss[:, 0:1], in1=npre[:],
                                                             op0=ALU.mult, op1=ALU.mult), reads=[bxt, bss, bnpre], writes=[bhn])
                transpose8(C, hn, bhn, idb, bidb, ptr, bptr, hT[:, :, t * 128:(t + 1) * 128], bhT, eng="act")
            for ct in range(16):
                sel_ = [GB[3 * (ct % 2) + 0], GB[3 * (ct % 2) + 1], GB[3 * (ct % 2) + 2], GB[6]]
                pmm = [x_[0] for x_ in sel_]; bpmm = [x_[1] for x_ in sel_]
                for part in range(4):
                    col0 = (part * 16 + ct) * 128
                    pb = pmm[part]
                    fns = [(lambda E, n=n, kt=kt, col0=col0, pb=pb: E.matmul(pb[:, 0:n], lhsT=win[:, kt, col0:col0 + 128], rhs=hT[:, kt, 0:n],
                                                                        start=(kt == 0), stop=(kt == 7))) for kt in range(8)]
                    P.mm_group(fns, reads=[bwin, bhT], writes=[bpmm[part]])
                P.op("act", lambda E, n=n, pmm=pmm: E.copy(out=gcs[:, 0:n], in_=pmm[1][:, 0:n]), reads=[bpmm[1]], writes=[bgcs])
                P.op("act", lambda E, ct=ct: E.copy(out=pbuf[:, 0:2], in_=phalo[:, ct, :]), reads=[bphalo], writes=[bpbuf])
                P.op("dve", lambda E, n=n, pmm=pmm: E.tensor_tensor(out=pbuf[:, 2:2 + n], in0=gcs[:, 0:n], in1=pmm[2][:, 0:n], op=ALU.mult),
                     reads=[bgcs, bpmm[2]], writes=[bpbuf])
                P.op("act", lambda E, n=n, ct=ct: E.copy(out=phalo[:, ct, :], in_=pbuf[:, n:n + 2]), reads=[bpbuf], writes=[bphalo])
                if t0 == 0:
                    continue
                P.op("dve", lambda E, n=n, ct=ct: E.tensor_scalar(out=cv[:, 0:n], in0=pbuf[:, 0:n], scalar1=cw[:, 0, ct:ct + 1], scalar2=None, op0=ALU.mult),
                     reads=[bpbuf, bcw], writes=[bcv])
                P.op("dve", lambda E, n=n, ct=ct: E.scalar_tensor_tensor(out=cv[:, 0:n], in0=pbuf[:, 1:1 + n], scalar=cw[:, 1, ct:ct + 1], in1=cv[:, 0:n],
                                                                    op0=ALU.mult, op1=ALU.add), reads=[bpbuf, bcw, bcv], writes=[bcv])
                P.op("dve", lambda E, n=n, ct=ct: E.scalar_tensor_tensor(out=cv[:, 0:n], in0=pbuf[:, 2:2 + n], scalar=cw[:, 2, ct:ct + 1], in1=cv[:, 0:n],
                                                                    op0=ALU.mult, op1=ALU.add), reads=[bpbuf, bcw, bcv], writes=[bcv])
                P.op("dve", lambda E, n=n, pmm=pmm: E.tensor_tensor(out=cv[:, 0:n], in0=cv[:, 0:n], in1=pmm[0][:, 0:n], op=ALU.mult), reads=[bcv, bpmm[0]], writes=[bcv])
                P.op("act", lambda E, n=n, pmm=pmm: E.activation(out=sz[:, 0:n], in_=pmm[3][:, 0:n], func=AF.Silu), reads=[bpmm[3]], writes=[bsz])
                P.op("dve", lambda E, n=n, ct=ct: E.tensor_tensor(out=y1T[:, ct, 0:n], in0=cv[:, 0:n], in1=sz[:, 0:n], op=ALU.mult),
                     reads=[bcv, bsz], writes=[by1T])
            if t0 == 0:
                continue
            for t in range(ntl):
                r0 = t0 + t * 128
                P.dma("sp", xt[:], x_d[r0:r0 + 128, :], reads=([fz["xbuf"]] if fz else []), writes=[bxt])
                outproj_post(C, y1T, by1T, 16, wout, bwout, t, xt[:], bxt, npost, bnpost, pso, bpso, yo, byo, sq, bsq, ss, bss,
                             out_d[r0 - 128:r0, :], bout)
        if fz:
            barrier(P)
        else:
            P.finish([bout])
    return nc


_IDENT = np.eye(128, dtype=np.float32)
_CACHE = {}


def _get(name, fn):
    if name not in _CACHE:
        _CACHE[name] = fn()
    return _CACHE[name]


def run_L2(inp, o_full, ys_full):
    nc = _get("L2", build_L2)
    w_in = inp["w_in_even"][0]
    wz = np.ascontiguousarray(np.concatenate([w_in[:, 3072:4096], w_in[:, 5136:6160]], axis=1))
    maps = []
    for c in range(8):
        b, r = divmod(c, 4)
        sl = slice(r * 2048, (r + 1) * 2048)
        maps.append({"x": np.ascontiguousarray(inp["x"][b, sl]), "o": np.ascontiguousarray(o_full[b, sl]),
                     "ys": np.ascontiguousarray(ys_full[b, sl]), "wz": wz, "wglu": np.ascontiguousarray(inp["w_glu"][0]),
                     "wout": np.ascontiguousarray(inp["w_out_even"][0]), "npre": np.ascontiguousarray(inp["norm_pre"][0]),
                     "npost": np.ascontiguousarray(inp["norm_post"][0]), "gnw": np.ascontiguousarray(inp["gdn_norm_w"][0]),
                     "ident": _IDENT})
    res = run_bass_kernel_spmd(nc, maps, core_ids=list(range(8)))
    x1 = np.empty((2, 8192, 1024), np.float32)
    for c in range(8):
        b, r = divmod(c, 4)
        x1[b, r * 2048:(r + 1) * 2048] = res.results[c]["out"]
    return x1


def run_L3(inp, x1):
    nc = _get("L3", build_L3)
    maps = []
    for c in range(8):
        b, r = divmod(c, 4)
        xh = np.zeros((2048 + 128, 1024), np.float32)
        xh[128:] = x1[b, r * 2048:(r + 1) * 2048]
        if r > 0:
            xh[:128] = x1[b, r * 2048 - 128:r * 2048]
        maps.append({"x": xh, "win": np.ascontiguousarray(inp["w_in_odd"][0]), "wout": np.ascontiguousarray(inp["w_out_odd"][0]),
                     "conv": np.ascontiguousarray(inp["conv_short"][0]), "npre": np.ascontiguousarray(inp["norm_pre"][1]),
                     "npost": np.ascontiguousarray(inp["norm_post"][1]), "ident": _IDENT})
    res = run_bass_kernel_spmd(nc, maps, core_ids=list(range(8)))
    out = np.empty((2, 8192, 1024), np.float32)
    for c in range(8):
        b, r = divmod(c, 4)
        out[b, r * 2048:(r + 1) * 2048] = res.results[c]["out"]
    return out


I32 = mybir.dt.int32
TAUS = np.array(list(range(17)) + [32, 64, 128, 256, 512, 1024, 2048, 4096] + list(range(15, -1, -1)), np.float32)
NTAU = len(TAUS)


def _s5_consts():
    mk = np.zeros((128, 2, 16, 16), np.float32)
    idm = np.zeros((128, 2, 16, 16), np.float32)
    for kt2 in range(2):
        for sp in range(8):
            s = kt2 * 8 + sp
            for h in range(16):
                mk[sp * 16 + h, kt2, s:, :] = 1.0
                idm[sp * 16 + h, kt2, s, h] = 1.0
    return mk.reshape(128, 2, 256), idm.reshape(128, 2, 256)


def barrier(P):
    for e in P.ENG:
        for e2 in P.ENG:
            if P.cnt[e2] > 0:
                P._wait(e, ("c", e2, P.cnt[e2]))
        for q in P.dsem:
            j1 = P.dcnt[q]
            for j in range(max(0, j1 - NDS), j1):
                P._wait(e, ("d", q, j % NDS, 16 * (j // NDS + 1)))


def build_L1b(S=8192, fz=None):
    nc = fz["nc"] if fz else bass.Bass("TRN2", target_bir_lowering=False)
    pfx = fz["pfx"] if fz else ""

    def D(name, shape):
        if fz and name in fz["share"]:
            return fz["share"][name]
        return nc.dram_tensor(pfx + name, shape, F32, kind="ExternalInput").ap()
    x_d = D("x", [S, 1024]); npre_d = D("npre", [1024]); wu_d = D("wu", [1024, 256])
    lre_d = D("lre", [16, 64]); lim_d = D("lim", [16, 64]); bre_d = D("bre", [16, 64, 16]); bim_d = D("bim", [16, 64, 16])
    cre_d = D("cre", [16, 16, 64]); cim_d = D("cim", [16, 16, 64]); ldt_d = D("ldt", [16]); dd_d = D("dd", [256])
    taus_d = D("taus", [NTAU]); mk_d = D("mk", [128, 2, 256]); idm_d = D("idm", [128, 2, 256]); ident_d = D("ident", [128, 128])
    ys_d = fz["out"] if fz else nc.dram_tensor("ys", [S, 256], F32, kind="ExternalOutput").ap()
    NCH = S // 16
    NST = S // 512
    with ExitStack() as st:
        C = Ctx(nc, st, fz["P"], pfx) if fz else Ctx(nc, st); P = C.P
        idf, bidf, idb, bidb = make_ident(C, ident_d)
        ptr, bptr = C.ps("ptr", [128, 1024], BF16)
        py, bpy = C.ps("py", [128, 1024])
        G = []; bG = []
        for i in range(4):
            t_, b_ = C.ps("g%d" % i, [128, 512]); G.append(t_); bG.append(b_)
        U, bU = C.sb("U", [128, 2, 16, NCH], BF16)
        with ExitStack() as st2:
            C2 = Ctx(nc, st2, P, C.pfx)
            ext = fz.get("uTp") if fz else None
            if ext:
                uTp, buTp = ext
            else:
                uTp, buTp = C2.sb("uTp", [128, 2, 16, NCH], BF16)
            with ExitStack() as st1:
                C1 = Ctx(nc, st1, P, C.pfx)
                npre, bnpre = bcast_row_load(C1, "npre", npre_d, 1024)
                wu, bwu = load_w_bf16(C1, "wu", wu_d, 8, 256)
                xt, bxt = C1.sb("xt", [128, 1024])
                sq, bsq = C1.sb("sq", [128, 1024])
                hn, bhn = C1.sb("hn", [128, 1024], BF16)
                ss, bss = C1.sb("ss", [128, 1])
                hT, bhT = C1.sb("hT", [128, 8, 512], BF16)
                for s_ in range(0 if ext else NST):
                    for t in range(4):
                        r0 = s_ * 512 + t * 128
                        P.dma("sp", xt[:], x_d[r0:r0 + 128, :], writes=[bxt])
                        rms_rstd(C1, xt[:], bxt, 1024, sq[:], bsq, ss, bss)
                        P.op("dve", lambda E: E.scalar_tensor_tensor(out=hn[:], in0=xt[:], scalar=ss[:, 0:1], in1=npre[:],
                                                                     op0=ALU.mult, op1=ALU.mult), reads=[bxt, bss, bnpre], writes=[bhn])
                        transpose8(C1, hn, bhn, idb, bidb, ptr, bptr, hT[:, :, t * 128:(t + 1) * 128], bhT, eng="act")
                    for blk in range(2):
                        pb = G[blk]
                        fns = [(lambda E, kt=kt, blk=blk, pb=pb: E.matmul(
                            pb[:].rearrange("p (s n) -> p s n", s=16), lhsT=wu[:, kt, blk * 128:(blk + 1) * 128],
                            rhs=hT[:, kt, :].rearrange("p (n s) -> p s n", s=16), start=(kt == 0), stop=(kt == 7))) for kt in range(8)]
                        P.mm_group(fns, reads=[bwu, bhT], writes=[bG[blk]])
                        P.op("act" if blk == 0 else "dve",
                             (lambda E, blk=blk, pb=pb, s_=s_: E.copy(out=uTp[:, blk, :, 32 * s_:32 * s_ + 32], in_=pb[:].rearrange("p (s n) -> p s n", s=16)))
                             if blk == 0 else
                             (lambda E, blk=blk, pb=pb, s_=s_: E.tensor_copy(out=uTp[:, blk, :, 32 * s_:32 * s_ + 32], in_=pb[:].rearrange("p (s n) -> p s n", s=16))),
                             reads=[bG[blk]], writes=[buTp])
                barrier(P)
            ud2 = nc.dram_tensor(pfx + "ud2", [16, 2, 8, 16, NCH], BF16)
            bud2 = Buf("ud2", multi=True)
            bU.multi = True; bU.w = []
            for g in range(16):
                P.dma("sp", ud2.ap()[g].rearrange("k sp h n -> h (k sp) n"),
                      uTp[(g % 8) * 16:(g % 8 + 1) * 16, g // 8, :, :], reads=[buTp], writes=[bud2])
            for g in range(16):
                P.dma("sp", U[:, :, g, :], ud2.ap()[g].rearrange("k sp h n -> (sp h) k n"), reads=[bud2], writes=[bU])
            barrier(P)
        lre, blre = C.sb("lre", [128, 8]); lim, blim = C.sb("lim", [128, 8]); ldt, bldt = C.sb("ldt", [128, 8])
        TAU, bTAU = bcast_row_load(C, "TAU", taus_d, NTAU)
        Er, bEr = C.sb("Er", [128, 8, NTAU]); Ei, bEi = C.sb("Ei", [128, 8, NTAU]); NEi, bNEi = C.sb("NEi", [128, 8, NTAU])
        Hr, bHr = C.sb("Hr", [128, 8, 17, 16]); nHi, bnHi = C.sb("nHi", [128, 8, 17, 16])
        WbT, bWbT = C.sb("WbT", [128, 2, 8, 2, 128], BF16)
        Toep, bToep = C.sb("Toep", [128, 2, 16, 256], BF16)
        with ExitStack() as st3:
            C3 = Ctx(nc, st3, P, C.pfx)
            Br, bBr = C3.sb("Br", [128, 8, 16]); Bi, bBi = C3.sb("Bi", [128, 8, 16])
            Cr, bCr = C3.sb("Cr", [128, 8, 16]); Ci, bCi = C3.sb("Ci", [128, 8, 16])
            dcol, bdcol = C3.sb("dcol", [128, 16])
            MK, bMK = C3.sb("MK", [128, 2, 256]); IDM, bIDM = C3.sb("IDM", [128, 2, 256])
            P.dma("sp", MK[:], mk_d, writes=[bMK]); P.dma("sp", IDM[:], idm_d, writes=[bIDM])
            for two in range(2):
                hs = slice(64 * two, 64 * two + 64)
                P.dma("sp", lre[hs, :], lre_d.rearrange("(gp two) p -> two p gp", two=2)[two], writes=[blre])
                P.dma("sp", lim[hs, :], lim_d.rearrange("(gp two) p -> two p gp", two=2)[two], writes=[blim])
                P.dma("sp", ldt[hs, :], ldt_d.rearrange("(gp two) -> two gp", two=2)[two].partition_broadcast(64), writes=[bldt])
                P.dma("sp", Br[hs], bre_d.rearrange("(gp two) p h -> two p gp h", two=2)[two], writes=[bBr])
                P.dma("sp", Bi[hs], bim_d.rearrange("(gp two) p h -> two p gp h", two=2)[two], writes=[bBi])
                for gp in range(8):
                    P.dma("sp", Cr[hs, gp, :], cre_d[2 * gp + two].rearrange("h p -> p h"), writes=[bCr])
                    P.dma("sp", Ci[hs, gp, :], cim_d[2 * gp + two].rearrange("h p -> p h"), writes=[bCi])
            for sp in range(8):
                P.dma("sp", dcol[sp * 16:(sp + 1) * 16, :], dd_d.rearrange("(g h) -> h g", h=16), writes=[bdcol])
            sm = {}
            for nm in ("dt", "lr", "lrdt", "th", "den", "nr", "fre", "fim", "t8a", "t8b"):
                sm[nm] = C3.sb("sm_" + nm, [128, 8])
            T41 = {}
            for nm in ("ARG", "MARG", "MAG", "MAGN", "SIN", "COS", "ErN", "EiN", "rt", "rk"):
                T41[nm] = C3.sb("t41_" + nm, [128, 8, NTAU])
            rki, brki = C3.sb("rki", [128, 8, NTAU], I32)

            def tt(eng, out, bo, a, ba, b, bb_, op):
                P.op(eng, lambda E: E.tensor_tensor(out=out, in0=a, in1=b, op=op), reads=[ba, bb_], writes=[bo])

            dt, bdt = sm["dt"]; lr, blr = sm["lr"]; lrdt, blrdt = sm["lrdt"]; th, bth = sm["th"]
            P.op("act", lambda E: E.activation(out=dt[:], in_=ldt[:], func=AF.Exp), reads=[bldt], writes=[bdt])
            P.op("dve", lambda E: E.tensor_scalar(out=lr[:], in0=lre[:], scalar1=-1e-4, scalar2=None, op0=ALU.min), reads=[blre], writes=[blr])
            tt("dve", lrdt[:], blrdt, lr[:], blr, dt[:], bdt, ALU.mult)
            tt("dve", th[:], bth, lim[:], blim, dt[:], bdt, ALU.mult)
            ARG, bARG = T41["ARG"]; MARG, bMARG = T41["MARG"]; MAG, bMAG = T41["MAG"]; MAGN, bMAGN = T41["MAGN"]
            SIN, bSIN = T41["SIN"]; COS, bCOS = T41["COS"]; ErN, bErN = T41["ErN"]; EiN, bEiN = T41["EiN"]
            rt, brt = T41["rt"]; rk, brk = T41["rk"]
            tb = TAU[:].unsqueeze(1).to_broadcast([128, 8, NTAU])
            tt("dve", ARG[:], bARG, th[:].unsqueeze(2).to_broadcast([128, 8, NTAU]), bth, tb, bTAU, ALU.mult)
            tt("dve", MARG[:], bMARG, lrdt[:].unsqueeze(2).to_broadcast([128, 8, NTAU]), blrdt, tb, bTAU, ALU.mult)
            P.op("act", lambda E: E.activation(out=MAG[:], in_=MARG[:], func=AF.Exp), reads=[bMARG], writes=[bMAG])
            P.op("act", lambda E: E.activation(out=MAGN[:, :, 0:17], in_=MARG[:, :, 0:17], func=AF.Exp, scale=-1.0), reads=[bMARG], writes=[bMAGN])

            def sin_of(dst, bdst, shift):
                P.op("dve", lambda E: E.tensor_scalar(out=rt[:], in0=ARG[:], scalar1=float(shift), scalar2=None, op0=ALU.add), reads=[bARG], writes=[brt])
                P.op("dve", lambda E: E.tensor_scalar(out=rki[:], in0=rt[:], scalar1=float(1.0 / (2 * np.pi)), scalar2=None, op0=ALU.mult), reads=[brt], writes=[brki])
                P.op("dve", lambda E: E.tensor_copy(out=rk[:], in_=rki[:]), reads=[brki], writes=[brk])
                P.op("dve", lambda E: E.scalar_tensor_tensor(out=rt[:], in0=rk[:], scalar=float(-2 * np.pi), in1=rt[:], op0=ALU.mult, op1=ALU.add),
                     reads=[brk, brt], writes=[brt])
                P.op("dve", lambda E: E.tensor_scalar(out=rt[:], in0=rt[:], scalar1=-3.14159, scalar2=3.14159, op0=ALU.max, op1=ALU.min), reads=[brt], writes=[brt])
                P.op("act", lambda E: E.activation(out=dst[:], in_=rt[:], func=AF.Sin), reads=[brt], writes=[bdst])

            sin_of(SIN, bSIN, 0.0)
            sin_of(COS, bCOS, np.pi / 2)
            tt("dve", Er[:], bEr, MAG[:], bMAG, COS[:], bCOS, ALU.mult)
            tt("dve", Ei[:], bEi, MAG[:], bMAG, SIN[:], bSIN, ALU.mult)
            P.op("dve", lambda E: E.tensor_scalar(out=NEi[:], in0=Ei[:], scalar1=-1.0, scalar2=None, op0=ALU.mult), reads=[bEi], writes=[bNEi])
            tt("dve", ErN[:, :, 0:17], bErN, MAGN[:, :, 0:17], bMAGN, COS[:, :, 0:17], bCOS, ALU.mult)
            tt("dve", EiN[:, :, 0:17], bEiN, MAGN[:, :, 0:17], bMAGN, SIN[:, :, 0:17], bSIN, ALU.mult)
            P.op("dve", lambda E: E.tensor_scalar(out=EiN[:, :, 0:17], in0=EiN[:, :, 0:17], scalar1=-1.0, scalar2=None, op0=ALU.mult), reads=[bEiN], writes=[bEiN])
            den, bden = sm["den"]; nr, bnr = sm["nr"]; fre, bfre = sm["fre"]; fim, bfim = sm["fim"]; t8a, bt8a = sm["t8a"]; t8b, bt8b = sm["t8b"]
            tt("dve", den[:], bden, lr[:], blr, lr[:], blr, ALU.mult)
            tt("dve", t8a[:], bt8a, lim[:], blim, lim[:], blim, ALU.mult)
            tt("dve", den[:], bden, den[:], bden, t8a[:], bt8a, ALU.add)
            P.op("dve", lambda E: E.reciprocal(out=den[:], in_=den[:]), reads=[bden], writes=[bden])
            P.op("dve", lambda E: E.tensor_scalar(out=nr[:], in0=Er[:, :, 1], scalar1=-1.0, scalar2=None, op0=ALU.add), reads=[bEr], writes=[bnr])
            tt("dve", fre[:], bfre, nr[:], bnr, lr[:], blr, ALU.mult)
            tt("dve", t8a[:], bt8a, Ei[:, :, 1], bEi, lim[:], blim, ALU.mult)
            tt("dve", fre[:], bfre, fre[:], bfre, t8a[:], bt8a, ALU.add)
            tt("dve", fre[:], bfre, fre[:], bfre, den[:], bden, ALU.mult)
            tt("dve", fim[:], bfim, Ei[:, :, 1], bEi, lr[:], blr, ALU.mult)
            tt("dve", t8b[:], bt8b, nr[:], bnr, lim[:], blim, ALU.mult)
            tt("dve", fim[:], bfim, fim[:], bfim, t8b[:], bt8b, ALU.subtract)
            tt("dve", fim[:], bfim, fim[:], bfim, den[:], bden, ALU.mult)

            def cmul(outr, boutr, outi, bouti, ar, bar, ai, bai, br_, bbr_, bi_, bbi_, tmp, btmp):
                tt("dve", outr, boutr, ar, bar, br_, bbr_, ALU.mult)
                tt("dve", tmp, btmp, ai, bai, bi_, bbi_, ALU.mult)
                tt("dve", outr, boutr, outr, boutr, tmp, btmp, ALU.subtract)
                tt("dve", outi, bouti, ar, bar, bi_, bbi_, ALU.mult)
                tt("dve", tmp, btmp, ai, bai, br_, bbr_, ALU.mult)
                tt("dve", outi, bouti, outi, bouti, tmp, btmp, ALU.add)

            bbr, bbbr = C3.sb("bbr", [128, 8, 16]); bbi, bbbi = C3.sb("bbi", [128, 8, 16]); tmp16, btmp16 = C3.sb("tmp16", [128, 8, 16])
            fb = lambda t_: t_[:].unsqueeze(2).to_broadcast([128, 8, 16])
            cmul(bbr[:], bbbr, bbi[:], bbbi, fb(fre), bfre, fb(fim), bfim, Br[:], bBr, Bi[:], bBi, tmp16[:], btmp16)
            Gr, bGr = C3.sb("Gr", [128, 8, 16, 16]); Gi, bGi = C3.sb("Gi", [128, 8, 16, 16])
            WPr, bWPr = C3.sb("WPr", [128, 8, 16, 16]); WPi, bWPi = C3.sb("WPi", [128, 8, 16, 16])
            Hi, bHi = C3.sb("Hi", [128, 8, 17, 16]); tmpH, btmpH = C3.sb("tmpH", [128, 8, 17, 16])
            eb = lambda t_, j0, j1: t_[:, :, j0:j1].unsqueeze(3).to_broadcast([128, 8, j1 - j0, 16])
            vb = lambda t_, n_: t_[:].unsqueeze(2).to_broadcast([128, 8, n_, 16])
            cmul(Gr[:], bGr, Gi[:], bGi, eb(ErN, 0, 16), bErN, eb(EiN, 0, 16), bEiN, vb(bbr, 16), bbbr, vb(bbi, 16), bbbi, tmpH[:, :, 0:16, :], btmpH)
            cmul(WPr[:], bWPr, WPi[:], bWPi, eb(Er, 25, 41), bEr, eb(Ei, 25, 41), bEi, vb(bbr, 16), bbbr, vb(bbi, 16), bbbi, tmpH[:, :, 0:16, :], btmpH)
            cmul(Hr[:], bHr, Hi[:], bHi, eb(Er, 0, 17), bEr, eb(Ei, 0, 17), bEi, vb(Cr, 17), bCr, vb(Ci, 17), bCi, tmpH[:], btmpH)
            P.op("dve", lambda E: E.tensor_scalar(out=nHi[:], in0=Hi[:], scalar1=-1.0, scalar2=None, op0=ALU.mult), reads=[bHi], writes=[bnHi])
            for gp in range(8):
                for kt2 in range(2):
                    for c, (WP_, bWP_) in enumerate(((WPr, bWPr), (WPi, bWPi))):
                        P.op("pe", lambda E, gp=gp, kt2=kt2, WP_=WP_: E.transpose(
                            out=G[2][:, 0:128], in_=WP_[:, gp, kt2 * 8:(kt2 + 1) * 8, :].rearrange("p s h -> p (s h)"), identity=idf[:]),
                            reads=[bWP_, bidf], writes=[bG[2]])
                        P.op("act", lambda E, gp=gp, kt2=kt2, c=c: E.copy(out=WbT[:, kt2, gp, c, :], in_=G[2][:, 0:128]), reads=[bG[2]], writes=[bWbT])
            tmpT, btmpT = C3.sb("tmpT", [128, 256])
            for g in range(16):
                gp = g // 2; hs = slice(64 * (g % 2), 64 * (g % 2) + 64)
                for kt2 in range(2):
                    fns = [
                        lambda E, gp=gp, hs=hs, kt2=kt2: E.matmul(G[3][:, 0:256], lhsT=Gr[hs, gp, kt2 * 8:(kt2 + 1) * 8, :].rearrange("p s h -> p (s h)"),
                                                                  rhs=Hr[hs, gp, 0:16, :].rearrange("p t h -> p (t h)"), start=True, stop=False),
                        lambda E, gp=gp, hs=hs, kt2=kt2: E.matmul(G[3][:, 0:256], lhsT=Gi[hs, gp, kt2 * 8:(kt2 + 1) * 8, :].rearrange("p s h -> p (s h)"),
                                                                  rhs=nHi[hs, gp, 0:16, :].rearrange("p t h -> p (t h)"), start=False, stop=True)]
                    P.mm_group(fns, reads=[bGr, bGi, bHr, bnHi], writes=[bG[3]])
                    P.op("dve", lambda E, kt2=kt2: E.tensor_tensor(out=tmpT[:], in0=G[3][:, 0:256], in1=MK[:, kt2, :], op=ALU.mult),
                         reads=[bG[3], bMK], writes=[btmpT])
                    P.op("dve", lambda E, kt2=kt2, g=g: E.scalar_tensor_tensor(out=Toep[:, kt2, g, :], in0=IDM[:, kt2, :], scalar=dcol[:, g:g + 1], in1=tmpT[:],
                                                                               op0=ALU.mult, op1=ALU.add), reads=[bIDM, bdcol, btmpT], writes=[bToep])
            barrier(P)
        X = {}
        for bufn in ("A", "B"):
            for c in ("re", "im"):
                X[(bufn, c)] = (C.sb("X%s%s" % (bufn, c), [128, 8, NCH + 1])[0], [Buf("X%s%s%d" % (bufn, c, gp)) for gp in range(8)])
        Ysb, bYsb = C.sb("Ysb", [128, 16, 256])
        for key in X:
            t_, bl = X[key]
            P.op("dve", lambda E, t_=t_: E.memset(t_[:, :, 0:1], 0.0), writes=bl)
        for gp in range(8):
            for c, cn in enumerate(("re", "im")):
                px = G[c]
                fns = []
                for two in range(2):
                    g = 2 * gp + two
                    for kt2 in range(2):
                        fns.append(lambda E, two=two, g=g, kt2=kt2, gp=gp, c=c, px=px: E.matmul(
                            px[64 * two:64 * two + 64, :], lhsT=WbT[:, kt2, gp, c, 64 * two:64 * two + 64], rhs=U[:, kt2, g, :],
                            start=(kt2 == 0), stop=(kt2 == 1)))
                P.mm_group(fns, reads=[bWbT, bU], writes=[bG[c]])
                xt_, xb_ = X[("A", cn)]
                P.op("act", lambda E, xt_=xt_, gp=gp, px=px: E.copy(out=xt_[:, gp, 1:NCH + 1], in_=px[:]), reads=[bG[c]], writes=[xb_[gp]])
        for k in range(9):
            d = 1 << k
            j = 16 if k == 0 else 16 + k
            src, dst = ("A", "B") if k % 2 == 0 else ("B", "A")
            sre, bsre = X[(src, "re")]; sim, bsim = X[(src, "im")]
            dre, bdre = X[(dst, "re")]; dim_, bdim = X[(dst, "im")]
            P.op("dve", lambda E, dre=dre, sre=sre, d=d: E.tensor_copy(out=dre[:, :, 1:1 + d], in_=sre[:, :, 1:1 + d]), reads=bsre, writes=bdre)
            P.op("pool", lambda E, dim_=dim_, sim=sim, d=d: E.tensor_copy(out=dim_[:, :, 1:1 + d], in_=sim[:, :, 1:1 + d]), reads=bsim, writes=bdim)
            for gp in range(8):
                lo = slice(1, NCH + 1 - d); hi = slice(1 + d, NCH + 1)
                P.op("dve", lambda E, gp=gp, j=j, dre=dre, sre=sre, lo=lo, hi=hi: E.scalar_tensor_tensor(
                    out=dre[:, gp, hi], in0=sre[:, gp, lo], scalar=Er[:, gp, j:j + 1], in1=sre[:, gp, hi], op0=ALU.mult, op1=ALU.add),
                    reads=[bsre[gp], bEr], writes=[bdre[gp]])
                P.op("dve", lambda E, gp=gp, j=j, dre=dre, sim=sim, lo=lo, hi=hi: E.scalar_tensor_tensor(
                    out=dre[:, gp, hi], in0=sim[:, gp, lo], scalar=NEi[:, gp, j:j + 1], in1=dre[:, gp, hi], op0=ALU.mult, op1=ALU.add),
                    reads=[bsim[gp], bNEi, bdre[gp]], writes=[bdre[gp]])
                P.op("dve", lambda E, gp=gp, j=j, dim_=dim_, sim=sim, lo=lo, hi=hi: E.scalar_tensor_tensor(
                    out=dim_[:, gp, hi], in0=sim[:, gp, lo], scalar=Er[:, gp, j:j + 1], in1=sim[:, gp, hi], op0=ALU.mult, op1=ALU.add),
                    reads=[bsim[gp], bEr], writes=[bdim[gp]])
                P.op("dve", lambda E, gp=gp, j=j, dim_=dim_, sre=sre, lo=lo, hi=hi: E.scalar_tensor_tensor(
                    out=dim_[:, gp, hi], in0=sre[:, gp, lo], scalar=Ei[:, gp, j:j + 1], in1=dim_[:, gp, hi], op0=ALU.mult, op1=ALU.add),
                    reads=[bsre[gp], bEi, bdim[gp]], writes=[bdim[gp]])
        fre_, bfre_ = X[("B", "re")]; fim_, bfim_ = X[("B", "im")]
        bys = None if fz else Buf("ys", multi=True)
        ysv = ys_d.rearrange("(n t) c -> n t c", t=16)
        for jt in range(NCH // 128):
            for gq in range(4):
                fns = []
                for gi in range(4):
                    g = 4 * gq + gi; gp = g // 2; hs = slice(64 * (g % 2), 64 * (g % 2) + 64)
                    o_ = (gi * 256, (gi + 1) * 256)
                    for kt2 in range(2):
                        fns.append(lambda E, o_=o_, g=g, kt2=kt2, jt=jt: E.matmul(
                            py[:, o_[0]:o_[1]], lhsT=U[:, kt2, g, jt * 128:(jt + 1) * 128], rhs=Toep[:, kt2, g, :], start=(kt2 == 0), stop=False))
                    fns.append(lambda E, o_=o_, gp=gp, hs=hs, jt=jt: E.matmul(
                        py[:, o_[0]:o_[1]], lhsT=fre_[hs, gp, jt * 128:(jt + 1) * 128], rhs=Hr[hs, gp, 1:17, :].rearrange("p t h -> p (t h)"),
                        start=False, stop=False))
                    fns.append(lambda E, o_=o_, gp=gp, hs=hs, jt=jt: E.matmul(
                        py[:, o_[0]:o_[1]], lhsT=fim_[hs, gp, jt * 128:(jt + 1) * 128], rhs=nHi[hs, gp, 1:17, :].rearrange("p t h -> p (t h)"),
                        start=False, stop=True))
                P.mm_group(fns, reads=[bU, bToep, bHr, bnHi] + bfre_ + bfim_, writes=[bpy])
                P.op("act" if gq % 2 == 0 else "dve",
                     (lambda E, gq=gq: E.copy(out=Ysb[:].rearrange("p t (g h) -> p g t h", h=16)[:, 4 * gq:4 * gq + 4],
                                              in_=py[:].rearrange("p (g t h) -> p g t h", g=4, h=16)))
                     if gq % 2 == 0 else
                     (lambda E, gq=gq: E.tensor_copy(out=Ysb[:].rearrange("p t (g h) -> p g t h", h=16)[:, 4 * gq:4 * gq + 4],
                                                     in_=py[:].rearrange("p (g t h) -> p g t h", g=4, h=16))),
                     reads=[bpy], writes=[bYsb])
            P.dma("sp", ysv[jt * 128:(jt + 1) * 128, :, :], Ysb[:], reads=[bYsb], writes=[fz["obuf_of"](jt) if fz else bys])
            if fz:
                fz["after_chunk"](jt)
        if fz:
            barrier(P)
        else:
            P.finish([bys])
    return nc


def run_L1b(inp):
    nc = _get("L1b", build_L1b)
    mk, idm = _s5_consts()
    w_in = inp["w_in_even"][0]
    maps = []
    for c in range(8):
        b, r = divmod(c, 4)
        gs = slice(16 * r, 16 * r + 16)
        maps.append({"x": np.ascontiguousarray(inp["x"][b]), "npre": np.ascontiguousarray(inp["norm_pre"][0]),
                     "wu": np.ascontiguousarray(w_in[:, 4112 + 256 * r:4112 + 256 * (r + 1)]),
                     "lre": np.ascontiguousarray(inp["s5_lam_re"][0, gs]), "lim": np.ascontiguousarray(inp["s5_lam_im"][0, gs]),
                     "bre": np.ascontiguousarray(inp["s5_b_re"][0, gs]), "bim": np.ascontiguousarray(inp["s5_b_im"][0, gs]),
                     "cre": np.ascontiguousarray(inp["s5_c_re"][0, gs]), "cim": np.ascontiguousarray(inp["s5_c_im"][0, gs]),
                     "ldt": np.ascontiguousarray(inp["s5_log_dt"][0, gs]), "dd": np.ascontiguousarray(inp["s5_d"][0, 256 * r:256 * (r + 1)]),
                     "taus": TAUS, "mk": mk, "idm": idm, "ident": _IDENT})
    res = run_bass_kernel_spmd(nc, maps, core_ids=list(range(8)))
    ys = np.empty((2, 8192, 1024), np.float32)
    for c in range(8):
        b, r = divmod(c, 4)
        ys[b, :, 256 * r:256 * (r + 1)] = res.results[c]["ys"]
    return ys


def _gdn_consts():
    p = np.arange(64)[:, None]; f = np.arange(64)[None, :]
    negu = np.where(f >= p, 0.0, -30000.0)
    negls = np.where(f < p, 0.0, -30000.0)
    nsu = np.where(f > p, -1.0, 0.0)
    i64 = np.eye(64)
    c64 = np.stack([negu, negls, nsu, i64], axis=1).astype(np.float32)
    cmask = np.ones((2, 512), np.float32); cmask[:, 0::64] = 0.0
    sel = np.zeros((2, 2, 128), np.float32); sel[0, 0, :] = 1.0; sel[1, 1, :] = 1.0
    return c64, cmask, sel


def build_L1a(S=8192, fz=None):
    nc = fz["nc"] if fz else bass.Bass("TRN2", target_bir_lowering=False)
    pfx = fz["pfx"] if fz else ""

    def D(name, shape):
        if fz and name in fz["share"]:
            return fz["share"][name]
        return nc.dram_tensor(pfx + name, shape, F32, kind="ExternalInput").ap()
    x_d = D("x", [S, 1024]); npre_d = D("npre", [1024]); w_d = D("w", [1024, 768]); wb_d = D("wb", [1024, 2]); wa_d = D("wa", [1024, 2])
    conv_d = D("conv", [4, 768]); alog_d = D("alog", [2]); dtb_d = D("dtb", [2])
    ident_d = D("ident", [128, 128]); c64_d = D("c64", [64, 4, 64]); cmask_d = D("cmask", [2, 512]); sel_d = D("sel", [2, 2, 128])
    ones_d = D("ones", [128, 128])
    o_d = fz["out"] if fz else nc.dram_tensor("o", [S, 256], F32, kind="ExternalOutput").ap()
    NST = S // 512
    with ExitStack() as st:
        C = Ctx(nc, st, fz["P"], pfx) if fz else Ctx(nc, st); P = C.P
        idf, bidf, idb, bidb = make_ident(C, ident_d)
        npre, bnpre = bcast_row_load(C, "npre", npre_d, 1024)
        w, bw = load_w_bf16(C, "w", w_d, 8, 768)
        wb, bwb = load_w_bf16(C, "wb", wb_d, 8, 2)
        wa, bwa = load_w_bf16(C, "wa", wa_d, 8, 2)
        cw, bcw = C.sb("cw", [128, 4, 6])
        P.dma("sp", cw[:], conv_d.rearrange("j (c p) -> p j c", p=128), writes=[bcw])
        extu = fz.get("uTp") if fz else None
        if extu:
            wu_d = D("wu", [1024, 256])
            wu, bwu = load_w_bf16(C, "wu", wu_d, 8, 256)
            uTp, buTp = extu
        c64, bc64 = C.sb("c64", [64, 4, 64]); P.dma("sp", c64[:], c64_d, writes=[bc64])
        NEGU = c64[:, 0, :]; NEGLS = c64[:, 1, :]; NSU = c64[:, 2, :]; I64 = c64[:, 3, :]
        cmask, bcmask = C.sb("cmask", [2, 512]); P.dma("sp", cmask[:], cmask_d, writes=[bcmask])
        sel, bsel = C.sb("sel", [2, 2, 128]); P.dma("sp", sel[:], sel_d, writes=[bsel])
        ones, bones = C.sb("ones", [128, 128]); P.dma("sp", ones[:], ones_d, writes=[bones])
        onesb, bonesb = C.sb("onesb", [128, 128], BF16)
        P.op("dve", lambda E: E.tensor_copy(out=onesb[:], in_=ones[:]), reads=[bones], writes=[bonesb])
        sqb, bsqb = C.sb("sqb", [128, 512], BF16)
        alog, balog = C.sb("alog", [2, 1]); P.dma("sp", alog[:], alog_d.rearrange("(a b) -> a b", b=1), writes=[balog])
        dtb, bdtb = C.sb("dtb", [2, 1]); P.dma("sp", dtb[:], dtb_d.rearrange("(a b) -> a b", b=1), writes=[bdtb])
        negA, bnegA = C.sb("negA", [2, 1])
        P.op("act", lambda E: E.activation(out=negA[:], in_=alog[:], func=AF.Exp), reads=[balog], writes=[bnegA])
        P.op("dve", lambda E: E.tensor_scalar(out=negA[:], in0=negA[:], scalar1=-1.0, scalar2=None, op0=ALU.mult), reads=[bnegA], writes=[bnegA])
        xt, bxt = C.sb("xt", [128, 1024]); sq, bsq = C.sb("sq", [128, 1024]); hn, bhn = C.sb("hn", [128, 1024], BF16)
        ss, bss = C.sb("ss", [128, 1]); hT, bhT = C.sb("hT", [128, 8, 512], BF16)
        raw, _ = C.sb("raw", [128, 6, 515]); braw = [Buf("raw%d" % i) for i in range(6)]
        cvq, bcvq = C.sb("cvq", [128, 512])
        act, _ = C.sb("act", [128, 4, 512]); bact = [Buf("act%d" % i) for i in range(4)]
        vbuf2 = []; qk2 = []; bqk2 = []
        for par_ in range(2):
            vt_, _ = C.sb("vbuf%d" % par_, [128, 2, 512]); vbuf2.append((vt_, [Buf("vb%d_%d" % (par_, i)) for i in range(2)]))
            qt_, _ = C.sb("qk%d" % par_, [128, 4, 512]); qk2.append(qt_); bqk2.append([Buf("qk%d_%d" % (par_, i)) for i in range(4)])
        rn, brn = C.sb("rn", [128, 512])
        brow, bbrow = C.sb("brow", [2, 512]); grow, bgrow = C.sb("grow", [2, 512]); gcrow, bgcrow = C.sb("gcrow", [2, 512])
        GCB2 = []; BB2 = []
        for par_ in range(2):
            GCB2.append([C.sb("GCB%d_%d" % (par_, h), [128, 512]) for h in range(2)])
            BB2.append([C.sb("BB%d_%d" % (par_, h), [128, 512]) for h in range(2)])
        m64 = {}
        for nm in ("arg1", "DT", "Ds", "tmp", "tmp2", "BBm"):
            m64[nm] = C.sb("m_" + nm, [64, 512])
        for nm in ("Pa", "Pb", "Qa", "Qb"):
            m64[nm] = C.sb("m_" + nm, [64, 512], BF16)
        heads = []
        for h in range(2):
            H = {}
            H["attnT"] = C.sb("attnT%d" % h, [64, 512], BF16); H["Y"] = C.sb("Y%d" % h, [64, 512]); H["Ybf"] = C.sb("Ybf%d" % h, [64, 512], BF16)
            H["EG"] = C.sb("EG%d" % h, [128, 512]); H["qdec"] = C.sb("qdec%d" % h, [128, 512], BF16)
            H["kTb"] = C.sb("kTb%d" % h, [128, 512], BF16); H["Sbf"] = C.sb("Sbf%d" % h, [128, 128], BF16)
            H["bv"] = C.sb("bv%d" % h, [64, 8, 128]); H["kdec"] = C.sb("kdec%d" % h, [64, 8, 128], BF16)
            H["nbg"] = C.sb("nbg%d" % h, [64, 8]); H["osb"] = C.sb("osb%d" % h, [128, 8, 128])
            H["vnew"] = C.sb("vnew%d" % h, [64, 128], BF16); H["rhs2"] = C.sb("rhs2%d" % h, [64, 128], BF16)
            heads.append(H)
        small = {}
        for nm in ("gccol", "bcol", "nbcol", "elast", "egc"):
            small[nm] = C.sb("s_" + nm, [64, 8])
        Sst = [C.sb("S%d" % h, [128, 128]) for h in range(2)]
        for h in range(2):
            P.op("dve", lambda E, h=h: E.memset(Sst[h][0][:], 0.0), writes=[Sst[h][1]])
            P.op("dve", lambda E, h=h: E.memset(heads[h]["Sbf"][0][:], 0.0), writes=[heads[h]["Sbf"][1]])
        P.op("dve", lambda E: E.memset(raw[:, :, 0:3], 0.0), writes=braw)
        ptr, bptr = C.ps("ptr", [128, 1024], BF16)
        G = [C.ps("gp%d" % i, [128, 512]) for i in range(7)]
        GP = G[0:4]
        GA = G[4:7]
        ga_ctr = [0]

        def next_ga():
            ga_ctr[0] += 1
            return GA[ga_ctr[0] % 3]
        bo = None if fz else Buf("o", multi=True)
        if fz is not None and fz.get("debug"):
            print("L1a sbuf remaining", nc.sbuf_bytes_remaining)

        def tt(out, bo_, a, ba, b, bb_, op, eng="dve"):
            P.op(eng, lambda E: E.tensor_tensor(out=out, in0=a, in1=b, op=op), reads=ba if isinstance(ba, list) else [ba], writes=[bo_])

        def stageA(s_):
            par = s_ % 2
            qk = qk2[par]; bqk = bqk2[par]; GCB = GCB2[par]; BB = BB2[par]; vb, bvb = vbuf2[par]
            for t in range(4):
                r0 = s_ * 512 + t * 128
                P.dma("sp", xt[:], x_d[r0:r0 + 128, :], writes=[bxt])
                rms_rstd(C, xt[:], bxt, 1024, sq[:], bsq, ss, bss)
                P.op("dve", lambda E: E.scalar_tensor_tensor(out=hn[:], in0=xt[:], scalar=ss[:, 0:1], in1=npre[:],
                                                             op0=ALU.mult, op1=ALU.mult), reads=[bxt, bss, bnpre], writes=[bhn])
                transpose8(C, hn, bhn, idb, bidb, ptr, bptr, hT[:, :, t * 128:(t + 1) * 128], bhT, eng="act")
                yield
            for ct in range(6):
                pa, bpa = next_ga()
                fns = [(lambda E, kt=kt, ct=ct, pa=pa: E.matmul(pa[:], lhsT=w[:, kt, ct * 128:(ct + 1) * 128], rhs=hT[:, kt, :],
                                                                start=(kt == 0), stop=(kt == 7))) for kt in range(8)]
                P.mm_group(fns, reads=[bw, bhT], writes=[bpa])
                P.op("act", lambda E, ct=ct, pa=pa: E.copy(out=raw[:, ct, 3:515], in_=pa[:]), reads=[bpa], writes=[braw[ct]])
                P.op("dve", lambda E, ct=ct: E.tensor_scalar(out=cvq[:], in0=raw[:, ct, 0:512], scalar1=cw[:, 0, ct:ct + 1], scalar2=None, op0=ALU.mult),
                     reads=[braw[ct], bcw], writes=[bcvq])
                for j in range(1, 4):
                    P.op("dve", lambda E, ct=ct, j=j: E.scalar_tensor_tensor(out=cvq[:], in0=raw[:, ct, j:j + 512], scalar=cw[:, j, ct:ct + 1], in1=cvq[:],
                                                                             op0=ALU.mult, op1=ALU.add), reads=[braw[ct], bcw, bcvq], writes=[bcvq])
                P.op("act", lambda E, ct=ct: E.copy(out=raw[:, ct, 0:3], in_=raw[:, ct, 512:515]), reads=[braw[ct]], writes=[braw[ct]])
                if ct < 4:
                    P.op("act", lambda E, ct=ct: E.activation(out=act[:, ct, :], in_=cvq[:], func=AF.Silu), reads=[bcvq], writes=[bact[ct]])
                else:
                    P.op("act", lambda E, ct=ct, vb=vb: E.activation(out=vb[:, ct - 4, :], in_=cvq[:], func=AF.Silu), reads=[bcvq], writes=[bvb[ct - 4]])
                yield
            if extu:
                for blk in range(2):
                    pa, bpa = next_ga()
                    fns = [(lambda E, kt=kt, blk=blk, pa=pa: E.matmul(
                        pa[:].rearrange("p (s n) -> p s n", s=16), lhsT=wu[:, kt, blk * 128:(blk + 1) * 128],
                        rhs=hT[:, kt, :].rearrange("p (n s) -> p s n", s=16), start=(kt == 0), stop=(kt == 7))) for kt in range(8)]
                    P.mm_group(fns, reads=[bwu, bhT], writes=[bpa])
                    P.op("act", lambda E, blk=blk, pa=pa, s_=s_: E.copy(out=uTp[:, blk, :, 32 * s_:32 * s_ + 32], in_=pa[:].rearrange("p (s n) -> p s n", s=16)),
                         reads=[bpa], writes=[buTp])
                    yield
            for ct in range(4):
                pa, bpa = next_ga()
                P.op("act", lambda E, ct=ct: E.activation(out=sqb[:], in_=act[:, ct, :], func=AF.Square), reads=[bact[ct]], writes=[bsqb])
                P.op("pe", lambda E, pa=pa: E.matmul(pa[:], lhsT=onesb[:], rhs=sqb[:], start=True, stop=True), reads=[bonesb, bsqb], writes=[bpa])
                P.op("act", lambda E, pa=pa: E.activation(out=rn[:], in_=pa[:], func=AF.Ln, bias=1e-6, scale=1.0), reads=[bpa], writes=[brn])
                P.op("act", lambda E: E.activation(out=rn[:], in_=rn[:], func=AF.Exp, scale=-0.5), reads=[brn], writes=[brn])
                if ct < 2:
                    P.op("dve", lambda E, ct=ct, qk=qk: E.scalar_tensor_tensor(out=qk[:, ct, :], in0=act[:, ct, :], scalar=float(128 ** -0.5), in1=rn[:],
                                                                               op0=ALU.mult, op1=ALU.mult), reads=[bact[ct], brn], writes=[bqk[ct]])
                else:
                    P.op("dve", lambda E, ct=ct, qk=qk: E.tensor_tensor(out=qk[:, ct, :], in0=act[:, ct, :], in1=rn[:], op=ALU.mult),
                         reads=[bact[ct], brn], writes=[bqk[ct]])
                yield
            pa, bpa = next_ga()
            fns = [(lambda E, kt=kt, pa=pa: E.matmul(pa[0:2, :], lhsT=wb[:, kt, 0:2], rhs=hT[:, kt, :], start=(kt == 0), stop=(kt == 7))) for kt in range(8)]
            P.mm_group(fns, reads=[bwb, bhT], writes=[bpa])
            P.op("act", lambda E, pa=pa: E.activation(out=brow[:], in_=pa[0:2, :], func=AF.Sigmoid), reads=[bpa], writes=[bbrow])
            pa2, bpa2 = next_ga()
            fns = [(lambda E, kt=kt, pa2=pa2: E.matmul(pa2[0:2, :], lhsT=wa[:, kt, 0:2], rhs=hT[:, kt, :], start=(kt == 0), stop=(kt == 7))) for kt in range(8)]
            P.mm_group(fns, reads=[bwa, bhT], writes=[bpa2])
            P.op("act", lambda E, pa2=pa2: E.activation(out=grow[:], in_=pa2[0:2, :], func=AF.Exp, bias=dtb[:, 0:1], scale=1.0), reads=[bpa2, bdtb], writes=[bgrow])
            P.op("act", lambda E: E.activation(out=grow[:], in_=grow[:], func=AF.Ln, bias=1.0, scale=1.0), reads=[bgrow], writes=[bgrow])
            P.op("dve", lambda E: E.tensor_scalar(out=grow[:], in0=grow[:], scalar1=negA[:, 0:1], scalar2=None, op0=ALU.mult), reads=[bgrow, bnegA], writes=[bgrow])
            P.op("dve", lambda E: E.tensor_tensor_scan(out=gcrow[:], data0=cmask[:], data1=grow[:], initial=0.0, op0=ALU.mult, op1=ALU.add),
                 reads=[bcmask, bgrow], writes=[bgcrow])
            yield
            for h in range(2):
                pa, bpa = next_ga()
                P.op("pe", lambda E, h=h, pa=pa: E.matmul(pa[:], lhsT=sel[:, h, :], rhs=gcrow[:], start=True, stop=True), reads=[bsel, bgcrow], writes=[bpa])
                P.op("act", lambda E, h=h, pa=pa, GCB=GCB: E.copy(out=GCB[h][0][:], in_=pa[:]), reads=[bpa], writes=[GCB[h][1]])
                pa, bpa = next_ga()
                P.op("pe", lambda E, h=h, pa=pa: E.matmul(pa[:], lhsT=sel[:, h, :], rhs=brow[:], start=True, stop=True), reads=[bsel, bbrow], writes=[bpa])
                P.op("act", lambda E, h=h, pa=pa, BB=BB: E.copy(out=BB[h][0][:], in_=pa[:]), reads=[bpa], writes=[BB[h][1]])
                yield

        for _ in stageA(0):
            pass
        for s_ in range(NST):
            par = s_ % 2
            qk = qk2[par]; bqk = bqk2[par]; GCB = GCB2[par]; BB = BB2[par]; vb, bvb = vbuf2[par]
            nxt = stageA(s_ + 1) if s_ + 1 < NST else None

            def advance(k):
                if nxt is not None:
                    for _ in range(k):
                        next(nxt, None)
            for h in range(2):
                qT = qk[:, h, :]; bqT = bqk[h]; kT = qk[:, 2 + h, :]; bkT = bqk[2 + h]; vT = vb[:, h, :]; bvT = bvb[h]
                gcb, bgcb = GCB[h]; bb, bbb = BB[h]
                H = heads[h]
                attnT, battnT = H["attnT"]; Y, bY = H["Y"]; EG, bEG = H["EG"]; qdec, bqdec = H["qdec"]
                Ybf, bYbf = H["Ybf"]
                bv, bbv = H["bv"]; kdec, bkdec = H["kdec"]; nbg, bnbg = H["nbg"]
                arg1, barg1 = m64["arg1"]; DT, bDT = m64["DT"]; Ds, bDs = m64["Ds"]
                tmp, btmp = m64["tmp"]; tmp2, btmp2 = m64["tmp2"]; BBm, bBBm = m64["BBm"]
                gccol, bgccol = small["gccol"]; bcol, bbcol = small["bcol"]; nbcol, bnbcol = small["nbcol"]
                elast, belast = small["elast"]; egc, begc = small["egc"]
                v3 = lambda t_: t_[:].rearrange("p (n f) -> p n f", f=64)
                i64b = I64.unsqueeze(1).to_broadcast([64, 8, 64])
                tt(v3(tmp), btmp, gcb[0:64, :].rearrange("p (n f) -> p n f", f=64), [bgcb, bc64], i64b, bc64, ALU.mult)
                P.op("dve", lambda E, tmp=tmp, gccol=gccol: E.tensor_reduce(out=gccol[:], in_=tmp[:].rearrange("p (n f) -> p n f", f=64), axis=AX.X, op=ALU.add), reads=[btmp], writes=[bgccol])
                tt(v3(tmp), btmp, bb[0:64, :].rearrange("p (n f) -> p n f", f=64), [bbb, bc64], i64b, bc64, ALU.mult)
                P.op("dve", lambda E, tmp=tmp, bcol=bcol: E.tensor_reduce(out=bcol[:], in_=tmp[:].rearrange("p (n f) -> p n f", f=64), axis=AX.X, op=ALU.add), reads=[btmp], writes=[bbcol])
                P.op("dve", lambda E: E.tensor_scalar(out=nbcol[:], in0=bcol[:], scalar1=-1.0, scalar2=None, op0=ALU.mult), reads=[bbcol], writes=[bnbcol])
                tt(v3(arg1), barg1, gcb[0:64, :].rearrange("p (n f) -> p n f", f=64), [bgcb, bgccol], gccol[:].unsqueeze(2).to_broadcast([64, 8, 64]), bgccol, ALU.subtract)
                tt(v3(DT), bDT, v3(arg1), [barg1, bc64], NEGU.unsqueeze(1).to_broadcast([64, 8, 64]), bc64, ALU.add)
                P.op("act", lambda E: E.activation(out=DT[:], in_=DT[:], func=AF.Exp), reads=[bDT], writes=[bDT])
                P.op("dve", lambda E: E.scalar_tensor_tensor(out=Ds[:].rearrange("p (n f) -> p n f", f=64), in0=arg1[:].rearrange("p (n f) -> p n f", f=64), scalar=-1.0,
                                                             in1=NEGLS.unsqueeze(1).to_broadcast([64, 8, 64]), op0=ALU.mult, op1=ALU.add), reads=[barg1, bc64], writes=[bDs])
                P.op("act", lambda E: E.activation(out=Ds[:], in_=Ds[:], func=AF.Exp), reads=[bDs], writes=[bDs])
                tt(v3(BBm), bBBm, bb[0:64, :].rearrange("p (n f) -> p n f", f=64), [bbb, bc64], NSU.unsqueeze(1).to_broadcast([64, 8, 64]), bc64, ALU.mult)
                pk, bpk = GP[0]; pq, bpq = GP[1]
                fns = [(lambda E, n=n, pk=pk, kT=kT: E.matmul(pk[0:64, n * 64:(n + 1) * 64], lhsT=kT[:, n * 64:(n + 1) * 64], rhs=kT[:, n * 64:(n + 1) * 64],
                                                              start=True, stop=True)) for n in range(8)]
                P.mm_group(fns, reads=[bkT], writes=[bpk])
                fns = [(lambda E, n=n, pq=pq, kT=kT, qT=qT: E.matmul(pq[0:64, n * 64:(n + 1) * 64], lhsT=kT[:, n * 64:(n + 1) * 64], rhs=qT[:, n * 64:(n + 1) * 64],
                                                                     start=True, stop=True)) for n in range(8)]
                P.mm_group(fns, reads=[bkT, bqT], writes=[bpq])
                tt(attnT[:], battnT, pq[0:64, :], [bpq, bDT], DT[:], bDT, ALU.mult)
                Pc, bPc = m64["Pa"]; Pn, bPn = m64["Pb"]; Qc, bQc = m64["Qa"]; Qn, bQn = m64["Qb"]
                tt(tmp[:], btmp, pk[0:64, :], [bpk, bDT], DT[:], bDT, ALU.mult)
                tt(Qc[:], bQc, tmp[:], [btmp, bBBm], BBm[:], bBBm, ALU.mult)
                tt(tmp2[:], btmp2, pk[0:64, :], [bpk, bDs], Ds[:], bDs, ALU.mult)
                tt(v3(Pc), bPc, v3(tmp2), [btmp2, bnbcol], nbcol[:].unsqueeze(2).to_broadcast([64, 8, 64]), bnbcol, ALU.mult)
                tt(v3(Y), bY, v3(Qc), [bQc, bc64], i64b, bc64, ALU.add)
                P.op("act", lambda E, Ybf=Ybf, Y=Y: E.copy(out=Ybf[:], in_=Y[:]), reads=[bY], writes=[bYbf])
                for j in range(5):
                    pP, bpP = GP[2]; pQ, bpQ = GP[3]
                    fns = [(lambda E, n=n, pP=pP, Qc=Qc, Pc=Pc: E.matmul(pP[0:64, n * 64:(n + 1) * 64], lhsT=Qc[:, n * 64:(n + 1) * 64], rhs=Pc[:, n * 64:(n + 1) * 64],
                                                                         start=True, stop=True)) for n in range(8)]
                    P.mm_group(fns, reads=[bQc, bPc], writes=[bpP])
                    if j < 4:
                        fns = [(lambda E, n=n, pQ=pQ, Qc=Qc, Pc=Pc: E.matmul(pQ[0:64, n * 64:(n + 1) * 64], lhsT=Pc[:, n * 64:(n + 1) * 64], rhs=Qc[:, n * 64:(n + 1) * 64],
                                                                             start=True, stop=True)) for n in range(8)]
                        P.mm_group(fns, reads=[bQc, bPc], writes=[bpQ])
                    P.op("act", lambda E, Pn=Pn, pP=pP: E.copy(out=Pn[:], in_=pP[0:64, :]), reads=[bpP], writes=[bPn])
                    if j < 4:
                        P.op("dve", lambda E, Qn=Qn, pQ=pQ: E.tensor_copy(out=Qn[:], in_=pQ[0:64, :]), reads=[bpQ], writes=[bQn])
                    pY, bpY = GP[0]
                    fns = [(lambda E, n=n, pY=pY, Pn=Pn, Ybf=Ybf: E.matmul(pY[0:64, n * 64:(n + 1) * 64], lhsT=Pn[:, n * 64:(n + 1) * 64], rhs=Ybf[:, n * 64:(n + 1) * 64],
                                                                         start=True, stop=True)) for n in range(8)]
                    P.mm_group(fns, reads=[bPn, bYbf], writes=[bpY])
                    tt(Y[:], bY, Y[:], [bY, bpY], pY[0:64, :], bpY, ALU.add)
                    P.op("act", lambda E, Ybf=Ybf, Y=Y: E.copy(out=Ybf[:], in_=Y[:]), reads=[bY], writes=[bYbf])
                    Pc, bPc, Pn, bPn = Pn, bPn, Pc, bPc
                    Qc, bQc, Qn, bQn = Qn, bQn, Qc, bQc
                for hf in range(2):
                    pth, bpth = GA[hf]
                    fns = [(lambda E, n=n, vT=vT, pth=pth, hf=hf: E.transpose(out=pth[0:64, n * 128:(n + 1) * 128], in_=vT[:, (4 * hf + n) * 64:(4 * hf + n + 1) * 64],
                                                                              identity=idf[:])) for n in range(4)]
                    P.mm_group(fns, reads=[bvT, bidf], writes=[bpth])
                    tt(bv[:, 4 * hf:4 * hf + 4, :], bbv, pth[0:64, :].rearrange("p (n d) -> p n d", d=128), [bpth, bbcol],
                       bcol[:, 4 * hf:4 * hf + 4].unsqueeze(2).to_broadcast([64, 4, 128]), bbcol, ALU.mult)
                tt(elast[:], belast, gcb[0:64, :].rearrange("p (n f) -> p n f", f=64)[:, :, 63], [bgcb, bgccol], gccol[:], bgccol, ALU.subtract)
                P.op("act", lambda E: E.activation(out=elast[:], in_=elast[:], func=AF.Exp), reads=[belast], writes=[belast])
                for hf in range(2):
                    pth, bpth = GA[hf]
                    fns = [(lambda E, n=n, kT=kT, pth=pth, hf=hf: E.transpose(out=pth[0:64, n * 128:(n + 1) * 128], in_=kT[:, (4 * hf + n) * 64:(4 * hf + n + 1) * 64],
                                                                              identity=idf[:])) for n in range(4)]
                    P.mm_group(fns, reads=[bkT, bidf], writes=[bpth])
                    tt(kdec[:, 4 * hf:4 * hf + 4, :], bkdec, pth[0:64, :].rearrange("p (n d) -> p n d", d=128), [bpth, belast],
                       elast[:, 4 * hf:4 * hf + 4].unsqueeze(2).to_broadcast([64, 4, 128]), belast, ALU.mult)
                P.op("act", lambda E, gcb=gcb, EG=EG: E.activation(out=EG[:], in_=gcb[:], func=AF.Exp), reads=[bgcb], writes=[bEG])
                tt(qdec[:], bqdec, qT, [bqT, bEG], EG[:], bEG, ALU.mult)
                kTb, bkTb = H["kTb"]
                P.op("act", lambda E, kTb=kTb, kT=kT: E.copy(out=kTb[:], in_=kT), reads=[bkT], writes=[bkTb])
                P.op("act", lambda E: E.activation(out=egc[:], in_=gccol[:], func=AF.Exp), reads=[bgccol], writes=[begc])
                P.op("dve", lambda E, nbg=nbg: E.scalar_tensor_tensor(out=nbg[:], in0=egc[:], scalar=-1.0, in1=bcol[:], op0=ALU.mult, op1=ALU.mult),
                     reads=[begc, bbcol], writes=[bnbg])
            banks = [(GP[0], GP[1]), (GP[2], GP[3])]
            for n in range(8):
                cs = slice(n * 64, (n + 1) * 64)
                for h in range(2):
                    H = heads[h]; S, bS = Sst[h]
                    kT, bkT = H["kTb"]; Sbf, bSbf = H["Sbf"]
                    attnT, battnT = H["attnT"]; Y, bY = H["Ybf"]; EG, bEG = H["EG"]; qdec, bqdec = H["qdec"]
                    bv, bbv = H["bv"]; kdec, bkdec = H["kdec"]; nbg, bnbg = H["nbg"]
                    vnew, bvnew = H["vnew"]; rhs2, brhs2 = H["rhs2"]; osb, bosb = H["osb"]
                    (KSO, bKSO), (Sb, bSb) = banks[h]
                    Vb, bVb = KSO, bKSO
                    P.op("pe", lambda E, cs=cs, kT=kT, Sbf=Sbf, KSO=KSO: E.matmul(KSO[0:64, 0:128], lhsT=kT[:, cs], rhs=Sbf[:], start=True, stop=True),
                         reads=[bkT, bSbf], writes=[bKSO])
                    P.op("dve", lambda E, n=n, KSO=KSO, rhs2=rhs2, nbg=nbg, bv=bv: E.scalar_tensor_tensor(
                        out=rhs2[:], in0=KSO[0:64, 0:128], scalar=nbg[:, n:n + 1], in1=bv[:, n, :], op0=ALU.mult, op1=ALU.add),
                        reads=[bKSO, bnbg, bbv], writes=[brhs2])
                    P.op("pe", lambda E, cs=cs, Y=Y, Vb=Vb, rhs2=rhs2: E.matmul(Vb[0:64, 128:256], lhsT=Y[:, cs], rhs=rhs2[:], start=True, stop=True),
                         reads=[bY, brhs2], writes=[bVb])
                    P.op("act", lambda E, vnew=vnew, Vb=Vb: E.copy(out=vnew[:], in_=Vb[0:64, 128:256]), reads=[bVb], writes=[bvnew])
                    fns = [lambda E, cs=cs, Sbf=Sbf, KSO=KSO, qdec=qdec: E.matmul(KSO[64:128, 0:128], lhsT=qdec[:, cs], rhs=Sbf[:], start=True, stop=False),
                           lambda E, cs=cs, KSO=KSO, attnT=attnT, vnew=vnew: E.matmul(KSO[64:128, 0:128], lhsT=attnT[:, cs], rhs=vnew[:], start=False, stop=True)]
                    P.mm_group(fns, reads=[bqdec, bSbf, battnT, bvnew], writes=[bKSO])
                    P.op("pe", lambda E, n=n, Sb=Sb, kdec=kdec, vnew=vnew: E.matmul(Sb[:, 0:128], lhsT=kdec[:, n, :], rhs=vnew[:], start=True, stop=True),
                         reads=[bkdec, bvnew], writes=[bSb])
                    P.op("dve", lambda E, n=n, S=S, EG=EG, Sb=Sb: E.scalar_tensor_tensor(out=S[:], in0=S[:], scalar=EG[:, n * 64 + 63:n * 64 + 64], in1=Sb[:, 0:128],
                                                                                         op0=ALU.mult, op1=ALU.add), reads=[bS, bEG, bSb], writes=[bS])
                    P.op("act", lambda E, S=S, Sbf=Sbf: E.copy(out=Sbf[:], in_=S[:]), reads=[bS], writes=[bSbf])
                    P.op("act", lambda E, n=n, osb=osb, KSO=KSO: E.copy(out=osb[64:128, n, :], in_=KSO[64:128, 0:128]), reads=[bKSO], writes=[bosb])
                    advance(1)
                advance(1)
            advance(100)
            for h in range(2):
                osb, bosb = heads[h]["osb"]
                P.dma("sp", o_d[s_ * 512:(s_ + 1) * 512, h * 128:(h + 1) * 128].rearrange("(n c) d -> c n d", c=64), osb[64:128, :, :], reads=[bosb],
                      writes=[fz["obuf_of"](s_) if fz else bo])
            if fz:
                fz["after_chunk"](s_)
        if fz:
            barrier(P)
        else:
            P.finish([bo])
    return nc


def run_L1a(inp):
    nc = _get("L1a", build_L1a)
    c64, cmask, sel = _gdn_consts()
    w_in = inp["w_in_even"][0]
    conv = inp["conv_qkv"][0]
    ones = np.ones((128, 128), np.float32)
    maps = []
    for c in range(8):
        b, r = divmod(c, 4)
        cols = np.concatenate([np.arange(256 * r, 256 * r + 256), 1024 + np.arange(256 * r, 256 * r + 256), 2048 + np.arange(256 * r, 256 * r + 256)])
        maps.append({"x": np.ascontiguousarray(inp["x"][b]), "npre": np.ascontiguousarray(inp["norm_pre"][0]),
                     "w": np.ascontiguousarray(w_in[:, cols]), "wb": np.ascontiguousarray(w_in[:, 4096 + 2 * r:4096 + 2 * r + 2]),
                     "wa": np.ascontiguousarray(w_in[:, 4104 + 2 * r:4104 + 2 * r + 2]), "conv": np.ascontiguousarray(conv[:, cols]),
                     "alog": np.ascontiguousarray(inp["a_log"][0, 2 * r:2 * r + 2]), "dtb": np.ascontiguousarray(inp["dt_bias"][0, 2 * r:2 * r + 2]),
                     "ident": _IDENT, "c64": c64, "cmask": cmask, "sel": sel, "ones": ones})
    res = run_bass_kernel_spmd(nc, maps, core_ids=list(range(8)))
    S_ = inp["x"].shape[1]
    o = np.empty((2, S_, 1024), np.float32)
    for c in range(8):
        b, r = divmod(c, 4)
        o[b, :, 256 * r:256 * (r + 1)] = res.results[c]["o"]
    return o


def kernel_unfused(**inputs):
    inp = {k: np.asarray(v) for k, v in inputs.items()}
    o = run_L1a(inp)
    ys = run_L1b(inp)
    x1 = run_L2(inp, o, ys)
    out = run_L3(inp, x1)
    return out.astype(np.float32)


def build_fused():
    nc = bass.Bass("TRN2", target_bir_lowering=False)
    x_full = nc.dram_tensor("x", [8192, 1024], F32, kind="ExternalInput").ap()
    ident_d = nc.dram_tensor("ident", [128, 128], F32, kind="ExternalInput").ap()
    npre0_d = nc.dram_tensor("npre0", [1024], F32, kind="ExternalInput").ap()
    gidx_d = nc.dram_tensor("gidx", [128, 2, 17, 4], I32, kind="ExternalInput").ap()
    out_d = nc.dram_tensor("out", [2048, 1024], F32, kind="ExternalOutput").ap()
    ag_in = [nc.dram_tensor("ag_in%d" % i, [8192, 256], F32) for i in range(2)]
    ag_out = [nc.dram_tensor("ag_out%d" % i, [4 * 8192, 256], F32) for i in range(2)]
    x1s = nc.dram_tensor("x1s", [2176, 1024], F32)
    GROUPS = [[0, 1, 2, 3], [4, 5, 6, 7]]
    with ExitStack() as st:
        C = Ctx(nc, st); P = C.P
        csem = st.enter_context(nc.semaphore("csem"))
        bag_out = Buf("ag_out"); bx1s = Buf("x1s", multi=True); bout = Buf("out", multi=True)
        bo_ch = [Buf("o_ch%d" % k, multi=True) for k in range(16)]
        by_jt = [Buf("y_jt%d" % k, multi=True) for k in range(4)]
        ncc = [0]

        def emit_cc(which, k, inbuf, rows=512):
            P._deps("pool", [inbuf], [])
            P.streams["pool"].append(lambda E, which=which, k=k, rows=rows: E.collective_compute(
                "AllGather", ALU.bypass, replica_groups=GROUPS,
                ins=[ag_in[which].ap()[k * rows:(k + 1) * rows, :].opt()],
                outs=[ag_out[which].ap()[k * 4 * rows:(k + 1) * 4 * rows, :].opt()]).then_inc(csem))
            ncc[0] += 1

        share1 = {"x": x_full, "ident": ident_d, "npre": npre0_d}

        def after_jt(jt):
            for k in range(2 * jt, 2 * jt + 2):
                emit_cc(1, k, by_jt[jt], rows=1024)

        with ExitStack() as stU:
            CU = Ctx(nc, stU, P, "u_")
            uext = CU.sb("uTp", [128, 2, 16, 512], BF16)
            build_L1a(8192, fz={"nc": nc, "P": P, "pfx": "a_", "share": share1, "out": ag_in[0].ap(), "uTp": uext,
                                "obuf_of": lambda s_: bo_ch[s_], "after_chunk": lambda s_: emit_cc(0, s_, bo_ch[s_])})
            build_L1b(8192, fz={"nc": nc, "P": P, "pfx": "b_", "share": share1, "out": ag_in[1].ap(), "uTp": uext,
                                "obuf_of": lambda jt: by_jt[jt], "after_chunk": after_jt})
        gidx, bgidx = C.sb("gidx", [128, 2, 17, 4], I32)
        P.dma("sp", gidx[:], gidx_d, writes=[bgidx])
        waited = [False]

        def gather(P_, ld, bld, tile, part):
            if not waited[0]:
                P.streams["pool"].append(lambda E: E.wait_ge(csem, ncc[0]))
                P.op("pool", lambda E: E.nop(), reads=[], writes=[bag_out])
                waited[0] = True
            for i in range(4):
                P_.dma_ind("pool", ld[:, i * 256:(i + 1) * 256], ag_out[part].ap(), gidx[:, part, tile, i:i + 1], reads=[bag_out, bgidx], writes=[bld])

        share2 = {"ident": ident_d, "npre": npre0_d, "o": None, "ys": None}
        build_L2(2176, fz={"nc": nc, "P": P, "pfx": "c_", "share": share2, "out": x1s.ap(), "obuf": bx1s, "gather": gather})
        share3 = {"ident": ident_d, "x": x1s.ap()}
        build_L3(2048, fz={"nc": nc, "P": P, "pfx": "d_", "share": share3, "out": out_d, "obuf": bout, "xbuf": bx1s})
        P.finish([bout])
    return nc


def _gidx(r):
    g = np.zeros((128, 2, 17, 4), np.int32)
    p = np.arange(128)[:, None, None]
    tile = np.arange(17)[None, :, None]
    src = np.arange(4)[None, None, :]
    tok = np.clip(2048 * r - 128 + tile * 128 + p, 0, 8191)
    for part, R in ((0, 512), (1, 1024)):
        g[:, part] = ((tok // R) * 4 + src) * R + tok % R
    return g


def kernel(**inputs):
    inp = {k: np.ascontiguousarray(np.asarray(v)) for k, v in inputs.items()}
    nc = _get("fused", build_fused)
    c64, cmask, sel = _gdn_consts()
    mk, idm = _s5_consts()
    ones = np.ones((128, 128), np.float32)
    w_in = inp["w_in_even"][0]
    conv = inp["conv_qkv"][0]
    wz = np.ascontiguousarray(np.concatenate([w_in[:, 3072:4096], w_in[:, 5136:6160]], axis=1))
    maps = []
    for c in range(8):
        b, r = divmod(c, 4)
        cols = np.concatenate([np.arange(256 * r, 256 * r + 256), 1024 + np.arange(256 * r, 256 * r + 256), 2048 + np.arange(256 * r, 256 * r + 256)])
        gs = slice(16 * r, 16 * r + 16)
        xq = np.zeros((2176, 1024), np.float32)
        xq[128:] = inp["x"][b, 2048 * r:2048 * (r + 1)]
        if r > 0:
            xq[:128] = inp["x"][b, 2048 * r - 128:2048 * r]
        m = {"x": inp["x"][b], "ident": _IDENT, "npre0": inp["norm_pre"][0], "gidx": _gidx(r),
             "a_w": np.ascontiguousarray(w_in[:, cols]), "a_wb": np.ascontiguousarray(w_in[:, 4096 + 2 * r:4096 + 2 * r + 2]),
             "a_wa": np.ascontiguousarray(w_in[:, 4104 + 2 * r:4104 + 2 * r + 2]), "a_conv": np.ascontiguousarray(conv[:, cols]),
             "a_alog": np.ascontiguousarray(inp["a_log"][0, 2 * r:2 * r + 2]), "a_dtb": np.ascontiguousarray(inp["dt_bias"][0, 2 * r:2 * r + 2]),
             "a_c64": c64, "a_cmask": cmask, "a_sel": sel, "a_ones": ones,
             "a_wu": np.ascontiguousarray(w_in[:, 4112 + 256 * r:4112 + 256 * (r + 1)]),
             "b_wu": np.ascontiguousarray(w_in[:, 4112 + 256 * r:4112 + 256 * (r + 1)]),
             "b_lre": np.ascontiguousarray(inp["s5_lam_re"][0, gs]), "b_lim": np.ascontiguousarray(inp["s5_lam_im"][0, gs]),
             "b_bre": np.ascontiguousarray(inp["s5_b_re"][0, gs]), "b_bim": np.ascontiguousarray(inp["s5_b_im"][0, gs]),
             "b_cre": np.ascontiguousarray(inp["s5_c_re"][0, gs]), "b_cim": np.ascontiguousarray(inp["s5_c_im"][0, gs]),
             "b_ldt": np.ascontiguousarray(inp["s5_log_dt"][0, gs]), "b_dd": np.ascontiguousarray(inp["s5_d"][0, 256 * r:256 * (r + 1)]),
             "b_taus": TAUS, "b_mk": mk, "b_idm": idm,
             "c_x": xq, "c_wz": wz, "c_wglu": inp["w_glu"][0], "c_wout": inp["w_out_even"][0], "c_npost": inp["norm_post"][0],
             "c_gnw": inp["gdn_norm_w"][0],
             "d_win": inp["w_in_odd"][0], "d_wout": inp["w_out_odd"][0], "d_conv": inp["conv_short"][0],
             "d_npre": inp["norm_pre"][1], "d_npost": inp["norm_post"][1]}
        maps.append(m)
    res = run_bass_kernel_spmd(nc, maps, core_ids=list(range(8)))
    out = np.empty((2, 8192, 1024), np.float32)
    for c in range(8):
        b, r = divmod(c, 4)
        out[b, r * 2048:(r + 1) * 2048] = res.results[c]["out"]
    return out
```

```python
from contextlib import ExitStack
import numpy as np
import concourse.bass as bass
import concourse.mybir as mybir
from concourse.bass_utils import run_bass_kernel_spmd

F32 = mybir.dt.float32
BF16 = mybir.dt.bfloat16
AF = mybir.ActivationFunctionType
ALU = mybir.AluOpType
AX = mybir.AxisListType

NDS = 12


class Buf:
    __slots__ = ("name", "w", "r", "multi")

    def __init__(self, name, multi=False):
        self.name = name
        self.w = [] if multi else None
        self.r = []
        self.multi = multi


class Prog:
    ENG = ("pe", "act", "dve", "pool", "sp")

    def __init__(self, nc, stack):
        self.nc = nc
        self.stack = stack
        self.streams = {e: [] for e in self.ENG}
        self.cnt = {e: 0 for e in self.ENG}
        self.sem = {e: stack.enter_context(nc.semaphore("s_" + e)) for e in self.ENG}
        self.seen = {e: {} for e in self.ENG}
        self.dcnt = {e: 0 for e in self.ENG}
        self.dsem = {}
        for e in ("sp", "pool", "act"):
            self.dsem[e] = [stack.enter_context(nc.semaphore("d_%s%d" % (e, i))) for i in range(NDS)]
        self.same_engine_sync = True
        self.nwaits = 0

    def _wait(self, eng, tok):
        if tok is None:
            return
        kind = tok[0]
        if kind == "c":
            _, e2, n = tok
            if e2 == eng and (eng == "pe" or not self.same_engine_sync):
                return
            key = e2
            if self.seen[eng].get(key, 0) >= n:
                return
            self.seen[eng][key] = n
            sem = self.sem[e2]
            self.streams[eng].append(lambda E, sem=sem, n=n: E.wait_ge(sem, n))
            self.nwaits += 1
        else:
            _, q, slot, val = tok
            key = ("d", q, slot)
            if self.seen[eng].get(key, 0) >= val:
                return
            self.seen[eng][key] = val
            sem = self.dsem[q][slot]
            self.streams[eng].append(lambda E, sem=sem, val=val: E.wait_ge(sem, val))
            self.nwaits += 1

    def _deps(self, eng, reads, writes):
        for b in reads:
            if b.multi:
                for t in b.w:
                    self._wait(eng, t)
            else:
                self._wait(eng, b.w)
        for b in writes:
            if not b.multi:
                self._wait(eng, b.w)
            for t in b.r:
                self._wait(eng, t)

    def _commit(self, tok, reads, writes):
        for b in writes:
            if b.multi:
                b.w.append(tok)
            else:
                b.w = tok
            b.r = []
        for b in reads:
            if b not in writes:
                b.r.append(tok)

    def op(self, eng, fn, reads=(), writes=()):
        reads = list(reads)
        writes = list(writes)
        self._deps(eng, reads, writes)
        self.cnt[eng] += 1
        n = self.cnt[eng]
        sem = self.sem[eng]
        self.streams[eng].append(lambda E, fn=fn, sem=sem: fn(E).then_inc(sem, 1))
        tok = ("c", eng, n)
        self._commit(tok, reads, writes)
        return tok

    def mm_group(self, fns, reads=(), writes=()):
        eng = "pe"
        reads = list(reads)
        writes = list(writes)
        self._deps(eng, reads, writes)
        self.cnt[eng] += 1
        n = self.cnt[eng]
        sem = self.sem[eng]
        for fn in fns[:-1]:
            self.streams[eng].append(lambda E, fn=fn: fn(E))
        last = fns[-1]
        self.streams[eng].append(lambda E, fn=last, sem=sem: fn(E).then_inc(sem, 1))
        tok = ("c", eng, n)
        self._commit(tok, reads, writes)
        return tok

    def dma(self, q, out_ap, in_ap, reads=(), writes=()):
        reads = list(reads)
        writes = list(writes)
        self._deps(q, reads, writes)
        j = self.dcnt[q]
        self.dcnt[q] += 1
        slot = j % NDS
        val = 16 * (j // NDS + 1)
        if j >= NDS:
            self._wait(q, ("d", q, slot, val - 16))
        sem = self.dsem[q][slot]
        self.streams[q].append(
            lambda E, o=out_ap, i=in_ap, sem=sem: E.dma_start(out=o, in_=i).then_inc(sem, 16))
        tok = ("d", q, slot, val)
        self._commit(tok, reads, writes)
        return tok

    def dma_ind(self, q, out_ap, table_ap, idx_ap, reads=(), writes=()):
        reads = list(reads)
        writes = list(writes)
        self._deps(q, reads, writes)
        j = self.dcnt[q]
        self.dcnt[q] += 1
        slot = j % NDS
        val = 16 * (j // NDS + 1)
        if j >= NDS:
            self._wait(q, ("d", q, slot, val - 16))
        sem = self.dsem[q][slot]
        self.streams[q].append(
            lambda E, o=out_ap, t=table_ap, i=idx_ap, sem=sem: E.indirect_dma_start(
                out=o, out_offset=None, in_=t, in_offset=bass.IndirectOffsetOnAxis(ap=i, axis=0)).then_inc(sem, 16))
        tok = ("d", q, slot, val)
        self._commit(tok, reads, writes)
        return tok

    def finish(self, final_bufs):
        for b in final_bufs:
            for t in (b.w if b.multi else [b.w]):
                self._wait("sp", t)
        nc = self.nc
        streams = self.streams
        with nc.Block() as block:
            @block.tensor
            def _(E):
                for f in streams["pe"]:
                    f(E)

            @block.scalar
            def _(E):
                for f in streams["act"]:
                    f(E)

            @block.vector
            def _(E):
                for f in streams["dve"]:
                    f(E)

            @block.gpsimd
            def _(E):
                for f in streams["pool"]:
                    f(E)

            @block.sync
            def _(E):
                for f in streams["sp"]:
                    f(E)


class Ctx:
    def __init__(self, nc, st, P=None, pfx=""):
        self.nc = nc
        self.st = st
        self.pfx = pfx
        if P is None:
            st.enter_context(nc.allow_non_contiguous_dma(reason="small parameter loads / layout transforms"))
            P = Prog(nc, st)
        self.P = P

    def sb(self, name, shape, dt=F32):
        t = self.st.enter_context(self.nc.sbuf_tensor("sb_" + self.pfx + name, shape, dt))
        return t, Buf(name)

    def ps(self, name, shape, dt=F32):
        t = self.st.enter_context(self.nc.psum_tensor("ps_" + self.pfx + name, shape, dt))
        return t, Buf(name)


def bcast_row_load(C, name, dram_vec, n, q="sp"):
    t, b = C.sb(name, [128, n])
    C.P.dma(q, t[:], dram_vec.partition_broadcast(128), writes=[b])
    return t, b


def make_ident(C, dram_ident):
    idf, bidf = C.sb("identf", [128, 128])
    C.P.dma("sp", idf[:], dram_ident, writes=[bidf])
    idb, bidb = C.sb("identb", [128, 128], BF16)
    C.P.op("dve", lambda E: E.tensor_copy(out=idb[:], in_=idf[:]), reads=[bidf], writes=[bidb])
    return idf, bidf, idb, bidb


def rms_rstd(C, src, bsrc, ncols, junk, bjunk, ss, bss, eps=1e-6):
    P = C.P
    P.op("act", lambda E: E.activation(out=junk, in_=src, func=AF.Square, accum_out=ss[:, 0:1]),
         reads=[bsrc], writes=[bjunk, bss])
    P.op("act", lambda E: E.activation(out=ss[:, 0:1], in_=ss[:, 0:1], func=AF.Sqrt, bias=float(eps), scale=float(1.0 / ncols)),
         reads=[bss], writes=[bss])
    P.op("dve", lambda E: E.reciprocal(out=ss[:, 0:1], in_=ss[:, 0:1]), reads=[bss], writes=[bss])


def transpose8(C, src_bf, bsrc, idb, bidb, ptr, bptr, dst3, bdst, eng="act"):
    P = C.P
    fns = [(lambda E, kt=kt: E.transpose(out=ptr[:, kt * 128:(kt + 1) * 128], in_=src_bf[:, kt * 128:(kt + 1) * 128],
                                         identity=idb[:])) for kt in range(8)]
    P.mm_group(fns, reads=[bsrc, bidb], writes=[bptr])
    src3 = ptr[:].rearrange("p (k t) -> p k t", k=8)
    if eng == "act":
        P.op("act", lambda E: E.copy(out=dst3, in_=src3), reads=[bptr], writes=[bdst])
    else:
        P.op("dve", lambda E: E.tensor_copy(out=dst3, in_=src3), reads=[bptr], writes=[bdst])


def outproj_post(C, catT, bcat, nkt, wout, bwout, t, xres, bxres, npw, bnpw, pso, bpso, yo, byo, junk, bjunk, ss, bss,
                 out_dram_rows, bout):
    P = C.P
    for hh in range(2):
        fns = [(lambda E, kt=kt, hh=hh: E.matmul(pso[hh][:], lhsT=catT[:, kt, t * 128:(t + 1) * 128],
                                                 rhs=wout[:, kt, hh * 512:(hh + 1) * 512],
                                                 start=(kt == 0), stop=(kt == nkt - 1))) for kt in range(nkt)]
        P.mm_group(fns, reads=[bcat, bwout], writes=[bpso[hh]])
        P.op("act", lambda E, hh=hh: E.copy(out=yo[:, hh * 512:(hh + 1) * 512], in_=pso[hh][:]),
             reads=[bpso[hh]], writes=[byo])
    rms_rstd(C, yo[:], byo, 1024, junk[:], bjunk, ss, bss)
    P.op("dve", lambda E: E.scalar_tensor_tensor(out=yo[:], in0=yo[:], scalar=ss[:, 0:1], in1=npw[:],
                                                 op0=ALU.mult, op1=ALU.mult), reads=[byo, bss, bnpw], writes=[byo])
    P.op("dve", lambda E: E.tensor_tensor(out=yo[:], in0=yo[:], in1=xres, op=ALU.add), reads=[byo, bxres], writes=[byo])
    P.dma("sp", out_dram_rows, yo[:], reads=[byo], writes=[bout])


def load_w_bf16(C, name, dram_w, kt_n, ncols, chunk=2048, groups=None):
    w, _ = C.sb(name, [128, kt_n, ncols], BF16)
    src = dram_w.rearrange("(k p) c -> p k c", p=128)
    if groups is None:
        bw = Buf(name, multi=True)
        for kt in range(kt_n):
            for c0 in range(0, ncols, chunk):
                c1 = min(ncols, c0 + chunk)
                C.P.dma("pool", w[:, kt, c0:c1], src[:, kt, c0:c1], writes=[bw])
        return w, bw
    bws = []
    for gi, sls in enumerate(groups):
        bg = Buf("%s_g%d" % (name, gi), multi=True)
        for (c0, c1) in sls:
            for kt in range(kt_n):
                C.P.dma("pool", w[:, kt, c0:c1], src[:, kt, c0:c1], writes=[bg])
        bws.append(bg)
    return w, bws


def build_L2(ntok=2048, fz=None):
    nc = fz["nc"] if fz else bass.Bass("TRN2", target_bir_lowering=False)
    pfx = fz["pfx"] if fz else ""

    def D(name, shape):
        if fz and name in fz["share"]:
            return fz["share"][name]
        return nc.dram_tensor(pfx + name, shape, F32, kind="ExternalInput").ap()
    x_d = D("x", [ntok, 1024]); o_d = D("o", [ntok, 1024]); ys_d = D("ys", [ntok, 1024])
    wz_d = D("wz", [1024, 2048]); wglu_d = D("wglu", [1024, 1024]); wout_d = D("wout", [2048, 1024])
    npre_d = D("npre", [1024]); npost_d = D("npost", [1024]); gnw_d = D("gnw", [128]); ident_d = D("ident", [128, 128])
    out_d = fz["out"] if fz else nc.dram_tensor("out", [ntok, 1024], F32, kind="ExternalOutput").ap()
    NT = 512
    with ExitStack() as st:
        C = Ctx(nc, st, fz["P"], pfx) if fz else Ctx(nc, st); P = C.P
        idf, bidf, idb, bidb = make_ident(C, ident_d)
        npre, bnpre = bcast_row_load(C, "npre", npre_d, 1024)
        npost, bnpost = bcast_row_load(C, "npost", npost_d, 1024)
        gnw, bgnw = bcast_row_load(C, "gnw", gnw_d, 128)
        wz, bwz = load_w_bf16(C, "wz", wz_d, 8, 2048)
        wglu, bwglu = load_w_bf16(C, "wglu", wglu_d, 8, 1024)
        wout, bwout = load_w_bf16(C, "wout", wout_d, 16, 1024)
        xt4, bxt4 = C.sb("xt4", [128, 4, 1024]); bxt = [Buf("xt%d" % i) for i in range(4)]
        ldo = [C.sb("ldo%d" % i, [128, 1024]) for i in range(2)]
        ldy = [C.sb("ldy%d" % i, [128, 1024]) for i in range(2)]
        sq, bsq = C.sb("sq", [128, 1024])
        hn, bhn = C.sb("hn", [128, 1024], BF16)
        ss, bss = C.sb("ss", [128, 1])
        ss8, bss8 = C.sb("ss8", [128, 8])
        hT, bhT = C.sb("hT", [128, 8, NT], BF16)
        oT, boT = C.sb("oT", [128, 8, NT], BF16)
        yT, byT = C.sb("yT", [128, 8, NT], BF16)
        gz, bgz = C.sb("gz", [128, 8, NT], BF16)
        sg, bsg = C.sb("sg", [128, NT], BF16)
        catT, bcat = C.sb("catT", [128, 16, NT], BF16)
        yo, byo = C.sb("yo", [128, 1024])
        ptr, bptr = C.ps("ptr", [128, 1024], BF16)
        pmm = []; bpmm = []
        for i in range(4):
            t_, b_ = C.ps("pmm%d" % i, [128, 512]); pmm.append(t_); bpmm.append(b_)
        pso = []; bpso = []
        for i in range(2):
            t_, b_ = C.ps("pso%d" % i, [128, 512]); pso.append(t_); bpso.append(b_)
        bout = fz["obuf"] if fz else Buf("out", multi=True)
        if fz:
            sts = [(0, 128)] + [(128 + i * NT, NT) for i in range((ntok - 128) // NT)]
        else:
            sts = [(i * NT, NT) for i in range(ntok // NT)]
        tile_r0 = [t0_ + t_ * 128 for (t0_, n_) in sts for t_ in range(n_ // 128)]

        def issue_loads(ti):
            r0_ = tile_r0[ti]
            lo, blo = ldo[ti % 2]; ly, bly = ldy[ti % 2]
            if fz:
                fz["gather"](P, lo, blo, r0_ // 128, 0)
                fz["gather"](P, ly, bly, r0_ // 128, 1)
            else:
                P.dma("sp", lo[:], o_d[r0_:r0_ + 128, :], writes=[blo])
                P.dma("sp", ly[:], ys_d[r0_:r0_ + 128, :], writes=[bly])

        issue_loads(0)
        for (t0, n) in sts:
            ntl = n // 128
            for t in range(ntl):
                r0 = t0 + t * 128
                ti = tile_r0.index(r0)
                if ti + 1 < len(tile_r0):
                    issue_loads(ti + 1)
                P.dma("sp", xt4[:, t, :], x_d[r0:r0 + 128, :], writes=[bxt[t]])
                rms_rstd(C, xt4[:, t, :], bxt[t], 1024, sq[:], bsq, ss, bss)
                P.op("dve", lambda E, t=t: E.scalar_tensor_tensor(out=hn[:], in0=xt4[:, t, :], scalar=ss[:, 0:1], in1=npre[:],
                                                                  op0=ALU.mult, op1=ALU.mult), reads=[bxt[t], bss, bnpre], writes=[bhn])
                transpose8(C, hn, bhn, idb, bidb, ptr, bptr, hT[:, :, t * 128:(t + 1) * 128], bhT, eng="act")
                ld, bld = ldo[ti % 2]
                P.op("act", lambda E, ld=ld: E.activation(out=sq[:], in_=ld[:], func=AF.Square), reads=[bld], writes=[bsq])
                P.op("dve", lambda E: E.tensor_reduce(out=ss8[:], in_=sq[:].rearrange("p (h d) -> p h d", h=8), axis=AX.X, op=ALU.add),
                     reads=[bsq], writes=[bss8])
                P.op("dve", lambda E: E.tensor_scalar(out=ss8[:], in0=ss8[:], scalar1=1.0 / 128, scalar2=1e-6, op0=ALU.mult, op1=ALU.add),
                     reads=[bss8], writes=[bss8])
                P.op("act", lambda E: E.activation(out=ss8[:], in_=ss8[:], func=AF.Sqrt), reads=[bss8], writes=[bss8])
                P.op("dve", lambda E: E.reciprocal(out=ss8[:], in_=ss8[:]), reads=[bss8], writes=[bss8])
                P.op("dve", lambda E, ld=ld: E.tensor_tensor(out=sq[:].rearrange("p (h d) -> p h d", h=8), in0=ld[:].rearrange("p (h d) -> p h d", h=8),
                                                      in1=ss8[:].unsqueeze(2).to_broadcast([128, 8, 128]), op=ALU.mult),
                     reads=[bld, bss8], writes=[bsq])
                P.op("dve", lambda E: E.tensor_tensor(out=hn[:].rearrange("p (h d) -> p h d", h=8), in0=sq[:].rearrange("p (h d) -> p h d", h=8),
                                                      in1=gnw[:].unsqueeze(1).to_broadcast([128, 8, 128]), op=ALU.mult),
                     reads=[bsq, bgnw], writes=[bhn])
                transpose8(C, hn, bhn, idb, bidb, ptr, bptr, oT[:, :, t * 128:(t + 1) * 128], boT, eng="act")
                ld, bld = ldy[ti % 2]
                P.op("act", lambda E, ld=ld: E.activation(out=hn[:], in_=ld[:], func=AF.Gelu_apprx_tanh), reads=[bld], writes=[bhn])
                transpose8(C, hn, bhn, idb, bidb, ptr, bptr, yT[:, :, t * 128:(t + 1) * 128], byT, eng="dve")
            for ct in range(16):
                pb = pmm[ct % 4]; bpb = bpmm[ct % 4]
                fns = [(lambda E, kt=kt, ct=ct, pb=pb, n=n: E.matmul(pb[:, 0:n], lhsT=wz[:, kt, ct * 128:(ct + 1) * 128], rhs=hT[:, kt, 0:n],
                                                                start=(kt == 0), stop=(kt == 7))) for kt in range(8)]
                P.mm_group(fns, reads=[bwz, bhT], writes=[bpb])
                if ct < 8:
                    P.op("act", lambda E, pb=pb, n=n: E.activation(out=sg[:, 0:n], in_=pb[:, 0:n], func=AF.Silu), reads=[bpb], writes=[bsg])
                    P.op("dve", lambda E, ct=ct, n=n: E.tensor_tensor(out=catT[:, ct, 0:n], in0=oT[:, ct, 0:n], in1=sg[:, 0:n], op=ALU.mult),
                         reads=[boT, bsg], writes=[bcat])
                else:
                    P.op("act", lambda E, pb=pb, ct=ct, n=n: E.activation(out=gz[:, ct - 8, 0:n], in_=pb[:, 0:n], func=AF.Silu), reads=[bpb], writes=[bgz])
            for ct in range(8):
                pb = pmm[ct % 4]; bpb = bpmm[ct % 4]
                fns = [(lambda E, kt=kt, ct=ct, pb=pb, n=n: E.matmul(pb[:, 0:n], lhsT=wglu[:, kt, ct * 128:(ct + 1) * 128], rhs=yT[:, kt, 0:n],
                                                                start=(kt == 0), stop=(kt == 7))) for kt in range(8)]
                P.mm_group(fns, reads=[bwglu, byT], writes=[bpb])
                P.op("act", lambda E, pb=pb, n=n: E.activation(out=sg[:, 0:n], in_=pb[:, 0:n], func=AF.Sigmoid), reads=[bpb], writes=[bsg])
                P.op("dve", lambda E, ct=ct, n=n: E.tensor_tensor(out=sg[:, 0:n], in0=sg[:, 0:n], in1=yT[:, ct, 0:n], op=ALU.mult), reads=[bsg, byT], writes=[bsg])
                P.op("dve", lambda E, ct=ct, n=n: E.tensor_tensor(out=catT[:, 8 + ct, 0:n], in0=sg[:, 0:n], in1=gz[:, ct, 0:n], op=ALU.mult),
                     reads=[bsg, bgz], writes=[bcat])
            for t in range(ntl):
                r0 = t0 + t * 128
                outproj_post(C, catT, bcat, 16, wout, bwout, t, xt4[:, t, :], bxt[t], npost, bnpost, pso, bpso, yo, byo, sq, bsq, ss, bss,
                             out_d[r0:r0 + 128, :], bout)
        if fz:
            barrier(P)
        else:
            P.finish([bout])
    return nc


def build_L3(ntok=2048, fz=None):
    nc = fz["nc"] if fz else bass.Bass("TRN2", target_bir_lowering=False)
    pfx = fz["pfx"] if fz else ""

    def D(name, shape):
        if fz and name in fz["share"]:
            return fz["share"][name]
        return nc.dram_tensor(pfx + name, shape, F32, kind="ExternalInput").ap()
    x_d = D("x", [ntok + 128, 1024])
    win_d = D("win", [1024, 8192]); wout_d = D("wout", [2048, 1024]); conv_d = D("conv", [3, 2048])
    npre_d = D("npre", [1024]); npost_d = D("npost", [1024]); ident_d = D("ident", [128, 128])
    out_d = fz["out"] if fz else nc.dram_tensor("out", [ntok, 1024], F32, kind="ExternalOutput").ap()
    NT = 256
    with ExitStack() as st:
        C = Ctx(nc, st, fz["P"], pfx) if fz else Ctx(nc, st); P = C.P
        idf, bidf, idb, bidb = make_ident(C, ident_d)
        npre, bnpre = bcast_row_load(C, "npre", npre_d, 1024)
        npost, bnpost = bcast_row_load(C, "npost", npost_d, 1024)
        cw, bcw = C.sb("cw", [128, 3, 16])
        P.dma("sp", cw[:], conv_d.rearrange("j (c p) -> p j c", p=128), writes=[bcw])
        win, bwin_g = load_w_bf16(C, "win", win_d, 8, 8192,
                                  groups=[[(part * 2048 + cg * 512, part * 2048 + cg * 512 + 512) for part in range(4)] for cg in range(4)])
        wout, bwout = load_w_bf16(C, "wout", wout_d, 16, 1024)
        xt, bxt = C.sb("xt", [128, 1024])
        sq, bsq = C.sb("sq", [128, 1024])
        hn, bhn = C.sb("hn", [128, 1024], BF16)
        ss, bss = C.sb("ss", [128, 1])
        hT, bhT = C.sb("hT", [128, 8, NT], BF16)
        y1T, by1T = C.sb("y1T", [128, 16, NT], BF16)
        pbuf, bpbuf = C.sb("pbuf", [128, NT + 2])
        phalo, bphalo = C.sb("phalo", [128, 16, 2])
        gcs, bgcs = C.sb("gcs", [128, NT])
        cv, bcv = C.sb("cv", [128, NT])
        sz, bsz = C.sb("sz", [128, NT])
        yo, byo = C.sb("yo", [128, 1024])
        P.op("dve", lambda E: E.memset(phalo[:], 0.0), writes=[bphalo])
        ptr, bptr = C.ps("ptr", [128, 1024], BF16)
        GB = [C.ps("g%d" % i, [128, 512]) for i in range(7)]
        pso = [GB[0][0], GB[1][0]]; bpso = [GB[0][1], GB[1][1]]
        bout = fz["obuf"] if fz else Buf("out", multi=True)
        sts = [(0, 128)] + [(128 + i * NT, NT) for i in range(ntok // NT)]
        for (t0, n) in sts:
            ntl = n // 128
            for t in range(ntl):
                r0 = t0 + t * 128
                P.dma("sp", xt[:], x_d[r0:r0 + 128, :], reads=([fz["xbuf"]] if fz else []), writes=[bxt])
                rms_rstd(C, xt[:], bxt, 1024, sq[:], bsq, ss, bss)
                P.op("dve", lambda E: E.scalar_tensor_tensor(out=hn[:], in0=xt[:], scalar=ss[:, 0:1], in1=npre[:],
                                                             op0=ALU.mult, op1=ALU.mult), reads=[bxt, bss, bnpre], writes=[bhn])
                transpose8(C, hn, bhn, idb, bidb, ptr, bptr, hT[:, :, t * 128:(t + 1) * 128], bhT, eng="act")
            for ct in range(16):
                sel_ = [GB[3 * (ct % 2) + 0], GB[3 * (ct % 2) + 1], GB[3 * (ct % 2) + 2], GB[6]]
                pmm = [x_[0] for x_ in sel_]; bpmm = [x_[1] for x_ in sel_]
                for part in range(4):
                    col0 = (part * 16 + ct) * 128
                    pb = pmm[part]
                    fns = [(lambda E, n=n, kt=kt, col0=col0, pb=pb: E.matmul(pb[:, 0:n], lhsT=win[:, kt, col0:col0 + 128], rhs=hT[:, kt, 0:n],
                                                                        start=(kt == 0), stop=(kt == 7))) for kt in range(8)]
                    P.mm_group(fns, reads=[bwin_g[ct // 4], bhT], writes=[bpmm[part]])
                P.op("act", lambda E, n=n, pmm=pmm: E.copy(out=gcs[:, 0:n], in_=pmm[1][:, 0:n]), reads=[bpmm[1]], writes=[bgcs])
                P.op("act", lambda E, ct=ct: E.copy(out=pbuf[:, 0:2], in_=phalo[:, ct, :]), reads=[bphalo], writes=[bpbuf])
                P.op("dve", lambda E, n=n, pmm=pmm: E.tensor_tensor(out=pbuf[:, 2:2 + n], in0=gcs[:, 0:n], in1=pmm[2][:, 0:n], op=ALU.mult),
                     reads=[bgcs, bpmm[2]], writes=[bpbuf])
                P.op("act", lambda E, n=n, ct=ct: E.copy(out=phalo[:, ct, :], in_=pbuf[:, n:n + 2]), reads=[bpbuf], writes=[bphalo])
                if t0 == 0:
                    continue
                P.op("dve", lambda E, n=n, ct=ct: E.tensor_scalar(out=cv[:, 0:n], in0=pbuf[:, 0:n], scalar1=cw[:, 0, ct:ct + 1], scalar2=None, op0=ALU.mult),
                     reads=[bpbuf, bcw], writes=[bcv])
                P.op("dve", lambda E, n=n, ct=ct: E.scalar_tensor_tensor(out=cv[:, 0:n], in0=pbuf[:, 1:1 + n], scalar=cw[:, 1, ct:ct + 1], in1=cv[:, 0:n],
                                                                    op0=ALU.mult, op1=ALU.add), reads=[bpbuf, bcw, bcv], writes=[bcv])
                P.op("dve", lambda E, n=n, ct=ct: E.scalar_tensor_tensor(out=cv[:, 0:n], in0=pbuf[:, 2:2 + n], scalar=cw[:, 2, ct:ct + 1], in1=cv[:, 0:n],
                                                                    op0=ALU.mult, op1=ALU.add), reads=[bpbuf, bcw, bcv], writes=[bcv])
                P.op("dve", lambda E, n=n, pmm=pmm: E.tensor_tensor(out=cv[:, 0:n], in0=cv[:, 0:n], in1=pmm[0][:, 0:n], op=ALU.mult), reads=[bcv, bpmm[0]], writes=[bcv])
                P.op("act", lambda E, n=n, pmm=pmm: E.activation(out=sz[:, 0:n], in_=pmm[3][:, 0:n], func=AF.Silu), reads=[bpmm[3]], writes=[bsz])
                P.op("dve", lambda E, n=n, ct=ct: E.tensor_tensor(out=y1T[:, ct, 0:n], in0=cv[:, 0:n], in1=sz[:, 0:n], op=ALU.mult),
                     reads=[bcv, bsz], writes=[by1T])
            if t0 == 0:
                continue
            for t in range(ntl):
                r0 = t0 + t * 128
                P.dma("sp", xt[:], x_d[r0:r0 + 128, :], reads=([fz["xbuf"]] if fz else []), writes=[bxt])
                outproj_post(C, y1T, by1T, 16, wout, bwout, t, xt[:], bxt, npost, bnpost, pso, bpso, yo, byo, sq, bsq, ss, bss,
                             out_d[r0 - 128:r0, :], bout)
        if fz:
            barrier(P)
        else:
            P.finish([bout])
    return nc


_IDENT = np.eye(128, dtype=np.float32)
_CACHE = {}


def _get(name, fn):
    if name not in _CACHE:
        _CACHE[name] = fn()
    return _CACHE[name]


def run_L2(inp, o_full, ys_full):
    nc = _get("L2", build_L2)
    w_in = inp["w_in_even"][0]
    wz = np.ascontiguousarray(np.concatenate([w_in[:, 3072:4096], w_in[:, 5136:6160]], axis=1))
    maps = []
    for c in range(8):
        b, r = divmod(c, 4)
        sl = slice(r * 2048, (r + 1) * 2048)
        maps.append({"x": np.ascontiguousarray(inp["x"][b, sl]), "o": np.ascontiguousarray(o_full[b, sl]),
                     "ys": np.ascontiguousarray(ys_full[b, sl]), "wz": wz, "wglu": np.ascontiguousarray(inp["w_glu"][0]),
                     "wout": np.ascontiguousarray(inp["w_out_even"][0]), "npre": np.ascontiguousarray(inp["norm_pre"][0]),
                     "npost": np.ascontiguousarray(inp["norm_post"][0]), "gnw": np.ascontiguousarray(inp["gdn_norm_w"][0]),
                     "ident": _IDENT})
    res = run_bass_kernel_spmd(nc, maps, core_ids=list(range(8)))
    x1 = np.empty((2, 8192, 1024), np.float32)
    for c in range(8):
        b, r = divmod(c, 4)
        x1[b, r * 2048:(r + 1) * 2048] = res.results[c]["out"]
    return x1


def run_L3(inp, x1):
    nc = _get("L3", build_L3)
    maps = []
    for c in range(8):
        b, r = divmod(c, 4)
        xh = np.zeros((2048 + 128, 1024), np.float32)
        xh[128:] = x1[b, r * 2048:(r + 1) * 2048]
        if r > 0:
            xh[:128] = x1[b, r * 2048 - 128:r * 2048]
        maps.append({"x": xh, "win": np.ascontiguousarray(inp["w_in_odd"][0]), "wout": np.ascontiguousarray(inp["w_out_odd"][0]),
                     "conv": np.ascontiguousarray(inp["conv_short"][0]), "npre": np.ascontiguousarray(inp["norm_pre"][1]),
                     "npost": np.ascontiguousarray(inp["norm_post"][1]), "ident": _IDENT})
    res = run_bass_kernel_spmd(nc, maps, core_ids=list(range(8)))
    out = np.empty((2, 8192, 1024), np.float32)
    for c in range(8):
        b, r = divmod(c, 4)
        out[b, r * 2048:(r + 1) * 2048] = res.results[c]["out"]
    return out


I32 = mybir.dt.int32
TAUS = np.array(list(range(17)) + [32, 64, 128, 256, 512, 1024, 2048, 4096] + list(range(15, -1, -1)), np.float32)
NTAU = len(TAUS)


def _s5_consts():
    mk = np.zeros((128, 2, 16, 16), np.float32)
    idm = np.zeros((128, 2, 16, 16), np.float32)
    for kt2 in range(2):
        for sp in range(8):
            s = kt2 * 8 + sp
            for h in range(16):
                mk[sp * 16 + h, kt2, s:, :] = 1.0
                idm[sp * 16 + h, kt2, s, h] = 1.0
    return mk.reshape(128, 2, 256), idm.reshape(128, 2, 256)


def barrier(P):
    for e in P.ENG:
        for e2 in P.ENG:
            if P.cnt[e2] > 0:
                P._wait(e, ("c", e2, P.cnt[e2]))
        for q in P.dsem:
            j1 = P.dcnt[q]
            for j in range(max(0, j1 - NDS), j1):
                P._wait(e, ("d", q, j % NDS, 16 * (j // NDS + 1)))


def build_L1b(S=8192, fz=None):
    nc = fz["nc"] if fz else bass.Bass("TRN2", target_bir_lowering=False)
    pfx = fz["pfx"] if fz else ""

    def D(name, shape):
        if fz and name in fz["share"]:
            return fz["share"][name]
        return nc.dram_tensor(pfx + name, shape, F32, kind="ExternalInput").ap()
    x_d = D("x", [S, 1024]); npre_d = D("npre", [1024]); wu_d = D("wu", [1024, 256])
    lre_d = D("lre", [16, 64]); lim_d = D("lim", [16, 64]); bre_d = D("bre", [16, 64, 16]); bim_d = D("bim", [16, 64, 16])
    cre_d = D("cre", [16, 16, 64]); cim_d = D("cim", [16, 16, 64]); ldt_d = D("ldt", [16]); dd_d = D("dd", [256])
    taus_d = D("taus", [NTAU]); mk_d = D("mk", [128, 2, 256]); idm_d = D("idm", [128, 2, 256]); ident_d = D("ident", [128, 128])
    ys_d = fz["out"] if fz else nc.dram_tensor("ys", [S, 256], F32, kind="ExternalOutput").ap()
    NCH = S // 16
    NST = S // 512
    with ExitStack() as st:
        C = Ctx(nc, st, fz["P"], pfx) if fz else Ctx(nc, st); P = C.P
        idf, bidf, idb, bidb = make_ident(C, ident_d)
        ptr, bptr = C.ps("ptr", [128, 1024], BF16)
        py, bpy = C.ps("py", [128, 1024])
        G = []; bG = []
        for i in range(4):
            t_, b_ = C.ps("g%d" % i, [128, 512]); G.append(t_); bG.append(b_)
        U, bU = C.sb("U", [128, 2, 16, NCH], BF16)
        with ExitStack() as st2:
            C2 = Ctx(nc, st2, P, C.pfx)
            ext = fz.get("uTp") if fz else None
            if ext:
                uTp, buTp = ext
            else:
                uTp, buTp = C2.sb("uTp", [128, 2, 16, NCH], BF16)
            with ExitStack() as st1:
                C1 = Ctx(nc, st1, P, C.pfx)
                npre, bnpre = bcast_row_load(C1, "npre", npre_d, 1024)
                wu, bwu = load_w_bf16(C1, "wu", wu_d, 8, 256)
                xt, bxt = C1.sb("xt", [128, 1024])
                sq, bsq = C1.sb("sq", [128, 1024])
                hn, bhn = C1.sb("hn", [128, 1024], BF16)
                ss, bss = C1.sb("ss", [128, 1])
                hT, bhT = C1.sb("hT", [128, 8, 512], BF16)
                for s_ in range(0 if ext else NST):
                    for t in range(4):
                        r0 = s_ * 512 + t * 128
                        P.dma("sp", xt[:], x_d[r0:r0 + 128, :], writes=[bxt])
                        rms_rstd(C1, xt[:], bxt, 1024, sq[:], bsq, ss, bss)
                        P.op("dve", lambda E: E.scalar_tensor_tensor(out=hn[:], in0=xt[:], scalar=ss[:, 0:1], in1=npre[:],
                                                                     op0=ALU.mult, op1=ALU.mult), reads=[bxt, bss, bnpre], writes=[bhn])
                        transpose8(C1, hn, bhn, idb, bidb, ptr, bptr, hT[:, :, t * 128:(t + 1) * 128], bhT, eng="act")
                    for blk in range(2):
                        pb = G[blk]
                        fns = [(lambda E, kt=kt, blk=blk, pb=pb: E.matmul(
                            pb[:].rearrange("p (s n) -> p s n", s=16), lhsT=wu[:, kt, blk * 128:(blk + 1) * 128],
                            rhs=hT[:, kt, :].rearrange("p (n s) -> p s n", s=16), start=(kt == 0), stop=(kt == 7))) for kt in range(8)]
                        P.mm_group(fns, reads=[bwu, bhT], writes=[bG[blk]])
                        P.op("act" if blk == 0 else "dve",
                             (lambda E, blk=blk, pb=pb, s_=s_: E.copy(out=uTp[:, blk, :, 32 * s_:32 * s_ + 32], in_=pb[:].rearrange("p (s n) -> p s n", s=16)))
                             if blk == 0 else
                             (lambda E, blk=blk, pb=pb, s_=s_: E.tensor_copy(out=uTp[:, blk, :, 32 * s_:32 * s_ + 32], in_=pb[:].rearrange("p (s n) -> p s n", s=16))),
                             reads=[bG[blk]], writes=[buTp])
                barrier(P)
            ud2 = nc.dram_tensor(pfx + "ud2", [16, 2, 8, 16, NCH], BF16)
            bud2 = Buf("ud2", multi=True)
            bU.multi = True; bU.w = []
            for g in range(16):
                P.dma("sp", ud2.ap()[g].rearrange("k sp h n -> h (k sp) n"),
                      uTp[(g % 8) * 16:(g % 8 + 1) * 16, g // 8, :, :], reads=[buTp], writes=[bud2])
            for g in range(16):
                P.dma("sp", U[:, :, g, :], ud2.ap()[g].rearrange("k sp h n -> (sp h) k n"), reads=[bud2], writes=[bU])
            barrier(P)
        lre, blre = C.sb("lre", [128, 8]); lim, blim = C.sb("lim", [128, 8]); ldt, bldt = C.sb("ldt", [128, 8])
        TAU, bTAU = bcast_row_load(C, "TAU", taus_d, NTAU)
        Er, bEr = C.sb("Er", [128, 8, NTAU]); Ei, bEi = C.sb("Ei", [128, 8, NTAU]); NEi, bNEi = C.sb("NEi", [128, 8, NTAU])
        Hr, bHr = C.sb("Hr", [128, 8, 17, 16]); nHi, bnHi = C.sb("nHi", [128, 8, 17, 16])
        WbT, bWbT = C.sb("WbT", [128, 2, 8, 2, 128], BF16)
        Toep, bToep = C.sb("Toep", [128, 2, 16, 256], BF16)
        with ExitStack() as st3:
            C3 = Ctx(nc, st3, P, C.pfx)
            Br, bBr = C3.sb("Br", [128, 8, 16]); Bi, bBi = C3.sb("Bi", [128, 8, 16])
            Cr, bCr = C3.sb("Cr", [128, 8, 16]); Ci, bCi = C3.sb("Ci", [128, 8, 16])
            dcol, bdcol = C3.sb("dcol", [128, 16])
            MK, bMK = C3.sb("MK", [128, 2, 256]); IDM, bIDM = C3.sb("IDM", [128, 2, 256])
            P.dma("sp", MK[:], mk_d, writes=[bMK]); P.dma("sp", IDM[:], idm_d, writes=[bIDM])
            for two in range(2):
                hs = slice(64 * two, 64 * two + 64)
                P.dma("sp", lre[hs, :], lre_d.rearrange("(gp two) p -> two p gp", two=2)[two], writes=[blre])
                P.dma("sp", lim[hs, :], lim_d.rearrange("(gp two) p -> two p gp", two=2)[two], writes=[blim])
                P.dma("sp", ldt[hs, :], ldt_d.rearrange("(gp two) -> two gp", two=2)[two].partition_broadcast(64), writes=[bldt])
                P.dma("sp", Br[hs], bre_d.rearrange("(gp two) p h -> two p gp h", two=2)[two], writes=[bBr])
                P.dma("sp", Bi[hs], bim_d.rearrange("(gp two) p h -> two p gp h", two=2)[two], writes=[bBi])
                for gp in range(8):
                    P.dma("sp", Cr[hs, gp, :], cre_d[2 * gp + two].rearrange("h p -> p h"), writes=[bCr])
                    P.dma("sp", Ci[hs, gp, :], cim_d[2 * gp + two].rearrange("h p -> p h"), writes=[bCi])
            for sp in range(8):
                P.dma("sp", dcol[sp * 16:(sp + 1) * 16, :], dd_d.rearrange("(g h) -> h g", h=16), writes=[bdcol])
            sm = {}
            for nm in ("dt", "lr", "lrdt", "th", "den", "nr", "fre", "fim", "t8a", "t8b"):
                sm[nm] = C3.sb("sm_" + nm, [128, 8])
            T41 = {}
            for nm in ("ARG", "MARG", "MAG", "MAGN", "SIN", "COS", "ErN", "EiN", "rt", "rk"):
                T41[nm] = C3.sb("t41_" + nm, [128, 8, NTAU])
            rki, brki = C3.sb("rki", [128, 8, NTAU], I32)

            def tt(eng, out, bo, a, ba, b, bb_, op):
                P.op(eng, lambda E: E.tensor_tensor(out=out, in0=a, in1=b, op=op), reads=[ba, bb_], writes=[bo])

            dt, bdt = sm["dt"]; lr, blr = sm["lr"]; lrdt, blrdt = sm["lrdt"]; th, bth = sm["th"]
            P.op("act", lambda E: E.activation(out=dt[:], in_=ldt[:], func=AF.Exp), reads=[bldt], writes=[bdt])
            P.op("dve", lambda E: E.tensor_scalar(out=lr[:], in0=lre[:], scalar1=-1e-4, scalar2=None, op0=ALU.min), reads=[blre], writes=[blr])
            tt("dve", lrdt[:], blrdt, lr[:], blr, dt[:], bdt, ALU.mult)
            tt("dve", th[:], bth, lim[:], blim, dt[:], bdt, ALU.mult)
            ARG, bARG = T41["ARG"]; MARG, bMARG = T41["MARG"]; MAG, bMAG = T41["MAG"]; MAGN, bMAGN = T41["MAGN"]
            SIN, bSIN = T41["SIN"]; COS, bCOS = T41["COS"]; ErN, bErN = T41["ErN"]; EiN, bEiN = T41["EiN"]
            rt, brt = T41["rt"]; rk, brk = T41["rk"]
            tb = TAU[:].unsqueeze(1).to_broadcast([128, 8, NTAU])
            tt("dve", ARG[:], bARG, th[:].unsqueeze(2).to_broadcast([128, 8, NTAU]), bth, tb, bTAU, ALU.mult)
            tt("dve", MARG[:], bMARG, lrdt[:].unsqueeze(2).to_broadcast([128, 8, NTAU]), blrdt, tb, bTAU, ALU.mult)
            P.op("act", lambda E: E.activation(out=MAG[:], in_=MARG[:], func=AF.Exp), reads=[bMARG], writes=[bMAG])
            P.op("act", lambda E: E.activation(out=MAGN[:, :, 0:17], in_=MARG[:, :, 0:17], func=AF.Exp, scale=-1.0), reads=[bMARG], writes=[bMAGN])

            def sin_of(dst, bdst, shift):
                P.op("dve", lambda E: E.tensor_scalar(out=rt[:], in0=ARG[:], scalar1=float(shift), scalar2=None, op0=ALU.add), reads=[bARG], writes=[brt])
                P.op("dve", lambda E: E.tensor_scalar(out=rki[:], in0=rt[:], scalar1=float(1.0 / (2 * np.pi)), scalar2=None, op0=ALU.mult), reads=[brt], writes=[brki])
                P.op("dve", lambda E: E.tensor_copy(out=rk[:], in_=rki[:]), reads=[brki], writes=[brk])
                P.op("dve", lambda E: E.scalar_tensor_tensor(out=rt[:], in0=rk[:], scalar=float(-2 * np.pi), in1=rt[:], op0=ALU.mult, op1=ALU.add),
                     reads=[brk, brt], writes=[brt])
                P.op("dve", lambda E: E.tensor_scalar(out=rt[:], in0=rt[:], scalar1=-3.14159, scalar2=3.14159, op0=ALU.max, op1=ALU.min), reads=[brt], writes=[brt])
                P.op("act", lambda E: E.activation(out=dst[:], in_=rt[:], func=AF.Sin), reads=[brt], writes=[bdst])

            sin_of(SIN, bSIN, 0.0)
            sin_of(COS, bCOS, np.pi / 2)
            tt("dve", Er[:], bEr, MAG[:], bMAG, COS[:], bCOS, ALU.mult)
            tt("dve", Ei[:], bEi, MAG[:], bMAG, SIN[:], bSIN, ALU.mult)
            P.op("dve", lambda E: E.tensor_scalar(out=NEi[:], in0=Ei[:], scalar1=-1.0, scalar2=None, op0=ALU.mult), reads=[bEi], writes=[bNEi])
            tt("dve", ErN[:, :, 0:17], bErN, MAGN[:, :, 0:17], bMAGN, COS[:, :, 0:17], bCOS, ALU.mult)
            tt("dve", EiN[:, :, 0:17], bEiN, MAGN[:, :, 0:17], bMAGN, SIN[:, :, 0:17], bSIN, ALU.mult)
            P.op("dve", lambda E: E.tensor_scalar(out=EiN[:, :, 0:17], in0=EiN[:, :, 0:17], scalar1=-1.0, scalar2=None, op0=ALU.mult), reads=[bEiN], writes=[bEiN])
            den, bden = sm["den"]; nr, bnr = sm["nr"]; fre, bfre = sm["fre"]; fim, bfim = sm["fim"]; t8a, bt8a = sm["t8a"]; t8b, bt8b = sm["t8b"]
            tt("dve", den[:], bden, lr[:], blr, lr[:], blr, ALU.mult)
            tt("dve", t8a[:], bt8a, lim[:], blim, lim[:], blim, ALU.mult)
            tt("dve", den[:], bden, den[:], bden, t8a[:], bt8a, ALU.add)
            P.op("dve", lambda E: E.reciprocal(out=den[:], in_=den[:]), reads=[bden], writes=[bden])
            P.op("dve", lambda E: E.tensor_scalar(out=nr[:], in0=Er[:, :, 1], scalar1=-1.0, scalar2=None, op0=ALU.add), reads=[bEr], writes=[bnr])
            tt("dve", fre[:], bfre, nr[:], bnr, lr[:], blr, ALU.mult)
            tt("dve", t8a[:], bt8a, Ei[:, :, 1], bEi, lim[:], blim, ALU.mult)
            tt("dve", fre[:], bfre, fre[:], bfre, t8a[:], bt8a, ALU.add)
            tt("dve", fre[:], bfre, fre[:], bfre, den[:], bden, ALU.mult)
            tt("dve", fim[:], bfim, Ei[:, :, 1], bEi, lr[:], blr, ALU.mult)
            tt("dve", t8b[:], bt8b, nr[:], bnr, lim[:], blim, ALU.mult)
            tt("dve", fim[:], bfim, fim[:], bfim, t8b[:], bt8b, ALU.subtract)
            tt("dve", fim[:], bfim, fim[:], bfim, den[:], bden, ALU.mult)

            def cmul(outr, boutr, outi, bouti, ar, bar, ai, bai, br_, bbr_, bi_, bbi_, tmp, btmp):
                tt("dve", outr, boutr, ar, bar, br_, bbr_, ALU.mult)
                tt("dve", tmp, btmp, ai, bai, bi_, bbi_, ALU.mult)
                tt("dve", outr, boutr, outr, boutr, tmp, btmp, ALU.subtract)
                tt("dve", outi, bouti, ar, bar, bi_, bbi_, ALU.mult)
                tt("dve", tmp, btmp, ai, bai, br_, bbr_, ALU.mult)
                tt("dve", outi, bouti, outi, bouti, tmp, btmp, ALU.add)

            bbr, bbbr = C3.sb("bbr", [128, 8, 16]); bbi, bbbi = C3.sb("bbi", [128, 8, 16]); tmp16, btmp16 = C3.sb("tmp16", [128, 8, 16])
            fb = lambda t_: t_[:].unsqueeze(2).to_broadcast([128, 8, 16])
            cmul(bbr[:], bbbr, bbi[:], bbbi, fb(fre), bfre, fb(fim), bfim, Br[:], bBr, Bi[:], bBi, tmp16[:], btmp16)
            Gr, bGr = C3.sb("Gr", [128, 8, 16, 16]); Gi, bGi = C3.sb("Gi", [128, 8, 16, 16])
            WPr, bWPr = C3.sb("WPr", [128, 8, 16, 16]); WPi, bWPi = C3.sb("WPi", [128, 8, 16, 16])
            Hi, bHi = C3.sb("Hi", [128, 8, 17, 16]); tmpH, btmpH = C3.sb("tmpH", [128, 8, 17, 16])
            eb = lambda t_, j0, j1: t_[:, :, j0:j1].unsqueeze(3).to_broadcast([128, 8, j1 - j0, 16])
            vb = lambda t_, n_: t_[:].unsqueeze(2).to_broadcast([128, 8, n_, 16])
            cmul(Gr[:], bGr, Gi[:], bGi, eb(ErN, 0, 16), bErN, eb(EiN, 0, 16), bEiN, vb(bbr, 16), bbbr, vb(bbi, 16), bbbi, tmpH[:, :, 0:16, :], btmpH)
            cmul(WPr[:], bWPr, WPi[:], bWPi, eb(Er, 25, 41), bEr, eb(Ei, 25, 41), bEi, vb(bbr, 16), bbbr, vb(bbi, 16), bbbi, tmpH[:, :, 0:16, :], btmpH)
            cmul(Hr[:], bHr, Hi[:], bHi, eb(Er, 0, 17), bEr, eb(Ei, 0, 17), bEi, vb(Cr, 17), bCr, vb(Ci, 17), bCi, tmpH[:], btmpH)
            P.op("dve", lambda E: E.tensor_scalar(out=nHi[:], in0=Hi[:], scalar1=-1.0, scalar2=None, op0=ALU.mult), reads=[bHi], writes=[bnHi])
            for gp in range(8):
                for kt2 in range(2):
                    for c, (WP_, bWP_) in enumerate(((WPr, bWPr), (WPi, bWPi))):
                        P.op("pe", lambda E, gp=gp, kt2=kt2, WP_=WP_: E.transpose(
                            out=G[2][:, 0:128], in_=WP_[:, gp, kt2 * 8:(kt2 + 1) * 8, :].rearrange("p s h -> p (s h)"), identity=idf[:]),
                            reads=[bWP_, bidf], writes=[bG[2]])
                        P.op("act", lambda E, gp=gp, kt2=kt2, c=c: E.copy(out=WbT[:, kt2, gp, c, :], in_=G[2][:, 0:128]), reads=[bG[2]], writes=[bWbT])
            tmpT, btmpT = C3.sb("tmpT", [128, 256])
            for g in range(16):
                gp = g // 2; hs = slice(64 * (g % 2), 64 * (g % 2) + 64)
                for kt2 in range(2):
                    fns = [
                        lambda E, gp=gp, hs=hs, kt2=kt2: E.matmul(G[3][:, 0:256], lhsT=Gr[hs, gp, kt2 * 8:(kt2 + 1) * 8, :].rearrange("p s h -> p (s h)"),
                                                                  rhs=Hr[hs, gp, 0:16, :].rearrange("p t h -> p (t h)"), start=True, stop=False),
                        lambda E, gp=gp, hs=hs, kt2=kt2: E.matmul(G[3][:, 0:256], lhsT=Gi[hs, gp, kt2 * 8:(kt2 + 1) * 8, :].rearrange("p s h -> p (s h)"),
                                                                  rhs=nHi[hs, gp, 0:16, :].rearrange("p t h -> p (t h)"), start=False, stop=True)]
                    P.mm_group(fns, reads=[bGr, bGi, bHr, bnHi], writes=[bG[3]])
                    P.op("dve", lambda E, kt2=kt2: E.tensor_tensor(out=tmpT[:], in0=G[3][:, 0:256], in1=MK[:, kt2, :], op=ALU.mult),
                         reads=[bG[3], bMK], writes=[btmpT])
                    P.op("dve", lambda E, kt2=kt2, g=g: E.scalar_tensor_tensor(out=Toep[:, kt2, g, :], in0=IDM[:, kt2, :], scalar=dcol[:, g:g + 1], in1=tmpT[:],
                                                                               op0=ALU.mult, op1=ALU.add), reads=[bIDM, bdcol, btmpT], writes=[bToep])
            barrier(P)
        X = {}
        for bufn in ("A", "B"):
            for c in ("re", "im"):
                X[(bufn, c)] = (C.sb("X%s%s" % (bufn, c), [128, 8, NCH + 1])[0], [Buf("X%s%s%d" % (bufn, c, gp)) for gp in range(8)])
        Ysb, bYsb = C.sb("Ysb", [128, 16, 256])
        for key in X:
            t_, bl = X[key]
            P.op("dve", lambda E, t_=t_: E.memset(t_[:, :, 0:1], 0.0), writes=bl)
        for gp in range(8):
            for c, cn in enumerate(("re", "im")):
                px = G[c]
                fns = []
                for two in range(2):
                    g = 2 * gp + two
                    for kt2 in range(2):
                        fns.append(lambda E, two=two, g=g, kt2=kt2, gp=gp, c=c, px=px: E.matmul(
                            px[64 * two:64 * two + 64, :], lhsT=WbT[:, kt2, gp, c, 64 * two:64 * two + 64], rhs=U[:, kt2, g, :],
                            start=(kt2 == 0), stop=(kt2 == 1)))
                P.mm_group(fns, reads=[bWbT, bU], writes=[bG[c]])
                xt_, xb_ = X[("A", cn)]
                P.op("act", lambda E, xt_=xt_, gp=gp, px=px: E.copy(out=xt_[:, gp, 1:NCH + 1], in_=px[:]), reads=[bG[c]], writes=[xb_[gp]])
        for k in range(9):
            d = 1 << k
            j = 16 if k == 0 else 16 + k
            src, dst = ("A", "B") if k % 2 == 0 else ("B", "A")
            sre, bsre = X[(src, "re")]; sim, bsim = X[(src, "im")]
            dre, bdre = X[(dst, "re")]; dim_, bdim = X[(dst, "im")]
            P.op("dve", lambda E, dre=dre, sre=sre, d=d: E.tensor_copy(out=dre[:, :, 1:1 + d], in_=sre[:, :, 1:1 + d]), reads=bsre, writes=bdre)
            P.op("pool", lambda E, dim_=dim_, sim=sim, d=d: E.tensor_copy(out=dim_[:, :, 1:1 + d], in_=sim[:, :, 1:1 + d]), reads=bsim, writes=bdim)
            for gp in range(8):
                lo = slice(1, NCH + 1 - d); hi = slice(1 + d, NCH + 1)
                P.op("dve", lambda E, gp=gp, j=j, dre=dre, sre=sre, lo=lo, hi=hi: E.scalar_tensor_tensor(
                    out=dre[:, gp, hi], in0=sre[:, gp, lo], scalar=Er[:, gp, j:j + 1], in1=sre[:, gp, hi], op0=ALU.mult, op1=ALU.add),
                    reads=[bsre[gp], bEr], writes=[bdre[gp]])
                P.op("dve", lambda E, gp=gp, j=j, dre=dre, sim=sim, lo=lo, hi=hi: E.scalar_tensor_tensor(
                    out=dre[:, gp, hi], in0=sim[:, gp, lo], scalar=NEi[:, gp, j:j + 1], in1=dre[:, gp, hi], op0=ALU.mult, op1=ALU.add),
                    reads=[bsim[gp], bNEi, bdre[gp]], writes=[bdre[gp]])
                P.op("dve", lambda E, gp=gp, j=j, dim_=dim_, sim=sim, lo=lo, hi=hi: E.scalar_tensor_tensor(
                    out=dim_[:, gp, hi], in0=sim[:, gp, lo], scalar=Er[:, gp, j:j + 1], in1=sim[:, gp, hi], op0=ALU.mult, op1=ALU.add),
                    reads=[bsim[gp], bEr], writes=[bdim[gp]])
                P.op("dve", lambda E, gp=gp, j=j, dim_=dim_, sre=sre, lo=lo, hi=hi: E.scalar_tensor_tensor(
                    out=dim_[:, gp, hi], in0=sre[:, gp, lo], scalar=Ei[:, gp, j:j + 1], in1=dim_[:, gp, hi], op0=ALU.mult, op1=ALU.add),
                    reads=[bsre[gp], bEi, bdim[gp]], writes=[bdim[gp]])
        fre_, bfre_ = X[("B", "re")]; fim_, bfim_ = X[("B", "im")]
        bys = None if fz else Buf("ys", multi=True)
        ysv = ys_d.rearrange("(n t) c -> n t c", t=16)
        for jt in range(NCH // 128):
            for gq in range(4):
                fns = []
                for gi in range(4):
                    g = 4 * gq + gi; gp = g // 2; hs = slice(64 * (g % 2), 64 * (g % 2) + 64)
                    o_ = (gi * 256, (gi + 1) * 256)
                    for kt2 in range(2):
                        fns.append(lambda E, o_=o_, g=g, kt2=kt2, jt=jt: E.matmul(
                            py[:, o_[0]:o_[1]], lhsT=U[:, kt2, g, jt * 128:(jt + 1) * 128], rhs=Toep[:, kt2, g, :], start=(kt2 == 0), stop=False))
                    fns.append(lambda E, o_=o_, gp=gp, hs=hs, jt=jt: E.matmul(
                        py[:, o_[0]:o_[1]], lhsT=fre_[hs, gp, jt * 128:(jt + 1) * 128], rhs=Hr[hs, gp, 1:17, :].rearrange("p t h -> p (t h)"),
                        start=False, stop=False))
                    fns.append(lambda E, o_=o_, gp=gp, hs=hs, jt=jt: E.matmul(
                        py[:, o_[0]:o_[1]], lhsT=fim_[hs, gp, jt * 128:(jt + 1) * 128], rhs=nHi[hs, gp, 1:17, :].rearrange("p t h -> p (t h)"),
                        start=False, stop=True))
                P.mm_group(fns, reads=[bU, bToep, bHr, bnHi] + bfre_ + bfim_, writes=[bpy])
                P.op("act" if gq % 2 == 0 else "dve",
                     (lambda E, gq=gq: E.copy(out=Ysb[:].rearrange("p t (g h) -> p g t h", h=16)[:, 4 * gq:4 * gq + 4],
                                              in_=py[:].rearrange("p (g t h) -> p g t h", g=4, h=16)))
                     if gq % 2 == 0 else
                     (lambda E, gq=gq: E.tensor_copy(out=Ysb[:].rearrange("p t (g h) -> p g t h", h=16)[:, 4 * gq:4 * gq + 4],
                                                     in_=py[:].rearrange("p (g t h) -> p g t h", g=4, h=16))),
                     reads=[bpy], writes=[bYsb])
            P.dma("sp", ysv[jt * 128:(jt + 1) * 128, :, :], Ysb[:], reads=[bYsb], writes=[fz["obuf_of"](jt) if fz else bys])
            if fz:
                fz["after_chunk"](jt)
        if fz:
            barrier(P)
        else:
            P.finish([bys])
    return nc


def run_L1b(inp):
    nc = _get("L1b", build_L1b)
    mk, idm = _s5_consts()
    w_in = inp["w_in_even"][0]
    maps = []
    for c in range(8):
        b, r = divmod(c, 4)
        gs = slice(16 * r, 16 * r + 16)
        maps.append({"x": np.ascontiguousarray(inp["x"][b]), "npre": np.ascontiguousarray(inp["norm_pre"][0]),
                     "wu": np.ascontiguousarray(w_in[:, 4112 + 256 * r:4112 + 256 * (r + 1)]),
                     "lre": np.ascontiguousarray(inp["s5_lam_re"][0, gs]), "lim": np.ascontiguousarray(inp["s5_lam_im"][0, gs]),
                     "bre": np.ascontiguousarray(inp["s5_b_re"][0, gs]), "bim": np.ascontiguousarray(inp["s5_b_im"][0, gs]),
                     "cre": np.ascontiguousarray(inp["s5_c_re"][0, gs]), "cim": np.ascontiguousarray(inp["s5_c_im"][0, gs]),
                     "ldt": np.ascontiguousarray(inp["s5_log_dt"][0, gs]), "dd": np.ascontiguousarray(inp["s5_d"][0, 256 * r:256 * (r + 1)]),
                     "taus": TAUS, "mk": mk, "idm": idm, "ident": _IDENT})
    res = run_bass_kernel_spmd(nc, maps, core_ids=list(range(8)))
    ys = np.empty((2, 8192, 1024), np.float32)
    for c in range(8):
        b, r = divmod(c, 4)
        ys[b, :, 256 * r:256 * (r + 1)] = res.results[c]["ys"]
    return ys


def _gdn_consts():
    p = np.arange(64)[:, None]; f = np.arange(64)[None, :]
    negu = np.where(f >= p, 0.0, -30000.0)
    negls = np.where(f < p, 0.0, -30000.0)
    nsu = np.where(f > p, -1.0, 0.0)
    i64 = np.eye(64)
    c64 = np.stack([negu, negls, nsu, i64], axis=1).astype(np.float32)
    cmask = np.ones((2, 512), np.float32); cmask[:, 0::64] = 0.0
    sel = np.zeros((2, 2, 128), np.float32); sel[0, 0, :] = 1.0; sel[1, 1, :] = 1.0
    return c64, cmask, sel


def build_L1a(S=8192, fz=None):
    nc = fz["nc"] if fz else bass.Bass("TRN2", target_bir_lowering=False)
    pfx = fz["pfx"] if fz else ""

    def D(name, shape):
        if fz and name in fz["share"]:
            return fz["share"][name]
        return nc.dram_tensor(pfx + name, shape, F32, kind="ExternalInput").ap()
    x_d = D("x", [S, 1024]); npre_d = D("npre", [1024]); w_d = D("w", [1024, 768]); wb_d = D("wb", [1024, 2]); wa_d = D("wa", [1024, 2])
    conv_d = D("conv", [4, 768]); alog_d = D("alog", [2]); dtb_d = D("dtb", [2])
    ident_d = D("ident", [128, 128]); c64_d = D("c64", [64, 4, 64]); cmask_d = D("cmask", [2, 512]); sel_d = D("sel", [2, 2, 128])
    ones_d = D("ones", [128, 128])
    o_d = fz["out"] if fz else nc.dram_tensor("o", [S, 256], F32, kind="ExternalOutput").ap()
    NST = S // 512
    with ExitStack() as st:
        C = Ctx(nc, st, fz["P"], pfx) if fz else Ctx(nc, st); P = C.P
        idf, bidf, idb, bidb = make_ident(C, ident_d)
        npre, bnpre = bcast_row_load(C, "npre", npre_d, 1024)
        w, bw = load_w_bf16(C, "w", w_d, 8, 768)
        wb, bwb = load_w_bf16(C, "wb", wb_d, 8, 2)
        wa, bwa = load_w_bf16(C, "wa", wa_d, 8, 2)
        cw, bcw = C.sb("cw", [128, 4, 6])
        P.dma("sp", cw[:], conv_d.rearrange("j (c p) -> p j c", p=128), writes=[bcw])
        extu = fz.get("uTp") if fz else None
        if extu:
            wu_d = D("wu", [1024, 256])
            wu, bwu = load_w_bf16(C, "wu", wu_d, 8, 256)
            uTp, buTp = extu
        c64, bc64 = C.sb("c64", [64, 4, 64]); P.dma("sp", c64[:], c64_d, writes=[bc64])
        NEGU = c64[:, 0, :]; NEGLS = c64[:, 1, :]; NSU = c64[:, 2, :]; I64 = c64[:, 3, :]
        cmask, bcmask = C.sb("cmask", [2, 512]); P.dma("sp", cmask[:], cmask_d, writes=[bcmask])
        sel, bsel = C.sb("sel", [2, 2, 128]); P.dma("sp", sel[:], sel_d, writes=[bsel])
        ones, bones = C.sb("ones", [128, 128]); P.dma("sp", ones[:], ones_d, writes=[bones])
        onesb, bonesb = C.sb("onesb", [128, 128], BF16)
        P.op("dve", lambda E: E.tensor_copy(out=onesb[:], in_=ones[:]), reads=[bones], writes=[bonesb])
        sqb, bsqb = C.sb("sqb", [128, 512], BF16)
        alog, balog = C.sb("alog", [2, 1]); P.dma("sp", alog[:], alog_d.rearrange("(a b) -> a b", b=1), writes=[balog])
        dtb, bdtb = C.sb("dtb", [2, 1]); P.dma("sp", dtb[:], dtb_d.rearrange("(a b) -> a b", b=1), writes=[bdtb])
        negA, bnegA = C.sb("negA", [2, 1])
        P.op("act", lambda E: E.activation(out=negA[:], in_=alog[:], func=AF.Exp), reads=[balog], writes=[bnegA])
        P.op("dve", lambda E: E.tensor_scalar(out=negA[:], in0=negA[:], scalar1=-1.0, scalar2=None, op0=ALU.mult), reads=[bnegA], writes=[bnegA])
        xt, bxt = C.sb("xt", [128, 1024]); sq, bsq = C.sb("sq", [128, 1024]); hn, bhn = C.sb("hn", [128, 1024], BF16)
        ss, bss = C.sb("ss", [128, 1]); hT, bhT = C.sb("hT", [128, 8, 512], BF16)
        raw, _ = C.sb("raw", [128, 6, 515]); braw = [Buf("raw%d" % i) for i in range(6)]
        cvq, bcvq = C.sb("cvq", [128, 512])
        act, _ = C.sb("act", [128, 4, 512]); bact = [Buf("act%d" % i) for i in range(4)]
        vbuf2 = []; qk2 = []; bqk2 = []
        for par_ in range(2):
            vt_, _ = C.sb("vbuf%d" % par_, [128, 2, 512]); vbuf2.append((vt_, [Buf("vb%d_%d" % (par_, i)) for i in range(2)]))
            qt_, _ = C.sb("qk%d" % par_, [128, 4, 512]); qk2.append(qt_); bqk2.append([Buf("qk%d_%d" % (par_, i)) for i in range(4)])
        rn, brn = C.sb("rn", [128, 512])
        brow, bbrow = C.sb("brow", [2, 512]); grow, bgrow = C.sb("grow", [2, 512]); gcrow, bgcrow = C.sb("gcrow", [2, 512])
        GCB2 = []; BB2 = []
        for par_ in range(2):
            GCB2.append([C.sb("GCB%d_%d" % (par_, h), [128, 512]) for h in range(2)])
            BB2.append([C.sb("BB%d_%d" % (par_, h), [128, 512]) for h in range(2)])
        m64 = {}
        for nm in ("arg1", "DT", "Ds", "tmp", "tmp2", "BBm"):
            m64[nm] = C.sb("m_" + nm, [64, 512])
        for nm in ("Pa", "Pb", "Qa", "Qb"):
            m64[nm] = C.sb("m_" + nm, [64, 512], BF16)
        heads = []
        for h in range(2):
            H = {}
            H["attnT"] = C.sb("attnT%d" % h, [64, 512], BF16); H["Y"] = C.sb("Y%d" % h, [64, 512]); H["Ybf"] = C.sb("Ybf%d" % h, [64, 512], BF16)
            H["EG"] = C.sb("EG%d" % h, [128, 512]); H["qdec"] = C.sb("qdec%d" % h, [128, 512], BF16)
            H["kTb"] = C.sb("kTb%d" % h, [128, 512], BF16); H["Sbf"] = C.sb("Sbf%d" % h, [128, 128], BF16)
            H["bv"] = C.sb("bv%d" % h, [64, 8, 128]); H["kdec"] = C.sb("kdec%d" % h, [64, 8, 128], BF16)
            H["nbg"] = C.sb("nbg%d" % h, [64, 8]); H["osb"] = C.sb("osb%d" % h, [128, 8, 128])
            H["vnew"] = C.sb("vnew%d" % h, [64, 128], BF16); H["rhs2"] = C.sb("rhs2%d" % h, [64, 128], BF16)
            heads.append(H)
        small = {}
        for nm in ("gccol", "bcol", "nbcol", "elast", "egc"):
            small[nm] = C.sb("s_" + nm, [64, 8])
        Sst = [C.sb("S%d" % h, [128, 128]) for h in range(2)]
        for h in range(2):
            P.op("dve", lambda E, h=h: E.memset(Sst[h][0][:], 0.0), writes=[Sst[h][1]])
            P.op("dve", lambda E, h=h: E.memset(heads[h]["Sbf"][0][:], 0.0), writes=[heads[h]["Sbf"][1]])
        P.op("dve", lambda E: E.memset(raw[:, :, 0:3], 0.0), writes=braw)
        ptr, bptr = C.ps("ptr", [128, 1024], BF16)
        G = [C.ps("gp%d" % i, [128, 512]) for i in range(7)]
        GP = G[0:4]
        GA = G[4:7]
        ga_ctr = [0]

        def next_ga():
            ga_ctr[0] += 1
            return GA[ga_ctr[0] % 3]
        bo = None if fz else Buf("o", multi=True)
        if fz is not None and fz.get("debug"):
            print("L1a sbuf remaining", nc.sbuf_bytes_remaining)

        def tt(out, bo_, a, ba, b, bb_, op, eng="dve"):
            P.op(eng, lambda E: E.tensor_tensor(out=out, in0=a, in1=b, op=op), reads=ba if isinstance(ba, list) else [ba], writes=[bo_])

        def stageA(s_):
            par = s_ % 2
            qk = qk2[par]; bqk = bqk2[par]; GCB = GCB2[par]; BB = BB2[par]; vb, bvb = vbuf2[par]
            for t in range(4):
                r0 = s_ * 512 + t * 128
                P.dma("sp", xt[:], x_d[r0:r0 + 128, :], writes=[bxt])
                rms_rstd(C, xt[:], bxt, 1024, sq[:], bsq, ss, bss)
                P.op("dve", lambda E: E.scalar_tensor_tensor(out=hn[:], in0=xt[:], scalar=ss[:, 0:1], in1=npre[:],
                                                             op0=ALU.mult, op1=ALU.mult), reads=[bxt, bss, bnpre], writes=[bhn])
                transpose8(C, hn, bhn, idb, bidb, ptr, bptr, hT[:, :, t * 128:(t + 1) * 128], bhT, eng="act")
                yield
            for ct in range(6):
                pa, bpa = next_ga()
                fns = [(lambda E, kt=kt, ct=ct, pa=pa: E.matmul(pa[:], lhsT=w[:, kt, ct * 128:(ct + 1) * 128], rhs=hT[:, kt, :],
                                                                start=(kt == 0), stop=(kt == 7))) for kt in range(8)]
                P.mm_group(fns, reads=[bw, bhT], writes=[bpa])
                P.op("act", lambda E, ct=ct, pa=pa: E.copy(out=raw[:, ct, 3:515], in_=pa[:]), reads=[bpa], writes=[braw[ct]])
                P.op("dve", lambda E, ct=ct: E.tensor_scalar(out=cvq[:], in0=raw[:, ct, 0:512], scalar1=cw[:, 0, ct:ct + 1], scalar2=None, op0=ALU.mult),
                     reads=[braw[ct], bcw], writes=[bcvq])
                for j in range(1, 4):
                    P.op("dve", lambda E, ct=ct, j=j: E.scalar_tensor_tensor(out=cvq[:], in0=raw[:, ct, j:j + 512], scalar=cw[:, j, ct:ct + 1], in1=cvq[:],
                                                                             op0=ALU.mult, op1=ALU.add), reads=[braw[ct], bcw, bcvq], writes=[bcvq])
                P.op("act", lambda E, ct=ct: E.copy(out=raw[:, ct, 0:3], in_=raw[:, ct, 512:515]), reads=[braw[ct]], writes=[braw[ct]])
                if ct < 4:
                    P.op("act", lambda E, ct=ct: E.activation(out=act[:, ct, :], in_=cvq[:], func=AF.Silu), reads=[bcvq], writes=[bact[ct]])
                else:
                    P.op("act", lambda E, ct=ct, vb=vb: E.activation(out=vb[:, ct - 4, :], in_=cvq[:], func=AF.Silu), reads=[bcvq], writes=[bvb[ct - 4]])
                yield
            if extu:
                for blk in range(2):
                    pa, bpa = next_ga()
                    fns = [(lambda E, kt=kt, blk=blk, pa=pa: E.matmul(
                        pa[:].rearrange("p (s n) -> p s n", s=16), lhsT=wu[:, kt, blk * 128:(blk + 1) * 128],
                        rhs=hT[:, kt, :].rearrange("p (n s) -> p s n", s=16), start=(kt == 0), stop=(kt == 7))) for kt in range(8)]
                    P.mm_group(fns, reads=[bwu, bhT], writes=[bpa])
                    P.op("act", lambda E, blk=blk, pa=pa, s_=s_: E.copy(out=uTp[:, blk, :, 32 * s_:32 * s_ + 32], in_=pa[:].rearrange("p (s n) -> p s n", s=16)),
                         reads=[bpa], writes=[buTp])
                    yield
            for ct in range(4):
                pa, bpa = next_ga()
                P.op("act", lambda E, ct=ct: E.activation(out=sqb[:], in_=act[:, ct, :], func=AF.Square), reads=[bact[ct]], writes=[bsqb])
                P.op("pe", lambda E, pa=pa: E.matmul(pa[:], lhsT=onesb[:], rhs=sqb[:], start=True, stop=True), reads=[bonesb, bsqb], writes=[bpa])
                P.op("act", lambda E, pa=pa: E.activation(out=rn[:], in_=pa[:], func=AF.Ln, bias=1e-6, scale=1.0), reads=[bpa], writes=[brn])
                P.op("act", lambda E: E.activation(out=rn[:], in_=rn[:], func=AF.Exp, scale=-0.5), reads=[brn], writes=[brn])
                if ct < 2:
                    P.op("dve", lambda E, ct=ct, qk=qk: E.scalar_tensor_tensor(out=qk[:, ct, :], in0=act[:, ct, :], scalar=float(128 ** -0.5), in1=rn[:],
                                                                               op0=ALU.mult, op1=ALU.mult), reads=[bact[ct], brn], writes=[bqk[ct]])
                else:
                    P.op("dve", lambda E, ct=ct, qk=qk: E.tensor_tensor(out=qk[:, ct, :], in0=act[:, ct, :], in1=rn[:], op=ALU.mult),
                         reads=[bact[ct], brn], writes=[bqk[ct]])
                yield
            pa, bpa = next_ga()
            fns = [(lambda E, kt=kt, pa=pa: E.matmul(pa[0:2, :], lhsT=wb[:, kt, 0:2], rhs=hT[:, kt, :], start=(kt == 0), stop=(kt == 7))) for kt in range(8)]
            P.mm_group(fns, reads=[bwb, bhT], writes=[bpa])
            P.op("act", lambda E, pa=pa: E.activation(out=brow[:], in_=pa[0:2, :], func=AF.Sigmoid), reads=[bpa], writes=[bbrow])
            pa2, bpa2 = next_ga()
            fns = [(lambda E, kt=kt, pa2=pa2: E.matmul(pa2[0:2, :], lhsT=wa[:, kt, 0:2], rhs=hT[:, kt, :], start=(kt == 0), stop=(kt == 7))) for kt in range(8)]
            P.mm_group(fns, reads=[bwa, bhT], writes=[bpa2])
            P.op("act", lambda E, pa2=pa2: E.activation(out=grow[:], in_=pa2[0:2, :], func=AF.Exp, bias=dtb[:, 0:1], scale=1.0), reads=[bpa2, bdtb], writes=[bgrow])
            P.op("act", lambda E: E.activation(out=grow[:], in_=grow[:], func=AF.Ln, bias=1.0, scale=1.0), reads=[bgrow], writes=[bgrow])
            P.op("dve", lambda E: E.tensor_scalar(out=grow[:], in0=grow[:], scalar1=negA[:, 0:1], scalar2=None, op0=ALU.mult), reads=[bgrow, bnegA], writes=[bgrow])
            P.op("dve", lambda E: E.tensor_tensor_scan(out=gcrow[:], data0=cmask[:], data1=grow[:], initial=0.0, op0=ALU.mult, op1=ALU.add),
                 reads=[bcmask, bgrow], writes=[bgcrow])
            yield
            for h in range(2):
                pa, bpa = next_ga()
                P.op("pe", lambda E, h=h, pa=pa: E.matmul(pa[:], lhsT=sel[:, h, :], rhs=gcrow[:], start=True, stop=True), reads=[bsel, bgcrow], writes=[bpa])
                P.op("act", lambda E, h=h, pa=pa, GCB=GCB: E.copy(out=GCB[h][0][:], in_=pa[:]), reads=[bpa], writes=[GCB[h][1]])
                pa, bpa = next_ga()
                P.op("pe", lambda E, h=h, pa=pa: E.matmul(pa[:], lhsT=sel[:, h, :], rhs=brow[:], start=True, stop=True), reads=[bsel, bbrow], writes=[bpa])
                P.op("act", lambda E, h=h, pa=pa, BB=BB: E.copy(out=BB[h][0][:], in_=pa[:]), reads=[bpa], writes=[BB[h][1]])
                yield

        for _ in stageA(0):
            pass
        for s_ in range(NST):
            par = s_ % 2
            qk = qk2[par]; bqk = bqk2[par]; GCB = GCB2[par]; BB = BB2[par]; vb, bvb = vbuf2[par]
            nxt = stageA(s_ + 1) if s_ + 1 < NST else None

            def advance(k):
                if nxt is not None:
                    for _ in range(k):
                        next(nxt, None)
            for h in range(2):
                qT = qk[:, h, :]; bqT = bqk[h]; kT = qk[:, 2 + h, :]; bkT = bqk[2 + h]; vT = vb[:, h, :]; bvT = bvb[h]
                gcb, bgcb = GCB[h]; bb, bbb = BB[h]
                H = heads[h]
                attnT, battnT = H["attnT"]; Y, bY = H["Y"]; EG, bEG = H["EG"]; qdec, bqdec = H["qdec"]
                Ybf, bYbf = H["Ybf"]
                bv, bbv = H["bv"]; kdec, bkdec = H["kdec"]; nbg, bnbg = H["nbg"]
                arg1, barg1 = m64["arg1"]; DT, bDT = m64["DT"]; Ds, bDs = m64["Ds"]
                tmp, btmp = m64["tmp"]; tmp2, btmp2 = m64["tmp2"]; BBm, bBBm = m64["BBm"]
                gccol, bgccol = small["gccol"]; bcol, bbcol = small["bcol"]; nbcol, bnbcol = small["nbcol"]
                elast, belast = small["elast"]; egc, begc = small["egc"]
                v3 = lambda t_: t_[:].rearrange("p (n f) -> p n f", f=64)
                i64b = I64.unsqueeze(1).to_broadcast([64, 8, 64])
                tt(v3(tmp), btmp, gcb[0:64, :].rearrange("p (n f) -> p n f", f=64), [bgcb, bc64], i64b, bc64, ALU.mult)
                P.op("dve", lambda E, tmp=tmp, gccol=gccol: E.tensor_reduce(out=gccol[:], in_=tmp[:].rearrange("p (n f) -> p n f", f=64), axis=AX.X, op=ALU.add), reads=[btmp], writes=[bgccol])
                tt(v3(tmp), btmp, bb[0:64, :].rearrange("p (n f) -> p n f", f=64), [bbb, bc64], i64b, bc64, ALU.mult)
                P.op("dve", lambda E, tmp=tmp, bcol=bcol: E.tensor_reduce(out=bcol[:], in_=tmp[:].rearrange("p (n f) -> p n f", f=64), axis=AX.X, op=ALU.add), reads=[btmp], writes=[bbcol])
                P.op("dve", lambda E: E.tensor_scalar(out=nbcol[:], in0=bcol[:], scalar1=-1.0, scalar2=None, op0=ALU.mult), reads=[bbcol], writes=[bnbcol])
                tt(v3(arg1), barg1, gcb[0:64, :].rearrange("p (n f) -> p n f", f=64), [bgcb, bgccol], gccol[:].unsqueeze(2).to_broadcast([64, 8, 64]), bgccol, ALU.subtract)
                tt(v3(DT), bDT, v3(arg1), [barg1, bc64], NEGU.unsqueeze(1).to_broadcast([64, 8, 64]), bc64, ALU.add)
                P.op("act", lambda E: E.activation(out=DT[:], in_=DT[:], func=AF.Exp), reads=[bDT], writes=[bDT])
                P.op("dve", lambda E: E.scalar_tensor_tensor(out=Ds[:].rearrange("p (n f) -> p n f", f=64), in0=arg1[:].rearrange("p (n f) -> p n f", f=64), scalar=-1.0,
                                                             in1=NEGLS.unsqueeze(1).to_broadcast([64, 8, 64]), op0=ALU.mult, op1=ALU.add), reads=[barg1, bc64], writes=[bDs])
                P.op("act", lambda E: E.activation(out=Ds[:], in_=Ds[:], func=AF.Exp), reads=[bDs], writes=[bDs])
                tt(v3(BBm), bBBm, bb[0:64, :].rearrange("p (n f) -> p n f", f=64), [bbb, bc64], NSU.unsqueeze(1).to_broadcast([64, 8, 64]), bc64, ALU.mult)
                pk, bpk = GP[0]; pq, bpq = GP[1]
                fns = [(lambda E, n=n, pk=pk, kT=kT: E.matmul(pk[0:64, n * 64:(n + 1) * 64], lhsT=kT[:, n * 64:(n + 1) * 64], rhs=kT[:, n * 64:(n + 1) * 64],
                                                              start=True, stop=True)) for n in range(8)]
                P.mm_group(fns, reads=[bkT], writes=[bpk])
                fns = [(lambda E, n=n, pq=pq, kT=kT, qT=qT: E.matmul(pq[0:64, n * 64:(n + 1) * 64], lhsT=kT[:, n * 64:(n + 1) * 64], rhs=qT[:, n * 64:(n + 1) * 64],
                                                                     start=True, stop=True)) for n in range(8)]
                P.mm_group(fns, reads=[bkT, bqT], writes=[bpq])
                tt(attnT[:], battnT, pq[0:64, :], [bpq, bDT], DT[:], bDT, ALU.mult)
                Pc, bPc = m64["Pa"]; Pn, bPn = m64["Pb"]; Qc, bQc = m64["Qa"]; Qn, bQn = m64["Qb"]
                tt(tmp[:], btmp, pk[0:64, :], [bpk, bDT], DT[:], bDT, ALU.mult)
                tt(Qc[:], bQc, tmp[:], [btmp, bBBm], BBm[:], bBBm, ALU.mult)
                tt(tmp2[:], btmp2, pk[0:64, :], [bpk, bDs], Ds[:], bDs, ALU.mult)
                tt(v3(Pc), bPc, v3(tmp2), [btmp2, bnbcol], nbcol[:].unsqueeze(2).to_broadcast([64, 8, 64]), bnbcol, ALU.mult)
                tt(v3(Y), bY, v3(Qc), [bQc, bc64], i64b, bc64, ALU.add)
                P.op("act", lambda E, Ybf=Ybf, Y=Y: E.copy(out=Ybf[:], in_=Y[:]), reads=[bY], writes=[bYbf])
                for j in range(5):
                    pP, bpP = GP[2]; pQ, bpQ = GP[3]
                    fns = [(lambda E, n=n, pP=pP, Qc=Qc, Pc=Pc: E.matmul(pP[0:64, n * 64:(n + 1) * 64], lhsT=Qc[:, n * 64:(n + 1) * 64], rhs=Pc[:, n * 64:(n + 1) * 64],
                                                                         start=True, stop=True)) for n in range(8)]
                    P.mm_group(fns, reads=[bQc, bPc], writes=[bpP])
                    if j < 4:
                        fns = [(lambda E, n=n, pQ=pQ, Qc=Qc, Pc=Pc: E.matmul(pQ[0:64, n * 64:(n + 1) * 64], lhsT=Pc[:, n * 64:(n + 1) * 64], rhs=Qc[:, n * 64:(n + 1) * 64],
                                                                             start=True, stop=True)) for n in range(8)]
                        P.mm_group(fns, reads=[bQc, bPc], writes=[bpQ])
                    P.op("act", lambda E, Pn=Pn, pP=pP: E.copy(out=Pn[:], in_=pP[0:64, :]), reads=[bpP], writes=[bPn])
                    if j < 4:
                        P.op("dve", lambda E, Qn=Qn, pQ=pQ: E.tensor_copy(out=Qn[:], in_=pQ[0:64, :]), reads=[bpQ], writes=[bQn])
                    pY, bpY = GP[0]
                    fns = [(lambda E, n=n, pY=pY, Pn=Pn, Ybf=Ybf: E.matmul(pY[0:64, n * 64:(n + 1) * 64], lhsT=Pn[:, n * 64:(n + 1) * 64], rhs=Ybf[:, n * 64:(n + 1) * 64],
                                                                         start=True, stop=True)) for n in range(8)]
                    P.mm_group(fns, reads=[bPn, bYbf], writes=[bpY])
                    tt(Y[:], bY, Y[:], [bY, bpY], pY[0:64, :], bpY, ALU.add)
                    P.op("act", lambda E, Ybf=Ybf, Y=Y: E.copy(out=Ybf[:], in_=Y[:]), reads=[bY], writes=[bYbf])
                    Pc, bPc, Pn, bPn = Pn, bPn, Pc, bPc
                    Qc, bQc, Qn, bQn = Qn, bQn, Qc, bQc
                for hf in range(2):
                    pth, bpth = GA[hf]
                    fns = [(lambda E, n=n, vT=vT, pth=pth, hf=hf: E.transpose(out=pth[0:64, n * 128:(n + 1) * 128], in_=vT[:, (4 * hf + n) * 64:(4 * hf + n + 1) * 64],
                                                                              identity=idf[:])) for n in range(4)]
                    P.mm_group(fns, reads=[bvT, bidf], writes=[bpth])
                    tt(bv[:, 4 * hf:4 * hf + 4, :], bbv, pth[0:64, :].rearrange("p (n d) -> p n d", d=128), [bpth, bbcol],
                       bcol[:, 4 * hf:4 * hf + 4].unsqueeze(2).to_broadcast([64, 4, 128]), bbcol, ALU.mult)
                tt(elast[:], belast, gcb[0:64, :].rearrange("p (n f) -> p n f", f=64)[:, :, 63], [bgcb, bgccol], gccol[:], bgccol, ALU.subtract)
                P.op("act", lambda E: E.activation(out=elast[:], in_=elast[:], func=AF.Exp), reads=[belast], writes=[belast])
                for hf in range(2):
                    pth, bpth = GA[hf]
                    fns = [(lambda E, n=n, kT=kT, pth=pth, hf=hf: E.transpose(out=pth[0:64, n * 128:(n + 1) * 128], in_=kT[:, (4 * hf + n) * 64:(4 * hf + n + 1) * 64],
                                                                              identity=idf[:])) for n in range(4)]
                    P.mm_group(fns, reads=[bkT, bidf], writes=[bpth])
                    tt(kdec[:, 4 * hf:4 * hf + 4, :], bkdec, pth[0:64, :].rearrange("p (n d) -> p n d", d=128), [bpth, belast],
                       elast[:, 4 * hf:4 * hf + 4].unsqueeze(2).to_broadcast([64, 4, 128]), belast, ALU.mult)
                P.op("act", lambda E, gcb=gcb, EG=EG: E.activation(out=EG[:], in_=gcb[:], func=AF.Exp), reads=[bgcb], writes=[bEG])
                tt(qdec[:], bqdec, qT, [bqT, bEG], EG[:], bEG, ALU.mult)
                kTb, bkTb = H["kTb"]
                P.op("act", lambda E, kTb=kTb, kT=kT: E.copy(out=kTb[:], in_=kT), reads=[bkT], writes=[bkTb])
                P.op("act", lambda E: E.activation(out=egc[:], in_=gccol[:], func=AF.Exp), reads=[bgccol], writes=[begc])
                P.op("dve", lambda E, nbg=nbg: E.scalar_tensor_tensor(out=nbg[:], in0=egc[:], scalar=-1.0, in1=bcol[:], op0=ALU.mult, op1=ALU.mult),
                     reads=[begc, bbcol], writes=[bnbg])
            banks = [(GP[0], GP[1]), (GP[2], GP[3])]
            for n in range(8):
                cs = slice(n * 64, (n + 1) * 64)
                for h in range(2):
                    H = heads[h]; S, bS = Sst[h]
                    kT, bkT = H["kTb"]; Sbf, bSbf = H["Sbf"]
                    attnT, battnT = H["attnT"]; Y, bY = H["Ybf"]; EG, bEG = H["EG"]; qdec, bqdec = H["qdec"]
                    bv, bbv = H["bv"]; kdec, bkdec = H["kdec"]; nbg, bnbg = H["nbg"]
                    vnew, bvnew = H["vnew"]; rhs2, brhs2 = H["rhs2"]; osb, bosb = H["osb"]
                    (KSO, bKSO), (Sb, bSb) = banks[h]
                    Vb, bVb = KSO, bKSO
                    P.op("pe", lambda E, cs=cs, kT=kT, Sbf=Sbf, KSO=KSO: E.matmul(KSO[0:64, 0:128], lhsT=kT[:, cs], rhs=Sbf[:], start=True, stop=True),
                         reads=[bkT, bSbf], writes=[bKSO])
                    P.op("dve", lambda E, n=n, KSO=KSO, rhs2=rhs2, nbg=nbg, bv=bv: E.scalar_tensor_tensor(
                        out=rhs2[:], in0=KSO[0:64, 0:128], scalar=nbg[:, n:n + 1], in1=bv[:, n, :], op0=ALU.mult, op1=ALU.add),
                        reads=[bKSO, bnbg, bbv], writes=[brhs2])
                    P.op("pe", lambda E, cs=cs, Y=Y, Vb=Vb, rhs2=rhs2: E.matmul(Vb[0:64, 128:256], lhsT=Y[:, cs], rhs=rhs2[:], start=True, stop=True),
                         reads=[bY, brhs2], writes=[bVb])
                    P.op("act", lambda E, vnew=vnew, Vb=Vb: E.copy(out=vnew[:], in_=Vb[0:64, 128:256]), reads=[bVb], writes=[bvnew])
                    fns = [lambda E, cs=cs, Sbf=Sbf, KSO=KSO, qdec=qdec: E.matmul(KSO[64:128, 0:128], lhsT=qdec[:, cs], rhs=Sbf[:], start=True, stop=False),
                           lambda E, cs=cs, KSO=KSO, attnT=attnT, vnew=vnew: E.matmul(KSO[64:128, 0:128], lhsT=attnT[:, cs], rhs=vnew[:], start=False, stop=True)]
                    P.mm_group(fns, reads=[bqdec, bSbf, battnT, bvnew], writes=[bKSO])
                    P.op("pe", lambda E, n=n, Sb=Sb, kdec=kdec, vnew=vnew: E.matmul(Sb[:, 0:128], lhsT=kdec[:, n, :], rhs=vnew[:], start=True, stop=True),
                         reads=[bkdec, bvnew], writes=[bSb])
                    P.op("dve", lambda E, n=n, S=S, EG=EG, Sb=Sb: E.scalar_tensor_tensor(out=S[:], in0=S[:], scalar=EG[:, n * 64 + 63:n * 64 + 64], in1=Sb[:, 0:128],
                                                                                         op0=ALU.mult, op1=ALU.add), reads=[bS, bEG, bSb], writes=[bS])
                    P.op("act", lambda E, S=S, Sbf=Sbf: E.copy(out=Sbf[:], in_=S[:]), reads=[bS], writes=[bSbf])
                    P.op("act", lambda E, n=n, osb=osb, KSO=KSO: E.copy(out=osb[64:128, n, :], in_=KSO[64:128, 0:128]), reads=[bKSO], writes=[bosb])
                    advance(1)
                advance(1)
            advance(100)
            for h in range(2):
                osb, bosb = heads[h]["osb"]
                P.dma("sp", o_d[s_ * 512:(s_ + 1) * 512, h * 128:(h + 1) * 128].rearrange("(n c) d -> c n d", c=64), osb[64:128, :, :], reads=[bosb],
                      writes=[fz["obuf_of"](s_) if fz else bo])
            if fz:
                fz["after_chunk"](s_)
        if fz:
            barrier(P)
        else:
            P.finish([bo])
    return nc


def run_L1a(inp):
    nc = _get("L1a", build_L1a)
    c64, cmask, sel = _gdn_consts()
    w_in = inp["w_in_even"][0]
    conv = inp["conv_qkv"][0]
    ones = np.ones((128, 128), np.float32)
    maps = []
    for c in range(8):
        b, r = divmod(c, 4)
        cols = np.concatenate([np.arange(256 * r, 256 * r + 256), 1024 + np.arange(256 * r, 256 * r + 256), 2048 + np.arange(256 * r, 256 * r + 256)])
        maps.append({"x": np.ascontiguousarray(inp["x"][b]), "npre": np.ascontiguousarray(inp["norm_pre"][0]),
                     "w": np.ascontiguousarray(w_in[:, cols]), "wb": np.ascontiguousarray(w_in[:, 4096 + 2 * r:4096 + 2 * r + 2]),
                     "wa": np.ascontiguousarray(w_in[:, 4104 + 2 * r:4104 + 2 * r + 2]), "conv": np.ascontiguousarray(conv[:, cols]),
                     "alog": np.ascontiguousarray(inp["a_log"][0, 2 * r:2 * r + 2]), "dtb": np.ascontiguousarray(inp["dt_bias"][0, 2 * r:2 * r + 2]),
                     "ident": _IDENT, "c64": c64, "cmask": cmask, "sel": sel, "ones": ones})
    res = run_bass_kernel_spmd(nc, maps, core_ids=list(range(8)))
    S_ = inp["x"].shape[1]
    o = np.empty((2, S_, 1024), np.float32)
    for c in range(8):
        b, r = divmod(c, 4)
        o[b, :, 256 * r:256 * (r + 1)] = res.results[c]["o"]
    return o


def kernel_unfused(**inputs):
    inp = {k: np.asarray(v) for k, v in inputs.items()}
    o = run_L1a(inp)
    ys = run_L1b(inp)
    x1 = run_L2(inp, o, ys)
    out = run_L3(inp, x1)
    return out.astype(np.float32)


def build_fused():
    nc = bass.Bass("TRN2", target_bir_lowering=False)
    x_full = nc.dram_tensor("x", [8192, 1024], F32, kind="ExternalInput").ap()
    ident_d = nc.dram_tensor("ident", [128, 128], F32, kind="ExternalInput").ap()
    npre0_d = nc.dram_tensor("npre0", [1024], F32, kind="ExternalInput").ap()
    gidx_d = nc.dram_tensor("gidx", [128, 2, 17, 4], I32, kind="ExternalInput").ap()
    out_d = nc.dram_tensor("out", [2048, 1024], F32, kind="ExternalOutput").ap()
    ag_in = [nc.dram_tensor("ag_in%d" % i, [8192, 256], F32) for i in range(2)]
    ag_out = [nc.dram_tensor("ag_out%d" % i, [4 * 8192, 256], F32) for i in range(2)]
    x1s = nc.dram_tensor("x1s", [2176, 1024], F32)
    GROUPS = [[0, 1, 2, 3], [4, 5, 6, 7]]
    with ExitStack() as st:
        C = Ctx(nc, st); P = C.P
        csem = st.enter_context(nc.semaphore("csem"))
        bag_out = Buf("ag_out"); bx1s = Buf("x1s", multi=True); bout = Buf("out", multi=True)
        bo_ch = [Buf("o_ch%d" % k, multi=True) for k in range(16)]
        by_jt = [Buf("y_jt%d" % k, multi=True) for k in range(4)]
        ncc = [0]

        def emit_cc(which, k, inbuf, rows=512):
            P._deps("pool", [inbuf], [])
            P.streams["pool"].append(lambda E, which=which, k=k, rows=rows: E.collective_compute(
                "AllGather", ALU.bypass, replica_groups=GROUPS,
                ins=[ag_in[which].ap()[k * rows:(k + 1) * rows, :].opt()],
                outs=[ag_out[which].ap()[k * 4 * rows:(k + 1) * 4 * rows, :].opt()]).then_inc(csem))
            ncc[0] += 1

        share1 = {"x": x_full, "ident": ident_d, "npre": npre0_d}

        def after_jt(jt):
            for k in range(2 * jt, 2 * jt + 2):
                emit_cc(1, k, by_jt[jt], rows=1024)

        with ExitStack() as stU:
            CU = Ctx(nc, stU, P, "u_")
            uext = CU.sb("uTp", [128, 2, 16, 512], BF16)
            build_L1a(8192, fz={"nc": nc, "P": P, "pfx": "a_", "share": share1, "out": ag_in[0].ap(), "uTp": uext,
                                "obuf_of": lambda s_: bo_ch[s_], "after_chunk": lambda s_: emit_cc(0, s_, bo_ch[s_])})
            build_L1b(8192, fz={"nc": nc, "P": P, "pfx": "b_", "share": share1, "out": ag_in[1].ap(), "uTp": uext,
                                "obuf_of": lambda jt: by_jt[jt], "after_chunk": after_jt})
        gidx, bgidx = C.sb("gidx", [128, 2, 17, 4], I32)
        P.dma("sp", gidx[:], gidx_d, writes=[bgidx])
        waited = [False]

        def gather(P_, ld, bld, tile, part):
            if not waited[0]:
                P.streams["pool"].append(lambda E: E.wait_ge(csem, ncc[0]))
                P.op("pool", lambda E: E.nop(), reads=[], writes=[bag_out])
                waited[0] = True
            for i in range(4):
                P_.dma_ind("pool", ld[:, i * 256:(i + 1) * 256], ag_out[part].ap(), gidx[:, part, tile, i:i + 1], reads=[bag_out, bgidx], writes=[bld])

        share2 = {"ident": ident_d, "npre": npre0_d, "o": None, "ys": None}
        build_L2(2176, fz={"nc": nc, "P": P, "pfx": "c_", "share": share2, "out": x1s.ap(), "obuf": bx1s, "gather": gather})
        share3 = {"ident": ident_d, "x": x1s.ap()}
        build_L3(2048, fz={"nc": nc, "P": P, "pfx": "d_", "share": share3, "out": out_d, "obuf": bout, "xbuf": bx1s})
        P.finish([bout])
    return nc


def _gidx(r):
    g = np.zeros((128, 2, 17, 4), np.int32)
    p = np.arange(128)[:, None, None]
    tile = np.arange(17)[None, :, None]
    src = np.arange(4)[None, None, :]
    tok = np.clip(2048 * r - 128 + tile * 128 + p, 0, 8191)
    for part, R in ((0, 512), (1, 1024)):
        g[:, part] = ((tok // R) * 4 + src) * R + tok % R
    return g


def kernel(**inputs):
    inp = {k: np.ascontiguousarray(np.asarray(v)) for k, v in inputs.items()}
    nc = _get("fused", build_fused)
    c64, cmask, sel = _gdn_consts()
    mk, idm = _s5_consts()
    ones = np.ones((128, 128), np.float32)
    w_in = inp["w_in_even"][0]
    conv = inp["conv_qkv"][0]
    wz = np.ascontiguousarray(np.concatenate([w_in[:, 3072:4096], w_in[:, 5136:6160]], axis=1))
    maps = []
    for c in range(8):
        b, r = divmod(c, 4)
        cols = np.concatenate([np.arange(256 * r, 256 * r + 256), 1024 + np.arange(256 * r, 256 * r + 256), 2048 + np.arange(256 * r, 256 * r + 256)])
        gs = slice(16 * r, 16 * r + 16)
        xq = np.zeros((2176, 1024), np.float32)
        xq[128:] = inp["x"][b, 2048 * r:2048 * (r + 1)]
        if r > 0:
            xq[:128] = inp["x"][b, 2048 * r - 128:2048 * r]
        m = {"x": inp["x"][b], "ident": _IDENT, "npre0": inp["norm_pre"][0], "gidx": _gidx(r),
             "a_w": np.ascontiguousarray(w_in[:, cols]), "a_wb": np.ascontiguousarray(w_in[:, 4096 + 2 * r:4096 + 2 * r + 2]),
             "a_wa": np.ascontiguousarray(w_in[:, 4104 + 2 * r:4104 + 2 * r + 2]), "a_conv": np.ascontiguousarray(conv[:, cols]),
             "a_alog": np.ascontiguousarray(inp["a_log"][0, 2 * r:2 * r + 2]), "a_dtb": np.ascontiguousarray(inp["dt_bias"][0, 2 * r:2 * r + 2]),
             "a_c64": c64, "a_cmask": cmask, "a_sel": sel, "a_ones": ones,
             "a_wu": np.ascontiguousarray(w_in[:, 4112 + 256 * r:4112 + 256 * (r + 1)]),
             "b_wu": np.ascontiguousarray(w_in[:, 4112 + 256 * r:4112 + 256 * (r + 1)]),
             "b_lre": np.ascontiguousarray(inp["s5_lam_re"][0, gs]), "b_lim": np.ascontiguousarray(inp["s5_lam_im"][0, gs]),
             "b_bre": np.ascontiguousarray(inp["s5_b_re"][0, gs]), "b_bim": np.ascontiguousarray(inp["s5_b_im"][0, gs]),
             "b_cre": np.ascontiguousarray(inp["s5_c_re"][0, gs]), "b_cim": np.ascontiguousarray(inp["s5_c_im"][0, gs]),
             "b_ldt": np.ascontiguousarray(inp["s5_log_dt"][0, gs]), "b_dd": np.ascontiguousarray(inp["s5_d"][0, 256 * r:256 * (r + 1)]),
             "b_taus": TAUS, "b_mk": mk, "b_idm": idm,
             "c_x": xq, "c_wz": wz, "c_wglu": inp["w_glu"][0], "c_wout": inp["w_out_even"][0], "c_npost": inp["norm_post"][0],
             "c_gnw": inp["gdn_norm_w"][0],
             "d_win": inp["w_in_odd"][0], "d_wout": inp["w_out_odd"][0], "d_conv": inp["conv_short"][0],
             "d_npre": inp["norm_pre"][1], "d_npost": inp["norm_post"][1]}
        maps.append(m)
    res = run_bass_kernel_spmd(nc, maps, core_ids=list(range(8)))
    out = np.empty((2, 8192, 1024), np.float32)
    for c in range(8):
        b, r = divmod(c, 4)
        out[b, r * 2048:(r + 1) * 2048] = res.results[c]["out"]
    return out
```

```python
from contextlib import ExitStack
import numpy as np
import concourse.bass as bass
import concourse.mybir as mybir
from concourse.bass_utils import run_bass_kernel_spmd

F32 = mybir.dt.float32
BF16 = mybir.dt.bfloat16
AF = mybir.ActivationFunctionType
ALU = mybir.AluOpType
AX = mybir.AxisListType

NDS = 12


class Buf:
    __slots__ = ("name", "w", "r", "multi")

    def __init__(self, name, multi=False):
        self.name = name
        self.w = [] if multi else None
        self.r = []
        self.multi = multi


class Prog:
    ENG = ("pe", "act", "dve", "pool", "sp")

    def __init__(self, nc, stack):
        self.nc = nc
        self.stack = stack
        self.streams = {e: [] for e in self.ENG}
        self.cnt = {e: 0 for e in self.ENG}
        self.sem = {e: stack.enter_context(nc.semaphore("s_" + e)) for e in self.ENG}
        self.seen = {e: {} for e in self.ENG}
        self.dcnt = {e: 0 for e in self.ENG}
        self.dsem = {}
        for e in ("sp", "pool", "act"):
            self.dsem[e] = [stack.enter_context(nc.semaphore("d_%s%d" % (e, i))) for i in range(NDS)]
        self.same_engine_sync = True
        self.nwaits = 0

    def _wait(self, eng, tok):
        if tok is None:
            return
        kind = tok[0]
        if kind == "c":
            _, e2, n = tok
            if e2 == eng and (eng == "pe" or not self.same_engine_sync):
                return
            key = e2
            if self.seen[eng].get(key, 0) >= n:
                return
            self.seen[eng][key] = n
            sem = self.sem[e2]
            self.streams[eng].append(lambda E, sem=sem, n=n: E.wait_ge(sem, n))
            self.nwaits += 1
        else:
            _, q, slot, val = tok
            key = ("d", q, slot)
            if self.seen[eng].get(key, 0) >= val:
                return
            self.seen[eng][key] = val
            sem = self.dsem[q][slot]
            self.streams[eng].append(lambda E, sem=sem, val=val: E.wait_ge(sem, val))
            self.nwaits += 1

    def _deps(self, eng, reads, writes):
        for b in reads:
            if b.multi:
                for t in b.w:
                    self._wait(eng, t)
            else:
                self._wait(eng, b.w)
        for b in writes:
            if not b.multi:
                self._wait(eng, b.w)
            for t in b.r:
                self._wait(eng, t)

    def _commit(self, tok, reads, writes):
        for b in writes:
            if b.multi:
                b.w.append(tok)
            else:
                b.w = tok
            b.r = []
        for b in reads:
            if b not in writes:
                b.r.append(tok)

    def op(self, eng, fn, reads=(), writes=()):
        reads = list(reads)
        writes = list(writes)
        self._deps(eng, reads, writes)
        self.cnt[eng] += 1
        n = self.cnt[eng]
        sem = self.sem[eng]
        self.streams[eng].append(lambda E, fn=fn, sem=sem: fn(E).then_inc(sem, 1))
        tok = ("c", eng, n)
        self._commit(tok, reads, writes)
        return tok

    def mm_group(self, fns, reads=(), writes=()):
        eng = "pe"
        reads = list(reads)
        writes = list(writes)
        self._deps(eng, reads, writes)
        self.cnt[eng] += 1
        n = self.cnt[eng]
        sem = self.sem[eng]
        for fn in fns[:-1]:
            self.streams[eng].append(lambda E, fn=fn: fn(E))
        last = fns[-1]
        self.streams[eng].append(lambda E, fn=last, sem=sem: fn(E).then_inc(sem, 1))
        tok = ("c", eng, n)
        self._commit(tok, reads, writes)
        return tok

    def dma(self, q, out_ap, in_ap, reads=(), writes=()):
        reads = list(reads)
        writes = list(writes)
        self._deps(q, reads, writes)
        j = self.dcnt[q]
        self.dcnt[q] += 1
        slot = j % NDS
        val = 16 * (j // NDS + 1)
        if j >= NDS:
            self._wait(q, ("d", q, slot, val - 16))
        sem = self.dsem[q][slot]
        self.streams[q].append(
            lambda E, o=out_ap, i=in_ap, sem=sem: E.dma_start(out=o, in_=i).then_inc(sem, 16))
        tok = ("d", q, slot, val)
        self._commit(tok, reads, writes)
        return tok

    def dma_ind(self, q, out_ap, table_ap, idx_ap, reads=(), writes=()):
        reads = list(reads)
        writes = list(writes)
        self._deps(q, reads, writes)
        j = self.dcnt[q]
        self.dcnt[q] += 1
        slot = j % NDS
        val = 16 * (j // NDS + 1)
        if j >= NDS:
            self._wait(q, ("d", q, slot, val - 16))
        sem = self.dsem[q][slot]
        self.streams[q].append(
            lambda E, o=out_ap, t=table_ap, i=idx_ap, sem=sem: E.indirect_dma_start(
                out=o, out_offset=None, in_=t, in_offset=bass.IndirectOffsetOnAxis(ap=i, axis=0)).then_inc(sem, 16))
        tok = ("d", q, slot, val)
        self._commit(tok, reads, writes)
        return tok

    def finish(self, final_bufs):
        for b in final_bufs:
            for t in (b.w if b.multi else [b.w]):
                self._wait("sp", t)
        nc = self.nc
        streams = self.streams
        with nc.Block() as block:
            @block.tensor
            def _(E):
                for f in streams["pe"]:
                    f(E)

            @block.scalar
            def _(E):
                for f in streams["act"]:
                    f(E)

            @block.vector
            def _(E):
                for f in streams["dve"]:
                    f(E)

            @block.gpsimd
            def _(E):
                for f in streams["pool"]:
                    f(E)

            @block.sync
            def _(E):
                for f in streams["sp"]:
                    f(E)


class Ctx:
    def __init__(self, nc, st, P=None, pfx=""):
        self.nc = nc
        self.st = st
        self.pfx = pfx
        if P is None:
            st.enter_context(nc.allow_non_contiguous_dma(reason="small parameter loads / layout transforms"))
            P = Prog(nc, st)
        self.P = P

    def sb(self, name, shape, dt=F32):
        t = self.st.enter_context(self.nc.sbuf_tensor("sb_" + self.pfx + name, shape, dt))
        return t, Buf(name)

    def ps(self, name, shape, dt=F32):
        t = self.st.enter_context(self.nc.psum_tensor("ps_" + self.pfx + name, shape, dt))
        return t, Buf(name)


def bcast_row_load(C, name, dram_vec, n, q="sp"):
    t, b = C.sb(name, [128, n])
    C.P.dma(q, t[:], dram_vec.partition_broadcast(128), writes=[b])
    return t, b


def make_ident(C, dram_ident):
    idf, bidf = C.sb("identf", [128, 128])
    C.P.dma("sp", idf[:], dram_ident, writes=[bidf])
    idb, bidb = C.sb("identb", [128, 128], BF16)
    C.P.op("dve", lambda E: E.tensor_copy(out=idb[:], in_=idf[:]), reads=[bidf], writes=[bidb])
    return idf, bidf, idb, bidb


def rms_rstd(C, src, bsrc, ncols, junk, bjunk, ss, bss, eps=1e-6):
    P = C.P
    P.op("act", lambda E: E.activation(out=junk, in_=src, func=AF.Square, accum_out=ss[:, 0:1]),
         reads=[bsrc], writes=[bjunk, bss])
    P.op("act", lambda E: E.activation(out=ss[:, 0:1], in_=ss[:, 0:1], func=AF.Sqrt, bias=float(eps), scale=float(1.0 / ncols)),
         reads=[bss], writes=[bss])
    P.op("dve", lambda E: E.reciprocal(out=ss[:, 0:1], in_=ss[:, 0:1]), reads=[bss], writes=[bss])


def transpose8(C, src_bf, bsrc, idb, bidb, ptr, bptr, dst3, bdst, eng="act"):
    P = C.P
    fns = [(lambda E, kt=kt: E.transpose(out=ptr[:, kt * 128:(kt + 1) * 128], in_=src_bf[:, kt * 128:(kt + 1) * 128],
                                         identity=idb[:])) for kt in range(8)]
    P.mm_group(fns, reads=[bsrc, bidb], writes=[bptr])
    src3 = ptr[:].rearrange("p (k t) -> p k t", k=8)
    if eng == "act":
        P.op("act", lambda E: E.copy(out=dst3, in_=src3), reads=[bptr], writes=[bdst])
    else:
        P.op("dve", lambda E: E.tensor_copy(out=dst3, in_=src3), reads=[bptr], writes=[bdst])


def outproj_post(C, catT, bcat, nkt, wout, bwout, t, xres, bxres, npw, bnpw, pso, bpso, yo, byo, junk, bjunk, ss, bss,
                 out_dram_rows, bout):
    P = C.P
    for hh in range(2):
        fns = [(lambda E, kt=kt, hh=hh: E.matmul(pso[hh][:], lhsT=catT[:, kt, t * 128:(t + 1) * 128],
                                                 rhs=wout[:, kt, hh * 512:(hh + 1) * 512],
                                                 start=(kt == 0), stop=(kt == nkt - 1))) for kt in range(nkt)]
        P.mm_group(fns, reads=[bcat, bwout], writes=[bpso[hh]])
        P.op("act", lambda E, hh=hh: E.copy(out=yo[:, hh * 512:(hh + 1) * 512], in_=pso[hh][:]),
             reads=[bpso[hh]], writes=[byo])
    rms_rstd(C, yo[:], byo, 1024, junk[:], bjunk, ss, bss)
    P.op("dve", lambda E: E.scalar_tensor_tensor(out=yo[:], in0=yo[:], scalar=ss[:, 0:1], in1=npw[:],
                                                 op0=ALU.mult, op1=ALU.mult), reads=[byo, bss, bnpw], writes=[byo])
    P.op("dve", lambda E: E.tensor_tensor(out=yo[:], in0=yo[:], in1=xres, op=ALU.add), reads=[byo, bxres], writes=[byo])
    P.dma("sp", out_dram_rows, yo[:], reads=[byo], writes=[bout])


def load_w_bf16(C, name, dram_w, kt_n, ncols, chunk=2048, groups=None):
    w, _ = C.sb(name, [128, kt_n, ncols], BF16)
    src = dram_w.rearrange("(k p) c -> p k c", p=128)
    if groups is None:
        bw = Buf(name, multi=True)
        for kt in range(kt_n):
            for c0 in range(0, ncols, chunk):
                c1 = min(ncols, c0 + chunk)
                C.P.dma("pool", w[:, kt, c0:c1], src[:, kt, c0:c1], writes=[bw])
        return w, bw
    bws = []
    for gi, sls in enumerate(groups):
        bg = Buf("%s_g%d" % (name, gi), multi=True)
        for (c0, c1) in sls:
            for kt in range(kt_n):
                C.P.dma("pool", w[:, kt, c0:c1], src[:, kt, c0:c1], writes=[bg])
        bws.append(bg)
    return w, bws


def build_L2(ntok=2048, fz=None):
    nc = fz["nc"] if fz else bass.Bass("TRN2", target_bir_lowering=False)
    pfx = fz["pfx"] if fz else ""

    def D(name, shape):
        if fz and name in fz["share"]:
            return fz["share"][name]
        return nc.dram_tensor(pfx + name, shape, F32, kind="ExternalInput").ap()
    x_d = D("x", [ntok, 1024]); o_d = D("o", [ntok, 1024]); ys_d = D("ys", [ntok, 1024])
    wz_d = D("wz", [1024, 2048]); wglu_d = D("wglu", [1024, 1024]); wout_d = D("wout", [2048, 1024])
    npre_d = D("npre", [1024]); npost_d = D("npost", [1024]); gnw_d = D("gnw", [128]); ident_d = D("ident", [128, 128])
    out_d = fz["out"] if fz else nc.dram_tensor("out", [ntok, 1024], F32, kind="ExternalOutput").ap()
    NT = 512
    with ExitStack() as st:
        C = Ctx(nc, st, fz["P"], pfx) if fz else Ctx(nc, st); P = C.P
        idf, bidf, idb, bidb = make_ident(C, ident_d)
        npre, bnpre = bcast_row_load(C, "npre", npre_d, 1024)
        npost, bnpost = bcast_row_load(C, "npost", npost_d, 1024)
        gnw, bgnw = bcast_row_load(C, "gnw", gnw_d, 128)
        wz, bwz = load_w_bf16(C, "wz", wz_d, 8, 2048)
        wglu, bwglu = load_w_bf16(C, "wglu", wglu_d, 8, 1024)
        wout, bwout = load_w_bf16(C, "wout", wout_d, 16, 1024)
        xt4, bxt4 = C.sb("xt4", [128, 4, 1024]); bxt = [Buf("xt%d" % i) for i in range(4)]
        ldo = [C.sb("ldo%d" % i, [128, 1024]) for i in range(2)]
        ldy = [C.sb("ldy%d" % i, [128, 1024]) for i in range(2)]
        for (_t, _b) in ldo + ldy:
            _b.multi = True; _b.w = []
        sq, bsq = C.sb("sq", [128, 1024])
        hn, bhn = C.sb("hn", [128, 1024], BF16)
        ss, bss = C.sb("ss", [128, 1])
        ss8, bss8 = C.sb("ss8", [128, 8])
        hT, bhT = C.sb("hT", [128, 8, NT], BF16)
        oT, boT = C.sb("oT", [128, 8, NT], BF16)
        yT, byT = C.sb("yT", [128, 8, NT], BF16)
        gz, bgz = C.sb("gz", [128, 8, NT], BF16)
        sg, bsg = C.sb("sg", [128, NT], BF16)
        catT, bcat = C.sb("catT", [128, 16, NT], BF16)
        yo, byo = C.sb("yo", [128, 1024])
        ptr, bptr = C.ps("ptr", [128, 1024], BF16)
        pmm = []; bpmm = []
        for i in range(4):
            t_, b_ = C.ps("pmm%d" % i, [128, 512]); pmm.append(t_); bpmm.append(b_)
        pso = []; bpso = []
        for i in range(2):
            t_, b_ = C.ps("pso%d" % i, [128, 512]); pso.append(t_); bpso.append(b_)
        bout = fz["obuf"] if fz else Buf("out", multi=True)
        if fz:
            sts = [(0, 128)] + [(128 + i * NT, NT) for i in range((ntok - 128) // NT)]
        else:
            sts = [(i * NT, NT) for i in range(ntok // NT)]
        tile_r0 = [t0_ + t_ * 128 for (t0_, n_) in sts for t_ in range(n_ // 128)]

        def issue_loads(ti):
            r0_ = tile_r0[ti]
            lo, blo = ldo[ti % 2]; ly, bly = ldy[ti % 2]
            if fz:
                fz["gather"](P, lo, blo, r0_ // 128, 0)
                fz["gather"](P, ly, bly, r0_ // 128, 1)
            else:
                P.dma("sp", lo[:], o_d[r0_:r0_ + 128, :], writes=[blo])
                P.dma("sp", ly[:], ys_d[r0_:r0_ + 128, :], writes=[bly])

        issue_loads(0)
        for (t0, n) in sts:
            ntl = n // 128
            for t in range(ntl):
                r0 = t0 + t * 128
                ti = tile_r0.index(r0)
                if ti + 1 < len(tile_r0):
                    issue_loads(ti + 1)
                P.dma("sp", xt4[:, t, :], x_d[r0:r0 + 128, :], writes=[bxt[t]])
                rms_rstd(C, xt4[:, t, :], bxt[t], 1024, sq[:], bsq, ss, bss)
                P.op("dve", lambda E, t=t: E.scalar_tensor_tensor(out=hn[:], in0=xt4[:, t, :], scalar=ss[:, 0:1], in1=npre[:],
                                                                  op0=ALU.mult, op1=ALU.mult), reads=[bxt[t], bss, bnpre], writes=[bhn])
                transpose8(C, hn, bhn, idb, bidb, ptr, bptr, hT[:, :, t * 128:(t + 1) * 128], bhT, eng="act")
                ld, bld = ldo[ti % 2]
                P.op("act", lambda E, ld=ld: E.activation(out=sq[:], in_=ld[:], func=AF.Square), reads=[bld], writes=[bsq])
                P.op("dve", lambda E: E.tensor_reduce(out=ss8[:], in_=sq[:].rearrange("p (h d) -> p h d", h=8), axis=AX.X, op=ALU.add),
                     reads=[bsq], writes=[bss8])
                P.op("dve", lambda E: E.tensor_scalar(out=ss8[:], in0=ss8[:], scalar1=1.0 / 128, scalar2=1e-6, op0=ALU.mult, op1=ALU.add),
                     reads=[bss8], writes=[bss8])
                P.op("act", lambda E: E.activation(out=ss8[:], in_=ss8[:], func=AF.Sqrt), reads=[bss8], writes=[bss8])
                P.op("dve", lambda E: E.reciprocal(out=ss8[:], in_=ss8[:]), reads=[bss8], writes=[bss8])
                P.op("dve", lambda E, ld=ld: E.tensor_tensor(out=sq[:].rearrange("p (h d) -> p h d", h=8), in0=ld[:].rearrange("p (h d) -> p h d", h=8),
                                                      in1=ss8[:].unsqueeze(2).to_broadcast([128, 8, 128]), op=ALU.mult),
                     reads=[bld, bss8], writes=[bsq])
                P.op("dve", lambda E: E.tensor_tensor(out=hn[:].rearrange("p (h d) -> p h d", h=8), in0=sq[:].rearrange("p (h d) -> p h d", h=8),
                                                      in1=gnw[:].unsqueeze(1).to_broadcast([128, 8, 128]), op=ALU.mult),
                     reads=[bsq, bgnw], writes=[bhn])
                transpose8(C, hn, bhn, idb, bidb, ptr, bptr, oT[:, :, t * 128:(t + 1) * 128], boT, eng="act")
                ld, bld = ldy[ti % 2]
                P.op("act", lambda E, ld=ld: E.activation(out=hn[:], in_=ld[:], func=AF.Gelu_apprx_tanh), reads=[bld], writes=[bhn])
                transpose8(C, hn, bhn, idb, bidb, ptr, bptr, yT[:, :, t * 128:(t + 1) * 128], byT, eng="dve")
            for ct in range(16):
                pb = pmm[ct % 4]; bpb = bpmm[ct % 4]
                fns = [(lambda E, kt=kt, ct=ct, pb=pb, n=n: E.matmul(pb[:, 0:n], lhsT=wz[:, kt, ct * 128:(ct + 1) * 128], rhs=hT[:, kt, 0:n],
                                                                start=(kt == 0), stop=(kt == 7))) for kt in range(8)]
                P.mm_group(fns, reads=[bwz, bhT], writes=[bpb])
                if ct < 8:
                    P.op("act", lambda E, pb=pb, n=n: E.activation(out=sg[:, 0:n], in_=pb[:, 0:n], func=AF.Silu), reads=[bpb], writes=[bsg])
                    P.op("dve", lambda E, ct=ct, n=n: E.tensor_tensor(out=catT[:, ct, 0:n], in0=oT[:, ct, 0:n], in1=sg[:, 0:n], op=ALU.mult),
                         reads=[boT, bsg], writes=[bcat])
                else:
                    P.op("act", lambda E, pb=pb, ct=ct, n=n: E.activation(out=gz[:, ct - 8, 0:n], in_=pb[:, 0:n], func=AF.Silu), reads=[bpb], writes=[bgz])
            for ct in range(8):
                pb = pmm[ct % 4]; bpb = bpmm[ct % 4]
                fns = [(lambda E, kt=kt, ct=ct, pb=pb, n=n: E.matmul(pb[:, 0:n], lhsT=wglu[:, kt, ct * 128:(ct + 1) * 128], rhs=yT[:, kt, 0:n],
                                                                start=(kt == 0), stop=(kt == 7))) for kt in range(8)]
                P.mm_group(fns, reads=[bwglu, byT], writes=[bpb])
                P.op("act", lambda E, pb=pb, n=n: E.activation(out=sg[:, 0:n], in_=pb[:, 0:n], func=AF.Sigmoid), reads=[bpb], writes=[bsg])
                P.op("dve", lambda E, ct=ct, n=n: E.tensor_tensor(out=sg[:, 0:n], in0=sg[:, 0:n], in1=yT[:, ct, 0:n], op=ALU.mult), reads=[bsg, byT], writes=[bsg])
                P.op("dve", lambda E, ct=ct, n=n: E.tensor_tensor(out=catT[:, 8 + ct, 0:n], in0=sg[:, 0:n], in1=gz[:, ct, 0:n], op=ALU.mult),
                     reads=[bsg, bgz], writes=[bcat])
            for t in range(ntl):
                r0 = t0 + t * 128
                outproj_post(C, catT, bcat, 16, wout, bwout, t, xt4[:, t, :], bxt[t], npost, bnpost, pso, bpso, yo, byo, sq, bsq, ss, bss,
                             out_d[r0:r0 + 128, :], bout)
        if fz:
            barrier(P)
        else:
            P.finish([bout])
    return nc


def build_L3(ntok=2048, fz=None):
    nc = fz["nc"] if fz else bass.Bass("TRN2", target_bir_lowering=False)
    pfx = fz["pfx"] if fz else ""

    def D(name, shape):
        if fz and name in fz["share"]:
            return fz["share"][name]
        return nc.dram_tensor(pfx + name, shape, F32, kind="ExternalInput").ap()
    x_d = D("x", [ntok + 128, 1024])
    win_d = D("win", [1024, 8192]); wout_d = D("wout", [2048, 1024]); conv_d = D("conv", [3, 2048])
    npre_d = D("npre", [1024]); npost_d = D("npost", [1024]); ident_d = D("ident", [128, 128])
    out_d = fz["out"] if fz else nc.dram_tensor("out", [ntok, 1024], F32, kind="ExternalOutput").ap()
    NT = 256
    with ExitStack() as st:
        C = Ctx(nc, st, fz["P"], pfx) if fz else Ctx(nc, st); P = C.P
        idf, bidf, idb, bidb = make_ident(C, ident_d)
        npre, bnpre = bcast_row_load(C, "npre", npre_d, 1024)
        npost, bnpost = bcast_row_load(C, "npost", npost_d, 1024)
        cw, bcw = C.sb("cw", [128, 3, 16])
        P.dma("sp", cw[:], conv_d.rearrange("j (c p) -> p j c", p=128), writes=[bcw])
        win, bwin_g = load_w_bf16(C, "win", win_d, 8, 8192,
                                  groups=[[(part * 2048 + cg * 512, part * 2048 + cg * 512 + 512) for part in range(4)] for cg in range(4)])
        wout, bwout = load_w_bf16(C, "wout", wout_d, 16, 1024)
        xt, bxt = C.sb("xt", [128, 1024])
        sq, bsq = C.sb("sq", [128, 1024])
        hn, bhn = C.sb("hn", [128, 1024], BF16)
        ss, bss = C.sb("ss", [128, 1])
        hT, bhT = C.sb("hT", [128, 8, NT], BF16)
        y1T, by1T = C.sb("y1T", [128, 16, NT], BF16)
        pbuf, bpbuf = C.sb("pbuf", [128, NT + 2])
        phalo, bphalo = C.sb("phalo", [128, 16, 2])
        gcs, bgcs = C.sb("gcs", [128, NT])
        cv, bcv = C.sb("cv", [128, NT])
        sz, bsz = C.sb("sz", [128, NT])
        yo, byo = C.sb("yo", [128, 1024])
        P.op("dve", lambda E: E.memset(phalo[:], 0.0), writes=[bphalo])
        ptr, bptr = C.ps("ptr", [128, 1024], BF16)
        GB = [C.ps("g%d" % i, [128, 512]) for i in range(7)]
        pso = [GB[0][0], GB[1][0]]; bpso = [GB[0][1], GB[1][1]]
        bout = fz["obuf"] if fz else Buf("out", multi=True)
        sts = [(0, 128)] + [(128 + i * NT, NT) for i in range(ntok // NT)]
        for (t0, n) in sts:
            ntl = n // 128
            for t in range(ntl):
                r0 = t0 + t * 128
                P.dma("sp", xt[:], x_d[r0:r0 + 128, :], reads=([fz["xbuf"]] if fz else []), writes=[bxt])
                rms_rstd(C, xt[:], bxt, 1024, sq[:], bsq, ss, bss)
                P.op("dve", lambda E: E.scalar_tensor_tensor(out=hn[:], in0=xt[:], scalar=ss[:, 0:1], in1=npre[:],
                                                             op0=ALU.mult, op1=ALU.mult), reads=[bxt, bss, bnpre], writes=[bhn])
                transpose8(C, hn, bhn, idb, bidb, ptr, bptr, hT[:, :, t * 128:(t + 1) * 128], bhT, eng="act")
            for ct in range(16):
                sel_ = [GB[3 * (ct % 2) + 0], GB[3 * (ct % 2) + 1], GB[3 * (ct % 2) + 2], GB[6]]
                pmm = [x_[0] for x_ in sel_]; bpmm = [x_[1] for x_ in sel_]
                for part in range(4):
                    col0 = (part * 16 + ct) * 128
                    pb = pmm[part]
                    fns = [(lambda E, n=n, kt=kt, col0=col0, pb=pb: E.matmul(pb[:, 0:n], lhsT=win[:, kt, col0:col0 + 128], rhs=hT[:, kt, 0:n],
                                                                        start=(kt == 0), stop=(kt == 7))) for kt in range(8)]
                    P.mm_group(fns, reads=[bwin_g[ct // 4], bhT], writes=[bpmm[part]])
                P.op("act", lambda E, n=n, pmm=pmm: E.copy(out=gcs[:, 0:n], in_=pmm[1][:, 0:n]), reads=[bpmm[1]], writes=[bgcs])
                P.op("act", lambda E, ct=ct: E.copy(out=pbuf[:, 0:2], in_=phalo[:, ct, :]), reads=[bphalo], writes=[bpbuf])
                P.op("dve", lambda E, n=n, pmm=pmm: E.tensor_tensor(out=pbuf[:, 2:2 + n], in0=gcs[:, 0:n], in1=pmm[2][:, 0:n], op=ALU.mult),
                     reads=[bgcs, bpmm[2]], writes=[bpbuf])
                P.op("act", lambda E, n=n, ct=ct: E.copy(out=phalo[:, ct, :], in_=pbuf[:, n:n + 2]), reads=[bpbuf], writes=[bphalo])
                if t0 == 0:
                    continue
                P.op("dve", lambda E, n=n, ct=ct: E.tensor_scalar(out=cv[:, 0:n], in0=pbuf[:, 0:n], scalar1=cw[:, 0, ct:ct + 1], scalar2=None, op0=ALU.mult),
                     reads=[bpbuf, bcw], writes=[bcv])
                P.op("dve", lambda E, n=n, ct=ct: E.scalar_tensor_tensor(out=cv[:, 0:n], in0=pbuf[:, 1:1 + n], scalar=cw[:, 1, ct:ct + 1], in1=cv[:, 0:n],
                                                                    op0=ALU.mult, op1=ALU.add), reads=[bpbuf, bcw, bcv], writes=[bcv])
                P.op("dve", lambda E, n=n, ct=ct: E.scalar_tensor_tensor(out=cv[:, 0:n], in0=pbuf[:, 2:2 + n], scalar=cw[:, 2, ct:ct + 1], in1=cv[:, 0:n],
                                                                    op0=ALU.mult, op1=ALU.add), reads=[bpbuf, bcw, bcv], writes=[bcv])
                P.op("dve", lambda E, n=n, pmm=pmm: E.tensor_tensor(out=cv[:, 0:n], in0=cv[:, 0:n], in1=pmm[0][:, 0:n], op=ALU.mult), reads=[bcv, bpmm[0]], writes=[bcv])
                P.op("act", lambda E, n=n, pmm=pmm: E.activation(out=sz[:, 0:n], in_=pmm[3][:, 0:n], func=AF.Silu), reads=[bpmm[3]], writes=[bsz])
                P.op("dve", lambda E, n=n, ct=ct: E.tensor_tensor(out=y1T[:, ct, 0:n], in0=cv[:, 0:n], in1=sz[:, 0:n], op=ALU.mult),
                     reads=[bcv, bsz], writes=[by1T])
            if t0 == 0:
                continue
            for t in range(ntl):
                r0 = t0 + t * 128
                P.dma("sp", xt[:], x_d[r0:r0 + 128, :], reads=([fz["xbuf"]] if fz else []), writes=[bxt])
                outproj_post(C, y1T, by1T, 16, wout, bwout, t, xt[:], bxt, npost, bnpost, pso, bpso, yo, byo, sq, bsq, ss, bss,
                             out_d[r0 - 128:r0, :], bout)
        if fz:
            barrier(P)
        else:
            P.finish([bout])
    return nc


_IDENT = np.eye(128, dtype=np.float32)
_CACHE = {}


def _get(name, fn):
    if name not in _CACHE:
        _CACHE[name] = fn()
    return _CACHE[name]


def run_L2(inp, o_full, ys_full):
    nc = _get("L2", build_L2)
    w_in = inp["w_in_even"][0]
    wz = np.ascontiguousarray(np.concatenate([w_in[:, 3072:4096], w_in[:, 5136:6160]], axis=1))
    maps = []
    for c in range(8):
        b, r = divmod(c, 4)
        sl = slice(r * 2048, (r + 1) * 2048)
        maps.append({"x": np.ascontiguousarray(inp["x"][b, sl]), "o": np.ascontiguousarray(o_full[b, sl]),
                     "ys": np.ascontiguousarray(ys_full[b, sl]), "wz": wz, "wglu": np.ascontiguousarray(inp["w_glu"][0]),
                     "wout": np.ascontiguousarray(inp["w_out_even"][0]), "npre": np.ascontiguousarray(inp["norm_pre"][0]),
                     "npost": np.ascontiguousarray(inp["norm_post"][0]), "gnw": np.ascontiguousarray(inp["gdn_norm_w"][0]),
                     "ident": _IDENT})
    res = run_bass_kernel_spmd(nc, maps, core_ids=list(range(8)))
    x1 = np.empty((2, 8192, 1024), np.float32)
    for c in range(8):
        b, r = divmod(c, 4)
        x1[b, r * 2048:(r + 1) * 2048] = res.results[c]["out"]
    return x1


def run_L3(inp, x1):
    nc = _get("L3", build_L3)
    maps = []
    for c in range(8):
        b, r = divmod(c, 4)
        xh = np.zeros((2048 + 128, 1024), np.float32)
        xh[128:] = x1[b, r * 2048:(r + 1) * 2048]
        if r > 0:
            xh[:128] = x1[b, r * 2048 - 128:r * 2048]
        maps.append({"x": xh, "win": np.ascontiguousarray(inp["w_in_odd"][0]), "wout": np.ascontiguousarray(inp["w_out_odd"][0]),
                     "conv": np.ascontiguousarray(inp["conv_short"][0]), "npre": np.ascontiguousarray(inp["norm_pre"][1]),
                     "npost": np.ascontiguousarray(inp["norm_post"][1]), "ident": _IDENT})
    res = run_bass_kernel_spmd(nc, maps, core_ids=list(range(8)))
    out = np.empty((2, 8192, 1024), np.float32)
    for c in range(8):
        b, r = divmod(c, 4)
        out[b, r * 2048:(r + 1) * 2048] = res.results[c]["out"]
    return out


I32 = mybir.dt.int32
TAUS = np.array(list(range(17)) + [32, 64, 128, 256, 512, 1024, 2048, 4096] + list(range(15, -1, -1)), np.float32)
NTAU = len(TAUS)


def _s5_consts():
    mk = np.zeros((128, 2, 16, 16), np.float32)
    idm = np.zeros((128, 2, 16, 16), np.float32)
    for kt2 in range(2):
        for sp in range(8):
            s = kt2 * 8 + sp
            for h in range(16):
                mk[sp * 16 + h, kt2, s:, :] = 1.0
                idm[sp * 16 + h, kt2, s, h] = 1.0
    return mk.reshape(128, 2, 256), idm.reshape(128, 2, 256)


def barrier(P):
    for e in P.ENG:
        for e2 in P.ENG:
            if P.cnt[e2] > 0:
                P._wait(e, ("c", e2, P.cnt[e2]))
        for q in P.dsem:
            j1 = P.dcnt[q]
            for j in range(max(0, j1 - NDS), j1):
                P._wait(e, ("d", q, j % NDS, 16 * (j // NDS + 1)))


def build_L1b(S=8192, fz=None):
    nc = fz["nc"] if fz else bass.Bass("TRN2", target_bir_lowering=False)
    pfx = fz["pfx"] if fz else ""

    def D(name, shape):
        if fz and name in fz["share"]:
            return fz["share"][name]
        return nc.dram_tensor(pfx + name, shape, F32, kind="ExternalInput").ap()
    x_d = D("x", [S, 1024]); npre_d = D("npre", [1024]); wu_d = D("wu", [1024, 256])
    lre_d = D("lre", [16, 64]); lim_d = D("lim", [16, 64]); bre_d = D("bre", [16, 64, 16]); bim_d = D("bim", [16, 64, 16])
    cre_d = D("cre", [16, 16, 64]); cim_d = D("cim", [16, 16, 64]); ldt_d = D("ldt", [16]); dd_d = D("dd", [256])
    taus_d = D("taus", [NTAU]); mk_d = D("mk", [128, 2, 256]); idm_d = D("idm", [128, 2, 256]); ident_d = D("ident", [128, 128])
    ys_d = fz["out"] if fz else nc.dram_tensor("ys", [S, 256], F32, kind="ExternalOutput").ap()
    NCH = S // 16
    NST = S // 512
    with ExitStack() as st:
        C = Ctx(nc, st, fz["P"], pfx) if fz else Ctx(nc, st); P = C.P
        idf, bidf, idb, bidb = make_ident(C, ident_d)
        ptr, bptr = C.ps("ptr", [128, 1024], BF16)
        py, bpy = C.ps("py", [128, 1024])
        G = []; bG = []
        for i in range(4):
            t_, b_ = C.ps("g%d" % i, [128, 512]); G.append(t_); bG.append(b_)
        U, bU = C.sb("U", [128, 2, 16, NCH], BF16)
        with ExitStack() as st2:
            C2 = Ctx(nc, st2, P, C.pfx)
            ext = fz.get("uTp") if fz else None
            if ext:
                uTp, buTp = ext
            else:
                uTp, buTp = C2.sb("uTp", [128, 2, 16, NCH], BF16)
            with ExitStack() as st1:
                C1 = Ctx(nc, st1, P, C.pfx)
                npre, bnpre = bcast_row_load(C1, "npre", npre_d, 1024)
                wu, bwu = load_w_bf16(C1, "wu", wu_d, 8, 256)
                xt, bxt = C1.sb("xt", [128, 1024])
                sq, bsq = C1.sb("sq", [128, 1024])
                hn, bhn = C1.sb("hn", [128, 1024], BF16)
                ss, bss = C1.sb("ss", [128, 1])
                hT, bhT = C1.sb("hT", [128, 8, 512], BF16)
                for s_ in range(0 if ext else NST):
                    for t in range(4):
                        r0 = s_ * 512 + t * 128
                        P.dma("sp", xt[:], x_d[r0:r0 + 128, :], writes=[bxt])
                        rms_rstd(C1, xt[:], bxt, 1024, sq[:], bsq, ss, bss)
                        P.op("dve", lambda E: E.scalar_tensor_tensor(out=hn[:], in0=xt[:], scalar=ss[:, 0:1], in1=npre[:],
                                                                     op0=ALU.mult, op1=ALU.mult), reads=[bxt, bss, bnpre], writes=[bhn])
                        transpose8(C1, hn, bhn, idb, bidb, ptr, bptr, hT[:, :, t * 128:(t + 1) * 128], bhT, eng="act")
                    for blk in range(2):
                        pb = G[blk]
                        fns = [(lambda E, kt=kt, blk=blk, pb=pb: E.matmul(
                            pb[:].rearrange("p (s n) -> p s n", s=16), lhsT=wu[:, kt, blk * 128:(blk + 1) * 128],
                            rhs=hT[:, kt, :].rearrange("p (n s) -> p s n", s=16), start=(kt == 0), stop=(kt == 7))) for kt in range(8)]
                        P.mm_group(fns, reads=[bwu, bhT], writes=[bG[blk]])
                        P.op("act" if blk == 0 else "dve",
                             (lambda E, blk=blk, pb=pb, s_=s_: E.copy(out=uTp[:, blk, :, 32 * s_:32 * s_ + 32], in_=pb[:].rearrange("p (s n) -> p s n", s=16)))
                             if blk == 0 else
                             (lambda E, blk=blk, pb=pb, s_=s_: E.tensor_copy(out=uTp[:, blk, :, 32 * s_:32 * s_ + 32], in_=pb[:].rearrange("p (s n) -> p s n", s=16))),
                             reads=[bG[blk]], writes=[buTp])
                barrier(P)
            ud2 = nc.dram_tensor(pfx + "ud2", [16, 2, 8, 16, NCH], BF16)
            bud2 = Buf("ud2", multi=True)
            bU.multi = True; bU.w = []
            for g in range(16):
                P.dma("sp", ud2.ap()[g].rearrange("k sp h n -> h (k sp) n"),
                      uTp[(g % 8) * 16:(g % 8 + 1) * 16, g // 8, :, :], reads=[buTp], writes=[bud2])
            for g in range(16):
                P.dma("sp", U[:, :, g, :], ud2.ap()[g].rearrange("k sp h n -> (sp h) k n"), reads=[bud2], writes=[bU])
            barrier(P)
        lre, blre = C.sb("lre", [128, 8]); lim, blim = C.sb("lim", [128, 8]); ldt, bldt = C.sb("ldt", [128, 8])
        TAU, bTAU = bcast_row_load(C, "TAU", taus_d, NTAU)
        Er, bEr = C.sb("Er", [128, 8, NTAU]); Ei, bEi = C.sb("Ei", [128, 8, NTAU]); NEi, bNEi = C.sb("NEi", [128, 8, NTAU])
        Hr, bHr = C.sb("Hr", [128, 8, 17, 16]); nHi, bnHi = C.sb("nHi", [128, 8, 17, 16])
        WbT, bWbT = C.sb("WbT", [128, 2, 8, 2, 128], BF16)
        Toep, bToep = C.sb("Toep", [128, 2, 16, 256], BF16)
        with ExitStack() as st3:
            C3 = Ctx(nc, st3, P, C.pfx)
            Br, bBr = C3.sb("Br", [128, 8, 16]); Bi, bBi = C3.sb("Bi", [128, 8, 16])
            Cr, bCr = C3.sb("Cr", [128, 8, 16]); Ci, bCi = C3.sb("Ci", [128, 8, 16])
            dcol, bdcol = C3.sb("dcol", [128, 16])
            MK, bMK = C3.sb("MK", [128, 2, 256]); IDM, bIDM = C3.sb("IDM", [128, 2, 256])
            P.dma("sp", MK[:], mk_d, writes=[bMK]); P.dma("sp", IDM[:], idm_d, writes=[bIDM])
            for _b in (blre, blim, bldt, bBr, bBi, bCr, bCi, bdcol):
                _b.multi = True; _b.w = []
            for two in range(2):
                hs = slice(64 * two, 64 * two + 64)
                P.dma("sp", lre[hs, :], lre_d.rearrange("(gp two) p -> two p gp", two=2)[two], writes=[blre])
                P.dma("sp", lim[hs, :], lim_d.rearrange("(gp two) p -> two p gp", two=2)[two], writes=[blim])
                P.dma("sp", ldt[hs, :], ldt_d.rearrange("(gp two) -> two gp", two=2)[two].partition_broadcast(64), writes=[bldt])
                P.dma("sp", Br[hs], bre_d.rearrange("(gp two) p h -> two p gp h", two=2)[two], writes=[bBr])
                P.dma("sp", Bi[hs], bim_d.rearrange("(gp two) p h -> two p gp h", two=2)[two], writes=[bBi])
                for gp in range(8):
                    P.dma("sp", Cr[hs, gp, :], cre_d[2 * gp + two].rearrange("h p -> p h"), writes=[bCr])
                    P.dma("sp", Ci[hs, gp, :], cim_d[2 * gp + two].rearrange("h p -> p h"), writes=[bCi])
            for sp in range(8):
                P.dma("sp", dcol[sp * 16:(sp + 1) * 16, :], dd_d.rearrange("(g h) -> h g", h=16), writes=[bdcol])
            sm = {}
            for nm in ("dt", "lr", "lrdt", "th", "den", "nr", "fre", "fim", "t8a", "t8b"):
                sm[nm] = C3.sb("sm_" + nm, [128, 8])
            T41 = {}
            for nm in ("ARG", "MARG", "MAG", "MAGN", "SIN", "COS", "ErN", "EiN", "rt", "rk"):
                T41[nm] = C3.sb("t41_" + nm, [128, 8, NTAU])
            rki, brki = C3.sb("rki", [128, 8, NTAU], I32)

            def tt(eng, out, bo, a, ba, b, bb_, op):
                P.op(eng, lambda E: E.tensor_tensor(out=out, in0=a, in1=b, op=op), reads=[ba, bb_], writes=[bo])

            dt, bdt = sm["dt"]; lr, blr = sm["lr"]; lrdt, blrdt = sm["lrdt"]; th, bth = sm["th"]
            P.op("act", lambda E: E.activation(out=dt[:], in_=ldt[:], func=AF.Exp), reads=[bldt], writes=[bdt])
            P.op("dve", lambda E: E.tensor_scalar(out=lr[:], in0=lre[:], scalar1=-1e-4, scalar2=None, op0=ALU.min), reads=[blre], writes=[blr])
            tt("dve", lrdt[:], blrdt, lr[:], blr, dt[:], bdt, ALU.mult)
            tt("dve", th[:], bth, lim[:], blim, dt[:], bdt, ALU.mult)
            ARG, bARG = T41["ARG"]; MARG, bMARG = T41["MARG"]; MAG, bMAG = T41["MAG"]; MAGN, bMAGN = T41["MAGN"]
            SIN, bSIN = T41["SIN"]; COS, bCOS = T41["COS"]; ErN, bErN = T41["ErN"]; EiN, bEiN = T41["EiN"]
            rt, brt = T41["rt"]; rk, brk = T41["rk"]
            tb = TAU[:].unsqueeze(1).to_broadcast([128, 8, NTAU])
            tt("dve", ARG[:], bARG, th[:].unsqueeze(2).to_broadcast([128, 8, NTAU]), bth, tb, bTAU, ALU.mult)
            tt("dve", MARG[:], bMARG, lrdt[:].unsqueeze(2).to_broadcast([128, 8, NTAU]), blrdt, tb, bTAU, ALU.mult)
            P.op("act", lambda E: E.activation(out=MAG[:], in_=MARG[:], func=AF.Exp), reads=[bMARG], writes=[bMAG])
            P.op("act", lambda E: E.activation(out=MAGN[:, :, 0:17], in_=MARG[:, :, 0:17], func=AF.Exp, scale=-1.0), reads=[bMARG], writes=[bMAGN])

            def sin_of(dst, bdst, shift):
                P.op("dve", lambda E: E.tensor_scalar(out=rt[:], in0=ARG[:], scalar1=float(shift), scalar2=None, op0=ALU.add), reads=[bARG], writes=[brt])
                P.op("dve", lambda E: E.tensor_scalar(out=rki[:], in0=rt[:], scalar1=float(1.0 / (2 * np.pi)), scalar2=None, op0=ALU.mult), reads=[brt], writes=[brki])
                P.op("dve", lambda E: E.tensor_copy(out=rk[:], in_=rki[:]), reads=[brki], writes=[brk])
                P.op("dve", lambda E: E.scalar_tensor_tensor(out=rt[:], in0=rk[:], scalar=float(-2 * np.pi), in1=rt[:], op0=ALU.mult, op1=ALU.add),
                     reads=[brk, brt], writes=[brt])
                P.op("dve", lambda E: E.tensor_scalar(out=rt[:], in0=rt[:], scalar1=-3.14159, scalar2=3.14159, op0=ALU.max, op1=ALU.min), reads=[brt], writes=[brt])
                P.op("act", lambda E: E.activation(out=dst[:], in_=rt[:], func=AF.Sin), reads=[brt], writes=[bdst])

            sin_of(SIN, bSIN, 0.0)
            sin_of(COS, bCOS, np.pi / 2)
            tt("dve", Er[:], bEr, MAG[:], bMAG, COS[:], bCOS, ALU.mult)
            tt("dve", Ei[:], bEi, MAG[:], bMAG, SIN[:], bSIN, ALU.mult)
            P.op("dve", lambda E: E.tensor_scalar(out=NEi[:], in0=Ei[:], scalar1=-1.0, scalar2=None, op0=ALU.mult), reads=[bEi], writes=[bNEi])
            tt("dve", ErN[:, :, 0:17], bErN, MAGN[:, :, 0:17], bMAGN, COS[:, :, 0:17], bCOS, ALU.mult)
            tt("dve", EiN[:, :, 0:17], bEiN, MAGN[:, :, 0:17], bMAGN, SIN[:, :, 0:17], bSIN, ALU.mult)
            P.op("dve", lambda E: E.tensor_scalar(out=EiN[:, :, 0:17], in0=EiN[:, :, 0:17], scalar1=-1.0, scalar2=None, op0=ALU.mult), reads=[bEiN], writes=[bEiN])
            den, bden = sm["den"]; nr, bnr = sm["nr"]; fre, bfre = sm["fre"]; fim, bfim = sm["fim"]; t8a, bt8a = sm["t8a"]; t8b, bt8b = sm["t8b"]
            tt("dve", den[:], bden, lr[:], blr, lr[:], blr, ALU.mult)
            tt("dve", t8a[:], bt8a, lim[:], blim, lim[:], blim, ALU.mult)
            tt("dve", den[:], bden, den[:], bden, t8a[:], bt8a, ALU.add)
            P.op("dve", lambda E: E.reciprocal(out=den[:], in_=den[:]), reads=[bden], writes=[bden])
            P.op("dve", lambda E: E.tensor_scalar(out=nr[:], in0=Er[:, :, 1], scalar1=-1.0, scalar2=None, op0=ALU.add), reads=[bEr], writes=[bnr])
            tt("dve", fre[:], bfre, nr[:], bnr, lr[:], blr, ALU.mult)
            tt("dve", t8a[:], bt8a, Ei[:, :, 1], bEi, lim[:], blim, ALU.mult)
            tt("dve", fre[:], bfre, fre[:], bfre, t8a[:], bt8a, ALU.add)
            tt("dve", fre[:], bfre, fre[:], bfre, den[:], bden, ALU.mult)
            tt("dve", fim[:], bfim, Ei[:, :, 1], bEi, lr[:], blr, ALU.mult)
            tt("dve", t8b[:], bt8b, nr[:], bnr, lim[:], blim, ALU.mult)
            tt("dve", fim[:], bfim, fim[:], bfim, t8b[:], bt8b, ALU.subtract)
            tt("dve", fim[:], bfim, fim[:], bfim, den[:], bden, ALU.mult)

            def cmul(outr, boutr, outi, bouti, ar, bar, ai, bai, br_, bbr_, bi_, bbi_, tmp, btmp):
                tt("dve", outr, boutr, ar, bar, br_, bbr_, ALU.mult)
                tt("dve", tmp, btmp, ai, bai, bi_, bbi_, ALU.mult)
                tt("dve", outr, boutr, outr, boutr, tmp, btmp, ALU.subtract)
                tt("dve", outi, bouti, ar, bar, bi_, bbi_, ALU.mult)
                tt("dve", tmp, btmp, ai, bai, br_, bbr_, ALU.mult)
                tt("dve", outi, bouti, outi, bouti, tmp, btmp, ALU.add)

            bbr, bbbr = C3.sb("bbr", [128, 8, 16]); bbi, bbbi = C3.sb("bbi", [128, 8, 16]); tmp16, btmp16 = C3.sb("tmp16", [128, 8, 16])
            fb = lambda t_: t_[:].unsqueeze(2).to_broadcast([128, 8, 16])
            cmul(bbr[:], bbbr, bbi[:], bbbi, fb(fre), bfre, fb(fim), bfim, Br[:], bBr, Bi[:], bBi, tmp16[:], btmp16)
            Gr, bGr = C3.sb("Gr", [128, 8, 16, 16]); Gi, bGi = C3.sb("Gi", [128, 8, 16, 16])
            WPr, bWPr = C3.sb("WPr", [128, 8, 16, 16]); WPi, bWPi = C3.sb("WPi", [128, 8, 16, 16])
            Hi, bHi = C3.sb("Hi", [128, 8, 17, 16]); tmpH, btmpH = C3.sb("tmpH", [128, 8, 17, 16])
            eb = lambda t_, j0, j1: t_[:, :, j0:j1].unsqueeze(3).to_broadcast([128, 8, j1 - j0, 16])
            vb = lambda t_, n_: t_[:].unsqueeze(2).to_broadcast([128, 8, n_, 16])
            cmul(Gr[:], bGr, Gi[:], bGi, eb(ErN, 0, 16), bErN, eb(EiN, 0, 16), bEiN, vb(bbr, 16), bbbr, vb(bbi, 16), bbbi, tmpH[:, :, 0:16, :], btmpH)
            cmul(WPr[:], bWPr, WPi[:], bWPi, eb(Er, 25, 41), bEr, eb(Ei, 25, 41), bEi, vb(bbr, 16), bbbr, vb(bbi, 16), bbbi, tmpH[:, :, 0:16, :], btmpH)
            cmul(Hr[:], bHr, Hi[:], bHi, eb(Er, 0, 17), bEr, eb(Ei, 0, 17), bEi, vb(Cr, 17), bCr, vb(Ci, 17), bCi, tmpH[:], btmpH)
            P.op("dve", lambda E: E.tensor_scalar(out=nHi[:], in0=Hi[:], scalar1=-1.0, scalar2=None, op0=ALU.mult), reads=[bHi], writes=[bnHi])
            for gp in range(8):
                for kt2 in range(2):
                    for c, (WP_, bWP_) in enumerate(((WPr, bWPr), (WPi, bWPi))):
                        P.op("pe", lambda E, gp=gp, kt2=kt2, WP_=WP_: E.transpose(
                            out=G[2][:, 0:128], in_=WP_[:, gp, kt2 * 8:(kt2 + 1) * 8, :].rearrange("p s h -> p (s h)"), identity=idf[:]),
                            reads=[bWP_, bidf], writes=[bG[2]])
                        P.op("act", lambda E, gp=gp, kt2=kt2, c=c: E.copy(out=WbT[:, kt2, gp, c, :], in_=G[2][:, 0:128]), reads=[bG[2]], writes=[bWbT])
            tmpT, btmpT = C3.sb("tmpT", [128, 256])
            for g in range(16):
                gp = g // 2; hs = slice(64 * (g % 2), 64 * (g % 2) + 64)
                for kt2 in range(2):
                    fns = [
                        lambda E, gp=gp, hs=hs, kt2=kt2: E.matmul(G[3][:, 0:256], lhsT=Gr[hs, gp, kt2 * 8:(kt2 + 1) * 8, :].rearrange("p s h -> p (s h)"),
                                                                  rhs=Hr[hs, gp, 0:16, :].rearrange("p t h -> p (t h)"), start=True, stop=False),
                        lambda E, gp=gp, hs=hs, kt2=kt2: E.matmul(G[3][:, 0:256], lhsT=Gi[hs, gp, kt2 * 8:(kt2 + 1) * 8, :].rearrange("p s h -> p (s h)"),
                                                                  rhs=nHi[hs, gp, 0:16, :].rearrange("p t h -> p (t h)"), start=False, stop=True)]
                    P.mm_group(fns, reads=[bGr, bGi, bHr, bnHi], writes=[bG[3]])
                    P.op("dve", lambda E, kt2=kt2: E.tensor_tensor(out=tmpT[:], in0=G[3][:, 0:256], in1=MK[:, kt2, :], op=ALU.mult),
                         reads=[bG[3], bMK], writes=[btmpT])
                    P.op("dve", lambda E, kt2=kt2, g=g: E.scalar_tensor_tensor(out=Toep[:, kt2, g, :], in0=IDM[:, kt2, :], scalar=dcol[:, g:g + 1], in1=tmpT[:],
                                                                               op0=ALU.mult, op1=ALU.add), reads=[bIDM, bdcol, btmpT], writes=[bToep])
            barrier(P)
        X = {}
        for bufn in ("A", "B"):
            for c in ("re", "im"):
                X[(bufn, c)] = (C.sb("X%s%s" % (bufn, c), [128, 8, NCH + 1])[0], [Buf("X%s%s%d" % (bufn, c, gp)) for gp in range(8)])
        Ysb, bYsb = C.sb("Ysb", [128, 16, 256])
        for key in X:
            t_, bl = X[key]
            P.op("dve", lambda E, t_=t_: E.memset(t_[:, :, 0:1], 0.0), writes=bl)
        for gp in range(8):
            for c, cn in enumerate(("re", "im")):
                px = G[c]
                fns = []
                for two in range(2):
                    g = 2 * gp + two
                    for kt2 in range(2):
                        fns.append(lambda E, two=two, g=g, kt2=kt2, gp=gp, c=c, px=px: E.matmul(
                            px[64 * two:64 * two + 64, :], lhsT=WbT[:, kt2, gp, c, 64 * two:64 * two + 64], rhs=U[:, kt2, g, :],
                            start=(kt2 == 0), stop=(kt2 == 1)))
                P.mm_group(fns, reads=[bWbT, bU], writes=[bG[c]])
                xt_, xb_ = X[("A", cn)]
                P.op("act", lambda E, xt_=xt_, gp=gp, px=px: E.copy(out=xt_[:, gp, 1:NCH + 1], in_=px[:]), reads=[bG[c]], writes=[xb_[gp]])
        for k in range(9):
            d = 1 << k
            j = 16 if k == 0 else 16 + k
            src, dst = ("A", "B") if k % 2 == 0 else ("B", "A")
            sre, bsre = X[(src, "re")]; sim, bsim = X[(src, "im")]
            dre, bdre = X[(dst, "re")]; dim_, bdim = X[(dst, "im")]
            P.op("dve", lambda E, dre=dre, sre=sre, d=d: E.tensor_copy(out=dre[:, :, 1:1 + d], in_=sre[:, :, 1:1 + d]), reads=bsre, writes=bdre)
            P.op("pool", lambda E, dim_=dim_, sim=sim, d=d: E.tensor_copy(out=dim_[:, :, 1:1 + d], in_=sim[:, :, 1:1 + d]), reads=bsim, writes=bdim)
            for gp in range(8):
                lo = slice(1, NCH + 1 - d); hi = slice(1 + d, NCH + 1)
                P.op("dve", lambda E, gp=gp, j=j, dre=dre, sre=sre, lo=lo, hi=hi: E.scalar_tensor_tensor(
                    out=dre[:, gp, hi], in0=sre[:, gp, lo], scalar=Er[:, gp, j:j + 1], in1=sre[:, gp, hi], op0=ALU.mult, op1=ALU.add),
                    reads=[bsre[gp], bEr], writes=[bdre[gp]])
                P.op("dve", lambda E, gp=gp, j=j, dre=dre, sim=sim, lo=lo, hi=hi: E.scalar_tensor_tensor(
                    out=dre[:, gp, hi], in0=sim[:, gp, lo], scalar=NEi[:, gp, j:j + 1], in1=dre[:, gp, hi], op0=ALU.mult, op1=ALU.add),
                    reads=[bsim[gp], bNEi, bdre[gp]], writes=[bdre[gp]])
                P.op("dve", lambda E, gp=gp, j=j, dim_=dim_, sim=sim, lo=lo, hi=hi: E.scalar_tensor_tensor(
                    out=dim_[:, gp, hi], in0=sim[:, gp, lo], scalar=Er[:, gp, j:j + 1], in1=sim[:, gp, hi], op0=ALU.mult, op1=ALU.add),
                    reads=[bsim[gp], bEr], writes=[bdim[gp]])
                P.op("dve", lambda E, gp=gp, j=j, dim_=dim_, sre=sre, lo=lo, hi=hi: E.scalar_tensor_tensor(
                    out=dim_[:, gp, hi], in0=sre[:, gp, lo], scalar=Ei[:, gp, j:j + 1], in1=dim_[:, gp, hi], op0=ALU.mult, op1=ALU.add),
                    reads=[bsre[gp], bEi, bdim[gp]], writes=[bdim[gp]])
        fre_, bfre_ = X[("B", "re")]; fim_, bfim_ = X[("B", "im")]
        bys = None if fz else Buf("ys", multi=True)
        ysv = ys_d.rearrange("(n t) c -> n t c", t=16)
        for jt in range(NCH // 128):
            for gq in range(4):
                fns = []
                for gi in range(4):
                    g = 4 * gq + gi; gp = g // 2; hs = slice(64 * (g % 2), 64 * (g % 2) + 64)
                    o_ = (gi * 256, (gi + 1) * 256)
                    for kt2 in range(2):
                        fns.append(lambda E, o_=o_, g=g, kt2=kt2, jt=jt: E.matmul(
                            py[:, o_[0]:o_[1]], lhsT=U[:, kt2, g, jt * 128:(jt + 1) * 128], rhs=Toep[:, kt2, g, :], start=(kt2 == 0), stop=False))
                    fns.append(lambda E, o_=o_, gp=gp, hs=hs, jt=jt: E.matmul(
                        py[:, o_[0]:o_[1]], lhsT=fre_[hs, gp, jt * 128:(jt + 1) * 128], rhs=Hr[hs, gp, 1:17, :].rearrange("p t h -> p (t h)"),
                        start=False, stop=False))
                    fns.append(lambda E, o_=o_, gp=gp, hs=hs, jt=jt: E.matmul(
                        py[:, o_[0]:o_[1]], lhsT=fim_[hs, gp, jt * 128:(jt + 1) * 128], rhs=nHi[hs, gp, 1:17, :].rearrange("p t h -> p (t h)"),
                        start=False, stop=True))
                P.mm_group(fns, reads=[bU, bToep, bHr, bnHi] + bfre_ + bfim_, writes=[bpy])
                P.op("act" if gq % 2 == 0 else "dve",
                     (lambda E, gq=gq: E.copy(out=Ysb[:].rearrange("p t (g h) -> p g t h", h=16)[:, 4 * gq:4 * gq + 4],
                                              in_=py[:].rearrange("p (g t h) -> p g t h", g=4, h=16)))
                     if gq % 2 == 0 else
                     (lambda E, gq=gq: E.tensor_copy(out=Ysb[:].rearrange("p t (g h) -> p g t h", h=16)[:, 4 * gq:4 * gq + 4],
                                                     in_=py[:].rearrange("p (g t h) -> p g t h", g=4, h=16))),
                     reads=[bpy], writes=[bYsb])
            P.dma("sp", ysv[jt * 128:(jt + 1) * 128, :, :], Ysb[:], reads=[bYsb], writes=[fz["obuf_of"](jt) if fz else bys])
            if fz:
                fz["after_chunk"](jt)
        if fz:
            barrier(P)
        else:
            P.finish([bys])
    return nc


def run_L1b(inp):
    nc = _get("L1b", build_L1b)
    mk, idm = _s5_consts()
    w_in = inp["w_in_even"][0]
    maps = []
    for c in range(8):
        b, r = divmod(c, 4)
        gs = slice(16 * r, 16 * r + 16)
        maps.append({"x": np.ascontiguousarray(inp["x"][b]), "npre": np.ascontiguousarray(inp["norm_pre"][0]),
                     "wu": np.ascontiguousarray(w_in[:, 4112 + 256 * r:4112 + 256 * (r + 1)]),
                     "lre": np.ascontiguousarray(inp["s5_lam_re"][0, gs]), "lim": np.ascontiguousarray(inp["s5_lam_im"][0, gs]),
                     "bre": np.ascontiguousarray(inp["s5_b_re"][0, gs]), "bim": np.ascontiguousarray(inp["s5_b_im"][0, gs]),
                     "cre": np.ascontiguousarray(inp["s5_c_re"][0, gs]), "cim": np.ascontiguousarray(inp["s5_c_im"][0, gs]),
                     "ldt": np.ascontiguousarray(inp["s5_log_dt"][0, gs]), "dd": np.ascontiguousarray(inp["s5_d"][0, 256 * r:256 * (r + 1)]),
                     "taus": TAUS, "mk": mk, "idm": idm, "ident": _IDENT})
    res = run_bass_kernel_spmd(nc, maps, core_ids=list(range(8)))
    ys = np.empty((2, 8192, 1024), np.float32)
    for c in range(8):
        b, r = divmod(c, 4)
        ys[b, :, 256 * r:256 * (r + 1)] = res.results[c]["ys"]
    return ys


def _gdn_consts():
    p = np.arange(64)[:, None]; f = np.arange(64)[None, :]
    negu = np.where(f >= p, 0.0, -30000.0)
    negls = np.where(f < p, 0.0, -30000.0)
    nsu = np.where(f > p, -1.0, 0.0)
    i64 = np.eye(64)
    c64 = np.stack([negu, negls, nsu, i64], axis=1).astype(np.float32)
    cmask = np.ones((2, 512), np.float32); cmask[:, 0::64] = 0.0
    sel = np.zeros((2, 2, 128), np.float32); sel[0, 0, :] = 1.0; sel[1, 1, :] = 1.0
    return c64, cmask, sel


def build_L1a(S=8192, fz=None):
    nc = fz["nc"] if fz else bass.Bass("TRN2", target_bir_lowering=False)
    pfx = fz["pfx"] if fz else ""

    def D(name, shape):
        if fz and name in fz["share"]:
            return fz["share"][name]
        return nc.dram_tensor(pfx + name, shape, F32, kind="ExternalInput").ap()
    x_d = D("x", [S, 1024]); npre_d = D("npre", [1024]); w_d = D("w", [1024, 768]); wb_d = D("wb", [1024, 2]); wa_d = D("wa", [1024, 2])
    conv_d = D("conv", [4, 768]); alog_d = D("alog", [2]); dtb_d = D("dtb", [2])
    ident_d = D("ident", [128, 128]); c64_d = D("c64", [64, 4, 64]); cmask_d = D("cmask", [2, 512]); sel_d = D("sel", [2, 2, 128])
    ones_d = D("ones", [128, 128])
    o_d = fz["out"] if fz else nc.dram_tensor("o", [S, 256], F32, kind="ExternalOutput").ap()
    NST = S // 512
    with ExitStack() as st:
        C = Ctx(nc, st, fz["P"], pfx) if fz else Ctx(nc, st); P = C.P
        idf, bidf, idb, bidb = make_ident(C, ident_d)
        npre, bnpre = bcast_row_load(C, "npre", npre_d, 1024)
        w, bw = load_w_bf16(C, "w", w_d, 8, 768)
        wb, bwb = load_w_bf16(C, "wb", wb_d, 8, 2)
        wa, bwa = load_w_bf16(C, "wa", wa_d, 8, 2)
        cw, bcw = C.sb("cw", [128, 4, 6])
        P.dma("sp", cw[:], conv_d.rearrange("j (c p) -> p j c", p=128), writes=[bcw])
        extu = fz.get("uTp") if fz else None
        if extu:
            wu_d = D("wu", [1024, 256])
            wu, bwu = load_w_bf16(C, "wu", wu_d, 8, 256)
            uTp, buTp = extu
        c64, bc64 = C.sb("c64", [64, 4, 64]); P.dma("sp", c64[:], c64_d, writes=[bc64])
        NEGU = c64[:, 0, :]; NEGLS = c64[:, 1, :]; NSU = c64[:, 2, :]; I64 = c64[:, 3, :]
        cmask, bcmask = C.sb("cmask", [2, 512]); P.dma("sp", cmask[:], cmask_d, writes=[bcmask])
        sel, bsel = C.sb("sel", [2, 2, 128]); P.dma("sp", sel[:], sel_d, writes=[bsel])
        ones, bones = C.sb("ones", [128, 128]); P.dma("sp", ones[:], ones_d, writes=[bones])
        onesb, bonesb = C.sb("onesb", [128, 128], BF16)
        P.op("dve", lambda E: E.tensor_copy(out=onesb[:], in_=ones[:]), reads=[bones], writes=[bonesb])
        sqb, bsqb = C.sb("sqb", [128, 512], BF16)
        alog, balog = C.sb("alog", [2, 1]); P.dma("sp", alog[:], alog_d.rearrange("(a b) -> a b", b=1), writes=[balog])
        dtb, bdtb = C.sb("dtb", [2, 1]); P.dma("sp", dtb[:], dtb_d.rearrange("(a b) -> a b", b=1), writes=[bdtb])
        negA, bnegA = C.sb("negA", [2, 1])
        P.op("act", lambda E: E.activation(out=negA[:], in_=alog[:], func=AF.Exp), reads=[balog], writes=[bnegA])
        P.op("dve", lambda E: E.tensor_scalar(out=negA[:], in0=negA[:], scalar1=-1.0, scalar2=None, op0=ALU.mult), reads=[bnegA], writes=[bnegA])
        xt, bxt = C.sb("xt", [128, 1024]); sq, bsq = C.sb("sq", [128, 1024], BF16); hn, bhn = C.sb("hn", [128, 1024], BF16)
        ss, bss = C.sb("ss", [128, 1]); hT, bhT = C.sb("hT", [128, 8, 512], BF16)
        raw, _ = C.sb("raw", [128, 6, 515]); braw = [Buf("raw%d" % i) for i in range(6)]
        cvq, bcvq = C.sb("cvq", [128, 512])
        act, _ = C.sb("act", [128, 4, 512]); bact = [Buf("act%d" % i) for i in range(4)]
        vbuf2 = []; qk2 = []; bqk2 = []
        for par_ in range(2):
            vt_, _ = C.sb("vbuf%d" % par_, [128, 2, 512]); vbuf2.append((vt_, [Buf("vb%d_%d" % (par_, i)) for i in range(2)]))
            qt_, _ = C.sb("qk%d" % par_, [128, 4, 512]); qk2.append(qt_); bqk2.append([Buf("qk%d_%d" % (par_, i)) for i in range(4)])
        rn, brn = C.sb("rn", [128, 512])
        brow, bbrow = C.sb("brow", [2, 512]); grow, bgrow = C.sb("grow", [2, 512]); gcrow, bgcrow = C.sb("gcrow", [2, 512])
        GCB2 = []; BB2 = []
        for par_ in range(2):
            GCB2.append([C.sb("GCB%d_%d" % (par_, h), [128, 512]) for h in range(2)])
            BB2.append([C.sb("BB%d_%d" % (par_, h), [128, 512]) for h in range(2)])
        m64h = []; smallh = []
        for h in range(2):
            d_ = {}
            for nm in ("arg1", "scr"):
                d_[nm] = C.sb("m%d_%s" % (h, nm), [64, 512])
            for nm in ("DT", "Ds", "tmp", "Pa", "Pb", "Qa", "Qb"):
                d_[nm] = C.sb("m%d_%s" % (h, nm), [64, 512], BF16)
            m64h.append(d_)
        heads = []
        for h in range(2):
            H = {}
            H["attnT"] = C.sb("attnT%d" % h, [64, 512], BF16); H["Y"] = C.sb("Y%d" % h, [64, 512]); H["Ybf"] = C.sb("Ybf%d" % h, [64, 512], BF16)
            H["EG"] = C.sb("EG%d" % h, [128, 512]); H["qdec"] = C.sb("qdec%d" % h, [128, 512], BF16)
            H["kTb"] = C.sb("kTb%d" % h, [128, 512], BF16); H["Sbf"] = C.sb("Sbf%d" % h, [128, 128], BF16)
            H["bv"] = C.sb("bv%d" % h, [64, 8, 128]); H["kdec"] = C.sb("kdec%d" % h, [64, 8, 128], BF16)
            H["nbg"] = C.sb("nbg%d" % h, [64, 8]); H["osb"] = C.sb("osb%d" % h, [128, 8, 128])
            H["vnew"] = C.sb("vnew%d" % h, [64, 128], BF16); H["rhs2"] = C.sb("rhs2%d" % h, [64, 128], BF16)
            heads.append(H)
        for h in range(2):
            d_ = {}
            for nm in ("gccol", "bcol", "nbcol", "elast", "egc"):
                d_[nm] = C.sb("s%d_%s" % (h, nm), [64, 8])
            smallh.append(d_)
        Sst = [C.sb("S%d" % h, [128, 128]) for h in range(2)]
        for h in range(2):
            P.op("dve", lambda E, h=h: E.memset(Sst[h][0][:], 0.0), writes=[Sst[h][1]])
            P.op("dve", lambda E, h=h: E.memset(heads[h]["Sbf"][0][:], 0.0), writes=[heads[h]["Sbf"][1]])
        P.op("dve", lambda E: E.memset(raw[:, :, 0:3], 0.0), writes=braw)
        ptr, bptr = C.ps("ptr", [128, 1024], BF16)
        G = [C.ps("gp%d" % i, [128, 512]) for i in range(7)]
        GP = G[0:4]
        GA = G[4:7]
        BKS = [(GP[0], GP[1], GP[2]), (GP[3], GA[0], GA[1])]
        ga_ctr = [0]

        def next_ga():
            ga_ctr[0] += 1
            return GA[ga_ctr[0] % 3]
        bo = None if fz else Buf("o", multi=True)
        if fz is not None and fz.get("debug"):
            print("L1a sbuf remaining", nc.sbuf_bytes_remaining)

        def tt(out, bo_, a, ba, b, bb_, op, eng="dve"):
            P.op(eng, lambda E: E.tensor_tensor(out=out, in0=a, in1=b, op=op), reads=ba if isinstance(ba, list) else [ba], writes=[bo_])

        def stageA(s_):
            par = s_ % 2
            qk = qk2[par]; bqk = bqk2[par]; GCB = GCB2[par]; BB = BB2[par]; vb, bvb = vbuf2[par]
            for t in range(4):
                r0 = s_ * 512 + t * 128
                P.dma("sp", xt[:], x_d[r0:r0 + 128, :], writes=[bxt])
                rms_rstd(C, xt[:], bxt, 1024, sq[:], bsq, ss, bss)
                P.op("dve", lambda E: E.scalar_tensor_tensor(out=hn[:], in0=xt[:], scalar=ss[:, 0:1], in1=npre[:],
                                                             op0=ALU.mult, op1=ALU.mult), reads=[bxt, bss, bnpre], writes=[bhn])
                transpose8(C, hn, bhn, idb, bidb, ptr, bptr, hT[:, :, t * 128:(t + 1) * 128], bhT, eng="act")
                yield
            for ct in range(6):
                pa, bpa = next_ga()
                fns = [(lambda E, kt=kt, ct=ct, pa=pa: E.matmul(pa[:], lhsT=w[:, kt, ct * 128:(ct + 1) * 128], rhs=hT[:, kt, :],
                                                                start=(kt == 0), stop=(kt == 7))) for kt in range(8)]
                P.mm_group(fns, reads=[bw, bhT], writes=[bpa])
                P.op("act", lambda E, ct=ct, pa=pa: E.copy(out=raw[:, ct, 3:515], in_=pa[:]), reads=[bpa], writes=[braw[ct]])
                P.op("dve", lambda E, ct=ct: E.tensor_scalar(out=cvq[:], in0=raw[:, ct, 0:512], scalar1=cw[:, 0, ct:ct + 1], scalar2=None, op0=ALU.mult),
                     reads=[braw[ct], bcw], writes=[bcvq])
                for j in range(1, 4):
                    P.op("dve", lambda E, ct=ct, j=j: E.scalar_tensor_tensor(out=cvq[:], in0=raw[:, ct, j:j + 512], scalar=cw[:, j, ct:ct + 1], in1=cvq[:],
                                                                             op0=ALU.mult, op1=ALU.add), reads=[braw[ct], bcw, bcvq], writes=[bcvq])
                P.op("act", lambda E, ct=ct: E.copy(out=raw[:, ct, 0:3], in_=raw[:, ct, 512:515]), reads=[braw[ct]], writes=[braw[ct]])
                if ct < 4:
                    P.op("act", lambda E, ct=ct: E.activation(out=act[:, ct, :], in_=cvq[:], func=AF.Silu), reads=[bcvq], writes=[bact[ct]])
                else:
                    P.op("act", lambda E, ct=ct, vb=vb: E.activation(out=vb[:, ct - 4, :], in_=cvq[:], func=AF.Silu), reads=[bcvq], writes=[bvb[ct - 4]])
                yield
            if extu:
                for blk in range(2):
                    pa, bpa = next_ga()
                    fns = [(lambda E, kt=kt, blk=blk, pa=pa: E.matmul(
                        pa[:].rearrange("p (s n) -> p s n", s=16), lhsT=wu[:, kt, blk * 128:(blk + 1) * 128],
                        rhs=hT[:, kt, :].rearrange("p (n s) -> p s n", s=16), start=(kt == 0), stop=(kt == 7))) for kt in range(8)]
                    P.mm_group(fns, reads=[bwu, bhT], writes=[bpa])
                    P.op("act", lambda E, blk=blk, pa=pa, s_=s_: E.copy(out=uTp[:, blk, :, 32 * s_:32 * s_ + 32], in_=pa[:].rearrange("p (s n) -> p s n", s=16)),
                         reads=[bpa], writes=[buTp])
                    yield
            for ct in range(4):
                pa, bpa = next_ga()
                P.op("act", lambda E, ct=ct: E.activation(out=sqb[:], in_=act[:, ct, :], func=AF.Square), reads=[bact[ct]], writes=[bsqb])
                P.op("pe", lambda E, pa=pa: E.matmul(pa[:], lhsT=onesb[:], rhs=sqb[:], start=True, stop=True), reads=[bonesb, bsqb], writes=[bpa])
                P.op("act", lambda E, pa=pa: E.activation(out=rn[:], in_=pa[:], func=AF.Ln, bias=1e-6, scale=1.0), reads=[bpa], writes=[brn])
                P.op("act", lambda E: E.activation(out=rn[:], in_=rn[:], func=AF.Exp, scale=-0.5), reads=[brn], writes=[brn])
                if ct < 2:
                    P.op("dve", lambda E, ct=ct, qk=qk: E.scalar_tensor_tensor(out=qk[:, ct, :], in0=act[:, ct, :], scalar=float(128 ** -0.5), in1=rn[:],
                                                                               op0=ALU.mult, op1=ALU.mult), reads=[bact[ct], brn], writes=[bqk[ct]])
                else:
                    P.op("dve", lambda E, ct=ct, qk=qk: E.tensor_tensor(out=qk[:, ct, :], in0=act[:, ct, :], in1=rn[:], op=ALU.mult),
                         reads=[bact[ct], brn], writes=[bqk[ct]])
                yield
            pa, bpa = next_ga()
            fns = [(lambda E, kt=kt, pa=pa: E.matmul(pa[0:2, :], lhsT=wb[:, kt, 0:2], rhs=hT[:, kt, :], start=(kt == 0), stop=(kt == 7))) for kt in range(8)]
            P.mm_group(fns, reads=[bwb, bhT], writes=[bpa])
            P.op("act", lambda E, pa=pa: E.activation(out=brow[:], in_=pa[0:2, :], func=AF.Sigmoid), reads=[bpa], writes=[bbrow])
            pa2, bpa2 = next_ga()
            fns = [(lambda E, kt=kt, pa2=pa2: E.matmul(pa2[0:2, :], lhsT=wa[:, kt, 0:2], rhs=hT[:, kt, :], start=(kt == 0), stop=(kt == 7))) for kt in range(8)]
            P.mm_group(fns, reads=[bwa, bhT], writes=[bpa2])
            P.op("act", lambda E, pa2=pa2: E.activation(out=grow[:], in_=pa2[0:2, :], func=AF.Exp, bias=dtb[:, 0:1], scale=1.0), reads=[bpa2, bdtb], writes=[bgrow])
            P.op("act", lambda E: E.activation(out=grow[:], in_=grow[:], func=AF.Ln, bias=1.0, scale=1.0), reads=[bgrow], writes=[bgrow])
            P.op("dve", lambda E: E.tensor_scalar(out=grow[:], in0=grow[:], scalar1=negA[:, 0:1], scalar2=None, op0=ALU.mult), reads=[bgrow, bnegA], writes=[bgrow])
            P.op("dve", lambda E: E.tensor_tensor_scan(out=gcrow[:], data0=cmask[:], data1=grow[:], initial=0.0, op0=ALU.mult, op1=ALU.add),
                 reads=[bcmask, bgrow], writes=[bgcrow])
            yield
            for h in range(2):
                pa, bpa = next_ga()
                P.op("pe", lambda E, h=h, pa=pa: E.matmul(pa[:], lhsT=sel[:, h, :], rhs=gcrow[:], start=True, stop=True), reads=[bsel, bgcrow], writes=[bpa])
                P.op("act", lambda E, h=h, pa=pa, GCB=GCB: E.copy(out=GCB[h][0][:], in_=pa[:]), reads=[bpa], writes=[GCB[h][1]])
                pa, bpa = next_ga()
                P.op("pe", lambda E, h=h, pa=pa: E.matmul(pa[:], lhsT=sel[:, h, :], rhs=brow[:], start=True, stop=True), reads=[bsel, bbrow], writes=[bpa])
                P.op("act", lambda E, h=h, pa=pa, BB=BB: E.copy(out=BB[h][0][:], in_=pa[:]), reads=[bpa], writes=[BB[h][1]])
                yield

        for _ in stageA(0):
            pass
        for s_ in range(NST):
            par = s_ % 2
            qk = qk2[par]; bqk = bqk2[par]; GCB = GCB2[par]; BB = BB2[par]; vb, bvb = vbuf2[par]
            nxt = stageA(s_ + 1) if s_ + 1 < NST else None

            def advance(k):
                if nxt is not None:
                    for _ in range(k):
                        next(nxt, None)
            def stageB(h, qk=qk, bqk=bqk, GCB=GCB, BB=BB, vb=vb, bvb=bvb):
                m64 = m64h[h]; small = smallh[h]; BK = BKS[h]
                qT = qk[:, h, :]; bqT = bqk[h]; kT = qk[:, 2 + h, :]; bkT = bqk[2 + h]; vT = vb[:, h, :]; bvT = bvb[h]
                gcb, bgcb = GCB[h]; bb, bbb = BB[h]
                H = heads[h]
                attnT, battnT = H["attnT"]; Y, bY = H["Y"]; EG, bEG = H["EG"]; qdec, bqdec = H["qdec"]
                Ybf, bYbf = H["Ybf"]
                bv, bbv = H["bv"]; kdec, bkdec = H["kdec"]; nbg, bnbg = H["nbg"]
                arg1, barg1 = m64["arg1"]; scr, bscr = m64["scr"]; DT, bDT = m64["DT"]; Ds, bDs = m64["Ds"]
                tmp, btmp = m64["tmp"]
                gccol, bgccol = small["gccol"]; bcol, bbcol = small["bcol"]; nbcol, bnbcol = small["nbcol"]
                elast, belast = small["elast"]; egc, begc = small["egc"]
                v3 = lambda t_: t_[:].rearrange("p (n f) -> p n f", f=64)
                i64b = I64.unsqueeze(1).to_broadcast([64, 8, 64])
                tt(v3(scr), bscr, gcb[0:64, :].rearrange("p (n f) -> p n f", f=64), [bgcb, bc64], i64b, bc64, ALU.mult)
                P.op("dve", lambda E, scr=scr, gccol=gccol: E.tensor_reduce(out=gccol[:], in_=scr[:].rearrange("p (n f) -> p n f", f=64), axis=AX.X, op=ALU.add), reads=[bscr], writes=[bgccol])
                tt(v3(scr), bscr, bb[0:64, :].rearrange("p (n f) -> p n f", f=64), [bbb, bc64], i64b, bc64, ALU.mult)
                P.op("dve", lambda E, scr=scr, bcol=bcol: E.tensor_reduce(out=bcol[:], in_=scr[:].rearrange("p (n f) -> p n f", f=64), axis=AX.X, op=ALU.add), reads=[bscr], writes=[bbcol])
                yield
                P.op("dve", lambda E: E.tensor_scalar(out=nbcol[:], in0=bcol[:], scalar1=-1.0, scalar2=None, op0=ALU.mult), reads=[bbcol], writes=[bnbcol])
                tt(v3(arg1), barg1, gcb[0:64, :].rearrange("p (n f) -> p n f", f=64), [bgcb, bgccol], gccol[:].unsqueeze(2).to_broadcast([64, 8, 64]), bgccol, ALU.subtract)
                tt(v3(scr), bscr, v3(arg1), [barg1, bc64], NEGU.unsqueeze(1).to_broadcast([64, 8, 64]), bc64, ALU.add)
                P.op("act", lambda E: E.activation(out=DT[:], in_=scr[:], func=AF.Exp), reads=[bscr], writes=[bDT])
                yield
                P.op("dve", lambda E: E.scalar_tensor_tensor(out=scr[:].rearrange("p (n f) -> p n f", f=64), in0=arg1[:].rearrange("p (n f) -> p n f", f=64), scalar=-1.0,
                                                             in1=NEGLS.unsqueeze(1).to_broadcast([64, 8, 64]), op0=ALU.mult, op1=ALU.add), reads=[barg1, bc64], writes=[bscr])
                P.op("act", lambda E: E.activation(out=Ds[:], in_=scr[:], func=AF.Exp), reads=[bscr], writes=[bDs])
                pk, bpk = BK[0]; pq, bpq = BK[1]
                fns = [(lambda E, n=n, pk=pk, kT=kT: E.matmul(pk[0:64, n * 64:(n + 1) * 64], lhsT=kT[:, n * 64:(n + 1) * 64], rhs=kT[:, n * 64:(n + 1) * 64],
                                                              start=True, stop=True)) for n in range(8)]
                P.mm_group(fns, reads=[bkT], writes=[bpk])
                fns = [(lambda E, n=n, pq=pq, kT=kT, qT=qT: E.matmul(pq[0:64, n * 64:(n + 1) * 64], lhsT=kT[:, n * 64:(n + 1) * 64], rhs=qT[:, n * 64:(n + 1) * 64],
                                                                     start=True, stop=True)) for n in range(8)]
                P.mm_group(fns, reads=[bkT, bqT], writes=[bpq])
                yield
                tt(attnT[:], battnT, pq[0:64, :], [bpq, bDT], DT[:], bDT, ALU.mult)
                Pc, bPc = m64["Pa"]; Pn, bPn = m64["Pb"]; Qc, bQc = m64["Qa"]; Qn, bQn = m64["Qb"]
                tt(tmp[:], btmp, pk[0:64, :], [bpk, bDT], DT[:], bDT, ALU.mult)
                tt(tmp[:], btmp, tmp[:], [btmp, bbb], bb[0:64, :], bbb, ALU.mult)
                tt(v3(Qc), bQc, v3(tmp), [btmp, bc64], NSU.unsqueeze(1).to_broadcast([64, 8, 64]), bc64, ALU.mult)
                yield
                tt(tmp[:], btmp, pk[0:64, :], [bpk, bDs], Ds[:], bDs, ALU.mult)
                tt(v3(Pc), bPc, v3(tmp), [btmp, bnbcol], nbcol[:].unsqueeze(2).to_broadcast([64, 8, 64]), bnbcol, ALU.mult)
                tt(v3(Y), bY, v3(Qc), [bQc, bc64], i64b, bc64, ALU.add)
                P.op("act", lambda E, Ybf=Ybf, Y=Y: E.copy(out=Ybf[:], in_=Y[:]), reads=[bY], writes=[bYbf])
                yield
                for j in range(5):
                    pP, bpP = BK[2]; pQ, bpQ = BK[1]
                    fns = [(lambda E, n=n, pP=pP, Qc=Qc, Pc=Pc: E.matmul(pP[0:64, n * 64:(n + 1) * 64], lhsT=Qc[:, n * 64:(n + 1) * 64], rhs=Pc[:, n * 64:(n + 1) * 64],
                                                                         start=True, stop=True)) for n in range(8)]
                    P.mm_group(fns, reads=[bQc, bPc], writes=[bpP])
                    if j < 4:
                        fns = [(lambda E, n=n, pQ=pQ, Qc=Qc, Pc=Pc: E.matmul(pQ[0:64, n * 64:(n + 1) * 64], lhsT=Pc[:, n * 64:(n + 1) * 64], rhs=Qc[:, n * 64:(n + 1) * 64],
                                                                             start=True, stop=True)) for n in range(8)]
                        P.mm_group(fns, reads=[bQc, bPc], writes=[bpQ])
                    yield
                    P.op("act", lambda E, Pn=Pn, pP=pP: E.copy(out=Pn[:], in_=pP[0:64, :]), reads=[bpP], writes=[bPn])
                    if j < 4:
                        P.op("dve", lambda E, Qn=Qn, pQ=pQ: E.tensor_copy(out=Qn[:], in_=pQ[0:64, :]), reads=[bpQ], writes=[bQn])
                    pY, bpY = BK[0]
                    fns = [(lambda E, n=n, pY=pY, Pn=Pn, Ybf=Ybf: E.matmul(pY[0:64, n * 64:(n + 1) * 64], lhsT=Pn[:, n * 64:(n + 1) * 64], rhs=Ybf[:, n * 64:(n + 1) * 64],
                                                                         start=True, stop=True)) for n in range(8)]
                    P.mm_group(fns, reads=[bPn, bYbf], writes=[bpY])
                    yield
                    tt(Y[:], bY, Y[:], [bY, bpY], pY[0:64, :], bpY, ALU.add)
                    P.op("act", lambda E, Ybf=Ybf, Y=Y: E.copy(out=Ybf[:], in_=Y[:]), reads=[bY], writes=[bYbf])
                    Pc, bPc, Pn, bPn = Pn, bPn, Pc, bPc
                    Qc, bQc, Qn, bQn = Qn, bQn, Qc, bQc
                for hf in range(2):
                    pth, bpth = BK[1 + hf]
                    fns = [(lambda E, n=n, vT=vT, pth=pth, hf=hf: E.transpose(out=pth[0:64, n * 128:(n + 1) * 128], in_=vT[:, (4 * hf + n) * 64:(4 * hf + n + 1) * 64],
                                                                              identity=idf[:])) for n in range(4)]
                    P.mm_group(fns, reads=[bvT, bidf], writes=[bpth])
                    tt(bv[:, 4 * hf:4 * hf + 4, :], bbv, pth[0:64, :].rearrange("p (n d) -> p n d", d=128), [bpth, bbcol],
                       bcol[:, 4 * hf:4 * hf + 4].unsqueeze(2).to_broadcast([64, 4, 128]), bbcol, ALU.mult)
                yield
                tt(elast[:], belast, gcb[0:64, :].rearrange("p (n f) -> p n f", f=64)[:, :, 63], [bgcb, bgccol], gccol[:], bgccol, ALU.subtract)
                P.op("act", lambda E: E.activation(out=elast[:], in_=elast[:], func=AF.Exp), reads=[belast], writes=[belast])
                for hf in range(2):
                    pth, bpth = BK[1 + hf]
                    fns = [(lambda E, n=n, kT=kT, pth=pth, hf=hf: E.transpose(out=pth[0:64, n * 128:(n + 1) * 128], in_=kT[:, (4 * hf + n) * 64:(4 * hf + n + 1) * 64],
                                                                              identity=idf[:])) for n in range(4)]
                    P.mm_group(fns, reads=[bkT, bidf], writes=[bpth])
                    tt(kdec[:, 4 * hf:4 * hf + 4, :], bkdec, pth[0:64, :].rearrange("p (n d) -> p n d", d=128), [bpth, belast],
                       elast[:, 4 * hf:4 * hf + 4].unsqueeze(2).to_broadcast([64, 4, 128]), belast, ALU.mult)
                yield
                P.op("act", lambda E, gcb=gcb, EG=EG: E.activation(out=EG[:], in_=gcb[:], func=AF.Exp), reads=[bgcb], writes=[bEG])
                tt(qdec[:], bqdec, qT, [bqT, bEG], EG[:], bEG, ALU.mult)
                kTb, bkTb = H["kTb"]
                P.op("act", lambda E, kTb=kTb, kT=kT: E.copy(out=kTb[:], in_=kT), reads=[bkT], writes=[bkTb])
                P.op("act", lambda E: E.activation(out=egc[:], in_=gccol[:], func=AF.Exp), reads=[bgccol], writes=[begc])
                P.op("dve", lambda E, nbg=nbg: E.scalar_tensor_tensor(out=nbg[:], in0=egc[:], scalar=-1.0, in1=bcol[:], op0=ALU.mult, op1=ALU.mult),
                     reads=[begc, bbcol], writes=[bnbg])
            gensB = [stageB(0), stageB(1)]
            aliveB = True
            while aliveB:
                aliveB = False
                for g_ in gensB:
                    try:
                        next(g_)
                        aliveB = True
                    except StopIteration:
                        pass
            banks = [(GP[0], GP[1]), (GP[2], GP[3])]
            for n in range(8):
                cs = slice(n * 64, (n + 1) * 64)
                for h in range(2):
                    H = heads[h]; S, bS = Sst[h]
                    kT, bkT = H["kTb"]; Sbf, bSbf = H["Sbf"]
                    attnT, battnT = H["attnT"]; Y, bY = H["Ybf"]; EG, bEG = H["EG"]; qdec, bqdec = H["qdec"]
                    bv, bbv = H["bv"]; kdec, bkdec = H["kdec"]; nbg, bnbg = H["nbg"]
                    vnew, bvnew = H["vnew"]; rhs2, brhs2 = H["rhs2"]; osb, bosb = H["osb"]
                    (KSO, bKSO), (Sb, bSb) = banks[h]
                    Vb, bVb = KSO, bKSO
                    P.op("pe", lambda E, cs=cs, kT=kT, Sbf=Sbf, KSO=KSO: E.matmul(KSO[0:64, 0:128], lhsT=kT[:, cs], rhs=Sbf[:], start=True, stop=True),
                         reads=[bkT, bSbf], writes=[bKSO])
                    P.op("dve", lambda E, n=n, KSO=KSO, rhs2=rhs2, nbg=nbg, bv=bv: E.scalar_tensor_tensor(
                        out=rhs2[:], in0=KSO[0:64, 0:128], scalar=nbg[:, n:n + 1], in1=bv[:, n, :], op0=ALU.mult, op1=ALU.add),
                        reads=[bKSO, bnbg, bbv], writes=[brhs2])
                    P.op("pe", lambda E, cs=cs, Y=Y, Vb=Vb, rhs2=rhs2: E.matmul(Vb[0:64, 128:256], lhsT=Y[:, cs], rhs=rhs2[:], start=True, stop=True),
                         reads=[bY, brhs2], writes=[bVb])
                    P.op("act", lambda E, vnew=vnew, Vb=Vb: E.copy(out=vnew[:], in_=Vb[0:64, 128:256]), reads=[bVb], writes=[bvnew])
                    fns = [lambda E, cs=cs, Sbf=Sbf, KSO=KSO, qdec=qdec: E.matmul(KSO[64:128, 0:128], lhsT=qdec[:, cs], rhs=Sbf[:], start=True, stop=False),
                           lambda E, cs=cs, KSO=KSO, attnT=attnT, vnew=vnew: E.matmul(KSO[64:128, 0:128], lhsT=attnT[:, cs], rhs=vnew[:], start=False, stop=True)]
                    P.mm_group(fns, reads=[bqdec, bSbf, battnT, bvnew], writes=[bKSO])
                    P.op("pe", lambda E, n=n, Sb=Sb, kdec=kdec, vnew=vnew: E.matmul(Sb[:, 0:128], lhsT=kdec[:, n, :], rhs=vnew[:], start=True, stop=True),
                         reads=[bkdec, bvnew], writes=[bSb])
                    P.op("dve", lambda E, n=n, S=S, EG=EG, Sb=Sb: E.scalar_tensor_tensor(out=S[:], in0=S[:], scalar=EG[:, n * 64 + 63:n * 64 + 64], in1=Sb[:, 0:128],
                                                                                         op0=ALU.mult, op1=ALU.add), reads=[bS, bEG, bSb], writes=[bS])
                    P.op("act", lambda E, S=S, Sbf=Sbf: E.copy(out=Sbf[:], in_=S[:]), reads=[bS], writes=[bSbf])
                    P.op("act", lambda E, n=n, osb=osb, KSO=KSO: E.copy(out=osb[64:128, n, :], in_=KSO[64:128, 0:128]), reads=[bKSO], writes=[bosb])
                    advance(1)
                advance(1)
            advance(100)
            for h in range(2):
                osb, bosb = heads[h]["osb"]
                P.dma("sp", o_d[s_ * 512:(s_ + 1) * 512, h * 128:(h + 1) * 128].rearrange("(n c) d -> c n d", c=64), osb[64:128, :, :], reads=[bosb],
                      writes=[fz["obuf_of"](s_) if fz else bo])
            if fz:
                fz["after_chunk"](s_)
        if fz:
            barrier(P)
        else:
            P.finish([bo])
    return nc


def run_L1a(inp):
    nc = _get("L1a", build_L1a)
    c64, cmask, sel = _gdn_consts()
    w_in = inp["w_in_even"][0]
    conv = inp["conv_qkv"][0]
    ones = np.ones((128, 128), np.float32)
    maps = []
    for c in range(8):
        b, r = divmod(c, 4)
        cols = np.concatenate([np.arange(256 * r, 256 * r + 256), 1024 + np.arange(256 * r, 256 * r + 256), 2048 + np.arange(256 * r, 256 * r + 256)])
        maps.append({"x": np.ascontiguousarray(inp["x"][b]), "npre": np.ascontiguousarray(inp["norm_pre"][0]),
                     "w": np.ascontiguousarray(w_in[:, cols]), "wb": np.ascontiguousarray(w_in[:, 4096 + 2 * r:4096 + 2 * r + 2]),
                     "wa": np.ascontiguousarray(w_in[:, 4104 + 2 * r:4104 + 2 * r + 2]), "conv": np.ascontiguousarray(conv[:, cols]),
                     "alog": np.ascontiguousarray(inp["a_log"][0, 2 * r:2 * r + 2]), "dtb": np.ascontiguousarray(inp["dt_bias"][0, 2 * r:2 * r + 2]),
                     "ident": _IDENT, "c64": c64, "cmask": cmask, "sel": sel, "ones": ones})
    res = run_bass_kernel_spmd(nc, maps, core_ids=list(range(8)))
    S_ = inp["x"].shape[1]
    o = np.empty((2, S_, 1024), np.float32)
    for c in range(8):
        b, r = divmod(c, 4)
        o[b, :, 256 * r:256 * (r + 1)] = res.results[c]["o"]
    return o


def kernel_unfused(**inputs):
    inp = {k: np.asarray(v) for k, v in inputs.items()}
    o = run_L1a(inp)
    ys = run_L1b(inp)
    x1 = run_L2(inp, o, ys)
    out = run_L3(inp, x1)
    return out.astype(np.float32)


def build_fused():
    nc = bass.Bass("TRN2", target_bir_lowering=False)
    x_full = nc.dram_tensor("x", [8192, 1024], F32, kind="ExternalInput").ap()
    ident_d = nc.dram_tensor("ident", [128, 128], F32, kind="ExternalInput").ap()
    npre0_d = nc.dram_tensor("npre0", [1024], F32, kind="ExternalInput").ap()
    gidx_d = nc.dram_tensor("gidx", [128, 2, 17, 4], I32, kind="ExternalInput").ap()
    out_d = nc.dram_tensor("out", [2048, 1024], F32, kind="ExternalOutput").ap()
    ag_in = [nc.dram_tensor("ag_in%d" % i, [8192, 256], F32) for i in range(2)]
    ag_out = [nc.dram_tensor("ag_out%d" % i, [4 * 8192, 256], F32) for i in range(2)]
    x1s = nc.dram_tensor("x1s", [2176, 1024], F32)
    GROUPS = [[0, 1, 2, 3], [4, 5, 6, 7]]
    with ExitStack() as st:
        C = Ctx(nc, st); P = C.P
        csem = st.enter_context(nc.semaphore("csem"))
        bag_out = Buf("ag_out"); bx1s = Buf("x1s", multi=True); bout = Buf("out", multi=True)
        bo_ch = [Buf("o_ch%d" % k, multi=True) for k in range(16)]
        by_jt = [Buf("y_jt%d" % k, multi=True) for k in range(4)]
        ncc = [0]

        def emit_cc(which, k, inbuf, rows=512):
            P._deps("pool", [inbuf], [])
            P.streams["pool"].append(lambda E, which=which, k=k, rows=rows: E.collective_compute(
                "AllGather", ALU.bypass, replica_groups=GROUPS,
                ins=[ag_in[which].ap()[k * rows:(k + 1) * rows, :].opt()],
                outs=[ag_out[which].ap()[k * 4 * rows:(k + 1) * 4 * rows, :].opt()]).then_inc(csem))
            ncc[0] += 1

        share1 = {"x": x_full, "ident": ident_d, "npre": npre0_d}

        def after_jt(jt):
            for k in range(2 * jt, 2 * jt + 2):
                emit_cc(1, k, by_jt[jt], rows=1024)

        with ExitStack() as stU:
            CU = Ctx(nc, stU, P, "u_")
            uext = CU.sb("uTp", [128, 2, 16, 512], BF16)
            build_L1a(8192, fz={"nc": nc, "P": P, "pfx": "a_", "share": share1, "out": ag_in[0].ap(), "uTp": uext,
                                "obuf_of": lambda s_: bo_ch[s_], "after_chunk": lambda s_: emit_cc(0, s_, bo_ch[s_])})
            build_L1b(8192, fz={"nc": nc, "P": P, "pfx": "b_", "share": share1, "out": ag_in[1].ap(), "uTp": uext,
                                "obuf_of": lambda jt: by_jt[jt], "after_chunk": after_jt})
        gidx, bgidx = C.sb("gidx", [128, 2, 17, 4], I32)
        P.dma("sp", gidx[:], gidx_d, writes=[bgidx])
        waited = [False]

        def gather(P_, ld, bld, tile, part):
            if not waited[0]:
                P.streams["pool"].append(lambda E: E.wait_ge(csem, ncc[0]))
                P.op("pool", lambda E: E.nop(), reads=[], writes=[bag_out])
                waited[0] = True
            for i in range(4):
                P_.dma_ind("pool", ld[:, i * 256:(i + 1) * 256], ag_out[part].ap(), gidx[:, part, tile, i:i + 1], reads=[bag_out, bgidx], writes=[bld])

        share2 = {"ident": ident_d, "npre": npre0_d, "o": None, "ys": None}
        build_L2(2176, fz={"nc": nc, "P": P, "pfx": "c_", "share": share2, "out": x1s.ap(), "obuf": bx1s, "gather": gather})
        share3 = {"ident": ident_d, "x": x1s.ap()}
        build_L3(2048, fz={"nc": nc, "P": P, "pfx": "d_", "share": share3, "out": out_d, "obuf": bout, "xbuf": bx1s})
        P.finish([bout])
    return nc


def _gidx(r):
    g = np.zeros((128, 2, 17, 4), np.int32)
    p = np.arange(128)[:, None, None]
    tile = np.arange(17)[None, :, None]
    src = np.arange(4)[None, None, :]
    tok = np.clip(2048 * r - 128 + tile * 128 + p, 0, 8191)
    for part, R in ((0, 512), (1, 1024)):
        g[:, part] = ((tok // R) * 4 + src) * R + tok % R
    return g


def kernel(**inputs):
    inp = {k: np.ascontiguousarray(np.asarray(v)) for k, v in inputs.items()}
    nc = _get("fused", build_fused)
    c64, cmask, sel = _gdn_consts()
    mk, idm = _s5_consts()
    ones = np.ones((128, 128), np.float32)
    w_in = inp["w_in_even"][0]
    conv = inp["conv_qkv"][0]
    wz = np.ascontiguousarray(np.concatenate([w_in[:, 3072:4096], w_in[:, 5136:6160]], axis=1))
    maps = []
    for c in range(8):
        b, r = divmod(c, 4)
        cols = np.concatenate([np.arange(256 * r, 256 * r + 256), 1024 + np.arange(256 * r, 256 * r + 256), 2048 + np.arange(256 * r, 256 * r + 256)])
        gs = slice(16 * r, 16 * r + 16)
        xq = np.zeros((2176, 1024), np.float32)
        xq[128:] = inp["x"][b, 2048 * r:2048 * (r + 1)]
        if r > 0:
            xq[:128] = inp["x"][b, 2048 * r - 128:2048 * r]
        m = {"x": inp["x"][b], "ident": _IDENT, "npre0": inp["norm_pre"][0], "gidx": _gidx(r),
             "a_w": np.ascontiguousarray(w_in[:, cols]), "a_wb": np.ascontiguousarray(w_in[:, 4096 + 2 * r:4096 + 2 * r + 2]),
             "a_wa": np.ascontiguousarray(w_in[:, 4104 + 2 * r:4104 + 2 * r + 2]), "a_conv": np.ascontiguousarray(conv[:, cols]),
             "a_alog": np.ascontiguousarray(inp["a_log"][0, 2 * r:2 * r + 2]), "a_dtb": np.ascontiguousarray(inp["dt_bias"][0, 2 * r:2 * r + 2]),
             "a_c64": c64, "a_cmask": cmask, "a_sel": sel, "a_ones": ones,
             "a_wu": np.ascontiguousarray(w_in[:, 4112 + 256 * r:4112 + 256 * (r + 1)]),
             "b_wu": np.ascontiguousarray(w_in[:, 4112 + 256 * r:4112 + 256 * (r + 1)]),
             "b_lre": np.ascontiguousarray(inp["s5_lam_re"][0, gs]), "b_lim": np.ascontiguousarray(inp["s5_lam_im"][0, gs]),
             "b_bre": np.ascontiguousarray(inp["s5_b_re"][0, gs]), "b_bim": np.ascontiguousarray(inp["s5_b_im"][0, gs]),
             "b_cre": np.ascontiguousarray(inp["s5_c_re"][0, gs]), "b_cim": np.ascontiguousarray(inp["s5_c_im"][0, gs]),
             "b_ldt": np.ascontiguousarray(inp["s5_log_dt"][0, gs]), "b_dd": np.ascontiguousarray(inp["s5_d"][0, 256 * r:256 * (r + 1)]),
             "b_taus": TAUS, "b_mk": mk, "b_idm": idm,
             "c_x": xq, "c_wz": wz, "c_wglu": inp["w_glu"][0], "c_wout": inp["w_out_even"][0], "c_npost": inp["norm_post"][0],
             "c_gnw": inp["gdn_norm_w"][0],
             "d_win": inp["w_in_odd"][0], "d_wout": inp["w_out_odd"][0], "d_conv": inp["conv_short"][0],
             "d_npre": inp["norm_pre"][1], "d_npost": inp["norm_post"][1]}
        maps.append(m)
    res = run_bass_kernel_spmd(nc, maps, core_ids=list(range(8)))
    out = np.empty((2, 8192, 1024), np.float32)
    for c in range(8):
        b, r = divmod(c, 4)
        out[b, r * 2048:(r + 1) * 2048] = res.results[c]["out"]
    return out
```

```python
from contextlib import ExitStack
import numpy as np
import concourse.bass as bass
import concourse.mybir as mybir
from concourse.bass_utils import run_bass_kernel_spmd

F32 = mybir.dt.float32
BF16 = mybir.dt.bfloat16
AF = mybir.ActivationFunctionType
ALU = mybir.AluOpType
AX = mybir.AxisListType

NDS = 12


class Buf:
    __slots__ = ("name", "w", "r", "multi")

    def __init__(self, name, multi=False):
        self.name = name
        self.w = [] if multi else None
        self.r = []
        self.multi = multi


class Prog:
    ENG = ("pe", "act", "dve", "pool", "sp")

    def __init__(self, nc, stack):
        self.nc = nc
        self.stack = stack
        self.streams = {e: [] for e in self.ENG}
        self.cnt = {e: 0 for e in self.ENG}
        self.sem = {e: stack.enter_context(nc.semaphore("s_" + e)) for e in self.ENG}
        self.seen = {e: {} for e in self.ENG}
        self.dcnt = {e: 0 for e in self.ENG}
        self.dsem = {}
        for e in ("sp", "pool", "act"):
            self.dsem[e] = [stack.enter_context(nc.semaphore("d_%s%d" % (e, i))) for i in range(NDS)]
        self.same_engine_sync = True
        self.nwaits = 0

    def _wait(self, eng, tok):
        if tok is None:
            return
        kind = tok[0]
        if kind == "c":
            _, e2, n = tok
            if e2 == eng and (eng == "pe" or not self.same_engine_sync):
                return
            key = e2
            if self.seen[eng].get(key, 0) >= n:
                return
            self.seen[eng][key] = n
            sem = self.sem[e2]
            self.streams[eng].append(lambda E, sem=sem, n=n: E.wait_ge(sem, n))
            self.nwaits += 1
        else:
            _, q, slot, val = tok
            key = ("d", q, slot)
            if self.seen[eng].get(key, 0) >= val:
                return
            self.seen[eng][key] = val
            sem = self.dsem[q][slot]
            self.streams[eng].append(lambda E, sem=sem, val=val: E.wait_ge(sem, val))
            self.nwaits += 1

    def _deps(self, eng, reads, writes):
        for b in reads:
            if b.multi:
                for t in b.w:
                    self._wait(eng, t)
            else:
                self._wait(eng, b.w)
        for b in writes:
            if not b.multi:
                self._wait(eng, b.w)
            for t in b.r:
                self._wait(eng, t)

    def _commit(self, tok, reads, writes):
        for b in writes:
            if b.multi:
                b.w.append(tok)
            else:
                b.w = tok
            b.r = []
        for b in reads:
            if b not in writes:
                b.r.append(tok)

    def op(self, eng, fn, reads=(), writes=()):
        reads = list(reads)
        writes = list(writes)
        self._deps(eng, reads, writes)
        self.cnt[eng] += 1
        n = self.cnt[eng]
        sem = self.sem[eng]
        self.streams[eng].append(lambda E, fn=fn, sem=sem: fn(E).then_inc(sem, 1))
        tok = ("c", eng, n)
        self._commit(tok, reads, writes)
        return tok

    def mm_group(self, fns, reads=(), writes=()):
        eng = "pe"
        reads = list(reads)
        writes = list(writes)
        self._deps(eng, reads, writes)
        self.cnt[eng] += 1
        n = self.cnt[eng]
        sem = self.sem[eng]
        for fn in fns[:-1]:
            self.streams[eng].append(lambda E, fn=fn: fn(E))
        last = fns[-1]
        self.streams[eng].append(lambda E, fn=last, sem=sem: fn(E).then_inc(sem, 1))
        tok = ("c", eng, n)
        self._commit(tok, reads, writes)
        return tok

    def dma(self, q, out_ap, in_ap, reads=(), writes=()):
        reads = list(reads)
        writes = list(writes)
        self._deps(q, reads, writes)
        j = self.dcnt[q]
        self.dcnt[q] += 1
        slot = j % NDS
        val = 16 * (j // NDS + 1)
        if j >= NDS:
            self._wait(q, ("d", q, slot, val - 16))
        sem = self.dsem[q][slot]
        self.streams[q].append(
            lambda E, o=out_ap, i=in_ap, sem=sem: E.dma_start(out=o, in_=i).then_inc(sem, 16))
        tok = ("d", q, slot, val)
        self._commit(tok, reads, writes)
        return tok

    def dma_ind(self, q, out_ap, table_ap, idx_ap, reads=(), writes=()):
        reads = list(reads)
        writes = list(writes)
        self._deps(q, reads, writes)
        j = self.dcnt[q]
        self.dcnt[q] += 1
        slot = j % NDS
        val = 16 * (j // NDS + 1)
        if j >= NDS:
            self._wait(q, ("d", q, slot, val - 16))
        sem = self.dsem[q][slot]
        self.streams[q].append(
            lambda E, o=out_ap, t=table_ap, i=idx_ap, sem=sem: E.indirect_dma_start(
                out=o, out_offset=None, in_=t, in_offset=bass.IndirectOffsetOnAxis(ap=i, axis=0)).then_inc(sem, 16))
        tok = ("d", q, slot, val)
        self._commit(tok, reads, writes)
        return tok

    def finish(self, final_bufs):
        for b in final_bufs:
            for t in (b.w if b.multi else [b.w]):
                self._wait("sp", t)
        nc = self.nc
        streams = self.streams
        with nc.Block() as block:
            @block.tensor
            def _(E):
                for f in streams["pe"]:
                    f(E)

            @block.scalar
            def _(E):
                for f in streams["act"]:
                    f(E)

            @block.vector
            def _(E):
                for f in streams["dve"]:
                    f(E)

            @block.gpsimd
            def _(E):
                for f in streams["pool"]:
                    f(E)

            @block.sync
            def _(E):
                for f in streams["sp"]:
                    f(E)


class Ctx:
    def __init__(self, nc, st, P=None, pfx=""):
        self.nc = nc
        self.st = st
        self.pfx = pfx
        if P is None:
            st.enter_context(nc.allow_non_contiguous_dma(reason="small parameter loads / layout transforms"))
            P = Prog(nc, st)
        self.P = P

    def sb(self, name, shape, dt=F32):
        t = self.st.enter_context(self.nc.sbuf_tensor("sb_" + self.pfx + name, shape, dt))
        return t, Buf(name)

    def ps(self, name, shape, dt=F32):
        t = self.st.enter_context(self.nc.psum_tensor("ps_" + self.pfx + name, shape, dt))
        return t, Buf(name)


def bcast_row_load(C, name, dram_vec, n, q="sp"):
    t, b = C.sb(name, [128, n])
    C.P.dma(q, t[:], dram_vec.partition_broadcast(128), writes=[b])
    return t, b


def make_ident(C, dram_ident):
    idf, bidf = C.sb("identf", [128, 128])
    C.P.dma("sp", idf[:], dram_ident, writes=[bidf])
    idb, bidb = C.sb("identb", [128, 128], BF16)
    C.P.op("dve", lambda E: E.tensor_copy(out=idb[:], in_=idf[:]), reads=[bidf], writes=[bidb])
    return idf, bidf, idb, bidb


def rms_rstd(C, src, bsrc, ncols, junk, bjunk, ss, bss, eps=1e-6):
    P = C.P
    P.op("act", lambda E: E.activation(out=junk, in_=src, func=AF.Square, accum_out=ss[:, 0:1]),
         reads=[bsrc], writes=[bjunk, bss])
    P.op("act", lambda E: E.activation(out=ss[:, 0:1], in_=ss[:, 0:1], func=AF.Sqrt, bias=float(eps), scale=float(1.0 / ncols)),
         reads=[bss], writes=[bss])
    P.op("dve", lambda E: E.reciprocal(out=ss[:, 0:1], in_=ss[:, 0:1]), reads=[bss], writes=[bss])


def transpose8(C, src_bf, bsrc, idb, bidb, ptr, bptr, dst3, bdst, eng="act"):
    P = C.P
    fns = [(lambda E, kt=kt: E.transpose(out=ptr[:, kt * 128:(kt + 1) * 128], in_=src_bf[:, kt * 128:(kt + 1) * 128],
                                         identity=idb[:])) for kt in range(8)]
    P.mm_group(fns, reads=[bsrc, bidb], writes=[bptr])
    src3 = ptr[:].rearrange("p (k t) -> p k t", k=8)
    if eng == "act":
        P.op("act", lambda E: E.copy(out=dst3, in_=src3), reads=[bptr], writes=[bdst])
    else:
        P.op("dve", lambda E: E.tensor_copy(out=dst3, in_=src3), reads=[bptr], writes=[bdst])


def outproj_post(C, catT, bcat, nkt, wout, bwout, t, xres, bxres, npw, bnpw, pso, bpso, yo, byo, junk, bjunk, ss, bss,
                 out_dram_rows, bout):
    P = C.P
    for hh in range(2):
        fns = [(lambda E, kt=kt, hh=hh: E.matmul(pso[hh][:], lhsT=catT[:, kt, t * 128:(t + 1) * 128],
                                                 rhs=wout[:, kt, hh * 512:(hh + 1) * 512],
                                                 start=(kt == 0), stop=(kt == nkt - 1))) for kt in range(nkt)]
        P.mm_group(fns, reads=[bcat, bwout], writes=[bpso[hh]])
        P.op("act", lambda E, hh=hh: E.copy(out=yo[:, hh * 512:(hh + 1) * 512], in_=pso[hh][:]),
             reads=[bpso[hh]], writes=[byo])
    rms_rstd(C, yo[:], byo, 1024, junk[:], bjunk, ss, bss)
    P.op("dve", lambda E: E.scalar_tensor_tensor(out=yo[:], in0=yo[:], scalar=ss[:, 0:1], in1=npw[:],
                                                 op0=ALU.mult, op1=ALU.mult), reads=[byo, bss, bnpw], writes=[byo])
    P.op("dve", lambda E: E.tensor_tensor(out=yo[:], in0=yo[:], in1=xres, op=ALU.add), reads=[byo, bxres], writes=[byo])
    P.dma("sp", out_dram_rows, yo[:], reads=[byo], writes=[bout])


def load_w_bf16(C, name, dram_w, kt_n, ncols, chunk=2048, groups=None):
    w, _ = C.sb(name, [128, kt_n, ncols], BF16)
    src = dram_w.rearrange("(k p) c -> p k c", p=128)
    if groups is None:
        bw = Buf(name, multi=True)
        for kt in range(kt_n):
            for c0 in range(0, ncols, chunk):
                c1 = min(ncols, c0 + chunk)
                C.P.dma("pool", w[:, kt, c0:c1], src[:, kt, c0:c1], writes=[bw])
        return w, bw
    bws = []
    for gi, sls in enumerate(groups):
        bg = Buf("%s_g%d" % (name, gi), multi=True)
        for (c0, c1) in sls:
            for kt in range(kt_n):
                C.P.dma("pool", w[:, kt, c0:c1], src[:, kt, c0:c1], writes=[bg])
        bws.append(bg)
    return w, bws


def build_L2(ntok=2048, fz=None):
    nc = fz["nc"] if fz else bass.Bass("TRN2", target_bir_lowering=False)
    pfx = fz["pfx"] if fz else ""

    def D(name, shape):
        if fz and name in fz["share"]:
            return fz["share"][name]
        return nc.dram_tensor(pfx + name, shape, F32, kind="ExternalInput").ap()
    x_d = D("x", [ntok, 1024]); o_d = D("o", [ntok, 1024]); ys_d = D("ys", [ntok, 1024])
    wz_d = D("wz", [1024, 2048]); wglu_d = D("wglu", [1024, 1024]); wout_d = D("wout", [2048, 1024])
    npre_d = D("npre", [1024]); npost_d = D("npost", [1024]); gnw_d = D("gnw", [128]); ident_d = D("ident", [128, 128])
    out_d = fz["out"] if fz else nc.dram_tensor("out", [ntok, 1024], F32, kind="ExternalOutput").ap()
    NT = 512
    with ExitStack() as st:
        C = Ctx(nc, st, fz["P"], pfx) if fz else Ctx(nc, st); P = C.P
        idf, bidf, idb, bidb = make_ident(C, ident_d)
        npre, bnpre = bcast_row_load(C, "npre", npre_d, 1024)
        npost, bnpost = bcast_row_load(C, "npost", npost_d, 1024)
        gnw, bgnw = bcast_row_load(C, "gnw", gnw_d, 128)
        wz, bwz = load_w_bf16(C, "wz", wz_d, 8, 2048)
        wglu, bwglu = load_w_bf16(C, "wglu", wglu_d, 8, 1024)
        wout, bwout = load_w_bf16(C, "wout", wout_d, 16, 1024)
        xt4, bxt4 = C.sb("xt4", [128, 4, 1024]); bxt = [Buf("xt%d" % i) for i in range(4)]
        ldo = [C.sb("ldo%d" % i, [128, 1024]) for i in range(2)]
        ldy = [C.sb("ldy%d" % i, [128, 1024], BF16 if (fz and fz.get("ybf16")) else F32) for i in range(2)]
        for (_t, _b) in ldo + ldy:
            _b.multi = True; _b.w = []
        sq, bsq = C.sb("sq", [128, 1024])
        hn, bhn = C.sb("hn", [128, 1024], BF16)
        ss, bss = C.sb("ss", [128, 1])
        ss8, bss8 = C.sb("ss8", [128, 8])
        hT, bhT = C.sb("hT", [128, 8, NT], BF16)
        oT, boT = C.sb("oT", [128, 8, NT], BF16)
        yT, byT = C.sb("yT", [128, 8, NT], BF16)
        gz, bgz = C.sb("gz", [128, 8, NT], BF16)
        sg, bsg = C.sb("sg", [128, NT], BF16)
        catT, bcat = C.sb("catT", [128, 16, NT], BF16)
        yo, byo = C.sb("yo", [128, 1024])
        ptr, bptr = C.ps("ptr", [128, 1024], BF16)
        pmm = []; bpmm = []
        for i in range(4):
            t_, b_ = C.ps("pmm%d" % i, [128, 512]); pmm.append(t_); bpmm.append(b_)
        pso = []; bpso = []
        for i in range(2):
            t_, b_ = C.ps("pso%d" % i, [128, 512]); pso.append(t_); bpso.append(b_)
        bout = fz["obuf"] if fz else Buf("out", multi=True)
        if fz:
            sts = [(0, 128)] + [(128 + i * NT, NT) for i in range((ntok - 128) // NT)]
        else:
            sts = [(i * NT, NT) for i in range(ntok // NT)]
        tile_r0 = [t0_ + t_ * 128 for (t0_, n_) in sts for t_ in range(n_ // 128)]

        def issue_loads(ti):
            r0_ = tile_r0[ti]
            lo, blo = ldo[ti % 2]; ly, bly = ldy[ti % 2]
            if fz:
                fz["gather"](P, lo, blo, r0_ // 128, 0)
                fz["gather"](P, ly, bly, r0_ // 128, 1)
            else:
                P.dma("sp", lo[:], o_d[r0_:r0_ + 128, :], writes=[blo])
                P.dma("sp", ly[:], ys_d[r0_:r0_ + 128, :], writes=[bly])

        issue_loads(0)
        for (t0, n) in sts:
            ntl = n // 128
            for t in range(ntl):
                r0 = t0 + t * 128
                ti = tile_r0.index(r0)
                if ti + 1 < len(tile_r0):
                    issue_loads(ti + 1)
                P.dma("sp", xt4[:, t, :], x_d[r0:r0 + 128, :], writes=[bxt[t]])
                rms_rstd(C, xt4[:, t, :], bxt[t], 1024, sq[:], bsq, ss, bss)
                P.op("dve", lambda E, t=t: E.scalar_tensor_tensor(out=hn[:], in0=xt4[:, t, :], scalar=ss[:, 0:1], in1=npre[:],
                                                                  op0=ALU.mult, op1=ALU.mult), reads=[bxt[t], bss, bnpre], writes=[bhn])
                transpose8(C, hn, bhn, idb, bidb, ptr, bptr, hT[:, :, t * 128:(t + 1) * 128], bhT, eng="act")
                ld, bld = ldo[ti % 2]
                P.op("act", lambda E, ld=ld: E.activation(out=sq[:], in_=ld[:], func=AF.Square), reads=[bld], writes=[bsq])
                P.op("dve", lambda E: E.tensor_reduce(out=ss8[:], in_=sq[:].rearrange("p (h d) -> p h d", h=8), axis=AX.X, op=ALU.add),
                     reads=[bsq], writes=[bss8])
                P.op("dve", lambda E: E.tensor_scalar(out=ss8[:], in0=ss8[:], scalar1=1.0 / 128, scalar2=1e-6, op0=ALU.mult, op1=ALU.add),
                     reads=[bss8], writes=[bss8])
                P.op("act", lambda E: E.activation(out=ss8[:], in_=ss8[:], func=AF.Sqrt), reads=[bss8], writes=[bss8])
                P.op("dve", lambda E: E.reciprocal(out=ss8[:], in_=ss8[:]), reads=[bss8], writes=[bss8])
                P.op("dve", lambda E, ld=ld: E.tensor_tensor(out=sq[:].rearrange("p (h d) -> p h d", h=8), in0=ld[:].rearrange("p (h d) -> p h d", h=8),
                                                      in1=ss8[:].unsqueeze(2).to_broadcast([128, 8, 128]), op=ALU.mult),
                     reads=[bld, bss8], writes=[bsq])
                P.op("dve", lambda E: E.tensor_tensor(out=hn[:].rearrange("p (h d) -> p h d", h=8), in0=sq[:].rearrange("p (h d) -> p h d", h=8),
                                                      in1=gnw[:].unsqueeze(1).to_broadcast([128, 8, 128]), op=ALU.mult),
                     reads=[bsq, bgnw], writes=[bhn])
                transpose8(C, hn, bhn, idb, bidb, ptr, bptr, oT[:, :, t * 128:(t + 1) * 128], boT, eng="act")
                ld, bld = ldy[ti % 2]
                P.op("act", lambda E, ld=ld: E.activation(out=hn[:], in_=ld[:], func=AF.Gelu_apprx_tanh), reads=[bld], writes=[bhn])
                transpose8(C, hn, bhn, idb, bidb, ptr, bptr, yT[:, :, t * 128:(t + 1) * 128], byT, eng="dve")
            for ct in range(16):
                pb = pmm[ct % 4]; bpb = bpmm[ct % 4]
                fns = [(lambda E, kt=kt, ct=ct, pb=pb, n=n: E.matmul(pb[:, 0:n], lhsT=wz[:, kt, ct * 128:(ct + 1) * 128], rhs=hT[:, kt, 0:n],
                                                                start=(kt == 0), stop=(kt == 7))) for kt in range(8)]
                P.mm_group(fns, reads=[bwz, bhT], writes=[bpb])
                if ct < 8:
                    P.op("act", lambda E, pb=pb, n=n: E.activation(out=sg[:, 0:n], in_=pb[:, 0:n], func=AF.Silu), reads=[bpb], writes=[bsg])
                    P.op("dve", lambda E, ct=ct, n=n: E.tensor_tensor(out=catT[:, ct, 0:n], in0=oT[:, ct, 0:n], in1=sg[:, 0:n], op=ALU.mult),
                         reads=[boT, bsg], writes=[bcat])
                else:
                    P.op("act", lambda E, pb=pb, ct=ct, n=n: E.activation(out=gz[:, ct - 8, 0:n], in_=pb[:, 0:n], func=AF.Silu), reads=[bpb], writes=[bgz])
            for ct in range(8):
                pb = pmm[ct % 4]; bpb = bpmm[ct % 4]
                fns = [(lambda E, kt=kt, ct=ct, pb=pb, n=n: E.matmul(pb[:, 0:n], lhsT=wglu[:, kt, ct * 128:(ct + 1) * 128], rhs=yT[:, kt, 0:n],
                                                                start=(kt == 0), stop=(kt == 7))) for kt in range(8)]
                P.mm_group(fns, reads=[bwglu, byT], writes=[bpb])
                P.op("act", lambda E, pb=pb, n=n: E.activation(out=sg[:, 0:n], in_=pb[:, 0:n], func=AF.Sigmoid), reads=[bpb], writes=[bsg])
                P.op("dve", lambda E, ct=ct, n=n: E.tensor_tensor(out=sg[:, 0:n], in0=sg[:, 0:n], in1=yT[:, ct, 0:n], op=ALU.mult), reads=[bsg, byT], writes=[bsg])
                P.op("dve", lambda E, ct=ct, n=n: E.tensor_tensor(out=catT[:, 8 + ct, 0:n], in0=sg[:, 0:n], in1=gz[:, ct, 0:n], op=ALU.mult),
                     reads=[bsg, bgz], writes=[bcat])
            for t in range(ntl):
                r0 = t0 + t * 128
                outproj_post(C, catT, bcat, 16, wout, bwout, t, xt4[:, t, :], bxt[t], npost, bnpost, pso, bpso, yo, byo, sq, bsq, ss, bss,
                             out_d[r0:r0 + 128, :], bout)
        if fz:
            barrier(P)
        else:
            P.finish([bout])
    return nc


def build_L3(ntok=2048, fz=None):
    nc = fz["nc"] if fz else bass.Bass("TRN2", target_bir_lowering=False)
    pfx = fz["pfx"] if fz else ""

    def D(name, shape):
        if fz and name in fz["share"]:
            return fz["share"][name]
        return nc.dram_tensor(pfx + name, shape, F32, kind="ExternalInput").ap()
    x_d = D("x", [ntok + 128, 1024])
    win_d = D("win", [1024, 8192]); wout_d = D("wout", [2048, 1024]); conv_d = D("conv", [3, 2048])
    npre_d = D("npre", [1024]); npost_d = D("npost", [1024]); ident_d = D("ident", [128, 128])
    out_d = fz["out"] if fz else nc.dram_tensor("out", [ntok, 1024], F32, kind="ExternalOutput").ap()
    NT = 256
    with ExitStack() as st:
        C = Ctx(nc, st, fz["P"], pfx) if fz else Ctx(nc, st); P = C.P
        idf, bidf, idb, bidb = make_ident(C, ident_d)
        npre, bnpre = bcast_row_load(C, "npre", npre_d, 1024)
        npost, bnpost = bcast_row_load(C, "npost", npost_d, 1024)
        cw, bcw = C.sb("cw", [128, 3, 16])
        P.dma("sp", cw[:], conv_d.rearrange("j (c p) -> p j c", p=128), writes=[bcw])
        win, bwin_g = load_w_bf16(C, "win", win_d, 8, 8192,
                                  groups=[[(part * 2048 + cg * 512, part * 2048 + cg * 512 + 512) for part in range(4)] for cg in range(4)])
        wout, bwout = load_w_bf16(C, "wout", wout_d, 16, 1024)
        xt, bxt = C.sb("xt", [128, 1024])
        sq, bsq = C.sb("sq", [128, 1024])
        hn, bhn = C.sb("hn", [128, 1024], BF16)
        ss, bss = C.sb("ss", [128, 1])
        hT, bhT = C.sb("hT", [128, 8, NT], BF16)
        y1T, by1T = C.sb("y1T", [128, 16, NT], BF16)
        pbuf, bpbuf = C.sb("pbuf", [128, NT + 2])
        phalo, bphalo = C.sb("phalo", [128, 16, 2])
        gcs, bgcs = C.sb("gcs", [128, NT])
        cv, bcv = C.sb("cv", [128, NT])
        sz, bsz = C.sb("sz", [128, NT])
        yo, byo = C.sb("yo", [128, 1024])
        P.op("dve", lambda E: E.memset(phalo[:], 0.0), writes=[bphalo])
        ptr, bptr = C.ps("ptr", [128, 1024], BF16)
        GB = [C.ps("g%d" % i, [128, 512]) for i in range(7)]
        pso = [GB[0][0], GB[1][0]]; bpso = [GB[0][1], GB[1][1]]
        bout = fz["obuf"] if fz else Buf("out", multi=True)
        sts = [(0, 128)] + [(128 + i * NT, NT) for i in range(ntok // NT)]
        for (t0, n) in sts:
            ntl = n // 128
            for t in range(ntl):
                r0 = t0 + t * 128
                P.dma("sp", xt[:], x_d[r0:r0 + 128, :], reads=([fz["xbuf"]] if fz else []), writes=[bxt])
                rms_rstd(C, xt[:], bxt, 1024, sq[:], bsq, ss, bss)
                P.op("dve", lambda E: E.scalar_tensor_tensor(out=hn[:], in0=xt[:], scalar=ss[:, 0:1], in1=npre[:],
                                                             op0=ALU.mult, op1=ALU.mult), reads=[bxt, bss, bnpre], writes=[bhn])
                transpose8(C, hn, bhn, idb, bidb, ptr, bptr, hT[:, :, t * 128:(t + 1) * 128], bhT, eng="act")
            for ct in range(16):
                sel_ = [GB[3 * (ct % 2) + 0], GB[3 * (ct % 2) + 1], GB[3 * (ct % 2) + 2], GB[6]]
                pmm = [x_[0] for x_ in sel_]; bpmm = [x_[1] for x_ in sel_]
                for part in range(4):
                    col0 = (part * 16 + ct) * 128
                    pb = pmm[part]
                    fns = [(lambda E, n=n, kt=kt, col0=col0, pb=pb: E.matmul(pb[:, 0:n], lhsT=win[:, kt, col0:col0 + 128], rhs=hT[:, kt, 0:n],
                                                                        start=(kt == 0), stop=(kt == 7))) for kt in range(8)]
                    P.mm_group(fns, reads=[bwin_g[ct // 4], bhT], writes=[bpmm[part]])
                P.op("act", lambda E, n=n, pmm=pmm: E.copy(out=gcs[:, 0:n], in_=pmm[1][:, 0:n]), reads=[bpmm[1]], writes=[bgcs])
                P.op("act", lambda E, ct=ct: E.copy(out=pbuf[:, 0:2], in_=phalo[:, ct, :]), reads=[bphalo], writes=[bpbuf])
                P.op("dve", lambda E, n=n, pmm=pmm: E.tensor_tensor(out=pbuf[:, 2:2 + n], in0=gcs[:, 0:n], in1=pmm[2][:, 0:n], op=ALU.mult),
                     reads=[bgcs, bpmm[2]], writes=[bpbuf])
                P.op("act", lambda E, n=n, ct=ct: E.copy(out=phalo[:, ct, :], in_=pbuf[:, n:n + 2]), reads=[bpbuf], writes=[bphalo])
                if t0 == 0:
                    continue
                P.op("dve", lambda E, n=n, ct=ct: E.tensor_scalar(out=cv[:, 0:n], in0=pbuf[:, 0:n], scalar1=cw[:, 0, ct:ct + 1], scalar2=None, op0=ALU.mult),
                     reads=[bpbuf, bcw], writes=[bcv])
                P.op("dve", lambda E, n=n, ct=ct: E.scalar_tensor_tensor(out=cv[:, 0:n], in0=pbuf[:, 1:1 + n], scalar=cw[:, 1, ct:ct + 1], in1=cv[:, 0:n],
                                                                    op0=ALU.mult, op1=ALU.add), reads=[bpbuf, bcw, bcv], writes=[bcv])
                P.op("dve", lambda E, n=n, ct=ct: E.scalar_tensor_tensor(out=cv[:, 0:n], in0=pbuf[:, 2:2 + n], scalar=cw[:, 2, ct:ct + 1], in1=cv[:, 0:n],
                                                                    op0=ALU.mult, op1=ALU.add), reads=[bpbuf, bcw, bcv], writes=[bcv])
                P.op("dve", lambda E, n=n, pmm=pmm: E.tensor_tensor(out=cv[:, 0:n], in0=cv[:, 0:n], in1=pmm[0][:, 0:n], op=ALU.mult), reads=[bcv, bpmm[0]], writes=[bcv])
                P.op("act", lambda E, n=n, pmm=pmm: E.activation(out=sz[:, 0:n], in_=pmm[3][:, 0:n], func=AF.Silu), reads=[bpmm[3]], writes=[bsz])
                P.op("dve", lambda E, n=n, ct=ct: E.tensor_tensor(out=y1T[:, ct, 0:n], in0=cv[:, 0:n], in1=sz[:, 0:n], op=ALU.mult),
                     reads=[bcv, bsz], writes=[by1T])
            if t0 == 0:
                continue
            for t in range(ntl):
                r0 = t0 + t * 128
                P.dma("sp", xt[:], x_d[r0:r0 + 128, :], reads=([fz["xbuf"]] if fz else []), writes=[bxt])
                outproj_post(C, y1T, by1T, 16, wout, bwout, t, xt[:], bxt, npost, bnpost, pso, bpso, yo, byo, sq, bsq, ss, bss,
                             out_d[r0 - 128:r0, :], bout)
        if fz:
            barrier(P)
        else:
            P.finish([bout])
    return nc


_IDENT = np.eye(128, dtype=np.float32)
_CACHE = {}


def _get(name, fn):
    if name not in _CACHE:
        _CACHE[name] = fn()
    return _CACHE[name]


def run_L2(inp, o_full, ys_full):
    nc = _get("L2", build_L2)
    w_in = inp["w_in_even"][0]
    wz = np.ascontiguousarray(np.concatenate([w_in[:, 3072:4096], w_in[:, 5136:6160]], axis=1))
    maps = []
    for c in range(8):
        b, r = divmod(c, 4)
        sl = slice(r * 2048, (r + 1) * 2048)
        maps.append({"x": np.ascontiguousarray(inp["x"][b, sl]), "o": np.ascontiguousarray(o_full[b, sl]),
                     "ys": np.ascontiguousarray(ys_full[b, sl]), "wz": wz, "wglu": np.ascontiguousarray(inp["w_glu"][0]),
                     "wout": np.ascontiguousarray(inp["w_out_even"][0]), "npre": np.ascontiguousarray(inp["norm_pre"][0]),
                     "npost": np.ascontiguousarray(inp["norm_post"][0]), "gnw": np.ascontiguousarray(inp["gdn_norm_w"][0]),
                     "ident": _IDENT})
    res = run_bass_kernel_spmd(nc, maps, core_ids=list(range(8)))
    x1 = np.empty((2, 8192, 1024), np.float32)
    for c in range(8):
        b, r = divmod(c, 4)
        x1[b, r * 2048:(r + 1) * 2048] = res.results[c]["out"]
    return x1


def run_L3(inp, x1):
    nc = _get("L3", build_L3)
    maps = []
    for c in range(8):
        b, r = divmod(c, 4)
        xh = np.zeros((2048 + 128, 1024), np.float32)
        xh[128:] = x1[b, r * 2048:(r + 1) * 2048]
        if r > 0:
            xh[:128] = x1[b, r * 2048 - 128:r * 2048]
        maps.append({"x": xh, "win": np.ascontiguousarray(inp["w_in_odd"][0]), "wout": np.ascontiguousarray(inp["w_out_odd"][0]),
                     "conv": np.ascontiguousarray(inp["conv_short"][0]), "npre": np.ascontiguousarray(inp["norm_pre"][1]),
                     "npost": np.ascontiguousarray(inp["norm_post"][1]), "ident": _IDENT})
    res = run_bass_kernel_spmd(nc, maps, core_ids=list(range(8)))
    out = np.empty((2, 8192, 1024), np.float32)
    for c in range(8):
        b, r = divmod(c, 4)
        out[b, r * 2048:(r + 1) * 2048] = res.results[c]["out"]
    return out


I32 = mybir.dt.int32
TAUS = np.array(list(range(17)) + [32, 64, 128, 256, 512, 1024, 2048, 4096] + list(range(15, -1, -1)), np.float32)
NTAU = len(TAUS)


def _s5_consts():
    mk = np.zeros((128, 2, 16, 16), np.float32)
    idm = np.zeros((128, 2, 16, 16), np.float32)
    for kt2 in range(2):
        for sp in range(8):
            s = kt2 * 8 + sp
            for h in range(16):
                mk[sp * 16 + h, kt2, s:, :] = 1.0
                idm[sp * 16 + h, kt2, s, h] = 1.0
    return mk.reshape(128, 2, 256), idm.reshape(128, 2, 256)


def barrier(P):
    for e in P.ENG:
        for e2 in P.ENG:
            if P.cnt[e2] > 0:
                P._wait(e, ("c", e2, P.cnt[e2]))
        for q in P.dsem:
            j1 = P.dcnt[q]
            for j in range(max(0, j1 - NDS), j1):
                P._wait(e, ("d", q, j % NDS, 16 * (j // NDS + 1)))


def build_L1b(S=8192, fz=None):
    nc = fz["nc"] if fz else bass.Bass("TRN2", target_bir_lowering=False)
    pfx = fz["pfx"] if fz else ""

    def D(name, shape):
        if fz and name in fz["share"]:
            return fz["share"][name]
        return nc.dram_tensor(pfx + name, shape, F32, kind="ExternalInput").ap()
    x_d = D("x", [S, 1024]); npre_d = D("npre", [1024]); wu_d = D("wu", [1024, 256])
    lre_d = D("lre", [16, 64]); lim_d = D("lim", [16, 64]); bre_d = D("bre", [16, 64, 16]); bim_d = D("bim", [16, 64, 16])
    cre_d = D("cre", [16, 16, 64]); cim_d = D("cim", [16, 16, 64]); ldt_d = D("ldt", [16]); dd_d = D("dd", [256])
    taus_d = D("taus", [NTAU]); mk_d = D("mk", [128, 2, 256]); idm_d = D("idm", [128, 2, 256]); ident_d = D("ident", [128, 128])
    ys_d = fz["out"] if fz else nc.dram_tensor("ys", [S, 256], F32, kind="ExternalOutput").ap()
    NCH = S // 16
    NST = S // 512
    with ExitStack() as st:
        C = Ctx(nc, st, fz["P"], pfx) if fz else Ctx(nc, st); P = C.P
        idf, bidf, idb, bidb = make_ident(C, ident_d)
        ptr, bptr = C.ps("ptr", [128, 1024], BF16)
        py, bpy = C.ps("py", [128, 1024])
        G = []; bG = []
        for i in range(4):
            t_, b_ = C.ps("g%d" % i, [128, 512]); G.append(t_); bG.append(b_)
        U, bU = C.sb("U", [128, 2, 16, NCH], BF16)
        with ExitStack() as st2:
            C2 = Ctx(nc, st2, P, C.pfx)
            ext = fz.get("uTp") if fz else None
            if ext:
                uTp, buTp = ext
            else:
                uTp, buTp = C2.sb("uTp", [128, 2, 16, NCH], BF16)
            with ExitStack() as st1:
                C1 = Ctx(nc, st1, P, C.pfx)
                npre, bnpre = bcast_row_load(C1, "npre", npre_d, 1024)
                wu, bwu = load_w_bf16(C1, "wu", wu_d, 8, 256)
                xt, bxt = C1.sb("xt", [128, 1024])
                sq, bsq = C1.sb("sq", [128, 1024])
                hn, bhn = C1.sb("hn", [128, 1024], BF16)
                ss, bss = C1.sb("ss", [128, 1])
                hT, bhT = C1.sb("hT", [128, 8, 512], BF16)
                for s_ in range(0 if ext else NST):
                    for t in range(4):
                        r0 = s_ * 512 + t * 128
                        P.dma("sp", xt[:], x_d[r0:r0 + 128, :], writes=[bxt])
                        rms_rstd(C1, xt[:], bxt, 1024, sq[:], bsq, ss, bss)
                        P.op("dve", lambda E: E.scalar_tensor_tensor(out=hn[:], in0=xt[:], scalar=ss[:, 0:1], in1=npre[:],
                                                                     op0=ALU.mult, op1=ALU.mult), reads=[bxt, bss, bnpre], writes=[bhn])
                        transpose8(C1, hn, bhn, idb, bidb, ptr, bptr, hT[:, :, t * 128:(t + 1) * 128], bhT, eng="act")
                    for blk in range(2):
                        pb = G[blk]
                        fns = [(lambda E, kt=kt, blk=blk, pb=pb: E.matmul(
                            pb[:].rearrange("p (s n) -> p s n", s=16), lhsT=wu[:, kt, blk * 128:(blk + 1) * 128],
                            rhs=hT[:, kt, :].rearrange("p (n s) -> p s n", s=16), start=(kt == 0), stop=(kt == 7))) for kt in range(8)]
                        P.mm_group(fns, reads=[bwu, bhT], writes=[bG[blk]])
                        P.op("act" if blk == 0 else "dve",
                             (lambda E, blk=blk, pb=pb, s_=s_: E.copy(out=uTp[:, blk, :, 32 * s_:32 * s_ + 32], in_=pb[:].rearrange("p (s n) -> p s n", s=16)))
                             if blk == 0 else
                             (lambda E, blk=blk, pb=pb, s_=s_: E.tensor_copy(out=uTp[:, blk, :, 32 * s_:32 * s_ + 32], in_=pb[:].rearrange("p (s n) -> p s n", s=16))),
                             reads=[bG[blk]], writes=[buTp])
                barrier(P)
            ud2 = nc.dram_tensor(pfx + "ud2", [16, 2, 8, 16, NCH], BF16)
            bud2 = Buf("ud2", multi=True)
            bU.multi = True; bU.w = []
            for g in range(16):
                P.dma("sp", ud2.ap()[g].rearrange("k sp h n -> h (k sp) n"),
                      uTp[(g % 8) * 16:(g % 8 + 1) * 16, g // 8, :, :], reads=[buTp], writes=[bud2])
            for g in range(16):
                P.dma("sp", U[:, :, g, :], ud2.ap()[g].rearrange("k sp h n -> (sp h) k n"), reads=[bud2], writes=[bU])
            barrier(P)
        lre, blre = C.sb("lre", [128, 8]); lim, blim = C.sb("lim", [128, 8]); ldt, bldt = C.sb("ldt", [128, 8])
        TAU, bTAU = bcast_row_load(C, "TAU", taus_d, NTAU)
        Er, bEr = C.sb("Er", [128, 8, NTAU]); Ei, bEi = C.sb("Ei", [128, 8, NTAU]); NEi, bNEi = C.sb("NEi", [128, 8, NTAU])
        Hr, bHr = C.sb("Hr", [128, 8, 17, 16]); nHi, bnHi = C.sb("nHi", [128, 8, 17, 16])
        WbT, bWbT = C.sb("WbT", [128, 2, 8, 2, 128], BF16)
        Toep, bToep = C.sb("Toep", [128, 2, 16, 256], BF16)
        with ExitStack() as st3:
            C3 = Ctx(nc, st3, P, C.pfx)
            Br, bBr = C3.sb("Br", [128, 8, 16]); Bi, bBi = C3.sb("Bi", [128, 8, 16])
            Cr, bCr = C3.sb("Cr", [128, 8, 16]); Ci, bCi = C3.sb("Ci", [128, 8, 16])
            dcol, bdcol = C3.sb("dcol", [128, 16])
            MK, bMK = C3.sb("MK", [128, 2, 256]); IDM, bIDM = C3.sb("IDM", [128, 2, 256])
            P.dma("sp", MK[:], mk_d, writes=[bMK]); P.dma("sp", IDM[:], idm_d, writes=[bIDM])
            for _b in (blre, blim, bldt, bBr, bBi, bCr, bCi, bdcol):
                _b.multi = True; _b.w = []
            for two in range(2):
                hs = slice(64 * two, 64 * two + 64)
                P.dma("sp", lre[hs, :], lre_d.rearrange("(gp two) p -> two p gp", two=2)[two], writes=[blre])
                P.dma("sp", lim[hs, :], lim_d.rearrange("(gp two) p -> two p gp", two=2)[two], writes=[blim])
                P.dma("sp", ldt[hs, :], ldt_d.rearrange("(gp two) -> two gp", two=2)[two].partition_broadcast(64), writes=[bldt])
                P.dma("sp", Br[hs], bre_d.rearrange("(gp two) p h -> two p gp h", two=2)[two], writes=[bBr])
                P.dma("sp", Bi[hs], bim_d.rearrange("(gp two) p h -> two p gp h", two=2)[two], writes=[bBi])
                for gp in range(8):
                    P.dma("sp", Cr[hs, gp, :], cre_d[2 * gp + two].rearrange("h p -> p h"), writes=[bCr])
                    P.dma("sp", Ci[hs, gp, :], cim_d[2 * gp + two].rearrange("h p -> p h"), writes=[bCi])
            for sp in range(8):
                P.dma("sp", dcol[sp * 16:(sp + 1) * 16, :], dd_d.rearrange("(g h) -> h g", h=16), writes=[bdcol])
            sm = {}
            for nm in ("dt", "lr", "lrdt", "th", "den", "nr", "fre", "fim", "t8a", "t8b"):
                sm[nm] = C3.sb("sm_" + nm, [128, 8])
            T41 = {}
            for nm in ("ARG", "MARG", "MAG", "MAGN", "SIN", "COS", "ErN", "EiN", "rt", "rk"):
                T41[nm] = C3.sb("t41_" + nm, [128, 8, NTAU])
            rki, brki = C3.sb("rki", [128, 8, NTAU], I32)

            def tt(eng, out, bo, a, ba, b, bb_, op):
                P.op(eng, lambda E: E.tensor_tensor(out=out, in0=a, in1=b, op=op), reads=[ba, bb_], writes=[bo])

            dt, bdt = sm["dt"]; lr, blr = sm["lr"]; lrdt, blrdt = sm["lrdt"]; th, bth = sm["th"]
            P.op("act", lambda E: E.activation(out=dt[:], in_=ldt[:], func=AF.Exp), reads=[bldt], writes=[bdt])
            P.op("dve", lambda E: E.tensor_scalar(out=lr[:], in0=lre[:], scalar1=-1e-4, scalar2=None, op0=ALU.min), reads=[blre], writes=[blr])
            tt("dve", lrdt[:], blrdt, lr[:], blr, dt[:], bdt, ALU.mult)
            tt("dve", th[:], bth, lim[:], blim, dt[:], bdt, ALU.mult)
            ARG, bARG = T41["ARG"]; MARG, bMARG = T41["MARG"]; MAG, bMAG = T41["MAG"]; MAGN, bMAGN = T41["MAGN"]
            SIN, bSIN = T41["SIN"]; COS, bCOS = T41["COS"]; ErN, bErN = T41["ErN"]; EiN, bEiN = T41["EiN"]
            rt, brt = T41["rt"]; rk, brk = T41["rk"]
            tb = TAU[:].unsqueeze(1).to_broadcast([128, 8, NTAU])
            tt("dve", ARG[:], bARG, th[:].unsqueeze(2).to_broadcast([128, 8, NTAU]), bth, tb, bTAU, ALU.mult)
            tt("dve", MARG[:], bMARG, lrdt[:].unsqueeze(2).to_broadcast([128, 8, NTAU]), blrdt, tb, bTAU, ALU.mult)
            P.op("act", lambda E: E.activation(out=MAG[:], in_=MARG[:], func=AF.Exp), reads=[bMARG], writes=[bMAG])
            P.op("act", lambda E: E.activation(out=MAGN[:, :, 0:17], in_=MARG[:, :, 0:17], func=AF.Exp, scale=-1.0), reads=[bMARG], writes=[bMAGN])

            def sin_of(dst, bdst, shift):
                P.op("dve", lambda E: E.tensor_scalar(out=rt[:], in0=ARG[:], scalar1=float(shift), scalar2=None, op0=ALU.add), reads=[bARG], writes=[brt])
                P.op("dve", lambda E: E.tensor_scalar(out=rki[:], in0=rt[:], scalar1=float(1.0 / (2 * np.pi)), scalar2=None, op0=ALU.mult), reads=[brt], writes=[brki])
                P.op("dve", lambda E: E.tensor_copy(out=rk[:], in_=rki[:]), reads=[brki], writes=[brk])
                P.op("dve", lambda E: E.scalar_tensor_tensor(out=rt[:], in0=rk[:], scalar=float(-2 * np.pi), in1=rt[:], op0=ALU.mult, op1=ALU.add),
                     reads=[brk, brt], writes=[brt])
                P.op("dve", lambda E: E.tensor_scalar(out=rt[:], in0=rt[:], scalar1=-3.14159, scalar2=3.14159, op0=ALU.max, op1=ALU.min), reads=[brt], writes=[brt])
                P.op("act", lambda E: E.activation(out=dst[:], in_=rt[:], func=AF.Sin), reads=[brt], writes=[bdst])

            sin_of(SIN, bSIN, 0.0)
            sin_of(COS, bCOS, np.pi / 2)
            tt("dve", Er[:], bEr, MAG[:], bMAG, COS[:], bCOS, ALU.mult)
            tt("dve", Ei[:], bEi, MAG[:], bMAG, SIN[:], bSIN, ALU.mult)
            P.op("dve", lambda E: E.tensor_scalar(out=NEi[:], in0=Ei[:], scalar1=-1.0, scalar2=None, op0=ALU.mult), reads=[bEi], writes=[bNEi])
            tt("dve", ErN[:, :, 0:17], bErN, MAGN[:, :, 0:17], bMAGN, COS[:, :, 0:17], bCOS, ALU.mult)
            tt("dve", EiN[:, :, 0:17], bEiN, MAGN[:, :, 0:17], bMAGN, SIN[:, :, 0:17], bSIN, ALU.mult)
            P.op("dve", lambda E: E.tensor_scalar(out=EiN[:, :, 0:17], in0=EiN[:, :, 0:17], scalar1=-1.0, scalar2=None, op0=ALU.mult), reads=[bEiN], writes=[bEiN])
            den, bden = sm["den"]; nr, bnr = sm["nr"]; fre, bfre = sm["fre"]; fim, bfim = sm["fim"]; t8a, bt8a = sm["t8a"]; t8b, bt8b = sm["t8b"]
            tt("dve", den[:], bden, lr[:], blr, lr[:], blr, ALU.mult)
            tt("dve", t8a[:], bt8a, lim[:], blim, lim[:], blim, ALU.mult)
            tt("dve", den[:], bden, den[:], bden, t8a[:], bt8a, ALU.add)
            P.op("dve", lambda E: E.reciprocal(out=den[:], in_=den[:]), reads=[bden], writes=[bden])
            P.op("dve", lambda E: E.tensor_scalar(out=nr[:], in0=Er[:, :, 1], scalar1=-1.0, scalar2=None, op0=ALU.add), reads=[bEr], writes=[bnr])
            tt("dve", fre[:], bfre, nr[:], bnr, lr[:], blr, ALU.mult)
            tt("dve", t8a[:], bt8a, Ei[:, :, 1], bEi, lim[:], blim, ALU.mult)
            tt("dve", fre[:], bfre, fre[:], bfre, t8a[:], bt8a, ALU.add)
            tt("dve", fre[:], bfre, fre[:], bfre, den[:], bden, ALU.mult)
            tt("dve", fim[:], bfim, Ei[:, :, 1], bEi, lr[:], blr, ALU.mult)
            tt("dve", t8b[:], bt8b, nr[:], bnr, lim[:], blim, ALU.mult)
            tt("dve", fim[:], bfim, fim[:], bfim, t8b[:], bt8b, ALU.subtract)
            tt("dve", fim[:], bfim, fim[:], bfim, den[:], bden, ALU.mult)

            def cmul(outr, boutr, outi, bouti, ar, bar, ai, bai, br_, bbr_, bi_, bbi_, tmp, btmp):
                tt("dve", outr, boutr, ar, bar, br_, bbr_, ALU.mult)
                tt("dve", tmp, btmp, ai, bai, bi_, bbi_, ALU.mult)
                tt("dve", outr, boutr, outr, boutr, tmp, btmp, ALU.subtract)
                tt("dve", outi, bouti, ar, bar, bi_, bbi_, ALU.mult)
                tt("dve", tmp, btmp, ai, bai, br_, bbr_, ALU.mult)
                tt("dve", outi, bouti, outi, bouti, tmp, btmp, ALU.add)

            bbr, bbbr = C3.sb("bbr", [128, 8, 16]); bbi, bbbi = C3.sb("bbi", [128, 8, 16]); tmp16, btmp16 = C3.sb("tmp16", [128, 8, 16])
            fb = lambda t_: t_[:].unsqueeze(2).to_broadcast([128, 8, 16])
            cmul(bbr[:], bbbr, bbi[:], bbbi, fb(fre), bfre, fb(fim), bfim, Br[:], bBr, Bi[:], bBi, tmp16[:], btmp16)
            Gr, bGr = C3.sb("Gr", [128, 8, 16, 16]); Gi, bGi = C3.sb("Gi", [128, 8, 16, 16])
            WPr, bWPr = C3.sb("WPr", [128, 8, 16, 16]); WPi, bWPi = C3.sb("WPi", [128, 8, 16, 16])
            Hi, bHi = C3.sb("Hi", [128, 8, 17, 16]); tmpH, btmpH = C3.sb("tmpH", [128, 8, 17, 16])
            eb = lambda t_, j0, j1: t_[:, :, j0:j1].unsqueeze(3).to_broadcast([128, 8, j1 - j0, 16])
            vb = lambda t_, n_: t_[:].unsqueeze(2).to_broadcast([128, 8, n_, 16])
            cmul(Gr[:], bGr, Gi[:], bGi, eb(ErN, 0, 16), bErN, eb(EiN, 0, 16), bEiN, vb(bbr, 16), bbbr, vb(bbi, 16), bbbi, tmpH[:, :, 0:16, :], btmpH)
            cmul(WPr[:], bWPr, WPi[:], bWPi, eb(Er, 25, 41), bEr, eb(Ei, 25, 41), bEi, vb(bbr, 16), bbbr, vb(bbi, 16), bbbi, tmpH[:, :, 0:16, :], btmpH)
            cmul(Hr[:], bHr, Hi[:], bHi, eb(Er, 0, 17), bEr, eb(Ei, 0, 17), bEi, vb(Cr, 17), bCr, vb(Ci, 17), bCi, tmpH[:], btmpH)
            P.op("dve", lambda E: E.tensor_scalar(out=nHi[:], in0=Hi[:], scalar1=-1.0, scalar2=None, op0=ALU.mult), reads=[bHi], writes=[bnHi])
            for gp in range(8):
                for kt2 in range(2):
                    for c, (WP_, bWP_) in enumerate(((WPr, bWPr), (WPi, bWPi))):
                        P.op("pe", lambda E, gp=gp, kt2=kt2, WP_=WP_: E.transpose(
                            out=G[2][:, 0:128], in_=WP_[:, gp, kt2 * 8:(kt2 + 1) * 8, :].rearrange("p s h -> p (s h)"), identity=idf[:]),
                            reads=[bWP_, bidf], writes=[bG[2]])
                        P.op("act", lambda E, gp=gp, kt2=kt2, c=c: E.copy(out=WbT[:, kt2, gp, c, :], in_=G[2][:, 0:128]), reads=[bG[2]], writes=[bWbT])
            tmpT, btmpT = C3.sb("tmpT", [128, 256])
            for g in range(16):
                gp = g // 2; hs = slice(64 * (g % 2), 64 * (g % 2) + 64)
                for kt2 in range(2):
                    fns = [
                        lambda E, gp=gp, hs=hs, kt2=kt2: E.matmul(G[3][:, 0:256], lhsT=Gr[hs, gp, kt2 * 8:(kt2 + 1) * 8, :].rearrange("p s h -> p (s h)"),
                                                                  rhs=Hr[hs, gp, 0:16, :].rearrange("p t h -> p (t h)"), start=True, stop=False),
                        lambda E, gp=gp, hs=hs, kt2=kt2: E.matmul(G[3][:, 0:256], lhsT=Gi[hs, gp, kt2 * 8:(kt2 + 1) * 8, :].rearrange("p s h -> p (s h)"),
                                                                  rhs=nHi[hs, gp, 0:16, :].rearrange("p t h -> p (t h)"), start=False, stop=True)]
                    P.mm_group(fns, reads=[bGr, bGi, bHr, bnHi], writes=[bG[3]])
                    P.op("dve", lambda E, kt2=kt2: E.tensor_tensor(out=tmpT[:], in0=G[3][:, 0:256], in1=MK[:, kt2, :], op=ALU.mult),
                         reads=[bG[3], bMK], writes=[btmpT])
                    P.op("dve", lambda E, kt2=kt2, g=g: E.scalar_tensor_tensor(out=Toep[:, kt2, g, :], in0=IDM[:, kt2, :], scalar=dcol[:, g:g + 1], in1=tmpT[:],
                                                                               op0=ALU.mult, op1=ALU.add), reads=[bIDM, bdcol, btmpT], writes=[bToep])
            barrier(P)
        X = {}
        for bufn in ("A", "B"):
            for c in ("re", "im"):
                X[(bufn, c)] = (C.sb("X%s%s" % (bufn, c), [128, 8, NCH + 1])[0], [Buf("X%s%s%d" % (bufn, c, gp)) for gp in range(8)])
        Ysb, bYsb = C.sb("Ysb", [128, 16, 256], BF16 if (fz and fz.get("ybf16")) else F32)
        for key in X:
            t_, bl = X[key]
            P.op("dve", lambda E, t_=t_: E.memset(t_[:, :, 0:1], 0.0), writes=bl)
        for gp in range(8):
            for c, cn in enumerate(("re", "im")):
                px = G[c]
                fns = []
                for two in range(2):
                    g = 2 * gp + two
                    for kt2 in range(2):
                        fns.append(lambda E, two=two, g=g, kt2=kt2, gp=gp, c=c, px=px: E.matmul(
                            px[64 * two:64 * two + 64, :], lhsT=WbT[:, kt2, gp, c, 64 * two:64 * two + 64], rhs=U[:, kt2, g, :],
                            start=(kt2 == 0), stop=(kt2 == 1)))
                P.mm_group(fns, reads=[bWbT, bU], writes=[bG[c]])
                xt_, xb_ = X[("A", cn)]
                P.op("act", lambda E, xt_=xt_, gp=gp, px=px: E.copy(out=xt_[:, gp, 1:NCH + 1], in_=px[:]), reads=[bG[c]], writes=[xb_[gp]])
        for k in range(9):
            d = 1 << k
            j = 16 if k == 0 else 16 + k
            src, dst = ("A", "B") if k % 2 == 0 else ("B", "A")
            sre, bsre = X[(src, "re")]; sim, bsim = X[(src, "im")]
            dre, bdre = X[(dst, "re")]; dim_, bdim = X[(dst, "im")]
            P.op("dve", lambda E, dre=dre, sre=sre, d=d: E.tensor_copy(out=dre[:, :, 1:1 + d], in_=sre[:, :, 1:1 + d]), reads=bsre, writes=bdre)
            P.op("pool", lambda E, dim_=dim_, sim=sim, d=d: E.tensor_copy(out=dim_[:, :, 1:1 + d], in_=sim[:, :, 1:1 + d]), reads=bsim, writes=bdim)
            for gp in range(8):
                lo = slice(1, NCH + 1 - d); hi = slice(1 + d, NCH + 1)
                P.op("dve", lambda E, gp=gp, j=j, dre=dre, sre=sre, lo=lo, hi=hi: E.scalar_tensor_tensor(
                    out=dre[:, gp, hi], in0=sre[:, gp, lo], scalar=Er[:, gp, j:j + 1], in1=sre[:, gp, hi], op0=ALU.mult, op1=ALU.add),
                    reads=[bsre[gp], bEr], writes=[bdre[gp]])
                P.op("dve", lambda E, gp=gp, j=j, dre=dre, sim=sim, lo=lo, hi=hi: E.scalar_tensor_tensor(
                    out=dre[:, gp, hi], in0=sim[:, gp, lo], scalar=NEi[:, gp, j:j + 1], in1=dre[:, gp, hi], op0=ALU.mult, op1=ALU.add),
                    reads=[bsim[gp], bNEi, bdre[gp]], writes=[bdre[gp]])
                P.op("dve", lambda E, gp=gp, j=j, dim_=dim_, sim=sim, lo=lo, hi=hi: E.scalar_tensor_tensor(
                    out=dim_[:, gp, hi], in0=sim[:, gp, lo], scalar=Er[:, gp, j:j + 1], in1=sim[:, gp, hi], op0=ALU.mult, op1=ALU.add),
                    reads=[bsim[gp], bEr], writes=[bdim[gp]])
                P.op("dve", lambda E, gp=gp, j=j, dim_=dim_, sre=sre, lo=lo, hi=hi: E.scalar_tensor_tensor(
                    out=dim_[:, gp, hi], in0=sre[:, gp, lo], scalar=Ei[:, gp, j:j + 1], in1=dim_[:, gp, hi], op0=ALU.mult, op1=ALU.add),
                    reads=[bsre[gp], bEi, bdim[gp]], writes=[bdim[gp]])
        fre_, bfre_ = X[("B", "re")]; fim_, bfim_ = X[("B", "im")]
        bys = None if fz else Buf("ys", multi=True)
        ysv = ys_d.rearrange("(n t) c -> n t c", t=16)
        for jt in range(NCH // 128):
            for gq in range(4):
                fns = []
                for gi in range(4):
                    g = 4 * gq + gi; gp = g // 2; hs = slice(64 * (g % 2), 64 * (g % 2) + 64)
                    o_ = (gi * 256, (gi + 1) * 256)
                    for kt2 in range(2):
                        fns.append(lambda E, o_=o_, g=g, kt2=kt2, jt=jt: E.matmul(
                            py[:, o_[0]:o_[1]], lhsT=U[:, kt2, g, jt * 128:(jt + 1) * 128], rhs=Toep[:, kt2, g, :], start=(kt2 == 0), stop=False))
                    fns.append(lambda E, o_=o_, gp=gp, hs=hs, jt=jt: E.matmul(
                        py[:, o_[0]:o_[1]], lhsT=fre_[hs, gp, jt * 128:(jt + 1) * 128], rhs=Hr[hs, gp, 1:17, :].rearrange("p t h -> p (t h)"),
                        start=False, stop=False))
                    fns.append(lambda E, o_=o_, gp=gp, hs=hs, jt=jt: E.matmul(
                        py[:, o_[0]:o_[1]], lhsT=fim_[hs, gp, jt * 128:(jt + 1) * 128], rhs=nHi[hs, gp, 1:17, :].rearrange("p t h -> p (t h)"),
                        start=False, stop=True))
                P.mm_group(fns, reads=[bU, bToep, bHr, bnHi] + bfre_ + bfim_, writes=[bpy])
                P.op("act" if gq % 2 == 0 else "dve",
                     (lambda E, gq=gq: E.copy(out=Ysb[:].rearrange("p t (g h) -> p g t h", h=16)[:, 4 * gq:4 * gq + 4],
                                              in_=py[:].rearrange("p (g t h) -> p g t h", g=4, h=16)))
                     if gq % 2 == 0 else
                     (lambda E, gq=gq: E.tensor_copy(out=Ysb[:].rearrange("p t (g h) -> p g t h", h=16)[:, 4 * gq:4 * gq + 4],
                                                     in_=py[:].rearrange("p (g t h) -> p g t h", g=4, h=16))),
                     reads=[bpy], writes=[bYsb])
            P.dma("sp", ysv[jt * 128:(jt + 1) * 128, :, :], Ysb[:], reads=[bYsb], writes=[fz["obuf_of"](jt) if fz else bys])
            if fz:
                fz["after_chunk"](jt)
        if fz:
            barrier(P)
        else:
            P.finish([bys])
    return nc


def run_L1b(inp):
    nc = _get("L1b", build_L1b)
    mk, idm = _s5_consts()
    w_in = inp["w_in_even"][0]
    maps = []
    for c in range(8):
        b, r = divmod(c, 4)
        gs = slice(16 * r, 16 * r + 16)
        maps.append({"x": np.ascontiguousarray(inp["x"][b]), "npre": np.ascontiguousarray(inp["norm_pre"][0]),
                     "wu": np.ascontiguousarray(w_in[:, 4112 + 256 * r:4112 + 256 * (r + 1)]),
                     "lre": np.ascontiguousarray(inp["s5_lam_re"][0, gs]), "lim": np.ascontiguousarray(inp["s5_lam_im"][0, gs]),
                     "bre": np.ascontiguousarray(inp["s5_b_re"][0, gs]), "bim": np.ascontiguousarray(inp["s5_b_im"][0, gs]),
                     "cre": np.ascontiguousarray(inp["s5_c_re"][0, gs]), "cim": np.ascontiguousarray(inp["s5_c_im"][0, gs]),
                     "ldt": np.ascontiguousarray(inp["s5_log_dt"][0, gs]), "dd": np.ascontiguousarray(inp["s5_d"][0, 256 * r:256 * (r + 1)]),
                     "taus": TAUS, "mk": mk, "idm": idm, "ident": _IDENT})
    res = run_bass_kernel_spmd(nc, maps, core_ids=list(range(8)))
    ys = np.empty((2, 8192, 1024), np.float32)
    for c in range(8):
        b, r = divmod(c, 4)
        ys[b, :, 256 * r:256 * (r + 1)] = res.results[c]["ys"]
    return ys


def _gdn_consts():
    p = np.arange(64)[:, None]; f = np.arange(64)[None, :]
    negu = np.where(f >= p, 0.0, -30000.0)
    negls = np.where(f < p, 0.0, -30000.0)
    nsu = np.where(f > p, -1.0, 0.0)
    i64 = np.eye(64)
    c64 = np.stack([negu, negls, nsu, i64], axis=1).astype(np.float32)
    cmask = np.ones((2, 512), np.float32); cmask[:, 0::64] = 0.0
    sel = np.zeros((2, 2, 128), np.float32); sel[0, 0, :] = 1.0; sel[1, 1, :] = 1.0
    return c64, cmask, sel


def build_L1a(S=8192, fz=None):
    nc = fz["nc"] if fz else bass.Bass("TRN2", target_bir_lowering=False)
    pfx = fz["pfx"] if fz else ""

    def D(name, shape):
        if fz and name in fz["share"]:
            return fz["share"][name]
        return nc.dram_tensor(pfx + name, shape, F32, kind="ExternalInput").ap()
    x_d = D("x", [S, 1024]); npre_d = D("npre", [1024]); w_d = D("w", [1024, 768]); wb_d = D("wb", [1024, 2]); wa_d = D("wa", [1024, 2])
    conv_d = D("conv", [4, 768]); alog_d = D("alog", [2]); dtb_d = D("dtb", [2])
    ident_d = D("ident", [128, 128]); c64_d = D("c64", [64, 4, 64]); cmask_d = D("cmask", [2, 512]); sel_d = D("sel", [2, 2, 128])
    ones_d = D("ones", [128, 128])
    o_d = fz["out"] if fz else nc.dram_tensor("o", [S, 256], F32, kind="ExternalOutput").ap()
    NST = S // 512
    with ExitStack() as st:
        C = Ctx(nc, st, fz["P"], pfx) if fz else Ctx(nc, st); P = C.P
        idf, bidf, idb, bidb = make_ident(C, ident_d)
        npre, bnpre = bcast_row_load(C, "npre", npre_d, 1024)
        w, bw = load_w_bf16(C, "w", w_d, 8, 768)
        wb, bwb = load_w_bf16(C, "wb", wb_d, 8, 2)
        wa, bwa = load_w_bf16(C, "wa", wa_d, 8, 2)
        cw, bcw = C.sb("cw", [128, 4, 6])
        P.dma("sp", cw[:], conv_d.rearrange("j (c p) -> p j c", p=128), writes=[bcw])
        extu = fz.get("uTp") if fz else None
        if extu:
            wu_d = D("wu", [1024, 256])
            wu, bwu = load_w_bf16(C, "wu", wu_d, 8, 256)
            uTp, buTp = extu
        c64, bc64 = C.sb("c64", [64, 4, 64]); P.dma("sp", c64[:], c64_d, writes=[bc64])
        NEGU = c64[:, 0, :]; NEGLS = c64[:, 1, :]; NSU = c64[:, 2, :]; I64 = c64[:, 3, :]
        cmask, bcmask = C.sb("cmask", [2, 512]); P.dma("sp", cmask[:], cmask_d, writes=[bcmask])
        sel, bsel = C.sb("sel", [2, 2, 128]); P.dma("sp", sel[:], sel_d, writes=[bsel])
        ones, bones = C.sb("ones", [128, 128]); P.dma("sp", ones[:], ones_d, writes=[bones])
        onesb, bonesb = C.sb("onesb", [128, 128], BF16)
        P.op("dve", lambda E: E.tensor_copy(out=onesb[:], in_=ones[:]), reads=[bones], writes=[bonesb])
        sqb, bsqb = C.sb("sqb", [128, 512], BF16)
        alog, balog = C.sb("alog", [2, 1]); P.dma("sp", alog[:], alog_d.rearrange("(a b) -> a b", b=1), writes=[balog])
        dtb, bdtb = C.sb("dtb", [2, 1]); P.dma("sp", dtb[:], dtb_d.rearrange("(a b) -> a b", b=1), writes=[bdtb])
        negA, bnegA = C.sb("negA", [2, 1])
        P.op("act", lambda E: E.activation(out=negA[:], in_=alog[:], func=AF.Exp), reads=[balog], writes=[bnegA])
        P.op("dve", lambda E: E.tensor_scalar(out=negA[:], in0=negA[:], scalar1=-1.0, scalar2=None, op0=ALU.mult), reads=[bnegA], writes=[bnegA])
        xt, bxt = C.sb("xt", [128, 1024]); sq, bsq = C.sb("sq", [128, 1024], BF16); hn, bhn = C.sb("hn", [128, 1024], BF16)
        ss, bss = C.sb("ss", [128, 1]); hT, bhT = C.sb("hT", [128, 8, 512], BF16)
        raw, _ = C.sb("raw", [128, 6, 515]); braw = [Buf("raw%d" % i) for i in range(6)]
        cvq, bcvq = C.sb("cvq", [128, 512])
        act, _ = C.sb("act", [128, 4, 512]); bact = [Buf("act%d" % i) for i in range(4)]
        vbuf2 = []; qk2 = []; bqk2 = []
        for par_ in range(2):
            vt_, _ = C.sb("vbuf%d" % par_, [128, 2, 512]); vbuf2.append((vt_, [Buf("vb%d_%d" % (par_, i)) for i in range(2)]))
            qt_, _ = C.sb("qk%d" % par_, [128, 4, 512]); qk2.append(qt_); bqk2.append([Buf("qk%d_%d" % (par_, i)) for i in range(4)])
        rn, brn = C.sb("rn", [128, 512])
        brow, bbrow = C.sb("brow", [2, 512]); grow, bgrow = C.sb("grow", [2, 512]); gcrow, bgcrow = C.sb("gcrow", [2, 512])
        GCB2 = []; BB2 = []
        for par_ in range(2):
            GCB2.append([C.sb("GCB%d_%d" % (par_, h), [128, 512]) for h in range(2)])
            BB2.append([C.sb("BB%d_%d" % (par_, h), [128, 512]) for h in range(2)])
        m64h = []; smallh = []
        for h in range(2):
            d_ = {}
            for nm in ("arg1", "scr"):
                d_[nm] = C.sb("m%d_%s" % (h, nm), [64, 512])
            for nm in ("DT", "Ds", "tmp", "Pa", "Pb", "Qa", "Qb"):
                d_[nm] = C.sb("m%d_%s" % (h, nm), [64, 512], BF16)
            m64h.append(d_)
        heads = []
        for h in range(2):
            H = {}
            H["attnT"] = C.sb("attnT%d" % h, [64, 512], BF16); H["Y"] = C.sb("Y%d" % h, [64, 512]); H["Ybf"] = C.sb("Ybf%d" % h, [64, 512], BF16)
            H["EG"] = C.sb("EG%d" % h, [128, 512]); H["qdec"] = C.sb("qdec%d" % h, [128, 512], BF16)
            H["kTb"] = C.sb("kTb%d" % h, [128, 512], BF16); H["Sbf"] = C.sb("Sbf%d" % h, [128, 128], BF16)
            H["bv"] = C.sb("bv%d" % h, [64, 8, 128]); H["kdec"] = C.sb("kdec%d" % h, [64, 8, 128], BF16)
            H["nbg"] = C.sb("nbg%d" % h, [64, 8]); H["osb"] = C.sb("osb%d" % h, [128, 8, 128])
            H["vnew"] = C.sb("vnew%d" % h, [64, 128], BF16); H["rhs2"] = C.sb("rhs2%d" % h, [64, 128], BF16)
            heads.append(H)
        for h in range(2):
            d_ = {}
            for nm in ("gccol", "bcol", "nbcol", "elast", "egc"):
                d_[nm] = C.sb("s%d_%s" % (h, nm), [64, 8])
            smallh.append(d_)
        Sst = [C.sb("S%d" % h, [128, 128]) for h in range(2)]
        for h in range(2):
            P.op("dve", lambda E, h=h: E.memset(Sst[h][0][:], 0.0), writes=[Sst[h][1]])
            P.op("dve", lambda E, h=h: E.memset(heads[h]["Sbf"][0][:], 0.0), writes=[heads[h]["Sbf"][1]])
        P.op("dve", lambda E: E.memset(raw[:, :, 0:3], 0.0), writes=braw)
        ptr, bptr = C.ps("ptr", [128, 1024], BF16)
        G = [C.ps("gp%d" % i, [128, 512]) for i in range(7)]
        GP = G[0:4]
        GA = G[4:7]
        BKS = [(GP[0], GP[1], GP[2]), (GP[3], GA[0], GA[1])]
        ga_ctr = [0]

        def next_ga():
            ga_ctr[0] += 1
            return GA[ga_ctr[0] % 3]
        bo = None if fz else Buf("o", multi=True)
        if fz is not None and fz.get("debug"):
            print("L1a sbuf remaining", nc.sbuf_bytes_remaining)

        def tt(out, bo_, a, ba, b, bb_, op, eng="dve"):
            P.op(eng, lambda E: E.tensor_tensor(out=out, in0=a, in1=b, op=op), reads=ba if isinstance(ba, list) else [ba], writes=[bo_])

        def stageA(s_):
            par = s_ % 2
            qk = qk2[par]; bqk = bqk2[par]; GCB = GCB2[par]; BB = BB2[par]; vb, bvb = vbuf2[par]
            for t in range(4):
                r0 = s_ * 512 + t * 128
                P.dma("sp", xt[:], x_d[r0:r0 + 128, :], writes=[bxt])
                rms_rstd(C, xt[:], bxt, 1024, sq[:], bsq, ss, bss)
                P.op("dve", lambda E: E.scalar_tensor_tensor(out=hn[:], in0=xt[:], scalar=ss[:, 0:1], in1=npre[:],
                                                             op0=ALU.mult, op1=ALU.mult), reads=[bxt, bss, bnpre], writes=[bhn])
                transpose8(C, hn, bhn, idb, bidb, ptr, bptr, hT[:, :, t * 128:(t + 1) * 128], bhT, eng="act")
                yield
            for ct in range(6):
                pa, bpa = next_ga()
                fns = [(lambda E, kt=kt, ct=ct, pa=pa: E.matmul(pa[:], lhsT=w[:, kt, ct * 128:(ct + 1) * 128], rhs=hT[:, kt, :],
                                                                start=(kt == 0), stop=(kt == 7))) for kt in range(8)]
                P.mm_group(fns, reads=[bw, bhT], writes=[bpa])
                P.op("act", lambda E, ct=ct, pa=pa: E.copy(out=raw[:, ct, 3:515], in_=pa[:]), reads=[bpa], writes=[braw[ct]])
                P.op("dve", lambda E, ct=ct: E.tensor_scalar(out=cvq[:], in0=raw[:, ct, 0:512], scalar1=cw[:, 0, ct:ct + 1], scalar2=None, op0=ALU.mult),
                     reads=[braw[ct], bcw], writes=[bcvq])
                for j in range(1, 4):
                    P.op("dve", lambda E, ct=ct, j=j: E.scalar_tensor_tensor(out=cvq[:], in0=raw[:, ct, j:j + 512], scalar=cw[:, j, ct:ct + 1], in1=cvq[:],
                                                                             op0=ALU.mult, op1=ALU.add), reads=[braw[ct], bcw, bcvq], writes=[bcvq])
                P.op("act", lambda E, ct=ct: E.copy(out=raw[:, ct, 0:3], in_=raw[:, ct, 512:515]), reads=[braw[ct]], writes=[braw[ct]])
                if ct < 4:
                    P.op("act", lambda E, ct=ct: E.activation(out=act[:, ct, :], in_=cvq[:], func=AF.Silu), reads=[bcvq], writes=[bact[ct]])
                else:
                    P.op("act", lambda E, ct=ct, vb=vb: E.activation(out=vb[:, ct - 4, :], in_=cvq[:], func=AF.Silu), reads=[bcvq], writes=[bvb[ct - 4]])
                yield
            if extu:
                for blk in range(2):
                    pa, bpa = next_ga()
                    fns = [(lambda E, kt=kt, blk=blk, pa=pa: E.matmul(
                        pa[:].rearrange("p (s n) -> p s n", s=16), lhsT=wu[:, kt, blk * 128:(blk + 1) * 128],
                        rhs=hT[:, kt, :].rearrange("p (n s) -> p s n", s=16), start=(kt == 0), stop=(kt == 7))) for kt in range(8)]
                    P.mm_group(fns, reads=[bwu, bhT], writes=[bpa])
                    P.op("act", lambda E, blk=blk, pa=pa, s_=s_: E.copy(out=uTp[:, blk, :, 32 * s_:32 * s_ + 32], in_=pa[:].rearrange("p (s n) -> p s n", s=16)),
                         reads=[bpa], writes=[buTp])
                    yield
            for ct in range(4):
                pa, bpa = next_ga()
                P.op("act", lambda E, ct=ct: E.activation(out=sqb[:], in_=act[:, ct, :], func=AF.Square), reads=[bact[ct]], writes=[bsqb])
                P.op("pe", lambda E, pa=pa: E.matmul(pa[:], lhsT=onesb[:], rhs=sqb[:], start=True, stop=True), reads=[bonesb, bsqb], writes=[bpa])
                P.op("act", lambda E, pa=pa: E.activation(out=rn[:], in_=pa[:], func=AF.Ln, bias=1e-6, scale=1.0), reads=[bpa], writes=[brn])
                P.op("act", lambda E: E.activation(out=rn[:], in_=rn[:], func=AF.Exp, scale=-0.5), reads=[brn], writes=[brn])
                if ct < 2:
                    P.op("dve", lambda E, ct=ct, qk=qk: E.scalar_tensor_tensor(out=qk[:, ct, :], in0=act[:, ct, :], scalar=float(128 ** -0.5), in1=rn[:],
                                                                               op0=ALU.mult, op1=ALU.mult), reads=[bact[ct], brn], writes=[bqk[ct]])
                else:
                    P.op("dve", lambda E, ct=ct, qk=qk: E.tensor_tensor(out=qk[:, ct, :], in0=act[:, ct, :], in1=rn[:], op=ALU.mult),
                         reads=[bact[ct], brn], writes=[bqk[ct]])
                yield
            pa, bpa = next_ga()
            fns = [(lambda E, kt=kt, pa=pa: E.matmul(pa[0:2, :], lhsT=wb[:, kt, 0:2], rhs=hT[:, kt, :], start=(kt == 0), stop=(kt == 7))) for kt in range(8)]
            P.mm_group(fns, reads=[bwb, bhT], writes=[bpa])
            P.op("act", lambda E, pa=pa: E.activation(out=brow[:], in_=pa[0:2, :], func=AF.Sigmoid), reads=[bpa], writes=[bbrow])
            pa2, bpa2 = next_ga()
            fns = [(lambda E, kt=kt, pa2=pa2: E.matmul(pa2[0:2, :], lhsT=wa[:, kt, 0:2], rhs=hT[:, kt, :], start=(kt == 0), stop=(kt == 7))) for kt in range(8)]
            P.mm_group(fns, reads=[bwa, bhT], writes=[bpa2])
            P.op("act", lambda E, pa2=pa2: E.activation(out=grow[:], in_=pa2[0:2, :], func=AF.Exp, bias=dtb[:, 0:1], scale=1.0), reads=[bpa2, bdtb], writes=[bgrow])
            P.op("act", lambda E: E.activation(out=grow[:], in_=grow[:], func=AF.Ln, bias=1.0, scale=1.0), reads=[bgrow], writes=[bgrow])
            P.op("dve", lambda E: E.tensor_scalar(out=grow[:], in0=grow[:], scalar1=negA[:, 0:1], scalar2=None, op0=ALU.mult), reads=[bgrow, bnegA], writes=[bgrow])
            P.op("dve", lambda E: E.tensor_tensor_scan(out=gcrow[:], data0=cmask[:], data1=grow[:], initial=0.0, op0=ALU.mult, op1=ALU.add),
                 reads=[bcmask, bgrow], writes=[bgcrow])
            yield
            for h in range(2):
                pa, bpa = next_ga()
                P.op("pe", lambda E, h=h, pa=pa: E.matmul(pa[:], lhsT=sel[:, h, :], rhs=gcrow[:], start=True, stop=True), reads=[bsel, bgcrow], writes=[bpa])
                P.op("act", lambda E, h=h, pa=pa, GCB=GCB: E.copy(out=GCB[h][0][:], in_=pa[:]), reads=[bpa], writes=[GCB[h][1]])
                pa, bpa = next_ga()
                P.op("pe", lambda E, h=h, pa=pa: E.matmul(pa[:], lhsT=sel[:, h, :], rhs=brow[:], start=True, stop=True), reads=[bsel, bbrow], writes=[bpa])
                P.op("act", lambda E, h=h, pa=pa, BB=BB: E.copy(out=BB[h][0][:], in_=pa[:]), reads=[bpa], writes=[BB[h][1]])
                yield

        for _ in stageA(0):
            pass
        for s_ in range(NST):
            par = s_ % 2
            qk = qk2[par]; bqk = bqk2[par]; GCB = GCB2[par]; BB = BB2[par]; vb, bvb = vbuf2[par]
            nxt = stageA(s_ + 1) if s_ + 1 < NST else None

            def advance(k):
                if nxt is not None:
                    for _ in range(k):
                        next(nxt, None)
            def stageB(h, qk=qk, bqk=bqk, GCB=GCB, BB=BB, vb=vb, bvb=bvb):
                m64 = m64h[h]; small = smallh[h]; BK = BKS[h]
                qT = qk[:, h, :]; bqT = bqk[h]; kT = qk[:, 2 + h, :]; bkT = bqk[2 + h]; vT = vb[:, h, :]; bvT = bvb[h]
                gcb, bgcb = GCB[h]; bb, bbb = BB[h]
                H = heads[h]
                attnT, battnT = H["attnT"]; Y, bY = H["Y"]; EG, bEG = H["EG"]; qdec, bqdec = H["qdec"]
                Ybf, bYbf = H["Ybf"]
                bv, bbv = H["bv"]; kdec, bkdec = H["kdec"]; nbg, bnbg = H["nbg"]
                arg1, barg1 = m64["arg1"]; scr, bscr = m64["scr"]; DT, bDT = m64["DT"]; Ds, bDs = m64["Ds"]
                tmp, btmp = m64["tmp"]
                gccol, bgccol = small["gccol"]; bcol, bbcol = small["bcol"]; nbcol, bnbcol = small["nbcol"]
                elast, belast = small["elast"]; egc, begc = small["egc"]
                v3 = lambda t_: t_[:].rearrange("p (n f) -> p n f", f=64)
                i64b = I64.unsqueeze(1).to_broadcast([64, 8, 64])
                tt(v3(scr), bscr, gcb[0:64, :].rearrange("p (n f) -> p n f", f=64), [bgcb, bc64], i64b, bc64, ALU.mult)
                P.op("dve", lambda E, scr=scr, gccol=gccol: E.tensor_reduce(out=gccol[:], in_=scr[:].rearrange("p (n f) -> p n f", f=64), axis=AX.X, op=ALU.add), reads=[bscr], writes=[bgccol])
                tt(v3(scr), bscr, bb[0:64, :].rearrange("p (n f) -> p n f", f=64), [bbb, bc64], i64b, bc64, ALU.mult)
                P.op("dve", lambda E, scr=scr, bcol=bcol: E.tensor_reduce(out=bcol[:], in_=scr[:].rearrange("p (n f) -> p n f", f=64), axis=AX.X, op=ALU.add), reads=[bscr], writes=[bbcol])
                yield
                P.op("dve", lambda E: E.tensor_scalar(out=nbcol[:], in0=bcol[:], scalar1=-1.0, scalar2=None, op0=ALU.mult), reads=[bbcol], writes=[bnbcol])
                tt(v3(arg1), barg1, gcb[0:64, :].rearrange("p (n f) -> p n f", f=64), [bgcb, bgccol], gccol[:].unsqueeze(2).to_broadcast([64, 8, 64]), bgccol, ALU.subtract)
                tt(v3(scr), bscr, v3(arg1), [barg1, bc64], NEGU.unsqueeze(1).to_broadcast([64, 8, 64]), bc64, ALU.add)
                P.op("act", lambda E: E.activation(out=DT[:], in_=scr[:], func=AF.Exp), reads=[bscr], writes=[bDT])
                yield
                P.op("dve", lambda E: E.scalar_tensor_tensor(out=scr[:].rearrange("p (n f) -> p n f", f=64), in0=arg1[:].rearrange("p (n f) -> p n f", f=64), scalar=-1.0,
                                                             in1=NEGLS.unsqueeze(1).to_broadcast([64, 8, 64]), op0=ALU.mult, op1=ALU.add), reads=[barg1, bc64], writes=[bscr])
                P.op("act", lambda E: E.activation(out=Ds[:], in_=scr[:], func=AF.Exp), reads=[bscr], writes=[bDs])
                pk, bpk = BK[0]; pq, bpq = BK[1]
                fns = [(lambda E, n=n, pk=pk, kT=kT: E.matmul(pk[0:64, n * 64:(n + 1) * 64], lhsT=kT[:, n * 64:(n + 1) * 64], rhs=kT[:, n * 64:(n + 1) * 64],
                                                              start=True, stop=True)) for n in range(8)]
                P.mm_group(fns, reads=[bkT], writes=[bpk])
                fns = [(lambda E, n=n, pq=pq, kT=kT, qT=qT: E.matmul(pq[0:64, n * 64:(n + 1) * 64], lhsT=kT[:, n * 64:(n + 1) * 64], rhs=qT[:, n * 64:(n + 1) * 64],
                                                                     start=True, stop=True)) for n in range(8)]
                P.mm_group(fns, reads=[bkT, bqT], writes=[bpq])
                yield
                tt(attnT[:], battnT, pq[0:64, :], [bpq, bDT], DT[:], bDT, ALU.mult)
                Pc, bPc = m64["Pa"]; Pn, bPn = m64["Pb"]; Qc, bQc = m64["Qa"]; Qn, bQn = m64["Qb"]
                tt(tmp[:], btmp, pk[0:64, :], [bpk, bDT], DT[:], bDT, ALU.mult)
                tt(tmp[:], btmp, tmp[:], [btmp, bbb], bb[0:64, :], bbb, ALU.mult)
                tt(v3(Qc), bQc, v3(tmp), [btmp, bc64], NSU.unsqueeze(1).to_broadcast([64, 8, 64]), bc64, ALU.mult)
                yield
                tt(tmp[:], btmp, pk[0:64, :], [bpk, bDs], Ds[:], bDs, ALU.mult)
                tt(v3(Pc), bPc, v3(tmp), [btmp, bnbcol], nbcol[:].unsqueeze(2).to_broadcast([64, 8, 64]), bnbcol, ALU.mult)
                tt(v3(Y), bY, v3(Qc), [bQc, bc64], i64b, bc64, ALU.add)
                P.op("act", lambda E, Ybf=Ybf, Y=Y: E.copy(out=Ybf[:], in_=Y[:]), reads=[bY], writes=[bYbf])
                yield
                for j in range(5):
                    pP, bpP = BK[2]; pQ, bpQ = BK[1]
                    fns = [(lambda E, n=n, pP=pP, Qc=Qc, Pc=Pc: E.matmul(pP[0:64, n * 64:(n + 1) * 64], lhsT=Qc[:, n * 64:(n + 1) * 64], rhs=Pc[:, n * 64:(n + 1) * 64],
                                                                         start=True, stop=True)) for n in range(8)]
                    P.mm_group(fns, reads=[bQc, bPc], writes=[bpP])
                    if j < 4:
                        fns = [(lambda E, n=n, pQ=pQ, Qc=Qc, Pc=Pc: E.matmul(pQ[0:64, n * 64:(n + 1) * 64], lhsT=Pc[:, n * 64:(n + 1) * 64], rhs=Qc[:, n * 64:(n + 1) * 64],
                                                                             start=True, stop=True)) for n in range(8)]
                        P.mm_group(fns, reads=[bQc, bPc], writes=[bpQ])
                    yield
                    P.op("act", lambda E, Pn=Pn, pP=pP: E.copy(out=Pn[:], in_=pP[0:64, :]), reads=[bpP], writes=[bPn])
                    if j < 4:
                        P.op("dve", lambda E, Qn=Qn, pQ=pQ: E.tensor_copy(out=Qn[:], in_=pQ[0:64, :]), reads=[bpQ], writes=[bQn])
                    pY, bpY = BK[0]
                    fns = [(lambda E, n=n, pY=pY, Pn=Pn, Ybf=Ybf: E.matmul(pY[0:64, n * 64:(n + 1) * 64], lhsT=Pn[:, n * 64:(n + 1) * 64], rhs=Ybf[:, n * 64:(n + 1) * 64],
                                                                         start=True, stop=True)) for n in range(8)]
                    P.mm_group(fns, reads=[bPn, bYbf], writes=[bpY])
                    yield
                    tt(Y[:], bY, Y[:], [bY, bpY], pY[0:64, :], bpY, ALU.add)
                    P.op("act", lambda E, Ybf=Ybf, Y=Y: E.copy(out=Ybf[:], in_=Y[:]), reads=[bY], writes=[bYbf])
                    Pc, bPc, Pn, bPn = Pn, bPn, Pc, bPc
                    Qc, bQc, Qn, bQn = Qn, bQn, Qc, bQc
                for hf in range(2):
                    pth, bpth = BK[1 + hf]
                    fns = [(lambda E, n=n, vT=vT, pth=pth, hf=hf: E.transpose(out=pth[0:64, n * 128:(n + 1) * 128], in_=vT[:, (4 * hf + n) * 64:(4 * hf + n + 1) * 64],
                                                                              identity=idf[:])) for n in range(4)]
                    P.mm_group(fns, reads=[bvT, bidf], writes=[bpth])
                    tt(bv[:, 4 * hf:4 * hf + 4, :], bbv, pth[0:64, :].rearrange("p (n d) -> p n d", d=128), [bpth, bbcol],
                       bcol[:, 4 * hf:4 * hf + 4].unsqueeze(2).to_broadcast([64, 4, 128]), bbcol, ALU.mult)
                yield
                tt(elast[:], belast, gcb[0:64, :].rearrange("p (n f) -> p n f", f=64)[:, :, 63], [bgcb, bgccol], gccol[:], bgccol, ALU.subtract)
                P.op("act", lambda E: E.activation(out=elast[:], in_=elast[:], func=AF.Exp), reads=[belast], writes=[belast])
                for hf in range(2):
                    pth, bpth = BK[1 + hf]
                    fns = [(lambda E, n=n, kT=kT, pth=pth, hf=hf: E.transpose(out=pth[0:64, n * 128:(n + 1) * 128], in_=kT[:, (4 * hf + n) * 64:(4 * hf + n + 1) * 64],
                                                                              identity=idf[:])) for n in range(4)]
                    P.mm_group(fns, reads=[bkT, bidf], writes=[bpth])
                    tt(kdec[:, 4 * hf:4 * hf + 4, :], bkdec, pth[0:64, :].rearrange("p (n d) -> p n d", d=128), [bpth, belast],
                       elast[:, 4 * hf:4 * hf + 4].unsqueeze(2).to_broadcast([64, 4, 128]), belast, ALU.mult)
                yield
                P.op("act", lambda E, gcb=gcb, EG=EG: E.activation(out=EG[:], in_=gcb[:], func=AF.Exp), reads=[bgcb], writes=[bEG])
                tt(qdec[:], bqdec, qT, [bqT, bEG], EG[:], bEG, ALU.mult)
                kTb, bkTb = H["kTb"]
                P.op("act", lambda E, kTb=kTb, kT=kT: E.copy(out=kTb[:], in_=kT), reads=[bkT], writes=[bkTb])
                P.op("act", lambda E: E.activation(out=egc[:], in_=gccol[:], func=AF.Exp), reads=[bgccol], writes=[begc])
                P.op("dve", lambda E, nbg=nbg: E.scalar_tensor_tensor(out=nbg[:], in0=egc[:], scalar=-1.0, in1=bcol[:], op0=ALU.mult, op1=ALU.mult),
                     reads=[begc, bbcol], writes=[bnbg])
            gensB = [stageB(0), stageB(1)]
            aliveB = True
            while aliveB:
                aliveB = False
                for g_ in gensB:
                    try:
                        next(g_)
                        aliveB = True
                    except StopIteration:
                        pass
            banks = [(GP[0], GP[1]), (GP[2], GP[3])]
            for n in range(8):
                cs = slice(n * 64, (n + 1) * 64)
                for h in range(2):
                    H = heads[h]; S, bS = Sst[h]
                    kT, bkT = H["kTb"]; Sbf, bSbf = H["Sbf"]
                    attnT, battnT = H["attnT"]; Y, bY = H["Ybf"]; EG, bEG = H["EG"]; qdec, bqdec = H["qdec"]
                    bv, bbv = H["bv"]; kdec, bkdec = H["kdec"]; nbg, bnbg = H["nbg"]
                    vnew, bvnew = H["vnew"]; rhs2, brhs2 = H["rhs2"]; osb, bosb = H["osb"]
                    (KSO, bKSO), (Sb, bSb) = banks[h]
                    Vb, bVb = KSO, bKSO
                    P.op("pe", lambda E, cs=cs, kT=kT, Sbf=Sbf, KSO=KSO: E.matmul(KSO[0:64, 0:128], lhsT=kT[:, cs], rhs=Sbf[:], start=True, stop=True),
                         reads=[bkT, bSbf], writes=[bKSO])
                    P.op("dve", lambda E, n=n, KSO=KSO, rhs2=rhs2, nbg=nbg, bv=bv: E.scalar_tensor_tensor(
                        out=rhs2[:], in0=KSO[0:64, 0:128], scalar=nbg[:, n:n + 1], in1=bv[:, n, :], op0=ALU.mult, op1=ALU.add),
                        reads=[bKSO, bnbg, bbv], writes=[brhs2])
                    P.op("pe", lambda E, cs=cs, Y=Y, Vb=Vb, rhs2=rhs2: E.matmul(Vb[0:64, 128:256], lhsT=Y[:, cs], rhs=rhs2[:], start=True, stop=True),
                         reads=[bY, brhs2], writes=[bVb])
                    P.op("act", lambda E, vnew=vnew, Vb=Vb: E.copy(out=vnew[:], in_=Vb[0:64, 128:256]), reads=[bVb], writes=[bvnew])
                    fns = [lambda E, cs=cs, Sbf=Sbf, KSO=KSO, qdec=qdec: E.matmul(KSO[64:128, 0:128], lhsT=qdec[:, cs], rhs=Sbf[:], start=True, stop=False),
                           lambda E, cs=cs, KSO=KSO, attnT=attnT, vnew=vnew: E.matmul(KSO[64:128, 0:128], lhsT=attnT[:, cs], rhs=vnew[:], start=False, stop=True)]
                    P.mm_group(fns, reads=[bqdec, bSbf, battnT, bvnew], writes=[bKSO])
                    P.op("pe", lambda E, n=n, Sb=Sb, kdec=kdec, vnew=vnew: E.matmul(Sb[:, 0:128], lhsT=kdec[:, n, :], rhs=vnew[:], start=True, stop=True),
                         reads=[bkdec, bvnew], writes=[bSb])
                    P.op("dve", lambda E, n=n, S=S, EG=EG, Sb=Sb: E.scalar_tensor_tensor(out=S[:], in0=S[:], scalar=EG[:, n * 64 + 63:n * 64 + 64], in1=Sb[:, 0:128],
                                                                                         op0=ALU.mult, op1=ALU.add), reads=[bS, bEG, bSb], writes=[bS])
                    P.op("act", lambda E, S=S, Sbf=Sbf: E.copy(out=Sbf[:], in_=S[:]), reads=[bS], writes=[bSbf])
                    P.op("act", lambda E, n=n, osb=osb, KSO=KSO: E.copy(out=osb[64:128, n, :], in_=KSO[64:128, 0:128]), reads=[bKSO], writes=[bosb])
                    advance(1)
                advance(1)
            advance(100)
            for h in range(2):
                osb, bosb = heads[h]["osb"]
                P.dma("sp", o_d[s_ * 512:(s_ + 1) * 512, h * 128:(h + 1) * 128].rearrange("(n c) d -> c n d", c=64), osb[64:128, :, :], reads=[bosb],
                      writes=[fz["obuf_of"](s_) if fz else bo])
            if fz:
                fz["after_chunk"](s_)
        if fz:
            barrier(P)
        else:
            P.finish([bo])
    return nc


def run_L1a(inp):
    nc = _get("L1a", build_L1a)
    c64, cmask, sel = _gdn_consts()
    w_in = inp["w_in_even"][0]
    conv = inp["conv_qkv"][0]
    ones = np.ones((128, 128), np.float32)
    maps = []
    for c in range(8):
        b, r = divmod(c, 4)
        cols = np.concatenate([np.arange(256 * r, 256 * r + 256), 1024 + np.arange(256 * r, 256 * r + 256), 2048 + np.arange(256 * r, 256 * r + 256)])
        maps.append({"x": np.ascontiguousarray(inp["x"][b]), "npre": np.ascontiguousarray(inp["norm_pre"][0]),
                     "w": np.ascontiguousarray(w_in[:, cols]), "wb": np.ascontiguousarray(w_in[:, 4096 + 2 * r:4096 + 2 * r + 2]),
                     "wa": np.ascontiguousarray(w_in[:, 4104 + 2 * r:4104 + 2 * r + 2]), "conv": np.ascontiguousarray(conv[:, cols]),
                     "alog": np.ascontiguousarray(inp["a_log"][0, 2 * r:2 * r + 2]), "dtb": np.ascontiguousarray(inp["dt_bias"][0, 2 * r:2 * r + 2]),
                     "ident": _IDENT, "c64": c64, "cmask": cmask, "sel": sel, "ones": ones})
    res = run_bass_kernel_spmd(nc, maps, core_ids=list(range(8)))
    S_ = inp["x"].shape[1]
    o = np.empty((2, S_, 1024), np.float32)
    for c in range(8):
        b, r = divmod(c, 4)
        o[b, :, 256 * r:256 * (r + 1)] = res.results[c]["o"]
    return o


def kernel_unfused(**inputs):
    inp = {k: np.asarray(v) for k, v in inputs.items()}
    o = run_L1a(inp)
    ys = run_L1b(inp)
    x1 = run_L2(inp, o, ys)
    out = run_L3(inp, x1)
    return out.astype(np.float32)


def build_fused():
    nc = bass.Bass("TRN2", target_bir_lowering=False)
    x_full = nc.dram_tensor("x", [8192, 1024], F32, kind="ExternalInput").ap()
    ident_d = nc.dram_tensor("ident", [128, 128], F32, kind="ExternalInput").ap()
    npre0_d = nc.dram_tensor("npre0", [1024], F32, kind="ExternalInput").ap()
    gidx_d = nc.dram_tensor("gidx", [128, 2, 17, 4], I32, kind="ExternalInput").ap()
    out_d = nc.dram_tensor("out", [2048, 1024], F32, kind="ExternalOutput").ap()
    ag_in = [nc.dram_tensor("ag_in%d" % i, [8192, 256], (F32, BF16)[i]) for i in range(2)]
    ag_out = [nc.dram_tensor("ag_out%d" % i, [4 * 8192, 256], (F32, BF16)[i]) for i in range(2)]
    x1s = nc.dram_tensor("x1s", [2176, 1024], F32)
    GROUPS = [[0, 1, 2, 3], [4, 5, 6, 7]]
    with ExitStack() as st:
        C = Ctx(nc, st); P = C.P
        csem = st.enter_context(nc.semaphore("csem"))
        bag_out = Buf("ag_out"); bx1s = Buf("x1s", multi=True); bout = Buf("out", multi=True)
        bo_ch = [Buf("o_ch%d" % k, multi=True) for k in range(16)]
        by_jt = [Buf("y_jt%d" % k, multi=True) for k in range(4)]
        ncc = [0]

        def emit_cc(which, k, inbuf, rows=512):
            P._deps("pool", [inbuf], [])
            P.streams["pool"].append(lambda E, which=which, k=k, rows=rows: E.collective_compute(
                "AllGather", ALU.bypass, replica_groups=GROUPS,
                ins=[ag_in[which].ap()[k * rows:(k + 1) * rows, :].opt()],
                outs=[ag_out[which].ap()[k * 4 * rows:(k + 1) * 4 * rows, :].opt()]).then_inc(csem))
            ncc[0] += 1

        share1 = {"x": x_full, "ident": ident_d, "npre": npre0_d}

        def after_jt(jt):
            emit_cc(1, jt, by_jt[jt], rows=2048)

        with ExitStack() as stU:
            CU = Ctx(nc, stU, P, "u_")
            uext = CU.sb("uTp", [128, 2, 16, 512], BF16)
            build_L1a(8192, fz={"nc": nc, "P": P, "pfx": "a_", "share": share1, "out": ag_in[0].ap(), "uTp": uext,
                                "obuf_of": lambda s_: bo_ch[s_], "after_chunk": lambda s_: emit_cc(0, s_, bo_ch[s_])})
            build_L1b(8192, fz={"nc": nc, "P": P, "pfx": "b_", "share": share1, "out": ag_in[1].ap(), "uTp": uext,
                                "obuf_of": lambda jt: by_jt[jt], "after_chunk": after_jt, "ybf16": True})
        gidx, bgidx = C.sb("gidx", [128, 2, 17, 4], I32)
        P.dma("sp", gidx[:], gidx_d, writes=[bgidx])
        waited = [False]

        def gather(P_, ld, bld, tile, part):
            if not waited[0]:
                P.streams["pool"].append(lambda E: E.wait_ge(csem, ncc[0]))
                P.op("pool", lambda E: E.nop(), reads=[], writes=[bag_out])
                waited[0] = True
            for i in range(4):
                P_.dma_ind("pool", ld[:, i * 256:(i + 1) * 256], ag_out[part].ap(), gidx[:, part, tile, i:i + 1], reads=[bag_out, bgidx], writes=[bld])

        share2 = {"ident": ident_d, "npre": npre0_d, "o": None, "ys": None}
        build_L2(2176, fz={"nc": nc, "P": P, "pfx": "c_", "share": share2, "out": x1s.ap(), "obuf": bx1s, "gather": gather, "ybf16": True})
        share3 = {"ident": ident_d, "x": x1s.ap()}
        build_L3(2048, fz={"nc": nc, "P": P, "pfx": "d_", "share": share3, "out": out_d, "obuf": bout, "xbuf": bx1s})
        P.finish([bout])
    return nc


def _gidx(r):
    g = np.zeros((128, 2, 17, 4), np.int32)
    p = np.arange(128)[:, None, None]
    tile = np.arange(17)[None, :, None]
    src = np.arange(4)[None, None, :]
    tok = np.clip(2048 * r - 128 + tile * 128 + p, 0, 8191)
    for part, R in ((0, 512), (1, 2048)):
        g[:, part] = ((tok // R) * 4 + src) * R + tok % R
    return g


def kernel(**inputs):
    inp = {k: np.ascontiguousarray(np.asarray(v)) for k, v in inputs.items()}
    nc = _get("fused", build_fused)
    c64, cmask, sel = _gdn_consts()
    mk, idm = _s5_consts()
    ones = np.ones((128, 128), np.float32)
    w_in = inp["w_in_even"][0]
    conv = inp["conv_qkv"][0]
    wz = np.ascontiguousarray(np.concatenate([w_in[:, 3072:4096], w_in[:, 5136:6160]], axis=1))
    maps = []
    for c in range(8):
        b, r = divmod(c, 4)
        cols = np.concatenate([np.arange(256 * r, 256 * r + 256), 1024 + np.arange(256 * r, 256 * r + 256), 2048 + np.arange(256 * r, 256 * r + 256)])
        gs = slice(16 * r, 16 * r + 16)
        xq = np.zeros((2176, 1024), np.float32)
        xq[128:] = inp["x"][b, 2048 * r:2048 * (r + 1)]
        if r > 0:
            xq[:128] = inp["x"][b, 2048 * r - 128:2048 * r]
        m = {"x": inp["x"][b], "ident": _IDENT, "npre0": inp["norm_pre"][0], "gidx": _gidx(r),
             "a_w": np.ascontiguousarray(w_in[:, cols]), "a_wb": np.ascontiguousarray(w_in[:, 4096 + 2 * r:4096 + 2 * r + 2]),
             "a_wa": np.ascontiguousarray(w_in[:, 4104 + 2 * r:4104 + 2 * r + 2]), "a_conv": np.ascontiguousarray(conv[:, cols]),
             "a_alog": np.ascontiguousarray(inp["a_log"][0, 2 * r:2 * r + 2]), "a_dtb": np.ascontiguousarray(inp["dt_bias"][0, 2 * r:2 * r + 2]),
             "a_c64": c64, "a_cmask": cmask, "a_sel": sel, "a_ones": ones,
             "a_wu": np.ascontiguousarray(w_in[:, 4112 + 256 * r:4112 + 256 * (r + 1)]),
             "b_wu": np.ascontiguousarray(w_in[:, 4112 + 256 * r:4112 + 256 * (r + 1)]),
             "b_lre": np.ascontiguousarray(inp["s5_lam_re"][0, gs]), "b_lim": np.ascontiguousarray(inp["s5_lam_im"][0, gs]),
             "b_bre": np.ascontiguousarray(inp["s5_b_re"][0, gs]), "b_bim": np.ascontiguousarray(inp["s5_b_im"][0, gs]),
             "b_cre": np.ascontiguousarray(inp["s5_c_re"][0, gs]), "b_cim": np.ascontiguousarray(inp["s5_c_im"][0, gs]),
             "b_ldt": np.ascontiguousarray(inp["s5_log_dt"][0, gs]), "b_dd": np.ascontiguousarray(inp["s5_d"][0, 256 * r:256 * (r + 1)]),
             "b_taus": TAUS, "b_mk": mk, "b_idm": idm,
             "c_x": xq, "c_wz": wz, "c_wglu": inp["w_glu"][0], "c_wout": inp["w_out_even"][0], "c_npost": inp["norm_post"][0],
             "c_gnw": inp["gdn_norm_w"][0],
             "d_win": inp["w_in_odd"][0], "d_wout": inp["w_out_odd"][0], "d_conv": inp["conv_short"][0],
             "d_npre": inp["norm_pre"][1], "d_npost": inp["norm_post"][1]}
        maps.append(m)
    res = run_bass_kernel_spmd(nc, maps, core_ids=list(range(8)))
    out = np.empty((2, 8192, 1024), np.float32)
    for c in range(8):
        b, r = divmod(c, 4)
        out[b, r * 2048:(r + 1) * 2048] = res.results[c]["out"]
    return out
```

```python
from contextlib import ExitStack
import numpy as np
import concourse.bass as bass
import concourse.mybir as mybir
from concourse.bass_utils import run_bass_kernel_spmd

F32 = mybir.dt.float32
BF16 = mybir.dt.bfloat16
AF = mybir.ActivationFunctionType
ALU = mybir.AluOpType
AX = mybir.AxisListType

NDS = 12


class Buf:
    __slots__ = ("name", "w", "r", "multi")

    def __init__(self, name, multi=False):
        self.name = name
        self.w = [] if multi else None
        self.r = []
        self.multi = multi


class Prog:
    ENG = ("pe", "act", "dve", "pool", "sp")

    def __init__(self, nc, stack):
        self.nc = nc
        self.stack = stack
        self.streams = {e: [] for e in self.ENG}
        self.cnt = {e: 0 for e in self.ENG}
        self.sem = {e: stack.enter_context(nc.semaphore("s_" + e)) for e in self.ENG}
        self.seen = {e: {} for e in self.ENG}
        self.dcnt = {e: 0 for e in self.ENG}
        self.dsem = {}
        for e in ("sp", "pool", "act"):
            self.dsem[e] = [stack.enter_context(nc.semaphore("d_%s%d" % (e, i))) for i in range(NDS)]
        self.same_engine_sync = True
        self.nwaits = 0

    def _wait(self, eng, tok):
        if tok is None:
            return
        kind = tok[0]
        if kind == "c":
            _, e2, n = tok
            if e2 == eng and (eng == "pe" or not self.same_engine_sync):
                return
            key = e2
            if self.seen[eng].get(key, 0) >= n:
                return
            self.seen[eng][key] = n
            sem = self.sem[e2]
            self.streams[eng].append(lambda E, sem=sem, n=n: E.wait_ge(sem, n))
            self.nwaits += 1
        else:
            _, q, slot, val = tok
            key = ("d", q, slot)
            if self.seen[eng].get(key, 0) >= val:
                return
            self.seen[eng][key] = val
            sem = self.dsem[q][slot]
            self.streams[eng].append(lambda E, sem=sem, val=val: E.wait_ge(sem, val))
            self.nwaits += 1

    def _deps(self, eng, reads, writes):
        for b in reads:
            if b.multi:
                for t in b.w:
                    self._wait(eng, t)
            else:
                self._wait(eng, b.w)
        for b in writes:
            if not b.multi:
                self._wait(eng, b.w)
            for t in b.r:
                self._wait(eng, t)

    def _commit(self, tok, reads, writes):
        for b in writes:
            if b.multi:
                b.w.append(tok)
            else:
                b.w = tok
            b.r = []
        for b in reads:
            if b not in writes:
                b.r.append(tok)

    def op(self, eng, fn, reads=(), writes=()):
        reads = list(reads)
        writes = list(writes)
        self._deps(eng, reads, writes)
        self.cnt[eng] += 1
        n = self.cnt[eng]
        sem = self.sem[eng]
        self.streams[eng].append(lambda E, fn=fn, sem=sem: fn(E).then_inc(sem, 1))
        tok = ("c", eng, n)
        self._commit(tok, reads, writes)
        return tok

    def mm_group(self, fns, reads=(), writes=()):
        eng = "pe"
        reads = list(reads)
        writes = list(writes)
        self._deps(eng, reads, writes)
        self.cnt[eng] += 1
        n = self.cnt[eng]
        sem = self.sem[eng]
        for fn in fns[:-1]:
            self.streams[eng].append(lambda E, fn=fn: fn(E))
        last = fns[-1]
        self.streams[eng].append(lambda E, fn=last, sem=sem: fn(E).then_inc(sem, 1))
        tok = ("c", eng, n)
        self._commit(tok, reads, writes)
        return tok

    def dma(self, q, out_ap, in_ap, reads=(), writes=()):
        reads = list(reads)
        writes = list(writes)
        self._deps(q, reads, writes)
        j = self.dcnt[q]
        self.dcnt[q] += 1
        slot = j % NDS
        val = 16 * (j // NDS + 1)
        if j >= NDS:
            self._wait(q, ("d", q, slot, val - 16))
        sem = self.dsem[q][slot]
        self.streams[q].append(
            lambda E, o=out_ap, i=in_ap, sem=sem: E.dma_start(out=o, in_=i).then_inc(sem, 16))
        tok = ("d", q, slot, val)
        self._commit(tok, reads, writes)
        return tok

    def dma_ind(self, q, out_ap, table_ap, idx_ap, reads=(), writes=()):
        reads = list(reads)
        writes = list(writes)
        self._deps(q, reads, writes)
        j = self.dcnt[q]
        self.dcnt[q] += 1
        slot = j % NDS
        val = 16 * (j // NDS + 1)
        if j >= NDS:
            self._wait(q, ("d", q, slot, val - 16))
        sem = self.dsem[q][slot]
        self.streams[q].append(
            lambda E, o=out_ap, t=table_ap, i=idx_ap, sem=sem: E.indirect_dma_start(
                out=o, out_offset=None, in_=t, in_offset=bass.IndirectOffsetOnAxis(ap=i, axis=0)).then_inc(sem, 16))
        tok = ("d", q, slot, val)
        self._commit(tok, reads, writes)
        return tok

    def finish(self, final_bufs):
        for b in final_bufs:
            for t in (b.w if b.multi else [b.w]):
                self._wait("sp", t)
        nc = self.nc
        streams = self.streams
        with nc.Block() as block:
            @block.tensor
            def _(E):
                for f in streams["pe"]:
                    f(E)

            @block.scalar
            def _(E):
                for f in streams["act"]:
                    f(E)

            @block.vector
            def _(E):
                for f in streams["dve"]:
                    f(E)

            @block.gpsimd
            def _(E):
                for f in streams["pool"]:
                    f(E)

            @block.sync
            def _(E):
                for f in streams["sp"]:
                    f(E)


class Ctx:
    def __init__(self, nc, st, P=None, pfx=""):
        self.nc = nc
        self.st = st
        self.pfx = pfx
        if P is None:
            st.enter_context(nc.allow_non_contiguous_dma(reason="small parameter loads / layout transforms"))
            P = Prog(nc, st)
        self.P = P

    def sb(self, name, shape, dt=F32):
        t = self.st.enter_context(self.nc.sbuf_tensor("sb_" + self.pfx + name, shape, dt))
        return t, Buf(name)

    def ps(self, name, shape, dt=F32):
        t = self.st.enter_context(self.nc.psum_tensor("ps_" + self.pfx + name, shape, dt))
        return t, Buf(name)


def bcast_row_load(C, name, dram_vec, n, q="sp"):
    t, b = C.sb(name, [128, n])
    C.P.dma(q, t[:], dram_vec.partition_broadcast(128), writes=[b])
    return t, b


def make_ident(C, dram_ident):
    idf, bidf = C.sb("identf", [128, 128])
    C.P.dma("sp", idf[:], dram_ident, writes=[bidf])
    idb, bidb = C.sb("identb", [128, 128], BF16)
    C.P.op("dve", lambda E: E.tensor_copy(out=idb[:], in_=idf[:]), reads=[bidf], writes=[bidb])
    return idf, bidf, idb, bidb


def rms_rstd(C, src, bsrc, ncols, junk, bjunk, ss, bss, eps=1e-6):
    P = C.P
    P.op("act", lambda E: E.activation(out=junk, in_=src, func=AF.Square, accum_out=ss[:, 0:1]),
         reads=[bsrc], writes=[bjunk, bss])
    P.op("act", lambda E: E.activation(out=ss[:, 0:1], in_=ss[:, 0:1], func=AF.Sqrt, bias=float(eps), scale=float(1.0 / ncols)),
         reads=[bss], writes=[bss])
    P.op("dve", lambda E: E.reciprocal(out=ss[:, 0:1], in_=ss[:, 0:1]), reads=[bss], writes=[bss])


def transpose8(C, src_bf, bsrc, idb, bidb, ptr, bptr, dst3, bdst, eng="act"):
    P = C.P
    fns = [(lambda E, kt=kt: E.transpose(out=ptr[:, kt * 128:(kt + 1) * 128], in_=src_bf[:, kt * 128:(kt + 1) * 128],
                                         identity=idb[:])) for kt in range(8)]
    P.mm_group(fns, reads=[bsrc, bidb], writes=[bptr])
    src3 = ptr[:].rearrange("p (k t) -> p k t", k=8)
    if eng == "act":
        P.op("act", lambda E: E.copy(out=dst3, in_=src3), reads=[bptr], writes=[bdst])
    else:
        P.op("dve", lambda E: E.tensor_copy(out=dst3, in_=src3), reads=[bptr], writes=[bdst])


def outproj_post(C, catT, bcat, nkt, wout, bwout, t, xres, bxres, npw, bnpw, pso, bpso, yo, byo, junk, bjunk, ss, bss,
                 out_dram_rows, bout):
    P = C.P
    for hh in range(2):
        fns = [(lambda E, kt=kt, hh=hh: E.matmul(pso[hh][:], lhsT=catT[:, kt, t * 128:(t + 1) * 128],
                                                 rhs=wout[:, kt, hh * 512:(hh + 1) * 512],
                                                 start=(kt == 0), stop=(kt == nkt - 1))) for kt in range(nkt)]
        P.mm_group(fns, reads=[bcat, bwout], writes=[bpso[hh]])
        P.op("act", lambda E, hh=hh: E.copy(out=yo[:, hh * 512:(hh + 1) * 512], in_=pso[hh][:]),
             reads=[bpso[hh]], writes=[byo])
    rms_rstd(C, yo[:], byo, 1024, junk[:], bjunk, ss, bss)
    P.op("dve", lambda E: E.scalar_tensor_tensor(out=yo[:], in0=yo[:], scalar=ss[:, 0:1], in1=npw[:],
                                                 op0=ALU.mult, op1=ALU.mult), reads=[byo, bss, bnpw], writes=[byo])
    P.op("dve", lambda E: E.tensor_tensor(out=yo[:], in0=yo[:], in1=xres, op=ALU.add), reads=[byo, bxres], writes=[byo])
    P.dma("sp", out_dram_rows, yo[:], reads=[byo], writes=[bout])


def load_w_bf16(C, name, dram_w, kt_n, ncols, chunk=2048, groups=None):
    w, _ = C.sb(name, [128, kt_n, ncols], BF16)
    src = dram_w.rearrange("(k p) c -> p k c", p=128)
    if groups is None:
        bw = Buf(name, multi=True)
        for kt in range(kt_n):
            for c0 in range(0, ncols, chunk):
                c1 = min(ncols, c0 + chunk)
                C.P.dma("pool", w[:, kt, c0:c1], src[:, kt, c0:c1], writes=[bw])
        return w, bw
    bws = []
    for gi, sls in enumerate(groups):
        bg = Buf("%s_g%d" % (name, gi), multi=True)
        for (c0, c1) in sls:
            for kt in range(kt_n):
                C.P.dma("pool", w[:, kt, c0:c1], src[:, kt, c0:c1], writes=[bg])
        bws.append(bg)
    return w, bws


def build_L2(ntok=2048, fz=None):
    nc = fz["nc"] if fz else bass.Bass("TRN2", target_bir_lowering=False)
    pfx = fz["pfx"] if fz else ""

    def D(name, shape):
        if fz and name in fz["share"]:
            return fz["share"][name]
        return nc.dram_tensor(pfx + name, shape, F32, kind="ExternalInput").ap()
    x_d = D("x", [ntok, 1024]); o_d = D("o", [ntok, 1024]); ys_d = D("ys", [ntok, 1024])
    wz_d = D("wz", [1024, 2048]); wglu_d = D("wglu", [1024, 1024]); wout_d = D("wout", [2048, 1024])
    npre_d = D("npre", [1024]); npost_d = D("npost", [1024]); gnw_d = D("gnw", [128]); ident_d = D("ident", [128, 128])
    out_d = fz["out"] if fz else nc.dram_tensor("out", [ntok, 1024], F32, kind="ExternalOutput").ap()
    NT = 512
    with ExitStack() as st:
        C = Ctx(nc, st, fz["P"], pfx) if fz else Ctx(nc, st); P = C.P
        idf, bidf, idb, bidb = make_ident(C, ident_d)
        npre, bnpre = bcast_row_load(C, "npre", npre_d, 1024)
        npost, bnpost = bcast_row_load(C, "npost", npost_d, 1024)
        gnw, bgnw = bcast_row_load(C, "gnw", gnw_d, 128)
        wz, bwz = load_w_bf16(C, "wz", wz_d, 8, 2048)
        wglu, bwglu = load_w_bf16(C, "wglu", wglu_d, 8, 1024)
        wout, bwout = load_w_bf16(C, "wout", wout_d, 16, 1024)
        xt4, bxt4 = C.sb("xt4", [128, 4, 1024]); bxt = [Buf("xt%d" % i) for i in range(4)]
        ldo = [C.sb("ldo%d" % i, [128, 1024]) for i in range(2)]
        ldy = [C.sb("ldy%d" % i, [128, 1024], BF16 if (fz and fz.get("ybf16")) else F32) for i in range(2)]
        for (_t, _b) in ldo + ldy:
            _b.multi = True; _b.w = []
        sq, bsq = C.sb("sq", [128, 1024])
        hn, bhn = C.sb("hn", [128, 1024], BF16)
        ss, bss = C.sb("ss", [128, 1])
        ss8, bss8 = C.sb("ss8", [128, 8])
        hT, bhT = C.sb("hT", [128, 8, NT], BF16)
        oT, boT = C.sb("oT", [128, 8, NT], BF16)
        yT, byT = C.sb("yT", [128, 8, NT], BF16)
        gz, bgz = C.sb("gz", [128, 8, NT], BF16)
        sg, bsg = C.sb("sg", [128, NT], BF16)
        catT, bcat = C.sb("catT", [128, 16, NT], BF16)
        yo, byo = C.sb("yo", [128, 1024])
        ptr, bptr = C.ps("ptr", [128, 1024], BF16)
        pmm = []; bpmm = []
        for i in range(4):
            t_, b_ = C.ps("pmm%d" % i, [128, 512]); pmm.append(t_); bpmm.append(b_)
        pso = []; bpso = []
        for i in range(2):
            t_, b_ = C.ps("pso%d" % i, [128, 512]); pso.append(t_); bpso.append(b_)
        bout = fz["obuf"] if fz else Buf("out", multi=True)
        if fz:
            sts = [(0, 128)] + [(128 + i * NT, NT) for i in range((ntok - 128) // NT)]
        else:
            sts = [(i * NT, NT) for i in range(ntok // NT)]
        tile_r0 = [t0_ + t_ * 128 for (t0_, n_) in sts for t_ in range(n_ // 128)]

        def issue_loads(ti):
            r0_ = tile_r0[ti]
            lo, blo = ldo[ti % 2]; ly, bly = ldy[ti % 2]
            if fz:
                fz["gather"](P, lo, blo, r0_ // 128, 0)
                fz["gather"](P, ly, bly, r0_ // 128, 1)
            else:
                P.dma("sp", lo[:], o_d[r0_:r0_ + 128, :], writes=[blo])
                P.dma("sp", ly[:], ys_d[r0_:r0_ + 128, :], writes=[bly])

        issue_loads(0)
        for (t0, n) in sts:
            ntl = n // 128
            for t in range(ntl):
                r0 = t0 + t * 128
                ti = tile_r0.index(r0)
                if ti + 1 < len(tile_r0):
                    issue_loads(ti + 1)
                P.dma("sp", xt4[:, t, :], x_d[r0:r0 + 128, :], writes=[bxt[t]])
                rms_rstd(C, xt4[:, t, :], bxt[t], 1024, sq[:], bsq, ss, bss)
                P.op("dve", lambda E, t=t: E.scalar_tensor_tensor(out=hn[:], in0=xt4[:, t, :], scalar=ss[:, 0:1], in1=npre[:],
                                                                  op0=ALU.mult, op1=ALU.mult), reads=[bxt[t], bss, bnpre], writes=[bhn])
                transpose8(C, hn, bhn, idb, bidb, ptr, bptr, hT[:, :, t * 128:(t + 1) * 128], bhT, eng="act")
                ld, bld = ldo[ti % 2]
                P.op("act", lambda E, ld=ld: E.activation(out=sq[:], in_=ld[:], func=AF.Square), reads=[bld], writes=[bsq])
                P.op("dve", lambda E: E.tensor_reduce(out=ss8[:], in_=sq[:].rearrange("p (h d) -> p h d", h=8), axis=AX.X, op=ALU.add),
                     reads=[bsq], writes=[bss8])
                P.op("dve", lambda E: E.tensor_scalar(out=ss8[:], in0=ss8[:], scalar1=1.0 / 128, scalar2=1e-6, op0=ALU.mult, op1=ALU.add),
                     reads=[bss8], writes=[bss8])
                P.op("act", lambda E: E.activation(out=ss8[:], in_=ss8[:], func=AF.Sqrt), reads=[bss8], writes=[bss8])
                P.op("dve", lambda E: E.reciprocal(out=ss8[:], in_=ss8[:]), reads=[bss8], writes=[bss8])
                P.op("dve", lambda E, ld=ld: E.tensor_tensor(out=sq[:].rearrange("p (h d) -> p h d", h=8), in0=ld[:].rearrange("p (h d) -> p h d", h=8),
                                                      in1=ss8[:].unsqueeze(2).to_broadcast([128, 8, 128]), op=ALU.mult),
                     reads=[bld, bss8], writes=[bsq])
                P.op("dve", lambda E: E.tensor_tensor(out=hn[:].rearrange("p (h d) -> p h d", h=8), in0=sq[:].rearrange("p (h d) -> p h d", h=8),
                                                      in1=gnw[:].unsqueeze(1).to_broadcast([128, 8, 128]), op=ALU.mult),
                     reads=[bsq, bgnw], writes=[bhn])
                transpose8(C, hn, bhn, idb, bidb, ptr, bptr, oT[:, :, t * 128:(t + 1) * 128], boT, eng="act")
                ld, bld = ldy[ti % 2]
                P.op("act", lambda E, ld=ld: E.activation(out=hn[:], in_=ld[:], func=AF.Gelu_apprx_tanh), reads=[bld], writes=[bhn])
                transpose8(C, hn, bhn, idb, bidb, ptr, bptr, yT[:, :, t * 128:(t + 1) * 128], byT, eng="dve")
            for ct in range(16):
                pb = pmm[ct % 4]; bpb = bpmm[ct % 4]
                fns = [(lambda E, kt=kt, ct=ct, pb=pb, n=n: E.matmul(pb[:, 0:n], lhsT=wz[:, kt, ct * 128:(ct + 1) * 128], rhs=hT[:, kt, 0:n],
                                                                start=(kt == 0), stop=(kt == 7))) for kt in range(8)]
                P.mm_group(fns, reads=[bwz, bhT], writes=[bpb])
                if ct < 8:
                    P.op("act", lambda E, pb=pb, n=n: E.activation(out=sg[:, 0:n], in_=pb[:, 0:n], func=AF.Silu), reads=[bpb], writes=[bsg])
                    P.op("dve", lambda E, ct=ct, n=n: E.tensor_tensor(out=catT[:, ct, 0:n], in0=oT[:, ct, 0:n], in1=sg[:, 0:n], op=ALU.mult),
                         reads=[boT, bsg], writes=[bcat])
                else:
                    P.op("act", lambda E, pb=pb, ct=ct, n=n: E.activation(out=gz[:, ct - 8, 0:n], in_=pb[:, 0:n], func=AF.Silu), reads=[bpb], writes=[bgz])
            for ct in range(8):
                pb = pmm[ct % 4]; bpb = bpmm[ct % 4]
                fns = [(lambda E, kt=kt, ct=ct, pb=pb, n=n: E.matmul(pb[:, 0:n], lhsT=wglu[:, kt, ct * 128:(ct + 1) * 128], rhs=yT[:, kt, 0:n],
                                                                start=(kt == 0), stop=(kt == 7))) for kt in range(8)]
                P.mm_group(fns, reads=[bwglu, byT], writes=[bpb])
                P.op("act", lambda E, pb=pb, n=n: E.activation(out=sg[:, 0:n], in_=pb[:, 0:n], func=AF.Sigmoid), reads=[bpb], writes=[bsg])
                P.op("dve", lambda E, ct=ct, n=n: E.tensor_tensor(out=sg[:, 0:n], in0=sg[:, 0:n], in1=yT[:, ct, 0:n], op=ALU.mult), reads=[bsg, byT], writes=[bsg])
                P.op("dve", lambda E, ct=ct, n=n: E.tensor_tensor(out=catT[:, 8 + ct, 0:n], in0=sg[:, 0:n], in1=gz[:, ct, 0:n], op=ALU.mult),
                     reads=[bsg, bgz], writes=[bcat])
            for t in range(ntl):
                r0 = t0 + t * 128
                outproj_post(C, catT, bcat, 16, wout, bwout, t, xt4[:, t, :], bxt[t], npost, bnpost, pso, bpso, yo, byo, sq, bsq, ss, bss,
                             out_d[r0:r0 + 128, :], bout)
        if fz:
            barrier(P)
        else:
            P.finish([bout])
    return nc


def build_L3(ntok=2048, fz=None):
    nc = fz["nc"] if fz else bass.Bass("TRN2", target_bir_lowering=False)
    pfx = fz["pfx"] if fz else ""

    def D(name, shape):
        if fz and name in fz["share"]:
            return fz["share"][name]
        return nc.dram_tensor(pfx + name, shape, F32, kind="ExternalInput").ap()
    x_d = D("x", [ntok + 128, 1024])
    win_d = D("win", [1024, 8192]); wout_d = D("wout", [2048, 1024]); conv_d = D("conv", [3, 2048])
    npre_d = D("npre", [1024]); npost_d = D("npost", [1024]); ident_d = D("ident", [128, 128])
    out_d = fz["out"] if fz else nc.dram_tensor("out", [ntok, 1024], F32, kind="ExternalOutput").ap()
    NT = 256
    with ExitStack() as st:
        C = Ctx(nc, st, fz["P"], pfx) if fz else Ctx(nc, st); P = C.P
        idf, bidf, idb, bidb = make_ident(C, ident_d)
        npre, bnpre = bcast_row_load(C, "npre", npre_d, 1024)
        npost, bnpost = bcast_row_load(C, "npost", npost_d, 1024)
        cw, bcw = C.sb("cw", [128, 3, 16])
        P.dma("sp", cw[:], conv_d.rearrange("j (c p) -> p j c", p=128), writes=[bcw])
        win, bwin_g = load_w_bf16(C, "win", win_d, 8, 8192,
                                  groups=[[(part * 2048 + cg * 512, part * 2048 + cg * 512 + 512) for part in range(4)] for cg in range(4)])
        wout, bwout = load_w_bf16(C, "wout", wout_d, 16, 1024)
        xt, bxt = C.sb("xt", [128, 1024])
        sq, bsq = C.sb("sq", [128, 1024])
        hn, bhn = C.sb("hn", [128, 1024], BF16)
        ss, bss = C.sb("ss", [128, 1])
        hT, bhT = C.sb("hT", [128, 8, NT], BF16)
        y1T, by1T = C.sb("y1T", [128, 16, NT], BF16)
        pbuf, bpbuf = C.sb("pbuf", [128, NT + 2])
        phalo, bphalo = C.sb("phalo", [128, 16, 2])
        gcs, bgcs = C.sb("gcs", [128, NT])
        cv, bcv = C.sb("cv", [128, NT])
        sz, bsz = C.sb("sz", [128, NT])
        yo, byo = C.sb("yo", [128, 1024])
        P.op("dve", lambda E: E.memset(phalo[:], 0.0), writes=[bphalo])
        ptr, bptr = C.ps("ptr", [128, 1024], BF16)
        GB = [C.ps("g%d" % i, [128, 512]) for i in range(7)]
        pso = [GB[0][0], GB[1][0]]; bpso = [GB[0][1], GB[1][1]]
        bout = fz["obuf"] if fz else Buf("out", multi=True)
        sts = [(0, 128)] + [(128 + i * NT, NT) for i in range(ntok // NT)]
        for (t0, n) in sts:
            ntl = n // 128
            for t in range(ntl):
                r0 = t0 + t * 128
                P.dma("sp", xt[:], x_d[r0:r0 + 128, :], reads=([fz["xbuf"]] if fz else []), writes=[bxt])
                rms_rstd(C, xt[:], bxt, 1024, sq[:], bsq, ss, bss)
                P.op("dve", lambda E: E.scalar_tensor_tensor(out=hn[:], in0=xt[:], scalar=ss[:, 0:1], in1=npre[:],
                                                             op0=ALU.mult, op1=ALU.mult), reads=[bxt, bss, bnpre], writes=[bhn])
                transpose8(C, hn, bhn, idb, bidb, ptr, bptr, hT[:, :, t * 128:(t + 1) * 128], bhT, eng="act")
            for ct in range(16):
                sel_ = [GB[3 * (ct % 2) + 0], GB[3 * (ct % 2) + 1], GB[3 * (ct % 2) + 2], GB[6]]
                pmm = [x_[0] for x_ in sel_]; bpmm = [x_[1] for x_ in sel_]
                for part in range(4):
                    col0 = (part * 16 + ct) * 128
                    pb = pmm[part]
                    fns = [(lambda E, n=n, kt=kt, col0=col0, pb=pb: E.matmul(pb[:, 0:n], lhsT=win[:, kt, col0:col0 + 128], rhs=hT[:, kt, 0:n],
                                                                        start=(kt == 0), stop=(kt == 7))) for kt in range(8)]
                    P.mm_group(fns, reads=[bwin_g[ct // 4], bhT], writes=[bpmm[part]])
                P.op("act", lambda E, n=n, pmm=pmm: E.copy(out=gcs[:, 0:n], in_=pmm[1][:, 0:n]), reads=[bpmm[1]], writes=[bgcs])
                P.op("act", lambda E, ct=ct: E.copy(out=pbuf[:, 0:2], in_=phalo[:, ct, :]), reads=[bphalo], writes=[bpbuf])
                P.op("dve", lambda E, n=n, pmm=pmm: E.tensor_tensor(out=pbuf[:, 2:2 + n], in0=gcs[:, 0:n], in1=pmm[2][:, 0:n], op=ALU.mult),
                     reads=[bgcs, bpmm[2]], writes=[bpbuf])
                P.op("act", lambda E, n=n, ct=ct: E.copy(out=phalo[:, ct, :], in_=pbuf[:, n:n + 2]), reads=[bpbuf], writes=[bphalo])
                if t0 == 0:
                    continue
                P.op("dve", lambda E, n=n, ct=ct: E.tensor_scalar(out=cv[:, 0:n], in0=pbuf[:, 0:n], scalar1=cw[:, 0, ct:ct + 1], scalar2=None, op0=ALU.mult),
                     reads=[bpbuf, bcw], writes=[bcv])
                P.op("dve", lambda E, n=n, ct=ct: E.scalar_tensor_tensor(out=cv[:, 0:n], in0=pbuf[:, 1:1 + n], scalar=cw[:, 1, ct:ct + 1], in1=cv[:, 0:n],
                                                                    op0=ALU.mult, op1=ALU.add), reads=[bpbuf, bcw, bcv], writes=[bcv])
                P.op("dve", lambda E, n=n, ct=ct: E.scalar_tensor_tensor(out=cv[:, 0:n], in0=pbuf[:, 2:2 + n], scalar=cw[:, 2, ct:ct + 1], in1=cv[:, 0:n],
                                                                    op0=ALU.mult, op1=ALU.add), reads=[bpbuf, bcw, bcv], writes=[bcv])
                P.op("dve", lambda E, n=n, pmm=pmm: E.tensor_tensor(out=cv[:, 0:n], in0=cv[:, 0:n], in1=pmm[0][:, 0:n], op=ALU.mult), reads=[bcv, bpmm[0]], writes=[bcv])
                P.op("act", lambda E, n=n, pmm=pmm: E.activation(out=sz[:, 0:n], in_=pmm[3][:, 0:n], func=AF.Silu), reads=[bpmm[3]], writes=[bsz])
                P.op("dve", lambda E, n=n, ct=ct: E.tensor_tensor(out=y1T[:, ct, 0:n], in0=cv[:, 0:n], in1=sz[:, 0:n], op=ALU.mult),
                     reads=[bcv, bsz], writes=[by1T])
            if t0 == 0:
                continue
            for t in range(ntl):
                r0 = t0 + t * 128
                P.dma("sp", xt[:], x_d[r0:r0 + 128, :], reads=([fz["xbuf"]] if fz else []), writes=[bxt])
                outproj_post(C, y1T, by1T, 16, wout, bwout, t, xt[:], bxt, npost, bnpost, pso, bpso, yo, byo, sq, bsq, ss, bss,
                             out_d[r0 - 128:r0, :], bout)
        if fz:
            barrier(P)
        else:
            P.finish([bout])
    return nc


_IDENT = np.eye(128, dtype=np.float32)
_CACHE = {}


def _get(name, fn):
    if name not in _CACHE:
        _CACHE[name] = fn()
    return _CACHE[name]


def run_L2(inp, o_full, ys_full):
    nc = _get("L2", build_L2)
    w_in = inp["w_in_even"][0]
    wz = np.ascontiguousarray(np.concatenate([w_in[:, 3072:4096], w_in[:, 5136:6160]], axis=1))
    maps = []
    for c in range(8):
        b, r = divmod(c, 4)
        sl = slice(r * 2048, (r + 1) * 2048)
        maps.append({"x": np.ascontiguousarray(inp["x"][b, sl]), "o": np.ascontiguousarray(o_full[b, sl]),
                     "ys": np.ascontiguousarray(ys_full[b, sl]), "wz": wz, "wglu": np.ascontiguousarray(inp["w_glu"][0]),
                     "wout": np.ascontiguousarray(inp["w_out_even"][0]), "npre": np.ascontiguousarray(inp["norm_pre"][0]),
                     "npost": np.ascontiguousarray(inp["norm_post"][0]), "gnw": np.ascontiguousarray(inp["gdn_norm_w"][0]),
                     "ident": _IDENT})
    res = run_bass_kernel_spmd(nc, maps, core_ids=list(range(8)))
    x1 = np.empty((2, 8192, 1024), np.float32)
    for c in range(8):
        b, r = divmod(c, 4)
        x1[b, r * 2048:(r + 1) * 2048] = res.results[c]["out"]
    return x1


def run_L3(inp, x1):
    nc = _get("L3", build_L3)
    maps = []
    for c in range(8):
        b, r = divmod(c, 4)
        xh = np.zeros((2048 + 128, 1024), np.float32)
        xh[128:] = x1[b, r * 2048:(r + 1) * 2048]
        if r > 0:
            xh[:128] = x1[b, r * 2048 - 128:r * 2048]
        maps.append({"x": xh, "win": np.ascontiguousarray(inp["w_in_odd"][0]), "wout": np.ascontiguousarray(inp["w_out_odd"][0]),
                     "conv": np.ascontiguousarray(inp["conv_short"][0]), "npre": np.ascontiguousarray(inp["norm_pre"][1]),
                     "npost": np.ascontiguousarray(inp["norm_post"][1]), "ident": _IDENT})
    res = run_bass_kernel_spmd(nc, maps, core_ids=list(range(8)))
    out = np.empty((2, 8192, 1024), np.float32)
    for c in range(8):
        b, r = divmod(c, 4)
        out[b, r * 2048:(r + 1) * 2048] = res.results[c]["out"]
    return out


I32 = mybir.dt.int32
TAUS = np.array(list(range(17)) + [32, 64, 128, 256, 512, 1024, 2048, 4096] + list(range(15, -1, -1)), np.float32)
NTAU = len(TAUS)


def _s5_consts():
    mk = np.zeros((128, 2, 16, 16), np.float32)
    idm = np.zeros((128, 2, 16, 16), np.float32)
    for kt2 in range(2):
        for sp in range(8):
            s = kt2 * 8 + sp
            for h in range(16):
                mk[sp * 16 + h, kt2, s:, :] = 1.0
                idm[sp * 16 + h, kt2, s, h] = 1.0
    return mk.reshape(128, 2, 256), idm.reshape(128, 2, 256)


def barrier(P):
    for e in P.ENG:
        for e2 in P.ENG:
            if P.cnt[e2] > 0:
                P._wait(e, ("c", e2, P.cnt[e2]))
        for q in P.dsem:
            j1 = P.dcnt[q]
            for j in range(max(0, j1 - NDS), j1):
                P._wait(e, ("d", q, j % NDS, 16 * (j // NDS + 1)))


def build_L1b(S=8192, fz=None):
    nc = fz["nc"] if fz else bass.Bass("TRN2", target_bir_lowering=False)
    pfx = fz["pfx"] if fz else ""

    def D(name, shape):
        if fz and name in fz["share"]:
            return fz["share"][name]
        return nc.dram_tensor(pfx + name, shape, F32, kind="ExternalInput").ap()
    x_d = D("x", [S, 1024]); npre_d = D("npre", [1024]); wu_d = D("wu", [1024, 256])
    lre_d = D("lre", [16, 64]); lim_d = D("lim", [16, 64]); bre_d = D("bre", [16, 64, 16]); bim_d = D("bim", [16, 64, 16])
    cre_d = D("cre", [16, 16, 64]); cim_d = D("cim", [16, 16, 64]); ldt_d = D("ldt", [16]); dd_d = D("dd", [256])
    taus_d = D("taus", [NTAU]); mk_d = D("mk", [128, 2, 256]); idm_d = D("idm", [128, 2, 256]); ident_d = D("ident", [128, 128])
    ys_d = fz["out"] if fz else nc.dram_tensor("ys", [S, 256], F32, kind="ExternalOutput").ap()
    NCH = S // 16
    NST = S // 512
    with ExitStack() as st:
        C = Ctx(nc, st, fz["P"], pfx) if fz else Ctx(nc, st); P = C.P
        idf, bidf, idb, bidb = make_ident(C, ident_d)
        ptr, bptr = C.ps("ptr", [128, 1024], BF16)
        py, bpy = C.ps("py", [128, 1024])
        G = []; bG = []
        for i in range(4):
            t_, b_ = C.ps("g%d" % i, [128, 512]); G.append(t_); bG.append(b_)
        U, bU = C.sb("U", [128, 2, 16, NCH], BF16)
        with ExitStack() as st2:
            C2 = Ctx(nc, st2, P, C.pfx)
            ext = fz.get("uTp") if fz else None
            if ext:
                uTp, buTp = ext
            else:
                uTp, buTp = C2.sb("uTp", [128, 2, 16, NCH], BF16)
            with ExitStack() as st1:
                C1 = Ctx(nc, st1, P, C.pfx)
                npre, bnpre = bcast_row_load(C1, "npre", npre_d, 1024)
                wu, bwu = load_w_bf16(C1, "wu", wu_d, 8, 256)
                xt, bxt = C1.sb("xt", [128, 1024])
                sq, bsq = C1.sb("sq", [128, 1024])
                hn, bhn = C1.sb("hn", [128, 1024], BF16)
                ss, bss = C1.sb("ss", [128, 1])
                hT, bhT = C1.sb("hT", [128, 8, 512], BF16)
                for s_ in range(0 if ext else NST):
                    for t in range(4):
                        r0 = s_ * 512 + t * 128
                        P.dma("sp", xt[:], x_d[r0:r0 + 128, :], writes=[bxt])
                        rms_rstd(C1, xt[:], bxt, 1024, sq[:], bsq, ss, bss)
                        P.op("dve", lambda E: E.scalar_tensor_tensor(out=hn[:], in0=xt[:], scalar=ss[:, 0:1], in1=npre[:],
                                                                     op0=ALU.mult, op1=ALU.mult), reads=[bxt, bss, bnpre], writes=[bhn])
                        transpose8(C1, hn, bhn, idb, bidb, ptr, bptr, hT[:, :, t * 128:(t + 1) * 128], bhT, eng="act")
                    for blk in range(2):
                        pb = G[blk]
                        fns = [(lambda E, kt=kt, blk=blk, pb=pb: E.matmul(
                            pb[:].rearrange("p (s n) -> p s n", s=16), lhsT=wu[:, kt, blk * 128:(blk + 1) * 128],
                            rhs=hT[:, kt, :].rearrange("p (n s) -> p s n", s=16), start=(kt == 0), stop=(kt == 7))) for kt in range(8)]
                        P.mm_group(fns, reads=[bwu, bhT], writes=[bG[blk]])
                        P.op("act" if blk == 0 else "dve",
                             (lambda E, blk=blk, pb=pb, s_=s_: E.copy(out=uTp[:, blk, :, 32 * s_:32 * s_ + 32], in_=pb[:].rearrange("p (s n) -> p s n", s=16)))
                             if blk == 0 else
                             (lambda E, blk=blk, pb=pb, s_=s_: E.tensor_copy(out=uTp[:, blk, :, 32 * s_:32 * s_ + 32], in_=pb[:].rearrange("p (s n) -> p s n", s=16))),
                             reads=[bG[blk]], writes=[buTp])
                barrier(P)
            ud2 = nc.dram_tensor(pfx + "ud2", [16, 2, 8, 16, NCH], BF16)
            bud2 = Buf("ud2", multi=True)
            bU.multi = True; bU.w = []
            for g in range(16):
                P.dma("sp", ud2.ap()[g].rearrange("k sp h n -> h (k sp) n"),
                      uTp[(g % 8) * 16:(g % 8 + 1) * 16, g // 8, :, :], reads=[buTp], writes=[bud2])
            for g in range(16):
                P.dma("sp", U[:, :, g, :], ud2.ap()[g].rearrange("k sp h n -> (sp h) k n"), reads=[bud2], writes=[bU])
            barrier(P)
        lre, blre = C.sb("lre", [128, 8]); lim, blim = C.sb("lim", [128, 8]); ldt, bldt = C.sb("ldt", [128, 8])
        TAU, bTAU = bcast_row_load(C, "TAU", taus_d, NTAU)
        Er, bEr = C.sb("Er", [128, 8, NTAU]); Ei, bEi = C.sb("Ei", [128, 8, NTAU]); NEi, bNEi = C.sb("NEi", [128, 8, NTAU])
        Hr, bHr = C.sb("Hr", [128, 8, 17, 16]); nHi, bnHi = C.sb("nHi", [128, 8, 17, 16])
        WbT, bWbT = C.sb("WbT", [128, 2, 8, 2, 128], BF16)
        Toep, bToep = C.sb("Toep", [128, 2, 16, 256], BF16)
        with ExitStack() as st3:
            C3 = Ctx(nc, st3, P, C.pfx)
            Br, bBr = C3.sb("Br", [128, 8, 16]); Bi, bBi = C3.sb("Bi", [128, 8, 16])
            Cr, bCr = C3.sb("Cr", [128, 8, 16]); Ci, bCi = C3.sb("Ci", [128, 8, 16])
            dcol, bdcol = C3.sb("dcol", [128, 16])
            MK, bMK = C3.sb("MK", [128, 2, 256]); IDM, bIDM = C3.sb("IDM", [128, 2, 256])
            P.dma("sp", MK[:], mk_d, writes=[bMK]); P.dma("sp", IDM[:], idm_d, writes=[bIDM])
            for _b in (blre, blim, bldt, bBr, bBi, bCr, bCi, bdcol):
                _b.multi = True; _b.w = []
            for two in range(2):
                hs = slice(64 * two, 64 * two + 64)
                P.dma("sp", lre[hs, :], lre_d.rearrange("(gp two) p -> two p gp", two=2)[two], writes=[blre])
                P.dma("sp", lim[hs, :], lim_d.rearrange("(gp two) p -> two p gp", two=2)[two], writes=[blim])
                P.dma("sp", ldt[hs, :], ldt_d.rearrange("(gp two) -> two gp", two=2)[two].partition_broadcast(64), writes=[bldt])
                P.dma("sp", Br[hs], bre_d.rearrange("(gp two) p h -> two p gp h", two=2)[two], writes=[bBr])
                P.dma("sp", Bi[hs], bim_d.rearrange("(gp two) p h -> two p gp h", two=2)[two], writes=[bBi])
                for gp in range(8):
                    P.dma("sp", Cr[hs, gp, :], cre_d[2 * gp + two].rearrange("h p -> p h"), writes=[bCr])
                    P.dma("sp", Ci[hs, gp, :], cim_d[2 * gp + two].rearrange("h p -> p h"), writes=[bCi])
            for sp in range(8):
                P.dma("sp", dcol[sp * 16:(sp + 1) * 16, :], dd_d.rearrange("(g h) -> h g", h=16), writes=[bdcol])
            sm = {}
            for nm in ("dt", "lr", "lrdt", "th", "den", "nr", "fre", "fim", "t8a", "t8b"):
                sm[nm] = C3.sb("sm_" + nm, [128, 8])
            T41 = {}
            for nm in ("ARG", "MARG", "MAG", "MAGN", "SIN", "COS", "ErN", "EiN", "rt", "rk"):
                T41[nm] = C3.sb("t41_" + nm, [128, 8, NTAU])
            rki, brki = C3.sb("rki", [128, 8, NTAU], I32)

            def tt(eng, out, bo, a, ba, b, bb_, op):
                P.op(eng, lambda E: E.tensor_tensor(out=out, in0=a, in1=b, op=op), reads=[ba, bb_], writes=[bo])

            dt, bdt = sm["dt"]; lr, blr = sm["lr"]; lrdt, blrdt = sm["lrdt"]; th, bth = sm["th"]
            P.op("act", lambda E: E.activation(out=dt[:], in_=ldt[:], func=AF.Exp), reads=[bldt], writes=[bdt])
            P.op("dve", lambda E: E.tensor_scalar(out=lr[:], in0=lre[:], scalar1=-1e-4, scalar2=None, op0=ALU.min), reads=[blre], writes=[blr])
            tt("dve", lrdt[:], blrdt, lr[:], blr, dt[:], bdt, ALU.mult)
            tt("dve", th[:], bth, lim[:], blim, dt[:], bdt, ALU.mult)
            ARG, bARG = T41["ARG"]; MARG, bMARG = T41["MARG"]; MAG, bMAG = T41["MAG"]; MAGN, bMAGN = T41["MAGN"]
            SIN, bSIN = T41["SIN"]; COS, bCOS = T41["COS"]; ErN, bErN = T41["ErN"]; EiN, bEiN = T41["EiN"]
            rt, brt = T41["rt"]; rk, brk = T41["rk"]
            tb = TAU[:].unsqueeze(1).to_broadcast([128, 8, NTAU])
            tt("dve", ARG[:], bARG, th[:].unsqueeze(2).to_broadcast([128, 8, NTAU]), bth, tb, bTAU, ALU.mult)
            tt("dve", MARG[:], bMARG, lrdt[:].unsqueeze(2).to_broadcast([128, 8, NTAU]), blrdt, tb, bTAU, ALU.mult)
            P.op("act", lambda E: E.activation(out=MAG[:], in_=MARG[:], func=AF.Exp), reads=[bMARG], writes=[bMAG])
            P.op("act", lambda E: E.activation(out=MAGN[:, :, 0:17], in_=MARG[:, :, 0:17], func=AF.Exp, scale=-1.0), reads=[bMARG], writes=[bMAGN])

            def sin_of(dst, bdst, shift):
                P.op("dve", lambda E: E.tensor_scalar(out=rt[:], in0=ARG[:], scalar1=float(shift), scalar2=None, op0=ALU.add), reads=[bARG], writes=[brt])
                P.op("dve", lambda E: E.tensor_scalar(out=rki[:], in0=rt[:], scalar1=float(1.0 / (2 * np.pi)), scalar2=None, op0=ALU.mult), reads=[brt], writes=[brki])
                P.op("dve", lambda E: E.tensor_copy(out=rk[:], in_=rki[:]), reads=[brki], writes=[brk])
                P.op("dve", lambda E: E.scalar_tensor_tensor(out=rt[:], in0=rk[:], scalar=float(-2 * np.pi), in1=rt[:], op0=ALU.mult, op1=ALU.add),
                     reads=[brk, brt], writes=[brt])
                P.op("dve", lambda E: E.tensor_scalar(out=rt[:], in0=rt[:], scalar1=-3.14159, scalar2=3.14159, op0=ALU.max, op1=ALU.min), reads=[brt], writes=[brt])
                P.op("act", lambda E: E.activation(out=dst[:], in_=rt[:], func=AF.Sin), reads=[brt], writes=[bdst])

            sin_of(SIN, bSIN, 0.0)
            sin_of(COS, bCOS, np.pi / 2)
            tt("dve", Er[:], bEr, MAG[:], bMAG, COS[:], bCOS, ALU.mult)
            tt("dve", Ei[:], bEi, MAG[:], bMAG, SIN[:], bSIN, ALU.mult)
            P.op("dve", lambda E: E.tensor_scalar(out=NEi[:], in0=Ei[:], scalar1=-1.0, scalar2=None, op0=ALU.mult), reads=[bEi], writes=[bNEi])
            tt("dve", ErN[:, :, 0:17], bErN, MAGN[:, :, 0:17], bMAGN, COS[:, :, 0:17], bCOS, ALU.mult)
            tt("dve", EiN[:, :, 0:17], bEiN, MAGN[:, :, 0:17], bMAGN, SIN[:, :, 0:17], bSIN, ALU.mult)
            P.op("dve", lambda E: E.tensor_scalar(out=EiN[:, :, 0:17], in0=EiN[:, :, 0:17], scalar1=-1.0, scalar2=None, op0=ALU.mult), reads=[bEiN], writes=[bEiN])
            den, bden = sm["den"]; nr, bnr = sm["nr"]; fre, bfre = sm["fre"]; fim, bfim = sm["fim"]; t8a, bt8a = sm["t8a"]; t8b, bt8b = sm["t8b"]
            tt("dve", den[:], bden, lr[:], blr, lr[:], blr, ALU.mult)
            tt("dve", t8a[:], bt8a, lim[:], blim, lim[:], blim, ALU.mult)
            tt("dve", den[:], bden, den[:], bden, t8a[:], bt8a, ALU.add)
            P.op("dve", lambda E: E.reciprocal(out=den[:], in_=den[:]), reads=[bden], writes=[bden])
            P.op("dve", lambda E: E.tensor_scalar(out=nr[:], in0=Er[:, :, 1], scalar1=-1.0, scalar2=None, op0=ALU.add), reads=[bEr], writes=[bnr])
            tt("dve", fre[:], bfre, nr[:], bnr, lr[:], blr, ALU.mult)
            tt("dve", t8a[:], bt8a, Ei[:, :, 1], bEi, lim[:], blim, ALU.mult)
            tt("dve", fre[:], bfre, fre[:], bfre, t8a[:], bt8a, ALU.add)
            tt("dve", fre[:], bfre, fre[:], bfre, den[:], bden, ALU.mult)
            tt("dve", fim[:], bfim, Ei[:, :, 1], bEi, lr[:], blr, ALU.mult)
            tt("dve", t8b[:], bt8b, nr[:], bnr, lim[:], blim, ALU.mult)
            tt("dve", fim[:], bfim, fim[:], bfim, t8b[:], bt8b, ALU.subtract)
            tt("dve", fim[:], bfim, fim[:], bfim, den[:], bden, ALU.mult)

            def cmul(outr, boutr, outi, bouti, ar, bar, ai, bai, br_, bbr_, bi_, bbi_, tmp, btmp):
                tt("dve", outr, boutr, ar, bar, br_, bbr_, ALU.mult)
                tt("dve", tmp, btmp, ai, bai, bi_, bbi_, ALU.mult)
                tt("dve", outr, boutr, outr, boutr, tmp, btmp, ALU.subtract)
                tt("dve", outi, bouti, ar, bar, bi_, bbi_, ALU.mult)
                tt("dve", tmp, btmp, ai, bai, br_, bbr_, ALU.mult)
                tt("dve", outi, bouti, outi, bouti, tmp, btmp, ALU.add)

            bbr, bbbr = C3.sb("bbr", [128, 8, 16]); bbi, bbbi = C3.sb("bbi", [128, 8, 16]); tmp16, btmp16 = C3.sb("tmp16", [128, 8, 16])
            fb = lambda t_: t_[:].unsqueeze(2).to_broadcast([128, 8, 16])
            cmul(bbr[:], bbbr, bbi[:], bbbi, fb(fre), bfre, fb(fim), bfim, Br[:], bBr, Bi[:], bBi, tmp16[:], btmp16)
            Gr, bGr = C3.sb("Gr", [128, 8, 16, 16]); Gi, bGi = C3.sb("Gi", [128, 8, 16, 16])
            WPr, bWPr = C3.sb("WPr", [128, 8, 16, 16]); WPi, bWPi = C3.sb("WPi", [128, 8, 16, 16])
            Hi, bHi = C3.sb("Hi", [128, 8, 17, 16]); tmpH, btmpH = C3.sb("tmpH", [128, 8, 17, 16])
            eb = lambda t_, j0, j1: t_[:, :, j0:j1].unsqueeze(3).to_broadcast([128, 8, j1 - j0, 16])
            vb = lambda t_, n_: t_[:].unsqueeze(2).to_broadcast([128, 8, n_, 16])
            cmul(Gr[:], bGr, Gi[:], bGi, eb(ErN, 0, 16), bErN, eb(EiN, 0, 16), bEiN, vb(bbr, 16), bbbr, vb(bbi, 16), bbbi, tmpH[:, :, 0:16, :], btmpH)
            cmul(WPr[:], bWPr, WPi[:], bWPi, eb(Er, 25, 41), bEr, eb(Ei, 25, 41), bEi, vb(bbr, 16), bbbr, vb(bbi, 16), bbbi, tmpH[:, :, 0:16, :], btmpH)
            cmul(Hr[:], bHr, Hi[:], bHi, eb(Er, 0, 17), bEr, eb(Ei, 0, 17), bEi, vb(Cr, 17), bCr, vb(Ci, 17), bCi, tmpH[:], btmpH)
            P.op("dve", lambda E: E.tensor_scalar(out=nHi[:], in0=Hi[:], scalar1=-1.0, scalar2=None, op0=ALU.mult), reads=[bHi], writes=[bnHi])
            for gp in range(8):
                for kt2 in range(2):
                    for c, (WP_, bWP_) in enumerate(((WPr, bWPr), (WPi, bWPi))):
                        P.op("pe", lambda E, gp=gp, kt2=kt2, WP_=WP_: E.transpose(
                            out=G[2][:, 0:128], in_=WP_[:, gp, kt2 * 8:(kt2 + 1) * 8, :].rearrange("p s h -> p (s h)"), identity=idf[:]),
                            reads=[bWP_, bidf], writes=[bG[2]])
                        P.op("act", lambda E, gp=gp, kt2=kt2, c=c: E.copy(out=WbT[:, kt2, gp, c, :], in_=G[2][:, 0:128]), reads=[bG[2]], writes=[bWbT])
            tmpT, btmpT = C3.sb("tmpT", [128, 256])
            for g in range(16):
                gp = g // 2; hs = slice(64 * (g % 2), 64 * (g % 2) + 64)
                for kt2 in range(2):
                    fns = [
                        lambda E, gp=gp, hs=hs, kt2=kt2: E.matmul(G[3][:, 0:256], lhsT=Gr[hs, gp, kt2 * 8:(kt2 + 1) * 8, :].rearrange("p s h -> p (s h)"),
                                                                  rhs=Hr[hs, gp, 0:16, :].rearrange("p t h -> p (t h)"), start=True, stop=False),
                        lambda E, gp=gp, hs=hs, kt2=kt2: E.matmul(G[3][:, 0:256], lhsT=Gi[hs, gp, kt2 * 8:(kt2 + 1) * 8, :].rearrange("p s h -> p (s h)"),
                                                                  rhs=nHi[hs, gp, 0:16, :].rearrange("p t h -> p (t h)"), start=False, stop=True)]
                    P.mm_group(fns, reads=[bGr, bGi, bHr, bnHi], writes=[bG[3]])
                    P.op("dve", lambda E, kt2=kt2: E.tensor_tensor(out=tmpT[:], in0=G[3][:, 0:256], in1=MK[:, kt2, :], op=ALU.mult),
                         reads=[bG[3], bMK], writes=[btmpT])
                    P.op("dve", lambda E, kt2=kt2, g=g: E.scalar_tensor_tensor(out=Toep[:, kt2, g, :], in0=IDM[:, kt2, :], scalar=dcol[:, g:g + 1], in1=tmpT[:],
                                                                               op0=ALU.mult, op1=ALU.add), reads=[bIDM, bdcol, btmpT], writes=[bToep])
            barrier(P)
        X = {}
        for bufn in ("A", "B"):
            for c in ("re", "im"):
                X[(bufn, c)] = (C.sb("X%s%s" % (bufn, c), [128, 8, NCH + 1])[0], [Buf("X%s%s%d" % (bufn, c, gp)) for gp in range(8)])
        Ysb, bYsb = C.sb("Ysb", [128, 16, 256], BF16 if (fz and fz.get("ybf16")) else F32)
        for key in X:
            t_, bl = X[key]
            P.op("dve", lambda E, t_=t_: E.memset(t_[:, :, 0:1], 0.0), writes=bl)
        for gp in range(8):
            for c, cn in enumerate(("re", "im")):
                px = G[c]
                fns = []
                for two in range(2):
                    g = 2 * gp + two
                    for kt2 in range(2):
                        fns.append(lambda E, two=two, g=g, kt2=kt2, gp=gp, c=c, px=px: E.matmul(
                            px[64 * two:64 * two + 64, :], lhsT=WbT[:, kt2, gp, c, 64 * two:64 * two + 64], rhs=U[:, kt2, g, :],
                            start=(kt2 == 0), stop=(kt2 == 1)))
                P.mm_group(fns, reads=[bWbT, bU], writes=[bG[c]])
                xt_, xb_ = X[("A", cn)]
                P.op("act", lambda E, xt_=xt_, gp=gp, px=px: E.copy(out=xt_[:, gp, 1:NCH + 1], in_=px[:]), reads=[bG[c]], writes=[xb_[gp]])
        for k in range(9):
            d = 1 << k
            j = 16 if k == 0 else 16 + k
            src, dst = ("A", "B") if k % 2 == 0 else ("B", "A")
            sre, bsre = X[(src, "re")]; sim, bsim = X[(src, "im")]
            dre, bdre = X[(dst, "re")]; dim_, bdim = X[(dst, "im")]
            P.op("dve", lambda E, dre=dre, sre=sre, d=d: E.tensor_copy(out=dre[:, :, 1:1 + d], in_=sre[:, :, 1:1 + d]), reads=bsre, writes=bdre)
            P.op("pool", lambda E, dim_=dim_, sim=sim, d=d: E.tensor_copy(out=dim_[:, :, 1:1 + d], in_=sim[:, :, 1:1 + d]), reads=bsim, writes=bdim)
            for gp in range(8):
                lo = slice(1, NCH + 1 - d); hi = slice(1 + d, NCH + 1)
                P.op("dve", lambda E, gp=gp, j=j, dre=dre, sre=sre, lo=lo, hi=hi: E.scalar_tensor_tensor(
                    out=dre[:, gp, hi], in0=sre[:, gp, lo], scalar=Er[:, gp, j:j + 1], in1=sre[:, gp, hi], op0=ALU.mult, op1=ALU.add),
                    reads=[bsre[gp], bEr], writes=[bdre[gp]])
                P.op("dve", lambda E, gp=gp, j=j, dre=dre, sim=sim, lo=lo, hi=hi: E.scalar_tensor_tensor(
                    out=dre[:, gp, hi], in0=sim[:, gp, lo], scalar=NEi[:, gp, j:j + 1], in1=dre[:, gp, hi], op0=ALU.mult, op1=ALU.add),
                    reads=[bsim[gp], bNEi, bdre[gp]], writes=[bdre[gp]])
                P.op("dve", lambda E, gp=gp, j=j, dim_=dim_, sim=sim, lo=lo, hi=hi: E.scalar_tensor_tensor(
                    out=dim_[:, gp, hi], in0=sim[:, gp, lo], scalar=Er[:, gp, j:j + 1], in1=sim[:, gp, hi], op0=ALU.mult, op1=ALU.add),
                    reads=[bsim[gp], bEr], writes=[bdim[gp]])
                P.op("dve", lambda E, gp=gp, j=j, dim_=dim_, sre=sre, lo=lo, hi=hi: E.scalar_tensor_tensor(
                    out=dim_[:, gp, hi], in0=sre[:, gp, lo], scalar=Ei[:, gp, j:j + 1], in1=dim_[:, gp, hi], op0=ALU.mult, op1=ALU.add),
                    reads=[bsre[gp], bEi, bdim[gp]], writes=[bdim[gp]])
        fre_, bfre_ = X[("B", "re")]; fim_, bfim_ = X[("B", "im")]
        bys = None if fz else Buf("ys", multi=True)
        ysv = ys_d.rearrange("(n t) c -> n t c", t=16)
        for jt in range(NCH // 128):
            for gq in range(4):
                fns = []
                for gi in range(4):
                    g = 4 * gq + gi; gp = g // 2; hs = slice(64 * (g % 2), 64 * (g % 2) + 64)
                    o_ = (gi * 256, (gi + 1) * 256)
                    for kt2 in range(2):
                        fns.append(lambda E, o_=o_, g=g, kt2=kt2, jt=jt: E.matmul(
                            py[:, o_[0]:o_[1]], lhsT=U[:, kt2, g, jt * 128:(jt + 1) * 128], rhs=Toep[:, kt2, g, :], start=(kt2 == 0), stop=False))
                    fns.append(lambda E, o_=o_, gp=gp, hs=hs, jt=jt: E.matmul(
                        py[:, o_[0]:o_[1]], lhsT=fre_[hs, gp, jt * 128:(jt + 1) * 128], rhs=Hr[hs, gp, 1:17, :].rearrange("p t h -> p (t h)"),
                        start=False, stop=False))
                    fns.append(lambda E, o_=o_, gp=gp, hs=hs, jt=jt: E.matmul(
                        py[:, o_[0]:o_[1]], lhsT=fim_[hs, gp, jt * 128:(jt + 1) * 128], rhs=nHi[hs, gp, 1:17, :].rearrange("p t h -> p (t h)"),
                        start=False, stop=True))
                P.mm_group(fns, reads=[bU, bToep, bHr, bnHi] + bfre_ + bfim_, writes=[bpy])
                P.op("act" if gq % 2 == 0 else "dve",
                     (lambda E, gq=gq: E.copy(out=Ysb[:].rearrange("p t (g h) -> p g t h", h=16)[:, 4 * gq:4 * gq + 4],
                                              in_=py[:].rearrange("p (g t h) -> p g t h", g=4, h=16)))
                     if gq % 2 == 0 else
                     (lambda E, gq=gq: E.tensor_copy(out=Ysb[:].rearrange("p t (g h) -> p g t h", h=16)[:, 4 * gq:4 * gq + 4],
                                                     in_=py[:].rearrange("p (g t h) -> p g t h", g=4, h=16))),
                     reads=[bpy], writes=[bYsb])
            P.dma("sp", ysv[jt * 128:(jt + 1) * 128, :, :], Ysb[:], reads=[bYsb], writes=[fz["obuf_of"](jt) if fz else bys])
            if fz:
                fz["after_chunk"](jt)
        if fz:
            barrier(P)
        else:
            P.finish([bys])
    return nc


def run_L1b(inp):
    nc = _get("L1b", build_L1b)
    mk, idm = _s5_consts()
    w_in = inp["w_in_even"][0]
    maps = []
    for c in range(8):
        b, r = divmod(c, 4)
        gs = slice(16 * r, 16 * r + 16)
        maps.append({"x": np.ascontiguousarray(inp["x"][b]), "npre": np.ascontiguousarray(inp["norm_pre"][0]),
                     "wu": np.ascontiguousarray(w_in[:, 4112 + 256 * r:4112 + 256 * (r + 1)]),
                     "lre": np.ascontiguousarray(inp["s5_lam_re"][0, gs]), "lim": np.ascontiguousarray(inp["s5_lam_im"][0, gs]),
                     "bre": np.ascontiguousarray(inp["s5_b_re"][0, gs]), "bim": np.ascontiguousarray(inp["s5_b_im"][0, gs]),
                     "cre": np.ascontiguousarray(inp["s5_c_re"][0, gs]), "cim": np.ascontiguousarray(inp["s5_c_im"][0, gs]),
                     "ldt": np.ascontiguousarray(inp["s5_log_dt"][0, gs]), "dd": np.ascontiguousarray(inp["s5_d"][0, 256 * r:256 * (r + 1)]),
                     "taus": TAUS, "mk": mk, "idm": idm, "ident": _IDENT})
    res = run_bass_kernel_spmd(nc, maps, core_ids=list(range(8)))
    ys = np.empty((2, 8192, 1024), np.float32)
    for c in range(8):
        b, r = divmod(c, 4)
        ys[b, :, 256 * r:256 * (r + 1)] = res.results[c]["ys"]
    return ys


def _gdn_consts():
    p = np.arange(64)[:, None]; f = np.arange(64)[None, :]
    negu = np.where(f >= p, 0.0, -30000.0)
    negls = np.where(f < p, 0.0, -30000.0)
    nsu = np.where(f > p, -1.0, 0.0)
    i64 = np.eye(64)
    c64 = np.stack([negu, negls, nsu, i64], axis=1).astype(np.float32)
    cmask = np.ones((2, 512), np.float32); cmask[:, 0::64] = 0.0
    sel = np.zeros((2, 2, 128), np.float32); sel[0, 0, :] = 1.0; sel[1, 1, :] = 1.0
    return c64, cmask, sel


def build_L1a(S=8192, fz=None):
    nc = fz["nc"] if fz else bass.Bass("TRN2", target_bir_lowering=False)
    pfx = fz["pfx"] if fz else ""

    def D(name, shape):
        if fz and name in fz["share"]:
            return fz["share"][name]
        return nc.dram_tensor(pfx + name, shape, F32, kind="ExternalInput").ap()
    x_d = D("x", [S, 1024]); npre_d = D("npre", [1024]); w_d = D("w", [1024, 768]); wb_d = D("wb", [1024, 2]); wa_d = D("wa", [1024, 2])
    conv_d = D("conv", [4, 768]); alog_d = D("alog", [2]); dtb_d = D("dtb", [2])
    ident_d = D("ident", [128, 128]); c64_d = D("c64", [64, 4, 64]); cmask_d = D("cmask", [2, 512]); sel_d = D("sel", [2, 2, 128])
    ones_d = D("ones", [128, 128])
    o_d = fz["out"] if fz else nc.dram_tensor("o", [S, 256], F32, kind="ExternalOutput").ap()
    NST = S // 512
    with ExitStack() as st:
        C = Ctx(nc, st, fz["P"], pfx) if fz else Ctx(nc, st); P = C.P
        idf, bidf, idb, bidb = make_ident(C, ident_d)
        npre, bnpre = bcast_row_load(C, "npre", npre_d, 1024)
        w, bw = load_w_bf16(C, "w", w_d, 8, 768)
        wb, bwb = load_w_bf16(C, "wb", wb_d, 8, 2)
        wa, bwa = load_w_bf16(C, "wa", wa_d, 8, 2)
        cw, bcw = C.sb("cw", [128, 4, 6])
        P.dma("sp", cw[:], conv_d.rearrange("j (c p) -> p j c", p=128), writes=[bcw])
        extu = fz.get("uTp") if fz else None
        if extu:
            wu_d = D("wu", [1024, 256])
            wu, bwu = load_w_bf16(C, "wu", wu_d, 8, 256)
            uTp, buTp = extu
        c64, bc64 = C.sb("c64", [64, 4, 64]); P.dma("sp", c64[:], c64_d, writes=[bc64])
        NEGU = c64[:, 0, :]; NEGLS = c64[:, 1, :]; NSU = c64[:, 2, :]; I64 = c64[:, 3, :]
        cmask, bcmask = C.sb("cmask", [2, 512]); P.dma("sp", cmask[:], cmask_d, writes=[bcmask])
        sel, bsel = C.sb("sel", [2, 2, 128]); P.dma("sp", sel[:], sel_d, writes=[bsel])
        ones, bones = C.sb("ones", [128, 128]); P.dma("sp", ones[:], ones_d, writes=[bones])
        onesb, bonesb = C.sb("onesb", [128, 128], BF16)
        P.op("dve", lambda E: E.tensor_copy(out=onesb[:], in_=ones[:]), reads=[bones], writes=[bonesb])
        sqb, bsqb = C.sb("sqb", [128, 512], BF16)
        alog, balog = C.sb("alog", [2, 1]); P.dma("sp", alog[:], alog_d.rearrange("(a b) -> a b", b=1), writes=[balog])
        dtb, bdtb = C.sb("dtb", [2, 1]); P.dma("sp", dtb[:], dtb_d.rearrange("(a b) -> a b", b=1), writes=[bdtb])
        negA, bnegA = C.sb("negA", [2, 1])
        P.op("act", lambda E: E.activation(out=negA[:], in_=alog[:], func=AF.Exp), reads=[balog], writes=[bnegA])
        P.op("dve", lambda E: E.tensor_scalar(out=negA[:], in0=negA[:], scalar1=-1.0, scalar2=None, op0=ALU.mult), reads=[bnegA], writes=[bnegA])
        xt, bxt = C.sb("xt", [128, 1024]); sq, bsq = C.sb("sq", [128, 1024], BF16); hn, bhn = C.sb("hn", [128, 1024], BF16)
        ss, bss = C.sb("ss", [128, 1]); hT, bhT = C.sb("hT", [128, 8, 512], BF16)
        raw, _ = C.sb("raw", [128, 6, 515]); braw = [Buf("raw%d" % i) for i in range(6)]
        cvq, bcvq = C.sb("cvq", [128, 512])
        act, _ = C.sb("act", [128, 4, 512]); bact = [Buf("act%d" % i) for i in range(4)]
        vbuf2 = []; qk2 = []; bqk2 = []
        for par_ in range(2):
            vt_, _ = C.sb("vbuf%d" % par_, [128, 2, 512]); vbuf2.append((vt_, [Buf("vb%d_%d" % (par_, i)) for i in range(2)]))
            qt_, _ = C.sb("qk%d" % par_, [128, 4, 512]); qk2.append(qt_); bqk2.append([Buf("qk%d_%d" % (par_, i)) for i in range(4)])
        rn, brn = C.sb("rn", [128, 512])
        brow, bbrow = C.sb("brow", [2, 512]); grow, bgrow = C.sb("grow", [2, 512]); gcrow, bgcrow = C.sb("gcrow", [2, 512])
        GCB2 = []; BB2 = []
        for par_ in range(2):
            GCB2.append([C.sb("GCB%d_%d" % (par_, h), [128, 512]) for h in range(2)])
            BB2.append([C.sb("BB%d_%d" % (par_, h), [128, 512]) for h in range(2)])
        m64h = []; smallh = []
        for h in range(2):
            d_ = {}
            for nm in ("arg1", "scr"):
                d_[nm] = C.sb("m%d_%s" % (h, nm), [64, 512])
            for nm in ("DT", "Ds", "tmp", "Pa", "Pb", "Qa", "Qb"):
                d_[nm] = C.sb("m%d_%s" % (h, nm), [64, 512], BF16)
            m64h.append(d_)
        heads = []
        for h in range(2):
            H = {}
            H["attnT"] = C.sb("attnT%d" % h, [64, 512], BF16); H["Y"] = C.sb("Y%d" % h, [64, 512]); H["Ybf"] = C.sb("Ybf%d" % h, [64, 512], BF16)
            H["EG"] = C.sb("EG%d" % h, [128, 512]); H["qdec"] = C.sb("qdec%d" % h, [128, 512], BF16)
            H["kTb"] = C.sb("kTb%d" % h, [128, 512], BF16); H["Sbf"] = C.sb("Sbf%d" % h, [128, 128], BF16)
            H["bv"] = C.sb("bv%d" % h, [64, 8, 128]); H["kdec"] = C.sb("kdec%d" % h, [64, 8, 128], BF16)
            H["nbg"] = C.sb("nbg%d" % h, [64, 8]); H["osb"] = C.sb("osb%d" % h, [128, 8, 128])
            H["vnew"] = C.sb("vnew%d" % h, [64, 128], BF16); H["rhs2"] = C.sb("rhs2%d" % h, [64, 128], BF16)
            heads.append(H)
        for h in range(2):
            d_ = {}
            for nm in ("gccol", "bcol", "nbcol", "elast", "egc"):
                d_[nm] = C.sb("s%d_%s" % (h, nm), [64, 8])
            smallh.append(d_)
        Sst = [C.sb("S%d" % h, [128, 128]) for h in range(2)]
        for h in range(2):
            P.op("dve", lambda E, h=h: E.memset(Sst[h][0][:], 0.0), writes=[Sst[h][1]])
            P.op("dve", lambda E, h=h: E.memset(heads[h]["Sbf"][0][:], 0.0), writes=[heads[h]["Sbf"][1]])
        P.op("dve", lambda E: E.memset(raw[:, :, 0:3], 0.0), writes=braw)
        ptr, bptr = C.ps("ptr", [128, 1024], BF16)
        G = [C.ps("gp%d" % i, [128, 512]) for i in range(7)]
        GP = G[0:4]
        GA = G[4:7]
        BKS = [(GP[0], GP[1], GP[2]), (GP[3], GA[0], GA[1])]
        ga_ctr = [0]

        def next_ga():
            ga_ctr[0] += 1
            return GA[ga_ctr[0] % 3]
        bo = None if fz else Buf("o", multi=True)
        if fz is not None and fz.get("debug"):
            print("L1a sbuf remaining", nc.sbuf_bytes_remaining)

        def tt(out, bo_, a, ba, b, bb_, op, eng="dve"):
            P.op(eng, lambda E: E.tensor_tensor(out=out, in0=a, in1=b, op=op), reads=ba if isinstance(ba, list) else [ba], writes=[bo_])

        def stageA(s_):
            par = s_ % 2
            qk = qk2[par]; bqk = bqk2[par]; GCB = GCB2[par]; BB = BB2[par]; vb, bvb = vbuf2[par]
            for t in range(4):
                r0 = s_ * 512 + t * 128
                P.dma("sp", xt[:], x_d[r0:r0 + 128, :], writes=[bxt])
                rms_rstd(C, xt[:], bxt, 1024, sq[:], bsq, ss, bss)
                P.op("dve", lambda E: E.scalar_tensor_tensor(out=hn[:], in0=xt[:], scalar=ss[:, 0:1], in1=npre[:],
                                                             op0=ALU.mult, op1=ALU.mult), reads=[bxt, bss, bnpre], writes=[bhn])
                transpose8(C, hn, bhn, idb, bidb, ptr, bptr, hT[:, :, t * 128:(t + 1) * 128], bhT, eng="act")
                yield
            for ct in range(6):
                pa, bpa = next_ga()
                fns = [(lambda E, kt=kt, ct=ct, pa=pa: E.matmul(pa[:], lhsT=w[:, kt, ct * 128:(ct + 1) * 128], rhs=hT[:, kt, :],
                                                                start=(kt == 0), stop=(kt == 7))) for kt in range(8)]
                P.mm_group(fns, reads=[bw, bhT], writes=[bpa])
                P.op("act", lambda E, ct=ct, pa=pa: E.copy(out=raw[:, ct, 3:515], in_=pa[:]), reads=[bpa], writes=[braw[ct]])
                P.op("dve", lambda E, ct=ct: E.tensor_scalar(out=cvq[:], in0=raw[:, ct, 0:512], scalar1=cw[:, 0, ct:ct + 1], scalar2=None, op0=ALU.mult),
                     reads=[braw[ct], bcw], writes=[bcvq])
                for j in range(1, 4):
                    P.op("dve", lambda E, ct=ct, j=j: E.scalar_tensor_tensor(out=cvq[:], in0=raw[:, ct, j:j + 512], scalar=cw[:, j, ct:ct + 1], in1=cvq[:],
                                                                             op0=ALU.mult, op1=ALU.add), reads=[braw[ct], bcw, bcvq], writes=[bcvq])
                P.op("act", lambda E, ct=ct: E.copy(out=raw[:, ct, 0:3], in_=raw[:, ct, 512:515]), reads=[braw[ct]], writes=[braw[ct]])
                if ct < 4:
                    P.op("act", lambda E, ct=ct: E.activation(out=act[:, ct, :], in_=cvq[:], func=AF.Silu), reads=[bcvq], writes=[bact[ct]])
                else:
                    P.op("act", lambda E, ct=ct, vb=vb: E.activation(out=vb[:, ct - 4, :], in_=cvq[:], func=AF.Silu), reads=[bcvq], writes=[bvb[ct - 4]])
                yield
            if extu:
                for blk in range(2):
                    pa, bpa = next_ga()
                    fns = [(lambda E, kt=kt, blk=blk, pa=pa: E.matmul(
                        pa[:].rearrange("p (s n) -> p s n", s=16), lhsT=wu[:, kt, blk * 128:(blk + 1) * 128],
                        rhs=hT[:, kt, :].rearrange("p (n s) -> p s n", s=16), start=(kt == 0), stop=(kt == 7))) for kt in range(8)]
                    P.mm_group(fns, reads=[bwu, bhT], writes=[bpa])
                    P.op("act", lambda E, blk=blk, pa=pa, s_=s_: E.copy(out=uTp[:, blk, :, 32 * s_:32 * s_ + 32], in_=pa[:].rearrange("p (s n) -> p s n", s=16)),
                         reads=[bpa], writes=[buTp])
                    yield
            for ct in range(4):
                pa, bpa = next_ga()
                P.op("act", lambda E, ct=ct: E.activation(out=sqb[:], in_=act[:, ct, :], func=AF.Square), reads=[bact[ct]], writes=[bsqb])
                P.op("pe", lambda E, pa=pa: E.matmul(pa[:], lhsT=onesb[:], rhs=sqb[:], start=True, stop=True), reads=[bonesb, bsqb], writes=[bpa])
                P.op("act", lambda E, pa=pa: E.activation(out=rn[:], in_=pa[:], func=AF.Ln, bias=1e-6, scale=1.0), reads=[bpa], writes=[brn])
                P.op("act", lambda E: E.activation(out=rn[:], in_=rn[:], func=AF.Exp, scale=-0.5), reads=[brn], writes=[brn])
                if ct < 2:
                    P.op("dve", lambda E, ct=ct, qk=qk: E.scalar_tensor_tensor(out=qk[:, ct, :], in0=act[:, ct, :], scalar=float(128 ** -0.5), in1=rn[:],
                                                                               op0=ALU.mult, op1=ALU.mult), reads=[bact[ct], brn], writes=[bqk[ct]])
                else:
                    P.op("dve", lambda E, ct=ct, qk=qk: E.tensor_tensor(out=qk[:, ct, :], in0=act[:, ct, :], in1=rn[:], op=ALU.mult),
                         reads=[bact[ct], brn], writes=[bqk[ct]])
                yield
            pa, bpa = next_ga()
            fns = [(lambda E, kt=kt, pa=pa: E.matmul(pa[0:2, :], lhsT=wb[:, kt, 0:2], rhs=hT[:, kt, :], start=(kt == 0), stop=(kt == 7))) for kt in range(8)]
            P.mm_group(fns, reads=[bwb, bhT], writes=[bpa])
            P.op("act", lambda E, pa=pa: E.activation(out=brow[:], in_=pa[0:2, :], func=AF.Sigmoid), reads=[bpa], writes=[bbrow])
            pa2, bpa2 = next_ga()
            fns = [(lambda E, kt=kt, pa2=pa2: E.matmul(pa2[0:2, :], lhsT=wa[:, kt, 0:2], rhs=hT[:, kt, :], start=(kt == 0), stop=(kt == 7))) for kt in range(8)]
            P.mm_group(fns, reads=[bwa, bhT], writes=[bpa2])
            P.op("act", lambda E, pa2=pa2: E.activation(out=grow[:], in_=pa2[0:2, :], func=AF.Exp, bias=dtb[:, 0:1], scale=1.0), reads=[bpa2, bdtb], writes=[bgrow])
            P.op("act", lambda E: E.activation(out=grow[:], in_=grow[:], func=AF.Ln, bias=1.0, scale=1.0), reads=[bgrow], writes=[bgrow])
            P.op("dve", lambda E: E.tensor_scalar(out=grow[:], in0=grow[:], scalar1=negA[:, 0:1], scalar2=None, op0=ALU.mult), reads=[bgrow, bnegA], writes=[bgrow])
            P.op("dve", lambda E: E.tensor_tensor_scan(out=gcrow[:], data0=cmask[:], data1=grow[:], initial=0.0, op0=ALU.mult, op1=ALU.add),
                 reads=[bcmask, bgrow], writes=[bgcrow])
            yield
            for h in range(2):
                pa, bpa = next_ga()
                P.op("pe", lambda E, h=h, pa=pa: E.matmul(pa[:], lhsT=sel[:, h, :], rhs=gcrow[:], start=True, stop=True), reads=[bsel, bgcrow], writes=[bpa])
                P.op("act", lambda E, h=h, pa=pa, GCB=GCB: E.copy(out=GCB[h][0][:], in_=pa[:]), reads=[bpa], writes=[GCB[h][1]])
                pa, bpa = next_ga()
                P.op("pe", lambda E, h=h, pa=pa: E.matmul(pa[:], lhsT=sel[:, h, :], rhs=brow[:], start=True, stop=True), reads=[bsel, bbrow], writes=[bpa])
                P.op("act", lambda E, h=h, pa=pa, BB=BB: E.copy(out=BB[h][0][:], in_=pa[:]), reads=[bpa], writes=[BB[h][1]])
                yield

        for _ in stageA(0):
            pass
        for s_ in range(NST):
            par = s_ % 2
            qk = qk2[par]; bqk = bqk2[par]; GCB = GCB2[par]; BB = BB2[par]; vb, bvb = vbuf2[par]
            nxt = stageA(s_ + 1) if s_ + 1 < NST else None

            def advance(k):
                if nxt is not None:
                    for _ in range(k):
                        next(nxt, None)
            def stageB(h, qk=qk, bqk=bqk, GCB=GCB, BB=BB, vb=vb, bvb=bvb):
                m64 = m64h[h]; small = smallh[h]; BK = BKS[h]
                qT = qk[:, h, :]; bqT = bqk[h]; kT = qk[:, 2 + h, :]; bkT = bqk[2 + h]; vT = vb[:, h, :]; bvT = bvb[h]
                gcb, bgcb = GCB[h]; bb, bbb = BB[h]
                H = heads[h]
                attnT, battnT = H["attnT"]; Y, bY = H["Y"]; EG, bEG = H["EG"]; qdec, bqdec = H["qdec"]
                Ybf, bYbf = H["Ybf"]
                bv, bbv = H["bv"]; kdec, bkdec = H["kdec"]; nbg, bnbg = H["nbg"]
                arg1, barg1 = m64["arg1"]; scr, bscr = m64["scr"]; DT, bDT = m64["DT"]; Ds, bDs = m64["Ds"]
                tmp, btmp = m64["tmp"]
                gccol, bgccol = small["gccol"]; bcol, bbcol = small["bcol"]; nbcol, bnbcol = small["nbcol"]
                elast, belast = small["elast"]; egc, begc = small["egc"]
                v3 = lambda t_: t_[:].rearrange("p (n f) -> p n f", f=64)
                i64b = I64.unsqueeze(1).to_broadcast([64, 8, 64])
                tt(v3(scr), bscr, gcb[0:64, :].rearrange("p (n f) -> p n f", f=64), [bgcb, bc64], i64b, bc64, ALU.mult)
                P.op("dve", lambda E, scr=scr, gccol=gccol: E.tensor_reduce(out=gccol[:], in_=scr[:].rearrange("p (n f) -> p n f", f=64), axis=AX.X, op=ALU.add), reads=[bscr], writes=[bgccol])
                tt(v3(scr), bscr, bb[0:64, :].rearrange("p (n f) -> p n f", f=64), [bbb, bc64], i64b, bc64, ALU.mult)
                P.op("dve", lambda E, scr=scr, bcol=bcol: E.tensor_reduce(out=bcol[:], in_=scr[:].rearrange("p (n f) -> p n f", f=64), axis=AX.X, op=ALU.add), reads=[bscr], writes=[bbcol])
                yield
                P.op("dve", lambda E: E.tensor_scalar(out=nbcol[:], in0=bcol[:], scalar1=-1.0, scalar2=None, op0=ALU.mult), reads=[bbcol], writes=[bnbcol])
                tt(v3(arg1), barg1, gcb[0:64, :].rearrange("p (n f) -> p n f", f=64), [bgcb, bgccol], gccol[:].unsqueeze(2).to_broadcast([64, 8, 64]), bgccol, ALU.subtract)
                tt(v3(scr), bscr, v3(arg1), [barg1, bc64], NEGU.unsqueeze(1).to_broadcast([64, 8, 64]), bc64, ALU.add)
                P.op("act", lambda E: E.activation(out=DT[:], in_=scr[:], func=AF.Exp), reads=[bscr], writes=[bDT])
                yield
                P.op("dve", lambda E: E.scalar_tensor_tensor(out=scr[:].rearrange("p (n f) -> p n f", f=64), in0=arg1[:].rearrange("p (n f) -> p n f", f=64), scalar=-1.0,
                                                             in1=NEGLS.unsqueeze(1).to_broadcast([64, 8, 64]), op0=ALU.mult, op1=ALU.add), reads=[barg1, bc64], writes=[bscr])
                P.op("act", lambda E: E.activation(out=Ds[:], in_=scr[:], func=AF.Exp), reads=[bscr], writes=[bDs])
                pk, bpk = BK[0]; pq, bpq = BK[1]
                fns = [(lambda E, n=n, pk=pk, kT=kT: E.matmul(pk[0:64, n * 64:(n + 1) * 64], lhsT=kT[:, n * 64:(n + 1) * 64], rhs=kT[:, n * 64:(n + 1) * 64],
                                                              start=True, stop=True)) for n in range(8)]
                P.mm_group(fns, reads=[bkT], writes=[bpk])
                fns = [(lambda E, n=n, pq=pq, kT=kT, qT=qT: E.matmul(pq[0:64, n * 64:(n + 1) * 64], lhsT=kT[:, n * 64:(n + 1) * 64], rhs=qT[:, n * 64:(n + 1) * 64],
                                                                     start=True, stop=True)) for n in range(8)]
                P.mm_group(fns, reads=[bkT, bqT], writes=[bpq])
                yield
                tt(attnT[:], battnT, pq[0:64, :], [bpq, bDT], DT[:], bDT, ALU.mult)
                Pc, bPc = m64["Pa"]; Pn, bPn = m64["Pb"]; Qc, bQc = m64["Qa"]; Qn, bQn = m64["Qb"]
                tt(tmp[:], btmp, pk[0:64, :], [bpk, bDT], DT[:], bDT, ALU.mult)
                tt(tmp[:], btmp, tmp[:], [btmp, bbb], bb[0:64, :], bbb, ALU.mult)
                tt(v3(Qc), bQc, v3(tmp), [btmp, bc64], NSU.unsqueeze(1).to_broadcast([64, 8, 64]), bc64, ALU.mult)
                yield
                tt(tmp[:], btmp, pk[0:64, :], [bpk, bDs], Ds[:], bDs, ALU.mult)
                tt(v3(Pc), bPc, v3(tmp), [btmp, bnbcol], nbcol[:].unsqueeze(2).to_broadcast([64, 8, 64]), bnbcol, ALU.mult)
                tt(v3(Y), bY, v3(Qc), [bQc, bc64], i64b, bc64, ALU.add)
                P.op("act", lambda E, Ybf=Ybf, Y=Y: E.copy(out=Ybf[:], in_=Y[:]), reads=[bY], writes=[bYbf])
                yield
                for j in range(5):
                    pP, bpP = BK[2]; pQ, bpQ = BK[1]
                    fns = [(lambda E, n=n, pP=pP, Qc=Qc, Pc=Pc: E.matmul(pP[0:64, n * 64:(n + 1) * 64], lhsT=Qc[:, n * 64:(n + 1) * 64], rhs=Pc[:, n * 64:(n + 1) * 64],
                                                                         start=True, stop=True)) for n in range(8)]
                    P.mm_group(fns, reads=[bQc, bPc], writes=[bpP])
                    if j < 4:
                        fns = [(lambda E, n=n, pQ=pQ, Qc=Qc, Pc=Pc: E.matmul(pQ[0:64, n * 64:(n + 1) * 64], lhsT=Pc[:, n * 64:(n + 1) * 64], rhs=Qc[:, n * 64:(n + 1) * 64],
                                                                             start=True, stop=True)) for n in range(8)]
                        P.mm_group(fns, reads=[bQc, bPc], writes=[bpQ])
                    yield
                    P.op("act", lambda E, Pn=Pn, pP=pP: E.copy(out=Pn[:], in_=pP[0:64, :]), reads=[bpP], writes=[bPn])
                    if j < 4:
                        P.op("dve", lambda E, Qn=Qn, pQ=pQ: E.tensor_copy(out=Qn[:], in_=pQ[0:64, :]), reads=[bpQ], writes=[bQn])
                    pY, bpY = BK[0]
                    fns = [(lambda E, n=n, pY=pY, Pn=Pn, Ybf=Ybf: E.matmul(pY[0:64, n * 64:(n + 1) * 64], lhsT=Pn[:, n * 64:(n + 1) * 64], rhs=Ybf[:, n * 64:(n + 1) * 64],
                                                                         start=True, stop=True)) for n in range(8)]
                    P.mm_group(fns, reads=[bPn, bYbf], writes=[bpY])
                    yield
                    tt(Y[:], bY, Y[:], [bY, bpY], pY[0:64, :], bpY, ALU.add)
                    P.op("act", lambda E, Ybf=Ybf, Y=Y: E.copy(out=Ybf[:], in_=Y[:]), reads=[bY], writes=[bYbf])
                    Pc, bPc, Pn, bPn = Pn, bPn, Pc, bPc
                    Qc, bQc, Qn, bQn = Qn, bQn, Qc, bQc
                for hf in range(2):
                    pth, bpth = BK[1 + hf]
                    fns = [(lambda E, n=n, vT=vT, pth=pth, hf=hf: E.transpose(out=pth[0:64, n * 128:(n + 1) * 128], in_=vT[:, (4 * hf + n) * 64:(4 * hf + n + 1) * 64],
                                                                              identity=idf[:])) for n in range(4)]
                    P.mm_group(fns, reads=[bvT, bidf], writes=[bpth])
                    tt(bv[:, 4 * hf:4 * hf + 4, :], bbv, pth[0:64, :].rearrange("p (n d) -> p n d", d=128), [bpth, bbcol],
                       bcol[:, 4 * hf:4 * hf + 4].unsqueeze(2).to_broadcast([64, 4, 128]), bbcol, ALU.mult)
                yield
                tt(elast[:], belast, gcb[0:64, :].rearrange("p (n f) -> p n f", f=64)[:, :, 63], [bgcb, bgccol], gccol[:], bgccol, ALU.subtract)
                P.op("act", lambda E: E.activation(out=elast[:], in_=elast[:], func=AF.Exp), reads=[belast], writes=[belast])
                for hf in range(2):
                    pth, bpth = BK[1 + hf]
                    fns = [(lambda E, n=n, kT=kT, pth=pth, hf=hf: E.transpose(out=pth[0:64, n * 128:(n + 1) * 128], in_=kT[:, (4 * hf + n) * 64:(4 * hf + n + 1) * 64],
                                                                              identity=idf[:])) for n in range(4)]
                    P.mm_group(fns, reads=[bkT, bidf], writes=[bpth])
                    tt(kdec[:, 4 * hf:4 * hf + 4, :], bkdec, pth[0:64, :].rearrange("p (n d) -> p n d", d=128), [bpth, belast],
                       elast[:, 4 * hf:4 * hf + 4].unsqueeze(2).to_broadcast([64, 4, 128]), belast, ALU.mult)
                yield
                P.op("act", lambda E, gcb=gcb, EG=EG: E.activation(out=EG[:], in_=gcb[:], func=AF.Exp), reads=[bgcb], writes=[bEG])
                tt(qdec[:], bqdec, qT, [bqT, bEG], EG[:], bEG, ALU.mult)
                kTb, bkTb = H["kTb"]
                P.op("act", lambda E, kTb=kTb, kT=kT: E.copy(out=kTb[:], in_=kT), reads=[bkT], writes=[bkTb])
                P.op("act", lambda E: E.activation(out=egc[:], in_=gccol[:], func=AF.Exp), reads=[bgccol], writes=[begc])
                P.op("dve", lambda E, nbg=nbg: E.scalar_tensor_tensor(out=nbg[:], in0=egc[:], scalar=-1.0, in1=bcol[:], op0=ALU.mult, op1=ALU.mult),
                     reads=[begc, bbcol], writes=[bnbg])
            gensB = [stageB(0), stageB(1)]
            aliveB = True
            while aliveB:
                aliveB = False
                for g_ in gensB:
                    try:
                        next(g_)
                        aliveB = True
                    except StopIteration:
                        pass
            banks = [(GP[0], GP[1]), (GP[2], GP[3])]
            for n in range(8):
                cs = slice(n * 64, (n + 1) * 64)
                for h in range(2):
                    H = heads[h]; S, bS = Sst[h]
                    kT, bkT = H["kTb"]; Sbf, bSbf = H["Sbf"]
                    attnT, battnT = H["attnT"]; Y, bY = H["Ybf"]; EG, bEG = H["EG"]; qdec, bqdec = H["qdec"]
                    bv, bbv = H["bv"]; kdec, bkdec = H["kdec"]; nbg, bnbg = H["nbg"]
                    vnew, bvnew = H["vnew"]; rhs2, brhs2 = H["rhs2"]; osb, bosb = H["osb"]
                    (KSO, bKSO), (Sb, bSb) = banks[h]
                    Vb, bVb = KSO, bKSO
                    P.op("pe", lambda E, cs=cs, kT=kT, Sbf=Sbf, KSO=KSO: E.matmul(KSO[0:64, 0:128], lhsT=kT[:, cs], rhs=Sbf[:], start=True, stop=True),
                         reads=[bkT, bSbf], writes=[bKSO])
                    P.op("dve", lambda E, n=n, KSO=KSO, rhs2=rhs2, nbg=nbg, bv=bv: E.scalar_tensor_tensor(
                        out=rhs2[:], in0=KSO[0:64, 0:128], scalar=nbg[:, n:n + 1], in1=bv[:, n, :], op0=ALU.mult, op1=ALU.add),
                        reads=[bKSO, bnbg, bbv], writes=[brhs2])
                    P.op("pe", lambda E, cs=cs, Y=Y, Vb=Vb, rhs2=rhs2: E.matmul(Vb[0:64, 128:256], lhsT=Y[:, cs], rhs=rhs2[:], start=True, stop=True),
                         reads=[bY, brhs2], writes=[bVb])
                    P.op("act", lambda E, vnew=vnew, Vb=Vb: E.copy(out=vnew[:], in_=Vb[0:64, 128:256]), reads=[bVb], writes=[bvnew])
                    fns = [lambda E, cs=cs, Sbf=Sbf, KSO=KSO, qdec=qdec: E.matmul(KSO[64:128, 0:128], lhsT=qdec[:, cs], rhs=Sbf[:], start=True, stop=False),
                           lambda E, cs=cs, KSO=KSO, attnT=attnT, vnew=vnew: E.matmul(KSO[64:128, 0:128], lhsT=attnT[:, cs], rhs=vnew[:], start=False, stop=True)]
                    P.mm_group(fns, reads=[bqdec, bSbf, battnT, bvnew], writes=[bKSO])
                    P.op("pe", lambda E, n=n, Sb=Sb, kdec=kdec, vnew=vnew: E.matmul(Sb[:, 0:128], lhsT=kdec[:, n, :], rhs=vnew[:], start=True, stop=True),
                         reads=[bkdec, bvnew], writes=[bSb])
                    P.op("dve", lambda E, n=n, S=S, EG=EG, Sb=Sb, Sbf=Sbf: E.scalar_tensor_tensor(out=Sbf[:], in0=S[:], scalar=EG[:, n * 64 + 63:n * 64 + 64], in1=Sb[:, 0:128],
                                                                                                  op0=ALU.mult, op1=ALU.add), reads=[bS, bEG, bSb], writes=[bSbf])
                    P.op("dve", lambda E, n=n, S=S, EG=EG, Sb=Sb: E.scalar_tensor_tensor(out=S[:], in0=S[:], scalar=EG[:, n * 64 + 63:n * 64 + 64], in1=Sb[:, 0:128],
                                                                                         op0=ALU.mult, op1=ALU.add), reads=[bS, bEG, bSb], writes=[bS])
                    P.op("act", lambda E, n=n, osb=osb, KSO=KSO: E.copy(out=osb[64:128, n, :], in_=KSO[64:128, 0:128]), reads=[bKSO], writes=[bosb])
                    advance(1)
                advance(1)
            advance(100)
            for h in range(2):
                osb, bosb = heads[h]["osb"]
                P.dma("sp", o_d[s_ * 512:(s_ + 1) * 512, h * 128:(h + 1) * 128].rearrange("(n c) d -> c n d", c=64), osb[64:128, :, :], reads=[bosb],
                      writes=[fz["obuf_of"](s_) if fz else bo])
            if fz:
                fz["after_chunk"](s_)
        if fz:
            barrier(P)
        else:
            P.finish([bo])
    return nc


def run_L1a(inp):
    nc = _get("L1a", build_L1a)
    c64, cmask, sel = _gdn_consts()
    w_in = inp["w_in_even"][0]
    conv = inp["conv_qkv"][0]
    ones = np.ones((128, 128), np.float32)
    maps = []
    for c in range(8):
        b, r = divmod(c, 4)
        cols = np.concatenate([np.arange(256 * r, 256 * r + 256), 1024 + np.arange(256 * r, 256 * r + 256), 2048 + np.arange(256 * r, 256 * r + 256)])
        maps.append({"x": np.ascontiguousarray(inp["x"][b]), "npre": np.ascontiguousarray(inp["norm_pre"][0]),
                     "w": np.ascontiguousarray(w_in[:, cols]), "wb": np.ascontiguousarray(w_in[:, 4096 + 2 * r:4096 + 2 * r + 2]),
                     "wa": np.ascontiguousarray(w_in[:, 4104 + 2 * r:4104 + 2 * r + 2]), "conv": np.ascontiguousarray(conv[:, cols]),
                     "alog": np.ascontiguousarray(inp["a_log"][0, 2 * r:2 * r + 2]), "dtb": np.ascontiguousarray(inp["dt_bias"][0, 2 * r:2 * r + 2]),
                     "ident": _IDENT, "c64": c64, "cmask": cmask, "sel": sel, "ones": ones})
    res = run_bass_kernel_spmd(nc, maps, core_ids=list(range(8)))
    S_ = inp["x"].shape[1]
    o = np.empty((2, S_, 1024), np.float32)
    for c in range(8):
        b, r = divmod(c, 4)
        o[b, :, 256 * r:256 * (r + 1)] = res.results[c]["o"]
    return o


def kernel_unfused(**inputs):
    inp = {k: np.asarray(v) for k, v in inputs.items()}
    o = run_L1a(inp)
    ys = run_L1b(inp)
    x1 = run_L2(inp, o, ys)
    out = run_L3(inp, x1)
    return out.astype(np.float32)


def build_fused():
    nc = bass.Bass("TRN2", target_bir_lowering=False)
    x_full = nc.dram_tensor("x", [8192, 1024], F32, kind="ExternalInput").ap()
    ident_d = nc.dram_tensor("ident", [128, 128], F32, kind="ExternalInput").ap()
    npre0_d = nc.dram_tensor("npre0", [1024], F32, kind="ExternalInput").ap()
    gidx_d = nc.dram_tensor("gidx", [128, 2, 17, 4], I32, kind="ExternalInput").ap()
    out_d = nc.dram_tensor("out", [2048, 1024], F32, kind="ExternalOutput").ap()
    ag_in = [nc.dram_tensor("ag_in%d" % i, [8192, 256], (F32, BF16)[i]) for i in range(2)]
    ag_out = [nc.dram_tensor("ag_out%d" % i, [4 * 8192, 256], (F32, BF16)[i]) for i in range(2)]
    x1s = nc.dram_tensor("x1s", [2176, 1024], F32)
    GROUPS = [[0, 1, 2, 3], [4, 5, 6, 7]]
    with ExitStack() as st:
        C = Ctx(nc, st); P = C.P
        csem = st.enter_context(nc.semaphore("csem"))
        bag_out = Buf("ag_out"); bx1s = Buf("x1s", multi=True); bout = Buf("out", multi=True)
        bo_ch = [Buf("o_ch%d" % k, multi=True) for k in range(16)]
        by_jt = [Buf("y_jt%d" % k, multi=True) for k in range(4)]
        ncc = [0]

        def emit_cc(which, k, inbuf, rows=512):
            P._deps("pool", [inbuf], [])
            P.streams["pool"].append(lambda E, which=which, k=k, rows=rows: E.collective_compute(
                "AllGather", ALU.bypass, replica_groups=GROUPS,
                ins=[ag_in[which].ap()[k * rows:(k + 1) * rows, :].opt()],
                outs=[ag_out[which].ap()[k * 4 * rows:(k + 1) * 4 * rows, :].opt()]).then_inc(csem))
            ncc[0] += 1

        share1 = {"x": x_full, "ident": ident_d, "npre": npre0_d}

        def after_jt(jt):
            emit_cc(1, jt, by_jt[jt], rows=2048)

        with ExitStack() as stU:
            CU = Ctx(nc, stU, P, "u_")
            uext = CU.sb("uTp", [128, 2, 16, 512], BF16)
            build_L1a(8192, fz={"nc": nc, "P": P, "pfx": "a_", "share": share1, "out": ag_in[0].ap(), "uTp": uext,
                                "obuf_of": lambda s_: bo_ch[s_], "after_chunk": lambda s_: emit_cc(0, s_, bo_ch[s_])})
            build_L1b(8192, fz={"nc": nc, "P": P, "pfx": "b_", "share": share1, "out": ag_in[1].ap(), "uTp": uext,
                                "obuf_of": lambda jt: by_jt[jt], "after_chunk": after_jt, "ybf16": True})
        gidx, bgidx = C.sb("gidx", [128, 2, 17, 4], I32)
        P.dma("sp", gidx[:], gidx_d, writes=[bgidx])
        waited = [False]

        def gather(P_, ld, bld, tile, part):
            if not waited[0]:
                P.streams["pool"].append(lambda E: E.wait_ge(csem, ncc[0]))
                P.op("pool", lambda E: E.nop(), reads=[], writes=[bag_out])
                waited[0] = True
            for i in range(4):
                P_.dma_ind("pool", ld[:, i * 256:(i + 1) * 256], ag_out[part].ap(), gidx[:, part, tile, i:i + 1], reads=[bag_out, bgidx], writes=[bld])

        share2 = {"ident": ident_d, "npre": npre0_d, "o": None, "ys": None}
        build_L2(2176, fz={"nc": nc, "P": P, "pfx": "c_", "share": share2, "out": x1s.ap(), "obuf": bx1s, "gather": gather, "ybf16": True})
        share3 = {"ident": ident_d, "x": x1s.ap()}
        build_L3(2048, fz={"nc": nc, "P": P, "pfx": "d_", "share": share3, "out": out_d, "obuf": bout, "xbuf": bx1s})
        P.finish([bout])
    return nc


def _gidx(r):
    g = np.zeros((128, 2, 17, 4), np.int32)
    p = np.arange(128)[:, None, None]
    tile = np.arange(17)[None, :, None]
    src = np.arange(4)[None, None, :]
    tok = np.clip(2048 * r - 128 + tile * 128 + p, 0, 8191)
    for part, R in ((0, 512), (1, 2048)):
        g[:, part] = ((tok // R) * 4 + src) * R + tok % R
    return g


def kernel(**inputs):
    inp = {k: np.ascontiguousarray(np.asarray(v)) for k, v in inputs.items()}
    nc = _get("fused", build_fused)
    c64, cmask, sel = _gdn_consts()
    mk, idm = _s5_consts()
    ones = np.ones((128, 128), np.float32)
    w_in = inp["w_in_even"][0]
    conv = inp["conv_qkv"][0]
    wz = np.ascontiguousarray(np.concatenate([w_in[:, 3072:4096], w_in[:, 5136:6160]], axis=1))
    maps = []
    for c in range(8):
        b, r = divmod(c, 4)
        cols = np.concatenate([np.arange(256 * r, 256 * r + 256), 1024 + np.arange(256 * r, 256 * r + 256), 2048 + np.arange(256 * r, 256 * r + 256)])
        gs = slice(16 * r, 16 * r + 16)
        xq = np.zeros((2176, 1024), np.float32)
        xq[128:] = inp["x"][b, 2048 * r:2048 * (r + 1)]
        if r > 0:
            xq[:128] = inp["x"][b, 2048 * r - 128:2048 * r]
        m = {"x": inp["x"][b], "ident": _IDENT, "npre0": inp["norm_pre"][0], "gidx": _gidx(r),
             "a_w": np.ascontiguousarray(w_in[:, cols]), "a_wb": np.ascontiguousarray(w_in[:, 4096 + 2 * r:4096 + 2 * r + 2]),
             "a_wa": np.ascontiguousarray(w_in[:, 4104 + 2 * r:4104 + 2 * r + 2]), "a_conv": np.ascontiguousarray(conv[:, cols]),
             "a_alog": np.ascontiguousarray(inp["a_log"][0, 2 * r:2 * r + 2]), "a_dtb": np.ascontiguousarray(inp["dt_bias"][0, 2 * r:2 * r + 2]),
             "a_c64": c64, "a_cmask": cmask, "a_sel": sel, "a_ones": ones,
             "a_wu": np.ascontiguousarray(w_in[:, 4112 + 256 * r:4112 + 256 * (r + 1)]),
             "b_wu": np.ascontiguousarray(w_in[:, 4112 + 256 * r:4112 + 256 * (r + 1)]),
             "b_lre": np.ascontiguousarray(inp["s5_lam_re"][0, gs]), "b_lim": np.ascontiguousarray(inp["s5_lam_im"][0, gs]),
             "b_bre": np.ascontiguousarray(inp["s5_b_re"][0, gs]), "b_bim": np.ascontiguousarray(inp["s5_b_im"][0, gs]),
             "b_cre": np.ascontiguousarray(inp["s5_c_re"][0, gs]), "b_cim": np.ascontiguousarray(inp["s5_c_im"][0, gs]),
             "b_ldt": np.ascontiguousarray(inp["s5_log_dt"][0, gs]), "b_dd": np.ascontiguousarray(inp["s5_d"][0, 256 * r:256 * (r + 1)]),
             "b_taus": TAUS, "b_mk": mk, "b_idm": idm,
             "c_x": xq, "c_wz": wz, "c_wglu": inp["w_glu"][0], "c_wout": inp["w_out_even"][0], "c_npost": inp["norm_post"][0],
             "c_gnw": inp["gdn_norm_w"][0],
             "d_win": inp["w_in_odd"][0], "d_wout": inp["w_out_odd"][0], "d_conv": inp["conv_short"][0],
             "d_npre": inp["norm_pre"][1], "d_npost": inp["norm_post"][1]}
        maps.append(m)
    res = run_bass_kernel_spmd(nc, maps, core_ids=list(range(8)))
    out = np.empty((2, 8192, 1024), np.float32)
    for c in range(8):
        b, r = divmod(c, 4)
        out[b, r * 2048:(r + 1) * 2048] = res.results[c]["out"]
    return out
```

```python
from contextlib import ExitStack
import numpy as np
import concourse.bass as bass
import concourse.mybir as mybir
from concourse.bass_utils import run_bass_kernel_spmd

F32 = mybir.dt.float32
BF16 = mybir.dt.bfloat16
AF = mybir.ActivationFunctionType
ALU = mybir.AluOpType
AX = mybir.AxisListType

NDS = 12


class Buf:
    __slots__ = ("name", "w", "r", "multi")

    def __init__(self, name, multi=False):
        self.name = name
        self.w = [] if multi else None
        self.r = []
        self.multi = multi


class Prog:
    ENG = ("pe", "act", "dve", "pool", "sp")

    def __init__(self, nc, stack):
        self.nc = nc
        self.stack = stack
        self.streams = {e: [] for e in self.ENG}
        self.cnt = {e: 0 for e in self.ENG}
        self.sem = {e: stack.enter_context(nc.semaphore("s_" + e)) for e in self.ENG}
        self.seen = {e: {} for e in self.ENG}
        self.dcnt = {e: 0 for e in self.ENG}
        self.dsem = {}
        for e in ("sp", "pool", "act"):
            self.dsem[e] = [stack.enter_context(nc.semaphore("d_%s%d" % (e, i))) for i in range(NDS)]
        self.same_engine_sync = True
        self.nwaits = 0

    def _wait(self, eng, tok):
        if tok is None:
            return
        kind = tok[0]
        if kind == "c":
            _, e2, n = tok
            if e2 == eng and (eng == "pe" or not self.same_engine_sync):
                return
            key = e2
            if self.seen[eng].get(key, 0) >= n:
                return
            self.seen[eng][key] = n
            sem = self.sem[e2]
            self.streams[eng].append(lambda E, sem=sem, n=n: E.wait_ge(sem, n))
            self.nwaits += 1
        else:
            _, q, slot, val = tok
            key = ("d", q, slot)
            if self.seen[eng].get(key, 0) >= val:
                return
            self.seen[eng][key] = val
            sem = self.dsem[q][slot]
            self.streams[eng].append(lambda E, sem=sem, val=val: E.wait_ge(sem, val))
            self.nwaits += 1

    def _deps(self, eng, reads, writes):
        for b in reads:
            if b.multi:
                for t in b.w:
                    self._wait(eng, t)
            else:
                self._wait(eng, b.w)
        for b in writes:
            if not b.multi:
                self._wait(eng, b.w)
            for t in b.r:
                self._wait(eng, t)

    def _commit(self, tok, reads, writes):
        for b in writes:
            if b.multi:
                b.w.append(tok)
            else:
                b.w = tok
            b.r = []
        for b in reads:
            if b not in writes:
                b.r.append(tok)

    def op(self, eng, fn, reads=(), writes=()):
        reads = list(reads)
        writes = list(writes)
        self._deps(eng, reads, writes)
        self.cnt[eng] += 1
        n = self.cnt[eng]
        sem = self.sem[eng]
        self.streams[eng].append(lambda E, fn=fn, sem=sem: fn(E).then_inc(sem, 1))
        tok = ("c", eng, n)
        self._commit(tok, reads, writes)
        return tok

    def mm_group(self, fns, reads=(), writes=()):
        eng = "pe"
        reads = list(reads)
        writes = list(writes)
        self._deps(eng, reads, writes)
        self.cnt[eng] += 1
        n = self.cnt[eng]
        sem = self.sem[eng]
        for fn in fns[:-1]:
            self.streams[eng].append(lambda E, fn=fn: fn(E))
        last = fns[-1]
        self.streams[eng].append(lambda E, fn=last, sem=sem: fn(E).then_inc(sem, 1))
        tok = ("c", eng, n)
        self._commit(tok, reads, writes)
        return tok

    def dma(self, q, out_ap, in_ap, reads=(), writes=()):
        reads = list(reads)
        writes = list(writes)
        self._deps(q, reads, writes)
        j = self.dcnt[q]
        self.dcnt[q] += 1
        slot = j % NDS
        val = 16 * (j // NDS + 1)
        if j >= NDS:
            self._wait(q, ("d", q, slot, val - 16))
        sem = self.dsem[q][slot]
        self.streams[q].append(
            lambda E, o=out_ap, i=in_ap, sem=sem: E.dma_start(out=o, in_=i).then_inc(sem, 16))
        tok = ("d", q, slot, val)
        self._commit(tok, reads, writes)
        return tok

    def dma_ind(self, q, out_ap, table_ap, idx_ap, reads=(), writes=()):
        reads = list(reads)
        writes = list(writes)
        self._deps(q, reads, writes)
        j = self.dcnt[q]
        self.dcnt[q] += 1
        slot = j % NDS
        val = 16 * (j // NDS + 1)
        if j >= NDS:
            self._wait(q, ("d", q, slot, val - 16))
        sem = self.dsem[q][slot]
        self.streams[q].append(
            lambda E, o=out_ap, t=table_ap, i=idx_ap, sem=sem: E.indirect_dma_start(
                out=o, out_offset=None, in_=t, in_offset=bass.IndirectOffsetOnAxis(ap=i, axis=0)).then_inc(sem, 16))
        tok = ("d", q, slot, val)
        self._commit(tok, reads, writes)
        return tok

    def finish(self, final_bufs):
        for b in final_bufs:
            for t in (b.w if b.multi else [b.w]):
                self._wait("sp", t)
        nc = self.nc
        streams = self.streams
        with nc.Block() as block:
            @block.tensor
            def _(E):
                for f in streams["pe"]:
                    f(E)

            @block.scalar
            def _(E):
                for f in streams["act"]:
                    f(E)

            @block.vector
            def _(E):
                for f in streams["dve"]:
                    f(E)

            @block.gpsimd
            def _(E):
                for f in streams["pool"]:
                    f(E)

            @block.sync
            def _(E):
                for f in streams["sp"]:
                    f(E)


class Ctx:
    def __init__(self, nc, st, P=None, pfx=""):
        self.nc = nc
        self.st = st
        self.pfx = pfx
        if P is None:
            st.enter_context(nc.allow_non_contiguous_dma(reason="small parameter loads / layout transforms"))
            P = Prog(nc, st)
        self.P = P

    def sb(self, name, shape, dt=F32):
        t = self.st.enter_context(self.nc.sbuf_tensor("sb_" + self.pfx + name, shape, dt))
        return t, Buf(name)

    def ps(self, name, shape, dt=F32):
        t = self.st.enter_context(self.nc.psum_tensor("ps_" + self.pfx + name, shape, dt))
        return t, Buf(name)


def bcast_row_load(C, name, dram_vec, n, q="sp"):
    t, b = C.sb(name, [128, n])
    C.P.dma(q, t[:], dram_vec.partition_broadcast(128), writes=[b])
    return t, b


def make_ident(C, dram_ident):
    idf, bidf = C.sb("identf", [128, 128])
    C.P.dma("sp", idf[:], dram_ident, writes=[bidf])
    idb, bidb = C.sb("identb", [128, 128], BF16)
    C.P.op("dve", lambda E: E.tensor_copy(out=idb[:], in_=idf[:]), reads=[bidf], writes=[bidb])
    return idf, bidf, idb, bidb


def rms_rstd(C, src, bsrc, ncols, junk, bjunk, ss, bss, eps=1e-6):
    P = C.P
    P.op("act", lambda E: E.activation(out=junk, in_=src, func=AF.Square, accum_out=ss[:, 0:1]),
         reads=[bsrc], writes=[bjunk, bss])
    P.op("act", lambda E: E.activation(out=ss[:, 0:1], in_=ss[:, 0:1], func=AF.Sqrt, bias=float(eps), scale=float(1.0 / ncols)),
         reads=[bss], writes=[bss])
    P.op("dve", lambda E: E.reciprocal(out=ss[:, 0:1], in_=ss[:, 0:1]), reads=[bss], writes=[bss])


def transpose8(C, src_bf, bsrc, idb, bidb, ptr, bptr, dst3, bdst, eng="act"):
    P = C.P
    fns = [(lambda E, kt=kt: E.transpose(out=ptr[:, kt * 128:(kt + 1) * 128], in_=src_bf[:, kt * 128:(kt + 1) * 128],
                                         identity=idb[:])) for kt in range(8)]
    P.mm_group(fns, reads=[bsrc, bidb], writes=[bptr])
    src3 = ptr[:].rearrange("p (k t) -> p k t", k=8)
    if eng == "act":
        P.op("act", lambda E: E.copy(out=dst3, in_=src3), reads=[bptr], writes=[bdst])
    else:
        P.op("dve", lambda E: E.tensor_copy(out=dst3, in_=src3), reads=[bptr], writes=[bdst])


def outproj_post(C, catT, bcat, nkt, wout, bwout, t, xres, bxres, npw, bnpw, pso, bpso, yo, byo, junk, bjunk, ss, bss,
                 out_dram_rows, bout):
    P = C.P
    for hh in range(2):
        fns = [(lambda E, kt=kt, hh=hh: E.matmul(pso[hh][:], lhsT=catT[:, kt, t * 128:(t + 1) * 128],
                                                 rhs=wout[:, kt, hh * 512:(hh + 1) * 512],
                                                 start=(kt == 0), stop=(kt == nkt - 1))) for kt in range(nkt)]
        P.mm_group(fns, reads=[bcat, bwout], writes=[bpso[hh]])
        P.op("act", lambda E, hh=hh: E.copy(out=yo[:, hh * 512:(hh + 1) * 512], in_=pso[hh][:]),
             reads=[bpso[hh]], writes=[byo])
    rms_rstd(C, yo[:], byo, 1024, junk[:], bjunk, ss, bss)
    P.op("dve", lambda E: E.scalar_tensor_tensor(out=yo[:], in0=yo[:], scalar=ss[:, 0:1], in1=npw[:],
                                                 op0=ALU.mult, op1=ALU.mult), reads=[byo, bss, bnpw], writes=[byo])
    P.op("dve", lambda E: E.tensor_tensor(out=yo[:], in0=yo[:], in1=xres, op=ALU.add), reads=[byo, bxres], writes=[byo])
    P.dma("sp", out_dram_rows, yo[:], reads=[byo], writes=[bout])


def load_w_bf16(C, name, dram_w, kt_n, ncols, chunk=2048, groups=None):
    w, _ = C.sb(name, [128, kt_n, ncols], BF16)
    src = dram_w.rearrange("(k p) c -> p k c", p=128)
    if groups is None:
        bw = Buf(name, multi=True)
        for kt in range(kt_n):
            for c0 in range(0, ncols, chunk):
                c1 = min(ncols, c0 + chunk)
                C.P.dma("pool", w[:, kt, c0:c1], src[:, kt, c0:c1], writes=[bw])
        return w, bw
    bws = []
    for gi, sls in enumerate(groups):
        bg = Buf("%s_g%d" % (name, gi), multi=True)
        for (c0, c1) in sls:
            for kt in range(kt_n):
                C.P.dma("pool", w[:, kt, c0:c1], src[:, kt, c0:c1], writes=[bg])
        bws.append(bg)
    return w, bws


def build_L2(ntok=2048, fz=None):
    nc = fz["nc"] if fz else bass.Bass("TRN2", target_bir_lowering=False)
    pfx = fz["pfx"] if fz else ""

    def D(name, shape):
        if fz and name in fz["share"]:
            return fz["share"][name]
        return nc.dram_tensor(pfx + name, shape, F32, kind="ExternalInput").ap()
    x_d = D("x", [ntok, 1024]); o_d = D("o", [ntok, 1024]); ys_d = D("ys", [ntok, 1024])
    wz_d = D("wz", [1024, 2048]); wglu_d = D("wglu", [1024, 1024]); wout_d = D("wout", [2048, 1024])
    npre_d = D("npre", [1024]); npost_d = D("npost", [1024]); gnw_d = D("gnw", [128]); ident_d = D("ident", [128, 128])
    out_d = fz["out"] if fz else nc.dram_tensor("out", [ntok, 1024], F32, kind="ExternalOutput").ap()
    NT = 512
    with ExitStack() as st:
        C = Ctx(nc, st, fz["P"], pfx) if fz else Ctx(nc, st); P = C.P
        idf, bidf, idb, bidb = make_ident(C, ident_d)
        npre, bnpre = bcast_row_load(C, "npre", npre_d, 1024)
        npost, bnpost = bcast_row_load(C, "npost", npost_d, 1024)
        gnw, bgnw = bcast_row_load(C, "gnw", gnw_d, 128)
        wz, bwz = load_w_bf16(C, "wz", wz_d, 8, 2048)
        wglu, bwglu = load_w_bf16(C, "wglu", wglu_d, 8, 1024)
        wout, bwout = load_w_bf16(C, "wout", wout_d, 16, 1024)
        xt4, bxt4 = C.sb("xt4", [128, 4, 1024]); bxt = [Buf("xt%d" % i) for i in range(4)]
        ldo = [C.sb("ldo%d" % i, [128, 1024]) for i in range(2)]
        ldy = [C.sb("ldy%d" % i, [128, 1024], BF16 if (fz and fz.get("ybf16")) else F32) for i in range(2)]
        for (_t, _b) in ldo + ldy:
            _b.multi = True; _b.w = []
        sq, bsq = C.sb("sq", [128, 1024])
        hn, bhn = C.sb("hn", [128, 1024], BF16)
        ss, bss = C.sb("ss", [128, 1])
        ss8, bss8 = C.sb("ss8", [128, 8])
        hT, bhT = C.sb("hT", [128, 8, NT], BF16)
        oT, boT = C.sb("oT", [128, 8, NT], BF16)
        yT, byT = C.sb("yT", [128, 8, NT], BF16)
        gz, bgz = C.sb("gz", [128, 8, NT], BF16)
        sg, bsg = C.sb("sg", [128, NT], BF16)
        catT, bcat = C.sb("catT", [128, 16, NT], BF16)
        yo, byo = C.sb("yo", [128, 1024])
        ptr, bptr = C.ps("ptr", [128, 1024], BF16)
        pmm = []; bpmm = []
        for i in range(4):
            t_, b_ = C.ps("pmm%d" % i, [128, 512]); pmm.append(t_); bpmm.append(b_)
        pso = []; bpso = []
        for i in range(2):
            t_, b_ = C.ps("pso%d" % i, [128, 512]); pso.append(t_); bpso.append(b_)
        bout = fz["obuf"] if fz else Buf("out", multi=True)
        if fz:
            sts = [(0, 128)] + [(128 + i * NT, NT) for i in range((ntok - 128) // NT)]
        else:
            sts = [(i * NT, NT) for i in range(ntok // NT)]
        tile_r0 = [t0_ + t_ * 128 for (t0_, n_) in sts for t_ in range(n_ // 128)]

        def issue_loads(ti):
            r0_ = tile_r0[ti]
            lo, blo = ldo[ti % 2]; ly, bly = ldy[ti % 2]
            if fz:
                fz["gather"](P, lo, blo, r0_ // 128, 0)
                fz["gather"](P, ly, bly, r0_ // 128, 1)
            else:
                P.dma("sp", lo[:], o_d[r0_:r0_ + 128, :], writes=[blo])
                P.dma("sp", ly[:], ys_d[r0_:r0_ + 128, :], writes=[bly])

        issue_loads(0)
        for (t0, n) in sts:
            ntl = n // 128
            for t in range(ntl):
                r0 = t0 + t * 128
                ti = tile_r0.index(r0)
                if ti + 1 < len(tile_r0):
                    issue_loads(ti + 1)
                P.dma("sp", xt4[:, t, :], x_d[r0:r0 + 128, :], writes=[bxt[t]])
                rms_rstd(C, xt4[:, t, :], bxt[t], 1024, sq[:], bsq, ss, bss)
                P.op("dve", lambda E, t=t: E.scalar_tensor_tensor(out=hn[:], in0=xt4[:, t, :], scalar=ss[:, 0:1], in1=npre[:],
                                                                  op0=ALU.mult, op1=ALU.mult), reads=[bxt[t], bss, bnpre], writes=[bhn])
                transpose8(C, hn, bhn, idb, bidb, ptr, bptr, hT[:, :, t * 128:(t + 1) * 128], bhT, eng="act")
                ld, bld = ldo[ti % 2]
                P.op("act", lambda E, ld=ld: E.activation(out=sq[:], in_=ld[:], func=AF.Square), reads=[bld], writes=[bsq])
                P.op("dve", lambda E: E.tensor_reduce(out=ss8[:], in_=sq[:].rearrange("p (h d) -> p h d", h=8), axis=AX.X, op=ALU.add),
                     reads=[bsq], writes=[bss8])
                P.op("dve", lambda E: E.tensor_scalar(out=ss8[:], in0=ss8[:], scalar1=1.0 / 128, scalar2=1e-6, op0=ALU.mult, op1=ALU.add),
                     reads=[bss8], writes=[bss8])
                P.op("act", lambda E: E.activation(out=ss8[:], in_=ss8[:], func=AF.Sqrt), reads=[bss8], writes=[bss8])
                P.op("dve", lambda E: E.reciprocal(out=ss8[:], in_=ss8[:]), reads=[bss8], writes=[bss8])
                P.op("dve", lambda E, ld=ld: E.tensor_tensor(out=sq[:].rearrange("p (h d) -> p h d", h=8), in0=ld[:].rearrange("p (h d) -> p h d", h=8),
                                                      in1=ss8[:].unsqueeze(2).to_broadcast([128, 8, 128]), op=ALU.mult),
                     reads=[bld, bss8], writes=[bsq])
                P.op("dve", lambda E: E.tensor_tensor(out=hn[:].rearrange("p (h d) -> p h d", h=8), in0=sq[:].rearrange("p (h d) -> p h d", h=8),
                                                      in1=gnw[:].unsqueeze(1).to_broadcast([128, 8, 128]), op=ALU.mult),
                     reads=[bsq, bgnw], writes=[bhn])
                transpose8(C, hn, bhn, idb, bidb, ptr, bptr, oT[:, :, t * 128:(t + 1) * 128], boT, eng="act")
                ld, bld = ldy[ti % 2]
                P.op("act", lambda E, ld=ld: E.activation(out=hn[:], in_=ld[:], func=AF.Gelu_apprx_tanh), reads=[bld], writes=[bhn])
                transpose8(C, hn, bhn, idb, bidb, ptr, bptr, yT[:, :, t * 128:(t + 1) * 128], byT, eng="dve")
            for ct in range(16):
                pb = pmm[ct % 4]; bpb = bpmm[ct % 4]
                fns = [(lambda E, kt=kt, ct=ct, pb=pb, n=n: E.matmul(pb[:, 0:n], lhsT=wz[:, kt, ct * 128:(ct + 1) * 128], rhs=hT[:, kt, 0:n],
                                                                start=(kt == 0), stop=(kt == 7))) for kt in range(8)]
                P.mm_group(fns, reads=[bwz, bhT], writes=[bpb])
                if ct < 8:
                    P.op("act", lambda E, pb=pb, n=n: E.activation(out=sg[:, 0:n], in_=pb[:, 0:n], func=AF.Silu), reads=[bpb], writes=[bsg])
                    P.op("dve", lambda E, ct=ct, n=n: E.tensor_tensor(out=catT[:, ct, 0:n], in0=oT[:, ct, 0:n], in1=sg[:, 0:n], op=ALU.mult),
                         reads=[boT, bsg], writes=[bcat])
                else:
                    P.op("act", lambda E, pb=pb, ct=ct, n=n: E.activation(out=gz[:, ct - 8, 0:n], in_=pb[:, 0:n], func=AF.Silu), reads=[bpb], writes=[bgz])
            for ct in range(8):
                pb = pmm[ct % 4]; bpb = bpmm[ct % 4]
                fns = [(lambda E, kt=kt, ct=ct, pb=pb, n=n: E.matmul(pb[:, 0:n], lhsT=wglu[:, kt, ct * 128:(ct + 1) * 128], rhs=yT[:, kt, 0:n],
                                                                start=(kt == 0), stop=(kt == 7))) for kt in range(8)]
                P.mm_group(fns, reads=[bwglu, byT], writes=[bpb])
                P.op("act", lambda E, pb=pb, n=n: E.activation(out=sg[:, 0:n], in_=pb[:, 0:n], func=AF.Sigmoid), reads=[bpb], writes=[bsg])
                P.op("dve", lambda E, ct=ct, n=n: E.tensor_tensor(out=sg[:, 0:n], in0=sg[:, 0:n], in1=yT[:, ct, 0:n], op=ALU.mult), reads=[bsg, byT], writes=[bsg])
                P.op("dve", lambda E, ct=ct, n=n: E.tensor_tensor(out=catT[:, 8 + ct, 0:n], in0=sg[:, 0:n], in1=gz[:, ct, 0:n], op=ALU.mult),
                     reads=[bsg, bgz], writes=[bcat])
            for t in range(ntl):
                r0 = t0 + t * 128
                outproj_post(C, catT, bcat, 16, wout, bwout, t, xt4[:, t, :], bxt[t], npost, bnpost, pso, bpso, yo, byo, sq, bsq, ss, bss,
                             out_d[r0:r0 + 128, :], bout)
        if fz:
            barrier(P)
        else:
            P.finish([bout])
    return nc


def build_L3(ntok=2048, fz=None):
    nc = fz["nc"] if fz else bass.Bass("TRN2", target_bir_lowering=False)
    pfx = fz["pfx"] if fz else ""

    def D(name, shape):
        if fz and name in fz["share"]:
            return fz["share"][name]
        return nc.dram_tensor(pfx + name, shape, F32, kind="ExternalInput").ap()
    x_d = D("x", [ntok + 128, 1024])
    win_d = D("win", [1024, 8192]); wout_d = D("wout", [2048, 1024]); conv_d = D("conv", [3, 2048])
    npre_d = D("npre", [1024]); npost_d = D("npost", [1024]); ident_d = D("ident", [128, 128])
    out_d = fz["out"] if fz else nc.dram_tensor("out", [ntok, 1024], F32, kind="ExternalOutput").ap()
    NT = 512
    HN = 256
    with ExitStack() as st:
        C = Ctx(nc, st, fz["P"], pfx) if fz else Ctx(nc, st); P = C.P
        idf, bidf, idb, bidb = make_ident(C, ident_d)
        npre, bnpre = bcast_row_load(C, "npre", npre_d, 1024)
        npost, bnpost = bcast_row_load(C, "npost", npost_d, 1024)
        cw, bcw = C.sb("cw", [128, 3, 16])
        P.dma("sp", cw[:], conv_d.rearrange("j (c p) -> p j c", p=128), writes=[bcw])
        win, bwin_g = load_w_bf16(C, "win", win_d, 8, 8192,
                                  groups=[[(part * 2048 + cg * 512, part * 2048 + cg * 512 + 512) for part in range(4)] for cg in range(4)])
        wout, bwout = load_w_bf16(C, "wout", wout_d, 16, 1024)
        xt, bxt = C.sb("xt", [128, 1024])
        hn, bhn = C.sb("hn", [128, 1024], BF16)
        ss, bss = C.sb("ss", [128, 1])
        hT, bhT = C.sb("hT", [128, 8, NT], BF16)
        y1T, by1T = C.sb("y1T", [128, 16, NT], BF16)
        pbuf, bpbuf = C.sb("pbuf", [128, HN + 2])
        phalo, bphalo = C.sb("phalo", [128, 16, 2])
        gcs, bgcs = C.sb("gcs", [128, HN])
        cv, bcv = C.sb("cv", [128, HN])
        sz, bsz = C.sb("sz", [128, HN])
        yo, byo = C.sb("yo", [128, 1024])
        P.op("dve", lambda E: E.memset(phalo[:], 0.0), writes=[bphalo])
        ptr, bptr = C.ps("ptr", [128, 1024], BF16)
        GB = [C.ps("g%d" % i, [128, 512]) for i in range(7)]
        pso = [GB[0][0], GB[1][0]]; bpso = [GB[0][1], GB[1][1]]
        bout = fz["obuf"] if fz else Buf("out", multi=True)
        sts = [(0, 128)] + [(128 + i * NT, NT) for i in range(ntok // NT)]
        for (t0, n) in sts:
            ntl = n // 128
            for t in range(ntl):
                r0 = t0 + t * 128
                P.dma("sp", xt[:], x_d[r0:r0 + 128, :], reads=([fz["xbuf"]] if fz else []), writes=[bxt])
                rms_rstd(C, xt[:], bxt, 1024, hn[:], bhn, ss, bss)
                P.op("dve", lambda E: E.scalar_tensor_tensor(out=hn[:], in0=xt[:], scalar=ss[:, 0:1], in1=npre[:],
                                                             op0=ALU.mult, op1=ALU.mult), reads=[bxt, bss, bnpre], writes=[bhn])
                transpose8(C, hn, bhn, idb, bidb, ptr, bptr, hT[:, :, t * 128:(t + 1) * 128], bhT, eng="act")
            for ct in range(16):
                sel_ = [GB[3 * (ct % 2) + 0], GB[3 * (ct % 2) + 1], GB[3 * (ct % 2) + 2], GB[6]]
                pmm = [x_[0] for x_ in sel_]; bpmm = [x_[1] for x_ in sel_]
                for part in range(4):
                    col0 = (part * 16 + ct) * 128
                    pb = pmm[part]
                    fns = [(lambda E, n=n, kt=kt, col0=col0, pb=pb: E.matmul(pb[:, 0:n], lhsT=win[:, kt, col0:col0 + 128], rhs=hT[:, kt, 0:n],
                                                                        start=(kt == 0), stop=(kt == 7))) for kt in range(8)]
                    P.mm_group(fns, reads=[bwin_g[ct // 4], bhT], writes=[bpmm[part]])
                for h0 in range(0, n, HN):
                    nn = min(HN, n - h0)
                    P.op("act", lambda E, nn=nn, h0=h0, pmm=pmm: E.copy(out=gcs[:, 0:nn], in_=pmm[1][:, h0:h0 + nn]), reads=[bpmm[1]], writes=[bgcs])
                    P.op("act", lambda E, ct=ct: E.copy(out=pbuf[:, 0:2], in_=phalo[:, ct, :]), reads=[bphalo], writes=[bpbuf])
                    P.op("dve", lambda E, nn=nn, h0=h0, pmm=pmm: E.tensor_tensor(out=pbuf[:, 2:2 + nn], in0=gcs[:, 0:nn], in1=pmm[2][:, h0:h0 + nn], op=ALU.mult),
                         reads=[bgcs, bpmm[2]], writes=[bpbuf])
                    P.op("act", lambda E, nn=nn, ct=ct: E.copy(out=phalo[:, ct, :], in_=pbuf[:, nn:nn + 2]), reads=[bpbuf], writes=[bphalo])
                    if t0 == 0:
                        continue
                    P.op("dve", lambda E, nn=nn, ct=ct: E.tensor_scalar(out=cv[:, 0:nn], in0=pbuf[:, 0:nn], scalar1=cw[:, 0, ct:ct + 1], scalar2=None, op0=ALU.mult),
                         reads=[bpbuf, bcw], writes=[bcv])
                    P.op("dve", lambda E, nn=nn, ct=ct: E.scalar_tensor_tensor(out=cv[:, 0:nn], in0=pbuf[:, 1:1 + nn], scalar=cw[:, 1, ct:ct + 1], in1=cv[:, 0:nn],
                                                                               op0=ALU.mult, op1=ALU.add), reads=[bpbuf, bcw, bcv], writes=[bcv])
                    P.op("dve", lambda E, nn=nn, ct=ct: E.scalar_tensor_tensor(out=cv[:, 0:nn], in0=pbuf[:, 2:2 + nn], scalar=cw[:, 2, ct:ct + 1], in1=cv[:, 0:nn],
                                                                               op0=ALU.mult, op1=ALU.add), reads=[bpbuf, bcw, bcv], writes=[bcv])
                    P.op("dve", lambda E, nn=nn, h0=h0, pmm=pmm: E.tensor_tensor(out=cv[:, 0:nn], in0=cv[:, 0:nn], in1=pmm[0][:, h0:h0 + nn], op=ALU.mult),
                         reads=[bcv, bpmm[0]], writes=[bcv])
                    P.op("act", lambda E, nn=nn, h0=h0, pmm=pmm: E.activation(out=sz[:, 0:nn], in_=pmm[3][:, h0:h0 + nn], func=AF.Silu), reads=[bpmm[3]], writes=[bsz])
                    P.op("dve", lambda E, nn=nn, h0=h0, ct=ct: E.tensor_tensor(out=y1T[:, ct, h0:h0 + nn], in0=cv[:, 0:nn], in1=sz[:, 0:nn], op=ALU.mult),
                         reads=[bcv, bsz], writes=[by1T])
            if t0 == 0:
                continue
            for t in range(ntl):
                r0 = t0 + t * 128
                P.dma("sp", xt[:], x_d[r0:r0 + 128, :], reads=([fz["xbuf"]] if fz else []), writes=[bxt])
                outproj_post(C, y1T, by1T, 16, wout, bwout, t, xt[:], bxt, npost, bnpost, pso, bpso, yo, byo, hn, bhn, ss, bss,
                             out_d[r0 - 128:r0, :], bout)
        if fz:
            barrier(P)
        else:
            P.finish([bout])
    return nc


_IDENT = np.eye(128, dtype=np.float32)
_CACHE = {}


def _get(name, fn):
    if name not in _CACHE:
        _CACHE[name] = fn()
    return _CACHE[name]


def run_L2(inp, o_full, ys_full):
    nc = _get("L2", build_L2)
    w_in = inp["w_in_even"][0]
    wz = np.ascontiguousarray(np.concatenate([w_in[:, 3072:4096], w_in[:, 5136:6160]], axis=1))
    maps = []
    for c in range(8):
        b, r = divmod(c, 4)
        sl = slice(r * 2048, (r + 1) * 2048)
        maps.append({"x": np.ascontiguousarray(inp["x"][b, sl]), "o": np.ascontiguousarray(o_full[b, sl]),
                     "ys": np.ascontiguousarray(ys_full[b, sl]), "wz": wz, "wglu": np.ascontiguousarray(inp["w_glu"][0]),
                     "wout": np.ascontiguousarray(inp["w_out_even"][0]), "npre": np.ascontiguousarray(inp["norm_pre"][0]),
                     "npost": np.ascontiguousarray(inp["norm_post"][0]), "gnw": np.ascontiguousarray(inp["gdn_norm_w"][0]),
                     "ident": _IDENT})
    res = run_bass_kernel_spmd(nc, maps, core_ids=list(range(8)))
    x1 = np.empty((2, 8192, 1024), np.float32)
    for c in range(8):
        b, r = divmod(c, 4)
        x1[b, r * 2048:(r + 1) * 2048] = res.results[c]["out"]
    return x1


def run_L3(inp, x1):
    nc = _get("L3", build_L3)
    maps = []
    for c in range(8):
        b, r = divmod(c, 4)
        xh = np.zeros((2048 + 128, 1024), np.float32)
        xh[128:] = x1[b, r * 2048:(r + 1) * 2048]
        if r > 0:
            xh[:128] = x1[b, r * 2048 - 128:r * 2048]
        maps.append({"x": xh, "win": np.ascontiguousarray(inp["w_in_odd"][0]), "wout": np.ascontiguousarray(inp["w_out_odd"][0]),
                     "conv": np.ascontiguousarray(inp["conv_short"][0]), "npre": np.ascontiguousarray(inp["norm_pre"][1]),
                     "npost": np.ascontiguousarray(inp["norm_post"][1]), "ident": _IDENT})
    res = run_bass_kernel_spmd(nc, maps, core_ids=list(range(8)))
    out = np.empty((2, 8192, 1024), np.float32)
    for c in range(8):
        b, r = divmod(c, 4)
        out[b, r * 2048:(r + 1) * 2048] = res.results[c]["out"]
    return out


I32 = mybir.dt.int32
TAUS = np.array(list(range(17)) + [32, 64, 128, 256, 512, 1024, 2048, 4096] + list(range(15, -1, -1)), np.float32)
NTAU = len(TAUS)


def _s5_consts():
    mk = np.zeros((128, 2, 16, 16), np.float32)
    idm = np.zeros((128, 2, 16, 16), np.float32)
    for kt2 in range(2):
        for sp in range(8):
            s = kt2 * 8 + sp
            for h in range(16):
                mk[sp * 16 + h, kt2, s:, :] = 1.0
                idm[sp * 16 + h, kt2, s, h] = 1.0
    return mk.reshape(128, 2, 256), idm.reshape(128, 2, 256)


def barrier(P):
    for e in P.ENG:
        for e2 in P.ENG:
            if P.cnt[e2] > 0:
                P._wait(e, ("c", e2, P.cnt[e2]))
        for q in P.dsem:
            j1 = P.dcnt[q]
            for j in range(max(0, j1 - NDS), j1):
                P._wait(e, ("d", q, j % NDS, 16 * (j // NDS + 1)))


def build_L1b(S=8192, fz=None):
    nc = fz["nc"] if fz else bass.Bass("TRN2", target_bir_lowering=False)
    pfx = fz["pfx"] if fz else ""

    def D(name, shape):
        if fz and name in fz["share"]:
            return fz["share"][name]
        return nc.dram_tensor(pfx + name, shape, F32, kind="ExternalInput").ap()
    x_d = D("x", [S, 1024]); npre_d = D("npre", [1024]); wu_d = D("wu", [1024, 256])
    lre_d = D("lre", [16, 64]); lim_d = D("lim", [16, 64]); bre_d = D("bre", [16, 64, 16]); bim_d = D("bim", [16, 64, 16])
    cre_d = D("cre", [16, 16, 64]); cim_d = D("cim", [16, 16, 64]); ldt_d = D("ldt", [16]); dd_d = D("dd", [256])
    taus_d = D("taus", [NTAU]); mk_d = D("mk", [128, 2, 256]); idm_d = D("idm", [128, 2, 256]); ident_d = D("ident", [128, 128])
    ys_d = fz["out"] if fz else nc.dram_tensor("ys", [S, 256], F32, kind="ExternalOutput").ap()
    NCH = S // 16
    NST = S // 512
    with ExitStack() as st:
        C = Ctx(nc, st, fz["P"], pfx) if fz else Ctx(nc, st); P = C.P
        idf, bidf, idb, bidb = make_ident(C, ident_d)
        ptr, bptr = C.ps("ptr", [128, 1024], BF16)
        py, bpy = C.ps("py", [128, 1024])
        G = []; bG = []
        for i in range(4):
            t_, b_ = C.ps("g%d" % i, [128, 512]); G.append(t_); bG.append(b_)
        U, bU = C.sb("U", [128, 2, 16, NCH], BF16)
        with ExitStack() as st2:
            C2 = Ctx(nc, st2, P, C.pfx)
            ext = fz.get("uTp") if fz else None
            if ext:
                uTp, buTp = ext
            else:
                uTp, buTp = C2.sb("uTp", [128, 2, 16, NCH], BF16)
            with ExitStack() as st1:
                C1 = Ctx(nc, st1, P, C.pfx)
                npre, bnpre = bcast_row_load(C1, "npre", npre_d, 1024)
                wu, bwu = load_w_bf16(C1, "wu", wu_d, 8, 256)
                xt, bxt = C1.sb("xt", [128, 1024])
                sq, bsq = C1.sb("sq", [128, 1024])
                hn, bhn = C1.sb("hn", [128, 1024], BF16)
                ss, bss = C1.sb("ss", [128, 1])
                hT, bhT = C1.sb("hT", [128, 8, 512], BF16)
                for s_ in range(0 if ext else NST):
                    for t in range(4):
                        r0 = s_ * 512 + t * 128
                        P.dma("sp", xt[:], x_d[r0:r0 + 128, :], writes=[bxt])
                        rms_rstd(C1, xt[:], bxt, 1024, sq[:], bsq, ss, bss)
                        P.op("dve", lambda E: E.scalar_tensor_tensor(out=hn[:], in0=xt[:], scalar=ss[:, 0:1], in1=npre[:],
                                                                     op0=ALU.mult, op1=ALU.mult), reads=[bxt, bss, bnpre], writes=[bhn])
                        transpose8(C1, hn, bhn, idb, bidb, ptr, bptr, hT[:, :, t * 128:(t + 1) * 128], bhT, eng="act")
                    for blk in range(2):
                        pb = G[blk]
                        fns = [(lambda E, kt=kt, blk=blk, pb=pb: E.matmul(
                            pb[:].rearrange("p (s n) -> p s n", s=16), lhsT=wu[:, kt, blk * 128:(blk + 1) * 128],
                            rhs=hT[:, kt, :].rearrange("p (n s) -> p s n", s=16), start=(kt == 0), stop=(kt == 7))) for kt in range(8)]
                        P.mm_group(fns, reads=[bwu, bhT], writes=[bG[blk]])
                        P.op("act" if blk == 0 else "dve",
                             (lambda E, blk=blk, pb=pb, s_=s_: E.copy(out=uTp[:, blk, :, 32 * s_:32 * s_ + 32], in_=pb[:].rearrange("p (s n) -> p s n", s=16)))
                             if blk == 0 else
                             (lambda E, blk=blk, pb=pb, s_=s_: E.tensor_copy(out=uTp[:, blk, :, 32 * s_:32 * s_ + 32], in_=pb[:].rearrange("p (s n) -> p s n", s=16))),
                             reads=[bG[blk]], writes=[buTp])
                barrier(P)
            ud2 = nc.dram_tensor(pfx + "ud2", [16, 2, 8, 16, NCH], BF16)
            bud2 = Buf("ud2", multi=True)
            bU.multi = True; bU.w = []
            for g in range(16):
                P.dma("sp", ud2.ap()[g].rearrange("k sp h n -> h (k sp) n"),
                      uTp[(g % 8) * 16:(g % 8 + 1) * 16, g // 8, :, :], reads=[buTp], writes=[bud2])
            for g in range(16):
                P.dma("sp", U[:, :, g, :], ud2.ap()[g].rearrange("k sp h n -> (sp h) k n"), reads=[bud2], writes=[bU])
            barrier(P)
        lre, blre = C.sb("lre", [128, 8]); lim, blim = C.sb("lim", [128, 8]); ldt, bldt = C.sb("ldt", [128, 8])
        TAU, bTAU = bcast_row_load(C, "TAU", taus_d, NTAU)
        Er, bEr = C.sb("Er", [128, 8, NTAU]); Ei, bEi = C.sb("Ei", [128, 8, NTAU]); NEi, bNEi = C.sb("NEi", [128, 8, NTAU])
        Hr, bHr = C.sb("Hr", [128, 8, 17, 16]); nHi, bnHi = C.sb("nHi", [128, 8, 17, 16])
        WbT, bWbT = C.sb("WbT", [128, 2, 8, 2, 128], BF16)
        Toep, bToep = C.sb("Toep", [128, 2, 16, 256], BF16)
        with ExitStack() as st3:
            C3 = Ctx(nc, st3, P, C.pfx)
            Br, bBr = C3.sb("Br", [128, 8, 16]); Bi, bBi = C3.sb("Bi", [128, 8, 16])
            Cr, bCr = C3.sb("Cr", [128, 8, 16]); Ci, bCi = C3.sb("Ci", [128, 8, 16])
            dcol, bdcol = C3.sb("dcol", [128, 16])
            MK, bMK = C3.sb("MK", [128, 2, 256]); IDM, bIDM = C3.sb("IDM", [128, 2, 256])
            P.dma("sp", MK[:], mk_d, writes=[bMK]); P.dma("sp", IDM[:], idm_d, writes=[bIDM])
            for _b in (blre, blim, bldt, bBr, bBi, bCr, bCi, bdcol):
                _b.multi = True; _b.w = []
            for two in range(2):
                hs = slice(64 * two, 64 * two + 64)
                P.dma("sp", lre[hs, :], lre_d.rearrange("(gp two) p -> two p gp", two=2)[two], writes=[blre])
                P.dma("sp", lim[hs, :], lim_d.rearrange("(gp two) p -> two p gp", two=2)[two], writes=[blim])
                P.dma("sp", ldt[hs, :], ldt_d.rearrange("(gp two) -> two gp", two=2)[two].partition_broadcast(64), writes=[bldt])
                P.dma("sp", Br[hs], bre_d.rearrange("(gp two) p h -> two p gp h", two=2)[two], writes=[bBr])
                P.dma("sp", Bi[hs], bim_d.rearrange("(gp two) p h -> two p gp h", two=2)[two], writes=[bBi])
                for gp in range(8):
                    P.dma("sp", Cr[hs, gp, :], cre_d[2 * gp + two].rearrange("h p -> p h"), writes=[bCr])
                    P.dma("sp", Ci[hs, gp, :], cim_d[2 * gp + two].rearrange("h p -> p h"), writes=[bCi])
            for sp in range(8):
                P.dma("sp", dcol[sp * 16:(sp + 1) * 16, :], dd_d.rearrange("(g h) -> h g", h=16), writes=[bdcol])
            sm = {}
            for nm in ("dt", "lr", "lrdt", "th", "den", "nr", "fre", "fim", "t8a", "t8b"):
                sm[nm] = C3.sb("sm_" + nm, [128, 8])
            T41 = {}
            for nm in ("ARG", "MARG", "MAG", "MAGN", "SIN", "COS", "ErN", "EiN", "rt", "rk"):
                T41[nm] = C3.sb("t41_" + nm, [128, 8, NTAU])
            rki, brki = C3.sb("rki", [128, 8, NTAU], I32)

            def tt(eng, out, bo, a, ba, b, bb_, op):
                P.op(eng, lambda E: E.tensor_tensor(out=out, in0=a, in1=b, op=op), reads=[ba, bb_], writes=[bo])

            dt, bdt = sm["dt"]; lr, blr = sm["lr"]; lrdt, blrdt = sm["lrdt"]; th, bth = sm["th"]
            P.op("act", lambda E: E.activation(out=dt[:], in_=ldt[:], func=AF.Exp), reads=[bldt], writes=[bdt])
            P.op("dve", lambda E: E.tensor_scalar(out=lr[:], in0=lre[:], scalar1=-1e-4, scalar2=None, op0=ALU.min), reads=[blre], writes=[blr])
            tt("dve", lrdt[:], blrdt, lr[:], blr, dt[:], bdt, ALU.mult)
            tt("dve", th[:], bth, lim[:], blim, dt[:], bdt, ALU.mult)
            ARG, bARG = T41["ARG"]; MARG, bMARG = T41["MARG"]; MAG, bMAG = T41["MAG"]; MAGN, bMAGN = T41["MAGN"]
            SIN, bSIN = T41["SIN"]; COS, bCOS = T41["COS"]; ErN, bErN = T41["ErN"]; EiN, bEiN = T41["EiN"]
            rt, brt = T41["rt"]; rk, brk = T41["rk"]
            tb = TAU[:].unsqueeze(1).to_broadcast([128, 8, NTAU])
            tt("dve", ARG[:], bARG, th[:].unsqueeze(2).to_broadcast([128, 8, NTAU]), bth, tb, bTAU, ALU.mult)
            tt("dve", MARG[:], bMARG, lrdt[:].unsqueeze(2).to_broadcast([128, 8, NTAU]), blrdt, tb, bTAU, ALU.mult)
            P.op("act", lambda E: E.activation(out=MAG[:], in_=MARG[:], func=AF.Exp), reads=[bMARG], writes=[bMAG])
            P.op("act", lambda E: E.activation(out=MAGN[:, :, 0:17], in_=MARG[:, :, 0:17], func=AF.Exp, scale=-1.0), reads=[bMARG], writes=[bMAGN])

            def sin_of(dst, bdst, shift):
                P.op("dve", lambda E: E.tensor_scalar(out=rt[:], in0=ARG[:], scalar1=float(shift), scalar2=None, op0=ALU.add), reads=[bARG], writes=[brt])
                P.op("dve", lambda E: E.tensor_scalar(out=rki[:], in0=rt[:], scalar1=float(1.0 / (2 * np.pi)), scalar2=None, op0=ALU.mult), reads=[brt], writes=[brki])
                P.op("dve", lambda E: E.tensor_copy(out=rk[:], in_=rki[:]), reads=[brki], writes=[brk])
                P.op("dve", lambda E: E.scalar_tensor_tensor(out=rt[:], in0=rk[:], scalar=float(-2 * np.pi), in1=rt[:], op0=ALU.mult, op1=ALU.add),
                     reads=[brk, brt], writes=[brt])
                P.op("dve", lambda E: E.tensor_scalar(out=rt[:], in0=rt[:], scalar1=-3.14159, scalar2=3.14159, op0=ALU.max, op1=ALU.min), reads=[brt], writes=[brt])
                P.op("act", lambda E: E.activation(out=dst[:], in_=rt[:], func=AF.Sin), reads=[brt], writes=[bdst])

            sin_of(SIN, bSIN, 0.0)
            sin_of(COS, bCOS, np.pi / 2)
            tt("dve", Er[:], bEr, MAG[:], bMAG, COS[:], bCOS, ALU.mult)
            tt("dve", Ei[:], bEi, MAG[:], bMAG, SIN[:], bSIN, ALU.mult)
            P.op("dve", lambda E: E.tensor_scalar(out=NEi[:], in0=Ei[:], scalar1=-1.0, scalar2=None, op0=ALU.mult), reads=[bEi], writes=[bNEi])
            tt("dve", ErN[:, :, 0:17], bErN, MAGN[:, :, 0:17], bMAGN, COS[:, :, 0:17], bCOS, ALU.mult)
            tt("dve", EiN[:, :, 0:17], bEiN, MAGN[:, :, 0:17], bMAGN, SIN[:, :, 0:17], bSIN, ALU.mult)
            P.op("dve", lambda E: E.tensor_scalar(out=EiN[:, :, 0:17], in0=EiN[:, :, 0:17], scalar1=-1.0, scalar2=None, op0=ALU.mult), reads=[bEiN], writes=[bEiN])
            den, bden = sm["den"]; nr, bnr = sm["nr"]; fre, bfre = sm["fre"]; fim, bfim = sm["fim"]; t8a, bt8a = sm["t8a"]; t8b, bt8b = sm["t8b"]
            tt("dve", den[:], bden, lr[:], blr, lr[:], blr, ALU.mult)
            tt("dve", t8a[:], bt8a, lim[:], blim, lim[:], blim, ALU.mult)
            tt("dve", den[:], bden, den[:], bden, t8a[:], bt8a, ALU.add)
            P.op("dve", lambda E: E.reciprocal(out=den[:], in_=den[:]), reads=[bden], writes=[bden])
            P.op("dve", lambda E: E.tensor_scalar(out=nr[:], in0=Er[:, :, 1], scalar1=-1.0, scalar2=None, op0=ALU.add), reads=[bEr], writes=[bnr])
            tt("dve", fre[:], bfre, nr[:], bnr, lr[:], blr, ALU.mult)
            tt("dve", t8a[:], bt8a, Ei[:, :, 1], bEi, lim[:], blim, ALU.mult)
            tt("dve", fre[:], bfre, fre[:], bfre, t8a[:], bt8a, ALU.add)
            tt("dve", fre[:], bfre, fre[:], bfre, den[:], bden, ALU.mult)
            tt("dve", fim[:], bfim, Ei[:, :, 1], bEi, lr[:], blr, ALU.mult)
            tt("dve", t8b[:], bt8b, nr[:], bnr, lim[:], blim, ALU.mult)
            tt("dve", fim[:], bfim, fim[:], bfim, t8b[:], bt8b, ALU.subtract)
            tt("dve", fim[:], bfim, fim[:], bfim, den[:], bden, ALU.mult)

            def cmul(outr, boutr, outi, bouti, ar, bar, ai, bai, br_, bbr_, bi_, bbi_, tmp, btmp):
                tt("dve", outr, boutr, ar, bar, br_, bbr_, ALU.mult)
                tt("dve", tmp, btmp, ai, bai, bi_, bbi_, ALU.mult)
                tt("dve", outr, boutr, outr, boutr, tmp, btmp, ALU.subtract)
                tt("dve", outi, bouti, ar, bar, bi_, bbi_, ALU.mult)
                tt("dve", tmp, btmp, ai, bai, br_, bbr_, ALU.mult)
                tt("dve", outi, bouti, outi, bouti, tmp, btmp, ALU.add)

            bbr, bbbr = C3.sb("bbr", [128, 8, 16]); bbi, bbbi = C3.sb("bbi", [128, 8, 16]); tmp16, btmp16 = C3.sb("tmp16", [128, 8, 16])
            fb = lambda t_: t_[:].unsqueeze(2).to_broadcast([128, 8, 16])
            cmul(bbr[:], bbbr, bbi[:], bbbi, fb(fre), bfre, fb(fim), bfim, Br[:], bBr, Bi[:], bBi, tmp16[:], btmp16)
            Gr, bGr = C3.sb("Gr", [128, 8, 16, 16]); Gi, bGi = C3.sb("Gi", [128, 8, 16, 16])
            WPr, bWPr = C3.sb("WPr", [128, 8, 16, 16]); WPi, bWPi = C3.sb("WPi", [128, 8, 16, 16])
            Hi, bHi = C3.sb("Hi", [128, 8, 17, 16]); tmpH, btmpH = C3.sb("tmpH", [128, 8, 17, 16])
            eb = lambda t_, j0, j1: t_[:, :, j0:j1].unsqueeze(3).to_broadcast([128, 8, j1 - j0, 16])
            vb = lambda t_, n_: t_[:].unsqueeze(2).to_broadcast([128, 8, n_, 16])
            cmul(Gr[:], bGr, Gi[:], bGi, eb(ErN, 0, 16), bErN, eb(EiN, 0, 16), bEiN, vb(bbr, 16), bbbr, vb(bbi, 16), bbbi, tmpH[:, :, 0:16, :], btmpH)
            cmul(WPr[:], bWPr, WPi[:], bWPi, eb(Er, 25, 41), bEr, eb(Ei, 25, 41), bEi, vb(bbr, 16), bbbr, vb(bbi, 16), bbbi, tmpH[:, :, 0:16, :], btmpH)
            cmul(Hr[:], bHr, Hi[:], bHi, eb(Er, 0, 17), bEr, eb(Ei, 0, 17), bEi, vb(Cr, 17), bCr, vb(Ci, 17), bCi, tmpH[:], btmpH)
            P.op("dve", lambda E: E.tensor_scalar(out=nHi[:], in0=Hi[:], scalar1=-1.0, scalar2=None, op0=ALU.mult), reads=[bHi], writes=[bnHi])
            for gp in range(8):
                for kt2 in range(2):
                    for c, (WP_, bWP_) in enumerate(((WPr, bWPr), (WPi, bWPi))):
                        P.op("pe", lambda E, gp=gp, kt2=kt2, WP_=WP_: E.transpose(
                            out=G[2][:, 0:128], in_=WP_[:, gp, kt2 * 8:(kt2 + 1) * 8, :].rearrange("p s h -> p (s h)"), identity=idf[:]),
                            reads=[bWP_, bidf], writes=[bG[2]])
                        P.op("act", lambda E, gp=gp, kt2=kt2, c=c: E.copy(out=WbT[:, kt2, gp, c, :], in_=G[2][:, 0:128]), reads=[bG[2]], writes=[bWbT])
            tmpT, btmpT = C3.sb("tmpT", [128, 256])
            for g in range(16):
                gp = g // 2; hs = slice(64 * (g % 2), 64 * (g % 2) + 64)
                for kt2 in range(2):
                    fns = [
                        lambda E, gp=gp, hs=hs, kt2=kt2: E.matmul(G[3][:, 0:256], lhsT=Gr[hs, gp, kt2 * 8:(kt2 + 1) * 8, :].rearrange("p s h -> p (s h)"),
                                                                  rhs=Hr[hs, gp, 0:16, :].rearrange("p t h -> p (t h)"), start=True, stop=False),
                        lambda E, gp=gp, hs=hs, kt2=kt2: E.matmul(G[3][:, 0:256], lhsT=Gi[hs, gp, kt2 * 8:(kt2 + 1) * 8, :].rearrange("p s h -> p (s h)"),
                                                                  rhs=nHi[hs, gp, 0:16, :].rearrange("p t h -> p (t h)"), start=False, stop=True)]
                    P.mm_group(fns, reads=[bGr, bGi, bHr, bnHi], writes=[bG[3]])
                    P.op("dve", lambda E, kt2=kt2: E.tensor_tensor(out=tmpT[:], in0=G[3][:, 0:256], in1=MK[:, kt2, :], op=ALU.mult),
                         reads=[bG[3], bMK], writes=[btmpT])
                    P.op("dve", lambda E, kt2=kt2, g=g: E.scalar_tensor_tensor(out=Toep[:, kt2, g, :], in0=IDM[:, kt2, :], scalar=dcol[:, g:g + 1], in1=tmpT[:],
                                                                               op0=ALU.mult, op1=ALU.add), reads=[bIDM, bdcol, btmpT], writes=[bToep])
            barrier(P)
        X = {}
        for bufn in ("A", "B"):
            for c in ("re", "im"):
                X[(bufn, c)] = (C.sb("X%s%s" % (bufn, c), [128, 8, NCH + 1])[0], [Buf("X%s%s%d" % (bufn, c, gp)) for gp in range(8)])
        Ysb, bYsb = C.sb("Ysb", [128, 16, 256], BF16 if (fz and fz.get("ybf16")) else F32)
        for key in X:
            t_, bl = X[key]
            P.op("dve", lambda E, t_=t_: E.memset(t_[:, :, 0:1], 0.0), writes=bl)
        for gp in range(8):
            for c, cn in enumerate(("re", "im")):
                px = G[c]
                fns = []
                for two in range(2):
                    g = 2 * gp + two
                    for kt2 in range(2):
                        fns.append(lambda E, two=two, g=g, kt2=kt2, gp=gp, c=c, px=px: E.matmul(
                            px[64 * two:64 * two + 64, :], lhsT=WbT[:, kt2, gp, c, 64 * two:64 * two + 64], rhs=U[:, kt2, g, :],
                            start=(kt2 == 0), stop=(kt2 == 1)))
                P.mm_group(fns, reads=[bWbT, bU], writes=[bG[c]])
                xt_, xb_ = X[("A", cn)]
                P.op("act", lambda E, xt_=xt_, gp=gp, px=px: E.copy(out=xt_[:, gp, 1:NCH + 1], in_=px[:]), reads=[bG[c]], writes=[xb_[gp]])
        for k in range(9):
            d = 1 << k
            j = 16 if k == 0 else 16 + k
            src, dst = ("A", "B") if k % 2 == 0 else ("B", "A")
            sre, bsre = X[(src, "re")]; sim, bsim = X[(src, "im")]
            dre, bdre = X[(dst, "re")]; dim_, bdim = X[(dst, "im")]
            P.op("dve", lambda E, dre=dre, sre=sre, d=d: E.tensor_copy(out=dre[:, :, 1:1 + d], in_=sre[:, :, 1:1 + d]), reads=bsre, writes=bdre)
            P.op("pool", lambda E, dim_=dim_, sim=sim, d=d: E.tensor_copy(out=dim_[:, :, 1:1 + d], in_=sim[:, :, 1:1 + d]), reads=bsim, writes=bdim)
            for gp in range(8):
                lo = slice(1, NCH + 1 - d); hi = slice(1 + d, NCH + 1)
                P.op("dve", lambda E, gp=gp, j=j, dre=dre, sre=sre, lo=lo, hi=hi: E.scalar_tensor_tensor(
                    out=dre[:, gp, hi], in0=sre[:, gp, lo], scalar=Er[:, gp, j:j + 1], in1=sre[:, gp, hi], op0=ALU.mult, op1=ALU.add),
                    reads=[bsre[gp], bEr], writes=[bdre[gp]])
                P.op("dve", lambda E, gp=gp, j=j, dre=dre, sim=sim, lo=lo, hi=hi: E.scalar_tensor_tensor(
                    out=dre[:, gp, hi], in0=sim[:, gp, lo], scalar=NEi[:, gp, j:j + 1], in1=dre[:, gp, hi], op0=ALU.mult, op1=ALU.add),
                    reads=[bsim[gp], bNEi, bdre[gp]], writes=[bdre[gp]])
                P.op("dve", lambda E, gp=gp, j=j, dim_=dim_, sim=sim, lo=lo, hi=hi: E.scalar_tensor_tensor(
                    out=dim_[:, gp, hi], in0=sim[:, gp, lo], scalar=Er[:, gp, j:j + 1], in1=sim[:, gp, hi], op0=ALU.mult, op1=ALU.add),
                    reads=[bsim[gp], bEr], writes=[bdim[gp]])
                P.op("dve", lambda E, gp=gp, j=j, dim_=dim_, sre=sre, lo=lo, hi=hi: E.scalar_tensor_tensor(
                    out=dim_[:, gp, hi], in0=sre[:, gp, lo], scalar=Ei[:, gp, j:j + 1], in1=dim_[:, gp, hi], op0=ALU.mult, op1=ALU.add),
                    reads=[bsre[gp], bEi, bdim[gp]], writes=[bdim[gp]])
        fre_, bfre_ = X[("B", "re")]; fim_, bfim_ = X[("B", "im")]
        bys = None if fz else Buf("ys", multi=True)
        ysv = ys_d.rearrange("(n t) c -> n t c", t=16)
        for jt in range(NCH // 128):
            for gq in range(4):
                fns = []
                for gi in range(4):
                    g = 4 * gq + gi; gp = g // 2; hs = slice(64 * (g % 2), 64 * (g % 2) + 64)
                    o_ = (gi * 256, (gi + 1) * 256)
                    for kt2 in range(2):
                        fns.append(lambda E, o_=o_, g=g, kt2=kt2, jt=jt: E.matmul(
                            py[:, o_[0]:o_[1]], lhsT=U[:, kt2, g, jt * 128:(jt + 1) * 128], rhs=Toep[:, kt2, g, :], start=(kt2 == 0), stop=False))
                    fns.append(lambda E, o_=o_, gp=gp, hs=hs, jt=jt: E.matmul(
                        py[:, o_[0]:o_[1]], lhsT=fre_[hs, gp, jt * 128:(jt + 1) * 128], rhs=Hr[hs, gp, 1:17, :].rearrange("p t h -> p (t h)"),
                        start=False, stop=False))
                    fns.append(lambda E, o_=o_, gp=gp, hs=hs, jt=jt: E.matmul(
                        py[:, o_[0]:o_[1]], lhsT=fim_[hs, gp, jt * 128:(jt + 1) * 128], rhs=nHi[hs, gp, 1:17, :].rearrange("p t h -> p (t h)"),
                        start=False, stop=True))
                P.mm_group(fns, reads=[bU, bToep, bHr, bnHi] + bfre_ + bfim_, writes=[bpy])
                P.op("act" if gq % 2 == 0 else "dve",
                     (lambda E, gq=gq: E.copy(out=Ysb[:].rearrange("p t (g h) -> p g t h", h=16)[:, 4 * gq:4 * gq + 4],
                                              in_=py[:].rearrange("p (g t h) -> p g t h", g=4, h=16)))
                     if gq % 2 == 0 else
                     (lambda E, gq=gq: E.tensor_copy(out=Ysb[:].rearrange("p t (g h) -> p g t h", h=16)[:, 4 * gq:4 * gq + 4],
                                                     in_=py[:].rearrange("p (g t h) -> p g t h", g=4, h=16))),
                     reads=[bpy], writes=[bYsb])
            P.dma("sp", ysv[jt * 128:(jt + 1) * 128, :, :], Ysb[:], reads=[bYsb], writes=[fz["obuf_of"](jt) if fz else bys])
            if fz:
                fz["after_chunk"](jt)
        if fz:
            barrier(P)
        else:
            P.finish([bys])
    return nc


def run_L1b(inp):
    nc = _get("L1b", build_L1b)
    mk, idm = _s5_consts()
    w_in = inp["w_in_even"][0]
    maps = []
    for c in range(8):
        b, r = divmod(c, 4)
        gs = slice(16 * r, 16 * r + 16)
        maps.append({"x": np.ascontiguousarray(inp["x"][b]), "npre": np.ascontiguousarray(inp["norm_pre"][0]),
                     "wu": np.ascontiguousarray(w_in[:, 4112 + 256 * r:4112 + 256 * (r + 1)]),
                     "lre": np.ascontiguousarray(inp["s5_lam_re"][0, gs]), "lim": np.ascontiguousarray(inp["s5_lam_im"][0, gs]),
                     "bre": np.ascontiguousarray(inp["s5_b_re"][0, gs]), "bim": np.ascontiguousarray(inp["s5_b_im"][0, gs]),
                     "cre": np.ascontiguousarray(inp["s5_c_re"][0, gs]), "cim": np.ascontiguousarray(inp["s5_c_im"][0, gs]),
                     "ldt": np.ascontiguousarray(inp["s5_log_dt"][0, gs]), "dd": np.ascontiguousarray(inp["s5_d"][0, 256 * r:256 * (r + 1)]),
                     "taus": TAUS, "mk": mk, "idm": idm, "ident": _IDENT})
    res = run_bass_kernel_spmd(nc, maps, core_ids=list(range(8)))
    ys = np.empty((2, 8192, 1024), np.float32)
    for c in range(8):
        b, r = divmod(c, 4)
        ys[b, :, 256 * r:256 * (r + 1)] = res.results[c]["ys"]
    return ys


def _gdn_consts():
    p = np.arange(64)[:, None]; f = np.arange(64)[None, :]
    negu = np.where(f >= p, 0.0, -30000.0)
    negls = np.where(f < p, 0.0, -30000.0)
    nsu = np.where(f > p, -1.0, 0.0)
    i64 = np.eye(64)
    c64 = np.stack([negu, negls, nsu, i64], axis=1).astype(np.float32)
    cmask = np.ones((2, 512), np.float32); cmask[:, 0::64] = 0.0
    sel = np.zeros((2, 2, 128), np.float32); sel[0, 0, :] = 1.0; sel[1, 1, :] = 1.0
    return c64, cmask, sel


def build_L1a(S=8192, fz=None):
    nc = fz["nc"] if fz else bass.Bass("TRN2", target_bir_lowering=False)
    pfx = fz["pfx"] if fz else ""

    def D(name, shape):
        if fz and name in fz["share"]:
            return fz["share"][name]
        return nc.dram_tensor(pfx + name, shape, F32, kind="ExternalInput").ap()
    x_d = D("x", [S, 1024]); npre_d = D("npre", [1024]); w_d = D("w", [1024, 768]); wb_d = D("wb", [1024, 2]); wa_d = D("wa", [1024, 2])
    conv_d = D("conv", [4, 768]); alog_d = D("alog", [2]); dtb_d = D("dtb", [2])
    ident_d = D("ident", [128, 128]); c64_d = D("c64", [64, 4, 64]); cmask_d = D("cmask", [2, 512]); sel_d = D("sel", [2, 2, 128])
    ones_d = D("ones", [128, 128])
    o_d = fz["out"] if fz else nc.dram_tensor("o", [S, 256], F32, kind="ExternalOutput").ap()
    NST = S // 512
    with ExitStack() as st:
        C = Ctx(nc, st, fz["P"], pfx) if fz else Ctx(nc, st); P = C.P
        idf, bidf, idb, bidb = make_ident(C, ident_d)
        npre, bnpre = bcast_row_load(C, "npre", npre_d, 1024)
        w, bw = load_w_bf16(C, "w", w_d, 8, 768)
        wb, bwb = load_w_bf16(C, "wb", wb_d, 8, 2)
        wa, bwa = load_w_bf16(C, "wa", wa_d, 8, 2)
        cw, bcw = C.sb("cw", [128, 4, 6])
        P.dma("sp", cw[:], conv_d.rearrange("j (c p) -> p j c", p=128), writes=[bcw])
        extu = fz.get("uTp") if fz else None
        if extu:
            wu_d = D("wu", [1024, 256])
            wu, bwu = load_w_bf16(C, "wu", wu_d, 8, 256)
            uTp, buTp = extu
        c64, bc64 = C.sb("c64", [64, 4, 64]); P.dma("sp", c64[:], c64_d, writes=[bc64])
        NEGU = c64[:, 0, :]; NEGLS = c64[:, 1, :]; NSU = c64[:, 2, :]; I64 = c64[:, 3, :]
        cmask, bcmask = C.sb("cmask", [2, 512]); P.dma("sp", cmask[:], cmask_d, writes=[bcmask])
        sel, bsel = C.sb("sel", [2, 2, 128]); P.dma("sp", sel[:], sel_d, writes=[bsel])
        ones, bones = C.sb("ones", [128, 128]); P.dma("sp", ones[:], ones_d, writes=[bones])
        onesb, bonesb = C.sb("onesb", [128, 128], BF16)
        P.op("dve", lambda E: E.tensor_copy(out=onesb[:], in_=ones[:]), reads=[bones], writes=[bonesb])
        sqb, bsqb = C.sb("sqb", [128, 512], BF16)
        alog, balog = C.sb("alog", [2, 1]); P.dma("sp", alog[:], alog_d.rearrange("(a b) -> a b", b=1), writes=[balog])
        dtb, bdtb = C.sb("dtb", [2, 1]); P.dma("sp", dtb[:], dtb_d.rearrange("(a b) -> a b", b=1), writes=[bdtb])
        negA, bnegA = C.sb("negA", [2, 1])
        P.op("act", lambda E: E.activation(out=negA[:], in_=alog[:], func=AF.Exp), reads=[balog], writes=[bnegA])
        P.op("dve", lambda E: E.tensor_scalar(out=negA[:], in0=negA[:], scalar1=-1.0, scalar2=None, op0=ALU.mult), reads=[bnegA], writes=[bnegA])
        xt, bxt = C.sb("xt", [128, 1024]); sq, bsq = C.sb("sq", [128, 1024], BF16); hn, bhn = C.sb("hn", [128, 1024], BF16)
        ss, bss = C.sb("ss", [128, 1]); hT, bhT = C.sb("hT", [128, 8, 512], BF16)
        raw, _ = C.sb("raw", [128, 6, 515]); braw = [Buf("raw%d" % i) for i in range(6)]
        cvq, bcvq = C.sb("cvq", [128, 512])
        act, _ = C.sb("act", [128, 4, 512]); bact = [Buf("act%d" % i) for i in range(4)]
        vbuf2 = []; qk2 = []; bqk2 = []
        for par_ in range(2):
            vt_, _ = C.sb("vbuf%d" % par_, [128, 2, 512]); vbuf2.append((vt_, [Buf("vb%d_%d" % (par_, i)) for i in range(2)]))
            qt_, _ = C.sb("qk%d" % par_, [128, 4, 512]); qk2.append(qt_); bqk2.append([Buf("qk%d_%d" % (par_, i)) for i in range(4)])
        rn, brn = C.sb("rn", [128, 512])
        brow, bbrow = C.sb("brow", [2, 512]); grow, bgrow = C.sb("grow", [2, 512]); gcrow, bgcrow = C.sb("gcrow", [2, 512])
        GCB2 = []; BB2 = []
        for par_ in range(2):
            GCB2.append([C.sb("GCB%d_%d" % (par_, h), [128, 512]) for h in range(2)])
            BB2.append([C.sb("BB%d_%d" % (par_, h), [128, 512]) for h in range(2)])
        m64h = []; smallh = []
        for h in range(2):
            d_ = {}
            for nm in ("arg1", "scr"):
                d_[nm] = C.sb("m%d_%s" % (h, nm), [64, 512])
            for nm in ("DT", "Ds", "tmp", "Pa", "Pb", "Qa", "Qb"):
                d_[nm] = C.sb("m%d_%s" % (h, nm), [64, 512], BF16)
            m64h.append(d_)
        heads = []
        for h in range(2):
            H = {}
            H["attnT"] = C.sb("attnT%d" % h, [64, 512], BF16); H["Y"] = C.sb("Y%d" % h, [64, 512]); H["Ybf"] = C.sb("Ybf%d" % h, [64, 512], BF16)
            H["EG"] = C.sb("EG%d" % h, [128, 512]); H["qdec"] = C.sb("qdec%d" % h, [128, 512], BF16)
            H["kTb"] = C.sb("kTb%d" % h, [128, 512], BF16); H["Sbf"] = C.sb("Sbf%d" % h, [128, 128], BF16)
            H["bv"] = C.sb("bv%d" % h, [64, 8, 128]); H["kdec"] = C.sb("kdec%d" % h, [64, 8, 128], BF16)
            H["nbg"] = C.sb("nbg%d" % h, [64, 8]); H["osb"] = C.sb("osb%d" % h, [128, 8, 128])
            H["vnew"] = C.sb("vnew%d" % h, [64, 128], BF16); H["rhs2"] = C.sb("rhs2%d" % h, [64, 128], BF16)
            heads.append(H)
        for h in range(2):
            d_ = {}
            for nm in ("gccol", "bcol", "nbcol", "elast", "egc"):
                d_[nm] = C.sb("s%d_%s" % (h, nm), [64, 8])
            smallh.append(d_)
        Sst = [C.sb("S%d" % h, [128, 128]) for h in range(2)]
        for h in range(2):
            P.op("dve", lambda E, h=h: E.memset(Sst[h][0][:], 0.0), writes=[Sst[h][1]])
            P.op("dve", lambda E, h=h: E.memset(heads[h]["Sbf"][0][:], 0.0), writes=[heads[h]["Sbf"][1]])
        P.op("dve", lambda E: E.memset(raw[:, :, 0:3], 0.0), writes=braw)
        ptr, bptr = C.ps("ptr", [128, 1024], BF16)
        G = [C.ps("gp%d" % i, [128, 512]) for i in range(7)]
        GP = G[0:4]
        GA = G[4:7]
        BKS = [(GP[0], GP[1], GP[2]), (GP[3], GA[0], GA[1])]
        ga_ctr = [0]

        def next_ga():
            ga_ctr[0] += 1
            return GA[ga_ctr[0] % 3]
        bo = None if fz else Buf("o", multi=True)
        if fz is not None and fz.get("debug"):
            print("L1a sbuf remaining", nc.sbuf_bytes_remaining)

        def tt(out, bo_, a, ba, b, bb_, op, eng="dve"):
            P.op(eng, lambda E: E.tensor_tensor(out=out, in0=a, in1=b, op=op), reads=ba if isinstance(ba, list) else [ba], writes=[bo_])

        def stageA(s_):
            par = s_ % 2
            qk = qk2[par]; bqk = bqk2[par]; GCB = GCB2[par]; BB = BB2[par]; vb, bvb = vbuf2[par]
            for t in range(4):
                r0 = s_ * 512 + t * 128
                P.dma("sp", xt[:], x_d[r0:r0 + 128, :], writes=[bxt])
                rms_rstd(C, xt[:], bxt, 1024, sq[:], bsq, ss, bss)
                P.op("dve", lambda E: E.scalar_tensor_tensor(out=hn[:], in0=xt[:], scalar=ss[:, 0:1], in1=npre[:],
                                                             op0=ALU.mult, op1=ALU.mult), reads=[bxt, bss, bnpre], writes=[bhn])
                transpose8(C, hn, bhn, idb, bidb, ptr, bptr, hT[:, :, t * 128:(t + 1) * 128], bhT, eng="act")
                yield
            for ct in range(6):
                pa, bpa = next_ga()
                fns = [(lambda E, kt=kt, ct=ct, pa=pa: E.matmul(pa[:], lhsT=w[:, kt, ct * 128:(ct + 1) * 128], rhs=hT[:, kt, :],
                                                                start=(kt == 0), stop=(kt == 7))) for kt in range(8)]
                P.mm_group(fns, reads=[bw, bhT], writes=[bpa])
                P.op("act", lambda E, ct=ct, pa=pa: E.copy(out=raw[:, ct, 3:515], in_=pa[:]), reads=[bpa], writes=[braw[ct]])
                P.op("dve", lambda E, ct=ct: E.tensor_scalar(out=cvq[:], in0=raw[:, ct, 0:512], scalar1=cw[:, 0, ct:ct + 1], scalar2=None, op0=ALU.mult),
                     reads=[braw[ct], bcw], writes=[bcvq])
                for j in range(1, 4):
                    P.op("dve", lambda E, ct=ct, j=j: E.scalar_tensor_tensor(out=cvq[:], in0=raw[:, ct, j:j + 512], scalar=cw[:, j, ct:ct + 1], in1=cvq[:],
                                                                             op0=ALU.mult, op1=ALU.add), reads=[braw[ct], bcw, bcvq], writes=[bcvq])
                P.op("act", lambda E, ct=ct: E.copy(out=raw[:, ct, 0:3], in_=raw[:, ct, 512:515]), reads=[braw[ct]], writes=[braw[ct]])
                if ct < 4:
                    P.op("act", lambda E, ct=ct: E.activation(out=act[:, ct, :], in_=cvq[:], func=AF.Silu), reads=[bcvq], writes=[bact[ct]])
                else:
                    P.op("act", lambda E, ct=ct, vb=vb: E.activation(out=vb[:, ct - 4, :], in_=cvq[:], func=AF.Silu), reads=[bcvq], writes=[bvb[ct - 4]])
                yield
            if extu:
                for blk in range(2):
                    pa, bpa = next_ga()
                    fns = [(lambda E, kt=kt, blk=blk, pa=pa: E.matmul(
                        pa[:].rearrange("p (s n) -> p s n", s=16), lhsT=wu[:, kt, blk * 128:(blk + 1) * 128],
                        rhs=hT[:, kt, :].rearrange("p (n s) -> p s n", s=16), start=(kt == 0), stop=(kt == 7))) for kt in range(8)]
                    P.mm_group(fns, reads=[bwu, bhT], writes=[bpa])
                    P.op("act", lambda E, blk=blk, pa=pa, s_=s_: E.copy(out=uTp[:, blk, :, 32 * s_:32 * s_ + 32], in_=pa[:].rearrange("p (s n) -> p s n", s=16)),
                         reads=[bpa], writes=[buTp])
                    yield
            for ct in range(4):
                pa, bpa = next_ga()
                P.op("act", lambda E, ct=ct: E.activation(out=sqb[:], in_=act[:, ct, :], func=AF.Square), reads=[bact[ct]], writes=[bsqb])
                P.op("pe", lambda E, pa=pa: E.matmul(pa[:], lhsT=onesb[:], rhs=sqb[:], start=True, stop=True), reads=[bonesb, bsqb], writes=[bpa])
                P.op("act", lambda E, pa=pa: E.activation(out=rn[:], in_=pa[:], func=AF.Ln, bias=1e-6, scale=1.0), reads=[bpa], writes=[brn])
                P.op("act", lambda E: E.activation(out=rn[:], in_=rn[:], func=AF.Exp, scale=-0.5), reads=[brn], writes=[brn])
                if ct < 2:
                    P.op("dve", lambda E, ct=ct, qk=qk: E.scalar_tensor_tensor(out=qk[:, ct, :], in0=act[:, ct, :], scalar=float(128 ** -0.5), in1=rn[:],
                                                                               op0=ALU.mult, op1=ALU.mult), reads=[bact[ct], brn], writes=[bqk[ct]])
                else:
                    P.op("dve", lambda E, ct=ct, qk=qk: E.tensor_tensor(out=qk[:, ct, :], in0=act[:, ct, :], in1=rn[:], op=ALU.mult),
                         reads=[bact[ct], brn], writes=[bqk[ct]])
                yield
            pa, bpa = next_ga()
            fns = [(lambda E, kt=kt, pa=pa: E.matmul(pa[0:2, :], lhsT=wb[:, kt, 0:2], rhs=hT[:, kt, :], start=(kt == 0), stop=(kt == 7))) for kt in range(8)]
            P.mm_group(fns, reads=[bwb, bhT], writes=[bpa])
            P.op("act", lambda E, pa=pa: E.activation(out=brow[:], in_=pa[0:2, :], func=AF.Sigmoid), reads=[bpa], writes=[bbrow])
            pa2, bpa2 = next_ga()
            fns = [(lambda E, kt=kt, pa2=pa2: E.matmul(pa2[0:2, :], lhsT=wa[:, kt, 0:2], rhs=hT[:, kt, :], start=(kt == 0), stop=(kt == 7))) for kt in range(8)]
            P.mm_group(fns, reads=[bwa, bhT], writes=[bpa2])
            P.op("act", lambda E, pa2=pa2: E.activation(out=grow[:], in_=pa2[0:2, :], func=AF.Exp, bias=dtb[:, 0:1], scale=1.0), reads=[bpa2, bdtb], writes=[bgrow])
            P.op("act", lambda E: E.activation(out=grow[:], in_=grow[:], func=AF.Ln, bias=1.0, scale=1.0), reads=[bgrow], writes=[bgrow])
            P.op("dve", lambda E: E.tensor_scalar(out=grow[:], in0=grow[:], scalar1=negA[:, 0:1], scalar2=None, op0=ALU.mult), reads=[bgrow, bnegA], writes=[bgrow])
            P.op("dve", lambda E: E.tensor_tensor_scan(out=gcrow[:], data0=cmask[:], data1=grow[:], initial=0.0, op0=ALU.mult, op1=ALU.add),
                 reads=[bcmask, bgrow], writes=[bgcrow])
            yield
            for h in range(2):
                pa, bpa = next_ga()
                P.op("pe", lambda E, h=h, pa=pa: E.matmul(pa[:], lhsT=sel[:, h, :], rhs=gcrow[:], start=True, stop=True), reads=[bsel, bgcrow], writes=[bpa])
                P.op("act", lambda E, h=h, pa=pa, GCB=GCB: E.copy(out=GCB[h][0][:], in_=pa[:]), reads=[bpa], writes=[GCB[h][1]])
                pa, bpa = next_ga()
                P.op("pe", lambda E, h=h, pa=pa: E.matmul(pa[:], lhsT=sel[:, h, :], rhs=brow[:], start=True, stop=True), reads=[bsel, bbrow], writes=[bpa])
                P.op("act", lambda E, h=h, pa=pa, BB=BB: E.copy(out=BB[h][0][:], in_=pa[:]), reads=[bpa], writes=[BB[h][1]])
                yield

        for _ in stageA(0):
            pass
        for s_ in range(NST):
            par = s_ % 2
            qk = qk2[par]; bqk = bqk2[par]; GCB = GCB2[par]; BB = BB2[par]; vb, bvb = vbuf2[par]
            nxt = stageA(s_ + 1) if s_ + 1 < NST else None

            def advance(k):
                if nxt is not None:
                    for _ in range(k):
                        next(nxt, None)
            def stageB(h, qk=qk, bqk=bqk, GCB=GCB, BB=BB, vb=vb, bvb=bvb):
                m64 = m64h[h]; small = smallh[h]; BK = BKS[h]
                qT = qk[:, h, :]; bqT = bqk[h]; kT = qk[:, 2 + h, :]; bkT = bqk[2 + h]; vT = vb[:, h, :]; bvT = bvb[h]
                gcb, bgcb = GCB[h]; bb, bbb = BB[h]
                H = heads[h]
                attnT, battnT = H["attnT"]; Y, bY = H["Y"]; EG, bEG = H["EG"]; qdec, bqdec = H["qdec"]
                Ybf, bYbf = H["Ybf"]
                bv, bbv = H["bv"]; kdec, bkdec = H["kdec"]; nbg, bnbg = H["nbg"]
                arg1, barg1 = m64["arg1"]; scr, bscr = m64["scr"]; DT, bDT = m64["DT"]; Ds, bDs = m64["Ds"]
                tmp, btmp = m64["tmp"]
                gccol, bgccol = small["gccol"]; bcol, bbcol = small["bcol"]; nbcol, bnbcol = small["nbcol"]
                elast, belast = small["elast"]; egc, begc = small["egc"]
                v3 = lambda t_: t_[:].rearrange("p (n f) -> p n f", f=64)
                i64b = I64.unsqueeze(1).to_broadcast([64, 8, 64])
                tt(v3(scr), bscr, gcb[0:64, :].rearrange("p (n f) -> p n f", f=64), [bgcb, bc64], i64b, bc64, ALU.mult)
                P.op("dve", lambda E, scr=scr, gccol=gccol: E.tensor_reduce(out=gccol[:], in_=scr[:].rearrange("p (n f) -> p n f", f=64), axis=AX.X, op=ALU.add), reads=[bscr], writes=[bgccol])
                tt(v3(scr), bscr, bb[0:64, :].rearrange("p (n f) -> p n f", f=64), [bbb, bc64], i64b, bc64, ALU.mult)
                P.op("dve", lambda E, scr=scr, bcol=bcol: E.tensor_reduce(out=bcol[:], in_=scr[:].rearrange("p (n f) -> p n f", f=64), axis=AX.X, op=ALU.add), reads=[bscr], writes=[bbcol])
                yield
                P.op("dve", lambda E: E.tensor_scalar(out=nbcol[:], in0=bcol[:], scalar1=-1.0, scalar2=None, op0=ALU.mult), reads=[bbcol], writes=[bnbcol])
                tt(v3(arg1), barg1, gcb[0:64, :].rearrange("p (n f) -> p n f", f=64), [bgcb, bgccol], gccol[:].unsqueeze(2).to_broadcast([64, 8, 64]), bgccol, ALU.subtract)
                tt(v3(scr), bscr, v3(arg1), [barg1, bc64], NEGU.unsqueeze(1).to_broadcast([64, 8, 64]), bc64, ALU.add)
                P.op("act", lambda E: E.activation(out=DT[:], in_=scr[:], func=AF.Exp), reads=[bscr], writes=[bDT])
                yield
                P.op("dve", lambda E: E.scalar_tensor_tensor(out=scr[:].rearrange("p (n f) -> p n f", f=64), in0=arg1[:].rearrange("p (n f) -> p n f", f=64), scalar=-1.0,
                                                             in1=NEGLS.unsqueeze(1).to_broadcast([64, 8, 64]), op0=ALU.mult, op1=ALU.add), reads=[barg1, bc64], writes=[bscr])
                P.op("act", lambda E: E.activation(out=Ds[:], in_=scr[:], func=AF.Exp), reads=[bscr], writes=[bDs])
                pk, bpk = BK[0]; pq, bpq = BK[1]
                fns = [(lambda E, n=n, pk=pk, kT=kT: E.matmul(pk[0:64, n * 64:(n + 1) * 64], lhsT=kT[:, n * 64:(n + 1) * 64], rhs=kT[:, n * 64:(n + 1) * 64],
                                                              start=True, stop=True)) for n in range(8)]
                P.mm_group(fns, reads=[bkT], writes=[bpk])
                fns = [(lambda E, n=n, pq=pq, kT=kT, qT=qT: E.matmul(pq[0:64, n * 64:(n + 1) * 64], lhsT=kT[:, n * 64:(n + 1) * 64], rhs=qT[:, n * 64:(n + 1) * 64],
                                                                     start=True, stop=True)) for n in range(8)]
                P.mm_group(fns, reads=[bkT, bqT], writes=[bpq])
                yield
                tt(attnT[:], battnT, pq[0:64, :], [bpq, bDT], DT[:], bDT, ALU.mult)
                Pc, bPc = m64["Pa"]; Pn, bPn = m64["Pb"]; Qc, bQc = m64["Qa"]; Qn, bQn = m64["Qb"]
                tt(tmp[:], btmp, pk[0:64, :], [bpk, bDT], DT[:], bDT, ALU.mult)
                tt(tmp[:], btmp, tmp[:], [btmp, bbb], bb[0:64, :], bbb, ALU.mult)
                tt(v3(Qc), bQc, v3(tmp), [btmp, bc64], NSU.unsqueeze(1).to_broadcast([64, 8, 64]), bc64, ALU.mult)
                yield
                tt(tmp[:], btmp, pk[0:64, :], [bpk, bDs], Ds[:], bDs, ALU.mult)
                tt(v3(Pc), bPc, v3(tmp), [btmp, bnbcol], nbcol[:].unsqueeze(2).to_broadcast([64, 8, 64]), bnbcol, ALU.mult)
                tt(v3(Y), bY, v3(Qc), [bQc, bc64], i64b, bc64, ALU.add)
                P.op("act", lambda E, Ybf=Ybf, Y=Y: E.copy(out=Ybf[:], in_=Y[:]), reads=[bY], writes=[bYbf])
                yield
                for j in range(5):
                    pP, bpP = BK[2]; pQ, bpQ = BK[1]
                    fns = [(lambda E, n=n, pP=pP, Qc=Qc, Pc=Pc: E.matmul(pP[0:64, n * 64:(n + 1) * 64], lhsT=Qc[:, n * 64:(n + 1) * 64], rhs=Pc[:, n * 64:(n + 1) * 64],
                                                                         start=True, stop=True)) for n in range(8)]
                    P.mm_group(fns, reads=[bQc, bPc], writes=[bpP])
                    if j < 4:
                        fns = [(lambda E, n=n, pQ=pQ, Qc=Qc, Pc=Pc: E.matmul(pQ[0:64, n * 64:(n + 1) * 64], lhsT=Pc[:, n * 64:(n + 1) * 64], rhs=Qc[:, n * 64:(n + 1) * 64],
                                                                             start=True, stop=True)) for n in range(8)]
                        P.mm_group(fns, reads=[bQc, bPc], writes=[bpQ])
                    yield
                    P.op("act", lambda E, Pn=Pn, pP=pP: E.copy(out=Pn[:], in_=pP[0:64, :]), reads=[bpP], writes=[bPn])
                    if j < 4:
                        P.op("dve", lambda E, Qn=Qn, pQ=pQ: E.tensor_copy(out=Qn[:], in_=pQ[0:64, :]), reads=[bpQ], writes=[bQn])
                    pY, bpY = BK[0]
                    fns = [(lambda E, n=n, pY=pY, Pn=Pn, Ybf=Ybf: E.matmul(pY[0:64, n * 64:(n + 1) * 64], lhsT=Pn[:, n * 64:(n + 1) * 64], rhs=Ybf[:, n * 64:(n + 1) * 64],
                                                                         start=True, stop=True)) for n in range(8)]
                    P.mm_group(fns, reads=[bPn, bYbf], writes=[bpY])
                    yield
                    tt(Y[:], bY, Y[:], [bY, bpY], pY[0:64, :], bpY, ALU.add)
                    P.op("act", lambda E, Ybf=Ybf, Y=Y: E.copy(out=Ybf[:], in_=Y[:]), reads=[bY], writes=[bYbf])
                    Pc, bPc, Pn, bPn = Pn, bPn, Pc, bPc
                    Qc, bQc, Qn, bQn = Qn, bQn, Qc, bQc
                for hf in range(2):
                    pth, bpth = BK[1 + hf]
                    fns = [(lambda E, n=n, vT=vT, pth=pth, hf=hf: E.transpose(out=pth[0:64, n * 128:(n + 1) * 128], in_=vT[:, (4 * hf + n) * 64:(4 * hf + n + 1) * 64],
                                                                              identity=idf[:])) for n in range(4)]
                    P.mm_group(fns, reads=[bvT, bidf], writes=[bpth])
                    tt(bv[:, 4 * hf:4 * hf + 4, :], bbv, pth[0:64, :].rearrange("p (n d) -> p n d", d=128), [bpth, bbcol],
                       bcol[:, 4 * hf:4 * hf + 4].unsqueeze(2).to_broadcast([64, 4, 128]), bbcol, ALU.mult)
                yield
                tt(elast[:], belast, gcb[0:64, :].rearrange("p (n f) -> p n f", f=64)[:, :, 63], [bgcb, bgccol], gccol[:], bgccol, ALU.subtract)
                P.op("act", lambda E: E.activation(out=elast[:], in_=elast[:], func=AF.Exp), reads=[belast], writes=[belast])
                for hf in range(2):
                    pth, bpth = BK[1 + hf]
                    fns = [(lambda E, n=n, kT=kT, pth=pth, hf=hf: E.transpose(out=pth[0:64, n * 128:(n + 1) * 128], in_=kT[:, (4 * hf + n) * 64:(4 * hf + n + 1) * 64],
                                                                              identity=idf[:])) for n in range(4)]
                    P.mm_group(fns, reads=[bkT, bidf], writes=[bpth])
                    tt(kdec[:, 4 * hf:4 * hf + 4, :], bkdec, pth[0:64, :].rearrange("p (n d) -> p n d", d=128), [bpth, belast],
                       elast[:, 4 * hf:4 * hf + 4].unsqueeze(2).to_broadcast([64, 4, 128]), belast, ALU.mult)
                yield
                P.op("act", lambda E, gcb=gcb, EG=EG: E.activation(out=EG[:], in_=gcb[:], func=AF.Exp), reads=[bgcb], writes=[bEG])
                tt(qdec[:], bqdec, qT, [bqT, bEG], EG[:], bEG, ALU.mult)
                kTb, bkTb = H["kTb"]
                P.op("act", lambda E, kTb=kTb, kT=kT: E.copy(out=kTb[:], in_=kT), reads=[bkT], writes=[bkTb])
                P.op("act", lambda E: E.activation(out=egc[:], in_=gccol[:], func=AF.Exp), reads=[bgccol], writes=[begc])
                P.op("dve", lambda E, nbg=nbg: E.scalar_tensor_tensor(out=nbg[:], in0=egc[:], scalar=-1.0, in1=bcol[:], op0=ALU.mult, op1=ALU.mult),
                     reads=[begc, bbcol], writes=[bnbg])
            gensB = [stageB(0), stageB(1)]
            aliveB = True
            while aliveB:
                aliveB = False
                for g_ in gensB:
                    try:
                        next(g_)
                        aliveB = True
                    except StopIteration:
                        pass
            banks = [(GP[0], GP[1]), (GP[2], GP[3])]
            for n in range(8):
                cs = slice(n * 64, (n + 1) * 64)
                for h in range(2):
                    H = heads[h]; S, bS = Sst[h]
                    kT, bkT = H["kTb"]; Sbf, bSbf = H["Sbf"]
                    attnT, battnT = H["attnT"]; Y, bY = H["Ybf"]; EG, bEG = H["EG"]; qdec, bqdec = H["qdec"]
                    bv, bbv = H["bv"]; kdec, bkdec = H["kdec"]; nbg, bnbg = H["nbg"]
                    vnew, bvnew = H["vnew"]; rhs2, brhs2 = H["rhs2"]; osb, bosb = H["osb"]
                    (KSO, bKSO), (Sb, bSb) = banks[h]
                    Vb, bVb = KSO, bKSO
                    P.op("pe", lambda E, cs=cs, kT=kT, Sbf=Sbf, KSO=KSO: E.matmul(KSO[0:64, 0:128], lhsT=kT[:, cs], rhs=Sbf[:], start=True, stop=True),
                         reads=[bkT, bSbf], writes=[bKSO])
                    P.op("dve", lambda E, n=n, KSO=KSO, rhs2=rhs2, nbg=nbg, bv=bv: E.scalar_tensor_tensor(
                        out=rhs2[:], in0=KSO[0:64, 0:128], scalar=nbg[:, n:n + 1], in1=bv[:, n, :], op0=ALU.mult, op1=ALU.add),
                        reads=[bKSO, bnbg, bbv], writes=[brhs2])
                    P.op("pe", lambda E, cs=cs, Y=Y, Vb=Vb, rhs2=rhs2: E.matmul(Vb[0:64, 128:256], lhsT=Y[:, cs], rhs=rhs2[:], start=True, stop=True),
                         reads=[bY, brhs2], writes=[bVb])
                    P.op("act", lambda E, vnew=vnew, Vb=Vb: E.copy(out=vnew[:], in_=Vb[0:64, 128:256]), reads=[bVb], writes=[bvnew])
                    fns = [lambda E, cs=cs, Sbf=Sbf, KSO=KSO, qdec=qdec: E.matmul(KSO[64:128, 0:128], lhsT=qdec[:, cs], rhs=Sbf[:], start=True, stop=False),
                           lambda E, cs=cs, KSO=KSO, attnT=attnT, vnew=vnew: E.matmul(KSO[64:128, 0:128], lhsT=attnT[:, cs], rhs=vnew[:], start=False, stop=True)]
                    P.mm_group(fns, reads=[bqdec, bSbf, battnT, bvnew], writes=[bKSO])
                    P.op("pe", lambda E, n=n, Sb=Sb, kdec=kdec, vnew=vnew: E.matmul(Sb[:, 0:128], lhsT=kdec[:, n, :], rhs=vnew[:], start=True, stop=True),
                         reads=[bkdec, bvnew], writes=[bSb])
                    P.op("dve", lambda E, n=n, S=S, EG=EG, Sb=Sb, Sbf=Sbf: E.scalar_tensor_tensor(out=Sbf[:], in0=S[:], scalar=EG[:, n * 64 + 63:n * 64 + 64], in1=Sb[:, 0:128],
                                                                                                  op0=ALU.mult, op1=ALU.add), reads=[bS, bEG, bSb], writes=[bSbf])
                    P.op("dve", lambda E, n=n, S=S, EG=EG, Sb=Sb: E.scalar_tensor_tensor(out=S[:], in0=S[:], scalar=EG[:, n * 64 + 63:n * 64 + 64], in1=Sb[:, 0:128],
                                                                                         op0=ALU.mult, op1=ALU.add), reads=[bS, bEG, bSb], writes=[bS])
                    P.op("act", lambda E, n=n, osb=osb, KSO=KSO: E.copy(out=osb[64:128, n, :], in_=KSO[64:128, 0:128]), reads=[bKSO], writes=[bosb])
                    advance(1)
                advance(1)
            advance(100)
            for h in range(2):
                osb, bosb = heads[h]["osb"]
                P.dma("sp", o_d[s_ * 512:(s_ + 1) * 512, h * 128:(h + 1) * 128].rearrange("(n c) d -> c n d", c=64), osb[64:128, :, :], reads=[bosb],
                      writes=[fz["obuf_of"](s_) if fz else bo])
            if fz:
                fz["after_chunk"](s_)
        if fz:
            barrier(P)
        else:
            P.finish([bo])
    return nc


def run_L1a(inp):
    nc = _get("L1a", build_L1a)
    c64, cmask, sel = _gdn_consts()
    w_in = inp["w_in_even"][0]
    conv = inp["conv_qkv"][0]
    ones = np.ones((128, 128), np.float32)
    maps = []
    for c in range(8):
        b, r = divmod(c, 4)
        cols = np.concatenate([np.arange(256 * r, 256 * r + 256), 1024 + np.arange(256 * r, 256 * r + 256), 2048 + np.arange(256 * r, 256 * r + 256)])
        maps.append({"x": np.ascontiguousarray(inp["x"][b]), "npre": np.ascontiguousarray(inp["norm_pre"][0]),
                     "w": np.ascontiguousarray(w_in[:, cols]), "wb": np.ascontiguousarray(w_in[:, 4096 + 2 * r:4096 + 2 * r + 2]),
                     "wa": np.ascontiguousarray(w_in[:, 4104 + 2 * r:4104 + 2 * r + 2]), "conv": np.ascontiguousarray(conv[:, cols]),
                     "alog": np.ascontiguousarray(inp["a_log"][0, 2 * r:2 * r + 2]), "dtb": np.ascontiguousarray(inp["dt_bias"][0, 2 * r:2 * r + 2]),
                     "ident": _IDENT, "c64": c64, "cmask": cmask, "sel": sel, "ones": ones})
    res = run_bass_kernel_spmd(nc, maps, core_ids=list(range(8)))
    S_ = inp["x"].shape[1]
    o = np.empty((2, S_, 1024), np.float32)
    for c in range(8):
        b, r = divmod(c, 4)
        o[b, :, 256 * r:256 * (r + 1)] = res.results[c]["o"]
    return o


def kernel_unfused(**inputs):
    inp = {k: np.asarray(v) for k, v in inputs.items()}
    o = run_L1a(inp)
    ys = run_L1b(inp)
    x1 = run_L2(inp, o, ys)
    out = run_L3(inp, x1)
    return out.astype(np.float32)


def build_fused():
    nc = bass.Bass("TRN2", target_bir_lowering=False)
    x_full = nc.dram_tensor("x", [8192, 1024], F32, kind="ExternalInput").ap()
    ident_d = nc.dram_tensor("ident", [128, 128], F32, kind="ExternalInput").ap()
    npre0_d = nc.dram_tensor("npre0", [1024], F32, kind="ExternalInput").ap()
    gidx_d = nc.dram_tensor("gidx", [128, 2, 17, 4], I32, kind="ExternalInput").ap()
    out_d = nc.dram_tensor("out", [2048, 1024], F32, kind="ExternalOutput").ap()
    ag_in = [nc.dram_tensor("ag_in%d" % i, [8192, 256], (F32, BF16)[i]) for i in range(2)]
    ag_out = [nc.dram_tensor("ag_out%d" % i, [4 * 8192, 256], (F32, BF16)[i]) for i in range(2)]
    x1s = nc.dram_tensor("x1s", [2176, 1024], F32)
    GROUPS = [[0, 1, 2, 3], [4, 5, 6, 7]]
    with ExitStack() as st:
        C = Ctx(nc, st); P = C.P
        csem = st.enter_context(nc.semaphore("csem"))
        bag_out = Buf("ag_out"); bx1s = Buf("x1s", multi=True); bout = Buf("out", multi=True)
        bo_ch = [Buf("o_ch%d" % k, multi=True) for k in range(16)]
        by_jt = [Buf("y_jt%d" % k, multi=True) for k in range(4)]
        ncc = [0]

        def emit_cc(which, k, inbuf, rows=512):
            P._deps("pool", [inbuf], [])
            P.streams["pool"].append(lambda E, which=which, k=k, rows=rows: E.collective_compute(
                "AllGather", ALU.bypass, replica_groups=GROUPS,
                ins=[ag_in[which].ap()[k * rows:(k + 1) * rows, :].opt()],
                outs=[ag_out[which].ap()[k * 4 * rows:(k + 1) * 4 * rows, :].opt()]).then_inc(csem))
            ncc[0] += 1

        share1 = {"x": x_full, "ident": ident_d, "npre": npre0_d}

        def after_jt(jt):
            emit_cc(1, jt, by_jt[jt], rows=2048)

        with ExitStack() as stU:
            CU = Ctx(nc, stU, P, "u_")
            uext = CU.sb("uTp", [128, 2, 16, 512], BF16)
            build_L1a(8192, fz={"nc": nc, "P": P, "pfx": "a_", "share": share1, "out": ag_in[0].ap(), "uTp": uext,
                                "obuf_of": lambda s_: bo_ch[s_], "after_chunk": lambda s_: emit_cc(0, s_, bo_ch[s_])})
            build_L1b(8192, fz={"nc": nc, "P": P, "pfx": "b_", "share": share1, "out": ag_in[1].ap(), "uTp": uext,
                                "obuf_of": lambda jt: by_jt[jt], "after_chunk": after_jt, "ybf16": True})
        gidx, bgidx = C.sb("gidx", [128, 2, 17, 4], I32)
        P.dma("sp", gidx[:], gidx_d, writes=[bgidx])
        waited = [False]

        def gather(P_, ld, bld, tile, part):
            if not waited[0]:
                P.streams["pool"].append(lambda E: E.wait_ge(csem, ncc[0]))
                P.op("pool", lambda E: E.nop(), reads=[], writes=[bag_out])
                waited[0] = True
            for i in range(4):
                P_.dma_ind("pool", ld[:, i * 256:(i + 1) * 256], ag_out[part].ap(), gidx[:, part, tile, i:i + 1], reads=[bag_out, bgidx], writes=[bld])

        share2 = {"ident": ident_d, "npre": npre0_d, "o": None, "ys": None}
        build_L2(2176, fz={"nc": nc, "P": P, "pfx": "c_", "share": share2, "out": x1s.ap(), "obuf": bx1s, "gather": gather, "ybf16": True})
        share3 = {"ident": ident_d, "x": x1s.ap()}
        build_L3(2048, fz={"nc": nc, "P": P, "pfx": "d_", "share": share3, "out": out_d, "obuf": bout, "xbuf": bx1s})
        P.finish([bout])
    return nc


def _gidx(r):
    g = np.zeros((128, 2, 17, 4), np.int32)
    p = np.arange(128)[:, None, None]
    tile = np.arange(17)[None, :, None]
    src = np.arange(4)[None, None, :]
    tok = np.clip(2048 * r - 128 + tile * 128 + p, 0, 8191)
    for part, R in ((0, 512), (1, 2048)):
        g[:, part] = ((tok // R) * 4 + src) * R + tok % R
    return g


def kernel(**inputs):
    inp = {k: np.ascontiguousarray(np.asarray(v)) for k, v in inputs.items()}
    nc = _get("fused", build_fused)
    c64, cmask, sel = _gdn_consts()
    mk, idm = _s5_consts()
    ones = np.ones((128, 128), np.float32)
    w_in = inp["w_in_even"][0]
    conv = inp["conv_qkv"][0]
    wz = np.ascontiguousarray(np.concatenate([w_in[:, 3072:4096], w_in[:, 5136:6160]], axis=1))
    maps = []
    for c in range(8):
        b, r = divmod(c, 4)
        cols = np.concatenate([np.arange(256 * r, 256 * r + 256), 1024 + np.arange(256 * r, 256 * r + 256), 2048 + np.arange(256 * r, 256 * r + 256)])
        gs = slice(16 * r, 16 * r + 16)
        xq = np.zeros((2176, 1024), np.float32)
        xq[128:] = inp["x"][b, 2048 * r:2048 * (r + 1)]
        if r > 0:
            xq[:128] = inp["x"][b, 2048 * r - 128:2048 * r]
        m = {"x": inp["x"][b], "ident": _IDENT, "npre0": inp["norm_pre"][0], "gidx": _gidx(r),
             "a_w": np.ascontiguousarray(w_in[:, cols]), "a_wb": np.ascontiguousarray(w_in[:, 4096 + 2 * r:4096 + 2 * r + 2]),
             "a_wa": np.ascontiguousarray(w_in[:, 4104 + 2 * r:4104 + 2 * r + 2]), "a_conv": np.ascontiguousarray(conv[:, cols]),
             "a_alog": np.ascontiguousarray(inp["a_log"][0, 2 * r:2 * r + 2]), "a_dtb": np.ascontiguousarray(inp["dt_bias"][0, 2 * r:2 * r + 2]),
             "a_c64": c64, "a_cmask": cmask, "a_sel": sel, "a_ones": ones,
             "a_wu": np.ascontiguousarray(w_in[:, 4112 + 256 * r:4112 + 256 * (r + 1)]),
             "b_wu": np.ascontiguousarray(w_in[:, 4112 + 256 * r:4112 + 256 * (r + 1)]),
             "b_lre": np.ascontiguousarray(inp["s5_lam_re"][0, gs]), "b_lim": np.ascontiguousarray(inp["s5_lam_im"][0, gs]),
             "b_bre": np.ascontiguousarray(inp["s5_b_re"][0, gs]), "b_bim": np.ascontiguousarray(inp["s5_b_im"][0, gs]),
             "b_cre": np.ascontiguousarray(inp["s5_c_re"][0, gs]), "b_cim": np.ascontiguousarray(inp["s5_c_im"][0, gs]),
             "b_ldt": np.ascontiguousarray(inp["s5_log_dt"][0, gs]), "b_dd": np.ascontiguousarray(inp["s5_d"][0, 256 * r:256 * (r + 1)]),
             "b_taus": TAUS, "b_mk": mk, "b_idm": idm,
             "c_x": xq, "c_wz": wz, "c_wglu": inp["w_glu"][0], "c_wout": inp["w_out_even"][0], "c_npost": inp["norm_post"][0],
             "c_gnw": inp["gdn_norm_w"][0],
             "d_win": inp["w_in_odd"][0], "d_wout": inp["w_out_odd"][0], "d_conv": inp["conv_short"][0],
             "d_npre": inp["norm_pre"][1], "d_npost": inp["norm_post"][1]}
        maps.append(m)
    res = run_bass_kernel_spmd(nc, maps, core_ids=list(range(8)))
    out = np.empty((2, 8192, 1024), np.float32)
    for c in range(8):
        b, r = divmod(c, 4)
        out[b, r * 2048:(r + 1) * 2048] = res.results[c]["out"]
    return out
```

```python
from contextlib import ExitStack
import numpy as np
import concourse.bass as bass
import concourse.mybir as mybir
from concourse.bass_utils import run_bass_kernel_spmd

F32 = mybir.dt.float32
BF16 = mybir.dt.bfloat16
AF = mybir.ActivationFunctionType
ALU = mybir.AluOpType
AX = mybir.AxisListType

NDS = 12


class Buf:
    __slots__ = ("name", "w", "r", "multi")

    def __init__(self, name, multi=False):
        self.name = name
        self.w = [] if multi else None
        self.r = []
        self.multi = multi


class Prog:
    ENG = ("pe", "act", "dve", "pool", "sp")

    def __init__(self, nc, stack):
        self.nc = nc
        self.stack = stack
        self.streams = {e: [] for e in self.ENG}
        self.cnt = {e: 0 for e in self.ENG}
        self.sem = {e: stack.enter_context(nc.semaphore("s_" + e)) for e in self.ENG}
        self.seen = {e: {} for e in self.ENG}
        self.dcnt = {e: 0 for e in self.ENG}
        self.dsem = {}
        for e in ("sp", "pool", "act"):
            self.dsem[e] = [stack.enter_context(nc.semaphore("d_%s%d" % (e, i))) for i in range(NDS)]
        self.same_engine_sync = True
        self.nwaits = 0

    def _wait(self, eng, tok):
        if tok is None:
            return
        kind = tok[0]
        if kind == "c":
            _, e2, n = tok
            if e2 == eng and (eng == "pe" or not self.same_engine_sync):
                return
            key = e2
            if self.seen[eng].get(key, 0) >= n:
                return
            self.seen[eng][key] = n
            sem = self.sem[e2]
            self.streams[eng].append(lambda E, sem=sem, n=n: E.wait_ge(sem, n))
            self.nwaits += 1
        else:
            _, q, slot, val = tok
            key = ("d", q, slot)
            if self.seen[eng].get(key, 0) >= val:
                return
            self.seen[eng][key] = val
            sem = self.dsem[q][slot]
            self.streams[eng].append(lambda E, sem=sem, val=val: E.wait_ge(sem, val))
            self.nwaits += 1

    def _deps(self, eng, reads, writes):
        for b in reads:
            if b.multi:
                for t in b.w:
                    self._wait(eng, t)
            else:
                self._wait(eng, b.w)
        for b in writes:
            if not b.multi:
                self._wait(eng, b.w)
            for t in b.r:
                self._wait(eng, t)

    def _commit(self, tok, reads, writes):
        for b in writes:
            if b.multi:
                b.w.append(tok)
            else:
                b.w = tok
            b.r = []
        for b in reads:
            if b not in writes:
                b.r.append(tok)

    def op(self, eng, fn, reads=(), writes=()):
        reads = list(reads)
        writes = list(writes)
        self._deps(eng, reads, writes)
        self.cnt[eng] += 1
        n = self.cnt[eng]
        sem = self.sem[eng]
        self.streams[eng].append(lambda E, fn=fn, sem=sem: fn(E).then_inc(sem, 1))
        tok = ("c", eng, n)
        self._commit(tok, reads, writes)
        return tok

    def mm_group(self, fns, reads=(), writes=()):
        eng = "pe"
        reads = list(reads)
        writes = list(writes)
        self._deps(eng, reads, writes)
        self.cnt[eng] += 1
        n = self.cnt[eng]
        sem = self.sem[eng]
        for fn in fns[:-1]:
            self.streams[eng].append(lambda E, fn=fn: fn(E))
        last = fns[-1]
        self.streams[eng].append(lambda E, fn=last, sem=sem: fn(E).then_inc(sem, 1))
        tok = ("c", eng, n)
        self._commit(tok, reads, writes)
        return tok

    def dma(self, q, out_ap, in_ap, reads=(), writes=()):
        reads = list(reads)
        writes = list(writes)
        self._deps(q, reads, writes)
        j = self.dcnt[q]
        self.dcnt[q] += 1
        slot = j % NDS
        val = 16 * (j // NDS + 1)
        if j >= NDS:
            self._wait(q, ("d", q, slot, val - 16))
        sem = self.dsem[q][slot]
        self.streams[q].append(
            lambda E, o=out_ap, i=in_ap, sem=sem: E.dma_start(out=o, in_=i).then_inc(sem, 16))
        tok = ("d", q, slot, val)
        self._commit(tok, reads, writes)
        return tok

    def dma_ind(self, q, out_ap, table_ap, idx_ap, reads=(), writes=()):
        reads = list(reads)
        writes = list(writes)
        self._deps(q, reads, writes)
        j = self.dcnt[q]
        self.dcnt[q] += 1
        slot = j % NDS
        val = 16 * (j // NDS + 1)
        if j >= NDS:
            self._wait(q, ("d", q, slot, val - 16))
        sem = self.dsem[q][slot]
        self.streams[q].append(
            lambda E, o=out_ap, t=table_ap, i=idx_ap, sem=sem: E.indirect_dma_start(
                out=o, out_offset=None, in_=t, in_offset=bass.IndirectOffsetOnAxis(ap=i, axis=0)).then_inc(sem, 16))
        tok = ("d", q, slot, val)
        self._commit(tok, reads, writes)
        return tok

    def finish(self, final_bufs):
        for b in final_bufs:
            for t in (b.w if b.multi else [b.w]):
                self._wait("sp", t)
        nc = self.nc
        streams = self.streams
        with nc.Block() as block:
            @block.tensor
            def _(E):
                for f in streams["pe"]:
                    f(E)

            @block.scalar
            def _(E):
                for f in streams["act"]:
                    f(E)

            @block.vector
            def _(E):
                for f in streams["dve"]:
                    f(E)

            @block.gpsimd
            def _(E):
                for f in streams["pool"]:
                    f(E)

            @block.sync
            def _(E):
                for f in streams["sp"]:
                    f(E)


class Ctx:
    def __init__(self, nc, st, P=None, pfx=""):
        self.nc = nc
        self.st = st
        self.pfx = pfx
        if P is None:
            st.enter_context(nc.allow_non_contiguous_dma(reason="small parameter loads / layout transforms"))
            P = Prog(nc, st)
        self.P = P

    def sb(self, name, shape, dt=F32):
        t = self.st.enter_context(self.nc.sbuf_tensor("sb_" + self.pfx + name, shape, dt))
        return t, Buf(name)

    def ps(self, name, shape, dt=F32):
        t = self.st.enter_context(self.nc.psum_tensor("ps_" + self.pfx + name, shape, dt))
        return t, Buf(name)


def bcast_row_load(C, name, dram_vec, n, q="sp"):
    t, b = C.sb(name, [128, n])
    C.P.dma(q, t[:], dram_vec.partition_broadcast(128), writes=[b])
    return t, b


def make_ident(C, dram_ident):
    idf, bidf = C.sb("identf", [128, 128])
    C.P.dma("sp", idf[:], dram_ident, writes=[bidf])
    idb, bidb = C.sb("identb", [128, 128], BF16)
    C.P.op("dve", lambda E: E.tensor_copy(out=idb[:], in_=idf[:]), reads=[bidf], writes=[bidb])
    return idf, bidf, idb, bidb


def rms_rstd(C, src, bsrc, ncols, junk, bjunk, ss, bss, eps=1e-6):
    P = C.P
    P.op("act", lambda E: E.activation(out=junk, in_=src, func=AF.Square, accum_out=ss[:, 0:1]),
         reads=[bsrc], writes=[bjunk, bss])
    P.op("act", lambda E: E.activation(out=ss[:, 0:1], in_=ss[:, 0:1], func=AF.Sqrt, bias=float(eps), scale=float(1.0 / ncols)),
         reads=[bss], writes=[bss])
    P.op("dve", lambda E: E.reciprocal(out=ss[:, 0:1], in_=ss[:, 0:1]), reads=[bss], writes=[bss])


def transpose8(C, src_bf, bsrc, idb, bidb, ptr, bptr, dst3, bdst, eng="act"):
    P = C.P
    fns = [(lambda E, kt=kt: E.transpose(out=ptr[:, kt * 128:(kt + 1) * 128], in_=src_bf[:, kt * 128:(kt + 1) * 128],
                                         identity=idb[:])) for kt in range(8)]
    P.mm_group(fns, reads=[bsrc, bidb], writes=[bptr])
    src3 = ptr[:].rearrange("p (k t) -> p k t", k=8)
    if eng == "act":
        P.op("act", lambda E: E.copy(out=dst3, in_=src3), reads=[bptr], writes=[bdst])
    else:
        P.op("dve", lambda E: E.tensor_copy(out=dst3, in_=src3), reads=[bptr], writes=[bdst])


def outproj_post(C, catT, bcat, nkt, wout, bwout, t, xres, bxres, npw, bnpw, pso, bpso, yo, byo, junk, bjunk, ss, bss,
                 out_dram_rows, bout):
    P = C.P
    for hh in range(2):
        fns = [(lambda E, kt=kt, hh=hh: E.matmul(pso[hh][:], lhsT=catT[:, kt, t * 128:(t + 1) * 128],
                                                 rhs=wout[:, kt, hh * 512:(hh + 1) * 512],
                                                 start=(kt == 0), stop=(kt == nkt - 1))) for kt in range(nkt)]
        P.mm_group(fns, reads=[bcat, bwout], writes=[bpso[hh]])
        P.op("act", lambda E, hh=hh: E.copy(out=yo[:, hh * 512:(hh + 1) * 512], in_=pso[hh][:]),
             reads=[bpso[hh]], writes=[byo])
    rms_rstd(C, yo[:], byo, 1024, junk[:], bjunk, ss, bss)
    P.op("dve", lambda E: E.scalar_tensor_tensor(out=yo[:], in0=yo[:], scalar=ss[:, 0:1], in1=npw[:],
                                                 op0=ALU.mult, op1=ALU.mult), reads=[byo, bss, bnpw], writes=[byo])
    P.op("dve", lambda E: E.tensor_tensor(out=yo[:], in0=yo[:], in1=xres, op=ALU.add), reads=[byo, bxres], writes=[byo])
    P.dma("sp", out_dram_rows, yo[:], reads=[byo], writes=[bout])


def load_w_bf16(C, name, dram_w, kt_n, ncols, chunk=2048, groups=None):
    w, _ = C.sb(name, [128, kt_n, ncols], BF16)
    src = dram_w.rearrange("(k p) c -> p k c", p=128)
    if groups is None:
        bw = Buf(name, multi=True)
        for kt in range(kt_n):
            for c0 in range(0, ncols, chunk):
                c1 = min(ncols, c0 + chunk)
                C.P.dma("pool", w[:, kt, c0:c1], src[:, kt, c0:c1], writes=[bw])
        return w, bw
    bws = []
    for gi, sls in enumerate(groups):
        bg = Buf("%s_g%d" % (name, gi), multi=True)
        for (c0, c1) in sls:
            for kt in range(kt_n):
                C.P.dma("pool", w[:, kt, c0:c1], src[:, kt, c0:c1], writes=[bg])
        bws.append(bg)
    return w, bws


def build_L2(ntok=2048, fz=None):
    nc = fz["nc"] if fz else bass.Bass("TRN2", target_bir_lowering=False)
    pfx = fz["pfx"] if fz else ""

    def D(name, shape):
        if fz and name in fz["share"]:
            return fz["share"][name]
        return nc.dram_tensor(pfx + name, shape, F32, kind="ExternalInput").ap()
    x_d = D("x", [ntok, 1024]); o_d = D("o", [ntok, 1024]); ys_d = D("ys", [ntok, 1024])
    wz_d = D("wz", [1024, 2048]); wglu_d = D("wglu", [1024, 1024]); wout_d = D("wout", [2048, 1024])
    npre_d = D("npre", [1024]); npost_d = D("npost", [1024]); gnw_d = D("gnw", [128]); ident_d = D("ident", [128, 128])
    out_d = fz["out"] if fz else nc.dram_tensor("out", [ntok, 1024], F32, kind="ExternalOutput").ap()
    NT = 512
    with ExitStack() as st:
        C = Ctx(nc, st, fz["P"], pfx) if fz else Ctx(nc, st); P = C.P
        idf, bidf, idb, bidb = make_ident(C, ident_d)
        npre, bnpre = bcast_row_load(C, "npre", npre_d, 1024)
        npost, bnpost = bcast_row_load(C, "npost", npost_d, 1024)
        gnw, bgnw = bcast_row_load(C, "gnw", gnw_d, 128)
        wz, bwz = load_w_bf16(C, "wz", wz_d, 8, 2048)
        wglu, bwglu = load_w_bf16(C, "wglu", wglu_d, 8, 1024)
        wout, bwout = load_w_bf16(C, "wout", wout_d, 16, 1024)
        xt4, bxt4 = C.sb("xt4", [128, 4, 1024]); bxt = [Buf("xt%d" % i) for i in range(4)]
        ldo = [C.sb("ldo%d" % i, [128, 1024]) for i in range(2)]
        ldy = [C.sb("ldy%d" % i, [128, 1024], BF16 if (fz and fz.get("ybf16")) else F32) for i in range(2)]
        for (_t, _b) in ldo + ldy:
            _b.multi = True; _b.w = []
        sq, bsq = C.sb("sq", [128, 1024])
        hn, bhn = C.sb("hn", [128, 1024], BF16)
        ss, bss = C.sb("ss", [128, 1])
        ss8, bss8 = C.sb("ss8", [128, 8])
        hT, bhT = C.sb("hT", [128, 8, NT], BF16)
        oT, boT = C.sb("oT", [128, 8, NT], BF16)
        yT, byT = C.sb("yT", [128, 8, NT], BF16)
        gz, bgz = C.sb("gz", [128, 8, NT], BF16)
        sg, bsg = C.sb("sg", [128, NT], BF16)
        catT, bcat = C.sb("catT", [128, 16, NT], BF16)
        yo, byo = C.sb("yo", [128, 1024])
        ptr, bptr = C.ps("ptr", [128, 1024], BF16)
        pmm = []; bpmm = []
        for i in range(4):
            t_, b_ = C.ps("pmm%d" % i, [128, 512]); pmm.append(t_); bpmm.append(b_)
        pso = []; bpso = []
        for i in range(2):
            t_, b_ = C.ps("pso%d" % i, [128, 512]); pso.append(t_); bpso.append(b_)
        bout = fz["obuf"] if fz else Buf("out", multi=True)
        if fz:
            sts = [(0, 128)] + [(128 + i * NT, NT) for i in range((ntok - 128) // NT)]
        else:
            sts = [(i * NT, NT) for i in range(ntok // NT)]
        tile_r0 = [t0_ + t_ * 128 for (t0_, n_) in sts for t_ in range(n_ // 128)]

        def issue_loads(ti):
            r0_ = tile_r0[ti]
            lo, blo = ldo[ti % 2]; ly, bly = ldy[ti % 2]
            if fz:
                fz["gather"](P, lo, blo, r0_ // 128, 0)
                fz["gather"](P, ly, bly, r0_ // 128, 1)
            else:
                P.dma("sp", lo[:], o_d[r0_:r0_ + 128, :], writes=[blo])
                P.dma("sp", ly[:], ys_d[r0_:r0_ + 128, :], writes=[bly])

        issue_loads(0)
        for (t0, n) in sts:
            ntl = n // 128
            for t in range(ntl):
                r0 = t0 + t * 128
                ti = tile_r0.index(r0)
                if ti + 1 < len(tile_r0):
                    issue_loads(ti + 1)
                P.dma("sp", xt4[:, t, :], x_d[r0:r0 + 128, :], writes=[bxt[t]])
                rms_rstd(C, xt4[:, t, :], bxt[t], 1024, sq[:], bsq, ss, bss)
                P.op("dve", lambda E, t=t: E.scalar_tensor_tensor(out=hn[:], in0=xt4[:, t, :], scalar=ss[:, 0:1], in1=npre[:],
                                                                  op0=ALU.mult, op1=ALU.mult), reads=[bxt[t], bss, bnpre], writes=[bhn])
                transpose8(C, hn, bhn, idb, bidb, ptr, bptr, hT[:, :, t * 128:(t + 1) * 128], bhT, eng="act")
                ld, bld = ldo[ti % 2]
                P.op("act", lambda E, ld=ld: E.activation(out=sq[:], in_=ld[:], func=AF.Square), reads=[bld], writes=[bsq])
                P.op("dve", lambda E: E.tensor_reduce(out=ss8[:], in_=sq[:].rearrange("p (h d) -> p h d", h=8), axis=AX.X, op=ALU.add),
                     reads=[bsq], writes=[bss8])
                P.op("dve", lambda E: E.tensor_scalar(out=ss8[:], in0=ss8[:], scalar1=1.0 / 128, scalar2=1e-6, op0=ALU.mult, op1=ALU.add),
                     reads=[bss8], writes=[bss8])
                P.op("act", lambda E: E.activation(out=ss8[:], in_=ss8[:], func=AF.Sqrt), reads=[bss8], writes=[bss8])
                P.op("dve", lambda E: E.reciprocal(out=ss8[:], in_=ss8[:]), reads=[bss8], writes=[bss8])
                P.op("dve", lambda E, ld=ld: E.tensor_tensor(out=sq[:].rearrange("p (h d) -> p h d", h=8), in0=ld[:].rearrange("p (h d) -> p h d", h=8),
                                                      in1=ss8[:].unsqueeze(2).to_broadcast([128, 8, 128]), op=ALU.mult),
                     reads=[bld, bss8], writes=[bsq])
                P.op("dve", lambda E: E.tensor_tensor(out=hn[:].rearrange("p (h d) -> p h d", h=8), in0=sq[:].rearrange("p (h d) -> p h d", h=8),
                                                      in1=gnw[:].unsqueeze(1).to_broadcast([128, 8, 128]), op=ALU.mult),
                     reads=[bsq, bgnw], writes=[bhn])
                transpose8(C, hn, bhn, idb, bidb, ptr, bptr, oT[:, :, t * 128:(t + 1) * 128], boT, eng="act")
                ld, bld = ldy[ti % 2]
                P.op("act", lambda E, ld=ld: E.activation(out=hn[:], in_=ld[:], func=AF.Gelu_apprx_tanh), reads=[bld], writes=[bhn])
                transpose8(C, hn, bhn, idb, bidb, ptr, bptr, yT[:, :, t * 128:(t + 1) * 128], byT, eng="dve")
            for ct in range(16):
                pb = pmm[ct % 4]; bpb = bpmm[ct % 4]
                fns = [(lambda E, kt=kt, ct=ct, pb=pb, n=n: E.matmul(pb[:, 0:n], lhsT=wz[:, kt, ct * 128:(ct + 1) * 128], rhs=hT[:, kt, 0:n],
                                                                start=(kt == 0), stop=(kt == 7))) for kt in range(8)]
                P.mm_group(fns, reads=[bwz, bhT], writes=[bpb])
                if ct < 8:
                    P.op("act", lambda E, pb=pb, n=n: E.activation(out=sg[:, 0:n], in_=pb[:, 0:n], func=AF.Silu), reads=[bpb], writes=[bsg])
                    P.op("dve", lambda E, ct=ct, n=n: E.tensor_tensor(out=catT[:, ct, 0:n], in0=oT[:, ct, 0:n], in1=sg[:, 0:n], op=ALU.mult),
                         reads=[boT, bsg], writes=[bcat])
                else:
                    P.op("act", lambda E, pb=pb, ct=ct, n=n: E.activation(out=gz[:, ct - 8, 0:n], in_=pb[:, 0:n], func=AF.Silu), reads=[bpb], writes=[bgz])
            for ct in range(8):
                pb = pmm[ct % 4]; bpb = bpmm[ct % 4]
                fns = [(lambda E, kt=kt, ct=ct, pb=pb, n=n: E.matmul(pb[:, 0:n], lhsT=wglu[:, kt, ct * 128:(ct + 1) * 128], rhs=yT[:, kt, 0:n],
                                                                start=(kt == 0), stop=(kt == 7))) for kt in range(8)]
                P.mm_group(fns, reads=[bwglu, byT], writes=[bpb])
                P.op("act", lambda E, pb=pb, n=n: E.activation(out=sg[:, 0:n], in_=pb[:, 0:n], func=AF.Sigmoid), reads=[bpb], writes=[bsg])
                P.op("dve", lambda E, ct=ct, n=n: E.tensor_tensor(out=sg[:, 0:n], in0=sg[:, 0:n], in1=yT[:, ct, 0:n], op=ALU.mult), reads=[bsg, byT], writes=[bsg])
                P.op("dve", lambda E, ct=ct, n=n: E.tensor_tensor(out=catT[:, 8 + ct, 0:n], in0=sg[:, 0:n], in1=gz[:, ct, 0:n], op=ALU.mult),
                     reads=[bsg, bgz], writes=[bcat])
            for t in range(ntl):
                r0 = t0 + t * 128
                outproj_post(C, catT, bcat, 16, wout, bwout, t, xt4[:, t, :], bxt[t], npost, bnpost, pso, bpso, yo, byo, sq, bsq, ss, bss,
                             out_d[r0:r0 + 128, :], bout)
        if fz:
            barrier(P)
        else:
            P.finish([bout])
    return nc


def build_L3(ntok=2048, fz=None):
    nc = fz["nc"] if fz else bass.Bass("TRN2", target_bir_lowering=False)
    pfx = fz["pfx"] if fz else ""

    def D(name, shape):
        if fz and name in fz["share"]:
            return fz["share"][name]
        return nc.dram_tensor(pfx + name, shape, F32, kind="ExternalInput").ap()
    x_d = D("x", [ntok + 128, 1024])
    win_d = D("win", [1024, 8192]); wout_d = D("wout", [2048, 1024]); conv_d = D("conv", [3, 2048])
    npre_d = D("npre", [1024]); npost_d = D("npost", [1024]); ident_d = D("ident", [128, 128])
    out_d = fz["out"] if fz else nc.dram_tensor("out", [ntok, 1024], F32, kind="ExternalOutput").ap()
    NT = 512
    HN = 256
    with ExitStack() as st:
        C = Ctx(nc, st, fz["P"], pfx) if fz else Ctx(nc, st); P = C.P
        idf, bidf, idb, bidb = make_ident(C, ident_d)
        npre, bnpre = bcast_row_load(C, "npre", npre_d, 1024)
        npost, bnpost = bcast_row_load(C, "npost", npost_d, 1024)
        cw, bcw = C.sb("cw", [128, 3, 16])
        P.dma("sp", cw[:], conv_d.rearrange("j (c p) -> p j c", p=128), writes=[bcw])
        win, bwin_g = load_w_bf16(C, "win", win_d, 8, 8192,
                                  groups=[[(part * 2048 + cg * 512, part * 2048 + cg * 512 + 512) for part in (1, 2)] for cg in range(4)] +
                                         [[(part * 2048 + cg * 512, part * 2048 + cg * 512 + 512) for part in (0, 3)] for cg in range(4)])
        wout, bwout = load_w_bf16(C, "wout", wout_d, 16, 1024)
        xt, bxt = C.sb("xt", [128, 1024])
        hn, bhn = C.sb("hn", [128, 1024], BF16)
        ss, bss = C.sb("ss", [128, 1])
        hT, bhT = C.sb("hT", [128, 8, NT], BF16)
        y1T, by1T = C.sb("y1T", [128, 16, NT], BF16)
        pbuf, bpbuf = C.sb("pbuf", [128, HN + 2])
        phalo, bphalo = C.sb("phalo", [128, 16, 2])
        gcs, bgcs = C.sb("gcs", [128, HN])
        cv, bcv = C.sb("cv", [128, HN])
        sz, bsz = C.sb("sz", [128, HN])
        yo, byo = C.sb("yo", [128, 1024])
        P.op("dve", lambda E: E.memset(phalo[:], 0.0), writes=[bphalo])
        ptr, bptr = C.ps("ptr", [128, 1024], BF16)
        GB = [C.ps("g%d" % i, [128, 512]) for i in range(7)]
        pso = [GB[0][0], GB[1][0]]; bpso = [GB[0][1], GB[1][1]]
        bout = fz["obuf"] if fz else Buf("out", multi=True)
        sts = [(0, 128)] + [(128 + i * NT, NT) for i in range(ntok // NT)]
        for (t0, n) in sts:
            ntl = n // 128
            for t in range(ntl):
                r0 = t0 + t * 128
                P.dma("sp", xt[:], x_d[r0:r0 + 128, :], reads=([fz["xbuf"]] if fz else []), writes=[bxt])
                rms_rstd(C, xt[:], bxt, 1024, hn[:], bhn, ss, bss)
                P.op("dve", lambda E: E.scalar_tensor_tensor(out=hn[:], in0=xt[:], scalar=ss[:, 0:1], in1=npre[:],
                                                             op0=ALU.mult, op1=ALU.mult), reads=[bxt, bss, bnpre], writes=[bhn])
                transpose8(C, hn, bhn, idb, bidb, ptr, bptr, hT[:, :, t * 128:(t + 1) * 128], bhT, eng="act")
            for ct in range(16):
                sel_ = [GB[3 * (ct % 2) + 0], GB[3 * (ct % 2) + 1], GB[3 * (ct % 2) + 2], GB[6]]
                pmm = [x_[0] for x_ in sel_]; bpmm = [x_[1] for x_ in sel_]
                for part in ((1, 2) if t0 == 0 else range(4)):
                    col0 = (part * 16 + ct) * 128
                    pb = pmm[part]
                    fns = [(lambda E, n=n, kt=kt, col0=col0, pb=pb: E.matmul(pb[:, 0:n], lhsT=win[:, kt, col0:col0 + 128], rhs=hT[:, kt, 0:n],
                                                                        start=(kt == 0), stop=(kt == 7))) for kt in range(8)]
                    P.mm_group(fns, reads=[bwin_g[(0 if part in (1, 2) else 4) + ct // 4], bhT], writes=[bpmm[part]])
                for h0 in range(0, n, HN):
                    nn = min(HN, n - h0)
                    P.op("act", lambda E, nn=nn, h0=h0, pmm=pmm: E.copy(out=gcs[:, 0:nn], in_=pmm[1][:, h0:h0 + nn]), reads=[bpmm[1]], writes=[bgcs])
                    P.op("act", lambda E, ct=ct: E.copy(out=pbuf[:, 0:2], in_=phalo[:, ct, :]), reads=[bphalo], writes=[bpbuf])
                    P.op("dve", lambda E, nn=nn, h0=h0, pmm=pmm: E.tensor_tensor(out=pbuf[:, 2:2 + nn], in0=gcs[:, 0:nn], in1=pmm[2][:, h0:h0 + nn], op=ALU.mult),
                         reads=[bgcs, bpmm[2]], writes=[bpbuf])
                    P.op("act", lambda E, nn=nn, ct=ct: E.copy(out=phalo[:, ct, :], in_=pbuf[:, nn:nn + 2]), reads=[bpbuf], writes=[bphalo])
                    if t0 == 0:
                        continue
                    P.op("dve", lambda E, nn=nn, ct=ct: E.tensor_scalar(out=cv[:, 0:nn], in0=pbuf[:, 0:nn], scalar1=cw[:, 0, ct:ct + 1], scalar2=None, op0=ALU.mult),
                         reads=[bpbuf, bcw], writes=[bcv])
                    P.op("dve", lambda E, nn=nn, ct=ct: E.scalar_tensor_tensor(out=cv[:, 0:nn], in0=pbuf[:, 1:1 + nn], scalar=cw[:, 1, ct:ct + 1], in1=cv[:, 0:nn],
                                                                               op0=ALU.mult, op1=ALU.add), reads=[bpbuf, bcw, bcv], writes=[bcv])
                    P.op("dve", lambda E, nn=nn, ct=ct: E.scalar_tensor_tensor(out=cv[:, 0:nn], in0=pbuf[:, 2:2 + nn], scalar=cw[:, 2, ct:ct + 1], in1=cv[:, 0:nn],
                                                                               op0=ALU.mult, op1=ALU.add), reads=[bpbuf, bcw, bcv], writes=[bcv])
                    P.op("dve", lambda E, nn=nn, h0=h0, pmm=pmm: E.tensor_tensor(out=cv[:, 0:nn], in0=cv[:, 0:nn], in1=pmm[0][:, h0:h0 + nn], op=ALU.mult),
                         reads=[bcv, bpmm[0]], writes=[bcv])
                    P.op("act", lambda E, nn=nn, h0=h0, pmm=pmm: E.activation(out=sz[:, 0:nn], in_=pmm[3][:, h0:h0 + nn], func=AF.Silu), reads=[bpmm[3]], writes=[bsz])
                    P.op("dve", lambda E, nn=nn, h0=h0, ct=ct: E.tensor_tensor(out=y1T[:, ct, h0:h0 + nn], in0=cv[:, 0:nn], in1=sz[:, 0:nn], op=ALU.mult),
                         reads=[bcv, bsz], writes=[by1T])
            if t0 == 0:
                continue
            for t in range(ntl):
                r0 = t0 + t * 128
                P.dma("sp", xt[:], x_d[r0:r0 + 128, :], reads=([fz["xbuf"]] if fz else []), writes=[bxt])
                outproj_post(C, y1T, by1T, 16, wout, bwout, t, xt[:], bxt, npost, bnpost, pso, bpso, yo, byo, hn, bhn, ss, bss,
                             out_d[r0 - 128:r0, :], bout)
        if fz:
            barrier(P)
        else:
            P.finish([bout])
    return nc


_IDENT = np.eye(128, dtype=np.float32)
_CACHE = {}


def _get(name, fn):
    if name not in _CACHE:
        _CACHE[name] = fn()
    return _CACHE[name]


def run_L2(inp, o_full, ys_full):
    nc = _get("L2", build_L2)
    w_in = inp["w_in_even"][0]
    wz = np.ascontiguousarray(np.concatenate([w_in[:, 3072:4096], w_in[:, 5136:6160]], axis=1))
    maps = []
    for c in range(8):
        b, r = divmod(c, 4)
        sl = slice(r * 2048, (r + 1) * 2048)
        maps.append({"x": np.ascontiguousarray(inp["x"][b, sl]), "o": np.ascontiguousarray(o_full[b, sl]),
                     "ys": np.ascontiguousarray(ys_full[b, sl]), "wz": wz, "wglu": np.ascontiguousarray(inp["w_glu"][0]),
                     "wout": np.ascontiguousarray(inp["w_out_even"][0]), "npre": np.ascontiguousarray(inp["norm_pre"][0]),
                     "npost": np.ascontiguousarray(inp["norm_post"][0]), "gnw": np.ascontiguousarray(inp["gdn_norm_w"][0]),
                     "ident": _IDENT})
    res = run_bass_kernel_spmd(nc, maps, core_ids=list(range(8)))
    x1 = np.empty((2, 8192, 1024), np.float32)
    for c in range(8):
        b, r = divmod(c, 4)
        x1[b, r * 2048:(r + 1) * 2048] = res.results[c]["out"]
    return x1


def run_L3(inp, x1):
    nc = _get("L3", build_L3)
    maps = []
    for c in range(8):
        b, r = divmod(c, 4)
        xh = np.zeros((2048 + 128, 1024), np.float32)
        xh[128:] = x1[b, r * 2048:(r + 1) * 2048]
        if r > 0:
            xh[:128] = x1[b, r * 2048 - 128:r * 2048]
        maps.append({"x": xh, "win": np.ascontiguousarray(inp["w_in_odd"][0]), "wout": np.ascontiguousarray(inp["w_out_odd"][0]),
                     "conv": np.ascontiguousarray(inp["conv_short"][0]), "npre": np.ascontiguousarray(inp["norm_pre"][1]),
                     "npost": np.ascontiguousarray(inp["norm_post"][1]), "ident": _IDENT})
    res = run_bass_kernel_spmd(nc, maps, core_ids=list(range(8)))
    out = np.empty((2, 8192, 1024), np.float32)
    for c in range(8):
        b, r = divmod(c, 4)
        out[b, r * 2048:(r + 1) * 2048] = res.results[c]["out"]
    return out


I32 = mybir.dt.int32
TAUS = np.array(list(range(17)) + [32, 64, 128, 256, 512, 1024, 2048, 4096] + list(range(15, -1, -1)), np.float32)
NTAU = len(TAUS)


def _s5_consts():
    mk = np.zeros((128, 2, 16, 16), np.float32)
    idm = np.zeros((128, 2, 16, 16), np.float32)
    for kt2 in range(2):
        for sp in range(8):
            s = kt2 * 8 + sp
            for h in range(16):
                mk[sp * 16 + h, kt2, s:, :] = 1.0
                idm[sp * 16 + h, kt2, s, h] = 1.0
    return mk.reshape(128, 2, 256), idm.reshape(128, 2, 256)


def barrier(P):
    for e in P.ENG:
        for e2 in P.ENG:
            if P.cnt[e2] > 0:
                P._wait(e, ("c", e2, P.cnt[e2]))
        for q in P.dsem:
            j1 = P.dcnt[q]
            for j in range(max(0, j1 - NDS), j1):
                P._wait(e, ("d", q, j % NDS, 16 * (j // NDS + 1)))


def build_L1b(S=8192, fz=None):
    nc = fz["nc"] if fz else bass.Bass("TRN2", target_bir_lowering=False)
    pfx = fz["pfx"] if fz else ""

    def D(name, shape):
        if fz and name in fz["share"]:
            return fz["share"][name]
        return nc.dram_tensor(pfx + name, shape, F32, kind="ExternalInput").ap()
    x_d = D("x", [S, 1024]); npre_d = D("npre", [1024]); wu_d = D("wu", [1024, 256])
    lre_d = D("lre", [16, 64]); lim_d = D("lim", [16, 64]); bre_d = D("bre", [16, 64, 16]); bim_d = D("bim", [16, 64, 16])
    cre_d = D("cre", [16, 16, 64]); cim_d = D("cim", [16, 16, 64]); ldt_d = D("ldt", [16]); dd_d = D("dd", [256])
    taus_d = D("taus", [NTAU]); mk_d = D("mk", [128, 2, 256]); idm_d = D("idm", [128, 2, 256]); ident_d = D("ident", [128, 128])
    ys_d = fz["out"] if fz else nc.dram_tensor("ys", [S, 256], F32, kind="ExternalOutput").ap()
    NCH = S // 16
    NST = S // 512
    with ExitStack() as st:
        C = Ctx(nc, st, fz["P"], pfx) if fz else Ctx(nc, st); P = C.P
        idf, bidf, idb, bidb = make_ident(C, ident_d)
        ptr, bptr = C.ps("ptr", [128, 1024], BF16)
        py, bpy = C.ps("py", [128, 1024])
        G = []; bG = []
        for i in range(4):
            t_, b_ = C.ps("g%d" % i, [128, 512]); G.append(t_); bG.append(b_)
        U, bU = C.sb("U", [128, 2, 16, NCH], BF16)
        with ExitStack() as st2:
            C2 = Ctx(nc, st2, P, C.pfx)
            ext = fz.get("uTp") if fz else None
            if ext:
                uTp, buTp = ext
            else:
                uTp, buTp = C2.sb("uTp", [128, 2, 16, NCH], BF16)
            with ExitStack() as st1:
                C1 = Ctx(nc, st1, P, C.pfx)
                npre, bnpre = bcast_row_load(C1, "npre", npre_d, 1024)
                wu, bwu = load_w_bf16(C1, "wu", wu_d, 8, 256)
                xt, bxt = C1.sb("xt", [128, 1024])
                sq, bsq = C1.sb("sq", [128, 1024])
                hn, bhn = C1.sb("hn", [128, 1024], BF16)
                ss, bss = C1.sb("ss", [128, 1])
                hT, bhT = C1.sb("hT", [128, 8, 512], BF16)
                for s_ in range(0 if ext else NST):
                    for t in range(4):
                        r0 = s_ * 512 + t * 128
                        P.dma("sp", xt[:], x_d[r0:r0 + 128, :], writes=[bxt])
                        rms_rstd(C1, xt[:], bxt, 1024, sq[:], bsq, ss, bss)
                        P.op("dve", lambda E: E.scalar_tensor_tensor(out=hn[:], in0=xt[:], scalar=ss[:, 0:1], in1=npre[:],
                                                                     op0=ALU.mult, op1=ALU.mult), reads=[bxt, bss, bnpre], writes=[bhn])
                        transpose8(C1, hn, bhn, idb, bidb, ptr, bptr, hT[:, :, t * 128:(t + 1) * 128], bhT, eng="act")
                    for blk in range(2):
                        pb = G[blk]
                        fns = [(lambda E, kt=kt, blk=blk, pb=pb: E.matmul(
                            pb[:].rearrange("p (s n) -> p s n", s=16), lhsT=wu[:, kt, blk * 128:(blk + 1) * 128],
                            rhs=hT[:, kt, :].rearrange("p (n s) -> p s n", s=16), start=(kt == 0), stop=(kt == 7))) for kt in range(8)]
                        P.mm_group(fns, reads=[bwu, bhT], writes=[bG[blk]])
                        P.op("act" if blk == 0 else "dve",
                             (lambda E, blk=blk, pb=pb, s_=s_: E.copy(out=uTp[:, blk, :, 32 * s_:32 * s_ + 32], in_=pb[:].rearrange("p (s n) -> p s n", s=16)))
                             if blk == 0 else
                             (lambda E, blk=blk, pb=pb, s_=s_: E.tensor_copy(out=uTp[:, blk, :, 32 * s_:32 * s_ + 32], in_=pb[:].rearrange("p (s n) -> p s n", s=16))),
                             reads=[bG[blk]], writes=[buTp])
                barrier(P)
            ud2 = nc.dram_tensor(pfx + "ud2", [16, 2, 8, 16, NCH], BF16)
            bud2 = Buf("ud2", multi=True)
            bU.multi = True; bU.w = []
            for g in range(16):
                P.dma("sp", ud2.ap()[g].rearrange("k sp h n -> h (k sp) n"),
                      uTp[(g % 8) * 16:(g % 8 + 1) * 16, g // 8, :, :], reads=[buTp], writes=[bud2])
            for g in range(16):
                P.dma("sp", U[:, :, g, :], ud2.ap()[g].rearrange("k sp h n -> (sp h) k n"), reads=[bud2], writes=[bU])
            barrier(P)
        lre, blre = C.sb("lre", [128, 8]); lim, blim = C.sb("lim", [128, 8]); ldt, bldt = C.sb("ldt", [128, 8])
        TAU, bTAU = bcast_row_load(C, "TAU", taus_d, NTAU)
        Er, bEr = C.sb("Er", [128, 8, NTAU]); Ei, bEi = C.sb("Ei", [128, 8, NTAU]); NEi, bNEi = C.sb("NEi", [128, 8, NTAU])
        Hr, bHr = C.sb("Hr", [128, 8, 17, 16]); nHi, bnHi = C.sb("nHi", [128, 8, 17, 16])
        WbT, bWbT = C.sb("WbT", [128, 2, 8, 2, 128], BF16)
        Toep, bToep = C.sb("Toep", [128, 2, 16, 256], BF16)
        with ExitStack() as st3:
            C3 = Ctx(nc, st3, P, C.pfx)
            Br, bBr = C3.sb("Br", [128, 8, 16]); Bi, bBi = C3.sb("Bi", [128, 8, 16])
            Cr, bCr = C3.sb("Cr", [128, 8, 16]); Ci, bCi = C3.sb("Ci", [128, 8, 16])
            dcol, bdcol = C3.sb("dcol", [128, 16])
            MK, bMK = C3.sb("MK", [128, 2, 256]); IDM, bIDM = C3.sb("IDM", [128, 2, 256])
            P.dma("sp", MK[:], mk_d, writes=[bMK]); P.dma("sp", IDM[:], idm_d, writes=[bIDM])
            for _b in (blre, blim, bldt, bBr, bBi, bCr, bCi, bdcol):
                _b.multi = True; _b.w = []
            for two in range(2):
                hs = slice(64 * two, 64 * two + 64)
                P.dma("sp", lre[hs, :], lre_d.rearrange("(gp two) p -> two p gp", two=2)[two], writes=[blre])
                P.dma("sp", lim[hs, :], lim_d.rearrange("(gp two) p -> two p gp", two=2)[two], writes=[blim])
                P.dma("sp", ldt[hs, :], ldt_d.rearrange("(gp two) -> two gp", two=2)[two].partition_broadcast(64), writes=[bldt])
                P.dma("sp", Br[hs], bre_d.rearrange("(gp two) p h -> two p gp h", two=2)[two], writes=[bBr])
                P.dma("sp", Bi[hs], bim_d.rearrange("(gp two) p h -> two p gp h", two=2)[two], writes=[bBi])
                for gp in range(8):
                    P.dma("sp", Cr[hs, gp, :], cre_d[2 * gp + two].rearrange("h p -> p h"), writes=[bCr])
                    P.dma("sp", Ci[hs, gp, :], cim_d[2 * gp + two].rearrange("h p -> p h"), writes=[bCi])
            for sp in range(8):
                P.dma("sp", dcol[sp * 16:(sp + 1) * 16, :], dd_d.rearrange("(g h) -> h g", h=16), writes=[bdcol])
            sm = {}
            for nm in ("dt", "lr", "lrdt", "th", "den", "nr", "fre", "fim", "t8a", "t8b"):
                sm[nm] = C3.sb("sm_" + nm, [128, 8])
            T41 = {}
            for nm in ("ARG", "MARG", "MAG", "MAGN", "SIN", "COS", "ErN", "EiN", "rt", "rk"):
                T41[nm] = C3.sb("t41_" + nm, [128, 8, NTAU])
            rki, brki = C3.sb("rki", [128, 8, NTAU], I32)

            def tt(eng, out, bo, a, ba, b, bb_, op):
                P.op(eng, lambda E: E.tensor_tensor(out=out, in0=a, in1=b, op=op), reads=[ba, bb_], writes=[bo])

            dt, bdt = sm["dt"]; lr, blr = sm["lr"]; lrdt, blrdt = sm["lrdt"]; th, bth = sm["th"]
            P.op("act", lambda E: E.activation(out=dt[:], in_=ldt[:], func=AF.Exp), reads=[bldt], writes=[bdt])
            P.op("dve", lambda E: E.tensor_scalar(out=lr[:], in0=lre[:], scalar1=-1e-4, scalar2=None, op0=ALU.min), reads=[blre], writes=[blr])
            tt("dve", lrdt[:], blrdt, lr[:], blr, dt[:], bdt, ALU.mult)
            tt("dve", th[:], bth, lim[:], blim, dt[:], bdt, ALU.mult)
            ARG, bARG = T41["ARG"]; MARG, bMARG = T41["MARG"]; MAG, bMAG = T41["MAG"]; MAGN, bMAGN = T41["MAGN"]
            SIN, bSIN = T41["SIN"]; COS, bCOS = T41["COS"]; ErN, bErN = T41["ErN"]; EiN, bEiN = T41["EiN"]
            rt, brt = T41["rt"]; rk, brk = T41["rk"]
            tb = TAU[:].unsqueeze(1).to_broadcast([128, 8, NTAU])
            tt("dve", ARG[:], bARG, th[:].unsqueeze(2).to_broadcast([128, 8, NTAU]), bth, tb, bTAU, ALU.mult)
            tt("dve", MARG[:], bMARG, lrdt[:].unsqueeze(2).to_broadcast([128, 8, NTAU]), blrdt, tb, bTAU, ALU.mult)
            P.op("act", lambda E: E.activation(out=MAG[:], in_=MARG[:], func=AF.Exp), reads=[bMARG], writes=[bMAG])
            P.op("act", lambda E: E.activation(out=MAGN[:, :, 0:17], in_=MARG[:, :, 0:17], func=AF.Exp, scale=-1.0), reads=[bMARG], writes=[bMAGN])

            def sin_of(dst, bdst, shift):
                P.op("dve", lambda E: E.tensor_scalar(out=rt[:], in0=ARG[:], scalar1=float(shift), scalar2=None, op0=ALU.add), reads=[bARG], writes=[brt])
                P.op("dve", lambda E: E.tensor_scalar(out=rki[:], in0=rt[:], scalar1=float(1.0 / (2 * np.pi)), scalar2=None, op0=ALU.mult), reads=[brt], writes=[brki])
                P.op("dve", lambda E: E.tensor_copy(out=rk[:], in_=rki[:]), reads=[brki], writes=[brk])
                P.op("dve", lambda E: E.scalar_tensor_tensor(out=rt[:], in0=rk[:], scalar=float(-2 * np.pi), in1=rt[:], op0=ALU.mult, op1=ALU.add),
                     reads=[brk, brt], writes=[brt])
                P.op("dve", lambda E: E.tensor_scalar(out=rt[:], in0=rt[:], scalar1=-3.14159, scalar2=3.14159, op0=ALU.max, op1=ALU.min), reads=[brt], writes=[brt])
                P.op("act", lambda E: E.activation(out=dst[:], in_=rt[:], func=AF.Sin), reads=[brt], writes=[bdst])

            sin_of(SIN, bSIN, 0.0)
            sin_of(COS, bCOS, np.pi / 2)
            tt("dve", Er[:], bEr, MAG[:], bMAG, COS[:], bCOS, ALU.mult)
            tt("dve", Ei[:], bEi, MAG[:], bMAG, SIN[:], bSIN, ALU.mult)
            P.op("dve", lambda E: E.tensor_scalar(out=NEi[:], in0=Ei[:], scalar1=-1.0, scalar2=None, op0=ALU.mult), reads=[bEi], writes=[bNEi])
            tt("dve", ErN[:, :, 0:17], bErN, MAGN[:, :, 0:17], bMAGN, COS[:, :, 0:17], bCOS, ALU.mult)
            tt("dve", EiN[:, :, 0:17], bEiN, MAGN[:, :, 0:17], bMAGN, SIN[:, :, 0:17], bSIN, ALU.mult)
            P.op("dve", lambda E: E.tensor_scalar(out=EiN[:, :, 0:17], in0=EiN[:, :, 0:17], scalar1=-1.0, scalar2=None, op0=ALU.mult), reads=[bEiN], writes=[bEiN])
            den, bden = sm["den"]; nr, bnr = sm["nr"]; fre, bfre = sm["fre"]; fim, bfim = sm["fim"]; t8a, bt8a = sm["t8a"]; t8b, bt8b = sm["t8b"]
            tt("dve", den[:], bden, lr[:], blr, lr[:], blr, ALU.mult)
            tt("dve", t8a[:], bt8a, lim[:], blim, lim[:], blim, ALU.mult)
            tt("dve", den[:], bden, den[:], bden, t8a[:], bt8a, ALU.add)
            P.op("dve", lambda E: E.reciprocal(out=den[:], in_=den[:]), reads=[bden], writes=[bden])
            P.op("dve", lambda E: E.tensor_scalar(out=nr[:], in0=Er[:, :, 1], scalar1=-1.0, scalar2=None, op0=ALU.add), reads=[bEr], writes=[bnr])
            tt("dve", fre[:], bfre, nr[:], bnr, lr[:], blr, ALU.mult)
            tt("dve", t8a[:], bt8a, Ei[:, :, 1], bEi, lim[:], blim, ALU.mult)
            tt("dve", fre[:], bfre, fre[:], bfre, t8a[:], bt8a, ALU.add)
            tt("dve", fre[:], bfre, fre[:], bfre, den[:], bden, ALU.mult)
            tt("dve", fim[:], bfim, Ei[:, :, 1], bEi, lr[:], blr, ALU.mult)
            tt("dve", t8b[:], bt8b, nr[:], bnr, lim[:], blim, ALU.mult)
            tt("dve", fim[:], bfim, fim[:], bfim, t8b[:], bt8b, ALU.subtract)
            tt("dve", fim[:], bfim, fim[:], bfim, den[:], bden, ALU.mult)

            def cmul(outr, boutr, outi, bouti, ar, bar, ai, bai, br_, bbr_, bi_, bbi_, tmp, btmp):
                tt("dve", outr, boutr, ar, bar, br_, bbr_, ALU.mult)
                tt("dve", tmp, btmp, ai, bai, bi_, bbi_, ALU.mult)
                tt("dve", outr, boutr, outr, boutr, tmp, btmp, ALU.subtract)
                tt("dve", outi, bouti, ar, bar, bi_, bbi_, ALU.mult)
                tt("dve", tmp, btmp, ai, bai, br_, bbr_, ALU.mult)
                tt("dve", outi, bouti, outi, bouti, tmp, btmp, ALU.add)

            bbr, bbbr = C3.sb("bbr", [128, 8, 16]); bbi, bbbi = C3.sb("bbi", [128, 8, 16]); tmp16, btmp16 = C3.sb("tmp16", [128, 8, 16])
            fb = lambda t_: t_[:].unsqueeze(2).to_broadcast([128, 8, 16])
            cmul(bbr[:], bbbr, bbi[:], bbbi, fb(fre), bfre, fb(fim), bfim, Br[:], bBr, Bi[:], bBi, tmp16[:], btmp16)
            Gr, bGr = C3.sb("Gr", [128, 8, 16, 16]); Gi, bGi = C3.sb("Gi", [128, 8, 16, 16])
            WPr, bWPr = C3.sb("WPr", [128, 8, 16, 16]); WPi, bWPi = C3.sb("WPi", [128, 8, 16, 16])
            Hi, bHi = C3.sb("Hi", [128, 8, 17, 16]); tmpH, btmpH = C3.sb("tmpH", [128, 8, 17, 16])
            eb = lambda t_, j0, j1: t_[:, :, j0:j1].unsqueeze(3).to_broadcast([128, 8, j1 - j0, 16])
            vb = lambda t_, n_: t_[:].unsqueeze(2).to_broadcast([128, 8, n_, 16])
            cmul(Gr[:], bGr, Gi[:], bGi, eb(ErN, 0, 16), bErN, eb(EiN, 0, 16), bEiN, vb(bbr, 16), bbbr, vb(bbi, 16), bbbi, tmpH[:, :, 0:16, :], btmpH)
            cmul(WPr[:], bWPr, WPi[:], bWPi, eb(Er, 25, 41), bEr, eb(Ei, 25, 41), bEi, vb(bbr, 16), bbbr, vb(bbi, 16), bbbi, tmpH[:, :, 0:16, :], btmpH)
            cmul(Hr[:], bHr, Hi[:], bHi, eb(Er, 0, 17), bEr, eb(Ei, 0, 17), bEi, vb(Cr, 17), bCr, vb(Ci, 17), bCi, tmpH[:], btmpH)
            P.op("dve", lambda E: E.tensor_scalar(out=nHi[:], in0=Hi[:], scalar1=-1.0, scalar2=None, op0=ALU.mult), reads=[bHi], writes=[bnHi])
            for gp in range(8):
                for kt2 in range(2):
                    for c, (WP_, bWP_) in enumerate(((WPr, bWPr), (WPi, bWPi))):
                        P.op("pe", lambda E, gp=gp, kt2=kt2, WP_=WP_: E.transpose(
                            out=G[2][:, 0:128], in_=WP_[:, gp, kt2 * 8:(kt2 + 1) * 8, :].rearrange("p s h -> p (s h)"), identity=idf[:]),
                            reads=[bWP_, bidf], writes=[bG[2]])
                        P.op("act", lambda E, gp=gp, kt2=kt2, c=c: E.copy(out=WbT[:, kt2, gp, c, :], in_=G[2][:, 0:128]), reads=[bG[2]], writes=[bWbT])
            tmpT, btmpT = C3.sb("tmpT", [128, 256])
            for g in range(16):
                gp = g // 2; hs = slice(64 * (g % 2), 64 * (g % 2) + 64)
                for kt2 in range(2):
                    fns = [
                        lambda E, gp=gp, hs=hs, kt2=kt2: E.matmul(G[3][:, 0:256], lhsT=Gr[hs, gp, kt2 * 8:(kt2 + 1) * 8, :].rearrange("p s h -> p (s h)"),
                                                                  rhs=Hr[hs, gp, 0:16, :].rearrange("p t h -> p (t h)"), start=True, stop=False),
                        lambda E, gp=gp, hs=hs, kt2=kt2: E.matmul(G[3][:, 0:256], lhsT=Gi[hs, gp, kt2 * 8:(kt2 + 1) * 8, :].rearrange("p s h -> p (s h)"),
                                                                  rhs=nHi[hs, gp, 0:16, :].rearrange("p t h -> p (t h)"), start=False, stop=True)]
                    P.mm_group(fns, reads=[bGr, bGi, bHr, bnHi], writes=[bG[3]])
                    P.op("dve", lambda E, kt2=kt2: E.tensor_tensor(out=tmpT[:], in0=G[3][:, 0:256], in1=MK[:, kt2, :], op=ALU.mult),
                         reads=[bG[3], bMK], writes=[btmpT])
                    P.op("dve", lambda E, kt2=kt2, g=g: E.scalar_tensor_tensor(out=Toep[:, kt2, g, :], in0=IDM[:, kt2, :], scalar=dcol[:, g:g + 1], in1=tmpT[:],
                                                                               op0=ALU.mult, op1=ALU.add), reads=[bIDM, bdcol, btmpT], writes=[bToep])
            barrier(P)
        X = {}
        for bufn in ("A", "B"):
            for c in ("re", "im"):
                X[(bufn, c)] = (C.sb("X%s%s" % (bufn, c), [128, 8, NCH + 1])[0], [Buf("X%s%s%d" % (bufn, c, gp)) for gp in range(8)])
        Ysb, bYsb = C.sb("Ysb", [128, 16, 256], BF16 if (fz and fz.get("ybf16")) else F32)
        for key in X:
            t_, bl = X[key]
            P.op("dve", lambda E, t_=t_: E.memset(t_[:, :, 0:1], 0.0), writes=bl)
        for gp in range(8):
            for c, cn in enumerate(("re", "im")):
                px = G[c]
                fns = []
                for two in range(2):
                    g = 2 * gp + two
                    for kt2 in range(2):
                        fns.append(lambda E, two=two, g=g, kt2=kt2, gp=gp, c=c, px=px: E.matmul(
                            px[64 * two:64 * two + 64, :], lhsT=WbT[:, kt2, gp, c, 64 * two:64 * two + 64], rhs=U[:, kt2, g, :],
                            start=(kt2 == 0), stop=(kt2 == 1)))
                P.mm_group(fns, reads=[bWbT, bU], writes=[bG[c]])
                xt_, xb_ = X[("A", cn)]
                P.op("act", lambda E, xt_=xt_, gp=gp, px=px: E.copy(out=xt_[:, gp, 1:NCH + 1], in_=px[:]), reads=[bG[c]], writes=[xb_[gp]])
        for k in range(9):
            d = 1 << k
            j = 16 if k == 0 else 16 + k
            src, dst = ("A", "B") if k % 2 == 0 else ("B", "A")
            sre, bsre = X[(src, "re")]; sim, bsim = X[(src, "im")]
            dre, bdre = X[(dst, "re")]; dim_, bdim = X[(dst, "im")]
            P.op("dve", lambda E, dre=dre, sre=sre, d=d: E.tensor_copy(out=dre[:, :, 1:1 + d], in_=sre[:, :, 1:1 + d]), reads=bsre, writes=bdre)
            P.op("pool", lambda E, dim_=dim_, sim=sim, d=d: E.tensor_copy(out=dim_[:, :, 1:1 + d], in_=sim[:, :, 1:1 + d]), reads=bsim, writes=bdim)
            for gp in range(8):
                lo = slice(1, NCH + 1 - d); hi = slice(1 + d, NCH + 1)
                P.op("dve", lambda E, gp=gp, j=j, dre=dre, sre=sre, lo=lo, hi=hi: E.scalar_tensor_tensor(
                    out=dre[:, gp, hi], in0=sre[:, gp, lo], scalar=Er[:, gp, j:j + 1], in1=sre[:, gp, hi], op0=ALU.mult, op1=ALU.add),
                    reads=[bsre[gp], bEr], writes=[bdre[gp]])
                P.op("dve", lambda E, gp=gp, j=j, dre=dre, sim=sim, lo=lo, hi=hi: E.scalar_tensor_tensor(
                    out=dre[:, gp, hi], in0=sim[:, gp, lo], scalar=NEi[:, gp, j:j + 1], in1=dre[:, gp, hi], op0=ALU.mult, op1=ALU.add),
                    reads=[bsim[gp], bNEi, bdre[gp]], writes=[bdre[gp]])
                P.op("dve", lambda E, gp=gp, j=j, dim_=dim_, sim=sim, lo=lo, hi=hi: E.scalar_tensor_tensor(
                    out=dim_[:, gp, hi], in0=sim[:, gp, lo], scalar=Er[:, gp, j:j + 1], in1=sim[:, gp, hi], op0=ALU.mult, op1=ALU.add),
                    reads=[bsim[gp], bEr], writes=[bdim[gp]])
                P.op("dve", lambda E, gp=gp, j=j, dim_=dim_, sre=sre, lo=lo, hi=hi: E.scalar_tensor_tensor(
                    out=dim_[:, gp, hi], in0=sre[:, gp, lo], scalar=Ei[:, gp, j:j + 1], in1=dim_[:, gp, hi], op0=ALU.mult, op1=ALU.add),
                    reads=[bsre[gp], bEi, bdim[gp]], writes=[bdim[gp]])
        fre_, bfre_ = X[("B", "re")]; fim_, bfim_ = X[("B", "im")]
        bys = None if fz else Buf("ys", multi=True)
        ysv = ys_d.rearrange("(n t) c -> n t c", t=16)
        for jt in range(NCH // 128):
            for gq in range(4):
                fns = []
                for gi in range(4):
                    g = 4 * gq + gi; gp = g // 2; hs = slice(64 * (g % 2), 64 * (g % 2) + 64)
                    o_ = (gi * 256, (gi + 1) * 256)
                    for kt2 in range(2):
                        fns.append(lambda E, o_=o_, g=g, kt2=kt2, jt=jt: E.matmul(
                            py[:, o_[0]:o_[1]], lhsT=U[:, kt2, g, jt * 128:(jt + 1) * 128], rhs=Toep[:, kt2, g, :], start=(kt2 == 0), stop=False))
                    fns.append(lambda E, o_=o_, gp=gp, hs=hs, jt=jt: E.matmul(
                        py[:, o_[0]:o_[1]], lhsT=fre_[hs, gp, jt * 128:(jt + 1) * 128], rhs=Hr[hs, gp, 1:17, :].rearrange("p t h -> p (t h)"),
                        start=False, stop=False))
                    fns.append(lambda E, o_=o_, gp=gp, hs=hs, jt=jt: E.matmul(
                        py[:, o_[0]:o_[1]], lhsT=fim_[hs, gp, jt * 128:(jt + 1) * 128], rhs=nHi[hs, gp, 1:17, :].rearrange("p t h -> p (t h)"),
                        start=False, stop=True))
                P.mm_group(fns, reads=[bU, bToep, bHr, bnHi] + bfre_ + bfim_, writes=[bpy])
                P.op("act" if gq % 2 == 0 else "dve",
                     (lambda E, gq=gq: E.copy(out=Ysb[:].rearrange("p t (g h) -> p g t h", h=16)[:, 4 * gq:4 * gq + 4],
                                              in_=py[:].rearrange("p (g t h) -> p g t h", g=4, h=16)))
                     if gq % 2 == 0 else
                     (lambda E, gq=gq: E.tensor_copy(out=Ysb[:].rearrange("p t (g h) -> p g t h", h=16)[:, 4 * gq:4 * gq + 4],
                                                     in_=py[:].rearrange("p (g t h) -> p g t h", g=4, h=16))),
                     reads=[bpy], writes=[bYsb])
            P.dma("sp", ysv[jt * 128:(jt + 1) * 128, :, :], Ysb[:], reads=[bYsb], writes=[fz["obuf_of"](jt) if fz else bys])
            if fz:
                fz["after_chunk"](jt)
        if fz:
            barrier(P)
        else:
            P.finish([bys])
    return nc


def run_L1b(inp):
    nc = _get("L1b", build_L1b)
    mk, idm = _s5_consts()
    w_in = inp["w_in_even"][0]
    maps = []
    for c in range(8):
        b, r = divmod(c, 4)
        gs = slice(16 * r, 16 * r + 16)
        maps.append({"x": np.ascontiguousarray(inp["x"][b]), "npre": np.ascontiguousarray(inp["norm_pre"][0]),
                     "wu": np.ascontiguousarray(w_in[:, 4112 + 256 * r:4112 + 256 * (r + 1)]),
                     "lre": np.ascontiguousarray(inp["s5_lam_re"][0, gs]), "lim": np.ascontiguousarray(inp["s5_lam_im"][0, gs]),
                     "bre": np.ascontiguousarray(inp["s5_b_re"][0, gs]), "bim": np.ascontiguousarray(inp["s5_b_im"][0, gs]),
                     "cre": np.ascontiguousarray(inp["s5_c_re"][0, gs]), "cim": np.ascontiguousarray(inp["s5_c_im"][0, gs]),
                     "ldt": np.ascontiguousarray(inp["s5_log_dt"][0, gs]), "dd": np.ascontiguousarray(inp["s5_d"][0, 256 * r:256 * (r + 1)]),
                     "taus": TAUS, "mk": mk, "idm": idm, "ident": _IDENT})
    res = run_bass_kernel_spmd(nc, maps, core_ids=list(range(8)))
    ys = np.empty((2, 8192, 1024), np.float32)
    for c in range(8):
        b, r = divmod(c, 4)
        ys[b, :, 256 * r:256 * (r + 1)] = res.results[c]["ys"]
    return ys


def _gdn_consts():
    p = np.arange(64)[:, None]; f = np.arange(64)[None, :]
    negu = np.where(f >= p, 0.0, -30000.0)
    negls = np.where(f < p, 0.0, -30000.0)
    nsu = np.where(f > p, -1.0, 0.0)
    i64 = np.eye(64)
    c64 = np.stack([negu, negls, nsu, i64], axis=1).astype(np.float32)
    cmask = np.ones((2, 512), np.float32); cmask[:, 0::64] = 0.0
    sel = np.zeros((2, 2, 128), np.float32); sel[0, 0, :] = 1.0; sel[1, 1, :] = 1.0
    return c64, cmask, sel


def build_L1a(S=8192, fz=None):
    nc = fz["nc"] if fz else bass.Bass("TRN2", target_bir_lowering=False)
    pfx = fz["pfx"] if fz else ""

    def D(name, shape):
        if fz and name in fz["share"]:
            return fz["share"][name]
        return nc.dram_tensor(pfx + name, shape, F32, kind="ExternalInput").ap()
    x_d = D("x", [S, 1024]); npre_d = D("npre", [1024]); w_d = D("w", [1024, 768]); wb_d = D("wb", [1024, 2]); wa_d = D("wa", [1024, 2])
    conv_d = D("conv", [4, 768]); alog_d = D("alog", [2]); dtb_d = D("dtb", [2])
    ident_d = D("ident", [128, 128]); c64_d = D("c64", [64, 4, 64]); cmask_d = D("cmask", [2, 512]); sel_d = D("sel", [2, 2, 128])
    ones_d = D("ones", [128, 128])
    o_d = fz["out"] if fz else nc.dram_tensor("o", [S, 256], F32, kind="ExternalOutput").ap()
    NST = S // 512
    with ExitStack() as st:
        C = Ctx(nc, st, fz["P"], pfx) if fz else Ctx(nc, st); P = C.P
        idf, bidf, idb, bidb = make_ident(C, ident_d)
        npre, bnpre = bcast_row_load(C, "npre", npre_d, 1024)
        w, bw = load_w_bf16(C, "w", w_d, 8, 768)
        wb, bwb = load_w_bf16(C, "wb", wb_d, 8, 2)
        wa, bwa = load_w_bf16(C, "wa", wa_d, 8, 2)
        cw, bcw = C.sb("cw", [128, 4, 6])
        P.dma("sp", cw[:], conv_d.rearrange("j (c p) -> p j c", p=128), writes=[bcw])
        extu = fz.get("uTp") if fz else None
        if extu:
            wu_d = D("wu", [1024, 256])
            wu, bwu = load_w_bf16(C, "wu", wu_d, 8, 256)
            uTp, buTp = extu
        c64, bc64 = C.sb("c64", [64, 4, 64]); P.dma("sp", c64[:], c64_d, writes=[bc64])
        NEGU = c64[:, 0, :]; NEGLS = c64[:, 1, :]; NSU = c64[:, 2, :]; I64 = c64[:, 3, :]
        cmask, bcmask = C.sb("cmask", [2, 512]); P.dma("sp", cmask[:], cmask_d, writes=[bcmask])
        sel, bsel = C.sb("sel", [2, 2, 128]); P.dma("sp", sel[:], sel_d, writes=[bsel])
        ones, bones = C.sb("ones", [128, 128]); P.dma("sp", ones[:], ones_d, writes=[bones])
        onesb, bonesb = C.sb("onesb", [128, 128], BF16)
        P.op("dve", lambda E: E.tensor_copy(out=onesb[:], in_=ones[:]), reads=[bones], writes=[bonesb])
        sqb, bsqb = C.sb("sqb", [128, 512], BF16)
        alog, balog = C.sb("alog", [2, 1]); P.dma("sp", alog[:], alog_d.rearrange("(a b) -> a b", b=1), writes=[balog])
        dtb, bdtb = C.sb("dtb", [2, 1]); P.dma("sp", dtb[:], dtb_d.rearrange("(a b) -> a b", b=1), writes=[bdtb])
        negA, bnegA = C.sb("negA", [2, 1])
        P.op("act", lambda E: E.activation(out=negA[:], in_=alog[:], func=AF.Exp), reads=[balog], writes=[bnegA])
        P.op("dve", lambda E: E.tensor_scalar(out=negA[:], in0=negA[:], scalar1=-1.0, scalar2=None, op0=ALU.mult), reads=[bnegA], writes=[bnegA])
        xt, bxt = C.sb("xt", [128, 1024]); sq, bsq = C.sb("sq", [128, 1024], BF16); hn, bhn = C.sb("hn", [128, 1024], BF16)
        ss, bss = C.sb("ss", [128, 1]); hT, bhT = C.sb("hT", [128, 8, 512], BF16)
        raw, _ = C.sb("raw", [128, 6, 515]); braw = [Buf("raw%d" % i) for i in range(6)]
        cvq, bcvq = C.sb("cvq", [128, 512])
        act, _ = C.sb("act", [128, 4, 512]); bact = [Buf("act%d" % i) for i in range(4)]
        vbuf2 = []; qk2 = []; bqk2 = []
        for par_ in range(2):
            vt_, _ = C.sb("vbuf%d" % par_, [128, 2, 512]); vbuf2.append((vt_, [Buf("vb%d_%d" % (par_, i)) for i in range(2)]))
            qt_, _ = C.sb("qk%d" % par_, [128, 4, 512]); qk2.append(qt_); bqk2.append([Buf("qk%d_%d" % (par_, i)) for i in range(4)])
        rn, brn = C.sb("rn", [128, 512])
        brow, bbrow = C.sb("brow", [2, 512]); grow, bgrow = C.sb("grow", [2, 512]); gcrow, bgcrow = C.sb("gcrow", [2, 512])
        GCB2 = []; BB2 = []
        for par_ in range(2):
            GCB2.append([C.sb("GCB%d_%d" % (par_, h), [128, 512]) for h in range(2)])
            BB2.append([C.sb("BB%d_%d" % (par_, h), [128, 512]) for h in range(2)])
        m64h = []; smallh = []
        for h in range(2):
            d_ = {}
            for nm in ("arg1", "scr"):
                d_[nm] = C.sb("m%d_%s" % (h, nm), [64, 512])
            for nm in ("DT", "Ds", "tmp", "Pa", "Pb", "Qa", "Qb"):
                d_[nm] = C.sb("m%d_%s" % (h, nm), [64, 512], BF16)
            m64h.append(d_)
        heads = []
        for h in range(2):
            H = {}
            H["attnT"] = C.sb("attnT%d" % h, [64, 512], BF16); H["Y"] = C.sb("Y%d" % h, [64, 512]); H["Ybf"] = C.sb("Ybf%d" % h, [64, 512], BF16)
            H["EG"] = C.sb("EG%d" % h, [128, 512]); H["qdec"] = C.sb("qdec%d" % h, [128, 512], BF16)
            H["kTb"] = C.sb("kTb%d" % h, [128, 512], BF16); H["Sbf"] = C.sb("Sbf%d" % h, [128, 128], BF16)
            H["bv"] = C.sb("bv%d" % h, [64, 8, 128]); H["kdec"] = C.sb("kdec%d" % h, [64, 8, 128], BF16)
            H["nbg"] = C.sb("nbg%d" % h, [64, 8]); H["osb"] = C.sb("osb%d" % h, [128, 8, 128])
            H["vnew"] = C.sb("vnew%d" % h, [64, 128], BF16); H["rhs2"] = C.sb("rhs2%d" % h, [64, 128], BF16)
            heads.append(H)
        for h in range(2):
            d_ = {}
            for nm in ("gccol", "bcol", "nbcol", "elast", "egc"):
                d_[nm] = C.sb("s%d_%s" % (h, nm), [64, 8])
            smallh.append(d_)
        Sst = [C.sb("S%d" % h, [128, 128]) for h in range(2)]
        for h in range(2):
            P.op("dve", lambda E, h=h: E.memset(Sst[h][0][:], 0.0), writes=[Sst[h][1]])
            P.op("dve", lambda E, h=h: E.memset(heads[h]["Sbf"][0][:], 0.0), writes=[heads[h]["Sbf"][1]])
        P.op("dve", lambda E: E.memset(raw[:, :, 0:3], 0.0), writes=braw)
        ptr, bptr = C.ps("ptr", [128, 1024], BF16)
        G = [C.ps("gp%d" % i, [128, 512]) for i in range(7)]
        GP = G[0:4]
        GA = G[4:7]
        BKS = [(GP[0], GP[1], GP[2]), (GP[3], GA[0], GA[1])]
        ga_ctr = [0]

        def next_ga():
            ga_ctr[0] += 1
            return GA[ga_ctr[0] % 3]
        bo = None if fz else Buf("o", multi=True)
        if fz is not None and fz.get("debug"):
            print("L1a sbuf remaining", nc.sbuf_bytes_remaining)

        def tt(out, bo_, a, ba, b, bb_, op, eng="dve"):
            P.op(eng, lambda E: E.tensor_tensor(out=out, in0=a, in1=b, op=op), reads=ba if isinstance(ba, list) else [ba], writes=[bo_])

        def stageA(s_):
            par = s_ % 2
            qk = qk2[par]; bqk = bqk2[par]; GCB = GCB2[par]; BB = BB2[par]; vb, bvb = vbuf2[par]
            for t in range(4):
                r0 = s_ * 512 + t * 128
                P.dma("sp", xt[:], x_d[r0:r0 + 128, :], writes=[bxt])
                rms_rstd(C, xt[:], bxt, 1024, sq[:], bsq, ss, bss)
                P.op("dve", lambda E: E.scalar_tensor_tensor(out=hn[:], in0=xt[:], scalar=ss[:, 0:1], in1=npre[:],
                                                             op0=ALU.mult, op1=ALU.mult), reads=[bxt, bss, bnpre], writes=[bhn])
                transpose8(C, hn, bhn, idb, bidb, ptr, bptr, hT[:, :, t * 128:(t + 1) * 128], bhT, eng="act")
                yield
            for ct in range(6):
                pa, bpa = next_ga()
                fns = [(lambda E, kt=kt, ct=ct, pa=pa: E.matmul(pa[:], lhsT=w[:, kt, ct * 128:(ct + 1) * 128], rhs=hT[:, kt, :],
                                                                start=(kt == 0), stop=(kt == 7))) for kt in range(8)]
                P.mm_group(fns, reads=[bw, bhT], writes=[bpa])
                P.op("act", lambda E, ct=ct, pa=pa: E.copy(out=raw[:, ct, 3:515], in_=pa[:]), reads=[bpa], writes=[braw[ct]])
                P.op("dve", lambda E, ct=ct: E.tensor_scalar(out=cvq[:], in0=raw[:, ct, 0:512], scalar1=cw[:, 0, ct:ct + 1], scalar2=None, op0=ALU.mult),
                     reads=[braw[ct], bcw], writes=[bcvq])
                for j in range(1, 4):
                    P.op("dve", lambda E, ct=ct, j=j: E.scalar_tensor_tensor(out=cvq[:], in0=raw[:, ct, j:j + 512], scalar=cw[:, j, ct:ct + 1], in1=cvq[:],
                                                                             op0=ALU.mult, op1=ALU.add), reads=[braw[ct], bcw, bcvq], writes=[bcvq])
                P.op("act", lambda E, ct=ct: E.copy(out=raw[:, ct, 0:3], in_=raw[:, ct, 512:515]), reads=[braw[ct]], writes=[braw[ct]])
                if ct < 4:
                    P.op("act", lambda E, ct=ct: E.activation(out=act[:, ct, :], in_=cvq[:], func=AF.Silu), reads=[bcvq], writes=[bact[ct]])
                else:
                    P.op("act", lambda E, ct=ct, vb=vb: E.activation(out=vb[:, ct - 4, :], in_=cvq[:], func=AF.Silu), reads=[bcvq], writes=[bvb[ct - 4]])
                yield
            if extu:
                for blk in range(2):
                    pa, bpa = next_ga()
                    fns = [(lambda E, kt=kt, blk=blk, pa=pa: E.matmul(
                        pa[:].rearrange("p (s n) -> p s n", s=16), lhsT=wu[:, kt, blk * 128:(blk + 1) * 128],
                        rhs=hT[:, kt, :].rearrange("p (n s) -> p s n", s=16), start=(kt == 0), stop=(kt == 7))) for kt in range(8)]
                    P.mm_group(fns, reads=[bwu, bhT], writes=[bpa])
                    P.op("act", lambda E, blk=blk, pa=pa, s_=s_: E.copy(out=uTp[:, blk, :, 32 * s_:32 * s_ + 32], in_=pa[:].rearrange("p (s n) -> p s n", s=16)),
                         reads=[bpa], writes=[buTp])
                    yield
            for ct in range(4):
                pa, bpa = next_ga()
                P.op("act", lambda E, ct=ct: E.activation(out=sqb[:], in_=act[:, ct, :], func=AF.Square), reads=[bact[ct]], writes=[bsqb])
                P.op("pe", lambda E, pa=pa: E.matmul(pa[:], lhsT=onesb[:], rhs=sqb[:], start=True, stop=True), reads=[bonesb, bsqb], writes=[bpa])
                P.op("act", lambda E, pa=pa: E.activation(out=rn[:], in_=pa[:], func=AF.Ln, bias=1e-6, scale=1.0), reads=[bpa], writes=[brn])
                P.op("act", lambda E: E.activation(out=rn[:], in_=rn[:], func=AF.Exp, scale=-0.5), reads=[brn], writes=[brn])
                if ct < 2:
                    P.op("dve", lambda E, ct=ct, qk=qk: E.scalar_tensor_tensor(out=qk[:, ct, :], in0=act[:, ct, :], scalar=float(128 ** -0.5), in1=rn[:],
                                                                               op0=ALU.mult, op1=ALU.mult), reads=[bact[ct], brn], writes=[bqk[ct]])
                else:
                    P.op("dve", lambda E, ct=ct, qk=qk: E.tensor_tensor(out=qk[:, ct, :], in0=act[:, ct, :], in1=rn[:], op=ALU.mult),
                         reads=[bact[ct], brn], writes=[bqk[ct]])
                yield
            pa, bpa = next_ga()
            fns = [(lambda E, kt=kt, pa=pa: E.matmul(pa[0:2, :], lhsT=wb[:, kt, 0:2], rhs=hT[:, kt, :], start=(kt == 0), stop=(kt == 7))) for kt in range(8)]
            P.mm_group(fns, reads=[bwb, bhT], writes=[bpa])
            P.op("act", lambda E, pa=pa: E.activation(out=brow[:], in_=pa[0:2, :], func=AF.Sigmoid), reads=[bpa], writes=[bbrow])
            pa2, bpa2 = next_ga()
            fns = [(lambda E, kt=kt, pa2=pa2: E.matmul(pa2[0:2, :], lhsT=wa[:, kt, 0:2], rhs=hT[:, kt, :], start=(kt == 0), stop=(kt == 7))) for kt in range(8)]
            P.mm_group(fns, reads=[bwa, bhT], writes=[bpa2])
            P.op("act", lambda E, pa2=pa2: E.activation(out=grow[:], in_=pa2[0:2, :], func=AF.Exp, bias=dtb[:, 0:1], scale=1.0), reads=[bpa2, bdtb], writes=[bgrow])
            P.op("act", lambda E: E.activation(out=grow[:], in_=grow[:], func=AF.Ln, bias=1.0, scale=1.0), reads=[bgrow], writes=[bgrow])
            P.op("dve", lambda E: E.tensor_scalar(out=grow[:], in0=grow[:], scalar1=negA[:, 0:1], scalar2=None, op0=ALU.mult), reads=[bgrow, bnegA], writes=[bgrow])
            P.op("dve", lambda E: E.tensor_tensor_scan(out=gcrow[:], data0=cmask[:], data1=grow[:], initial=0.0, op0=ALU.mult, op1=ALU.add),
                 reads=[bcmask, bgrow], writes=[bgcrow])
            yield
            for h in range(2):
                pa, bpa = next_ga()
                P.op("pe", lambda E, h=h, pa=pa: E.matmul(pa[:], lhsT=sel[:, h, :], rhs=gcrow[:], start=True, stop=True), reads=[bsel, bgcrow], writes=[bpa])
                P.op("act", lambda E, h=h, pa=pa, GCB=GCB: E.copy(out=GCB[h][0][:], in_=pa[:]), reads=[bpa], writes=[GCB[h][1]])
                pa, bpa = next_ga()
                P.op("pe", lambda E, h=h, pa=pa: E.matmul(pa[:], lhsT=sel[:, h, :], rhs=brow[:], start=True, stop=True), reads=[bsel, bbrow], writes=[bpa])
                P.op("act", lambda E, h=h, pa=pa, BB=BB: E.copy(out=BB[h][0][:], in_=pa[:]), reads=[bpa], writes=[BB[h][1]])
                yield

        for _ in stageA(0):
            pass
        for s_ in range(NST):
            par = s_ % 2
            qk = qk2[par]; bqk = bqk2[par]; GCB = GCB2[par]; BB = BB2[par]; vb, bvb = vbuf2[par]
            nxt = stageA(s_ + 1) if s_ + 1 < NST else None

            def advance(k):
                if nxt is not None:
                    for _ in range(k):
                        next(nxt, None)
            def stageB(h, qk=qk, bqk=bqk, GCB=GCB, BB=BB, vb=vb, bvb=bvb):
                m64 = m64h[h]; small = smallh[h]; BK = BKS[h]
                qT = qk[:, h, :]; bqT = bqk[h]; kT = qk[:, 2 + h, :]; bkT = bqk[2 + h]; vT = vb[:, h, :]; bvT = bvb[h]
                gcb, bgcb = GCB[h]; bb, bbb = BB[h]
                H = heads[h]
                attnT, battnT = H["attnT"]; Y, bY = H["Y"]; EG, bEG = H["EG"]; qdec, bqdec = H["qdec"]
                Ybf, bYbf = H["Ybf"]
                bv, bbv = H["bv"]; kdec, bkdec = H["kdec"]; nbg, bnbg = H["nbg"]
                arg1, barg1 = m64["arg1"]; scr, bscr = m64["scr"]; DT, bDT = m64["DT"]; Ds, bDs = m64["Ds"]
                tmp, btmp = m64["tmp"]
                gccol, bgccol = small["gccol"]; bcol, bbcol = small["bcol"]; nbcol, bnbcol = small["nbcol"]
                elast, belast = small["elast"]; egc, begc = small["egc"]
                v3 = lambda t_: t_[:].rearrange("p (n f) -> p n f", f=64)
                i64b = I64.unsqueeze(1).to_broadcast([64, 8, 64])
                tt(v3(scr), bscr, gcb[0:64, :].rearrange("p (n f) -> p n f", f=64), [bgcb, bc64], i64b, bc64, ALU.mult)
                P.op("dve", lambda E, scr=scr, gccol=gccol: E.tensor_reduce(out=gccol[:], in_=scr[:].rearrange("p (n f) -> p n f", f=64), axis=AX.X, op=ALU.add), reads=[bscr], writes=[bgccol])
                tt(v3(scr), bscr, bb[0:64, :].rearrange("p (n f) -> p n f", f=64), [bbb, bc64], i64b, bc64, ALU.mult)
                P.op("dve", lambda E, scr=scr, bcol=bcol: E.tensor_reduce(out=bcol[:], in_=scr[:].rearrange("p (n f) -> p n f", f=64), axis=AX.X, op=ALU.add), reads=[bscr], writes=[bbcol])
                yield
                P.op("dve", lambda E: E.tensor_scalar(out=nbcol[:], in0=bcol[:], scalar1=-1.0, scalar2=None, op0=ALU.mult), reads=[bbcol], writes=[bnbcol])
                tt(v3(arg1), barg1, gcb[0:64, :].rearrange("p (n f) -> p n f", f=64), [bgcb, bgccol], gccol[:].unsqueeze(2).to_broadcast([64, 8, 64]), bgccol, ALU.subtract)
                tt(v3(scr), bscr, v3(arg1), [barg1, bc64], NEGU.unsqueeze(1).to_broadcast([64, 8, 64]), bc64, ALU.add)
                P.op("act", lambda E: E.activation(out=DT[:], in_=scr[:], func=AF.Exp), reads=[bscr], writes=[bDT])
                yield
                P.op("dve", lambda E: E.scalar_tensor_tensor(out=scr[:].rearrange("p (n f) -> p n f", f=64), in0=arg1[:].rearrange("p (n f) -> p n f", f=64), scalar=-1.0,
                                                             in1=NEGLS.unsqueeze(1).to_broadcast([64, 8, 64]), op0=ALU.mult, op1=ALU.add), reads=[barg1, bc64], writes=[bscr])
                P.op("act", lambda E: E.activation(out=Ds[:], in_=scr[:], func=AF.Exp), reads=[bscr], writes=[bDs])
                pk, bpk = BK[0]; pq, bpq = BK[1]
                fns = [(lambda E, n=n, pk=pk, kT=kT: E.matmul(pk[0:64, n * 64:(n + 1) * 64], lhsT=kT[:, n * 64:(n + 1) * 64], rhs=kT[:, n * 64:(n + 1) * 64],
                                                              start=True, stop=True)) for n in range(8)]
                P.mm_group(fns, reads=[bkT], writes=[bpk])
                fns = [(lambda E, n=n, pq=pq, kT=kT, qT=qT: E.matmul(pq[0:64, n * 64:(n + 1) * 64], lhsT=kT[:, n * 64:(n + 1) * 64], rhs=qT[:, n * 64:(n + 1) * 64],
                                                                     start=True, stop=True)) for n in range(8)]
                P.mm_group(fns, reads=[bkT, bqT], writes=[bpq])
                yield
                tt(attnT[:], battnT, pq[0:64, :], [bpq, bDT], DT[:], bDT, ALU.mult)
                Pc, bPc = m64["Pa"]; Pn, bPn = m64["Pb"]; Qc, bQc = m64["Qa"]; Qn, bQn = m64["Qb"]
                tt(tmp[:], btmp, pk[0:64, :], [bpk, bDT], DT[:], bDT, ALU.mult)
                tt(tmp[:], btmp, tmp[:], [btmp, bbb], bb[0:64, :], bbb, ALU.mult)
                tt(v3(Qc), bQc, v3(tmp), [btmp, bc64], NSU.unsqueeze(1).to_broadcast([64, 8, 64]), bc64, ALU.mult)
                yield
                tt(tmp[:], btmp, pk[0:64, :], [bpk, bDs], Ds[:], bDs, ALU.mult)
                tt(v3(Pc), bPc, v3(tmp), [btmp, bnbcol], nbcol[:].unsqueeze(2).to_broadcast([64, 8, 64]), bnbcol, ALU.mult)
                tt(v3(Y), bY, v3(Qc), [bQc, bc64], i64b, bc64, ALU.add)
                P.op("act", lambda E, Ybf=Ybf, Y=Y: E.copy(out=Ybf[:], in_=Y[:]), reads=[bY], writes=[bYbf])
                yield
                for j in range(5):
                    pP, bpP = BK[2]; pQ, bpQ = BK[1]
                    fns = [(lambda E, n=n, pP=pP, Qc=Qc, Pc=Pc: E.matmul(pP[0:64, n * 64:(n + 1) * 64], lhsT=Qc[:, n * 64:(n + 1) * 64], rhs=Pc[:, n * 64:(n + 1) * 64],
                                                                         start=True, stop=True)) for n in range(8)]
                    P.mm_group(fns, reads=[bQc, bPc], writes=[bpP])
                    if j < 4:
                        fns = [(lambda E, n=n, pQ=pQ, Qc=Qc, Pc=Pc: E.matmul(pQ[0:64, n * 64:(n + 1) * 64], lhsT=Pc[:, n * 64:(n + 1) * 64], rhs=Qc[:, n * 64:(n + 1) * 64],
                                                                             start=True, stop=True)) for n in range(8)]
                        P.mm_group(fns, reads=[bQc, bPc], writes=[bpQ])
                    yield
                    P.op("act", lambda E, Pn=Pn, pP=pP: E.copy(out=Pn[:], in_=pP[0:64, :]), reads=[bpP], writes=[bPn])
                    if j < 4:
                        P.op("dve", lambda E, Qn=Qn, pQ=pQ: E.tensor_copy(out=Qn[:], in_=pQ[0:64, :]), reads=[bpQ], writes=[bQn])
                    pY, bpY = BK[0]
                    fns = [(lambda E, n=n, pY=pY, Pn=Pn, Ybf=Ybf: E.matmul(pY[0:64, n * 64:(n + 1) * 64], lhsT=Pn[:, n * 64:(n + 1) * 64], rhs=Ybf[:, n * 64:(n + 1) * 64],
                                                                         start=True, stop=True)) for n in range(8)]
                    P.mm_group(fns, reads=[bPn, bYbf], writes=[bpY])
                    yield
                    tt(Y[:], bY, Y[:], [bY, bpY], pY[0:64, :], bpY, ALU.add)
                    P.op("act", lambda E, Ybf=Ybf, Y=Y: E.copy(out=Ybf[:], in_=Y[:]), reads=[bY], writes=[bYbf])
                    Pc, bPc, Pn, bPn = Pn, bPn, Pc, bPc
                    Qc, bQc, Qn, bQn = Qn, bQn, Qc, bQc
                for hf in range(2):
                    pth, bpth = BK[1 + hf]
                    fns = [(lambda E, n=n, vT=vT, pth=pth, hf=hf: E.transpose(out=pth[0:64, n * 128:(n + 1) * 128], in_=vT[:, (4 * hf + n) * 64:(4 * hf + n + 1) * 64],
                                                                              identity=idf[:])) for n in range(4)]
                    P.mm_group(fns, reads=[bvT, bidf], writes=[bpth])
                    tt(bv[:, 4 * hf:4 * hf + 4, :], bbv, pth[0:64, :].rearrange("p (n d) -> p n d", d=128), [bpth, bbcol],
                       bcol[:, 4 * hf:4 * hf + 4].unsqueeze(2).to_broadcast([64, 4, 128]), bbcol, ALU.mult)
                yield
                tt(elast[:], belast, gcb[0:64, :].rearrange("p (n f) -> p n f", f=64)[:, :, 63], [bgcb, bgccol], gccol[:], bgccol, ALU.subtract)
                P.op("act", lambda E: E.activation(out=elast[:], in_=elast[:], func=AF.Exp), reads=[belast], writes=[belast])
                for hf in range(2):
                    pth, bpth = BK[1 + hf]
                    fns = [(lambda E, n=n, kT=kT, pth=pth, hf=hf: E.transpose(out=pth[0:64, n * 128:(n + 1) * 128], in_=kT[:, (4 * hf + n) * 64:(4 * hf + n + 1) * 64],
                                                                              identity=idf[:])) for n in range(4)]
                    P.mm_group(fns, reads=[bkT, bidf], writes=[bpth])
                    tt(kdec[:, 4 * hf:4 * hf + 4, :], bkdec, pth[0:64, :].rearrange("p (n d) -> p n d", d=128), [bpth, belast],
                       elast[:, 4 * hf:4 * hf + 4].unsqueeze(2).to_broadcast([64, 4, 128]), belast, ALU.mult)
                yield
                P.op("act", lambda E, gcb=gcb, EG=EG: E.activation(out=EG[:], in_=gcb[:], func=AF.Exp), reads=[bgcb], writes=[bEG])
                tt(qdec[:], bqdec, qT, [bqT, bEG], EG[:], bEG, ALU.mult)
                kTb, bkTb = H["kTb"]
                P.op("act", lambda E, kTb=kTb, kT=kT: E.copy(out=kTb[:], in_=kT), reads=[bkT], writes=[bkTb])
                P.op("act", lambda E: E.activation(out=egc[:], in_=gccol[:], func=AF.Exp), reads=[bgccol], writes=[begc])
                P.op("dve", lambda E, nbg=nbg: E.scalar_tensor_tensor(out=nbg[:], in0=egc[:], scalar=-1.0, in1=bcol[:], op0=ALU.mult, op1=ALU.mult),
                     reads=[begc, bbcol], writes=[bnbg])
            gensB = [stageB(0), stageB(1)]
            aliveB = True
            while aliveB:
                aliveB = False
                for g_ in gensB:
                    try:
                        next(g_)
                        aliveB = True
                    except StopIteration:
                        pass
            banks = [(GP[0], GP[1]), (GP[2], GP[3])]
            for n in range(8):
                cs = slice(n * 64, (n + 1) * 64)
                for h in range(2):
                    H = heads[h]; S, bS = Sst[h]
                    kT, bkT = H["kTb"]; Sbf, bSbf = H["Sbf"]
                    attnT, battnT = H["attnT"]; Y, bY = H["Ybf"]; EG, bEG = H["EG"]; qdec, bqdec = H["qdec"]
                    bv, bbv = H["bv"]; kdec, bkdec = H["kdec"]; nbg, bnbg = H["nbg"]
                    vnew, bvnew = H["vnew"]; rhs2, brhs2 = H["rhs2"]; osb, bosb = H["osb"]
                    (KSO, bKSO), (Sb, bSb) = banks[h]
                    Vb, bVb = KSO, bKSO
                    P.op("pe", lambda E, cs=cs, kT=kT, Sbf=Sbf, KSO=KSO: E.matmul(KSO[0:64, 0:128], lhsT=kT[:, cs], rhs=Sbf[:], start=True, stop=True),
                         reads=[bkT, bSbf], writes=[bKSO])
                    P.op("dve", lambda E, n=n, KSO=KSO, rhs2=rhs2, nbg=nbg, bv=bv: E.scalar_tensor_tensor(
                        out=rhs2[:], in0=KSO[0:64, 0:128], scalar=nbg[:, n:n + 1], in1=bv[:, n, :], op0=ALU.mult, op1=ALU.add),
                        reads=[bKSO, bnbg, bbv], writes=[brhs2])
                    P.op("pe", lambda E, cs=cs, Y=Y, Vb=Vb, rhs2=rhs2: E.matmul(Vb[0:64, 128:256], lhsT=Y[:, cs], rhs=rhs2[:], start=True, stop=True),
                         reads=[bY, brhs2], writes=[bVb])
                    P.op("act", lambda E, vnew=vnew, Vb=Vb: E.copy(out=vnew[:], in_=Vb[0:64, 128:256]), reads=[bVb], writes=[bvnew])
                    fns = [lambda E, cs=cs, Sbf=Sbf, KSO=KSO, qdec=qdec: E.matmul(KSO[64:128, 0:128], lhsT=qdec[:, cs], rhs=Sbf[:], start=True, stop=False),
                           lambda E, cs=cs, KSO=KSO, attnT=attnT, vnew=vnew: E.matmul(KSO[64:128, 0:128], lhsT=attnT[:, cs], rhs=vnew[:], start=False, stop=True)]
                    P.mm_group(fns, reads=[bqdec, bSbf, battnT, bvnew], writes=[bKSO])
                    P.op("pe", lambda E, n=n, Sb=Sb, kdec=kdec, vnew=vnew: E.matmul(Sb[:, 0:128], lhsT=kdec[:, n, :], rhs=vnew[:], start=True, stop=True),
                         reads=[bkdec, bvnew], writes=[bSb])
                    P.op("dve", lambda E, n=n, S=S, EG=EG, Sb=Sb, Sbf=Sbf: E.scalar_tensor_tensor(out=Sbf[:], in0=S[:], scalar=EG[:, n * 64 + 63:n * 64 + 64], in1=Sb[:, 0:128],
                                                                                                  op0=ALU.mult, op1=ALU.add), reads=[bS, bEG, bSb], writes=[bSbf])
                    P.op("dve", lambda E, n=n, S=S, EG=EG, Sb=Sb: E.scalar_tensor_tensor(out=S[:], in0=S[:], scalar=EG[:, n * 64 + 63:n * 64 + 64], in1=Sb[:, 0:128],
                                                                                         op0=ALU.mult, op1=ALU.add), reads=[bS, bEG, bSb], writes=[bS])
                    P.op("act", lambda E, n=n, osb=osb, KSO=KSO: E.copy(out=osb[64:128, n, :], in_=KSO[64:128, 0:128]), reads=[bKSO], writes=[bosb])
                    advance(1)
                advance(1)
            advance(100)
            for h in range(2):
                osb, bosb = heads[h]["osb"]
                P.dma("sp", o_d[s_ * 512:(s_ + 1) * 512, h * 128:(h + 1) * 128].rearrange("(n c) d -> c n d", c=64), osb[64:128, :, :], reads=[bosb],
                      writes=[fz["obuf_of"](s_) if fz else bo])
            if fz:
                fz["after_chunk"](s_)
        if fz:
            barrier(P)
        else:
            P.finish([bo])
    return nc


def run_L1a(inp):
    nc = _get("L1a", build_L1a)
    c64, cmask, sel = _gdn_consts()
    w_in = inp["w_in_even"][0]
    conv = inp["conv_qkv"][0]
    ones = np.ones((128, 128), np.float32)
    maps = []
    for c in range(8):
        b, r = divmod(c, 4)
        cols = np.concatenate([np.arange(256 * r, 256 * r + 256), 1024 + np.arange(256 * r, 256 * r + 256), 2048 + np.arange(256 * r, 256 * r + 256)])
        maps.append({"x": np.ascontiguousarray(inp["x"][b]), "npre": np.ascontiguousarray(inp["norm_pre"][0]),
                     "w": np.ascontiguousarray(w_in[:, cols]), "wb": np.ascontiguousarray(w_in[:, 4096 + 2 * r:4096 + 2 * r + 2]),
                     "wa": np.ascontiguousarray(w_in[:, 4104 + 2 * r:4104 + 2 * r + 2]), "conv": np.ascontiguousarray(conv[:, cols]),
                     "alog": np.ascontiguousarray(inp["a_log"][0, 2 * r:2 * r + 2]), "dtb": np.ascontiguousarray(inp["dt_bias"][0, 2 * r:2 * r + 2]),
                     "ident": _IDENT, "c64": c64, "cmask": cmask, "sel": sel, "ones": ones})
    res = run_bass_kernel_spmd(nc, maps, core_ids=list(range(8)))
    S_ = inp["x"].shape[1]
    o = np.empty((2, S_, 1024), np.float32)
    for c in range(8):
        b, r = divmod(c, 4)
        o[b, :, 256 * r:256 * (r + 1)] = res.results[c]["o"]
    return o


def kernel_unfused(**inputs):
    inp = {k: np.asarray(v) for k, v in inputs.items()}
    o = run_L1a(inp)
    ys = run_L1b(inp)
    x1 = run_L2(inp, o, ys)
    out = run_L3(inp, x1)
    return out.astype(np.float32)


def build_fused():
    nc = bass.Bass("TRN2", target_bir_lowering=False)
    x_full = nc.dram_tensor("x", [8192, 1024], F32, kind="ExternalInput").ap()
    ident_d = nc.dram_tensor("ident", [128, 128], F32, kind="ExternalInput").ap()
    npre0_d = nc.dram_tensor("npre0", [1024], F32, kind="ExternalInput").ap()
    gidx_d = nc.dram_tensor("gidx", [128, 2, 17, 4], I32, kind="ExternalInput").ap()
    out_d = nc.dram_tensor("out", [2048, 1024], F32, kind="ExternalOutput").ap()
    ag_in = [nc.dram_tensor("ag_in%d" % i, [8192, 256], (F32, BF16)[i]) for i in range(2)]
    ag_out = [nc.dram_tensor("ag_out%d" % i, [4 * 8192, 256], (F32, BF16)[i]) for i in range(2)]
    x1s = nc.dram_tensor("x1s", [2176, 1024], F32)
    GROUPS = [[0, 1, 2, 3], [4, 5, 6, 7]]
    with ExitStack() as st:
        C = Ctx(nc, st); P = C.P
        csem = st.enter_context(nc.semaphore("csem"))
        bag_out = Buf("ag_out"); bx1s = Buf("x1s", multi=True); bout = Buf("out", multi=True)
        bo_ch = [Buf("o_ch%d" % k, multi=True) for k in range(16)]
        by_jt = [Buf("y_jt%d" % k, multi=True) for k in range(4)]
        ncc = [0]

        def emit_cc(which, k, inbuf, rows=512):
            P._deps("pool", [inbuf], [])
            P.streams["pool"].append(lambda E, which=which, k=k, rows=rows: E.collective_compute(
                "AllGather", ALU.bypass, replica_groups=GROUPS,
                ins=[ag_in[which].ap()[k * rows:(k + 1) * rows, :].opt()],
                outs=[ag_out[which].ap()[k * 4 * rows:(k + 1) * 4 * rows, :].opt()]).then_inc(csem))
            ncc[0] += 1

        share1 = {"x": x_full, "ident": ident_d, "npre": npre0_d}

        def after_jt(jt):
            emit_cc(1, jt, by_jt[jt], rows=2048)

        with ExitStack() as stU:
            CU = Ctx(nc, stU, P, "u_")
            uext = CU.sb("uTp", [128, 2, 16, 512], BF16)
            build_L1a(8192, fz={"nc": nc, "P": P, "pfx": "a_", "share": share1, "out": ag_in[0].ap(), "uTp": uext,
                                "obuf_of": lambda s_: bo_ch[s_], "after_chunk": lambda s_: emit_cc(0, s_, bo_ch[s_])})
            build_L1b(8192, fz={"nc": nc, "P": P, "pfx": "b_", "share": share1, "out": ag_in[1].ap(), "uTp": uext,
                                "obuf_of": lambda jt: by_jt[jt], "after_chunk": after_jt, "ybf16": True})
        gidx, bgidx = C.sb("gidx", [128, 2, 17, 4], I32)
        P.dma("sp", gidx[:], gidx_d, writes=[bgidx])
        waited = [False]

        def gather(P_, ld, bld, tile, part):
            if not waited[0]:
                P.streams["pool"].append(lambda E: E.wait_ge(csem, ncc[0]))
                P.op("pool", lambda E: E.nop(), reads=[], writes=[bag_out])
                waited[0] = True
            for i in range(4):
                P_.dma_ind("pool", ld[:, i * 256:(i + 1) * 256], ag_out[part].ap(), gidx[:, part, tile, i:i + 1], reads=[bag_out, bgidx], writes=[bld])

        share2 = {"ident": ident_d, "npre": npre0_d, "o": None, "ys": None}
        build_L2(2176, fz={"nc": nc, "P": P, "pfx": "c_", "share": share2, "out": x1s.ap(), "obuf": bx1s, "gather": gather, "ybf16": True})
        share3 = {"ident": ident_d, "x": x1s.ap()}
        build_L3(2048, fz={"nc": nc, "P": P, "pfx": "d_", "share": share3, "out": out_d, "obuf": bout, "xbuf": bx1s})
        P.finish([bout])
    return nc


def _gidx(r):
    g = np.zeros((128, 2, 17, 4), np.int32)
    p = np.arange(128)[:, None, None]
    tile = np.arange(17)[None, :, None]
    src = np.arange(4)[None, None, :]
    tok = np.clip(2048 * r - 128 + tile * 128 + p, 0, 8191)
    for part, R in ((0, 512), (1, 2048)):
        g[:, part] = ((tok // R) * 4 + src) * R + tok % R
    return g


def kernel(**inputs):
    inp = {k: np.ascontiguousarray(np.asarray(v)) for k, v in inputs.items()}
    nc = _get("fused", build_fused)
    c64, cmask, sel = _gdn_consts()
    mk, idm = _s5_consts()
    ones = np.ones((128, 128), np.float32)
    w_in = inp["w_in_even"][0]
    conv = inp["conv_qkv"][0]
    wz = np.ascontiguousarray(np.concatenate([w_in[:, 3072:4096], w_in[:, 5136:6160]], axis=1))
    maps = []
    for c in range(8):
        b, r = divmod(c, 4)
        cols = np.concatenate([np.arange(256 * r, 256 * r + 256), 1024 + np.arange(256 * r, 256 * r + 256), 2048 + np.arange(256 * r, 256 * r + 256)])
        gs = slice(16 * r, 16 * r + 16)
        xq = np.zeros((2176, 1024), np.float32)
        xq[128:] = inp["x"][b, 2048 * r:2048 * (r + 1)]
        if r > 0:
            xq[:128] = inp["x"][b, 2048 * r - 128:2048 * r]
        m = {"x": inp["x"][b], "ident": _IDENT, "npre0": inp["norm_pre"][0], "gidx": _gidx(r),
             "a_w": np.ascontiguousarray(w_in[:, cols]), "a_wb": np.ascontiguousarray(w_in[:, 4096 + 2 * r:4096 + 2 * r + 2]),
             "a_wa": np.ascontiguousarray(w_in[:, 4104 + 2 * r:4104 + 2 * r + 2]), "a_conv": np.ascontiguousarray(conv[:, cols]),
             "a_alog": np.ascontiguousarray(inp["a_log"][0, 2 * r:2 * r + 2]), "a_dtb": np.ascontiguousarray(inp["dt_bias"][0, 2 * r:2 * r + 2]),
             "a_c64": c64, "a_cmask": cmask, "a_sel": sel, "a_ones": ones,
             "a_wu": np.ascontiguousarray(w_in[:, 4112 + 256 * r:4112 + 256 * (r + 1)]),
             "b_wu": np.ascontiguousarray(w_in[:, 4112 + 256 * r:4112 + 256 * (r + 1)]),
             "b_lre": np.ascontiguousarray(inp["s5_lam_re"][0, gs]), "b_lim": np.ascontiguousarray(inp["s5_lam_im"][0, gs]),
             "b_bre": np.ascontiguousarray(inp["s5_b_re"][0, gs]), "b_bim": np.ascontiguousarray(inp["s5_b_im"][0, gs]),
             "b_cre": np.ascontiguousarray(inp["s5_c_re"][0, gs]), "b_cim": np.ascontiguousarray(inp["s5_c_im"][0, gs]),
             "b_ldt": np.ascontiguousarray(inp["s5_log_dt"][0, gs]), "b_dd": np.ascontiguousarray(inp["s5_d"][0, 256 * r:256 * (r + 1)]),
             "b_taus": TAUS, "b_mk": mk, "b_idm": idm,
             "c_x": xq, "c_wz": wz, "c_wglu": inp["w_glu"][0], "c_wout": inp["w_out_even"][0], "c_npost": inp["norm_post"][0],
             "c_gnw": inp["gdn_norm_w"][0],
             "d_win": inp["w_in_odd"][0], "d_wout": inp["w_out_odd"][0], "d_conv": inp["conv_short"][0],
             "d_npre": inp["norm_pre"][1], "d_npost": inp["norm_post"][1]}
        maps.append(m)
    res = run_bass_kernel_spmd(nc, maps, core_ids=list(range(8)))
    out = np.empty((2, 8192, 1024), np.float32)
    for c in range(8):
        b, r = divmod(c, 4)
        out[b, r * 2048:(r + 1) * 2048] = res.results[c]["out"]
    return out
```

```python
from contextlib import ExitStack
import numpy as np
import concourse.bass as bass
import concourse.mybir as mybir
from concourse.bass_utils import run_bass_kernel_spmd

F32 = mybir.dt.float32
BF16 = mybir.dt.bfloat16
AF = mybir.ActivationFunctionType
ALU = mybir.AluOpType
AX = mybir.AxisListType

NDS = 12


class Buf:
    __slots__ = ("name", "w", "r", "multi")

    def __init__(self, name, multi=False):
        self.name = name
        self.w = [] if multi else None
        self.r = []
        self.multi = multi


class Prog:
    ENG = ("pe", "act", "dve", "pool", "sp")

    def __init__(self, nc, stack):
        self.nc = nc
        self.stack = stack
        self.streams = {e: [] for e in self.ENG}
        self.cnt = {e: 0 for e in self.ENG}
        self.sem = {e: stack.enter_context(nc.semaphore("s_" + e)) for e in self.ENG}
        self.seen = {e: {} for e in self.ENG}
        self.dcnt = {e: 0 for e in self.ENG}
        self.dsem = {}
        for e in ("sp", "pool", "act"):
            self.dsem[e] = [stack.enter_context(nc.semaphore("d_%s%d" % (e, i))) for i in range(NDS)]
        self.same_engine_sync = True
        self.nwaits = 0

    def _wait(self, eng, tok):
        if tok is None:
            return
        kind = tok[0]
        if kind == "c":
            _, e2, n = tok
            if e2 == eng and (eng == "pe" or not self.same_engine_sync):
                return
            key = e2
            if self.seen[eng].get(key, 0) >= n:
                return
            self.seen[eng][key] = n
            sem = self.sem[e2]
            self.streams[eng].append(lambda E, sem=sem, n=n: E.wait_ge(sem, n))
            self.nwaits += 1
        else:
            _, q, slot, val = tok
            key = ("d", q, slot)
            if self.seen[eng].get(key, 0) >= val:
                return
            self.seen[eng][key] = val
            sem = self.dsem[q][slot]
            self.streams[eng].append(lambda E, sem=sem, val=val: E.wait_ge(sem, val))
            self.nwaits += 1

    def _deps(self, eng, reads, writes):
        for b in reads:
            if b.multi:
                for t in b.w:
                    self._wait(eng, t)
            else:
                self._wait(eng, b.w)
        for b in writes:
            if not b.multi:
                self._wait(eng, b.w)
            for t in b.r:
                self._wait(eng, t)

    def _commit(self, tok, reads, writes):
        for b in writes:
            if b.multi:
                b.w.append(tok)
            else:
                b.w = tok
            b.r = []
        for b in reads:
            if b not in writes:
                b.r.append(tok)

    def op(self, eng, fn, reads=(), writes=()):
        reads = list(reads)
        writes = list(writes)
        self._deps(eng, reads, writes)
        self.cnt[eng] += 1
        n = self.cnt[eng]
        sem = self.sem[eng]
        self.streams[eng].append(lambda E, fn=fn, sem=sem: fn(E).then_inc(sem, 1))
        tok = ("c", eng, n)
        self._commit(tok, reads, writes)
        return tok

    def mm_group(self, fns, reads=(), writes=()):
        eng = "pe"
        reads = list(reads)
        writes = list(writes)
        self._deps(eng, reads, writes)
        self.cnt[eng] += 1
        n = self.cnt[eng]
        sem = self.sem[eng]
        for fn in fns[:-1]:
            self.streams[eng].append(lambda E, fn=fn: fn(E))
        last = fns[-1]
        self.streams[eng].append(lambda E, fn=last, sem=sem: fn(E).then_inc(sem, 1))
        tok = ("c", eng, n)
        self._commit(tok, reads, writes)
        return tok

    def dma(self, q, out_ap, in_ap, reads=(), writes=()):
        reads = list(reads)
        writes = list(writes)
        self._deps(q, reads, writes)
        j = self.dcnt[q]
        self.dcnt[q] += 1
        slot = j % NDS
        val = 16 * (j // NDS + 1)
        if j >= NDS:
            self._wait(q, ("d", q, slot, val - 16))
        sem = self.dsem[q][slot]
        self.streams[q].append(
            lambda E, o=out_ap, i=in_ap, sem=sem: E.dma_start(out=o, in_=i).then_inc(sem, 16))
        tok = ("d", q, slot, val)
        self._commit(tok, reads, writes)
        return tok

    def dma_ind(self, q, out_ap, table_ap, idx_ap, reads=(), writes=()):
        reads = list(reads)
        writes = list(writes)
        self._deps(q, reads, writes)
        j = self.dcnt[q]
        self.dcnt[q] += 1
        slot = j % NDS
        val = 16 * (j // NDS + 1)
        if j >= NDS:
            self._wait(q, ("d", q, slot, val - 16))
        sem = self.dsem[q][slot]
        self.streams[q].append(
            lambda E, o=out_ap, t=table_ap, i=idx_ap, sem=sem: E.indirect_dma_start(
                out=o, out_offset=None, in_=t, in_offset=bass.IndirectOffsetOnAxis(ap=i, axis=0)).then_inc(sem, 16))
        tok = ("d", q, slot, val)
        self._commit(tok, reads, writes)
        return tok

    def finish(self, final_bufs):
        for b in final_bufs:
            for t in (b.w if b.multi else [b.w]):
                self._wait("sp", t)
        nc = self.nc
        streams = self.streams
        with nc.Block() as block:
            @block.tensor
            def _(E):
                for f in streams["pe"]:
                    f(E)

            @block.scalar
            def _(E):
                for f in streams["act"]:
                    f(E)

            @block.vector
            def _(E):
                for f in streams["dve"]:
                    f(E)

            @block.gpsimd
            def _(E):
                for f in streams["pool"]:
                    f(E)

            @block.sync
            def _(E):
                for f in streams["sp"]:
                    f(E)


class Ctx:
    def __init__(self, nc, st, P=None, pfx=""):
        self.nc = nc
        self.st = st
        self.pfx = pfx
        if P is None:
            st.enter_context(nc.allow_non_contiguous_dma(reason="small parameter loads / layout transforms"))
            P = Prog(nc, st)
        self.P = P

    def sb(self, name, shape, dt=F32):
        t = self.st.enter_context(self.nc.sbuf_tensor("sb_" + self.pfx + name, shape, dt))
        return t, Buf(name)

    def ps(self, name, shape, dt=F32):
        t = self.st.enter_context(self.nc.psum_tensor("ps_" + self.pfx + name, shape, dt))
        return t, Buf(name)


def bcast_row_load(C, name, dram_vec, n, q="sp"):
    t, b = C.sb(name, [128, n])
    C.P.dma(q, t[:], dram_vec.partition_broadcast(128), writes=[b])
    return t, b


def make_ident(C, dram_ident):
    idf, bidf = C.sb("identf", [128, 128])
    C.P.dma("sp", idf[:], dram_ident, writes=[bidf])
    idb, bidb = C.sb("identb", [128, 128], BF16)
    C.P.op("dve", lambda E: E.tensor_copy(out=idb[:], in_=idf[:]), reads=[bidf], writes=[bidb])
    return idf, bidf, idb, bidb


def rms_rstd(C, src, bsrc, ncols, junk, bjunk, ss, bss, eps=1e-6):
    P = C.P
    P.op("act", lambda E: E.activation(out=junk, in_=src, func=AF.Square, accum_out=ss[:, 0:1]),
         reads=[bsrc], writes=[bjunk, bss])
    P.op("act", lambda E: E.activation(out=ss[:, 0:1], in_=ss[:, 0:1], func=AF.Sqrt, bias=float(eps), scale=float(1.0 / ncols)),
         reads=[bss], writes=[bss])
    P.op("dve", lambda E: E.reciprocal(out=ss[:, 0:1], in_=ss[:, 0:1]), reads=[bss], writes=[bss])


def transpose8(C, src_bf, bsrc, idb, bidb, ptr, bptr, dst3, bdst, eng="act"):
    P = C.P
    fns = [(lambda E, kt=kt: E.transpose(out=ptr[:, kt * 128:(kt + 1) * 128], in_=src_bf[:, kt * 128:(kt + 1) * 128],
                                         identity=idb[:])) for kt in range(8)]
    P.mm_group(fns, reads=[bsrc, bidb], writes=[bptr])
    src3 = ptr[:].rearrange("p (k t) -> p k t", k=8)
    if eng == "act":
        P.op("act", lambda E: E.copy(out=dst3, in_=src3), reads=[bptr], writes=[bdst])
    else:
        P.op("dve", lambda E: E.tensor_copy(out=dst3, in_=src3), reads=[bptr], writes=[bdst])


def outproj_post(C, catT, bcat, nkt, wout, bwout, t, xres, bxres, npw, bnpw, pso, bpso, yo, byo, junk, bjunk, ss, bss,
                 out_dram_rows, bout):
    P = C.P
    for hh in range(2):
        fns = [(lambda E, kt=kt, hh=hh: E.matmul(pso[hh][:], lhsT=catT[:, kt, t * 128:(t + 1) * 128],
                                                 rhs=wout[:, kt, hh * 512:(hh + 1) * 512],
                                                 start=(kt == 0), stop=(kt == nkt - 1))) for kt in range(nkt)]
        P.mm_group(fns, reads=[bcat, bwout], writes=[bpso[hh]])
        P.op("act", lambda E, hh=hh: E.copy(out=yo[:, hh * 512:(hh + 1) * 512], in_=pso[hh][:]),
             reads=[bpso[hh]], writes=[byo])
    rms_rstd(C, yo[:], byo, 1024, junk[:], bjunk, ss, bss)
    P.op("dve", lambda E: E.scalar_tensor_tensor(out=yo[:], in0=yo[:], scalar=ss[:, 0:1], in1=npw[:],
                                                 op0=ALU.mult, op1=ALU.mult), reads=[byo, bss, bnpw], writes=[byo])
    P.op("dve", lambda E: E.tensor_tensor(out=yo[:], in0=yo[:], in1=xres, op=ALU.add), reads=[byo, bxres], writes=[byo])
    P.dma("sp", out_dram_rows, yo[:], reads=[byo], writes=[bout])


def load_w_bf16(C, name, dram_w, kt_n, ncols, chunk=2048, groups=None):
    w, _ = C.sb(name, [128, kt_n, ncols], BF16)
    src = dram_w.rearrange("(k p) c -> p k c", p=128)
    if groups is None:
        bw = Buf(name, multi=True)
        for kt in range(kt_n):
            for c0 in range(0, ncols, chunk):
                c1 = min(ncols, c0 + chunk)
                C.P.dma("pool", w[:, kt, c0:c1], src[:, kt, c0:c1], writes=[bw])
        return w, bw
    bws = []
    for gi, sls in enumerate(groups):
        bg = Buf("%s_g%d" % (name, gi), multi=True)
        for (c0, c1) in sls:
            for kt in range(kt_n):
                C.P.dma("pool", w[:, kt, c0:c1], src[:, kt, c0:c1], writes=[bg])
        bws.append(bg)
    return w, bws


def load_w_staged(C, name, dram_w, kt_n, ncols, stg, chunk=2048):
    w, _ = C.sb(name, [128, kt_n, ncols], BF16)
    bw = Buf(name, multi=True)
    src = dram_w.rearrange("(k p) c -> p k c", p=128)
    for kt in range(kt_n):
        for c0 in range(0, ncols, chunk):
            c1 = min(ncols, c0 + chunk)
            i = stg["i"]; stg["i"] += 1
            st_, bst_ = stg["tiles"][i % len(stg["tiles"])]
            C.P.dma("sp", st_[:, 0:c1 - c0], src[:, kt, c0:c1], writes=[bst_])
            if i % 2 == 0:
                C.P.op("act", lambda E, st_=st_, kt=kt, c0=c0, c1=c1: E.copy(out=w[:, kt, c0:c1], in_=st_[:, 0:c1 - c0]), reads=[bst_], writes=[bw])
            else:
                C.P.op("dve", lambda E, st_=st_, kt=kt, c0=c0, c1=c1: E.tensor_copy(out=w[:, kt, c0:c1], in_=st_[:, 0:c1 - c0]), reads=[bst_], writes=[bw])
    return w, bw


def build_L2(ntok=2048, fz=None):
    nc = fz["nc"] if fz else bass.Bass("TRN2", target_bir_lowering=False)
    pfx = fz["pfx"] if fz else ""

    def D(name, shape):
        if fz and name in fz["share"]:
            return fz["share"][name]
        return nc.dram_tensor(pfx + name, shape, F32, kind="ExternalInput").ap()
    x_d = D("x", [ntok, 1024]); o_d = D("o", [ntok, 1024]); ys_d = D("ys", [ntok, 1024])
    wz_d = D("wz", [1024, 2048]); wglu_d = D("wglu", [1024, 1024]); wout_d = D("wout", [2048, 1024])
    npre_d = D("npre", [1024]); npost_d = D("npost", [1024]); gnw_d = D("gnw", [128]); ident_d = D("ident", [128, 128])
    out_d = fz["out"] if fz else nc.dram_tensor("out", [ntok, 1024], F32, kind="ExternalOutput").ap()
    NT = 512
    with ExitStack() as st:
        C = Ctx(nc, st, fz["P"], pfx) if fz else Ctx(nc, st); P = C.P
        idf, bidf, idb, bidb = make_ident(C, ident_d)
        npre, bnpre = bcast_row_load(C, "npre", npre_d, 1024)
        npost, bnpost = bcast_row_load(C, "npost", npost_d, 1024)
        gnw, bgnw = bcast_row_load(C, "gnw", gnw_d, 128)
        if fz:
            stg = {"i": 0, "tiles": [C.sb("wstg%d" % i, [128, 2048]) for i in range(2)]}
            wz, bwz = load_w_staged(C, "wz", wz_d, 8, 2048, stg)
            wglu, bwglu = load_w_staged(C, "wglu", wglu_d, 8, 1024, stg)
            wout, bwout = load_w_staged(C, "wout", wout_d, 16, 1024, stg)
        else:
            wz, bwz = load_w_bf16(C, "wz", wz_d, 8, 2048)
            wglu, bwglu = load_w_bf16(C, "wglu", wglu_d, 8, 1024)
            wout, bwout = load_w_bf16(C, "wout", wout_d, 16, 1024)
        xt4, bxt4 = C.sb("xt4", [128, 4, 1024]); bxt = [Buf("xt%d" % i) for i in range(4)]
        ldo = [C.sb("ldo%d" % i, [128, 1024]) for i in range(2)]
        ldy = [C.sb("ldy%d" % i, [128, 1024], BF16 if (fz and fz.get("ybf16")) else F32) for i in range(2)]
        for (_t, _b) in ldo + ldy:
            _b.multi = True; _b.w = []
        sq, bsq = C.sb("sq", [128, 1024])
        hn, bhn = C.sb("hn", [128, 1024], BF16)
        ss, bss = C.sb("ss", [128, 1])
        ss8, bss8 = C.sb("ss8", [128, 8])
        hT, bhT = C.sb("hT", [128, 8, NT], BF16)
        oT, boT = C.sb("oT", [128, 8, NT], BF16)
        yT, byT = C.sb("yT", [128, 8, NT], BF16)
        gz, bgz = C.sb("gz", [128, 8, NT], BF16)
        sg, bsg = C.sb("sg", [128, NT], BF16)
        catT, bcat = C.sb("catT", [128, 16, NT], BF16)
        yo, byo = C.sb("yo", [128, 1024])
        ptr, bptr = C.ps("ptr", [128, 1024], BF16)
        pmm = []; bpmm = []
        for i in range(4):
            t_, b_ = C.ps("pmm%d" % i, [128, 512]); pmm.append(t_); bpmm.append(b_)
        pso = []; bpso = []
        for i in range(2):
            t_, b_ = C.ps("pso%d" % i, [128, 512]); pso.append(t_); bpso.append(b_)
        bout = fz["obuf"] if fz else Buf("out", multi=True)
        if fz:
            sts = [(0, 128)] + [(128 + i * NT, NT) for i in range((ntok - 128) // NT)]
        else:
            sts = [(i * NT, NT) for i in range(ntok // NT)]
        tile_r0 = [t0_ + t_ * 128 for (t0_, n_) in sts for t_ in range(n_ // 128)]

        def issue_loads(ti):
            r0_ = tile_r0[ti]
            lo, blo = ldo[ti % 2]; ly, bly = ldy[ti % 2]
            if fz:
                fz["gather"](P, lo, blo, r0_ // 128, 0)
                fz["gather"](P, ly, bly, r0_ // 128, 1)
            else:
                P.dma("sp", lo[:], o_d[r0_:r0_ + 128, :], writes=[blo])
                P.dma("sp", ly[:], ys_d[r0_:r0_ + 128, :], writes=[bly])

        issue_loads(0)
        for (t0, n) in sts:
            ntl = n // 128
            for t in range(ntl):
                r0 = t0 + t * 128
                ti = tile_r0.index(r0)
                if ti + 1 < len(tile_r0):
                    issue_loads(ti + 1)
                P.dma("sp", xt4[:, t, :], x_d[r0:r0 + 128, :], writes=[bxt[t]])
                rms_rstd(C, xt4[:, t, :], bxt[t], 1024, sq[:], bsq, ss, bss)
                P.op("dve", lambda E, t=t: E.scalar_tensor_tensor(out=hn[:], in0=xt4[:, t, :], scalar=ss[:, 0:1], in1=npre[:],
                                                                  op0=ALU.mult, op1=ALU.mult), reads=[bxt[t], bss, bnpre], writes=[bhn])
                transpose8(C, hn, bhn, idb, bidb, ptr, bptr, hT[:, :, t * 128:(t + 1) * 128], bhT, eng="act")
                ld, bld = ldo[ti % 2]
                P.op("act", lambda E, ld=ld: E.activation(out=sq[:], in_=ld[:], func=AF.Square), reads=[bld], writes=[bsq])
                P.op("dve", lambda E: E.tensor_reduce(out=ss8[:], in_=sq[:].rearrange("p (h d) -> p h d", h=8), axis=AX.X, op=ALU.add),
                     reads=[bsq], writes=[bss8])
                P.op("dve", lambda E: E.tensor_scalar(out=ss8[:], in0=ss8[:], scalar1=1.0 / 128, scalar2=1e-6, op0=ALU.mult, op1=ALU.add),
                     reads=[bss8], writes=[bss8])
                P.op("act", lambda E: E.activation(out=ss8[:], in_=ss8[:], func=AF.Sqrt), reads=[bss8], writes=[bss8])
                P.op("dve", lambda E: E.reciprocal(out=ss8[:], in_=ss8[:]), reads=[bss8], writes=[bss8])
                P.op("dve", lambda E, ld=ld: E.tensor_tensor(out=sq[:].rearrange("p (h d) -> p h d", h=8), in0=ld[:].rearrange("p (h d) -> p h d", h=8),
                                                      in1=ss8[:].unsqueeze(2).to_broadcast([128, 8, 128]), op=ALU.mult),
                     reads=[bld, bss8], writes=[bsq])
                P.op("dve", lambda E: E.tensor_tensor(out=hn[:].rearrange("p (h d) -> p h d", h=8), in0=sq[:].rearrange("p (h d) -> p h d", h=8),
                                                      in1=gnw[:].unsqueeze(1).to_broadcast([128, 8, 128]), op=ALU.mult),
                     reads=[bsq, bgnw], writes=[bhn])
                transpose8(C, hn, bhn, idb, bidb, ptr, bptr, oT[:, :, t * 128:(t + 1) * 128], boT, eng="act")
                ld, bld = ldy[ti % 2]
                P.op("act", lambda E, ld=ld: E.activation(out=hn[:], in_=ld[:], func=AF.Gelu_apprx_tanh), reads=[bld], writes=[bhn])
                transpose8(C, hn, bhn, idb, bidb, ptr, bptr, yT[:, :, t * 128:(t + 1) * 128], byT, eng="dve")
            for ct in range(16):
                pb = pmm[ct % 4]; bpb = bpmm[ct % 4]
                fns = [(lambda E, kt=kt, ct=ct, pb=pb, n=n: E.matmul(pb[:, 0:n], lhsT=wz[:, kt, ct * 128:(ct + 1) * 128], rhs=hT[:, kt, 0:n],
                                                                start=(kt == 0), stop=(kt == 7))) for kt in range(8)]
                P.mm_group(fns, reads=[bwz, bhT], writes=[bpb])
                if ct < 8:
                    P.op("act", lambda E, pb=pb, n=n: E.activation(out=sg[:, 0:n], in_=pb[:, 0:n], func=AF.Silu), reads=[bpb], writes=[bsg])
                    P.op("dve", lambda E, ct=ct, n=n: E.tensor_tensor(out=catT[:, ct, 0:n], in0=oT[:, ct, 0:n], in1=sg[:, 0:n], op=ALU.mult),
                         reads=[boT, bsg], writes=[bcat])
                else:
                    P.op("act", lambda E, pb=pb, ct=ct, n=n: E.activation(out=gz[:, ct - 8, 0:n], in_=pb[:, 0:n], func=AF.Silu), reads=[bpb], writes=[bgz])
            for ct in range(8):
                pb = pmm[ct % 4]; bpb = bpmm[ct % 4]
                fns = [(lambda E, kt=kt, ct=ct, pb=pb, n=n: E.matmul(pb[:, 0:n], lhsT=wglu[:, kt, ct * 128:(ct + 1) * 128], rhs=yT[:, kt, 0:n],
                                                                start=(kt == 0), stop=(kt == 7))) for kt in range(8)]
                P.mm_group(fns, reads=[bwglu, byT], writes=[bpb])
                P.op("act", lambda E, pb=pb, n=n: E.activation(out=sg[:, 0:n], in_=pb[:, 0:n], func=AF.Sigmoid), reads=[bpb], writes=[bsg])
                P.op("dve", lambda E, ct=ct, n=n: E.tensor_tensor(out=sg[:, 0:n], in0=sg[:, 0:n], in1=yT[:, ct, 0:n], op=ALU.mult), reads=[bsg, byT], writes=[bsg])
                P.op("dve", lambda E, ct=ct, n=n: E.tensor_tensor(out=catT[:, 8 + ct, 0:n], in0=sg[:, 0:n], in1=gz[:, ct, 0:n], op=ALU.mult),
                     reads=[bsg, bgz], writes=[bcat])
            for t in range(ntl):
                r0 = t0 + t * 128
                outproj_post(C, catT, bcat, 16, wout, bwout, t, xt4[:, t, :], bxt[t], npost, bnpost, pso, bpso, yo, byo, sq, bsq, ss, bss,
                             out_d[r0:r0 + 128, :], bout)
        if fz:
            barrier(P)
        else:
            P.finish([bout])
    return nc


def build_L3(ntok=2048, fz=None):
    nc = fz["nc"] if fz else bass.Bass("TRN2", target_bir_lowering=False)
    pfx = fz["pfx"] if fz else ""

    def D(name, shape):
        if fz and name in fz["share"]:
            return fz["share"][name]
        return nc.dram_tensor(pfx + name, shape, F32, kind="ExternalInput").ap()
    x_d = D("x", [ntok + 128, 1024])
    win_d = D("win", [1024, 8192]); wout_d = D("wout", [2048, 1024]); conv_d = D("conv", [3, 2048])
    npre_d = D("npre", [1024]); npost_d = D("npost", [1024]); ident_d = D("ident", [128, 128])
    out_d = fz["out"] if fz else nc.dram_tensor("out", [ntok, 1024], F32, kind="ExternalOutput").ap()
    NT = 512
    HN = 256
    with ExitStack() as st:
        C = Ctx(nc, st, fz["P"], pfx) if fz else Ctx(nc, st); P = C.P
        idf, bidf, idb, bidb = make_ident(C, ident_d)
        npre, bnpre = bcast_row_load(C, "npre", npre_d, 1024)
        npost, bnpost = bcast_row_load(C, "npost", npost_d, 1024)
        cw, bcw = C.sb("cw", [128, 3, 16])
        P.dma("sp", cw[:], conv_d.rearrange("j (c p) -> p j c", p=128), writes=[bcw])
        win, bwin_g = load_w_bf16(C, "win", win_d, 8, 8192,
                                  groups=[[(part * 2048 + cg * 512, part * 2048 + cg * 512 + 512) for part in (1, 2)] for cg in range(4)] +
                                         [[(part * 2048 + cg * 512, part * 2048 + cg * 512 + 512) for part in (0, 3)] for cg in range(4)])
        wout, bwout = load_w_bf16(C, "wout", wout_d, 16, 1024)
        xt, bxt = C.sb("xt", [128, 1024])
        hn, bhn = C.sb("hn", [128, 1024], BF16)
        ss, bss = C.sb("ss", [128, 1])
        hT, bhT = C.sb("hT", [128, 8, NT], BF16)
        y1T, by1T = C.sb("y1T", [128, 16, NT], BF16)
        pbuf, bpbuf = C.sb("pbuf", [128, HN + 2])
        phalo, bphalo = C.sb("phalo", [128, 16, 2])
        gcs, bgcs = C.sb("gcs", [128, HN])
        cv, bcv = C.sb("cv", [128, HN])
        sz, bsz = C.sb("sz", [128, HN])
        yo, byo = C.sb("yo", [128, 1024])
        P.op("dve", lambda E: E.memset(phalo[:], 0.0), writes=[bphalo])
        ptr, bptr = C.ps("ptr", [128, 1024], BF16)
        GB = [C.ps("g%d" % i, [128, 512]) for i in range(7)]
        pso = [GB[0][0], GB[1][0]]; bpso = [GB[0][1], GB[1][1]]
        bout = fz["obuf"] if fz else Buf("out", multi=True)
        sts = [(0, 128)] + [(128 + i * NT, NT) for i in range(ntok // NT)]
        for (t0, n) in sts:
            ntl = n // 128
            for t in range(ntl):
                r0 = t0 + t * 128
                P.dma("sp", xt[:], x_d[r0:r0 + 128, :], reads=([fz["xbuf"]] if fz else []), writes=[bxt])
                rms_rstd(C, xt[:], bxt, 1024, hn[:], bhn, ss, bss)
                P.op("dve", lambda E: E.scalar_tensor_tensor(out=hn[:], in0=xt[:], scalar=ss[:, 0:1], in1=npre[:],
                                                             op0=ALU.mult, op1=ALU.mult), reads=[bxt, bss, bnpre], writes=[bhn])
                transpose8(C, hn, bhn, idb, bidb, ptr, bptr, hT[:, :, t * 128:(t + 1) * 128], bhT, eng="act")
            for ct in range(16):
                sel_ = [GB[3 * (ct % 2) + 0], GB[3 * (ct % 2) + 1], GB[3 * (ct % 2) + 2], GB[6]]
                pmm = [x_[0] for x_ in sel_]; bpmm = [x_[1] for x_ in sel_]
                for part in ((1, 2) if t0 == 0 else range(4)):
                    col0 = (part * 16 + ct) * 128
                    pb = pmm[part]
                    fns = [(lambda E, n=n, kt=kt, col0=col0, pb=pb: E.matmul(pb[:, 0:n], lhsT=win[:, kt, col0:col0 + 128], rhs=hT[:, kt, 0:n],
                                                                        start=(kt == 0), stop=(kt == 7))) for kt in range(8)]
                    P.mm_group(fns, reads=[bwin_g[(0 if part in (1, 2) else 4) + ct // 4], bhT], writes=[bpmm[part]])
                for h0 in range(0, n, HN):
                    nn = min(HN, n - h0)
                    P.op("act", lambda E, nn=nn, h0=h0, pmm=pmm: E.copy(out=gcs[:, 0:nn], in_=pmm[1][:, h0:h0 + nn]), reads=[bpmm[1]], writes=[bgcs])
                    P.op("act", lambda E, ct=ct: E.copy(out=pbuf[:, 0:2], in_=phalo[:, ct, :]), reads=[bphalo], writes=[bpbuf])
                    P.op("dve", lambda E, nn=nn, h0=h0, pmm=pmm: E.tensor_tensor(out=pbuf[:, 2:2 + nn], in0=gcs[:, 0:nn], in1=pmm[2][:, h0:h0 + nn], op=ALU.mult),
                         reads=[bgcs, bpmm[2]], writes=[bpbuf])
                    P.op("act", lambda E, nn=nn, ct=ct: E.copy(out=phalo[:, ct, :], in_=pbuf[:, nn:nn + 2]), reads=[bpbuf], writes=[bphalo])
                    if t0 == 0:
                        continue
                    P.op("dve", lambda E, nn=nn, ct=ct: E.tensor_scalar(out=cv[:, 0:nn], in0=pbuf[:, 0:nn], scalar1=cw[:, 0, ct:ct + 1], scalar2=None, op0=ALU.mult),
                         reads=[bpbuf, bcw], writes=[bcv])
                    P.op("dve", lambda E, nn=nn, ct=ct: E.scalar_tensor_tensor(out=cv[:, 0:nn], in0=pbuf[:, 1:1 + nn], scalar=cw[:, 1, ct:ct + 1], in1=cv[:, 0:nn],
                                                                               op0=ALU.mult, op1=ALU.add), reads=[bpbuf, bcw, bcv], writes=[bcv])
                    P.op("dve", lambda E, nn=nn, ct=ct: E.scalar_tensor_tensor(out=cv[:, 0:nn], in0=pbuf[:, 2:2 + nn], scalar=cw[:, 2, ct:ct + 1], in1=cv[:, 0:nn],
                                                                               op0=ALU.mult, op1=ALU.add), reads=[bpbuf, bcw, bcv], writes=[bcv])
                    P.op("dve", lambda E, nn=nn, h0=h0, pmm=pmm: E.tensor_tensor(out=cv[:, 0:nn], in0=cv[:, 0:nn], in1=pmm[0][:, h0:h0 + nn], op=ALU.mult),
                         reads=[bcv, bpmm[0]], writes=[bcv])
                    P.op("act", lambda E, nn=nn, h0=h0, pmm=pmm: E.activation(out=sz[:, 0:nn], in_=pmm[3][:, h0:h0 + nn], func=AF.Silu), reads=[bpmm[3]], writes=[bsz])
                    P.op("dve", lambda E, nn=nn, h0=h0, ct=ct: E.tensor_tensor(out=y1T[:, ct, h0:h0 + nn], in0=cv[:, 0:nn], in1=sz[:, 0:nn], op=ALU.mult),
                         reads=[bcv, bsz], writes=[by1T])
            if t0 == 0:
                continue
            for t in range(ntl):
                r0 = t0 + t * 128
                P.dma("sp", xt[:], x_d[r0:r0 + 128, :], reads=([fz["xbuf"]] if fz else []), writes=[bxt])
                outproj_post(C, y1T, by1T, 16, wout, bwout, t, xt[:], bxt, npost, bnpost, pso, bpso, yo, byo, hn, bhn, ss, bss,
                             out_d[r0 - 128:r0, :], bout)
        if fz:
            barrier(P)
        else:
            P.finish([bout])
    return nc


_IDENT = np.eye(128, dtype=np.float32)
_CACHE = {}


def _get(name, fn):
    if name not in _CACHE:
        _CACHE[name] = fn()
    return _CACHE[name]


def run_L2(inp, o_full, ys_full):
    nc = _get("L2", build_L2)
    w_in = inp["w_in_even"][0]
    wz = np.ascontiguousarray(np.concatenate([w_in[:, 3072:4096], w_in[:, 5136:6160]], axis=1))
    maps = []
    for c in range(8):
        b, r = divmod(c, 4)
        sl = slice(r * 2048, (r + 1) * 2048)
        maps.append({"x": np.ascontiguousarray(inp["x"][b, sl]), "o": np.ascontiguousarray(o_full[b, sl]),
                     "ys": np.ascontiguousarray(ys_full[b, sl]), "wz": wz, "wglu": np.ascontiguousarray(inp["w_glu"][0]),
                     "wout": np.ascontiguousarray(inp["w_out_even"][0]), "npre": np.ascontiguousarray(inp["norm_pre"][0]),
                     "npost": np.ascontiguousarray(inp["norm_post"][0]), "gnw": np.ascontiguousarray(inp["gdn_norm_w"][0]),
                     "ident": _IDENT})
    res = run_bass_kernel_spmd(nc, maps, core_ids=list(range(8)))
    x1 = np.empty((2, 8192, 1024), np.float32)
    for c in range(8):
        b, r = divmod(c, 4)
        x1[b, r * 2048:(r + 1) * 2048] = res.results[c]["out"]
    return x1


def run_L3(inp, x1):
    nc = _get("L3", build_L3)
    maps = []
    for c in range(8):
        b, r = divmod(c, 4)
        xh = np.zeros((2048 + 128, 1024), np.float32)
        xh[128:] = x1[b, r * 2048:(r + 1) * 2048]
        if r > 0:
            xh[:128] = x1[b, r * 2048 - 128:r * 2048]
        maps.append({"x": xh, "win": np.ascontiguousarray(inp["w_in_odd"][0]), "wout": np.ascontiguousarray(inp["w_out_odd"][0]),
                     "conv": np.ascontiguousarray(inp["conv_short"][0]), "npre": np.ascontiguousarray(inp["norm_pre"][1]),
                     "npost": np.ascontiguousarray(inp["norm_post"][1]), "ident": _IDENT})
    res = run_bass_kernel_spmd(nc, maps, core_ids=list(range(8)))
    out = np.empty((2, 8192, 1024), np.float32)
    for c in range(8):
        b, r = divmod(c, 4)
        out[b, r * 2048:(r + 1) * 2048] = res.results[c]["out"]
    return out


I32 = mybir.dt.int32
TAUS = np.array(list(range(17)) + [32, 64, 128, 256, 512, 1024, 2048, 4096] + list(range(15, -1, -1)), np.float32)
NTAU = len(TAUS)


def _s5_consts():
    mk = np.zeros((128, 2, 16, 16), np.float32)
    idm = np.zeros((128, 2, 16, 16), np.float32)
    for kt2 in range(2):
        for sp in range(8):
            s = kt2 * 8 + sp
            for h in range(16):
                mk[sp * 16 + h, kt2, s:, :] = 1.0
                idm[sp * 16 + h, kt2, s, h] = 1.0
    return mk.reshape(128, 2, 256), idm.reshape(128, 2, 256)


def barrier(P):
    for e in P.ENG:
        for e2 in P.ENG:
            if P.cnt[e2] > 0:
                P._wait(e, ("c", e2, P.cnt[e2]))
        for q in P.dsem:
            j1 = P.dcnt[q]
            for j in range(max(0, j1 - NDS), j1):
                P._wait(e, ("d", q, j % NDS, 16 * (j // NDS + 1)))


def build_L1b(S=8192, fz=None):
    nc = fz["nc"] if fz else bass.Bass("TRN2", target_bir_lowering=False)
    pfx = fz["pfx"] if fz else ""

    def D(name, shape):
        if fz and name in fz["share"]:
            return fz["share"][name]
        return nc.dram_tensor(pfx + name, shape, F32, kind="ExternalInput").ap()
    x_d = D("x", [S, 1024]); npre_d = D("npre", [1024]); wu_d = D("wu", [1024, 256])
    lre_d = D("lre", [16, 64]); lim_d = D("lim", [16, 64]); bre_d = D("bre", [16, 64, 16]); bim_d = D("bim", [16, 64, 16])
    cre_d = D("cre", [16, 16, 64]); cim_d = D("cim", [16, 16, 64]); ldt_d = D("ldt", [16]); dd_d = D("dd", [256])
    taus_d = D("taus", [NTAU]); mk_d = D("mk", [128, 2, 256]); idm_d = D("idm", [128, 2, 256]); ident_d = D("ident", [128, 128])
    ys_d = fz["out"] if fz else nc.dram_tensor("ys", [S, 256], F32, kind="ExternalOutput").ap()
    NCH = S // 16
    NST = S // 512
    with ExitStack() as st:
        C = Ctx(nc, st, fz["P"], pfx) if fz else Ctx(nc, st); P = C.P
        idf, bidf, idb, bidb = make_ident(C, ident_d)
        ptr, bptr = C.ps("ptr", [128, 1024], BF16)
        py, bpy = C.ps("py", [128, 1024])
        G = []; bG = []
        for i in range(4):
            t_, b_ = C.ps("g%d" % i, [128, 512]); G.append(t_); bG.append(b_)
        U, bU = C.sb("U", [128, 2, 16, NCH], BF16)
        with ExitStack() as st2:
            C2 = Ctx(nc, st2, P, C.pfx)
            ext = fz.get("uTp") if fz else None
            if ext:
                uTp, buTp = ext
            else:
                uTp, buTp = C2.sb("uTp", [128, 2, 16, NCH], BF16)
            with ExitStack() as st1:
                C1 = Ctx(nc, st1, P, C.pfx)
                npre, bnpre = bcast_row_load(C1, "npre", npre_d, 1024)
                wu, bwu = load_w_bf16(C1, "wu", wu_d, 8, 256)
                xt, bxt = C1.sb("xt", [128, 1024])
                sq, bsq = C1.sb("sq", [128, 1024])
                hn, bhn = C1.sb("hn", [128, 1024], BF16)
                ss, bss = C1.sb("ss", [128, 1])
                hT, bhT = C1.sb("hT", [128, 8, 512], BF16)
                for s_ in range(0 if ext else NST):
                    for t in range(4):
                        r0 = s_ * 512 + t * 128
                        P.dma("sp", xt[:], x_d[r0:r0 + 128, :], writes=[bxt])
                        rms_rstd(C1, xt[:], bxt, 1024, sq[:], bsq, ss, bss)
                        P.op("dve", lambda E: E.scalar_tensor_tensor(out=hn[:], in0=xt[:], scalar=ss[:, 0:1], in1=npre[:],
                                                                     op0=ALU.mult, op1=ALU.mult), reads=[bxt, bss, bnpre], writes=[bhn])
                        transpose8(C1, hn, bhn, idb, bidb, ptr, bptr, hT[:, :, t * 128:(t + 1) * 128], bhT, eng="act")
                    for blk in range(2):
                        pb = G[blk]
                        fns = [(lambda E, kt=kt, blk=blk, pb=pb: E.matmul(
                            pb[:].rearrange("p (s n) -> p s n", s=16), lhsT=wu[:, kt, blk * 128:(blk + 1) * 128],
                            rhs=hT[:, kt, :].rearrange("p (n s) -> p s n", s=16), start=(kt == 0), stop=(kt == 7))) for kt in range(8)]
                        P.mm_group(fns, reads=[bwu, bhT], writes=[bG[blk]])
                        P.op("act" if blk == 0 else "dve",
                             (lambda E, blk=blk, pb=pb, s_=s_: E.copy(out=uTp[:, blk, :, 32 * s_:32 * s_ + 32], in_=pb[:].rearrange("p (s n) -> p s n", s=16)))
                             if blk == 0 else
                             (lambda E, blk=blk, pb=pb, s_=s_: E.tensor_copy(out=uTp[:, blk, :, 32 * s_:32 * s_ + 32], in_=pb[:].rearrange("p (s n) -> p s n", s=16))),
                             reads=[bG[blk]], writes=[buTp])
                barrier(P)
            ud2 = nc.dram_tensor(pfx + "ud2", [16, 2, 8, 16, NCH], BF16)
            bud2 = Buf("ud2", multi=True)
            bU.multi = True; bU.w = []
            for g in range(16):
                P.dma("sp", ud2.ap()[g].rearrange("k sp h n -> h (k sp) n"),
                      uTp[(g % 8) * 16:(g % 8 + 1) * 16, g // 8, :, :], reads=[buTp], writes=[bud2])
            for g in range(16):
                P.dma("sp", U[:, :, g, :], ud2.ap()[g].rearrange("k sp h n -> (sp h) k n"), reads=[bud2], writes=[bU])
            barrier(P)
        lre, blre = C.sb("lre", [128, 8]); lim, blim = C.sb("lim", [128, 8]); ldt, bldt = C.sb("ldt", [128, 8])
        TAU, bTAU = bcast_row_load(C, "TAU", taus_d, NTAU)
        Er, bEr = C.sb("Er", [128, 8, NTAU]); Ei, bEi = C.sb("Ei", [128, 8, NTAU]); NEi, bNEi = C.sb("NEi", [128, 8, NTAU])
        Hr, bHr = C.sb("Hr", [128, 8, 17, 16]); nHi, bnHi = C.sb("nHi", [128, 8, 17, 16])
        WbT, bWbT = C.sb("WbT", [128, 2, 8, 2, 128], BF16)
        Toep, bToep = C.sb("Toep", [128, 2, 16, 256], BF16)
        with ExitStack() as st3:
            C3 = Ctx(nc, st3, P, C.pfx)
            Br, bBr = C3.sb("Br", [128, 8, 16]); Bi, bBi = C3.sb("Bi", [128, 8, 16])
            Cr, bCr = C3.sb("Cr", [128, 8, 16]); Ci, bCi = C3.sb("Ci", [128, 8, 16])
            dcol, bdcol = C3.sb("dcol", [128, 16])
            MK, bMK = C3.sb("MK", [128, 2, 256]); IDM, bIDM = C3.sb("IDM", [128, 2, 256])
            P.dma("sp", MK[:], mk_d, writes=[bMK]); P.dma("sp", IDM[:], idm_d, writes=[bIDM])
            for _b in (blre, blim, bldt, bBr, bBi, bCr, bCi, bdcol):
                _b.multi = True; _b.w = []
            for two in range(2):
                hs = slice(64 * two, 64 * two + 64)
                P.dma("sp", lre[hs, :], lre_d.rearrange("(gp two) p -> two p gp", two=2)[two], writes=[blre])
                P.dma("sp", lim[hs, :], lim_d.rearrange("(gp two) p -> two p gp", two=2)[two], writes=[blim])
                P.dma("sp", ldt[hs, :], ldt_d.rearrange("(gp two) -> two gp", two=2)[two].partition_broadcast(64), writes=[bldt])
                P.dma("sp", Br[hs], bre_d.rearrange("(gp two) p h -> two p gp h", two=2)[two], writes=[bBr])
                P.dma("sp", Bi[hs], bim_d.rearrange("(gp two) p h -> two p gp h", two=2)[two], writes=[bBi])
                for gp in range(8):
                    P.dma("sp", Cr[hs, gp, :], cre_d[2 * gp + two].rearrange("h p -> p h"), writes=[bCr])
                    P.dma("sp", Ci[hs, gp, :], cim_d[2 * gp + two].rearrange("h p -> p h"), writes=[bCi])
            for sp in range(8):
                P.dma("sp", dcol[sp * 16:(sp + 1) * 16, :], dd_d.rearrange("(g h) -> h g", h=16), writes=[bdcol])
            sm = {}
            for nm in ("dt", "lr", "lrdt", "th", "den", "nr", "fre", "fim", "t8a", "t8b"):
                sm[nm] = C3.sb("sm_" + nm, [128, 8])
            T41 = {}
            for nm in ("ARG", "MARG", "MAG", "MAGN", "SIN", "COS", "ErN", "EiN", "rt", "rk"):
                T41[nm] = C3.sb("t41_" + nm, [128, 8, NTAU])
            rki, brki = C3.sb("rki", [128, 8, NTAU], I32)

            def tt(eng, out, bo, a, ba, b, bb_, op):
                P.op(eng, lambda E: E.tensor_tensor(out=out, in0=a, in1=b, op=op), reads=[ba, bb_], writes=[bo])

            dt, bdt = sm["dt"]; lr, blr = sm["lr"]; lrdt, blrdt = sm["lrdt"]; th, bth = sm["th"]
            P.op("act", lambda E: E.activation(out=dt[:], in_=ldt[:], func=AF.Exp), reads=[bldt], writes=[bdt])
            P.op("dve", lambda E: E.tensor_scalar(out=lr[:], in0=lre[:], scalar1=-1e-4, scalar2=None, op0=ALU.min), reads=[blre], writes=[blr])
            tt("dve", lrdt[:], blrdt, lr[:], blr, dt[:], bdt, ALU.mult)
            tt("dve", th[:], bth, lim[:], blim, dt[:], bdt, ALU.mult)
            ARG, bARG = T41["ARG"]; MARG, bMARG = T41["MARG"]; MAG, bMAG = T41["MAG"]; MAGN, bMAGN = T41["MAGN"]
            SIN, bSIN = T41["SIN"]; COS, bCOS = T41["COS"]; ErN, bErN = T41["ErN"]; EiN, bEiN = T41["EiN"]
            rt, brt = T41["rt"]; rk, brk = T41["rk"]
            tb = TAU[:].unsqueeze(1).to_broadcast([128, 8, NTAU])
            tt("dve", ARG[:], bARG, th[:].unsqueeze(2).to_broadcast([128, 8, NTAU]), bth, tb, bTAU, ALU.mult)
            tt("dve", MARG[:], bMARG, lrdt[:].unsqueeze(2).to_broadcast([128, 8, NTAU]), blrdt, tb, bTAU, ALU.mult)
            P.op("act", lambda E: E.activation(out=MAG[:], in_=MARG[:], func=AF.Exp), reads=[bMARG], writes=[bMAG])
            P.op("act", lambda E: E.activation(out=MAGN[:, :, 0:17], in_=MARG[:, :, 0:17], func=AF.Exp, scale=-1.0), reads=[bMARG], writes=[bMAGN])

            def sin_of(dst, bdst, shift):
                P.op("dve", lambda E: E.tensor_scalar(out=rt[:], in0=ARG[:], scalar1=float(shift), scalar2=None, op0=ALU.add), reads=[bARG], writes=[brt])
                P.op("dve", lambda E: E.tensor_scalar(out=rki[:], in0=rt[:], scalar1=float(1.0 / (2 * np.pi)), scalar2=None, op0=ALU.mult), reads=[brt], writes=[brki])
                P.op("dve", lambda E: E.tensor_copy(out=rk[:], in_=rki[:]), reads=[brki], writes=[brk])
                P.op("dve", lambda E: E.scalar_tensor_tensor(out=rt[:], in0=rk[:], scalar=float(-2 * np.pi), in1=rt[:], op0=ALU.mult, op1=ALU.add),
                     reads=[brk, brt], writes=[brt])
                P.op("dve", lambda E: E.tensor_scalar(out=rt[:], in0=rt[:], scalar1=-3.14159, scalar2=3.14159, op0=ALU.max, op1=ALU.min), reads=[brt], writes=[brt])
                P.op("act", lambda E: E.activation(out=dst[:], in_=rt[:], func=AF.Sin), reads=[brt], writes=[bdst])

            sin_of(SIN, bSIN, 0.0)
            sin_of(COS, bCOS, np.pi / 2)
            tt("dve", Er[:], bEr, MAG[:], bMAG, COS[:], bCOS, ALU.mult)
            tt("dve", Ei[:], bEi, MAG[:], bMAG, SIN[:], bSIN, ALU.mult)
            P.op("dve", lambda E: E.tensor_scalar(out=NEi[:], in0=Ei[:], scalar1=-1.0, scalar2=None, op0=ALU.mult), reads=[bEi], writes=[bNEi])
            tt("dve", ErN[:, :, 0:17], bErN, MAGN[:, :, 0:17], bMAGN, COS[:, :, 0:17], bCOS, ALU.mult)
            tt("dve", EiN[:, :, 0:17], bEiN, MAGN[:, :, 0:17], bMAGN, SIN[:, :, 0:17], bSIN, ALU.mult)
            P.op("dve", lambda E: E.tensor_scalar(out=EiN[:, :, 0:17], in0=EiN[:, :, 0:17], scalar1=-1.0, scalar2=None, op0=ALU.mult), reads=[bEiN], writes=[bEiN])
            den, bden = sm["den"]; nr, bnr = sm["nr"]; fre, bfre = sm["fre"]; fim, bfim = sm["fim"]; t8a, bt8a = sm["t8a"]; t8b, bt8b = sm["t8b"]
            tt("dve", den[:], bden, lr[:], blr, lr[:], blr, ALU.mult)
            tt("dve", t8a[:], bt8a, lim[:], blim, lim[:], blim, ALU.mult)
            tt("dve", den[:], bden, den[:], bden, t8a[:], bt8a, ALU.add)
            P.op("dve", lambda E: E.reciprocal(out=den[:], in_=den[:]), reads=[bden], writes=[bden])
            P.op("dve", lambda E: E.tensor_scalar(out=nr[:], in0=Er[:, :, 1], scalar1=-1.0, scalar2=None, op0=ALU.add), reads=[bEr], writes=[bnr])
            tt("dve", fre[:], bfre, nr[:], bnr, lr[:], blr, ALU.mult)
            tt("dve", t8a[:], bt8a, Ei[:, :, 1], bEi, lim[:], blim, ALU.mult)
            tt("dve", fre[:], bfre, fre[:], bfre, t8a[:], bt8a, ALU.add)
            tt("dve", fre[:], bfre, fre[:], bfre, den[:], bden, ALU.mult)
            tt("dve", fim[:], bfim, Ei[:, :, 1], bEi, lr[:], blr, ALU.mult)
            tt("dve", t8b[:], bt8b, nr[:], bnr, lim[:], blim, ALU.mult)
            tt("dve", fim[:], bfim, fim[:], bfim, t8b[:], bt8b, ALU.subtract)
            tt("dve", fim[:], bfim, fim[:], bfim, den[:], bden, ALU.mult)

            def cmul(outr, boutr, outi, bouti, ar, bar, ai, bai, br_, bbr_, bi_, bbi_, tmp, btmp):
                tt("dve", outr, boutr, ar, bar, br_, bbr_, ALU.mult)
                tt("dve", tmp, btmp, ai, bai, bi_, bbi_, ALU.mult)
                tt("dve", outr, boutr, outr, boutr, tmp, btmp, ALU.subtract)
                tt("dve", outi, bouti, ar, bar, bi_, bbi_, ALU.mult)
                tt("dve", tmp, btmp, ai, bai, br_, bbr_, ALU.mult)
                tt("dve", outi, bouti, outi, bouti, tmp, btmp, ALU.add)

            bbr, bbbr = C3.sb("bbr", [128, 8, 16]); bbi, bbbi = C3.sb("bbi", [128, 8, 16]); tmp16, btmp16 = C3.sb("tmp16", [128, 8, 16])
            fb = lambda t_: t_[:].unsqueeze(2).to_broadcast([128, 8, 16])
            cmul(bbr[:], bbbr, bbi[:], bbbi, fb(fre), bfre, fb(fim), bfim, Br[:], bBr, Bi[:], bBi, tmp16[:], btmp16)
            Gr, bGr = C3.sb("Gr", [128, 8, 16, 16]); Gi, bGi = C3.sb("Gi", [128, 8, 16, 16])
            WPr, bWPr = C3.sb("WPr", [128, 8, 16, 16]); WPi, bWPi = C3.sb("WPi", [128, 8, 16, 16])
            Hi, bHi = C3.sb("Hi", [128, 8, 17, 16]); tmpH, btmpH = C3.sb("tmpH", [128, 8, 17, 16])
            eb = lambda t_, j0, j1: t_[:, :, j0:j1].unsqueeze(3).to_broadcast([128, 8, j1 - j0, 16])
            vb = lambda t_, n_: t_[:].unsqueeze(2).to_broadcast([128, 8, n_, 16])
            cmul(Gr[:], bGr, Gi[:], bGi, eb(ErN, 0, 16), bErN, eb(EiN, 0, 16), bEiN, vb(bbr, 16), bbbr, vb(bbi, 16), bbbi, tmpH[:, :, 0:16, :], btmpH)
            cmul(WPr[:], bWPr, WPi[:], bWPi, eb(Er, 25, 41), bEr, eb(Ei, 25, 41), bEi, vb(bbr, 16), bbbr, vb(bbi, 16), bbbi, tmpH[:, :, 0:16, :], btmpH)
            cmul(Hr[:], bHr, Hi[:], bHi, eb(Er, 0, 17), bEr, eb(Ei, 0, 17), bEi, vb(Cr, 17), bCr, vb(Ci, 17), bCi, tmpH[:], btmpH)
            P.op("dve", lambda E: E.tensor_scalar(out=nHi[:], in0=Hi[:], scalar1=-1.0, scalar2=None, op0=ALU.mult), reads=[bHi], writes=[bnHi])
            for gp in range(8):
                for kt2 in range(2):
                    for c, (WP_, bWP_) in enumerate(((WPr, bWPr), (WPi, bWPi))):
                        P.op("pe", lambda E, gp=gp, kt2=kt2, WP_=WP_: E.transpose(
                            out=G[2][:, 0:128], in_=WP_[:, gp, kt2 * 8:(kt2 + 1) * 8, :].rearrange("p s h -> p (s h)"), identity=idf[:]),
                            reads=[bWP_, bidf], writes=[bG[2]])
                        P.op("act", lambda E, gp=gp, kt2=kt2, c=c: E.copy(out=WbT[:, kt2, gp, c, :], in_=G[2][:, 0:128]), reads=[bG[2]], writes=[bWbT])
            tmpT, btmpT = C3.sb("tmpT", [128, 256])
            for g in range(16):
                gp = g // 2; hs = slice(64 * (g % 2), 64 * (g % 2) + 64)
                for kt2 in range(2):
                    fns = [
                        lambda E, gp=gp, hs=hs, kt2=kt2: E.matmul(G[3][:, 0:256], lhsT=Gr[hs, gp, kt2 * 8:(kt2 + 1) * 8, :].rearrange("p s h -> p (s h)"),
                                                                  rhs=Hr[hs, gp, 0:16, :].rearrange("p t h -> p (t h)"), start=True, stop=False),
                        lambda E, gp=gp, hs=hs, kt2=kt2: E.matmul(G[3][:, 0:256], lhsT=Gi[hs, gp, kt2 * 8:(kt2 + 1) * 8, :].rearrange("p s h -> p (s h)"),
                                                                  rhs=nHi[hs, gp, 0:16, :].rearrange("p t h -> p (t h)"), start=False, stop=True)]
                    P.mm_group(fns, reads=[bGr, bGi, bHr, bnHi], writes=[bG[3]])
                    P.op("dve", lambda E, kt2=kt2: E.tensor_tensor(out=tmpT[:], in0=G[3][:, 0:256], in1=MK[:, kt2, :], op=ALU.mult),
                         reads=[bG[3], bMK], writes=[btmpT])
                    P.op("dve", lambda E, kt2=kt2, g=g: E.scalar_tensor_tensor(out=Toep[:, kt2, g, :], in0=IDM[:, kt2, :], scalar=dcol[:, g:g + 1], in1=tmpT[:],
                                                                               op0=ALU.mult, op1=ALU.add), reads=[bIDM, bdcol, btmpT], writes=[bToep])
            barrier(P)
        X = {}
        for bufn in ("A", "B"):
            for c in ("re", "im"):
                X[(bufn, c)] = (C.sb("X%s%s" % (bufn, c), [128, 8, NCH + 1])[0], [Buf("X%s%s%d" % (bufn, c, gp)) for gp in range(8)])
        Ysb, bYsb = C.sb("Ysb", [128, 16, 256], BF16 if (fz and fz.get("ybf16")) else F32)
        for key in X:
            t_, bl = X[key]
            P.op("dve", lambda E, t_=t_: E.memset(t_[:, :, 0:1], 0.0), writes=bl)
        for gp in range(8):
            for c, cn in enumerate(("re", "im")):
                px = G[c]
                fns = []
                for two in range(2):
                    g = 2 * gp + two
                    for kt2 in range(2):
                        fns.append(lambda E, two=two, g=g, kt2=kt2, gp=gp, c=c, px=px: E.matmul(
                            px[64 * two:64 * two + 64, :], lhsT=WbT[:, kt2, gp, c, 64 * two:64 * two + 64], rhs=U[:, kt2, g, :],
                            start=(kt2 == 0), stop=(kt2 == 1)))
                P.mm_group(fns, reads=[bWbT, bU], writes=[bG[c]])
                xt_, xb_ = X[("A", cn)]
                P.op("act", lambda E, xt_=xt_, gp=gp, px=px: E.copy(out=xt_[:, gp, 1:NCH + 1], in_=px[:]), reads=[bG[c]], writes=[xb_[gp]])
        for k in range(9):
            d = 1 << k
            j = 16 if k == 0 else 16 + k
            src, dst = ("A", "B") if k % 2 == 0 else ("B", "A")
            sre, bsre = X[(src, "re")]; sim, bsim = X[(src, "im")]
            dre, bdre = X[(dst, "re")]; dim_, bdim = X[(dst, "im")]
            P.op("dve", lambda E, dre=dre, sre=sre, d=d: E.tensor_copy(out=dre[:, :, 1:1 + d], in_=sre[:, :, 1:1 + d]), reads=bsre, writes=bdre)
            P.op("pool", lambda E, dim_=dim_, sim=sim, d=d: E.tensor_copy(out=dim_[:, :, 1:1 + d], in_=sim[:, :, 1:1 + d]), reads=bsim, writes=bdim)
            for gp in range(8):
                lo = slice(1, NCH + 1 - d); hi = slice(1 + d, NCH + 1)
                P.op("dve", lambda E, gp=gp, j=j, dre=dre, sre=sre, lo=lo, hi=hi: E.scalar_tensor_tensor(
                    out=dre[:, gp, hi], in0=sre[:, gp, lo], scalar=Er[:, gp, j:j + 1], in1=sre[:, gp, hi], op0=ALU.mult, op1=ALU.add),
                    reads=[bsre[gp], bEr], writes=[bdre[gp]])
                P.op("dve", lambda E, gp=gp, j=j, dre=dre, sim=sim, lo=lo, hi=hi: E.scalar_tensor_tensor(
                    out=dre[:, gp, hi], in0=sim[:, gp, lo], scalar=NEi[:, gp, j:j + 1], in1=dre[:, gp, hi], op0=ALU.mult, op1=ALU.add),
                    reads=[bsim[gp], bNEi, bdre[gp]], writes=[bdre[gp]])
                P.op("dve", lambda E, gp=gp, j=j, dim_=dim_, sim=sim, lo=lo, hi=hi: E.scalar_tensor_tensor(
                    out=dim_[:, gp, hi], in0=sim[:, gp, lo], scalar=Er[:, gp, j:j + 1], in1=sim[:, gp, hi], op0=ALU.mult, op1=ALU.add),
                    reads=[bsim[gp], bEr], writes=[bdim[gp]])
                P.op("dve", lambda E, gp=gp, j=j, dim_=dim_, sre=sre, lo=lo, hi=hi: E.scalar_tensor_tensor(
                    out=dim_[:, gp, hi], in0=sre[:, gp, lo], scalar=Ei[:, gp, j:j + 1], in1=dim_[:, gp, hi], op0=ALU.mult, op1=ALU.add),
                    reads=[bsre[gp], bEi, bdim[gp]], writes=[bdim[gp]])
        fre_, bfre_ = X[("B", "re")]; fim_, bfim_ = X[("B", "im")]
        bys = None if fz else Buf("ys", multi=True)
        ysv = ys_d.rearrange("(n t) c -> n t c", t=16)
        for jt in range(NCH // 128):
            for gq in range(4):
                fns = []
                for gi in range(4):
                    g = 4 * gq + gi; gp = g // 2; hs = slice(64 * (g % 2), 64 * (g % 2) + 64)
                    o_ = (gi * 256, (gi + 1) * 256)
                    for kt2 in range(2):
                        fns.append(lambda E, o_=o_, g=g, kt2=kt2, jt=jt: E.matmul(
                            py[:, o_[0]:o_[1]], lhsT=U[:, kt2, g, jt * 128:(jt + 1) * 128], rhs=Toep[:, kt2, g, :], start=(kt2 == 0), stop=False))
                    fns.append(lambda E, o_=o_, gp=gp, hs=hs, jt=jt: E.matmul(
                        py[:, o_[0]:o_[1]], lhsT=fre_[hs, gp, jt * 128:(jt + 1) * 128], rhs=Hr[hs, gp, 1:17, :].rearrange("p t h -> p (t h)"),
                        start=False, stop=False))
                    fns.append(lambda E, o_=o_, gp=gp, hs=hs, jt=jt: E.matmul(
                        py[:, o_[0]:o_[1]], lhsT=fim_[hs, gp, jt * 128:(jt + 1) * 128], rhs=nHi[hs, gp, 1:17, :].rearrange("p t h -> p (t h)"),
                        start=False, stop=True))
                P.mm_group(fns, reads=[bU, bToep, bHr, bnHi] + bfre_ + bfim_, writes=[bpy])
                P.op("act" if gq % 2 == 0 else "dve",
                     (lambda E, gq=gq: E.copy(out=Ysb[:].rearrange("p t (g h) -> p g t h", h=16)[:, 4 * gq:4 * gq + 4],
                                              in_=py[:].rearrange("p (g t h) -> p g t h", g=4, h=16)))
                     if gq % 2 == 0 else
                     (lambda E, gq=gq: E.tensor_copy(out=Ysb[:].rearrange("p t (g h) -> p g t h", h=16)[:, 4 * gq:4 * gq + 4],
                                                     in_=py[:].rearrange("p (g t h) -> p g t h", g=4, h=16))),
                     reads=[bpy], writes=[bYsb])
            P.dma("sp", ysv[jt * 128:(jt + 1) * 128, :, :], Ysb[:], reads=[bYsb], writes=[fz["obuf_of"](jt) if fz else bys])
            if fz:
                fz["after_chunk"](jt)
        if fz:
            barrier(P)
        else:
            P.finish([bys])
    return nc


def run_L1b(inp):
    nc = _get("L1b", build_L1b)
    mk, idm = _s5_consts()
    w_in = inp["w_in_even"][0]
    maps = []
    for c in range(8):
        b, r = divmod(c, 4)
        gs = slice(16 * r, 16 * r + 16)
        maps.append({"x": np.ascontiguousarray(inp["x"][b]), "npre": np.ascontiguousarray(inp["norm_pre"][0]),
                     "wu": np.ascontiguousarray(w_in[:, 4112 + 256 * r:4112 + 256 * (r + 1)]),
                     "lre": np.ascontiguousarray(inp["s5_lam_re"][0, gs]), "lim": np.ascontiguousarray(inp["s5_lam_im"][0, gs]),
                     "bre": np.ascontiguousarray(inp["s5_b_re"][0, gs]), "bim": np.ascontiguousarray(inp["s5_b_im"][0, gs]),
                     "cre": np.ascontiguousarray(inp["s5_c_re"][0, gs]), "cim": np.ascontiguousarray(inp["s5_c_im"][0, gs]),
                     "ldt": np.ascontiguousarray(inp["s5_log_dt"][0, gs]), "dd": np.ascontiguousarray(inp["s5_d"][0, 256 * r:256 * (r + 1)]),
                     "taus": TAUS, "mk": mk, "idm": idm, "ident": _IDENT})
    res = run_bass_kernel_spmd(nc, maps, core_ids=list(range(8)))
    ys = np.empty((2, 8192, 1024), np.float32)
    for c in range(8):
        b, r = divmod(c, 4)
        ys[b, :, 256 * r:256 * (r + 1)] = res.results[c]["ys"]
    return ys


def _gdn_consts():
    p = np.arange(64)[:, None]; f = np.arange(64)[None, :]
    negu = np.where(f >= p, 0.0, -30000.0)
    negls = np.where(f < p, 0.0, -30000.0)
    nsu = np.where(f > p, -1.0, 0.0)
    i64 = np.eye(64)
    c64 = np.stack([negu, negls, nsu, i64], axis=1).astype(np.float32)
    cmask = np.ones((2, 512), np.float32); cmask[:, 0::64] = 0.0
    sel = np.zeros((2, 2, 128), np.float32); sel[0, 0, :] = 1.0; sel[1, 1, :] = 1.0
    return c64, cmask, sel


def build_L1a(S=8192, fz=None):
    nc = fz["nc"] if fz else bass.Bass("TRN2", target_bir_lowering=False)
    pfx = fz["pfx"] if fz else ""

    def D(name, shape):
        if fz and name in fz["share"]:
            return fz["share"][name]
        return nc.dram_tensor(pfx + name, shape, F32, kind="ExternalInput").ap()
    x_d = D("x", [S, 1024]); npre_d = D("npre", [1024]); w_d = D("w", [1024, 768]); wb_d = D("wb", [1024, 2]); wa_d = D("wa", [1024, 2])
    conv_d = D("conv", [4, 768]); alog_d = D("alog", [2]); dtb_d = D("dtb", [2])
    ident_d = D("ident", [128, 128]); c64_d = D("c64", [64, 4, 64]); cmask_d = D("cmask", [2, 512]); sel_d = D("sel", [2, 2, 128])
    ones_d = D("ones", [128, 128])
    o_d = fz["out"] if fz else nc.dram_tensor("o", [S, 256], F32, kind="ExternalOutput").ap()
    NST = S // 512
    with ExitStack() as st:
        C = Ctx(nc, st, fz["P"], pfx) if fz else Ctx(nc, st); P = C.P
        idf, bidf, idb, bidb = make_ident(C, ident_d)
        npre, bnpre = bcast_row_load(C, "npre", npre_d, 1024)
        w, bw = load_w_bf16(C, "w", w_d, 8, 768)
        wb, bwb = load_w_bf16(C, "wb", wb_d, 8, 2)
        wa, bwa = load_w_bf16(C, "wa", wa_d, 8, 2)
        cw, bcw = C.sb("cw", [128, 4, 6])
        P.dma("sp", cw[:], conv_d.rearrange("j (c p) -> p j c", p=128), writes=[bcw])
        extu = fz.get("uTp") if fz else None
        if extu:
            wu_d = D("wu", [1024, 256])
            wu, bwu = load_w_bf16(C, "wu", wu_d, 8, 256)
            uTp, buTp = extu
        c64, bc64 = C.sb("c64", [64, 4, 64]); P.dma("sp", c64[:], c64_d, writes=[bc64])
        NEGU = c64[:, 0, :]; NEGLS = c64[:, 1, :]; NSU = c64[:, 2, :]; I64 = c64[:, 3, :]
        cmask, bcmask = C.sb("cmask", [2, 512]); P.dma("sp", cmask[:], cmask_d, writes=[bcmask])
        sel, bsel = C.sb("sel", [2, 2, 128]); P.dma("sp", sel[:], sel_d, writes=[bsel])
        ones, bones = C.sb("ones", [128, 128]); P.dma("sp", ones[:], ones_d, writes=[bones])
        onesb, bonesb = C.sb("onesb", [128, 128], BF16)
        P.op("dve", lambda E: E.tensor_copy(out=onesb[:], in_=ones[:]), reads=[bones], writes=[bonesb])
        sqb, bsqb = C.sb("sqb", [128, 512], BF16)
        alog, balog = C.sb("alog", [2, 1]); P.dma("sp", alog[:], alog_d.rearrange("(a b) -> a b", b=1), writes=[balog])
        dtb, bdtb = C.sb("dtb", [2, 1]); P.dma("sp", dtb[:], dtb_d.rearrange("(a b) -> a b", b=1), writes=[bdtb])
        negA, bnegA = C.sb("negA", [2, 1])
        P.op("act", lambda E: E.activation(out=negA[:], in_=alog[:], func=AF.Exp), reads=[balog], writes=[bnegA])
        P.op("dve", lambda E: E.tensor_scalar(out=negA[:], in0=negA[:], scalar1=-1.0, scalar2=None, op0=ALU.mult), reads=[bnegA], writes=[bnegA])
        xt, bxt = C.sb("xt", [128, 1024]); sq, bsq = C.sb("sq", [128, 1024], BF16); hn, bhn = C.sb("hn", [128, 1024], BF16)
        ss, bss = C.sb("ss", [128, 1]); hT, bhT = C.sb("hT", [128, 8, 512], BF16)
        raw, _ = C.sb("raw", [128, 6, 515]); braw = [Buf("raw%d" % i) for i in range(6)]
        cvq, bcvq = C.sb("cvq", [128, 512])
        act, _ = C.sb("act", [128, 4, 512]); bact = [Buf("act%d" % i) for i in range(4)]
        vbuf2 = []; qk2 = []; bqk2 = []
        for par_ in range(2):
            vt_, _ = C.sb("vbuf%d" % par_, [128, 2, 512]); vbuf2.append((vt_, [Buf("vb%d_%d" % (par_, i)) for i in range(2)]))
            qt_, _ = C.sb("qk%d" % par_, [128, 4, 512]); qk2.append(qt_); bqk2.append([Buf("qk%d_%d" % (par_, i)) for i in range(4)])
        rn, brn = C.sb("rn", [128, 512])
        brow, bbrow = C.sb("brow", [2, 512]); grow, bgrow = C.sb("grow", [2, 512]); gcrow, bgcrow = C.sb("gcrow", [2, 512])
        GCB2 = []; BB2 = []
        for par_ in range(2):
            GCB2.append([C.sb("GCB%d_%d" % (par_, h), [128, 512]) for h in range(2)])
            BB2.append([C.sb("BB%d_%d" % (par_, h), [128, 512]) for h in range(2)])
        m64h = []; smallh = []
        for h in range(2):
            d_ = {}
            for nm in ("arg1", "scr"):
                d_[nm] = C.sb("m%d_%s" % (h, nm), [64, 512])
            for nm in ("DT", "Ds", "tmp", "Pa", "Pb", "Qa", "Qb"):
                d_[nm] = C.sb("m%d_%s" % (h, nm), [64, 512], BF16)
            m64h.append(d_)
        heads = []
        for h in range(2):
            H = {}
            H["attnT"] = C.sb("attnT%d" % h, [64, 512], BF16); H["Y"] = C.sb("Y%d" % h, [64, 512]); H["Ybf"] = C.sb("Ybf%d" % h, [64, 512], BF16)
            H["EG"] = C.sb("EG%d" % h, [128, 512]); H["qdec"] = C.sb("qdec%d" % h, [128, 512], BF16)
            H["kTb"] = C.sb("kTb%d" % h, [128, 512], BF16); H["Sbf"] = C.sb("Sbf%d" % h, [128, 128], BF16)
            H["bv"] = C.sb("bv%d" % h, [64, 8, 128]); H["kdec"] = C.sb("kdec%d" % h, [64, 8, 128], BF16)
            H["nbg"] = C.sb("nbg%d" % h, [64, 8]); H["osb"] = C.sb("osb%d" % h, [128, 8, 128])
            H["vnew"] = C.sb("vnew%d" % h, [64, 128], BF16); H["rhs2"] = C.sb("rhs2%d" % h, [64, 128], BF16)
            heads.append(H)
        for h in range(2):
            d_ = {}
            for nm in ("gccol", "bcol", "nbcol", "elast", "egc"):
                d_[nm] = C.sb("s%d_%s" % (h, nm), [64, 8])
            smallh.append(d_)
        Sst = [C.sb("S%d" % h, [128, 128]) for h in range(2)]
        for h in range(2):
            P.op("dve", lambda E, h=h: E.memset(Sst[h][0][:], 0.0), writes=[Sst[h][1]])
            P.op("dve", lambda E, h=h: E.memset(heads[h]["Sbf"][0][:], 0.0), writes=[heads[h]["Sbf"][1]])
        P.op("dve", lambda E: E.memset(raw[:, :, 0:3], 0.0), writes=braw)
        ptr, bptr = C.ps("ptr", [128, 1024], BF16)
        G = [C.ps("gp%d" % i, [128, 512]) for i in range(7)]
        GP = G[0:4]
        GA = G[4:7]
        BKS = [(GP[0], GP[1], GP[2]), (GP[3], GA[0], GA[1])]
        ga_ctr = [0]

        def next_ga():
            ga_ctr[0] += 1
            return GA[ga_ctr[0] % 3]
        bo = None if fz else Buf("o", multi=True)
        if fz is not None and fz.get("debug"):
            print("L1a sbuf remaining", nc.sbuf_bytes_remaining)

        def tt(out, bo_, a, ba, b, bb_, op, eng="dve"):
            P.op(eng, lambda E: E.tensor_tensor(out=out, in0=a, in1=b, op=op), reads=ba if isinstance(ba, list) else [ba], writes=[bo_])

        def stageA(s_):
            par = s_ % 2
            qk = qk2[par]; bqk = bqk2[par]; GCB = GCB2[par]; BB = BB2[par]; vb, bvb = vbuf2[par]
            for t in range(4):
                r0 = s_ * 512 + t * 128
                P.dma("sp", xt[:], x_d[r0:r0 + 128, :], writes=[bxt])
                rms_rstd(C, xt[:], bxt, 1024, sq[:], bsq, ss, bss)
                P.op("dve", lambda E: E.scalar_tensor_tensor(out=hn[:], in0=xt[:], scalar=ss[:, 0:1], in1=npre[:],
                                                             op0=ALU.mult, op1=ALU.mult), reads=[bxt, bss, bnpre], writes=[bhn])
                transpose8(C, hn, bhn, idb, bidb, ptr, bptr, hT[:, :, t * 128:(t + 1) * 128], bhT, eng="act")
                yield
            for ct in range(6):
                pa, bpa = next_ga()
                fns = [(lambda E, kt=kt, ct=ct, pa=pa: E.matmul(pa[:], lhsT=w[:, kt, ct * 128:(ct + 1) * 128], rhs=hT[:, kt, :],
                                                                start=(kt == 0), stop=(kt == 7))) for kt in range(8)]
                P.mm_group(fns, reads=[bw, bhT], writes=[bpa])
                P.op("act", lambda E, ct=ct, pa=pa: E.copy(out=raw[:, ct, 3:515], in_=pa[:]), reads=[bpa], writes=[braw[ct]])
                P.op("dve", lambda E, ct=ct: E.tensor_scalar(out=cvq[:], in0=raw[:, ct, 0:512], scalar1=cw[:, 0, ct:ct + 1], scalar2=None, op0=ALU.mult),
                     reads=[braw[ct], bcw], writes=[bcvq])
                for j in range(1, 4):
                    P.op("dve", lambda E, ct=ct, j=j: E.scalar_tensor_tensor(out=cvq[:], in0=raw[:, ct, j:j + 512], scalar=cw[:, j, ct:ct + 1], in1=cvq[:],
                                                                             op0=ALU.mult, op1=ALU.add), reads=[braw[ct], bcw, bcvq], writes=[bcvq])
                P.op("act", lambda E, ct=ct: E.copy(out=raw[:, ct, 0:3], in_=raw[:, ct, 512:515]), reads=[braw[ct]], writes=[braw[ct]])
                if ct < 4:
                    P.op("act", lambda E, ct=ct: E.activation(out=act[:, ct, :], in_=cvq[:], func=AF.Silu), reads=[bcvq], writes=[bact[ct]])
                else:
                    P.op("act", lambda E, ct=ct, vb=vb: E.activation(out=vb[:, ct - 4, :], in_=cvq[:], func=AF.Silu), reads=[bcvq], writes=[bvb[ct - 4]])
                yield
            if extu:
                for blk in range(2):
                    pa, bpa = next_ga()
                    fns = [(lambda E, kt=kt, blk=blk, pa=pa: E.matmul(
                        pa[:].rearrange("p (s n) -> p s n", s=16), lhsT=wu[:, kt, blk * 128:(blk + 1) * 128],
                        rhs=hT[:, kt, :].rearrange("p (n s) -> p s n", s=16), start=(kt == 0), stop=(kt == 7))) for kt in range(8)]
                    P.mm_group(fns, reads=[bwu, bhT], writes=[bpa])
                    P.op("act", lambda E, blk=blk, pa=pa, s_=s_: E.copy(out=uTp[:, blk, :, 32 * s_:32 * s_ + 32], in_=pa[:].rearrange("p (s n) -> p s n", s=16)),
                         reads=[bpa], writes=[buTp])
                    yield
            for ct in range(4):
                pa, bpa = next_ga()
                P.op("act", lambda E, ct=ct: E.activation(out=sqb[:], in_=act[:, ct, :], func=AF.Square), reads=[bact[ct]], writes=[bsqb])
                P.op("pe", lambda E, pa=pa: E.matmul(pa[:], lhsT=onesb[:], rhs=sqb[:], start=True, stop=True), reads=[bonesb, bsqb], writes=[bpa])
                P.op("act", lambda E, pa=pa: E.activation(out=rn[:], in_=pa[:], func=AF.Ln, bias=1e-6, scale=1.0), reads=[bpa], writes=[brn])
                P.op("act", lambda E: E.activation(out=rn[:], in_=rn[:], func=AF.Exp, scale=-0.5), reads=[brn], writes=[brn])
                if ct < 2:
                    P.op("dve", lambda E, ct=ct, qk=qk: E.scalar_tensor_tensor(out=qk[:, ct, :], in0=act[:, ct, :], scalar=float(128 ** -0.5), in1=rn[:],
                                                                               op0=ALU.mult, op1=ALU.mult), reads=[bact[ct], brn], writes=[bqk[ct]])
                else:
                    P.op("dve", lambda E, ct=ct, qk=qk: E.tensor_tensor(out=qk[:, ct, :], in0=act[:, ct, :], in1=rn[:], op=ALU.mult),
                         reads=[bact[ct], brn], writes=[bqk[ct]])
                yield
            pa, bpa = next_ga()
            fns = [(lambda E, kt=kt, pa=pa: E.matmul(pa[0:2, :], lhsT=wb[:, kt, 0:2], rhs=hT[:, kt, :], start=(kt == 0), stop=(kt == 7))) for kt in range(8)]
            P.mm_group(fns, reads=[bwb, bhT], writes=[bpa])
            P.op("act", lambda E, pa=pa: E.activation(out=brow[:], in_=pa[0:2, :], func=AF.Sigmoid), reads=[bpa], writes=[bbrow])
            pa2, bpa2 = next_ga()
            fns = [(lambda E, kt=kt, pa2=pa2: E.matmul(pa2[0:2, :], lhsT=wa[:, kt, 0:2], rhs=hT[:, kt, :], start=(kt == 0), stop=(kt == 7))) for kt in range(8)]
            P.mm_group(fns, reads=[bwa, bhT], writes=[bpa2])
            P.op("act", lambda E, pa2=pa2: E.activation(out=grow[:], in_=pa2[0:2, :], func=AF.Exp, bias=dtb[:, 0:1], scale=1.0), reads=[bpa2, bdtb], writes=[bgrow])
            P.op("act", lambda E: E.activation(out=grow[:], in_=grow[:], func=AF.Ln, bias=1.0, scale=1.0), reads=[bgrow], writes=[bgrow])
            P.op("dve", lambda E: E.tensor_scalar(out=grow[:], in0=grow[:], scalar1=negA[:, 0:1], scalar2=None, op0=ALU.mult), reads=[bgrow, bnegA], writes=[bgrow])
            P.op("dve", lambda E: E.tensor_tensor_scan(out=gcrow[:], data0=cmask[:], data1=grow[:], initial=0.0, op0=ALU.mult, op1=ALU.add),
                 reads=[bcmask, bgrow], writes=[bgcrow])
            yield
            for h in range(2):
                pa, bpa = next_ga()
                P.op("pe", lambda E, h=h, pa=pa: E.matmul(pa[:], lhsT=sel[:, h, :], rhs=gcrow[:], start=True, stop=True), reads=[bsel, bgcrow], writes=[bpa])
                P.op("act", lambda E, h=h, pa=pa, GCB=GCB: E.copy(out=GCB[h][0][:], in_=pa[:]), reads=[bpa], writes=[GCB[h][1]])
                pa, bpa = next_ga()
                P.op("pe", lambda E, h=h, pa=pa: E.matmul(pa[:], lhsT=sel[:, h, :], rhs=brow[:], start=True, stop=True), reads=[bsel, bbrow], writes=[bpa])
                P.op("act", lambda E, h=h, pa=pa, BB=BB: E.copy(out=BB[h][0][:], in_=pa[:]), reads=[bpa], writes=[BB[h][1]])
                yield

        for _ in stageA(0):
            pass
        for s_ in range(NST):
            par = s_ % 2
            qk = qk2[par]; bqk = bqk2[par]; GCB = GCB2[par]; BB = BB2[par]; vb, bvb = vbuf2[par]
            nxt = stageA(s_ + 1) if s_ + 1 < NST else None

            def advance(k):
                if nxt is not None:
                    for _ in range(k):
                        next(nxt, None)
            def stageB(h, qk=qk, bqk=bqk, GCB=GCB, BB=BB, vb=vb, bvb=bvb):
                m64 = m64h[h]; small = smallh[h]; BK = BKS[h]
                qT = qk[:, h, :]; bqT = bqk[h]; kT = qk[:, 2 + h, :]; bkT = bqk[2 + h]; vT = vb[:, h, :]; bvT = bvb[h]
                gcb, bgcb = GCB[h]; bb, bbb = BB[h]
                H = heads[h]
                attnT, battnT = H["attnT"]; Y, bY = H["Y"]; EG, bEG = H["EG"]; qdec, bqdec = H["qdec"]
                Ybf, bYbf = H["Ybf"]
                bv, bbv = H["bv"]; kdec, bkdec = H["kdec"]; nbg, bnbg = H["nbg"]
                arg1, barg1 = m64["arg1"]; scr, bscr = m64["scr"]; DT, bDT = m64["DT"]; Ds, bDs = m64["Ds"]
                tmp, btmp = m64["tmp"]
                gccol, bgccol = small["gccol"]; bcol, bbcol = small["bcol"]; nbcol, bnbcol = small["nbcol"]
                elast, belast = small["elast"]; egc, begc = small["egc"]
                v3 = lambda t_: t_[:].rearrange("p (n f) -> p n f", f=64)
                i64b = I64.unsqueeze(1).to_broadcast([64, 8, 64])
                tt(v3(scr), bscr, gcb[0:64, :].rearrange("p (n f) -> p n f", f=64), [bgcb, bc64], i64b, bc64, ALU.mult)
                P.op("dve", lambda E, scr=scr, gccol=gccol: E.tensor_reduce(out=gccol[:], in_=scr[:].rearrange("p (n f) -> p n f", f=64), axis=AX.X, op=ALU.add), reads=[bscr], writes=[bgccol])
                tt(v3(scr), bscr, bb[0:64, :].rearrange("p (n f) -> p n f", f=64), [bbb, bc64], i64b, bc64, ALU.mult)
                P.op("dve", lambda E, scr=scr, bcol=bcol: E.tensor_reduce(out=bcol[:], in_=scr[:].rearrange("p (n f) -> p n f", f=64), axis=AX.X, op=ALU.add), reads=[bscr], writes=[bbcol])
                yield
                P.op("dve", lambda E: E.tensor_scalar(out=nbcol[:], in0=bcol[:], scalar1=-1.0, scalar2=None, op0=ALU.mult), reads=[bbcol], writes=[bnbcol])
                tt(v3(arg1), barg1, gcb[0:64, :].rearrange("p (n f) -> p n f", f=64), [bgcb, bgccol], gccol[:].unsqueeze(2).to_broadcast([64, 8, 64]), bgccol, ALU.subtract)
                tt(v3(scr), bscr, v3(arg1), [barg1, bc64], NEGU.unsqueeze(1).to_broadcast([64, 8, 64]), bc64, ALU.add)
                P.op("act", lambda E: E.activation(out=DT[:], in_=scr[:], func=AF.Exp), reads=[bscr], writes=[bDT])
                yield
                P.op("dve", lambda E: E.scalar_tensor_tensor(out=scr[:].rearrange("p (n f) -> p n f", f=64), in0=arg1[:].rearrange("p (n f) -> p n f", f=64), scalar=-1.0,
                                                             in1=NEGLS.unsqueeze(1).to_broadcast([64, 8, 64]), op0=ALU.mult, op1=ALU.add), reads=[barg1, bc64], writes=[bscr])
                P.op("act", lambda E: E.activation(out=Ds[:], in_=scr[:], func=AF.Exp), reads=[bscr], writes=[bDs])
                pk, bpk = BK[0]; pq, bpq = BK[1]
                fns = [(lambda E, n=n, pk=pk, kT=kT: E.matmul(pk[0:64, n * 64:(n + 1) * 64], lhsT=kT[:, n * 64:(n + 1) * 64], rhs=kT[:, n * 64:(n + 1) * 64],
                                                              start=True, stop=True)) for n in range(8)]
                P.mm_group(fns, reads=[bkT], writes=[bpk])
                fns = [(lambda E, n=n, pq=pq, kT=kT, qT=qT: E.matmul(pq[0:64, n * 64:(n + 1) * 64], lhsT=kT[:, n * 64:(n + 1) * 64], rhs=qT[:, n * 64:(n + 1) * 64],
                                                                     start=True, stop=True)) for n in range(8)]
                P.mm_group(fns, reads=[bkT, bqT], writes=[bpq])
                yield
                tt(attnT[:], battnT, pq[0:64, :], [bpq, bDT], DT[:], bDT, ALU.mult)
                Pc, bPc = m64["Pa"]; Pn, bPn = m64["Pb"]; Qc, bQc = m64["Qa"]; Qn, bQn = m64["Qb"]
                tt(tmp[:], btmp, pk[0:64, :], [bpk, bDT], DT[:], bDT, ALU.mult)
                tt(tmp[:], btmp, tmp[:], [btmp, bbb], bb[0:64, :], bbb, ALU.mult)
                tt(v3(Qc), bQc, v3(tmp), [btmp, bc64], NSU.unsqueeze(1).to_broadcast([64, 8, 64]), bc64, ALU.mult)
                yield
                tt(tmp[:], btmp, pk[0:64, :], [bpk, bDs], Ds[:], bDs, ALU.mult)
                tt(v3(Pc), bPc, v3(tmp), [btmp, bnbcol], nbcol[:].unsqueeze(2).to_broadcast([64, 8, 64]), bnbcol, ALU.mult)
                tt(v3(Y), bY, v3(Qc), [bQc, bc64], i64b, bc64, ALU.add)
                P.op("act", lambda E, Ybf=Ybf, Y=Y: E.copy(out=Ybf[:], in_=Y[:]), reads=[bY], writes=[bYbf])
                yield
                for j in range(5):
                    pP, bpP = BK[2]; pQ, bpQ = BK[1]
                    fns = [(lambda E, n=n, pP=pP, Qc=Qc, Pc=Pc: E.matmul(pP[0:64, n * 64:(n + 1) * 64], lhsT=Qc[:, n * 64:(n + 1) * 64], rhs=Pc[:, n * 64:(n + 1) * 64],
                                                                         start=True, stop=True)) for n in range(8)]
                    P.mm_group(fns, reads=[bQc, bPc], writes=[bpP])
                    if j < 4:
                        fns = [(lambda E, n=n, pQ=pQ, Qc=Qc, Pc=Pc: E.matmul(pQ[0:64, n * 64:(n + 1) * 64], lhsT=Pc[:, n * 64:(n + 1) * 64], rhs=Qc[:, n * 64:(n + 1) * 64],
                                                                             start=True, stop=True)) for n in range(8)]
                        P.mm_group(fns, reads=[bQc, bPc], writes=[bpQ])
                    yield
                    P.op("act", lambda E, Pn=Pn, pP=pP: E.copy(out=Pn[:], in_=pP[0:64, :]), reads=[bpP], writes=[bPn])
                    if j < 4:
                        P.op("dve", lambda E, Qn=Qn, pQ=pQ: E.tensor_copy(out=Qn[:], in_=pQ[0:64, :]), reads=[bpQ], writes=[bQn])
                    pY, bpY = BK[0]
                    fns = [(lambda E, n=n, pY=pY, Pn=Pn, Ybf=Ybf: E.matmul(pY[0:64, n * 64:(n + 1) * 64], lhsT=Pn[:, n * 64:(n + 1) * 64], rhs=Ybf[:, n * 64:(n + 1) * 64],
                                                                         start=True, stop=True)) for n in range(8)]
                    P.mm_group(fns, reads=[bPn, bYbf], writes=[bpY])
                    yield
                    tt(Y[:], bY, Y[:], [bY, bpY], pY[0:64, :], bpY, ALU.add)
                    P.op("act", lambda E, Ybf=Ybf, Y=Y: E.copy(out=Ybf[:], in_=Y[:]), reads=[bY], writes=[bYbf])
                    Pc, bPc, Pn, bPn = Pn, bPn, Pc, bPc
                    Qc, bQc, Qn, bQn = Qn, bQn, Qc, bQc
                for hf in range(2):
                    pth, bpth = BK[1 + hf]
                    fns = [(lambda E, n=n, vT=vT, pth=pth, hf=hf: E.transpose(out=pth[0:64, n * 128:(n + 1) * 128], in_=vT[:, (4 * hf + n) * 64:(4 * hf + n + 1) * 64],
                                                                              identity=idf[:])) for n in range(4)]
                    P.mm_group(fns, reads=[bvT, bidf], writes=[bpth])
                    tt(bv[:, 4 * hf:4 * hf + 4, :], bbv, pth[0:64, :].rearrange("p (n d) -> p n d", d=128), [bpth, bbcol],
                       bcol[:, 4 * hf:4 * hf + 4].unsqueeze(2).to_broadcast([64, 4, 128]), bbcol, ALU.mult)
                yield
                tt(elast[:], belast, gcb[0:64, :].rearrange("p (n f) -> p n f", f=64)[:, :, 63], [bgcb, bgccol], gccol[:], bgccol, ALU.subtract)
                P.op("act", lambda E: E.activation(out=elast[:], in_=elast[:], func=AF.Exp), reads=[belast], writes=[belast])
                for hf in range(2):
                    pth, bpth = BK[1 + hf]
                    fns = [(lambda E, n=n, kT=kT, pth=pth, hf=hf: E.transpose(out=pth[0:64, n * 128:(n + 1) * 128], in_=kT[:, (4 * hf + n) * 64:(4 * hf + n + 1) * 64],
                                                                              identity=idf[:])) for n in range(4)]
                    P.mm_group(fns, reads=[bkT, bidf], writes=[bpth])
                    tt(kdec[:, 4 * hf:4 * hf + 4, :], bkdec, pth[0:64, :].rearrange("p (n d) -> p n d", d=128), [bpth, belast],
                       elast[:, 4 * hf:4 * hf + 4].unsqueeze(2).to_broadcast([64, 4, 128]), belast, ALU.mult)
                yield
                P.op("act", lambda E, gcb=gcb, EG=EG: E.activation(out=EG[:], in_=gcb[:], func=AF.Exp), reads=[bgcb], writes=[bEG])
                tt(qdec[:], bqdec, qT, [bqT, bEG], EG[:], bEG, ALU.mult)
                kTb, bkTb = H["kTb"]
                P.op("act", lambda E, kTb=kTb, kT=kT: E.copy(out=kTb[:], in_=kT), reads=[bkT], writes=[bkTb])
                P.op("act", lambda E: E.activation(out=egc[:], in_=gccol[:], func=AF.Exp), reads=[bgccol], writes=[begc])
                P.op("dve", lambda E, nbg=nbg: E.scalar_tensor_tensor(out=nbg[:], in0=egc[:], scalar=-1.0, in1=bcol[:], op0=ALU.mult, op1=ALU.mult),
                     reads=[begc, bbcol], writes=[bnbg])
            gensB = [stageB(0), stageB(1)]
            aliveB = True
            while aliveB:
                aliveB = False
                for g_ in gensB:
                    try:
                        next(g_)
                        aliveB = True
                    except StopIteration:
                        pass
            banks = [(GP[0], GP[1]), (GP[2], GP[3])]
            for n in range(8):
                cs = slice(n * 64, (n + 1) * 64)
                for h in range(2):
                    H = heads[h]; S, bS = Sst[h]
                    kT, bkT = H["kTb"]; Sbf, bSbf = H["Sbf"]
                    attnT, battnT = H["attnT"]; Y, bY = H["Ybf"]; EG, bEG = H["EG"]; qdec, bqdec = H["qdec"]
                    bv, bbv = H["bv"]; kdec, bkdec = H["kdec"]; nbg, bnbg = H["nbg"]
                    vnew, bvnew = H["vnew"]; rhs2, brhs2 = H["rhs2"]; osb, bosb = H["osb"]
                    (KSO, bKSO), (Sb, bSb) = banks[h]
                    Vb, bVb = KSO, bKSO
                    P.op("pe", lambda E, cs=cs, kT=kT, Sbf=Sbf, KSO=KSO: E.matmul(KSO[0:64, 0:128], lhsT=kT[:, cs], rhs=Sbf[:], start=True, stop=True),
                         reads=[bkT, bSbf], writes=[bKSO])
                    P.op("dve", lambda E, n=n, KSO=KSO, rhs2=rhs2, nbg=nbg, bv=bv: E.scalar_tensor_tensor(
                        out=rhs2[:], in0=KSO[0:64, 0:128], scalar=nbg[:, n:n + 1], in1=bv[:, n, :], op0=ALU.mult, op1=ALU.add),
                        reads=[bKSO, bnbg, bbv], writes=[brhs2])
                    P.op("pe", lambda E, cs=cs, Y=Y, Vb=Vb, rhs2=rhs2: E.matmul(Vb[0:64, 128:256], lhsT=Y[:, cs], rhs=rhs2[:], start=True, stop=True),
                         reads=[bY, brhs2], writes=[bVb])
                    P.op("act", lambda E, vnew=vnew, Vb=Vb: E.copy(out=vnew[:], in_=Vb[0:64, 128:256]), reads=[bVb], writes=[bvnew])
                    fns = [lambda E, cs=cs, Sbf=Sbf, KSO=KSO, qdec=qdec: E.matmul(KSO[64:128, 0:128], lhsT=qdec[:, cs], rhs=Sbf[:], start=True, stop=False),
                           lambda E, cs=cs, KSO=KSO, attnT=attnT, vnew=vnew: E.matmul(KSO[64:128, 0:128], lhsT=attnT[:, cs], rhs=vnew[:], start=False, stop=True)]
                    P.mm_group(fns, reads=[bqdec, bSbf, battnT, bvnew], writes=[bKSO])
                    P.op("pe", lambda E, n=n, Sb=Sb, kdec=kdec, vnew=vnew: E.matmul(Sb[:, 0:128], lhsT=kdec[:, n, :], rhs=vnew[:], start=True, stop=True),
                         reads=[bkdec, bvnew], writes=[bSb])
                    P.op("dve", lambda E, n=n, S=S, EG=EG, Sb=Sb, Sbf=Sbf: E.scalar_tensor_tensor(out=Sbf[:], in0=S[:], scalar=EG[:, n * 64 + 63:n * 64 + 64], in1=Sb[:, 0:128],
                                                                                                  op0=ALU.mult, op1=ALU.add), reads=[bS, bEG, bSb], writes=[bSbf])
                    P.op("dve", lambda E, n=n, S=S, EG=EG, Sb=Sb: E.scalar_tensor_tensor(out=S[:], in0=S[:], scalar=EG[:, n * 64 + 63:n * 64 + 64], in1=Sb[:, 0:128],
                                                                                         op0=ALU.mult, op1=ALU.add), reads=[bS, bEG, bSb], writes=[bS])
                    P.op("act", lambda E, n=n, osb=osb, KSO=KSO: E.copy(out=osb[64:128, n, :], in_=KSO[64:128, 0:128]), reads=[bKSO], writes=[bosb])
                    advance(1)
                advance(1)
            advance(100)
            for h in range(2):
                osb, bosb = heads[h]["osb"]
                P.dma("sp", o_d[s_ * 512:(s_ + 1) * 512, h * 128:(h + 1) * 128].rearrange("(n c) d -> c n d", c=64), osb[64:128, :, :], reads=[bosb],
                      writes=[fz["obuf_of"](s_) if fz else bo])
            if fz:
                fz["after_chunk"](s_)
        if fz:
            barrier(P)
        else:
            P.finish([bo])
    return nc


def run_L1a(inp):
    nc = _get("L1a", build_L1a)
    c64, cmask, sel = _gdn_consts()
    w_in = inp["w_in_even"][0]
    conv = inp["conv_qkv"][0]
    ones = np.ones((128, 128), np.float32)
    maps = []
    for c in range(8):
        b, r = divmod(c, 4)
        cols = np.concatenate([np.arange(256 * r, 256 * r + 256), 1024 + np.arange(256 * r, 256 * r + 256), 2048 + np.arange(256 * r, 256 * r + 256)])
        maps.append({"x": np.ascontiguousarray(inp["x"][b]), "npre": np.ascontiguousarray(inp["norm_pre"][0]),
                     "w": np.ascontiguousarray(w_in[:, cols]), "wb": np.ascontiguousarray(w_in[:, 4096 + 2 * r:4096 + 2 * r + 2]),
                     "wa": np.ascontiguousarray(w_in[:, 4104 + 2 * r:4104 + 2 * r + 2]), "conv": np.ascontiguousarray(conv[:, cols]),
                     "alog": np.ascontiguousarray(inp["a_log"][0, 2 * r:2 * r + 2]), "dtb": np.ascontiguousarray(inp["dt_bias"][0, 2 * r:2 * r + 2]),
                     "ident": _IDENT, "c64": c64, "cmask": cmask, "sel": sel, "ones": ones})
    res = run_bass_kernel_spmd(nc, maps, core_ids=list(range(8)))
    S_ = inp["x"].shape[1]
    o = np.empty((2, S_, 1024), np.float32)
    for c in range(8):
        b, r = divmod(c, 4)
        o[b, :, 256 * r:256 * (r + 1)] = res.results[c]["o"]
    return o


def kernel_unfused(**inputs):
    inp = {k: np.asarray(v) for k, v in inputs.items()}
    o = run_L1a(inp)
    ys = run_L1b(inp)
    x1 = run_L2(inp, o, ys)
    out = run_L3(inp, x1)
    return out.astype(np.float32)


def build_fused():
    nc = bass.Bass("TRN2", target_bir_lowering=False)
    x_full = nc.dram_tensor("x", [8192, 1024], F32, kind="ExternalInput").ap()
    ident_d = nc.dram_tensor("ident", [128, 128], F32, kind="ExternalInput").ap()
    npre0_d = nc.dram_tensor("npre0", [1024], F32, kind="ExternalInput").ap()
    gidx_d = nc.dram_tensor("gidx", [128, 2, 17, 4], I32, kind="ExternalInput").ap()
    out_d = nc.dram_tensor("out", [2048, 1024], F32, kind="ExternalOutput").ap()
    ag_in = [nc.dram_tensor("ag_in%d" % i, [8192, 256], (F32, BF16)[i]) for i in range(2)]
    ag_out = [nc.dram_tensor("ag_out%d" % i, [4 * 8192, 256], (F32, BF16)[i]) for i in range(2)]
    x1s = nc.dram_tensor("x1s", [2176, 1024], F32)
    GROUPS = [[0, 1, 2, 3], [4, 5, 6, 7]]
    with ExitStack() as st:
        C = Ctx(nc, st); P = C.P
        csem = st.enter_context(nc.semaphore("csem"))
        bag_out = Buf("ag_out"); bx1s = Buf("x1s", multi=True); bout = Buf("out", multi=True)
        bo_ch = [Buf("o_ch%d" % k, multi=True) for k in range(16)]
        by_jt = [Buf("y_jt%d" % k, multi=True) for k in range(4)]
        ncc = [0]

        def emit_cc(which, k, inbuf, rows=512):
            P._deps("pool", [inbuf], [])
            P.streams["pool"].append(lambda E, which=which, k=k, rows=rows: E.collective_compute(
                "AllGather", ALU.bypass, replica_groups=GROUPS,
                ins=[ag_in[which].ap()[k * rows:(k + 1) * rows, :].opt()],
                outs=[ag_out[which].ap()[k * 4 * rows:(k + 1) * 4 * rows, :].opt()]).then_inc(csem))
            ncc[0] += 1

        share1 = {"x": x_full, "ident": ident_d, "npre": npre0_d}

        def after_jt(jt):
            emit_cc(1, jt, by_jt[jt], rows=2048)

        with ExitStack() as stU:
            CU = Ctx(nc, stU, P, "u_")
            uext = CU.sb("uTp", [128, 2, 16, 512], BF16)
            build_L1a(8192, fz={"nc": nc, "P": P, "pfx": "a_", "share": share1, "out": ag_in[0].ap(), "uTp": uext,
                                "obuf_of": lambda s_: bo_ch[s_], "after_chunk": lambda s_: emit_cc(0, s_, bo_ch[s_])})
            build_L1b(8192, fz={"nc": nc, "P": P, "pfx": "b_", "share": share1, "out": ag_in[1].ap(), "uTp": uext,
                                "obuf_of": lambda jt: by_jt[jt], "after_chunk": after_jt, "ybf16": True})
        gidx, bgidx = C.sb("gidx", [128, 2, 17, 4], I32)
        P.dma("sp", gidx[:], gidx_d, writes=[bgidx])
        waited = [False]

        def gather(P_, ld, bld, tile, part):
            if not waited[0]:
                P.streams["pool"].append(lambda E: E.wait_ge(csem, ncc[0]))
                P.op("pool", lambda E: E.nop(), reads=[], writes=[bag_out])
                waited[0] = True
            for i in range(4):
                P_.dma_ind("pool", ld[:, i * 256:(i + 1) * 256], ag_out[part].ap(), gidx[:, part, tile, i:i + 1], reads=[bag_out, bgidx], writes=[bld])

        share2 = {"ident": ident_d, "npre": npre0_d, "o": None, "ys": None}
        build_L2(2176, fz={"nc": nc, "P": P, "pfx": "c_", "share": share2, "out": x1s.ap(), "obuf": bx1s, "gather": gather, "ybf16": True})
        share3 = {"ident": ident_d, "x": x1s.ap()}
        build_L3(2048, fz={"nc": nc, "P": P, "pfx": "d_", "share": share3, "out": out_d, "obuf": bout, "xbuf": bx1s})
        P.finish([bout])
    return nc


def _gidx(r):
    g = np.zeros((128, 2, 17, 4), np.int32)
    p = np.arange(128)[:, None, None]
    tile = np.arange(17)[None, :, None]
    src = np.arange(4)[None, None, :]
    tok = np.clip(2048 * r - 128 + tile * 128 + p, 0, 8191)
    for part, R in ((0, 512), (1, 2048)):
        g[:, part] = ((tok // R) * 4 + src) * R + tok % R
    return g


def kernel(**inputs):
    inp = {k: np.ascontiguousarray(np.asarray(v)) for k, v in inputs.items()}
    nc = _get("fused", build_fused)
    c64, cmask, sel = _gdn_consts()
    mk, idm = _s5_consts()
    ones = np.ones((128, 128), np.float32)
    w_in = inp["w_in_even"][0]
    conv = inp["conv_qkv"][0]
    wz = np.ascontiguousarray(np.concatenate([w_in[:, 3072:4096], w_in[:, 5136:6160]], axis=1))
    maps = []
    for c in range(8):
        b, r = divmod(c, 4)
        cols = np.concatenate([np.arange(256 * r, 256 * r + 256), 1024 + np.arange(256 * r, 256 * r + 256), 2048 + np.arange(256 * r, 256 * r + 256)])
        gs = slice(16 * r, 16 * r + 16)
        xq = np.zeros((2176, 1024), np.float32)
        xq[128:] = inp["x"][b, 2048 * r:2048 * (r + 1)]
        if r > 0:
            xq[:128] = inp["x"][b, 2048 * r - 128:2048 * r]
        m = {"x": inp["x"][b], "ident": _IDENT, "npre0": inp["norm_pre"][0], "gidx": _gidx(r),
             "a_w": np.ascontiguousarray(w_in[:, cols]), "a_wb": np.ascontiguousarray(w_in[:, 4096 + 2 * r:4096 + 2 * r + 2]),
             "a_wa": np.ascontiguousarray(w_in[:, 4104 + 2 * r:4104 + 2 * r + 2]), "a_conv": np.ascontiguousarray(conv[:, cols]),
             "a_alog": np.ascontiguousarray(inp["a_log"][0, 2 * r:2 * r + 2]), "a_dtb": np.ascontiguousarray(inp["dt_bias"][0, 2 * r:2 * r + 2]),
             "a_c64": c64, "a_cmask": cmask, "a_sel": sel, "a_ones": ones,
             "a_wu": np.ascontiguousarray(w_in[:, 4112 + 256 * r:4112 + 256 * (r + 1)]),
             "b_wu": np.ascontiguousarray(w_in[:, 4112 + 256 * r:4112 + 256 * (r + 1)]),
             "b_lre": np.ascontiguousarray(inp["s5_lam_re"][0, gs]), "b_lim": np.ascontiguousarray(inp["s5_lam_im"][0, gs]),
             "b_bre": np.ascontiguousarray(inp["s5_b_re"][0, gs]), "b_bim": np.ascontiguousarray(inp["s5_b_im"][0, gs]),
             "b_cre": np.ascontiguousarray(inp["s5_c_re"][0, gs]), "b_cim": np.ascontiguousarray(inp["s5_c_im"][0, gs]),
             "b_ldt": np.ascontiguousarray(inp["s5_log_dt"][0, gs]), "b_dd": np.ascontiguousarray(inp["s5_d"][0, 256 * r:256 * (r + 1)]),
             "b_taus": TAUS, "b_mk": mk, "b_idm": idm,
             "c_x": xq, "c_wz": wz, "c_wglu": inp["w_glu"][0], "c_wout": inp["w_out_even"][0], "c_npost": inp["norm_post"][0],
             "c_gnw": inp["gdn_norm_w"][0],
             "d_win": inp["w_in_odd"][0], "d_wout": inp["w_out_odd"][0], "d_conv": inp["conv_short"][0],
             "d_npre": inp["norm_pre"][1], "d_npost": inp["norm_post"][1]}
        maps.append(m)
    res = run_bass_kernel_spmd(nc, maps, core_ids=list(range(8)))
    out = np.empty((2, 8192, 1024), np.float32)
    for c in range(8):
        b, r = divmod(c, 4)
        out[b, r * 2048:(r + 1) * 2048] = res.results[c]["out"]
    return out
```

```python
from contextlib import ExitStack
import numpy as np
import concourse.bass as bass
import concourse.mybir as mybir
from concourse.bass_utils import run_bass_kernel_spmd

F32 = mybir.dt.float32
BF16 = mybir.dt.bfloat16
AF = mybir.ActivationFunctionType
ALU = mybir.AluOpType
AX = mybir.AxisListType

NDS = 24


class Buf:
    __slots__ = ("name", "w", "r", "multi")

    def __init__(self, name, multi=False):
        self.name = name
        self.w = [] if multi else None
        self.r = []
        self.multi = multi


class Prog:
    ENG = ("pe", "act", "dve", "pool", "sp")

    def __init__(self, nc, stack):
        self.nc = nc
        self.stack = stack
        self.streams = {e: [] for e in self.ENG}
        self.cnt = {e: 0 for e in self.ENG}
        self.sem = {e: stack.enter_context(nc.semaphore("s_" + e)) for e in self.ENG}
        self.seen = {e: {} for e in self.ENG}
        self.dcnt = {e: 0 for e in self.ENG}
        self.dsem = {}
        for e in ("sp", "pool", "act"):
            self.dsem[e] = [stack.enter_context(nc.semaphore("d_%s%d" % (e, i))) for i in range(NDS)]
        self.same_engine_sync = True
        self.nwaits = 0

    def _wait(self, eng, tok):
        if tok is None:
            return
        kind = tok[0]
        if kind == "c":
            _, e2, n = tok
            if e2 == eng and (eng == "pe" or not self.same_engine_sync):
                return
            key = e2
            if self.seen[eng].get(key, 0) >= n:
                return
            self.seen[eng][key] = n
            sem = self.sem[e2]
            self.streams[eng].append(lambda E, sem=sem, n=n: E.wait_ge(sem, n))
            self.nwaits += 1
        else:
            _, q, slot, val = tok
            key = ("d", q, slot)
            if self.seen[eng].get(key, 0) >= val:
                return
            self.seen[eng][key] = val
            sem = self.dsem[q][slot]
            self.streams[eng].append(lambda E, sem=sem, val=val: E.wait_ge(sem, val))
            self.nwaits += 1

    def _deps(self, eng, reads, writes):
        for b in reads:
            if b.multi:
                for t in b.w:
                    self._wait(eng, t)
            else:
                self._wait(eng, b.w)
        for b in writes:
            if not b.multi:
                self._wait(eng, b.w)
            for t in b.r:
                self._wait(eng, t)

    def _commit(self, tok, reads, writes):
        for b in writes:
            if b.multi:
                b.w.append(tok)
            else:
                b.w = tok
            b.r = []
        for b in reads:
            if b not in writes:
                b.r.append(tok)

    def op(self, eng, fn, reads=(), writes=()):
        reads = list(reads)
        writes = list(writes)
        self._deps(eng, reads, writes)
        self.cnt[eng] += 1
        n = self.cnt[eng]
        sem = self.sem[eng]
        self.streams[eng].append(lambda E, fn=fn, sem=sem: fn(E).then_inc(sem, 1))
        tok = ("c", eng, n)
        self._commit(tok, reads, writes)
        return tok

    def mm_group(self, fns, reads=(), writes=()):
        eng = "pe"
        reads = list(reads)
        writes = list(writes)
        self._deps(eng, reads, writes)
        self.cnt[eng] += 1
        n = self.cnt[eng]
        sem = self.sem[eng]
        for fn in fns[:-1]:
            self.streams[eng].append(lambda E, fn=fn: fn(E))
        last = fns[-1]
        self.streams[eng].append(lambda E, fn=last, sem=sem: fn(E).then_inc(sem, 1))
        tok = ("c", eng, n)
        self._commit(tok, reads, writes)
        return tok

    def dma(self, q, out_ap, in_ap, reads=(), writes=()):
        reads = list(reads)
        writes = list(writes)
        self._deps(q, reads, writes)
        j = self.dcnt[q]
        self.dcnt[q] += 1
        slot = j % NDS
        val = 16 * (j // NDS + 1)
        if j >= NDS:
            self._wait(q, ("d", q, slot, val - 16))
        sem = self.dsem[q][slot]
        self.streams[q].append(
            lambda E, o=out_ap, i=in_ap, sem=sem: E.dma_start(out=o, in_=i).then_inc(sem, 16))
        tok = ("d", q, slot, val)
        self._commit(tok, reads, writes)
        return tok

    def dma_ind(self, q, out_ap, table_ap, idx_ap, reads=(), writes=()):
        reads = list(reads)
        writes = list(writes)
        self._deps(q, reads, writes)
        j = self.dcnt[q]
        self.dcnt[q] += 1
        slot = j % NDS
        val = 16 * (j // NDS + 1)
        if j >= NDS:
            self._wait(q, ("d", q, slot, val - 16))
        sem = self.dsem[q][slot]
        self.streams[q].append(
            lambda E, o=out_ap, t=table_ap, i=idx_ap, sem=sem: E.indirect_dma_start(
                out=o, out_offset=None, in_=t, in_offset=bass.IndirectOffsetOnAxis(ap=i, axis=0)).then_inc(sem, 16))
        tok = ("d", q, slot, val)
        self._commit(tok, reads, writes)
        return tok

    def finish(self, final_bufs):
        for b in final_bufs:
            for t in (b.w if b.multi else [b.w]):
                self._wait("sp", t)
        nc = self.nc
        streams = self.streams
        with nc.Block() as block:
            @block.tensor
            def _(E):
                for f in streams["pe"]:
                    f(E)

            @block.scalar
            def _(E):
                for f in streams["act"]:
                    f(E)

            @block.vector
            def _(E):
                for f in streams["dve"]:
                    f(E)

            @block.gpsimd
            def _(E):
                for f in streams["pool"]:
                    f(E)

            @block.sync
            def _(E):
                for f in streams["sp"]:
                    f(E)


class Ctx:
    def __init__(self, nc, st, P=None, pfx=""):
        self.nc = nc
        self.st = st
        self.pfx = pfx
        if P is None:
            st.enter_context(nc.allow_non_contiguous_dma(reason="small parameter loads / layout transforms"))
            P = Prog(nc, st)
        self.P = P

    def sb(self, name, shape, dt=F32):
        t = self.st.enter_context(self.nc.sbuf_tensor("sb_" + self.pfx + name, shape, dt))
        return t, Buf(name)

    def ps(self, name, shape, dt=F32):
        t = self.st.enter_context(self.nc.psum_tensor("ps_" + self.pfx + name, shape, dt))
        return t, Buf(name)


def bcast_row_load(C, name, dram_vec, n, q="sp"):
    t, b = C.sb(name, [128, n])
    C.P.dma(q, t[:], dram_vec.partition_broadcast(128), writes=[b])
    return t, b


def make_ident(C, dram_ident):
    idf, bidf = C.sb("identf", [128, 128])
    C.P.dma("sp", idf[:], dram_ident, writes=[bidf])
    idb, bidb = C.sb("identb", [128, 128], BF16)
    C.P.op("dve", lambda E: E.tensor_copy(out=idb[:], in_=idf[:]), reads=[bidf], writes=[bidb])
    return idf, bidf, idb, bidb


def rms_rstd(C, src, bsrc, ncols, junk, bjunk, ss, bss, eps=1e-6):
    P = C.P
    P.op("act", lambda E: E.activation(out=junk, in_=src, func=AF.Square, accum_out=ss[:, 0:1]),
         reads=[bsrc], writes=[bjunk, bss])
    P.op("act", lambda E: E.activation(out=ss[:, 0:1], in_=ss[:, 0:1], func=AF.Sqrt, bias=float(eps), scale=float(1.0 / ncols)),
         reads=[bss], writes=[bss])
    P.op("dve", lambda E: E.reciprocal(out=ss[:, 0:1], in_=ss[:, 0:1]), reads=[bss], writes=[bss])


def transpose8(C, src_bf, bsrc, idb, bidb, ptr, bptr, dst3, bdst, eng="act"):
    P = C.P
    fns = [(lambda E, kt=kt: E.transpose(out=ptr[:, kt * 128:(kt + 1) * 128], in_=src_bf[:, kt * 128:(kt + 1) * 128],
                                         identity=idb[:])) for kt in range(8)]
    P.mm_group(fns, reads=[bsrc, bidb], writes=[bptr])
    src3 = ptr[:].rearrange("p (k t) -> p k t", k=8)
    if eng == "act":
        P.op("act", lambda E: E.copy(out=dst3, in_=src3), reads=[bptr], writes=[bdst])
    else:
        P.op("dve", lambda E: E.tensor_copy(out=dst3, in_=src3), reads=[bptr], writes=[bdst])


def outproj_post(C, catT, bcat, nkt, wout, bwout, t, xres, bxres, npw, bnpw, pso, bpso, yo, byo, junk, bjunk, ss, bss,
                 out_dram_rows, bout):
    P = C.P
    for hh in range(2):
        fns = [(lambda E, kt=kt, hh=hh: E.matmul(pso[hh][:], lhsT=catT[:, kt, t * 128:(t + 1) * 128],
                                                 rhs=wout[:, kt, hh * 512:(hh + 1) * 512],
                                                 start=(kt == 0), stop=(kt == nkt - 1))) for kt in range(nkt)]
        P.mm_group(fns, reads=[bcat, bwout], writes=[bpso[hh]])
        P.op("act", lambda E, hh=hh: E.copy(out=yo[:, hh * 512:(hh + 1) * 512], in_=pso[hh][:]),
             reads=[bpso[hh]], writes=[byo])
    rms_rstd(C, yo[:], byo, 1024, junk[:], bjunk, ss, bss)
    P.op("dve", lambda E: E.scalar_tensor_tensor(out=yo[:], in0=yo[:], scalar=ss[:, 0:1], in1=npw[:],
                                                 op0=ALU.mult, op1=ALU.mult), reads=[byo, bss, bnpw], writes=[byo])
    P.op("dve", lambda E: E.tensor_tensor(out=yo[:], in0=yo[:], in1=xres, op=ALU.add), reads=[byo, bxres], writes=[byo])
    P.dma("sp", out_dram_rows, yo[:], reads=[byo], writes=[bout])


def load_w_bf16(C, name, dram_w, kt_n, ncols, chunk=2048, groups=None):
    w, _ = C.sb(name, [128, kt_n, ncols], BF16)
    src = dram_w.rearrange("(k p) c -> p k c", p=128)
    if groups is None:
        bw = Buf(name, multi=True)
        for kt in range(kt_n):
            for c0 in range(0, ncols, chunk):
                c1 = min(ncols, c0 + chunk)
                C.P.dma("pool", w[:, kt, c0:c1], src[:, kt, c0:c1], writes=[bw])
        return w, bw
    bws = []
    for gi, sls in enumerate(groups):
        bg = Buf("%s_g%d" % (name, gi), multi=True)
        for (c0, c1) in sls:
            for kt in range(kt_n):
                C.P.dma("pool", w[:, kt, c0:c1], src[:, kt, c0:c1], writes=[bg])
        bws.append(bg)
    return w, bws


def build_L2(ntok=2048, fz=None):
    nc = fz["nc"] if fz else bass.Bass("TRN2", target_bir_lowering=False)
    pfx = fz["pfx"] if fz else ""

    def D(name, shape):
        if fz and name in fz["share"]:
            return fz["share"][name]
        return nc.dram_tensor(pfx + name, shape, F32, kind="ExternalInput").ap()
    x_d = D("x", [ntok, 1024]); o_d = D("o", [ntok, 1024]); ys_d = D("ys", [ntok, 1024])
    wz_d = D("wz", [1024, 2048]); wglu_d = D("wglu", [1024, 1024]); wout_d = D("wout", [2048, 1024])
    npre_d = D("npre", [1024]); npost_d = D("npost", [1024]); gnw_d = D("gnw", [128]); ident_d = D("ident", [128, 128])
    out_d = fz["out"] if fz else nc.dram_tensor("out", [ntok, 1024], F32, kind="ExternalOutput").ap()
    NT = 512
    with ExitStack() as st:
        C = Ctx(nc, st, fz["P"], pfx) if fz else Ctx(nc, st); P = C.P
        idf, bidf, idb, bidb = make_ident(C, ident_d)
        npre, bnpre = bcast_row_load(C, "npre", npre_d, 1024)
        npost, bnpost = bcast_row_load(C, "npost", npost_d, 1024)
        gnw, bgnw = bcast_row_load(C, "gnw", gnw_d, 128)
        wz, bwz = load_w_bf16(C, "wz", wz_d, 8, 2048)
        wglu, bwglu = load_w_bf16(C, "wglu", wglu_d, 8, 1024)
        wout, bwout = load_w_bf16(C, "wout", wout_d, 16, 1024)
        xt4, bxt4 = C.sb("xt4", [128, 4, 1024]); bxt = [Buf("xt%d" % i) for i in range(4)]
        ldo = [C.sb("ldo%d" % i, [128, 1024]) for i in range(2)]
        ldy = [C.sb("ldy%d" % i, [128, 1024], BF16 if (fz and fz.get("ybf16")) else F32) for i in range(2)]
        for (_t, _b) in ldo + ldy:
            _b.multi = True; _b.w = []
        sq, bsq = C.sb("sq", [128, 1024])
        hn, bhn = C.sb("hn", [128, 1024], BF16)
        ss, bss = C.sb("ss", [128, 1])
        ss8, bss8 = C.sb("ss8", [128, 8])
        hT, bhT = C.sb("hT", [128, 8, NT], BF16)
        oT, boT = C.sb("oT", [128, 8, NT], BF16)
        yT, byT = C.sb("yT", [128, 8, NT], BF16)
        gz, bgz = C.sb("gz", [128, 8, NT], BF16)
        sg, bsg = C.sb("sg", [128, NT], BF16)
        catT, bcat = C.sb("catT", [128, 16, NT], BF16)
        yo, byo = C.sb("yo", [128, 1024])
        ptr, bptr = C.ps("ptr", [128, 1024], BF16)
        pmm = []; bpmm = []
        for i in range(4):
            t_, b_ = C.ps("pmm%d" % i, [128, 512]); pmm.append(t_); bpmm.append(b_)
        pso = []; bpso = []
        for i in range(2):
            t_, b_ = C.ps("pso%d" % i, [128, 512]); pso.append(t_); bpso.append(b_)
        bout = fz["obuf"] if fz else Buf("out", multi=True)
        if fz:
            sts = [(0, 128)] + [(128 + i * NT, NT) for i in range((ntok - 128) // NT)]
        else:
            sts = [(i * NT, NT) for i in range(ntok // NT)]
        tile_r0 = [t0_ + t_ * 128 for (t0_, n_) in sts for t_ in range(n_ // 128)]

        def issue_loads(ti):
            r0_ = tile_r0[ti]
            lo, blo = ldo[ti % 2]; ly, bly = ldy[ti % 2]
            if fz:
                fz["gather"](P, lo, blo, r0_ // 128, 0)
                fz["gather"](P, ly, bly, r0_ // 128, 1)
            else:
                P.dma("sp", lo[:], o_d[r0_:r0_ + 128, :], writes=[blo])
                P.dma("sp", ly[:], ys_d[r0_:r0_ + 128, :], writes=[bly])

        issue_loads(0)
        for (t0, n) in sts:
            ntl = n // 128
            for t in range(ntl):
                r0 = t0 + t * 128
                ti = tile_r0.index(r0)
                if ti + 1 < len(tile_r0):
                    issue_loads(ti + 1)
                P.dma("sp", xt4[:, t, :], x_d[r0:r0 + 128, :], writes=[bxt[t]])
                rms_rstd(C, xt4[:, t, :], bxt[t], 1024, sq[:], bsq, ss, bss)
                P.op("dve", lambda E, t=t: E.scalar_tensor_tensor(out=hn[:], in0=xt4[:, t, :], scalar=ss[:, 0:1], in1=npre[:],
                                                                  op0=ALU.mult, op1=ALU.mult), reads=[bxt[t], bss, bnpre], writes=[bhn])
                transpose8(C, hn, bhn, idb, bidb, ptr, bptr, hT[:, :, t * 128:(t + 1) * 128], bhT, eng="act")
                ld, bld = ldo[ti % 2]
                P.op("act", lambda E, ld=ld: E.activation(out=sq[:], in_=ld[:], func=AF.Square), reads=[bld], writes=[bsq])
                P.op("dve", lambda E: E.tensor_reduce(out=ss8[:], in_=sq[:].rearrange("p (h d) -> p h d", h=8), axis=AX.X, op=ALU.add),
                     reads=[bsq], writes=[bss8])
                P.op("dve", lambda E: E.tensor_scalar(out=ss8[:], in0=ss8[:], scalar1=1.0 / 128, scalar2=1e-6, op0=ALU.mult, op1=ALU.add),
                     reads=[bss8], writes=[bss8])
                P.op("act", lambda E: E.activation(out=ss8[:], in_=ss8[:], func=AF.Sqrt), reads=[bss8], writes=[bss8])
                P.op("dve", lambda E: E.reciprocal(out=ss8[:], in_=ss8[:]), reads=[bss8], writes=[bss8])
                P.op("dve", lambda E, ld=ld: E.tensor_tensor(out=sq[:].rearrange("p (h d) -> p h d", h=8), in0=ld[:].rearrange("p (h d) -> p h d", h=8),
                                                      in1=ss8[:].unsqueeze(2).to_broadcast([128, 8, 128]), op=ALU.mult),
                     reads=[bld, bss8], writes=[bsq])
                P.op("dve", lambda E: E.tensor_tensor(out=hn[:].rearrange("p (h d) -> p h d", h=8), in0=sq[:].rearrange("p (h d) -> p h d", h=8),
                                                      in1=gnw[:].unsqueeze(1).to_broadcast([128, 8, 128]), op=ALU.mult),
                     reads=[bsq, bgnw], writes=[bhn])
                transpose8(C, hn, bhn, idb, bidb, ptr, bptr, oT[:, :, t * 128:(t + 1) * 128], boT, eng="act")
                ld, bld = ldy[ti % 2]
                P.op("act", lambda E, ld=ld: E.activation(out=hn[:], in_=ld[:], func=AF.Gelu_apprx_tanh), reads=[bld], writes=[bhn])
                transpose8(C, hn, bhn, idb, bidb, ptr, bptr, yT[:, :, t * 128:(t + 1) * 128], byT, eng="dve")
            for ct in range(16):
                pb = pmm[ct % 4]; bpb = bpmm[ct % 4]
                fns = [(lambda E, kt=kt, ct=ct, pb=pb, n=n: E.matmul(pb[:, 0:n], lhsT=wz[:, kt, ct * 128:(ct + 1) * 128], rhs=hT[:, kt, 0:n],
                                                                start=(kt == 0), stop=(kt == 7))) for kt in range(8)]
                P.mm_group(fns, reads=[bwz, bhT], writes=[bpb])
                if ct < 8:
                    P.op("act", lambda E, pb=pb, n=n: E.activation(out=sg[:, 0:n], in_=pb[:, 0:n], func=AF.Silu), reads=[bpb], writes=[bsg])
                    P.op("dve", lambda E, ct=ct, n=n: E.tensor_tensor(out=catT[:, ct, 0:n], in0=oT[:, ct, 0:n], in1=sg[:, 0:n], op=ALU.mult),
                         reads=[boT, bsg], writes=[bcat])
                else:
                    P.op("act", lambda E, pb=pb, ct=ct, n=n: E.activation(out=gz[:, ct - 8, 0:n], in_=pb[:, 0:n], func=AF.Silu), reads=[bpb], writes=[bgz])
            for ct in range(8):
                pb = pmm[ct % 4]; bpb = bpmm[ct % 4]
                fns = [(lambda E, kt=kt, ct=ct, pb=pb, n=n: E.matmul(pb[:, 0:n], lhsT=wglu[:, kt, ct * 128:(ct + 1) * 128], rhs=yT[:, kt, 0:n],
                                                                start=(kt == 0), stop=(kt == 7))) for kt in range(8)]
                P.mm_group(fns, reads=[bwglu, byT], writes=[bpb])
                P.op("act", lambda E, pb=pb, n=n: E.activation(out=sg[:, 0:n], in_=pb[:, 0:n], func=AF.Sigmoid), reads=[bpb], writes=[bsg])
                P.op("dve", lambda E, ct=ct, n=n: E.tensor_tensor(out=sg[:, 0:n], in0=sg[:, 0:n], in1=yT[:, ct, 0:n], op=ALU.mult), reads=[bsg, byT], writes=[bsg])
                P.op("dve", lambda E, ct=ct, n=n: E.tensor_tensor(out=catT[:, 8 + ct, 0:n], in0=sg[:, 0:n], in1=gz[:, ct, 0:n], op=ALU.mult),
                     reads=[bsg, bgz], writes=[bcat])
            for t in range(ntl):
                r0 = t0 + t * 128
                outproj_post(C, catT, bcat, 16, wout, bwout, t, xt4[:, t, :], bxt[t], npost, bnpost, pso, bpso, yo, byo, sq, bsq, ss, bss,
                             out_d[r0:r0 + 128, :], bout)
        if fz:
            barrier(P)
        else:
            P.finish([bout])
    return nc


def build_L3(ntok=2048, fz=None):
    nc = fz["nc"] if fz else bass.Bass("TRN2", target_bir_lowering=False)
    pfx = fz["pfx"] if fz else ""

    def D(name, shape):
        if fz and name in fz["share"]:
            return fz["share"][name]
        return nc.dram_tensor(pfx + name, shape, F32, kind="ExternalInput").ap()
    x_d = D("x", [ntok + 128, 1024])
    win_d = D("win", [1024, 8192]); wout_d = D("wout", [2048, 1024]); conv_d = D("conv", [3, 2048])
    npre_d = D("npre", [1024]); npost_d = D("npost", [1024]); ident_d = D("ident", [128, 128])
    out_d = fz["out"] if fz else nc.dram_tensor("out", [ntok, 1024], F32, kind="ExternalOutput").ap()
    NT = 512
    HN = 256
    with ExitStack() as st:
        C = Ctx(nc, st, fz["P"], pfx) if fz else Ctx(nc, st); P = C.P
        idf, bidf, idb, bidb = make_ident(C, ident_d)
        npre, bnpre = bcast_row_load(C, "npre", npre_d, 1024)
        npost, bnpost = bcast_row_load(C, "npost", npost_d, 1024)
        cw, bcw = C.sb("cw", [128, 3, 16])
        P.dma("sp", cw[:], conv_d.rearrange("j (c p) -> p j c", p=128), writes=[bcw])
        win, bwin_g = load_w_bf16(C, "win", win_d, 8, 8192,
                                  groups=[[(part * 2048 + cg * 512, part * 2048 + cg * 512 + 512) for part in (1, 2)] for cg in range(4)] +
                                         [[(part * 2048 + cg * 512, part * 2048 + cg * 512 + 512) for part in (0, 3)] for cg in range(4)])
        wout, bwout = load_w_bf16(C, "wout", wout_d, 16, 1024)
        xt, bxt = C.sb("xt", [128, 1024])
        hn, bhn = C.sb("hn", [128, 1024], BF16)
        ss, bss = C.sb("ss", [128, 1])
        hT, bhT = C.sb("hT", [128, 8, NT], BF16)
        y1T, by1T = C.sb("y1T", [128, 16, NT], BF16)
        pbuf, bpbuf = C.sb("pbuf", [128, HN + 2])
        phalo, bphalo = C.sb("phalo", [128, 16, 2])
        gcs, bgcs = C.sb("gcs", [128, HN])
        cv, bcv = C.sb("cv", [128, HN])
        sz, bsz = C.sb("sz", [128, HN])
        yo, byo = C.sb("yo", [128, 1024])
        P.op("dve", lambda E: E.memset(phalo[:], 0.0), writes=[bphalo])
        ptr, bptr = C.ps("ptr", [128, 1024], BF16)
        GB = [C.ps("g%d" % i, [128, 512]) for i in range(7)]
        pso = [GB[0][0], GB[1][0]]; bpso = [GB[0][1], GB[1][1]]
        bout = fz["obuf"] if fz else Buf("out", multi=True)
        sts = [(0, 128)] + [(128 + i * NT, NT) for i in range(ntok // NT)]
        for (t0, n) in sts:
            ntl = n // 128
            for t in range(ntl):
                r0 = t0 + t * 128
                P.dma("sp", xt[:], x_d[r0:r0 + 128, :], reads=([fz["xbuf"]] if fz else []), writes=[bxt])
                rms_rstd(C, xt[:], bxt, 1024, hn[:], bhn, ss, bss)
                P.op("dve", lambda E: E.scalar_tensor_tensor(out=hn[:], in0=xt[:], scalar=ss[:, 0:1], in1=npre[:],
                                                             op0=ALU.mult, op1=ALU.mult), reads=[bxt, bss, bnpre], writes=[bhn])
                transpose8(C, hn, bhn, idb, bidb, ptr, bptr, hT[:, :, t * 128:(t + 1) * 128], bhT, eng="act")
            for ct in range(16):
                sel_ = [GB[3 * (ct % 2) + 0], GB[3 * (ct % 2) + 1], GB[3 * (ct % 2) + 2], GB[6]]
                pmm = [x_[0] for x_ in sel_]; bpmm = [x_[1] for x_ in sel_]
                for part in ((1, 2) if t0 == 0 else range(4)):
                    col0 = (part * 16 + ct) * 128
                    pb = pmm[part]
                    fns = [(lambda E, n=n, kt=kt, col0=col0, pb=pb: E.matmul(pb[:, 0:n], lhsT=win[:, kt, col0:col0 + 128], rhs=hT[:, kt, 0:n],
                                                                        start=(kt == 0), stop=(kt == 7))) for kt in range(8)]
                    P.mm_group(fns, reads=[bwin_g[(0 if part in (1, 2) else 4) + ct // 4], bhT], writes=[bpmm[part]])
                for h0 in range(0, n, HN):
                    nn = min(HN, n - h0)
                    P.op("act", lambda E, nn=nn, h0=h0, pmm=pmm: E.copy(out=gcs[:, 0:nn], in_=pmm[1][:, h0:h0 + nn]), reads=[bpmm[1]], writes=[bgcs])
                    P.op("act", lambda E, ct=ct: E.copy(out=pbuf[:, 0:2], in_=phalo[:, ct, :]), reads=[bphalo], writes=[bpbuf])
                    P.op("dve", lambda E, nn=nn, h0=h0, pmm=pmm: E.tensor_tensor(out=pbuf[:, 2:2 + nn], in0=gcs[:, 0:nn], in1=pmm[2][:, h0:h0 + nn], op=ALU.mult),
                         reads=[bgcs, bpmm[2]], writes=[bpbuf])
                    P.op("act", lambda E, nn=nn, ct=ct: E.copy(out=phalo[:, ct, :], in_=pbuf[:, nn:nn + 2]), reads=[bpbuf], writes=[bphalo])
                    if t0 == 0:
                        continue
                    P.op("dve", lambda E, nn=nn, ct=ct: E.tensor_scalar(out=cv[:, 0:nn], in0=pbuf[:, 0:nn], scalar1=cw[:, 0, ct:ct + 1], scalar2=None, op0=ALU.mult),
                         reads=[bpbuf, bcw], writes=[bcv])
                    P.op("dve", lambda E, nn=nn, ct=ct: E.scalar_tensor_tensor(out=cv[:, 0:nn], in0=pbuf[:, 1:1 + nn], scalar=cw[:, 1, ct:ct + 1], in1=cv[:, 0:nn],
                                                                               op0=ALU.mult, op1=ALU.add), reads=[bpbuf, bcw, bcv], writes=[bcv])
                    P.op("dve", lambda E, nn=nn, ct=ct: E.scalar_tensor_tensor(out=cv[:, 0:nn], in0=pbuf[:, 2:2 + nn], scalar=cw[:, 2, ct:ct + 1], in1=cv[:, 0:nn],
                                                                               op0=ALU.mult, op1=ALU.add), reads=[bpbuf, bcw, bcv], writes=[bcv])
                    P.op("dve", lambda E, nn=nn, h0=h0, pmm=pmm: E.tensor_tensor(out=cv[:, 0:nn], in0=cv[:, 0:nn], in1=pmm[0][:, h0:h0 + nn], op=ALU.mult),
                         reads=[bcv, bpmm[0]], writes=[bcv])
                    P.op("act", lambda E, nn=nn, h0=h0, pmm=pmm: E.activation(out=sz[:, 0:nn], in_=pmm[3][:, h0:h0 + nn], func=AF.Silu), reads=[bpmm[3]], writes=[bsz])
                    P.op("dve", lambda E, nn=nn, h0=h0, ct=ct: E.tensor_tensor(out=y1T[:, ct, h0:h0 + nn], in0=cv[:, 0:nn], in1=sz[:, 0:nn], op=ALU.mult),
                         reads=[bcv, bsz], writes=[by1T])
            if t0 == 0:
                continue
            for t in range(ntl):
                r0 = t0 + t * 128
                P.dma("sp", xt[:], x_d[r0:r0 + 128, :], reads=([fz["xbuf"]] if fz else []), writes=[bxt])
                outproj_post(C, y1T, by1T, 16, wout, bwout, t, xt[:], bxt, npost, bnpost, pso, bpso, yo, byo, hn, bhn, ss, bss,
                             out_d[r0 - 128:r0, :], bout)
        if fz:
            barrier(P)
        else:
            P.finish([bout])
    return nc


_IDENT = np.eye(128, dtype=np.float32)
_CACHE = {}


def _get(name, fn):
    if name not in _CACHE:
        _CACHE[name] = fn()
    return _CACHE[name]


def run_L2(inp, o_full, ys_full):
    nc = _get("L2", build_L2)
    w_in = inp["w_in_even"][0]
    wz = np.ascontiguousarray(np.concatenate([w_in[:, 3072:4096], w_in[:, 5136:6160]], axis=1))
    maps = []
    for c in range(8):
        b, r = divmod(c, 4)
        sl = slice(r * 2048, (r + 1) * 2048)
        maps.append({"x": np.ascontiguousarray(inp["x"][b, sl]), "o": np.ascontiguousarray(o_full[b, sl]),
                     "ys": np.ascontiguousarray(ys_full[b, sl]), "wz": wz, "wglu": np.ascontiguousarray(inp["w_glu"][0]),
                     "wout": np.ascontiguousarray(inp["w_out_even"][0]), "npre": np.ascontiguousarray(inp["norm_pre"][0]),
                     "npost": np.ascontiguousarray(inp["norm_post"][0]), "gnw": np.ascontiguousarray(inp["gdn_norm_w"][0]),
                     "ident": _IDENT})
    res = run_bass_kernel_spmd(nc, maps, core_ids=list(range(8)))
    x1 = np.empty((2, 8192, 1024), np.float32)
    for c in range(8):
        b, r = divmod(c, 4)
        x1[b, r * 2048:(r + 1) * 2048] = res.results[c]["out"]
    return x1


def run_L3(inp, x1):
    nc = _get("L3", build_L3)
    maps = []
    for c in range(8):
        b, r = divmod(c, 4)
        xh = np.zeros((2048 + 128, 1024), np.float32)
        xh[128:] = x1[b, r * 2048:(r + 1) * 2048]
        if r > 0:
            xh[:128] = x1[b, r * 2048 - 128:r * 2048]
        maps.append({"x": xh, "win": np.ascontiguousarray(inp["w_in_odd"][0]), "wout": np.ascontiguousarray(inp["w_out_odd"][0]),
                     "conv": np.ascontiguousarray(inp["conv_short"][0]), "npre": np.ascontiguousarray(inp["norm_pre"][1]),
                     "npost": np.ascontiguousarray(inp["norm_post"][1]), "ident": _IDENT})
    res = run_bass_kernel_spmd(nc, maps, core_ids=list(range(8)))
    out = np.empty((2, 8192, 1024), np.float32)
    for c in range(8):
        b, r = divmod(c, 4)
        out[b, r * 2048:(r + 1) * 2048] = res.results[c]["out"]
    return out


I32 = mybir.dt.int32
TAUS = np.array(list(range(17)) + [32, 64, 128, 256, 512, 1024, 2048, 4096] + list(range(15, -1, -1)), np.float32)
NTAU = len(TAUS)


def _s5_consts():
    mk = np.zeros((128, 2, 16, 16), np.float32)
    idm = np.zeros((128, 2, 16, 16), np.float32)
    for kt2 in range(2):
        for sp in range(8):
            s = kt2 * 8 + sp
            for h in range(16):
                mk[sp * 16 + h, kt2, s:, :] = 1.0
                idm[sp * 16 + h, kt2, s, h] = 1.0
    return mk.reshape(128, 2, 256), idm.reshape(128, 2, 256)


def barrier(P):
    for e in P.ENG:
        for e2 in P.ENG:
            if P.cnt[e2] > 0:
                P._wait(e, ("c", e2, P.cnt[e2]))
        for q in P.dsem:
            j1 = P.dcnt[q]
            for j in range(max(0, j1 - NDS), j1):
                P._wait(e, ("d", q, j % NDS, 16 * (j // NDS + 1)))


def build_L1b(S=8192, fz=None):
    nc = fz["nc"] if fz else bass.Bass("TRN2", target_bir_lowering=False)
    pfx = fz["pfx"] if fz else ""

    def D(name, shape):
        if fz and name in fz["share"]:
            return fz["share"][name]
        return nc.dram_tensor(pfx + name, shape, F32, kind="ExternalInput").ap()
    x_d = D("x", [S, 1024]); npre_d = D("npre", [1024]); wu_d = D("wu", [1024, 256])
    lre_d = D("lre", [16, 64]); lim_d = D("lim", [16, 64]); bre_d = D("bre", [16, 64, 16]); bim_d = D("bim", [16, 64, 16])
    cre_d = D("cre", [16, 16, 64]); cim_d = D("cim", [16, 16, 64]); ldt_d = D("ldt", [16]); dd_d = D("dd", [256])
    taus_d = D("taus", [NTAU]); mk_d = D("mk", [128, 2, 256]); idm_d = D("idm", [128, 2, 256]); ident_d = D("ident", [128, 128])
    ys_d = fz["out"] if fz else nc.dram_tensor("ys", [S, 256], F32, kind="ExternalOutput").ap()
    NCH = S // 16
    NST = S // 512
    with ExitStack() as st:
        C = Ctx(nc, st, fz["P"], pfx) if fz else Ctx(nc, st); P = C.P
        idf, bidf, idb, bidb = make_ident(C, ident_d)
        ptr, bptr = C.ps("ptr", [128, 1024], BF16)
        py, bpy = C.ps("py", [128, 1024])
        G = []; bG = []
        for i in range(4):
            t_, b_ = C.ps("g%d" % i, [128, 512]); G.append(t_); bG.append(b_)
        U, bU = C.sb("U", [128, 2, 16, NCH], BF16)
        with ExitStack() as st2:
            C2 = Ctx(nc, st2, P, C.pfx)
            ext = fz.get("uTp") if fz else None
            if ext:
                uTp, buTp = ext
            else:
                uTp, buTp = C2.sb("uTp", [128, 2, 16, NCH], BF16)
            with ExitStack() as st1:
                C1 = Ctx(nc, st1, P, C.pfx)
                npre, bnpre = bcast_row_load(C1, "npre", npre_d, 1024)
                wu, bwu = load_w_bf16(C1, "wu", wu_d, 8, 256)
                xt, bxt = C1.sb("xt", [128, 1024])
                sq, bsq = C1.sb("sq", [128, 1024])
                hn, bhn = C1.sb("hn", [128, 1024], BF16)
                ss, bss = C1.sb("ss", [128, 1])
                hT, bhT = C1.sb("hT", [128, 8, 512], BF16)
                for s_ in range(0 if ext else NST):
                    for t in range(4):
                        r0 = s_ * 512 + t * 128
                        P.dma("sp", xt[:], x_d[r0:r0 + 128, :], writes=[bxt])
                        rms_rstd(C1, xt[:], bxt, 1024, sq[:], bsq, ss, bss)
                        P.op("dve", lambda E: E.scalar_tensor_tensor(out=hn[:], in0=xt[:], scalar=ss[:, 0:1], in1=npre[:],
                                                                     op0=ALU.mult, op1=ALU.mult), reads=[bxt, bss, bnpre], writes=[bhn])
                        transpose8(C1, hn, bhn, idb, bidb, ptr, bptr, hT[:, :, t * 128:(t + 1) * 128], bhT, eng="act")
                    for blk in range(2):
                        pb = G[blk]
                        fns = [(lambda E, kt=kt, blk=blk, pb=pb: E.matmul(
                            pb[:].rearrange("p (s n) -> p s n", s=16), lhsT=wu[:, kt, blk * 128:(blk + 1) * 128],
                            rhs=hT[:, kt, :].rearrange("p (n s) -> p s n", s=16), start=(kt == 0), stop=(kt == 7))) for kt in range(8)]
                        P.mm_group(fns, reads=[bwu, bhT], writes=[bG[blk]])
                        P.op("act" if blk == 0 else "dve",
                             (lambda E, blk=blk, pb=pb, s_=s_: E.copy(out=uTp[:, blk, :, 32 * s_:32 * s_ + 32], in_=pb[:].rearrange("p (s n) -> p s n", s=16)))
                             if blk == 0 else
                             (lambda E, blk=blk, pb=pb, s_=s_: E.tensor_copy(out=uTp[:, blk, :, 32 * s_:32 * s_ + 32], in_=pb[:].rearrange("p (s n) -> p s n", s=16))),
                             reads=[bG[blk]], writes=[buTp])
                barrier(P)
            ud2 = nc.dram_tensor(pfx + "ud2", [16, 2, 8, 16, NCH], BF16)
            bud2 = Buf("ud2", multi=True)
            bU.multi = True; bU.w = []
            for g in range(16):
                P.dma("sp", ud2.ap()[g].rearrange("k sp h n -> h (k sp) n"),
                      uTp[(g % 8) * 16:(g % 8 + 1) * 16, g // 8, :, :], reads=[buTp], writes=[bud2])
            for g in range(16):
                P.dma("sp", U[:, :, g, :], ud2.ap()[g].rearrange("k sp h n -> (sp h) k n"), reads=[bud2], writes=[bU])
            barrier(P)
        lre, blre = C.sb("lre", [128, 8]); lim, blim = C.sb("lim", [128, 8]); ldt, bldt = C.sb("ldt", [128, 8])
        TAU, bTAU = bcast_row_load(C, "TAU", taus_d, NTAU)
        Er, bEr = C.sb("Er", [128, 8, NTAU]); Ei, bEi = C.sb("Ei", [128, 8, NTAU]); NEi, bNEi = C.sb("NEi", [128, 8, NTAU])
        Hr, bHr = C.sb("Hr", [128, 8, 17, 16]); nHi, bnHi = C.sb("nHi", [128, 8, 17, 16])
        WbT, bWbT = C.sb("WbT", [128, 2, 8, 2, 128], BF16)
        Toep, bToep = C.sb("Toep", [128, 2, 16, 256], BF16)
        with ExitStack() as st3:
            C3 = Ctx(nc, st3, P, C.pfx)
            Br, bBr = C3.sb("Br", [128, 8, 16]); Bi, bBi = C3.sb("Bi", [128, 8, 16])
            Cr, bCr = C3.sb("Cr", [128, 8, 16]); Ci, bCi = C3.sb("Ci", [128, 8, 16])
            dcol, bdcol = C3.sb("dcol", [128, 16])
            MK, bMK = C3.sb("MK", [128, 2, 256]); IDM, bIDM = C3.sb("IDM", [128, 2, 256])
            P.dma("sp", MK[:], mk_d, writes=[bMK]); P.dma("sp", IDM[:], idm_d, writes=[bIDM])
            for _b in (blre, blim, bldt, bBr, bBi, bCr, bCi, bdcol):
                _b.multi = True; _b.w = []
            for two in range(2):
                hs = slice(64 * two, 64 * two + 64)
                P.dma("sp", lre[hs, :], lre_d.rearrange("(gp two) p -> two p gp", two=2)[two], writes=[blre])
                P.dma("sp", lim[hs, :], lim_d.rearrange("(gp two) p -> two p gp", two=2)[two], writes=[blim])
                P.dma("sp", ldt[hs, :], ldt_d.rearrange("(gp two) -> two gp", two=2)[two].partition_broadcast(64), writes=[bldt])
                P.dma("sp", Br[hs], bre_d.rearrange("(gp two) p h -> two p gp h", two=2)[two], writes=[bBr])
                P.dma("sp", Bi[hs], bim_d.rearrange("(gp two) p h -> two p gp h", two=2)[two], writes=[bBi])
                for gp in range(8):
                    P.dma("sp", Cr[hs, gp, :], cre_d[2 * gp + two].rearrange("h p -> p h"), writes=[bCr])
                    P.dma("sp", Ci[hs, gp, :], cim_d[2 * gp + two].rearrange("h p -> p h"), writes=[bCi])
            for sp in range(8):
                P.dma("sp", dcol[sp * 16:(sp + 1) * 16, :], dd_d.rearrange("(g h) -> h g", h=16), writes=[bdcol])
            sm = {}
            for nm in ("dt", "lr", "lrdt", "th", "den", "nr", "fre", "fim", "t8a", "t8b"):
                sm[nm] = C3.sb("sm_" + nm, [128, 8])
            T41 = {}
            for nm in ("ARG", "MARG", "MAG", "MAGN", "SIN", "COS", "ErN", "EiN", "rt", "rk"):
                T41[nm] = C3.sb("t41_" + nm, [128, 8, NTAU])
            rki, brki = C3.sb("rki", [128, 8, NTAU], I32)

            def tt(eng, out, bo, a, ba, b, bb_, op):
                P.op(eng, lambda E: E.tensor_tensor(out=out, in0=a, in1=b, op=op), reads=[ba, bb_], writes=[bo])

            dt, bdt = sm["dt"]; lr, blr = sm["lr"]; lrdt, blrdt = sm["lrdt"]; th, bth = sm["th"]
            P.op("act", lambda E: E.activation(out=dt[:], in_=ldt[:], func=AF.Exp), reads=[bldt], writes=[bdt])
            P.op("dve", lambda E: E.tensor_scalar(out=lr[:], in0=lre[:], scalar1=-1e-4, scalar2=None, op0=ALU.min), reads=[blre], writes=[blr])
            tt("dve", lrdt[:], blrdt, lr[:], blr, dt[:], bdt, ALU.mult)
            tt("dve", th[:], bth, lim[:], blim, dt[:], bdt, ALU.mult)
            ARG, bARG = T41["ARG"]; MARG, bMARG = T41["MARG"]; MAG, bMAG = T41["MAG"]; MAGN, bMAGN = T41["MAGN"]
            SIN, bSIN = T41["SIN"]; COS, bCOS = T41["COS"]; ErN, bErN = T41["ErN"]; EiN, bEiN = T41["EiN"]
            rt, brt = T41["rt"]; rk, brk = T41["rk"]
            tb = TAU[:].unsqueeze(1).to_broadcast([128, 8, NTAU])
            tt("dve", ARG[:], bARG, th[:].unsqueeze(2).to_broadcast([128, 8, NTAU]), bth, tb, bTAU, ALU.mult)
            tt("dve", MARG[:], bMARG, lrdt[:].unsqueeze(2).to_broadcast([128, 8, NTAU]), blrdt, tb, bTAU, ALU.mult)
            P.op("act", lambda E: E.activation(out=MAG[:], in_=MARG[:], func=AF.Exp), reads=[bMARG], writes=[bMAG])
            P.op("act", lambda E: E.activation(out=MAGN[:, :, 0:17], in_=MARG[:, :, 0:17], func=AF.Exp, scale=-1.0), reads=[bMARG], writes=[bMAGN])

            def sin_of(dst, bdst, shift):
                P.op("dve", lambda E: E.tensor_scalar(out=rt[:], in0=ARG[:], scalar1=float(shift), scalar2=None, op0=ALU.add), reads=[bARG], writes=[brt])
                P.op("dve", lambda E: E.tensor_scalar(out=rki[:], in0=rt[:], scalar1=float(1.0 / (2 * np.pi)), scalar2=None, op0=ALU.mult), reads=[brt], writes=[brki])
                P.op("dve", lambda E: E.tensor_copy(out=rk[:], in_=rki[:]), reads=[brki], writes=[brk])
                P.op("dve", lambda E: E.scalar_tensor_tensor(out=rt[:], in0=rk[:], scalar=float(-2 * np.pi), in1=rt[:], op0=ALU.mult, op1=ALU.add),
                     reads=[brk, brt], writes=[brt])
                P.op("dve", lambda E: E.tensor_scalar(out=rt[:], in0=rt[:], scalar1=-3.14159, scalar2=3.14159, op0=ALU.max, op1=ALU.min), reads=[brt], writes=[brt])
                P.op("act", lambda E: E.activation(out=dst[:], in_=rt[:], func=AF.Sin), reads=[brt], writes=[bdst])

            sin_of(SIN, bSIN, 0.0)
            sin_of(COS, bCOS, np.pi / 2)
            tt("dve", Er[:], bEr, MAG[:], bMAG, COS[:], bCOS, ALU.mult)
            tt("dve", Ei[:], bEi, MAG[:], bMAG, SIN[:], bSIN, ALU.mult)
            P.op("dve", lambda E: E.tensor_scalar(out=NEi[:], in0=Ei[:], scalar1=-1.0, scalar2=None, op0=ALU.mult), reads=[bEi], writes=[bNEi])
            tt("dve", ErN[:, :, 0:17], bErN, MAGN[:, :, 0:17], bMAGN, COS[:, :, 0:17], bCOS, ALU.mult)
            tt("dve", EiN[:, :, 0:17], bEiN, MAGN[:, :, 0:17], bMAGN, SIN[:, :, 0:17], bSIN, ALU.mult)
            P.op("dve", lambda E: E.tensor_scalar(out=EiN[:, :, 0:17], in0=EiN[:, :, 0:17], scalar1=-1.0, scalar2=None, op0=ALU.mult), reads=[bEiN], writes=[bEiN])
            den, bden = sm["den"]; nr, bnr = sm["nr"]; fre, bfre = sm["fre"]; fim, bfim = sm["fim"]; t8a, bt8a = sm["t8a"]; t8b, bt8b = sm["t8b"]
            tt("dve", den[:], bden, lr[:], blr, lr[:], blr, ALU.mult)
            tt("dve", t8a[:], bt8a, lim[:], blim, lim[:], blim, ALU.mult)
            tt("dve", den[:], bden, den[:], bden, t8a[:], bt8a, ALU.add)
            P.op("dve", lambda E: E.reciprocal(out=den[:], in_=den[:]), reads=[bden], writes=[bden])
            P.op("dve", lambda E: E.tensor_scalar(out=nr[:], in0=Er[:, :, 1], scalar1=-1.0, scalar2=None, op0=ALU.add), reads=[bEr], writes=[bnr])
            tt("dve", fre[:], bfre, nr[:], bnr, lr[:], blr, ALU.mult)
            tt("dve", t8a[:], bt8a, Ei[:, :, 1], bEi, lim[:], blim, ALU.mult)
            tt("dve", fre[:], bfre, fre[:], bfre, t8a[:], bt8a, ALU.add)
            tt("dve", fre[:], bfre, fre[:], bfre, den[:], bden, ALU.mult)
            tt("dve", fim[:], bfim, Ei[:, :, 1], bEi, lr[:], blr, ALU.mult)
            tt("dve", t8b[:], bt8b, nr[:], bnr, lim[:], blim, ALU.mult)
            tt("dve", fim[:], bfim, fim[:], bfim, t8b[:], bt8b, ALU.subtract)
            tt("dve", fim[:], bfim, fim[:], bfim, den[:], bden, ALU.mult)

            def cmul(outr, boutr, outi, bouti, ar, bar, ai, bai, br_, bbr_, bi_, bbi_, tmp, btmp):
                tt("dve", outr, boutr, ar, bar, br_, bbr_, ALU.mult)
                tt("dve", tmp, btmp, ai, bai, bi_, bbi_, ALU.mult)
                tt("dve", outr, boutr, outr, boutr, tmp, btmp, ALU.subtract)
                tt("dve", outi, bouti, ar, bar, bi_, bbi_, ALU.mult)
                tt("dve", tmp, btmp, ai, bai, br_, bbr_, ALU.mult)
                tt("dve", outi, bouti, outi, bouti, tmp, btmp, ALU.add)

            bbr, bbbr = C3.sb("bbr", [128, 8, 16]); bbi, bbbi = C3.sb("bbi", [128, 8, 16]); tmp16, btmp16 = C3.sb("tmp16", [128, 8, 16])
            fb = lambda t_: t_[:].unsqueeze(2).to_broadcast([128, 8, 16])
            cmul(bbr[:], bbbr, bbi[:], bbbi, fb(fre), bfre, fb(fim), bfim, Br[:], bBr, Bi[:], bBi, tmp16[:], btmp16)
            Gr, bGr = C3.sb("Gr", [128, 8, 16, 16]); Gi, bGi = C3.sb("Gi", [128, 8, 16, 16])
            WPr, bWPr = C3.sb("WPr", [128, 8, 16, 16]); WPi, bWPi = C3.sb("WPi", [128, 8, 16, 16])
            Hi, bHi = C3.sb("Hi", [128, 8, 17, 16]); tmpH, btmpH = C3.sb("tmpH", [128, 8, 17, 16])
            eb = lambda t_, j0, j1: t_[:, :, j0:j1].unsqueeze(3).to_broadcast([128, 8, j1 - j0, 16])
            vb = lambda t_, n_: t_[:].unsqueeze(2).to_broadcast([128, 8, n_, 16])
            cmul(Gr[:], bGr, Gi[:], bGi, eb(ErN, 0, 16), bErN, eb(EiN, 0, 16), bEiN, vb(bbr, 16), bbbr, vb(bbi, 16), bbbi, tmpH[:, :, 0:16, :], btmpH)
            cmul(WPr[:], bWPr, WPi[:], bWPi, eb(Er, 25, 41), bEr, eb(Ei, 25, 41), bEi, vb(bbr, 16), bbbr, vb(bbi, 16), bbbi, tmpH[:, :, 0:16, :], btmpH)
            cmul(Hr[:], bHr, Hi[:], bHi, eb(Er, 0, 17), bEr, eb(Ei, 0, 17), bEi, vb(Cr, 17), bCr, vb(Ci, 17), bCi, tmpH[:], btmpH)
            P.op("dve", lambda E: E.tensor_scalar(out=nHi[:], in0=Hi[:], scalar1=-1.0, scalar2=None, op0=ALU.mult), reads=[bHi], writes=[bnHi])
            for gp in range(8):
                for kt2 in range(2):
                    for c, (WP_, bWP_) in enumerate(((WPr, bWPr), (WPi, bWPi))):
                        P.op("pe", lambda E, gp=gp, kt2=kt2, WP_=WP_: E.transpose(
                            out=G[2][:, 0:128], in_=WP_[:, gp, kt2 * 8:(kt2 + 1) * 8, :].rearrange("p s h -> p (s h)"), identity=idf[:]),
                            reads=[bWP_, bidf], writes=[bG[2]])
                        P.op("act", lambda E, gp=gp, kt2=kt2, c=c: E.copy(out=WbT[:, kt2, gp, c, :], in_=G[2][:, 0:128]), reads=[bG[2]], writes=[bWbT])
            tmpT, btmpT = C3.sb("tmpT", [128, 256])
            for g in range(16):
                gp = g // 2; hs = slice(64 * (g % 2), 64 * (g % 2) + 64)
                for kt2 in range(2):
                    fns = [
                        lambda E, gp=gp, hs=hs, kt2=kt2: E.matmul(G[3][:, 0:256], lhsT=Gr[hs, gp, kt2 * 8:(kt2 + 1) * 8, :].rearrange("p s h -> p (s h)"),
                                                                  rhs=Hr[hs, gp, 0:16, :].rearrange("p t h -> p (t h)"), start=True, stop=False),
                        lambda E, gp=gp, hs=hs, kt2=kt2: E.matmul(G[3][:, 0:256], lhsT=Gi[hs, gp, kt2 * 8:(kt2 + 1) * 8, :].rearrange("p s h -> p (s h)"),
                                                                  rhs=nHi[hs, gp, 0:16, :].rearrange("p t h -> p (t h)"), start=False, stop=True)]
                    P.mm_group(fns, reads=[bGr, bGi, bHr, bnHi], writes=[bG[3]])
                    P.op("dve", lambda E, kt2=kt2: E.tensor_tensor(out=tmpT[:], in0=G[3][:, 0:256], in1=MK[:, kt2, :], op=ALU.mult),
                         reads=[bG[3], bMK], writes=[btmpT])
                    P.op("dve", lambda E, kt2=kt2, g=g: E.scalar_tensor_tensor(out=Toep[:, kt2, g, :], in0=IDM[:, kt2, :], scalar=dcol[:, g:g + 1], in1=tmpT[:],
                                                                               op0=ALU.mult, op1=ALU.add), reads=[bIDM, bdcol, btmpT], writes=[bToep])
            barrier(P)
        X = {}
        for bufn in ("A", "B"):
            for c in ("re", "im"):
                X[(bufn, c)] = (C.sb("X%s%s" % (bufn, c), [128, 8, NCH + 1])[0], [Buf("X%s%s%d" % (bufn, c, gp)) for gp in range(8)])
        Ysb, bYsb = C.sb("Ysb", [128, 16, 256], BF16 if (fz and fz.get("ybf16")) else F32)
        for key in X:
            t_, bl = X[key]
            P.op("dve", lambda E, t_=t_: E.memset(t_[:, :, 0:1], 0.0), writes=bl)
        for gp in range(8):
            for c, cn in enumerate(("re", "im")):
                px = G[c]
                fns = []
                for two in range(2):
                    g = 2 * gp + two
                    for kt2 in range(2):
                        fns.append(lambda E, two=two, g=g, kt2=kt2, gp=gp, c=c, px=px: E.matmul(
                            px[64 * two:64 * two + 64, :], lhsT=WbT[:, kt2, gp, c, 64 * two:64 * two + 64], rhs=U[:, kt2, g, :],
                            start=(kt2 == 0), stop=(kt2 == 1)))
                P.mm_group(fns, reads=[bWbT, bU], writes=[bG[c]])
                xt_, xb_ = X[("A", cn)]
                P.op("act", lambda E, xt_=xt_, gp=gp, px=px: E.copy(out=xt_[:, gp, 1:NCH + 1], in_=px[:]), reads=[bG[c]], writes=[xb_[gp]])
        for k in range(9):
            d = 1 << k
            j = 16 if k == 0 else 16 + k
            src, dst = ("A", "B") if k % 2 == 0 else ("B", "A")
            sre, bsre = X[(src, "re")]; sim, bsim = X[(src, "im")]
            dre, bdre = X[(dst, "re")]; dim_, bdim = X[(dst, "im")]
            P.op("dve", lambda E, dre=dre, sre=sre, d=d: E.tensor_copy(out=dre[:, :, 1:1 + d], in_=sre[:, :, 1:1 + d]), reads=bsre, writes=bdre)
            P.op("pool", lambda E, dim_=dim_, sim=sim, d=d: E.tensor_copy(out=dim_[:, :, 1:1 + d], in_=sim[:, :, 1:1 + d]), reads=bsim, writes=bdim)
            for gp in range(8):
                lo = slice(1, NCH + 1 - d); hi = slice(1 + d, NCH + 1)
                P.op("dve", lambda E, gp=gp, j=j, dre=dre, sre=sre, lo=lo, hi=hi: E.scalar_tensor_tensor(
                    out=dre[:, gp, hi], in0=sre[:, gp, lo], scalar=Er[:, gp, j:j + 1], in1=sre[:, gp, hi], op0=ALU.mult, op1=ALU.add),
                    reads=[bsre[gp], bEr], writes=[bdre[gp]])
                P.op("dve", lambda E, gp=gp, j=j, dre=dre, sim=sim, lo=lo, hi=hi: E.scalar_tensor_tensor(
                    out=dre[:, gp, hi], in0=sim[:, gp, lo], scalar=NEi[:, gp, j:j + 1], in1=dre[:, gp, hi], op0=ALU.mult, op1=ALU.add),
                    reads=[bsim[gp], bNEi, bdre[gp]], writes=[bdre[gp]])
                P.op("dve", lambda E, gp=gp, j=j, dim_=dim_, sim=sim, lo=lo, hi=hi: E.scalar_tensor_tensor(
                    out=dim_[:, gp, hi], in0=sim[:, gp, lo], scalar=Er[:, gp, j:j + 1], in1=sim[:, gp, hi], op0=ALU.mult, op1=ALU.add),
                    reads=[bsim[gp], bEr], writes=[bdim[gp]])
                P.op("dve", lambda E, gp=gp, j=j, dim_=dim_, sre=sre, lo=lo, hi=hi: E.scalar_tensor_tensor(
                    out=dim_[:, gp, hi], in0=sre[:, gp, lo], scalar=Ei[:, gp, j:j + 1], in1=dim_[:, gp, hi], op0=ALU.mult, op1=ALU.add),
                    reads=[bsre[gp], bEi, bdim[gp]], writes=[bdim[gp]])
        fre_, bfre_ = X[("B", "re")]; fim_, bfim_ = X[("B", "im")]
        bys = None if fz else Buf("ys", multi=True)
        ysv = ys_d.rearrange("(n t) c -> n t c", t=16)
        for jt in range(NCH // 128):
            for gq in range(4):
                fns = []
                for gi in range(4):
                    g = 4 * gq + gi; gp = g // 2; hs = slice(64 * (g % 2), 64 * (g % 2) + 64)
                    o_ = (gi * 256, (gi + 1) * 256)
                    for kt2 in range(2):
                        fns.append(lambda E, o_=o_, g=g, kt2=kt2, jt=jt: E.matmul(
                            py[:, o_[0]:o_[1]], lhsT=U[:, kt2, g, jt * 128:(jt + 1) * 128], rhs=Toep[:, kt2, g, :], start=(kt2 == 0), stop=False))
                    fns.append(lambda E, o_=o_, gp=gp, hs=hs, jt=jt: E.matmul(
                        py[:, o_[0]:o_[1]], lhsT=fre_[hs, gp, jt * 128:(jt + 1) * 128], rhs=Hr[hs, gp, 1:17, :].rearrange("p t h -> p (t h)"),
                        start=False, stop=False))
                    fns.append(lambda E, o_=o_, gp=gp, hs=hs, jt=jt: E.matmul(
                        py[:, o_[0]:o_[1]], lhsT=fim_[hs, gp, jt * 128:(jt + 1) * 128], rhs=nHi[hs, gp, 1:17, :].rearrange("p t h -> p (t h)"),
                        start=False, stop=True))
                P.mm_group(fns, reads=[bU, bToep, bHr, bnHi] + bfre_ + bfim_, writes=[bpy])
                P.op("act" if gq % 2 == 0 else "dve",
                     (lambda E, gq=gq: E.copy(out=Ysb[:].rearrange("p t (g h) -> p g t h", h=16)[:, 4 * gq:4 * gq + 4],
                                              in_=py[:].rearrange("p (g t h) -> p g t h", g=4, h=16)))
                     if gq % 2 == 0 else
                     (lambda E, gq=gq: E.tensor_copy(out=Ysb[:].rearrange("p t (g h) -> p g t h", h=16)[:, 4 * gq:4 * gq + 4],
                                                     in_=py[:].rearrange("p (g t h) -> p g t h", g=4, h=16))),
                     reads=[bpy], writes=[bYsb])
            P.dma("sp", ysv[jt * 128:(jt + 1) * 128, :, :], Ysb[:], reads=[bYsb], writes=[fz["obuf_of"](jt) if fz else bys])
            if fz:
                fz["after_chunk"](jt)
        if fz:
            barrier(P)
        else:
            P.finish([bys])
    return nc


def run_L1b(inp):
    nc = _get("L1b", build_L1b)
    mk, idm = _s5_consts()
    w_in = inp["w_in_even"][0]
    maps = []
    for c in range(8):
        b, r = divmod(c, 4)
        gs = slice(16 * r, 16 * r + 16)
        maps.append({"x": np.ascontiguousarray(inp["x"][b]), "npre": np.ascontiguousarray(inp["norm_pre"][0]),
                     "wu": np.ascontiguousarray(w_in[:, 4112 + 256 * r:4112 + 256 * (r + 1)]),
                     "lre": np.ascontiguousarray(inp["s5_lam_re"][0, gs]), "lim": np.ascontiguousarray(inp["s5_lam_im"][0, gs]),
                     "bre": np.ascontiguousarray(inp["s5_b_re"][0, gs]), "bim": np.ascontiguousarray(inp["s5_b_im"][0, gs]),
                     "cre": np.ascontiguousarray(inp["s5_c_re"][0, gs]), "cim": np.ascontiguousarray(inp["s5_c_im"][0, gs]),
                     "ldt": np.ascontiguousarray(inp["s5_log_dt"][0, gs]), "dd": np.ascontiguousarray(inp["s5_d"][0, 256 * r:256 * (r + 1)]),
                     "taus": TAUS, "mk": mk, "idm": idm, "ident": _IDENT})
    res = run_bass_kernel_spmd(nc, maps, core_ids=list(range(8)))
    ys = np.empty((2, 8192, 1024), np.float32)
    for c in range(8):
        b, r = divmod(c, 4)
        ys[b, :, 256 * r:256 * (r + 1)] = res.results[c]["ys"]
    return ys


def _gdn_consts():
    p = np.arange(64)[:, None]; f = np.arange(64)[None, :]
    negu = np.where(f >= p, 0.0, -30000.0)
    negls = np.where(f < p, 0.0, -30000.0)
    nsu = np.where(f > p, -1.0, 0.0)
    i64 = np.eye(64)
    c64 = np.stack([negu, negls, nsu, i64], axis=1).astype(np.float32)
    cmask = np.ones((2, 512), np.float32); cmask[:, 0::64] = 0.0
    sel = np.zeros((2, 2, 128), np.float32); sel[0, 0, :] = 1.0; sel[1, 1, :] = 1.0
    return c64, cmask, sel


def build_L1a(S=8192, fz=None):
    nc = fz["nc"] if fz else bass.Bass("TRN2", target_bir_lowering=False)
    pfx = fz["pfx"] if fz else ""

    def D(name, shape):
        if fz and name in fz["share"]:
            return fz["share"][name]
        return nc.dram_tensor(pfx + name, shape, F32, kind="ExternalInput").ap()
    x_d = D("x", [S, 1024]); npre_d = D("npre", [1024]); w_d = D("w", [1024, 768]); wb_d = D("wb", [1024, 2]); wa_d = D("wa", [1024, 2])
    conv_d = D("conv", [4, 768]); alog_d = D("alog", [2]); dtb_d = D("dtb", [2])
    ident_d = D("ident", [128, 128]); c64_d = D("c64", [64, 4, 64]); cmask_d = D("cmask", [2, 512]); sel_d = D("sel", [2, 2, 128])
    ones_d = D("ones", [128, 128])
    o_d = fz["out"] if fz else nc.dram_tensor("o", [S, 256], F32, kind="ExternalOutput").ap()
    NST = S // 512
    with ExitStack() as st:
        C = Ctx(nc, st, fz["P"], pfx) if fz else Ctx(nc, st); P = C.P
        idf, bidf, idb, bidb = make_ident(C, ident_d)
        npre, bnpre = bcast_row_load(C, "npre", npre_d, 1024)
        w, bw = load_w_bf16(C, "w", w_d, 8, 768)
        wb, bwb = load_w_bf16(C, "wb", wb_d, 8, 2)
        wa, bwa = load_w_bf16(C, "wa", wa_d, 8, 2)
        cw, bcw = C.sb("cw", [128, 4, 6])
        P.dma("sp", cw[:], conv_d.rearrange("j (c p) -> p j c", p=128), writes=[bcw])
        extu = fz.get("uTp") if fz else None
        if extu:
            wu_d = D("wu", [1024, 256])
            wu, bwu = load_w_bf16(C, "wu", wu_d, 8, 256)
            uTp, buTp = extu
        c64, bc64 = C.sb("c64", [64, 4, 64]); P.dma("sp", c64[:], c64_d, writes=[bc64])
        NEGU = c64[:, 0, :]; NEGLS = c64[:, 1, :]; NSU = c64[:, 2, :]; I64 = c64[:, 3, :]
        cmask, bcmask = C.sb("cmask", [2, 512]); P.dma("sp", cmask[:], cmask_d, writes=[bcmask])
        sel, bsel = C.sb("sel", [2, 2, 128]); P.dma("sp", sel[:], sel_d, writes=[bsel])
        ones, bones = C.sb("ones", [128, 128]); P.dma("sp", ones[:], ones_d, writes=[bones])
        onesb, bonesb = C.sb("onesb", [128, 128], BF16)
        P.op("dve", lambda E: E.tensor_copy(out=onesb[:], in_=ones[:]), reads=[bones], writes=[bonesb])
        sqb, bsqb = C.sb("sqb", [128, 512], BF16)
        alog, balog = C.sb("alog", [2, 1]); P.dma("sp", alog[:], alog_d.rearrange("(a b) -> a b", b=1), writes=[balog])
        dtb, bdtb = C.sb("dtb", [2, 1]); P.dma("sp", dtb[:], dtb_d.rearrange("(a b) -> a b", b=1), writes=[bdtb])
        negA, bnegA = C.sb("negA", [2, 1])
        P.op("act", lambda E: E.activation(out=negA[:], in_=alog[:], func=AF.Exp), reads=[balog], writes=[bnegA])
        P.op("dve", lambda E: E.tensor_scalar(out=negA[:], in0=negA[:], scalar1=-1.0, scalar2=None, op0=ALU.mult), reads=[bnegA], writes=[bnegA])
        xt, bxt = C.sb("xt", [128, 1024]); sq, bsq = C.sb("sq", [128, 1024], BF16); hn, bhn = C.sb("hn", [128, 1024], BF16)
        ss, bss = C.sb("ss", [128, 1]); hT, bhT = C.sb("hT", [128, 8, 512], BF16)
        raw, _ = C.sb("raw", [128, 6, 515]); braw = [Buf("raw%d" % i) for i in range(6)]
        cvq, bcvq = C.sb("cvq", [128, 512])
        act, _ = C.sb("act", [128, 4, 512]); bact = [Buf("act%d" % i) for i in range(4)]
        vbuf2 = []; qk2 = []; bqk2 = []
        for par_ in range(2):
            vt_, _ = C.sb("vbuf%d" % par_, [128, 2, 512]); vbuf2.append((vt_, [Buf("vb%d_%d" % (par_, i)) for i in range(2)]))
            qt_, _ = C.sb("qk%d" % par_, [128, 4, 512]); qk2.append(qt_); bqk2.append([Buf("qk%d_%d" % (par_, i)) for i in range(4)])
        rn, brn = C.sb("rn", [128, 512])
        brow, bbrow = C.sb("brow", [2, 512]); grow, bgrow = C.sb("grow", [2, 512]); gcrow, bgcrow = C.sb("gcrow", [2, 512])
        GCB2 = []; BB2 = []
        for par_ in range(2):
            GCB2.append([C.sb("GCB%d_%d" % (par_, h), [128, 512]) for h in range(2)])
            BB2.append([C.sb("BB%d_%d" % (par_, h), [128, 512]) for h in range(2)])
        m64h = []; smallh = []
        for h in range(2):
            d_ = {}
            for nm in ("arg1", "scr"):
                d_[nm] = C.sb("m%d_%s" % (h, nm), [64, 512])
            for nm in ("DT", "Ds", "tmp", "Pa", "Pb", "Qa", "Qb"):
                d_[nm] = C.sb("m%d_%s" % (h, nm), [64, 512], BF16)
            m64h.append(d_)
        heads = []
        for h in range(2):
            H = {}
            H["attnT"] = C.sb("attnT%d" % h, [64, 512], BF16); H["Y"] = C.sb("Y%d" % h, [64, 512]); H["Ybf"] = C.sb("Ybf%d" % h, [64, 512], BF16)
            H["EG"] = C.sb("EG%d" % h, [128, 512]); H["qdec"] = C.sb("qdec%d" % h, [128, 512], BF16)
            H["kTb"] = C.sb("kTb%d" % h, [128, 512], BF16); H["Sbf"] = C.sb("Sbf%d" % h, [128, 128], BF16)
            H["bv"] = C.sb("bv%d" % h, [64, 8, 128]); H["kdec"] = C.sb("kdec%d" % h, [64, 8, 128], BF16)
            H["nbg"] = C.sb("nbg%d" % h, [64, 8]); H["osb"] = C.sb("osb%d" % h, [128, 8, 128])
            H["vnew"] = C.sb("vnew%d" % h, [64, 128], BF16); H["rhs2"] = C.sb("rhs2%d" % h, [64, 128], BF16)
            heads.append(H)
        for h in range(2):
            d_ = {}
            for nm in ("gccol", "bcol", "nbcol", "elast", "egc"):
                d_[nm] = C.sb("s%d_%s" % (h, nm), [64, 8])
            smallh.append(d_)
        Sst = [C.sb("S%d" % h, [128, 128]) for h in range(2)]
        for h in range(2):
            P.op("dve", lambda E, h=h: E.memset(Sst[h][0][:], 0.0), writes=[Sst[h][1]])
            P.op("dve", lambda E, h=h: E.memset(heads[h]["Sbf"][0][:], 0.0), writes=[heads[h]["Sbf"][1]])
        P.op("dve", lambda E: E.memset(raw[:, :, 0:3], 0.0), writes=braw)
        ptr, bptr = C.ps("ptr", [128, 1024], BF16)
        G = [C.ps("gp%d" % i, [128, 512]) for i in range(7)]
        GP = G[0:4]
        GA = G[4:7]
        BKS = [(GP[0], GP[1], GP[2]), (GP[3], GA[0], GA[1])]
        ga_ctr = [0]

        def next_ga():
            ga_ctr[0] += 1
            return GA[ga_ctr[0] % 3]
        bo = None if fz else Buf("o", multi=True)
        if fz is not None and fz.get("debug"):
            print("L1a sbuf remaining", nc.sbuf_bytes_remaining)

        def tt(out, bo_, a, ba, b, bb_, op, eng="dve"):
            P.op(eng, lambda E: E.tensor_tensor(out=out, in0=a, in1=b, op=op), reads=ba if isinstance(ba, list) else [ba], writes=[bo_])

        def stageA(s_):
            par = s_ % 2
            qk = qk2[par]; bqk = bqk2[par]; GCB = GCB2[par]; BB = BB2[par]; vb, bvb = vbuf2[par]
            for t in range(4):
                r0 = s_ * 512 + t * 128
                P.dma("sp", xt[:], x_d[r0:r0 + 128, :], writes=[bxt])
                rms_rstd(C, xt[:], bxt, 1024, sq[:], bsq, ss, bss)
                P.op("dve", lambda E: E.scalar_tensor_tensor(out=hn[:], in0=xt[:], scalar=ss[:, 0:1], in1=npre[:],
                                                             op0=ALU.mult, op1=ALU.mult), reads=[bxt, bss, bnpre], writes=[bhn])
                transpose8(C, hn, bhn, idb, bidb, ptr, bptr, hT[:, :, t * 128:(t + 1) * 128], bhT, eng="act")
                yield
            for ct in range(6):
                pa, bpa = next_ga()
                fns = [(lambda E, kt=kt, ct=ct, pa=pa: E.matmul(pa[:], lhsT=w[:, kt, ct * 128:(ct + 1) * 128], rhs=hT[:, kt, :],
                                                                start=(kt == 0), stop=(kt == 7))) for kt in range(8)]
                P.mm_group(fns, reads=[bw, bhT], writes=[bpa])
                P.op("act", lambda E, ct=ct, pa=pa: E.copy(out=raw[:, ct, 3:515], in_=pa[:]), reads=[bpa], writes=[braw[ct]])
                P.op("dve", lambda E, ct=ct: E.tensor_scalar(out=cvq[:], in0=raw[:, ct, 0:512], scalar1=cw[:, 0, ct:ct + 1], scalar2=None, op0=ALU.mult),
                     reads=[braw[ct], bcw], writes=[bcvq])
                for j in range(1, 4):
                    P.op("dve", lambda E, ct=ct, j=j: E.scalar_tensor_tensor(out=cvq[:], in0=raw[:, ct, j:j + 512], scalar=cw[:, j, ct:ct + 1], in1=cvq[:],
                                                                             op0=ALU.mult, op1=ALU.add), reads=[braw[ct], bcw, bcvq], writes=[bcvq])
                P.op("act", lambda E, ct=ct: E.copy(out=raw[:, ct, 0:3], in_=raw[:, ct, 512:515]), reads=[braw[ct]], writes=[braw[ct]])
                if ct < 4:
                    P.op("act", lambda E, ct=ct: E.activation(out=act[:, ct, :], in_=cvq[:], func=AF.Silu), reads=[bcvq], writes=[bact[ct]])
                else:
                    P.op("act", lambda E, ct=ct, vb=vb: E.activation(out=vb[:, ct - 4, :], in_=cvq[:], func=AF.Silu), reads=[bcvq], writes=[bvb[ct - 4]])
                yield
            if extu:
                for blk in range(2):
                    pa, bpa = next_ga()
                    fns = [(lambda E, kt=kt, blk=blk, pa=pa: E.matmul(
                        pa[:].rearrange("p (s n) -> p s n", s=16), lhsT=wu[:, kt, blk * 128:(blk + 1) * 128],
                        rhs=hT[:, kt, :].rearrange("p (n s) -> p s n", s=16), start=(kt == 0), stop=(kt == 7))) for kt in range(8)]
                    P.mm_group(fns, reads=[bwu, bhT], writes=[bpa])
                    P.op("act", lambda E, blk=blk, pa=pa, s_=s_: E.copy(out=uTp[:, blk, :, 32 * s_:32 * s_ + 32], in_=pa[:].rearrange("p (s n) -> p s n", s=16)),
                         reads=[bpa], writes=[buTp])
                    yield
            for ct in range(4):
                pa, bpa = next_ga()
                P.op("act", lambda E, ct=ct: E.activation(out=sqb[:], in_=act[:, ct, :], func=AF.Square), reads=[bact[ct]], writes=[bsqb])
                P.op("pe", lambda E, pa=pa: E.matmul(pa[:], lhsT=onesb[:], rhs=sqb[:], start=True, stop=True), reads=[bonesb, bsqb], writes=[bpa])
                P.op("act", lambda E, pa=pa: E.activation(out=rn[:], in_=pa[:], func=AF.Ln, bias=1e-6, scale=1.0), reads=[bpa], writes=[brn])
                P.op("act", lambda E: E.activation(out=rn[:], in_=rn[:], func=AF.Exp, scale=-0.5), reads=[brn], writes=[brn])
                if ct < 2:
                    P.op("dve", lambda E, ct=ct, qk=qk: E.scalar_tensor_tensor(out=qk[:, ct, :], in0=act[:, ct, :], scalar=float(128 ** -0.5), in1=rn[:],
                                                                               op0=ALU.mult, op1=ALU.mult), reads=[bact[ct], brn], writes=[bqk[ct]])
                else:
                    P.op("dve", lambda E, ct=ct, qk=qk: E.tensor_tensor(out=qk[:, ct, :], in0=act[:, ct, :], in1=rn[:], op=ALU.mult),
                         reads=[bact[ct], brn], writes=[bqk[ct]])
                yield
            pa, bpa = next_ga()
            fns = [(lambda E, kt=kt, pa=pa: E.matmul(pa[0:2, :], lhsT=wb[:, kt, 0:2], rhs=hT[:, kt, :], start=(kt == 0), stop=(kt == 7))) for kt in range(8)]
            P.mm_group(fns, reads=[bwb, bhT], writes=[bpa])
            P.op("act", lambda E, pa=pa: E.activation(out=brow[:], in_=pa[0:2, :], func=AF.Sigmoid), reads=[bpa], writes=[bbrow])
            pa2, bpa2 = next_ga()
            fns = [(lambda E, kt=kt, pa2=pa2: E.matmul(pa2[0:2, :], lhsT=wa[:, kt, 0:2], rhs=hT[:, kt, :], start=(kt == 0), stop=(kt == 7))) for kt in range(8)]
            P.mm_group(fns, reads=[bwa, bhT], writes=[bpa2])
            P.op("act", lambda E, pa2=pa2: E.activation(out=grow[:], in_=pa2[0:2, :], func=AF.Exp, bias=dtb[:, 0:1], scale=1.0), reads=[bpa2, bdtb], writes=[bgrow])
            P.op("act", lambda E: E.activation(out=grow[:], in_=grow[:], func=AF.Ln, bias=1.0, scale=1.0), reads=[bgrow], writes=[bgrow])
            P.op("dve", lambda E: E.tensor_scalar(out=grow[:], in0=grow[:], scalar1=negA[:, 0:1], scalar2=None, op0=ALU.mult), reads=[bgrow, bnegA], writes=[bgrow])
            P.op("dve", lambda E: E.tensor_tensor_scan(out=gcrow[:], data0=cmask[:], data1=grow[:], initial=0.0, op0=ALU.mult, op1=ALU.add),
                 reads=[bcmask, bgrow], writes=[bgcrow])
            yield
            for h in range(2):
                pa, bpa = next_ga()
                P.op("pe", lambda E, h=h, pa=pa: E.matmul(pa[:], lhsT=sel[:, h, :], rhs=gcrow[:], start=True, stop=True), reads=[bsel, bgcrow], writes=[bpa])
                P.op("act", lambda E, h=h, pa=pa, GCB=GCB: E.copy(out=GCB[h][0][:], in_=pa[:]), reads=[bpa], writes=[GCB[h][1]])
                pa, bpa = next_ga()
                P.op("pe", lambda E, h=h, pa=pa: E.matmul(pa[:], lhsT=sel[:, h, :], rhs=brow[:], start=True, stop=True), reads=[bsel, bbrow], writes=[bpa])
                P.op("act", lambda E, h=h, pa=pa, BB=BB: E.copy(out=BB[h][0][:], in_=pa[:]), reads=[bpa], writes=[BB[h][1]])
                yield

        for _ in stageA(0):
            pass
        for s_ in range(NST):
            par = s_ % 2
            qk = qk2[par]; bqk = bqk2[par]; GCB = GCB2[par]; BB = BB2[par]; vb, bvb = vbuf2[par]
            nxt = stageA(s_ + 1) if s_ + 1 < NST else None

            def advance(k):
                if nxt is not None:
                    for _ in range(k):
                        next(nxt, None)
            def stageB(h, qk=qk, bqk=bqk, GCB=GCB, BB=BB, vb=vb, bvb=bvb):
                m64 = m64h[h]; small = smallh[h]; BK = BKS[h]
                qT = qk[:, h, :]; bqT = bqk[h]; kT = qk[:, 2 + h, :]; bkT = bqk[2 + h]; vT = vb[:, h, :]; bvT = bvb[h]
                gcb, bgcb = GCB[h]; bb, bbb = BB[h]
                H = heads[h]
                attnT, battnT = H["attnT"]; Y, bY = H["Y"]; EG, bEG = H["EG"]; qdec, bqdec = H["qdec"]
                Ybf, bYbf = H["Ybf"]
                bv, bbv = H["bv"]; kdec, bkdec = H["kdec"]; nbg, bnbg = H["nbg"]
                arg1, barg1 = m64["arg1"]; scr, bscr = m64["scr"]; DT, bDT = m64["DT"]; Ds, bDs = m64["Ds"]
                tmp, btmp = m64["tmp"]
                gccol, bgccol = small["gccol"]; bcol, bbcol = small["bcol"]; nbcol, bnbcol = small["nbcol"]
                elast, belast = small["elast"]; egc, begc = small["egc"]
                v3 = lambda t_: t_[:].rearrange("p (n f) -> p n f", f=64)
                i64b = I64.unsqueeze(1).to_broadcast([64, 8, 64])
                tt(v3(scr), bscr, gcb[0:64, :].rearrange("p (n f) -> p n f", f=64), [bgcb, bc64], i64b, bc64, ALU.mult)
                P.op("dve", lambda E, scr=scr, gccol=gccol: E.tensor_reduce(out=gccol[:], in_=scr[:].rearrange("p (n f) -> p n f", f=64), axis=AX.X, op=ALU.add), reads=[bscr], writes=[bgccol])
                tt(v3(scr), bscr, bb[0:64, :].rearrange("p (n f) -> p n f", f=64), [bbb, bc64], i64b, bc64, ALU.mult)
                P.op("dve", lambda E, scr=scr, bcol=bcol: E.tensor_reduce(out=bcol[:], in_=scr[:].rearrange("p (n f) -> p n f", f=64), axis=AX.X, op=ALU.add), reads=[bscr], writes=[bbcol])
                yield
                P.op("dve", lambda E: E.tensor_scalar(out=nbcol[:], in0=bcol[:], scalar1=-1.0, scalar2=None, op0=ALU.mult), reads=[bbcol], writes=[bnbcol])
                tt(v3(arg1), barg1, gcb[0:64, :].rearrange("p (n f) -> p n f", f=64), [bgcb, bgccol], gccol[:].unsqueeze(2).to_broadcast([64, 8, 64]), bgccol, ALU.subtract)
                tt(v3(scr), bscr, v3(arg1), [barg1, bc64], NEGU.unsqueeze(1).to_broadcast([64, 8, 64]), bc64, ALU.add)
                P.op("act", lambda E: E.activation(out=DT[:], in_=scr[:], func=AF.Exp), reads=[bscr], writes=[bDT])
                yield
                P.op("dve", lambda E: E.scalar_tensor_tensor(out=scr[:].rearrange("p (n f) -> p n f", f=64), in0=arg1[:].rearrange("p (n f) -> p n f", f=64), scalar=-1.0,
                                                             in1=NEGLS.unsqueeze(1).to_broadcast([64, 8, 64]), op0=ALU.mult, op1=ALU.add), reads=[barg1, bc64], writes=[bscr])
                P.op("act", lambda E: E.activation(out=Ds[:], in_=scr[:], func=AF.Exp), reads=[bscr], writes=[bDs])
                pk, bpk = BK[0]; pq, bpq = BK[1]
                fns = [(lambda E, n=n, pk=pk, kT=kT: E.matmul(pk[0:64, n * 64:(n + 1) * 64], lhsT=kT[:, n * 64:(n + 1) * 64], rhs=kT[:, n * 64:(n + 1) * 64],
                                                              start=True, stop=True)) for n in range(8)]
                P.mm_group(fns, reads=[bkT], writes=[bpk])
                fns = [(lambda E, n=n, pq=pq, kT=kT, qT=qT: E.matmul(pq[0:64, n * 64:(n + 1) * 64], lhsT=kT[:, n * 64:(n + 1) * 64], rhs=qT[:, n * 64:(n + 1) * 64],
                                                                     start=True, stop=True)) for n in range(8)]
                P.mm_group(fns, reads=[bkT, bqT], writes=[bpq])
                yield
                tt(attnT[:], battnT, pq[0:64, :], [bpq, bDT], DT[:], bDT, ALU.mult)
                Pc, bPc = m64["Pa"]; Pn, bPn = m64["Pb"]; Qc, bQc = m64["Qa"]; Qn, bQn = m64["Qb"]
                tt(tmp[:], btmp, pk[0:64, :], [bpk, bDT], DT[:], bDT, ALU.mult)
                tt(tmp[:], btmp, tmp[:], [btmp, bbb], bb[0:64, :], bbb, ALU.mult)
                tt(v3(Qc), bQc, v3(tmp), [btmp, bc64], NSU.unsqueeze(1).to_broadcast([64, 8, 64]), bc64, ALU.mult)
                yield
                tt(tmp[:], btmp, pk[0:64, :], [bpk, bDs], Ds[:], bDs, ALU.mult)
                tt(v3(Pc), bPc, v3(tmp), [btmp, bnbcol], nbcol[:].unsqueeze(2).to_broadcast([64, 8, 64]), bnbcol, ALU.mult)
                tt(v3(Y), bY, v3(Qc), [bQc, bc64], i64b, bc64, ALU.add)
                P.op("act", lambda E, Ybf=Ybf, Y=Y: E.copy(out=Ybf[:], in_=Y[:]), reads=[bY], writes=[bYbf])
                yield
                for j in range(5):
                    pP, bpP = BK[2]; pQ, bpQ = BK[1]
                    fns = [(lambda E, n=n, pP=pP, Qc=Qc, Pc=Pc: E.matmul(pP[0:64, n * 64:(n + 1) * 64], lhsT=Qc[:, n * 64:(n + 1) * 64], rhs=Pc[:, n * 64:(n + 1) * 64],
                                                                         start=True, stop=True)) for n in range(8)]
                    P.mm_group(fns, reads=[bQc, bPc], writes=[bpP])
                    if j < 4:
                        fns = [(lambda E, n=n, pQ=pQ, Qc=Qc, Pc=Pc: E.matmul(pQ[0:64, n * 64:(n + 1) * 64], lhsT=Pc[:, n * 64:(n + 1) * 64], rhs=Qc[:, n * 64:(n + 1) * 64],
                                                                             start=True, stop=True)) for n in range(8)]
                        P.mm_group(fns, reads=[bQc, bPc], writes=[bpQ])
                    yield
                    P.op("act", lambda E, Pn=Pn, pP=pP: E.copy(out=Pn[:], in_=pP[0:64, :]), reads=[bpP], writes=[bPn])
                    if j < 4:
                        P.op("dve", lambda E, Qn=Qn, pQ=pQ: E.tensor_copy(out=Qn[:], in_=pQ[0:64, :]), reads=[bpQ], writes=[bQn])
                    pY, bpY = BK[0]
                    fns = [(lambda E, n=n, pY=pY, Pn=Pn, Ybf=Ybf: E.matmul(pY[0:64, n * 64:(n + 1) * 64], lhsT=Pn[:, n * 64:(n + 1) * 64], rhs=Ybf[:, n * 64:(n + 1) * 64],
                                                                         start=True, stop=True)) for n in range(8)]
                    P.mm_group(fns, reads=[bPn, bYbf], writes=[bpY])
                    yield
                    tt(Y[:], bY, Y[:], [bY, bpY], pY[0:64, :], bpY, ALU.add)
                    P.op("act", lambda E, Ybf=Ybf, Y=Y: E.copy(out=Ybf[:], in_=Y[:]), reads=[bY], writes=[bYbf])
                    Pc, bPc, Pn, bPn = Pn, bPn, Pc, bPc
                    Qc, bQc, Qn, bQn = Qn, bQn, Qc, bQc
                for hf in range(2):
                    pth, bpth = BK[1 + hf]
                    fns = [(lambda E, n=n, vT=vT, pth=pth, hf=hf: E.transpose(out=pth[0:64, n * 128:(n + 1) * 128], in_=vT[:, (4 * hf + n) * 64:(4 * hf + n + 1) * 64],
                                                                              identity=idf[:])) for n in range(4)]
                    P.mm_group(fns, reads=[bvT, bidf], writes=[bpth])
                    tt(bv[:, 4 * hf:4 * hf + 4, :], bbv, pth[0:64, :].rearrange("p (n d) -> p n d", d=128), [bpth, bbcol],
                       bcol[:, 4 * hf:4 * hf + 4].unsqueeze(2).to_broadcast([64, 4, 128]), bbcol, ALU.mult)
                yield
                tt(elast[:], belast, gcb[0:64, :].rearrange("p (n f) -> p n f", f=64)[:, :, 63], [bgcb, bgccol], gccol[:], bgccol, ALU.subtract)
                P.op("act", lambda E: E.activation(out=elast[:], in_=elast[:], func=AF.Exp), reads=[belast], writes=[belast])
                for hf in range(2):
                    pth, bpth = BK[1 + hf]
                    fns = [(lambda E, n=n, kT=kT, pth=pth, hf=hf: E.transpose(out=pth[0:64, n * 128:(n + 1) * 128], in_=kT[:, (4 * hf + n) * 64:(4 * hf + n + 1) * 64],
                                                                              identity=idf[:])) for n in range(4)]
                    P.mm_group(fns, reads=[bkT, bidf], writes=[bpth])
                    tt(kdec[:, 4 * hf:4 * hf + 4, :], bkdec, pth[0:64, :].rearrange("p (n d) -> p n d", d=128), [bpth, belast],
                       elast[:, 4 * hf:4 * hf + 4].unsqueeze(2).to_broadcast([64, 4, 128]), belast, ALU.mult)
                yield
                P.op("act", lambda E, gcb=gcb, EG=EG: E.activation(out=EG[:], in_=gcb[:], func=AF.Exp), reads=[bgcb], writes=[bEG])
                tt(qdec[:], bqdec, qT, [bqT, bEG], EG[:], bEG, ALU.mult)
                kTb, bkTb = H["kTb"]
                P.op("act", lambda E, kTb=kTb, kT=kT: E.copy(out=kTb[:], in_=kT), reads=[bkT], writes=[bkTb])
                P.op("act", lambda E: E.activation(out=egc[:], in_=gccol[:], func=AF.Exp), reads=[bgccol], writes=[begc])
                P.op("dve", lambda E, nbg=nbg: E.scalar_tensor_tensor(out=nbg[:], in0=egc[:], scalar=-1.0, in1=bcol[:], op0=ALU.mult, op1=ALU.mult),
                     reads=[begc, bbcol], writes=[bnbg])
            gensB = [stageB(0), stageB(1)]
            aliveB = True
            while aliveB:
                aliveB = False
                for g_ in gensB:
                    try:
                        next(g_)
                        aliveB = True
                    except StopIteration:
                        pass
            banks = [(GP[0], GP[1]), (GP[2], GP[3])]
            for n in range(8):
                cs = slice(n * 64, (n + 1) * 64)
                for h in range(2):
                    H = heads[h]; S, bS = Sst[h]
                    kT, bkT = H["kTb"]; Sbf, bSbf = H["Sbf"]
                    attnT, battnT = H["attnT"]; Y, bY = H["Ybf"]; EG, bEG = H["EG"]; qdec, bqdec = H["qdec"]
                    bv, bbv = H["bv"]; kdec, bkdec = H["kdec"]; nbg, bnbg = H["nbg"]
                    vnew, bvnew = H["vnew"]; rhs2, brhs2 = H["rhs2"]; osb, bosb = H["osb"]
                    (KSO, bKSO), (Sb, bSb) = banks[h]
                    Vb, bVb = KSO, bKSO
                    P.op("pe", lambda E, cs=cs, kT=kT, Sbf=Sbf, KSO=KSO: E.matmul(KSO[0:64, 0:128], lhsT=kT[:, cs], rhs=Sbf[:], start=True, stop=True),
                         reads=[bkT, bSbf], writes=[bKSO])
                    P.op("dve", lambda E, n=n, KSO=KSO, rhs2=rhs2, nbg=nbg, bv=bv: E.scalar_tensor_tensor(
                        out=rhs2[:], in0=KSO[0:64, 0:128], scalar=nbg[:, n:n + 1], in1=bv[:, n, :], op0=ALU.mult, op1=ALU.add),
                        reads=[bKSO, bnbg, bbv], writes=[brhs2])
                    P.op("pe", lambda E, cs=cs, Y=Y, Vb=Vb, rhs2=rhs2: E.matmul(Vb[0:64, 128:256], lhsT=Y[:, cs], rhs=rhs2[:], start=True, stop=True),
                         reads=[bY, brhs2], writes=[bVb])
                    P.op("act", lambda E, vnew=vnew, Vb=Vb: E.copy(out=vnew[:], in_=Vb[0:64, 128:256]), reads=[bVb], writes=[bvnew])
                    fns = [lambda E, cs=cs, Sbf=Sbf, KSO=KSO, qdec=qdec: E.matmul(KSO[64:128, 0:128], lhsT=qdec[:, cs], rhs=Sbf[:], start=True, stop=False),
                           lambda E, cs=cs, KSO=KSO, attnT=attnT, vnew=vnew: E.matmul(KSO[64:128, 0:128], lhsT=attnT[:, cs], rhs=vnew[:], start=False, stop=True)]
                    P.mm_group(fns, reads=[bqdec, bSbf, battnT, bvnew], writes=[bKSO])
                    P.op("pe", lambda E, n=n, Sb=Sb, kdec=kdec, vnew=vnew: E.matmul(Sb[:, 0:128], lhsT=kdec[:, n, :], rhs=vnew[:], start=True, stop=True),
                         reads=[bkdec, bvnew], writes=[bSb])
                    P.op("dve", lambda E, n=n, S=S, EG=EG, Sb=Sb, Sbf=Sbf: E.scalar_tensor_tensor(out=Sbf[:], in0=S[:], scalar=EG[:, n * 64 + 63:n * 64 + 64], in1=Sb[:, 0:128],
                                                                                                  op0=ALU.mult, op1=ALU.add), reads=[bS, bEG, bSb], writes=[bSbf])
                    P.op("dve", lambda E, n=n, S=S, EG=EG, Sb=Sb: E.scalar_tensor_tensor(out=S[:], in0=S[:], scalar=EG[:, n * 64 + 63:n * 64 + 64], in1=Sb[:, 0:128],
                                                                                         op0=ALU.mult, op1=ALU.add), reads=[bS, bEG, bSb], writes=[bS])
                    P.op("act", lambda E, n=n, osb=osb, KSO=KSO: E.copy(out=osb[64:128, n, :], in_=KSO[64:128, 0:128]), reads=[bKSO], writes=[bosb])
                    advance(1)
                advance(1)
            advance(100)
            for h in range(2):
                osb, bosb = heads[h]["osb"]
                P.dma("sp", o_d[s_ * 512:(s_ + 1) * 512, h * 128:(h + 1) * 128].rearrange("(n c) d -> c n d", c=64), osb[64:128, :, :], reads=[bosb],
                      writes=[fz["obuf_of"](s_) if fz else bo])
            if fz:
                fz["after_chunk"](s_)
        if fz:
            barrier(P)
        else:
            P.finish([bo])
    return nc


def run_L1a(inp):
    nc = _get("L1a", build_L1a)
    c64, cmask, sel = _gdn_consts()
    w_in = inp["w_in_even"][0]
    conv = inp["conv_qkv"][0]
    ones = np.ones((128, 128), np.float32)
    maps = []
    for c in range(8):
        b, r = divmod(c, 4)
        cols = np.concatenate([np.arange(256 * r, 256 * r + 256), 1024 + np.arange(256 * r, 256 * r + 256), 2048 + np.arange(256 * r, 256 * r + 256)])
        maps.append({"x": np.ascontiguousarray(inp["x"][b]), "npre": np.ascontiguousarray(inp["norm_pre"][0]),
                     "w": np.ascontiguousarray(w_in[:, cols]), "wb": np.ascontiguousarray(w_in[:, 4096 + 2 * r:4096 + 2 * r + 2]),
                     "wa": np.ascontiguousarray(w_in[:, 4104 + 2 * r:4104 + 2 * r + 2]), "conv": np.ascontiguousarray(conv[:, cols]),
                     "alog": np.ascontiguousarray(inp["a_log"][0, 2 * r:2 * r + 2]), "dtb": np.ascontiguousarray(inp["dt_bias"][0, 2 * r:2 * r + 2]),
                     "ident": _IDENT, "c64": c64, "cmask": cmask, "sel": sel, "ones": ones})
    res = run_bass_kernel_spmd(nc, maps, core_ids=list(range(8)))
    S_ = inp["x"].shape[1]
    o = np.empty((2, S_, 1024), np.float32)
    for c in range(8):
        b, r = divmod(c, 4)
        o[b, :, 256 * r:256 * (r + 1)] = res.results[c]["o"]
    return o


def kernel_unfused(**inputs):
    inp = {k: np.asarray(v) for k, v in inputs.items()}
    o = run_L1a(inp)
    ys = run_L1b(inp)
    x1 = run_L2(inp, o, ys)
    out = run_L3(inp, x1)
    return out.astype(np.float32)


def build_fused():
    nc = bass.Bass("TRN2", target_bir_lowering=False)
    x_full = nc.dram_tensor("x", [8192, 1024], F32, kind="ExternalInput").ap()
    ident_d = nc.dram_tensor("ident", [128, 128], F32, kind="ExternalInput").ap()
    npre0_d = nc.dram_tensor("npre0", [1024], F32, kind="ExternalInput").ap()
    gidx_d = nc.dram_tensor("gidx", [128, 2, 17, 4], I32, kind="ExternalInput").ap()
    out_d = nc.dram_tensor("out", [2048, 1024], F32, kind="ExternalOutput").ap()
    ag_in = [nc.dram_tensor("ag_in%d" % i, [8192, 256], (F32, BF16)[i]) for i in range(2)]
    ag_out = [nc.dram_tensor("ag_out%d" % i, [4 * 8192, 256], (F32, BF16)[i]) for i in range(2)]
    x1s = nc.dram_tensor("x1s", [2176, 1024], F32)
    GROUPS = [[0, 1, 2, 3], [4, 5, 6, 7]]
    with ExitStack() as st:
        C = Ctx(nc, st); P = C.P
        csem = st.enter_context(nc.semaphore("csem"))
        bag_out = Buf("ag_out"); bx1s = Buf("x1s", multi=True); bout = Buf("out", multi=True)
        bo_ch = [Buf("o_ch%d" % k, multi=True) for k in range(16)]
        by_jt = [Buf("y_jt%d" % k, multi=True) for k in range(4)]
        ncc = [0]

        def emit_cc(which, k, inbuf, rows=512):
            P._deps("pool", [inbuf], [])
            P.streams["pool"].append(lambda E, which=which, k=k, rows=rows: E.collective_compute(
                "AllGather", ALU.bypass, replica_groups=GROUPS,
                ins=[ag_in[which].ap()[k * rows:(k + 1) * rows, :].opt()],
                outs=[ag_out[which].ap()[k * 4 * rows:(k + 1) * 4 * rows, :].opt()]).then_inc(csem))
            ncc[0] += 1

        share1 = {"x": x_full, "ident": ident_d, "npre": npre0_d}

        def after_jt(jt):
            emit_cc(1, jt, by_jt[jt], rows=2048)

        with ExitStack() as stU:
            CU = Ctx(nc, stU, P, "u_")
            uext = CU.sb("uTp", [128, 2, 16, 512], BF16)
            build_L1a(8192, fz={"nc": nc, "P": P, "pfx": "a_", "share": share1, "out": ag_in[0].ap(), "uTp": uext,
                                "obuf_of": lambda s_: bo_ch[s_], "after_chunk": lambda s_: emit_cc(0, s_, bo_ch[s_])})
            build_L1b(8192, fz={"nc": nc, "P": P, "pfx": "b_", "share": share1, "out": ag_in[1].ap(), "uTp": uext,
                                "obuf_of": lambda jt: by_jt[jt], "after_chunk": after_jt, "ybf16": True})
        gidx, bgidx = C.sb("gidx", [128, 2, 17, 4], I32)
        P.dma("sp", gidx[:], gidx_d, writes=[bgidx])
        waited = [False]

        def gather(P_, ld, bld, tile, part):
            if not waited[0]:
                P.streams["pool"].append(lambda E: E.wait_ge(csem, ncc[0]))
                P.op("pool", lambda E: E.nop(), reads=[], writes=[bag_out])
                waited[0] = True
            for i in range(4):
                P_.dma_ind("pool", ld[:, i * 256:(i + 1) * 256], ag_out[part].ap(), gidx[:, part, tile, i:i + 1], reads=[bag_out, bgidx], writes=[bld])

        share2 = {"ident": ident_d, "npre": npre0_d, "o": None, "ys": None}
        build_L2(2176, fz={"nc": nc, "P": P, "pfx": "c_", "share": share2, "out": x1s.ap(), "obuf": bx1s, "gather": gather, "ybf16": True})
        share3 = {"ident": ident_d, "x": x1s.ap()}
        build_L3(2048, fz={"nc": nc, "P": P, "pfx": "d_", "share": share3, "out": out_d, "obuf": bout, "xbuf": bx1s})
        P.finish([bout])
    return nc


def _gidx(r):
    g = np.zeros((128, 2, 17, 4), np.int32)
    p = np.arange(128)[:, None, None]
    tile = np.arange(17)[None, :, None]
    src = np.arange(4)[None, None, :]
    tok = np.clip(2048 * r - 128 + tile * 128 + p, 0, 8191)
    for part, R in ((0, 512), (1, 2048)):
        g[:, part] = ((tok // R) * 4 + src) * R + tok % R
    return g


def kernel(**inputs):
    inp = {k: np.ascontiguousarray(np.asarray(v)) for k, v in inputs.items()}
    nc = _get("fused", build_fused)
    c64, cmask, sel = _gdn_consts()
    mk, idm = _s5_consts()
    ones = np.ones((128, 128), np.float32)
    w_in = inp["w_in_even"][0]
    conv = inp["conv_qkv"][0]
    wz = np.ascontiguousarray(np.concatenate([w_in[:, 3072:4096], w_in[:, 5136:6160]], axis=1))
    maps = []
    for c in range(8):
        b, r = divmod(c, 4)
        cols = np.concatenate([np.arange(256 * r, 256 * r + 256), 1024 + np.arange(256 * r, 256 * r + 256), 2048 + np.arange(256 * r, 256 * r + 256)])
        gs = slice(16 * r, 16 * r + 16)
        xq = np.zeros((2176, 1024), np.float32)
        xq[128:] = inp["x"][b, 2048 * r:2048 * (r + 1)]
        if r > 0:
            xq[:128] = inp["x"][b, 2048 * r - 128:2048 * r]
        m = {"x": inp["x"][b], "ident": _IDENT, "npre0": inp["norm_pre"][0], "gidx": _gidx(r),
             "a_w": np.ascontiguousarray(w_in[:, cols]), "a_wb": np.ascontiguousarray(w_in[:, 4096 + 2 * r:4096 + 2 * r + 2]),
             "a_wa": np.ascontiguousarray(w_in[:, 4104 + 2 * r:4104 + 2 * r + 2]), "a_conv": np.ascontiguousarray(conv[:, cols]),
             "a_alog": np.ascontiguousarray(inp["a_log"][0, 2 * r:2 * r + 2]), "a_dtb": np.ascontiguousarray(inp["dt_bias"][0, 2 * r:2 * r + 2]),
             "a_c64": c64, "a_cmask": cmask, "a_sel": sel, "a_ones": ones,
             "a_wu": np.ascontiguousarray(w_in[:, 4112 + 256 * r:4112 + 256 * (r + 1)]),
             "b_wu": np.ascontiguousarray(w_in[:, 4112 + 256 * r:4112 + 256 * (r + 1)]),
             "b_lre": np.ascontiguousarray(inp["s5_lam_re"][0, gs]), "b_lim": np.ascontiguousarray(inp["s5_lam_im"][0, gs]),
             "b_bre": np.ascontiguousarray(inp["s5_b_re"][0, gs]), "b_bim": np.ascontiguousarray(inp["s5_b_im"][0, gs]),
             "b_cre": np.ascontiguousarray(inp["s5_c_re"][0, gs]), "b_cim": np.ascontiguousarray(inp["s5_c_im"][0, gs]),
             "b_ldt": np.ascontiguousarray(inp["s5_log_dt"][0, gs]), "b_dd": np.ascontiguousarray(inp["s5_d"][0, 256 * r:256 * (r + 1)]),
             "b_taus": TAUS, "b_mk": mk, "b_idm": idm,
             "c_x": xq, "c_wz": wz, "c_wglu": inp["w_glu"][0], "c_wout": inp["w_out_even"][0], "c_npost": inp["norm_post"][0],
             "c_gnw": inp["gdn_norm_w"][0],
             "d_win": inp["w_in_odd"][0], "d_wout": inp["w_out_odd"][0], "d_conv": inp["conv_short"][0],
             "d_npre": inp["norm_pre"][1], "d_npost": inp["norm_post"][1]}
        maps.append(m)
    res = run_bass_kernel_spmd(nc, maps, core_ids=list(range(8)))
    out = np.empty((2, 8192, 1024), np.float32)
    for c in range(8):
        b, r = divmod(c, 4)
        out[b, r * 2048:(r + 1) * 2048] = res.results[c]["out"]
    return out
```
